# Optimizing a Trainium2 kernel written in Bass

```python
import math
import jax, jax.numpy as jnp
from jax import lax
import numpy as np

D_MODEL = 1024
BATCH = 8
SEQ = 2048
DEPTH = 2

HEAD_DIM = 64
ROPE_THETA = 500000.0
Q_BLOCK = 128
NORM_EPS = 1e-6

SSD_D_INNER = D_MODEL
SSD_HEAD_DIM = 64
SSD_HEADS = SSD_D_INNER // SSD_HEAD_DIM
SSD_GROUPS = 2
SSD_STATE = 128
SSD_CONV = 4
SSD_CHUNK = 128
SSD_CONV_CH = SSD_D_INNER + 2 * SSD_GROUPS * SSD_STATE

SB_HEADS = 8
SB_WIDTH = SB_HEADS * HEAD_DIM

DSA_HEADS = 8
DSA_KV_HEADS = 2
DSA_WIDTH = DSA_HEADS * HEAD_DIM
DSA_KV_WIDTH = DSA_KV_HEADS * HEAD_DIM
IDX_HEADS = 8
IDX_DIM = 64
DSA_MAX_TOPK = 256

N_BRANCH = 3
MLP_HIDDEN = 4 * D_MODEL

IN_SPLITS = (
    SSD_D_INNER,
    SSD_CONV_CH,
    SSD_HEADS,
    SB_WIDTH, SB_WIDTH, SB_WIDTH,
    DSA_WIDTH, DSA_KV_WIDTH, DSA_KV_WIDTH,
    IDX_HEADS * IDX_DIM, IDX_DIM, IDX_HEADS,
    N_BRANCH * D_MODEL,
)
IN_WIDTH = sum(IN_SPLITS)

kernel_name = "hybrid_ssd_stickbreak_dsa_block"


def rms_norm(x, w):
    xf = x.astype(jnp.float32)
    y = xf * lax.rsqrt(jnp.mean(xf * xf, axis=-1, keepdims=True) + NORM_EPS)
    return (y * w.astype(jnp.float32)).astype(x.dtype)


def partial_rope(x, pos):
    d = x.shape[-1]
    rot = d // 4
    half = rot // 2
    inv_freq = jnp.exp(jnp.arange(half, dtype=jnp.float32) * (-2.0 * math.log(ROPE_THETA) / rot))
    ang = pos.astype(jnp.float32)[:, None] * inv_freq[None, :]
    cos = jnp.cos(ang)[None, :, None, :]
    sin = jnp.sin(ang)[None, :, None, :]
    xf = x.astype(jnp.float32)
    x1, x2, rest = xf[..., :half], xf[..., half:rot], xf[..., rot:]
    out = jnp.concatenate([x1 * cos - x2 * sin, x2 * cos + x1 * sin, rest], axis=-1)
    return out.astype(x.dtype)


def causal_depthwise_conv(x, w, b):
    k = w.shape[0]
    y = lax.conv_general_dilated(
        x, w[:, None, :].astype(x.dtype), window_strides=(1,), padding=[(k - 1, 0)],
        dimension_numbers=("NWC", "WIO", "NWC"), feature_group_count=x.shape[-1])
    return y + b.astype(x.dtype)


def ssd_chunked_scan(x, dt, a, bm, cm):
    bsz, L, g, r, p = x.shape
    n = bm.shape[-1]
    nc = L // SSD_CHUNK
    x = x.reshape(bsz, nc, SSD_CHUNK, g, r, p)
    dt = dt.reshape(bsz, nc, SSD_CHUNK, g, r)
    bm = bm.reshape(bsz, nc, SSD_CHUNK, g, n)
    cm = cm.reshape(bsz, nc, SSD_CHUNK, g, n)
    a_cum = jnp.cumsum(dt * a, axis=2)
    xdt = x * dt[..., None]
    ac = jnp.moveaxis(a_cum, 2, -1)
    causal = jnp.tril(jnp.ones((SSD_CHUNK, SSD_CHUNK), dtype=bool))
    decay_ij = jnp.exp(jnp.where(causal, ac[..., :, None] - ac[..., None, :], -jnp.inf))
    cb = jnp.einsum("bcign,bcjgn->bcgij", cm, bm)
    y_diag = jnp.einsum("bcgrij,bcjgrp->bcigrp", cb[:, :, :, None] * decay_ij, xdt)
    decay_to_end = jnp.exp(a_cum[:, :, -1:] - a_cum)
    states = jnp.einsum("bcjgn,bcjgrp->bcgrpn", bm, xdt * decay_to_end[..., None])
    chunk_decay = jnp.exp(a_cum[:, :, -1])

    def step(h, inp):
        s_c, d_c = inp
        return d_c[..., None, None] * h + s_c, h

    h0 = jnp.zeros((bsz, g, r, p, n), states.dtype)
    _, prev = lax.scan(step, h0, (jnp.moveaxis(states, 1, 0), jnp.moveaxis(chunk_decay, 1, 0)))
    prev = jnp.moveaxis(prev, 0, 1)
    y_off = jnp.einsum("bcign,bcgrpn->bcigrp", cm, prev) * jnp.exp(a_cum)[..., None]
    return (y_diag + y_off).reshape(bsz, L, g, r, p)


def ssd_mixer(z, xbc, dt_raw, conv_w, conv_b, dt_bias, a_log, d_skip, norm_w):
    bsz, L, _ = z.shape
    f32 = jnp.float32
    r = SSD_HEADS // SSD_GROUPS
    xbc = jax.nn.silu(causal_depthwise_conv(xbc, conv_w, conv_b))
    xs, bm, cm = jnp.split(xbc, [SSD_D_INNER, SSD_D_INNER + SSD_GROUPS * SSD_STATE], axis=-1)
    xs = xs.reshape(bsz, L, SSD_GROUPS, r, SSD_HEAD_DIM).astype(f32)
    bm = bm.reshape(bsz, L, SSD_GROUPS, SSD_STATE).astype(f32)
    cm = cm.reshape(bsz, L, SSD_GROUPS, SSD_STATE).astype(f32)
    dt = jax.nn.softplus(dt_raw.astype(f32) + dt_bias.astype(f32)).reshape(bsz, L, SSD_GROUPS, r)
    a = -jnp.exp(a_log.astype(f32)).reshape(SSD_GROUPS, r)
    y = ssd_chunked_scan(xs, dt, a, bm, cm)
    y = y + xs * d_skip.astype(f32).reshape(SSD_GROUPS, r)[..., None]
    y = y.reshape(bsz, L, SSD_D_INNER) * jax.nn.silu(z.astype(f32))
    return rms_norm(y, norm_w).astype(z.dtype)


def stick_breaking_attention(q, k, v):
    bsz, L, H, dh = q.shape
    scale = dh ** -0.5
    outs = []
    for s0 in range(0, L, Q_BLOCK):
        e0 = s0 + Q_BLOCK
        z = jnp.einsum("bthd,bshd->bhts", q[:, s0:e0], k[:, :e0],
                       preferred_element_type=jnp.float32) * scale
        t_idx = s0 + jnp.arange(Q_BLOCK)
        strict = jnp.arange(e0)[None, :] < t_idx[:, None]
        log_not = jnp.where(strict, jax.nn.log_sigmoid(-z), 0.0)
        after = lax.cumsum(log_not, axis=3, reverse=True) - log_not
        att = jnp.where(strict, jnp.exp(jax.nn.log_sigmoid(z) + after), 0.0)
        outs.append(jnp.einsum("bhts,bshd->bthd", att.astype(v.dtype), v[:, :e0]))
    return jnp.concatenate(outs, axis=1)


def dsa_attention(q, k, v, q_idx, k_idx, w_idx):
    bsz, L, H, dh = q.shape
    G = k.shape[2]
    topk = min(DSA_MAX_TOPK, L // 4)
    scale = dh ** -0.5
    outs = []
    for s0 in range(0, L, Q_BLOCK):
        e0 = s0 + Q_BLOCK
        t_idx = s0 + jnp.arange(Q_BLOCK)
        causal = jnp.arange(e0)[None, :] <= t_idx[:, None]
        logits = jnp.einsum("bthd,bsd->bths", q_idx[:, s0:e0], k_idx[:, :e0],
                            preferred_element_type=jnp.float32) * (IDX_DIM ** -0.5)
        score = jnp.einsum("bths,bth->bts", jax.nn.relu(logits),
                           w_idx[:, s0:e0].astype(jnp.float32) * (IDX_HEADS ** -0.5))
        score = jnp.where(causal, score, -jnp.inf)
        kk = min(topk, e0)
        _, sel = lax.top_k(score, kk)
        valid = sel <= t_idx[None, :, None]
        k_sel = jax.vmap(lambda kb, ib: kb[ib])(k, sel)
        v_sel = jax.vmap(lambda vb, ib: vb[ib])(v, sel)
        qb = q[:, s0:e0].reshape(bsz, Q_BLOCK, G, H // G, dh)
        s = jnp.einsum("btgrd,btkgd->btgrk", qb, k_sel, preferred_element_type=jnp.float32) * scale
        s = jnp.where(valid[:, :, None, None, :], s, -jnp.inf)
        p = jax.nn.softmax(s, axis=-1)
        o = jnp.einsum("btgrk,btkgd->btgrd", p.astype(v.dtype), v_sel)
        outs.append(o.reshape(bsz, Q_BLOCK, H, dh))
    return jnp.concatenate(outs, axis=1)


def setup_inputs(seed: int = 0) -> dict:
    key = jax.random.key(seed)
    ks = jax.random.split(key, 24)
    f32 = jnp.float32

    def nrm(k, shape, scale):
        return jax.random.normal(k, shape, f32) * scale

    dt = jnp.exp(jax.random.uniform(ks[10], (DEPTH, SSD_HEADS), f32)
                 * (math.log(0.1) - math.log(0.001)) + math.log(0.001))
    return {
        "x": nrm(ks[0], (BATCH, SEQ, D_MODEL), 1.0),
        "c": nrm(ks[1], (BATCH, D_MODEL), 1.0),
        "norm1_w": 1.0 + nrm(ks[2], (DEPTH, D_MODEL), 0.05),
        "ada_w": nrm(ks[3], (DEPTH, D_MODEL, 6 * D_MODEL), 0.5 * D_MODEL ** -0.5),
        "ada_b": nrm(ks[4], (DEPTH, 6 * D_MODEL), 0.02),
        "w_in": nrm(ks[5], (DEPTH, D_MODEL, IN_WIDTH), D_MODEL ** -0.5),
        "conv_w": nrm(ks[6], (DEPTH, SSD_CONV, SSD_CONV_CH), SSD_CONV ** -0.5),
        "conv_b": nrm(ks[7], (DEPTH, SSD_CONV_CH), 0.02),
        "dt_bias": dt + jnp.log(-jnp.expm1(-dt)),
        "a_log": jnp.log(jax.random.uniform(ks[8], (DEPTH, SSD_HEADS), f32, 1.0, 16.0)),
        "d_skip": 1.0 + nrm(ks[9], (DEPTH, SSD_HEADS), 0.1),
        "ssd_norm_w": 1.0 + nrm(ks[11], (DEPTH, SSD_D_INNER), 0.05),
        "w_br_ssd": nrm(ks[12], (DEPTH, SSD_D_INNER, D_MODEL), SSD_D_INNER ** -0.5),
        "w_br_sb": nrm(ks[13], (DEPTH, SB_WIDTH, D_MODEL), SB_WIDTH ** -0.5),
        "w_br_dsa": nrm(ks[14], (DEPTH, DSA_WIDTH, D_MODEL), DSA_WIDTH ** -0.5),
        "w_out": nrm(ks[15], (DEPTH, D_MODEL, D_MODEL), D_MODEL ** -0.5),
        "norm2_w": 1.0 + nrm(ks[16], (DEPTH, D_MODEL), 0.05),
        "w_up": nrm(ks[17], (DEPTH, D_MODEL, MLP_HIDDEN), D_MODEL ** -0.5),
        "w_down": nrm(ks[18], (DEPTH, MLP_HIDDEN, D_MODEL), MLP_HIDDEN ** -0.5),
        "final_norm_w": 1.0 + nrm(ks[19], (D_MODEL,), 0.05),
    }


def reference(x, c, norm1_w, ada_w, ada_b, w_in, conv_w, conv_b, dt_bias, a_log, d_skip,
              ssd_norm_w, w_br_ssd, w_br_sb, w_br_dsa, w_out, norm2_w, w_up, w_down,
              final_norm_w):
    bsz, L, _ = x.shape
    pos = jnp.arange(L, dtype=jnp.int32)
    split_at = np.cumsum(IN_SPLITS)[:-1].tolist()
    for l in range(DEPTH):
        mod = jax.nn.silu(c) @ ada_w[l] + ada_b[l]
        sh1, sc1, g1, sh2, sc2, g2 = jnp.split(mod[:, None, :], 6, axis=-1)

        h = rms_norm(x, norm1_w[l]) * (1.0 + sc1) + sh1
        (z, xbc, dt_raw, sb_q, sb_k, sb_v, ds_q, ds_k, ds_v,
         ix_q, ix_k, ix_w, gate_logits) = jnp.split(h @ w_in[l], split_at, axis=-1)

        y_ssd = ssd_mixer(z, xbc, dt_raw, conv_w[l], conv_b[l], dt_bias[l], a_log[l],
                          d_skip[l], ssd_norm_w[l])
        y_sb = stick_breaking_attention(
            sb_q.reshape(bsz, L, SB_HEADS, HEAD_DIM),
            sb_k.reshape(bsz, L, SB_HEADS, HEAD_DIM),
            sb_v.reshape(bsz, L, SB_HEADS, HEAD_DIM)).reshape(bsz, L, SB_WIDTH)
        y_dsa = dsa_attention(
            partial_rope(ds_q.reshape(bsz, L, DSA_HEADS, HEAD_DIM), pos),
            partial_rope(ds_k.reshape(bsz, L, DSA_KV_HEADS, HEAD_DIM), pos),
            ds_v.reshape(bsz, L, DSA_KV_HEADS, HEAD_DIM),
            partial_rope(ix_q.reshape(bsz, L, IDX_HEADS, IDX_DIM), pos),
            partial_rope(ix_k[:, :, None, :], pos)[:, :, 0],
            ix_w).reshape(bsz, L, DSA_WIDTH)

        g_ssd, g_sb, g_dsa = jnp.split(jax.nn.sigmoid(gate_logits), N_BRANCH, axis=-1)
        merged = (g_ssd * (y_ssd @ w_br_ssd[l]) + g_sb * (y_sb @ w_br_sb[l])
                  + g_dsa * (y_dsa @ w_br_dsa[l]))
        x = x + g1 * (merged @ w_out[l])

        h2 = rms_norm(x, norm2_w[l]) * (1.0 + sc2) + sh2
        x = x + g2 * (jnp.square(jax.nn.relu(h2 @ w_up[l])) @ w_down[l])
    return rms_norm(x, final_norm_w)
```

```python
import math
from contextlib import ExitStack

import numpy as np
import concourse.bass as bass
import concourse.mybir as mybir
from concourse.bass_utils import run_bass_kernel_spmd

F32 = mybir.dt.float32
BF16 = mybir.dt.bfloat16
I32 = mybir.dt.int32
AF = mybir.ActivationFunctionType
ALU = mybir.AluOpType
AX = mybir.AxisListType

D = 1024
L = 2048
DEPTH = 2
NT = L // 128
KC = D // 128
EPS = 1e-6
import os
NDS = int(os.environ.get("NDS", "12"))
STRICT = bool(int(os.environ.get("KSTRICT", "1")))


class Buf:
    __slots__ = ("name", "w", "r", "excl")

    def __init__(self, name):
        self.name = name
        self.excl = False
        self.w = None
        self.r = {}


class T:
    def __init__(self, t, buf):
        self.t = t
        self.b = buf

    def __getitem__(self, k):
        return self.t[k]


class KB:
    def __init__(self, nc):
        self.nc = nc
        self.es = ExitStack()
        self.sems = {}
        self.engs = {}
        for name, e in (("pe", nc.tensor), ("act", nc.scalar), ("dve", nc.vector),
                        ("pool", nc.gpsimd), ("sp", nc.sync)):
            key = "s_" + name
            self.sems[key] = self.es.enter_context(nc.semaphore(key))
            self.engs[name] = dict(e=e, key=key, cnt=0, seen={})
        self.dq = {}
        for q in ("sp", "pool"):
            keys = []
            for i in range(NDS):
                key = f"d_{q}{i}"
                self.sems[key] = self.es.enter_context(nc.semaphore(key))
                keys.append(key)
            self.dq[q] = dict(keys=keys, cnt=[0] * NDS, nxt=0)
        self.nbuf = 0
        self.psum = []
        self.ps_next = 0
        self.pinned = set()

    def buf(self, name=None):
        self.nbuf += 1
        return Buf(name or f"b{self.nbuf}")

    def sb(self, es, name, shape, dtype):
        self.nbuf += 1
        name = f"{name}_{self.nbuf}"
        t = es.enter_context(self.nc.sbuf_tensor(name, list(shape), dtype))
        return T(t, self.buf(name))

    def dram(self, name, shape, dtype, kind="Internal"):
        t = self.nc.dram_tensor(name, list(shape), dtype, kind=kind)
        return T(t.ap(), self.buf(name))

    def init_psum(self):
        for i in range(8):
            t = self.es.enter_context(self.nc.psum_tensor(f"ps{i}", [128, 512], F32))
            self.psum.append(T(t, self.buf(f"ps{i}")))
            self.psum[-1].b.excl = True

    def ps(self, pin=False):
        while True:
            i = self.ps_next
            self.ps_next = (self.ps_next + 1) % 8
            if i not in self.pinned:
                break
        if pin:
            self.pinned.add(i)
        return self.psum[i]

    def unpin(self, p):
        self.pinned.discard(self.psum.index(p))

    def _deps(self, own_key, R, W, same_raw, Wd=()):
        deps = {}

        def add(ev):
            if ev is None:
                return
            k, v = ev
            if deps.get(k, 0) < v:
                deps[k] = v

        for t in R:
            b = t.b if isinstance(t, T) else t
            if b.w is not None and (b.w[0] != own_key or same_raw):
                add(b.w)
            if b.excl:
                for k, v in b.r.items():
                    if k != own_key:
                        add((k, v))
        for t in W:
            b = t.b if isinstance(t, T) else t
            if b.w is not None and (b.w[0] != own_key or (STRICT and same_raw)):
                add(b.w)
            for k, v in b.r.items():
                if k != own_key or (STRICT and same_raw):
                    add((k, v))
        for t in Wd:
            b = t.b if isinstance(t, T) else t
            if b.w is not None and b.w[0] != own_key:
                add(b.w)
            for k, v in b.r.items():
                if k != own_key or (STRICT and same_raw):
                    add((k, v))
        return deps

    def _mark(self, ev, R, W):
        k, v = ev
        for t in R:
            b = t.b if isinstance(t, T) else t
            if b.r.get(k, 0) < v:
                b.r[k] = v
        for t in W:
            b = t.b if isinstance(t, T) else t
            b.w = ev
            b.r = {}

    def _wait(self, eng, deps):
        for k, v in deps.items():
            if eng["seen"].get(k, 0) < v:
                eng["e"].wait_ge(self.sems[k], v)
                eng["seen"][k] = v

    def op(self, engname, fn, R=(), W=(), Wd=()):
        eng = self.engs[engname]
        deps = self._deps(eng["key"], R, W, same_raw=(engname != "pe"), Wd=Wd)
        W = list(W) + list(Wd)
        self._wait(eng, deps)
        ins = fn(eng["e"])
        eng["cnt"] += 1
        ins.then_inc(self.sems[eng["key"]], 1)
        self._mark((eng["key"], eng["cnt"]), R, W)
        return ins

    def dma(self, q, out, in_, R=(), W=(), **kw):
        eng = self.engs[q]
        dq = self.dq[q]
        deps = self._deps(None, R, W, same_raw=True)
        i = dq["nxt"]
        dq["nxt"] = (i + 1) % NDS
        key = dq["keys"][i]
        if dq["cnt"][i] > 0:
            deps[key] = max(deps.get(key, 0), dq["cnt"][i])
        self._wait(eng, deps)
        dq["cnt"][i] += 16
        eng["e"].dma_start(out=out, in_=in_, **kw).then_inc(self.sems[key], 16)
        self._mark((key, dq["cnt"][i]), R, W)

    def barrier(self):
        allev = {}
        for name, eng in self.engs.items():
            if eng["cnt"] > 0:
                allev[eng["key"]] = eng["cnt"]
        for q, dq in self.dq.items():
            for key, c in zip(dq["keys"], dq["cnt"]):
                if c > 0:
                    allev[key] = c
        for name, eng in self.engs.items():
            deps = {k: v for k, v in allev.items() if k != eng["key"]}
            self._wait(eng, deps)

    def finish(self, out_bufs):
        eng = self.engs["sp"]
        deps = {}
        for t in out_bufs:
            b = t.b if isinstance(t, T) else t
            if b.w is not None:
                deps[b.w[0]] = max(deps.get(b.w[0], 0), b.w[1])
        self._wait(eng, deps)
        self.barrier()


IN_SHAPES = {
    "x": [L, D], "c": [D], "norm1_w": [DEPTH, D], "ada_w": [DEPTH, D, 6 * D], "ada_b": [DEPTH, 6 * D],
    "w_in": [DEPTH, D, 8536], "conv_w": [DEPTH, 4, 1536], "conv_b": [DEPTH, 1536], "dt_bias": [DEPTH, 16],
    "a_log": [DEPTH, 16], "d_skip": [DEPTH, 16], "ssd_norm_w": [DEPTH, D], "w_br_ssd": [DEPTH, D, D],
    "w_br_sb": [DEPTH, 512, D], "w_br_dsa": [DEPTH, 512, D], "w_out": [DEPTH, D, D], "norm2_w": [DEPTH, D],
    "w_up": [DEPTH, D, 4 * D], "w_down": [DEPTH, 4 * D, D], "final_norm_w": [D],
    "rope_cs": [L, 16],
}


def host_consts():
    half = 8
    inv_freq = np.exp(np.arange(half, dtype=np.float32) * np.float32(-2.0 * math.log(500000.0) / 16)).astype(np.float32)
    ang = np.arange(L, dtype=np.float32)[:, None] * inv_freq[None, :]
    return {"rope_cs": np.concatenate([np.cos(ang), np.sin(ang)], axis=1).astype(np.float32)}


class G:
    pass


def build(nlayers=DEPTH, dbg=(), skip_mixer=False, phases=("mod", "inproj", "ssd", "sb", "dsa", "merge", "norm2", "mlp")):
    nc = bass.Bass("TRN2", target_bir_lowering=False)
    kb = KB(nc)
    es = kb.es
    kb.init_psum()
    g = G()
    g.nc, g.kb, g.es, g.dbg = nc, kb, es, set(dbg)
    g.inp = {n: kb.dram(n, shp, F32, kind="ExternalInput") for n, shp in IN_SHAPES.items()}
    out_d = kb.dram("out", [L, D], F32, kind="ExternalOutput")
    g.dbg_out = {}

    g.ident = ident = kb.sb(es, "ident", [128, 128], F32)
    g.identb = identb = kb.sb(es, "identb", [128, 128], BF16)
    g.onesb = onesb = kb.sb(es, "onesb", [128, 128], BF16)
    g.onesf = kb.sb(es, "onesf", [128, 128], F32)
    g.triu = kb.sb(es, "triu", [128, 128], F32)
    g.negm = kb.sb(es, "negm", [128, 128], F32)
    g.triub = kb.sb(es, "triub", [128, 128], BF16)
    g.strl = kb.sb(es, "strl", [128, 128], F32)
    g.strlb = kb.sb(es, "strlb", [128, 128], BF16)
    g.trilb = kb.sb(es, "trilb", [128, 128], BF16)
    g.negup = kb.sb(es, "negup", [128, 128], F32)
    g.negmb = kb.sb(es, "negmb", [128, 128], BF16)
    g.pow2 = kb.sb(es, "pow2", [128, 24], F32)
    with ExitStack() as tmp:
        coli = kb.sb(tmp, "coli", [128, 128], I32)
        rowi = kb.sb(tmp, "rowi", [128, 1], I32)
        colf = kb.sb(tmp, "colf", [128, 128], F32)
        rowf = kb.sb(tmp, "rowf", [128, 1], F32)
        kb.op("pool", lambda e: e.iota(coli[:], [[1, 128]], base=0, channel_multiplier=0), W=[coli])
        kb.op("pool", lambda e: e.iota(rowi[:], [[0, 1]], base=0, channel_multiplier=1), W=[rowi])
        kb.op("dve", lambda e: e.tensor_copy(out=colf[:], in_=coli[:]), R=[coli], W=[colf])
        kb.op("dve", lambda e: e.tensor_copy(out=rowf[:], in_=rowi[:]), R=[rowi], W=[rowf])
        kb.op("dve", lambda e: e.tensor_scalar(out=ident[:], in0=colf[:], scalar1=rowf[:, 0:1], scalar2=None,
                                               op0=ALU.is_equal), R=[colf, rowf], W=[ident])
        kb.op("dve", lambda e: e.tensor_copy(out=identb[:], in_=ident[:]), R=[ident], W=[identb])
        kb.op("dve", lambda e: e.memset(onesb[:], 1.0), W=[onesb])
        kb.op("dve", lambda e: e.memset(g.onesf[:], 1.0), W=[g.onesf])
        kb.op("dve", lambda e: e.tensor_scalar(out=g.triu[:], in0=colf[:], scalar1=rowf[:, 0:1], scalar2=None, op0=ALU.is_ge),
              R=[colf, rowf], W=[g.triu])
        kb.op("dve", lambda e: e.tensor_scalar(out=g.negm[:], in0=g.triu[:], scalar1=-1.0, scalar2=30000.0, op0=ALU.add, op1=ALU.mult),
              R=[g.triu], W=[g.negm])
        kb.op("dve", lambda e: e.tensor_copy(out=g.triub[:], in_=g.triu[:]), R=[g.triu], W=[g.triub])
        kb.op("dve", lambda e: e.tensor_copy(out=g.negmb[:], in_=g.negm[:]), R=[g.negm], W=[g.negmb])
        kb.op("dve", lambda e: e.tensor_scalar(out=g.strl[:], in0=colf[:], scalar1=rowf[:, 0:1], scalar2=None, op0=ALU.is_lt),
              R=[colf, rowf], W=[g.strl])
        kb.op("dve", lambda e: e.tensor_copy(out=g.strlb[:], in_=g.strl[:]), R=[g.strl], W=[g.strlb])
        kb.op("dve", lambda e: e.tensor_scalar(out=g.trilb[:], in0=colf[:], scalar1=rowf[:, 0:1], scalar2=None, op0=ALU.is_le),
              R=[colf, rowf], W=[g.trilb])
        kb.op("dve", lambda e: e.tensor_scalar(out=g.negup[:], in0=g.trilb[:], scalar1=-1.0, scalar2=1.0e9, op0=ALU.add, op1=ALU.mult),
              R=[g.trilb], W=[g.negup])
        for kk_ in range(24):
            kb.op("dve", lambda e, kk_=kk_: e.memset(g.pow2[:, kk_:kk_ + 1], float(2.0 ** (-kk_))), Wd=[g.pow2])
        kb.barrier()

    g.xT = xT = kb.sb(es, "xT", [128, KC, L], F32)
    g.xTb = xTb = [[kb.buf(f"xT{c}_{tb}") for tb in range(4)] for c in range(KC)]
    g.hTb = [[kb.buf(f"hT{c}_{tb}") for tb in range(4)] for c in range(KC)]

    fnw = load_cols(g, es, "fnw", g.inp["final_norm_w"].t, KC)
    g.ccol = load_cols(g, es, "ccol", g.inp["c"].t, KC)
    g.csilu = kb.sb(es, "csilu", [128, KC], BF16)
    kb.op("act", lambda e: e.activation(out=g.csilu[:], in_=g.ccol[:], func=AF.Silu), R=[g.ccol], W=[g.csilu])
    g.modT_l = [kb.sb(es, "modT", [128, 48], F32) for _ in range(DEPTH)]
    g.modraw = [kb.sb(es, "modraw", [128, 48], F32) for _ in range(DEPTH)]
    g.A1_l = [kb.sb(es, "A1", [128, KC], F32) for _ in range(DEPTH)]
    g.A2_l = [kb.sb(es, "A2", [128, KC], F32) for _ in range(DEPTH)]
    g.dsa_hook = None

    x_in = g.inp["x"]
    with ExitStack() as ph:
        xin = [kb.sb(ph, f"xin{i}", [128, D], F32) for i in range(2)]
        for tt in range(NT):
            xi = xin[tt % 2]
            kb.dma("sp", xi[:], x_in.t[tt * 128:(tt + 1) * 128, :], W=[xi])
            for half in range(2):
                p = kb.ps()
                for j in range(4):
                    c = half * 4 + j
                    kb.op("pe", lambda e, c=c, j=j, p=p, xi=xi: e.transpose(
                        p[:, j * 128:(j + 1) * 128], xi[:, c * 128:(c + 1) * 128], ident[:]),
                        R=[xi, ident], W=[p])
                tb = tt // 4
                wb = [xTb[half * 4 + j][tb] for j in range(4)]
                dst = xT[:, half * 4:half * 4 + 4, tt * 128:(tt + 1) * 128]
                src = p[:, :].rearrange("p (j t) -> p j t", j=4)
                if half == 0:
                    kb.op("act", lambda e, dst=dst, src=src: e.activation(out=dst, in_=src, func=AF.Copy), R=[p], W=wb)
                else:
                    kb.op("dve", lambda e, dst=dst, src=src: e.tensor_copy(out=dst, in_=src), R=[p], W=wb)
        kb.barrier()

    def scr(name, shape, dtype=BF16):
        return kb.dram("scr_" + name, shape, dtype, kind=("ExternalOutput" if name in g.dbg else "Internal"))
    g.scr = dict(zs=scr("zs", [L, 1024]), sbv=scr("sbv", [L, 512]), dv=scr("dv", [L, 128]),
                 dqT=scr("dqT", [8, 64, L]), dkT=scr("dkT", [2, 64, L]), iqT=scr("iqT", [8, 64, L]), ikT=scr("ikT", [1, 64, L]),
                 yT=scr("yT", [16, 128, L]), sbqT=scr("sbqT", [4, 128, L]), sbkT=scr("sbkT", [4, 128, L]), gT=scr("gT", [24, 128, L]))
    for name in g.scr:
        if name in g.dbg:
            g.dbg_out["scr_" + name] = g.scr[name]
    g.ropecs = kb.sb(es, "ropecs", [128, NT, 16], F32)
    kb.dma("sp", g.ropecs[:], g.inp["rope_cs"].t.rearrange("(t p) c -> p t c", p=128), W=[g.ropecs])
    g.dt_tok = kb.sb(es, "dt_tok", [128, NT, 16], F32)
    g.wix_tok = kb.sb(es, "wix_tok", [128, NT, 8], F32)
    g.dtb_bc = kb.sb(es, "dtb_bc", [128, 16], F32)
    g.cb = kb.sb(es, "cb", [128, 12], F32)
    g.cw = [kb.sb(es, f"cw{k}", [128, 12], F32) for k in range(4)]

    for l in range(nlayers):
        g.modT, g.A1, g.A2 = g.modT_l[l], g.A1_l[l], g.A2_l[l]
        if "mod" in phases:
            if l == 0:
                phase_mod(g, l)
            else:
                mod_finish(g, l)
        g.dsa_hook = (l + 1) if (l + 1 < nlayers and "mod" in phases) else None
        dump_sb(g, f"mod{l}", g.modT, [128, 48])
        if not skip_mixer:
            with ExitStack() as ssd_scope:
                g.xs_tok = kb.sb(ssd_scope, "xs_tok", [128, NT, 1024], BF16)
                g.bt_tok = kb.sb(ssd_scope, "bt_tok", [128, NT, 256], BF16)
                g.bcT = kb.sb(ssd_scope, "bcT", [128, 4, L], BF16)
                with nc.allow_non_contiguous_dma(reason="tiny vectors"):
                    kb.dma("sp", g.dtb_bc[:], g.inp["dt_bias"].t[l].partition_broadcast(128), W=[g.dtb_bc])
                    kb.dma("sp", g.cb[:], g.inp["conv_b"].t[l].rearrange("(c p) -> p c", p=128), W=[g.cb])
                    for k in range(4):
                        kb.dma("sp", g.cw[k][:], g.inp["conv_w"].t[l][k].rearrange("(c p) -> p c", p=128), W=[g.cw[k]])
                with ExitStack() as hsc:
                    g.hT = kb.sb(hsc, "hT", [128, KC, L], BF16)
                    phase_norm(g, l, which=1)
                    dump_hT(g, f"h{l}")
                    if "inproj" in phases:
                        phase_inproj(g, l)
                dump_sb(g, f"xs_tok{l}", g.xs_tok, [128, NT, 1024], BF16)
                dump_sb(g, f"bt_tok{l}", g.bt_tok, [128, NT, 256], BF16)
                dump_sb(g, f"bcT{l}", g.bcT, [128, 4, L], BF16)
                dump_sb(g, f"dt_tok{l}", g.dt_tok, [128, NT, 16], F32)
                dump_sb(g, f"wix_tok{l}", g.wix_tok, [128, NT, 8], F32)
                kb.barrier()
                if "ssd" in phases:
                    phase_ssd(g, l)
            if "sb" in phases:
                phase_sb(g, l)
            if "dsa" in phases:
                phase_dsa(g, l)
            if "merge" in phases:
                phase_merge(g, l)
            dump_xT(g, f"xmix{l}")
        with ExitStack() as hsc:
            g.hT = kb.sb(hsc, "hT", [128, KC, L], BF16)
            if "norm2" in phases:
                phase_norm(g, l, which=2)
            if "mlp" in phases:
                phase_mlp(g, l)
        dump_xT(g, f"xout{l}")

    final_norm(kb, nc, xT, xTb, fnw, onesb, ident, out_d)
    kb.finish([out_d] + list(g.dbg_out.values()))
    return nc, sorted(g.dbg_out.keys())


def load_cols(g, es_, name, src_ap, n):
    kb = g.kb
    t = kb.sb(es_, name, [128, n], F32)
    with g.nc.allow_non_contiguous_dma(reason="tiny per-feature vector load"):
        for c0 in range(0, n, 8):
            c1 = min(n, c0 + 8)
            kb.dma("sp", t[:, c0:c1], src_ap[c0 * 128:c1 * 128].rearrange("(c p) -> p c", p=128), W=[t])
    return t


def dump_sb(g, name, t, shape, dtype=F32):
    if name not in g.dbg:
        return
    d = g.kb.dram("dbg_" + name, shape, dtype, kind="ExternalOutput")
    g.kb.barrier()
    g.kb.dma("sp", d.t, t[:], R=[t], W=[d])
    g.dbg_out["dbg_" + name] = d


def dump_xT(g, name):
    if name not in g.dbg:
        return
    d = g.kb.dram("dbg_" + name, [128, KC, L], F32, kind="ExternalOutput")
    g.kb.barrier()
    g.kb.dma("sp", d.t, g.xT[:], R=[b for row in g.xTb for b in row], W=[d])
    g.dbg_out["dbg_" + name] = d


def dump_hT(g, name):
    if name not in g.dbg:
        return
    d = g.kb.dram("dbg_" + name, [128, KC, L], BF16, kind="ExternalOutput")
    g.kb.barrier()
    g.kb.dma("sp", d.t, g.hT[:], R=[b for row in g.hTb for b in row], W=[d])
    g.dbg_out["dbg_" + name] = d


def load_w(g, t, src_rows_ap):
    g.kb.dma("pool", t[:], src_rows_ap.rearrange("(kc p) n -> p kc n", p=128), W=[t])


def mod_piece(g, l, j4, w):
    kb = g.kb
    kb.dma("pool", w[:], g.inp["ada_w"].t[l][:, j4 * 512:(j4 + 1) * 512].rearrange("(kc p) n -> p kc n", p=128), W=[w])
    p = kb.ps()
    for jl in range(4):
        for kc in range(KC):
            kb.op("pe", lambda e, jl=jl, kc=kc: e.matmul(p[:, jl:jl + 1], w[:, kc, jl * 128:(jl + 1) * 128], g.csilu[:, kc:kc + 1],
                                                         start=(kc == 0), stop=(kc == KC - 1)), R=[w, g.csilu], W=[p])
    raw = g.modraw[l]
    kb.op("act", lambda e: e.activation(out=raw[:, j4 * 4:(j4 + 1) * 4], in_=p[:, 0:4], func=AF.Copy), R=[p], Wd=[raw])


def mod_finish(g, l):
    kb = g.kb
    with ExitStack() as ph:
        adab = load_cols(g, ph, "adab", g.inp["ada_b"].t[l], 48)
        n1w = load_cols(g, ph, "n1w", g.inp["norm1_w"].t[l], KC)
        n2w = load_cols(g, ph, "n2w", g.inp["norm2_w"].t[l], KC)
        modT, A1, A2 = g.modT_l[l], g.A1_l[l], g.A2_l[l]
        kb.op("dve", lambda e: e.tensor_tensor(out=modT[:], in0=g.modraw[l][:], in1=adab[:], op=ALU.add),
              R=[g.modraw[l], adab], W=[modT])
        kb.op("dve", lambda e: e.scalar_tensor_tensor(out=A1[:], in0=modT[:, 8:16], scalar=1.0, in1=n1w[:],
                                                      op0=ALU.add, op1=ALU.mult), R=[modT, n1w], W=[A1])
        kb.op("dve", lambda e: e.scalar_tensor_tensor(out=A2[:], in0=modT[:, 32:40], scalar=1.0, in1=n2w[:],
                                                      op0=ALU.add, op1=ALU.mult), R=[modT, n2w], W=[A2])
        kb.barrier()


def phase_mod(g, l):
    kb = g.kb
    with ExitStack() as ph:
        wt = [kb.sb(ph, f"adaw{i}", [128, KC, 512], BF16) for i in range(3)]
        for j4 in range(12):
            mod_piece(g, l, j4, wt[j4 % 3])
        kb.barrier()
    mod_finish(g, l)


def phase_norm(g, l, which):
    kb = g.kb
    A = g.A1 if which == 1 else g.A2
    sh0 = 0 if which == 1 else 24
    with ExitStack() as ph:
        sq = [kb.sb(ph, f"nsq{i}", [128, 512], BF16) for i in range(2)]
        rs = kb.sb(ph, "nrs", [128, 512], F32)
        tmp = [kb.sb(ph, f"ntmp{i}", [128, 512], F32) for i in range(2)]
        for tb in range(4):
            rstd_bcast(kb, ph, g.xT, g.xTb, tb, g.onesb, sq, rs)
            for c in range(KC):
                t = tmp[c % 2]
                kb.op("dve", lambda e, t=t, c=c: e.tensor_tensor(out=t[:], in0=g.xT[:, c, tb * 512:(tb + 1) * 512], in1=rs[:],
                                                                 op=ALU.mult), R=[g.xTb[c][tb], rs], W=[t])
                kb.op("act", lambda e, t=t, c=c: e.activation(
                    out=g.hT[:, c, tb * 512:(tb + 1) * 512], in_=t[:], func=AF.Identity,
                    scale=A[:, c:c + 1], bias=g.modT[:, sh0 + c:sh0 + c + 1]), R=[t, A, g.modT], W=[g.hTb[c][tb]])
        kb.barrier()


OFF = dict(z=0, xbc=1024, dt=2560, sbq=2576, sbk=3088, sbv=3600, dsq=4112, dsk=4624, dsv=4752,
           ixq=4880, ixk=5392, ixw=5456, gate=5464)


def rope(g, f3, nh, tt, rt):
    kb = g.kb
    fb = f3._buf
    cos = g.ropecs[:, tt:tt + 1, 0:8].broadcast_to([128, nh, 8])
    sin = g.ropecs[:, tt:tt + 1, 8:16].broadcast_to([128, nh, 8])
    x1 = f3.ap[:, :, 0:8]
    x2 = f3.ap[:, :, 8:16]
    t = [rt[:, i, 0:nh, :] for i in range(4)]
    kb.op("dve", lambda e: e.tensor_tensor(out=t[0], in0=x1, in1=cos, op=ALU.mult), R=[fb, g.ropecs], W=[rt])
    kb.op("dve", lambda e: e.tensor_tensor(out=t[1], in0=x2, in1=sin, op=ALU.mult), R=[fb, g.ropecs], W=[rt])
    kb.op("dve", lambda e: e.tensor_tensor(out=t[2], in0=x2, in1=cos, op=ALU.mult), R=[fb, g.ropecs], W=[rt])
    kb.op("dve", lambda e: e.tensor_tensor(out=t[3], in0=x1, in1=sin, op=ALU.mult), R=[fb, g.ropecs], W=[rt])
    kb.op("dve", lambda e: e.tensor_tensor(out=x1, in0=t[0], in1=t[1], op=ALU.subtract), R=[rt], W=[fb])
    kb.op("dve", lambda e: e.tensor_tensor(out=x2, in0=t[2], in1=t[3], op=ALU.add), R=[rt], W=[fb])


class V:
    def __init__(self, ap, buf):
        self.ap = ap
        self._buf = buf


def phase_inproj(g, l):
    kb, nc = g.kb, g.nc
    win = g.inp["w_in"].t[l]
    S = g.scr
    with ExitStack() as ph:
        wt = [kb.sb(ph, "winw", [128, KC, 512], BF16) for _ in range(2)]
        nw = [0]

        def getw(n0, N):
            w = wt[nw[0] % 2]
            nw[0] += 1
            kb.dma("pool", w[:, :, 0:N], win[:, n0:n0 + N].rearrange("(kc p) n -> p kc n", p=128), W=[w])
            return w

        stgb = [kb.sb(ph, "stgb", [128, 512], BF16) for _ in range(3)]
        stgf = [kb.sb(ph, "stgf", [128, 512], F32) for _ in range(2)]
        rt = kb.sb(ph, "ropetmp", [128, 4, 16, 8], F32)
        tst = [kb.sb(ph, "tst", [64, 8, 512], BF16) for _ in range(2)]
        cnt = dict(b=0, f=0, t=0)

        def nxt(lst, k):
            r = lst[cnt[k] % len(lst)]
            cnt[k] += 1
            return r

        def tok_mm(w, N, tt):
            p = kb.ps()
            for kc in range(KC):
                kb.op("pe", lambda e, kc=kc: e.matmul(p[:, 0:N], g.hT[:, kc, tt * 128:(tt + 1) * 128], w[:, kc, 0:N],
                                                      start=(kc == 0), stop=(kc == KC - 1)),
                      R=[g.hTb[kc][tt // 4], w], W=[p])
            return p

        def feat_mm(w, ci, tb):
            p = kb.ps()
            for kc in range(KC):
                kb.op("pe", lambda e, kc=kc: e.matmul(p[:, :], w[:, kc, ci * 128:(ci + 1) * 128],
                                                      g.hT[:, kc, tb * 512:(tb + 1) * 512],
                                                      start=(kc == 0), stop=(kc == KC - 1)),
                      R=[g.hTb[kc][tb], w], W=[p])
            return p

        for zi in range(2):
            w = getw(OFF["z"] + zi * 512, 512)
            for tt in range(NT):
                p = tok_mm(w, 512, tt)
                sb_ = nxt(stgb, "b")
                kb.op("act", lambda e: e.activation(out=sb_[:], in_=p[:, :], func=AF.Silu), R=[p], W=[sb_])
                kb.dma("sp", S["zs"].t[tt * 128:(tt + 1) * 128, zi * 512:(zi + 1) * 512], sb_[:], R=[sb_])
        w = getw(OFF["dt"], 16)
        for tt in range(NT):
            p = tok_mm(w, 16, tt)
            f = nxt(stgf, "f")
            kb.op("dve", lambda e: e.tensor_tensor(out=f[:, 0:16], in0=p[:, 0:16], in1=g.dtb_bc[:], op=ALU.add),
                  R=[p, g.dtb_bc], W=[f])
            kb.op("act", lambda e: e.activation(out=f[:, 0:16], in_=f[:, 0:16], func=AF.Exp), R=[f], W=[f])
            kb.op("act", lambda e: e.activation(out=g.dt_tok[:, tt, :], in_=f[:, 0:16], func=AF.Ln, bias=1.0),
                  R=[f], W=[g.dt_tok])
        w = getw(OFF["sbv"], 512)
        for tt in range(NT):
            p = tok_mm(w, 512, tt)
            sb_ = nxt(stgb, "b")
            kb.op("act", lambda e: e.activation(out=sb_[:], in_=p[:, :], func=AF.Copy), R=[p], W=[sb_])
            kb.dma("sp", S["sbv"].t[tt * 128:(tt + 1) * 128, :], sb_[:], R=[sb_])

        def roped_group(n0, N, nh, dstT, extra=None):
            w = getw(n0, N)
            fs, bs, sts = {}, {}, {}

            def stA(tt):
                p = tok_mm(w, N, tt)
                f = nxt(stgf, "f")
                kb.op("act", lambda e: e.activation(out=f[:, 0:N], in_=p[:, 0:N], func=AF.Copy), R=[p], W=[f])
                fs[tt] = f

            def stB(tt):
                f = fs.pop(tt)
                rope(g, V(f[:, 0:nh * 64].rearrange("p (h d) -> p h d", d=64), f), nh, tt, rt)
                b = nxt(stgb, "b")
                kb.op("act", lambda e: e.activation(out=b[:, 0:N], in_=f[:, 0:N], func=AF.Copy), R=[f], W=[b])
                if extra is not None:
                    extra(tt, f, b)
                bs[tt] = b

            def stC(tt):
                b = bs.pop(tt)
                pt = kb.ps()
                ptb = pt[:, :].bitcast(BF16)
                for h in range(nh):
                    kb.op("pe", lambda e, h=h: e.transpose(ptb[0:64, h * 128:(h + 1) * 128], b[:, h * 64:(h + 1) * 64], g.identb[:]),
                          R=[b, g.identb], W=[pt])
                if tt % 4 == 0:
                    sts[tt // 4] = nxt(tst, "t")
                st = sts[tt // 4]
                kb.op("act", lambda e: e.activation(
                    out=st[0:64, 0:nh, (tt % 4) * 128:(tt % 4 + 1) * 128],
                    in_=ptb[0:64, 0:nh * 128].rearrange("p (h t) -> p h t", h=nh), func=AF.Copy), R=[pt], Wd=[st])
                if tt % 4 == 3:
                    tb = tt // 4
                    kb.dma("sp", dstT.t[:, :, tb * 512:(tb + 1) * 512].rearrange("h d t -> d h t"), st[0:64, 0:nh, :], R=[st])

            for step in range(NT + 2):
                if step < NT:
                    stA(step)
                if 0 <= step - 1 < NT:
                    stB(step - 1)
                if 0 <= step - 2 < NT:
                    stC(step - 2)

        roped_group(OFF["dsq"], 512, 8, S["dqT"])

        def dsv_extra(tt, f, b):
            kb.dma("sp", S["dv"].t[tt * 128:(tt + 1) * 128, :], b[:, 128:256], R=[b])
        roped_group(OFF["dsk"], 256, 2, S["dkT"], dsv_extra)
        roped_group(OFF["ixq"], 512, 8, S["iqT"])

        def ixw_extra(tt, f, b):
            kb.op("dve", lambda e: e.tensor_copy(out=g.wix_tok[:, tt, :], in_=f[:, 64:72]), R=[f], W=[g.wix_tok])
        roped_group(OFF["ixk"], 72, 1, S["ikT"], ixw_extra)

        def feat_group(n0, dstT, c0, func):
            w = getw(n0, 512)
            for ci in range(4):
                for tb in range(4):
                    p = feat_mm(w, ci, tb)
                    sb_ = nxt(stgb, "b")
                    kb.op("act", lambda e: e.activation(out=sb_[:], in_=p[:, :], func=func), R=[p], W=[sb_])
                    kb.dma("sp", dstT.t[c0 + ci, :, tb * 512:(tb + 1) * 512], sb_[:], R=[sb_])

        feat_group(OFF["sbq"], S["sbqT"], 0, AF.Copy)
        feat_group(OFF["sbk"], S["sbkT"], 0, AF.Copy)
        for gi in range(6):
            feat_group(OFF["gate"] + gi * 512, S["gT"], gi * 4, AF.Sigmoid)

        kb.barrier()

    with ExitStack() as ph:
        wt = [kb.sb(ph, "winw", [128, KC, 512], BF16) for _ in range(2)]
        nw = [0]
        xpad = [kb.sb(ph, "xpad", [128, 3 + L], F32) for _ in range(1)]
        cva = [kb.sb(ph, "cva", [128, L], F32) for _ in range(1)]
        cvo = [kb.sb(ph, "cvo", [128, L], BF16) for _ in range(2)]
        for xp in xpad:
            kb.op("dve", lambda e, xp=xp: e.memset(xp[:, 0:3], 0.0), W=[xp])
        for gi in range(3):
            w = getw(OFF["xbc"] + gi * 512, 512)
            for ci in range(4):
                cidx = gi * 4 + ci
                xp, acc = xpad[0], cva[0]
                for tb in range(4):
                    p = feat_mm(w, ci, tb)
                    kb.op("act", lambda e: e.activation(out=xp[:, 3 + tb * 512:3 + (tb + 1) * 512], in_=p[:, :], func=AF.Copy),
                          R=[p], Wd=[xp])
                kb.op("act", lambda e: e.activation(out=acc[:], in_=xp[:, 3:3 + L], func=AF.Identity,
                                                    scale=g.cw[3][:, cidx:cidx + 1], bias=g.cb[:, cidx:cidx + 1]),
                      R=[xp, g.cw[3], g.cb], W=[acc])
                for k in (2, 1, 0):
                    kb.op("dve", lambda e, k=k: e.scalar_tensor_tensor(out=acc[:], in0=xp[:, k:k + L], scalar=g.cw[k][:, cidx:cidx + 1],
                                                                       in1=acc[:], op0=ALU.mult, op1=ALU.add),
                          R=[xp, g.cw[k], acc], W=[acc])
                if cidx < 8:
                    o = cvo[cidx % 2]
                    ov = o[:, :]
                else:
                    o = g.bcT
                    ov = g.bcT[:, cidx - 8, :]
                kb.op("act", lambda e: e.activation(out=ov, in_=acc[:], func=AF.Silu), R=[acc], W=[o])
                if cidx < 10:
                    for half in range(2):
                        pt = kb.ps()
                        ptb = pt[:, :].bitcast(BF16)
                        for j in range(8):
                            tt = half * 8 + j
                            kb.op("pe", lambda e, j=j, tt=tt: e.transpose(ptb[:, j * 128:(j + 1) * 128], ov[:, tt * 128:(tt + 1) * 128],
                                                                          g.identb[:]), R=[o, g.identb], W=[pt])
                        if cidx < 8:
                            dst, dt_ = g.xs_tok[:, half * 8:(half + 1) * 8, cidx * 128:(cidx + 1) * 128], g.xs_tok
                        else:
                            dst, dt_ = g.bt_tok[:, half * 8:(half + 1) * 8, (cidx - 8) * 128:(cidx - 7) * 128], g.bt_tok
                        kb.op("dve", lambda e: e.tensor_copy(out=dst, in_=ptb.rearrange("p (j f) -> p j f", j=8)), R=[pt], W=[dt_])
        kb.barrier()


def phase_ssd(g, l):
    kb, nc = g.kb, g.nc
    S = g.scr
    with ExitStack() as ph:
        def sb(name, shape, dt_=F32):
            return kb.sb(ph, name, shape, dt_)
        a_bc = sb("a_bc", [128, 16])
        dsk_bc = sb("dsk_bc", [128, 16])
        nw_bc = sb("nw_bc", [128, 1024])
        Dmat = sb("Dmat", [128, 16, 128], BF16)
        with nc.allow_non_contiguous_dma(reason="tiny vectors"):
            kb.dma("sp", a_bc[:], g.inp["a_log"].t[l].partition_broadcast(128), W=[a_bc])
            kb.dma("sp", dsk_bc[:], g.inp["d_skip"].t[l].partition_broadcast(128), W=[dsk_bc])
            kb.dma("sp", nw_bc[:], g.inp["ssd_norm_w"].t[l].partition_broadcast(128), W=[nw_bc])
        kb.op("act", lambda e: e.activation(out=a_bc[:], in_=a_bc[:], func=AF.Exp), R=[a_bc], W=[a_bc])
        kb.op("dve", lambda e: e.tensor_scalar(out=a_bc[:], in0=a_bc[:], scalar1=-1.0, scalar2=None, op0=ALU.mult), R=[a_bc], W=[a_bc])
        for h in range(16):
            kb.op("dve", lambda e, h=h: e.tensor_scalar(out=Dmat[:, h, :], in0=g.ident[:], scalar1=dsk_bc[:, h:h + 1], scalar2=None,
                                                        op0=ALU.mult), R=[g.ident, dsk_bc], Wd=[Dmat])
        NB = 2
        dA = [sb("dA", [128, 16]) for _ in range(NB)]
        ac = [sb("ac", [128, 16]) for _ in range(NB)]
        eac = [sb("eac", [128, 16]) for _ in range(NB)]
        wgt = [sb("wgt", [128, 16]) for _ in range(NB)]
        cd = [sb("cd", [128, 16]) for _ in range(NB)]
        X = [sb("X", [128, 2, 16, 128], BF16)] * 2
        dAs = [sb("dAs", [128, 2, 16], BF16) for _ in range(NB)]
        acs = [sb("acs", [128, 2, 16], BF16) for _ in range(NB)]
        dd = [sb("dd", [128, 16, 128])] * 2
        Mt = [sb("Mt", [128, 16, 128], BF16) for _ in range(NB)]
        cbT = [sb("cbT", [128, 2, 128]) for _ in range(NB)]
        xw = [sb("xw", [128, 1024], BF16) for _ in range(NB)]
        zst = [sb("zst", [128, 1024], BF16) for _ in range(NB)]
        yy = [sb("yy", [128, 1024])] * 2
        yz = [sb("yz", [128, 1024])] * 2
        ss = [sb("ss", [128, 1]) for _ in range(NB)]
        yn = [sb("yn", [128, 1024], BF16) for _ in range(NB)]
        yst = [sb("yst", [128, KC, 512], BF16)] * 2
        hst = [sb("hst", [128, 512]) for _ in range(2)]
        htmp = [sb("htmp", [128, 512]) for _ in range(2)]
        prevb = [[sb("prevb", [128, 512], BF16) for _ in range(2)] for _ in range(2)]
        ytmp = [sb("ytmp", [128, 512]) for _ in range(2)]

        def P1(c):
            k = c % NB
            tsl = slice(c * 128, (c + 1) * 128)
            kb.op("dve", lambda e: e.tensor_tensor(out=dA[k][:], in0=g.dt_tok[:, c, :], in1=a_bc[:], op=ALU.mult),
                  R=[g.dt_tok, a_bc], W=[dA[k]])
            kb.op("dve", lambda e: e.tensor_copy(out=dAs[k][:, 0, :], in_=dA[k][:]), R=[dA[k]], W=[dAs[k]])
            kb.op("dve", lambda e: e.tensor_tensor(out=dAs[k][:, 1, :], in0=dA[k][:], in1=dAs[k][:, 0, :], op=ALU.subtract),
                  R=[dA[k], dAs[k]], W=[dAs[k]])
            p_ac = kb.ps()
            for hl_ in range(2):
                kb.op("pe", lambda e, hl_=hl_: e.matmul(p_ac[:, 0:16], g.triub[:], dAs[k][:, hl_, :], start=(hl_ == 0), stop=(hl_ == 1)),
                      R=[g.triub, dAs[k]], W=[p_ac])
            for hl_ in range(2):
                kb.op("pe", lambda e, hl_=hl_: e.matmul(p_ac[:, 16:32], g.onesb[:], dAs[k][:, hl_, :], start=(hl_ == 0), stop=(hl_ == 1)),
                      R=[g.onesb, dAs[k]], W=[p_ac])
            kb.op("dve", lambda e: e.tensor_copy(out=ac[k][:], in_=p_ac[:, 0:16]), R=[p_ac], W=[ac[k]])
            kb.op("act", lambda e: e.activation(out=eac[k][:], in_=p_ac[:, 0:16], func=AF.Exp), R=[p_ac], W=[eac[k]])
            kb.op("act", lambda e: e.activation(out=cd[k][:], in_=p_ac[:, 16:32], func=AF.Exp), R=[p_ac], W=[cd[k]])
            kb.op("dve", lambda e: e.tensor_tensor(out=wgt[k][:], in0=p_ac[:, 16:32], in1=ac[k][:], op=ALU.subtract),
                  R=[p_ac, ac[k]], W=[wgt[k]])
            kb.op("act", lambda e: e.activation(out=wgt[k][:], in_=wgt[k][:], func=AF.Exp), R=[wgt[k]], W=[wgt[k]])
            kb.op("dve", lambda e: e.tensor_tensor(out=wgt[k][:], in0=wgt[k][:], in1=g.dt_tok[:, c, :], op=ALU.mult),
                  R=[wgt[k], g.dt_tok], W=[wgt[k]])
            kb.op("dve", lambda e: e.tensor_tensor(
                out=xw[k][:, :].rearrange("p (h d) -> p h d", d=64), in0=g.xs_tok[:, c, :].rearrange("p (h d) -> p h d", d=64),
                in1=wgt[k][:, :].unsqueeze(2).broadcast_to([128, 16, 64]), op=ALU.mult), R=[g.xs_tok, wgt[k]], W=[xw[k]])
            kb.op("dve", lambda e: e.tensor_copy(out=acs[k][:, 0, :], in_=ac[k][:]), R=[ac[k]], W=[acs[k]])
            kb.op("dve", lambda e: e.tensor_tensor(out=acs[k][:, 1, :], in0=ac[k][:], in1=acs[k][:, 0, :], op=ALU.subtract),
                  R=[ac[k], acs[k]], W=[acs[k]])
            for hl_ in range(2):
                kb.op("dve", lambda e, hl_=hl_: e.tensor_tensor(
                    out=X[k][:, hl_, :, :], in0=g.identb[:, :].unsqueeze(1).broadcast_to([128, 16, 128]),
                    in1=acs[k][:, hl_, :].unsqueeze(2).broadcast_to([128, 16, 128]), op=ALU.mult), R=[g.identb, acs[k]], Wd=[X[k]])
            for q4 in range(4):
                pb = kb.ps()
                for hl_ in range(2):
                    kb.op("pe", lambda e, hl_=hl_: e.matmul(pb[:, :], g.onesb[:], X[k][:, hl_, q4 * 4:(q4 + 1) * 4, :],
                                                            start=(hl_ == 0), stop=(hl_ == 1)), R=[g.onesb, X[k]], W=[pb])
                for hh in range(4):
                    h = q4 * 4 + hh
                    kb.op("dve", lambda e, h=h, hh=hh: e.scalar_tensor_tensor(
                        out=dd[k][:, h, :], in0=pb[:, hh * 128:(hh + 1) * 128], scalar=ac[k][:, h:h + 1], in1=g.negm[:],
                        op0=ALU.subtract, op1=ALU.add), R=[pb, ac[k], g.negm], Wd=[dd[k]])
            kb.op("act", lambda e: e.activation(out=dd[k][:], in_=dd[k][:], func=AF.Exp), R=[dd[k]], W=[dd[k]])
            pcb = kb.ps()
            for gg in range(2):
                kb.op("pe", lambda e, gg=gg: e.matmul(pcb[:, gg * 128:(gg + 1) * 128], g.bcT[:, gg, tsl], g.bcT[:, 2 + gg, tsl],
                                                      start=True, stop=True), R=[g.bcT], W=[pcb])
            kb.op("act", lambda e: e.activation(out=cbT[k][:], in_=pcb[:, 0:256].rearrange("p (g i) -> p g i", g=2), func=AF.Copy),
                  R=[pcb], W=[cbT[k]])
            for h in range(16):
                kb.op("dve", lambda e, h=h: e.scalar_tensor_tensor(
                    out=Mt[k][:, h, :], in0=cbT[k][:, h // 8, :], scalar=g.dt_tok[:, c, h:h + 1], in1=dd[k][:, h, :],
                    op0=ALU.mult, op1=ALU.mult), R=[cbT[k], g.dt_tok, dd[k]], Wd=[Mt[k]])

        def P2(c):
            k = c % NB
            tsl = slice(c * 128, (c + 1) * 128)
            kb.dma("sp", zst[k][:], S["zs"].t[tsl, :], W=[zst[k]])
            for gg in range(2):
                pA = kb.ps()
                for hl in range(8):
                    h = gg * 8 + hl
                    xsl = g.xs_tok[:, c, h * 64:(h + 1) * 64]
                    kb.op("pe", lambda e, h=h, hl=hl, xsl=xsl: e.matmul(pA[:, hl * 64:(hl + 1) * 64], Mt[k][:, h, :], xsl, start=True, stop=False),
                          R=[Mt[k], g.xs_tok], W=[pA])
                    kb.op("pe", lambda e, h=h, hl=hl, xsl=xsl: e.matmul(pA[:, hl * 64:(hl + 1) * 64], Dmat[:, h, :], xsl, start=False, stop=True),
                          R=[Dmat, g.xs_tok], W=[pA])
                ysl = yy[k][:, gg * 512:(gg + 1) * 512]
                if c > 0:
                    pB = kb.ps()
                    pv = prevb[gg][(c - 1) % 2]
                    kb.op("pe", lambda e: e.matmul(pB[:, :], g.bcT[:, 2 + gg, tsl], pv[:], start=True, stop=True), R=[g.bcT, pv], W=[pB])
                    yt = ytmp[gg]
                    kb.op("dve", lambda e: e.tensor_tensor(
                        out=yt[:, :].rearrange("p (h d) -> p h d", d=64), in0=pB[:, :].rearrange("p (h d) -> p h d", d=64),
                        in1=eac[k][:, gg * 8:(gg + 1) * 8].unsqueeze(2).broadcast_to([128, 8, 64]), op=ALU.mult),
                        R=[pB, eac[k]], W=[yt])
                    kb.op("dve", lambda e: e.tensor_tensor(out=ysl, in0=yt[:], in1=pA[:, :], op=ALU.add), R=[yt, pA], W=[yy[k]])
                else:
                    kb.op("act", lambda e: e.activation(out=ysl, in_=pA[:, :], func=AF.Copy), R=[pA], W=[yy[k]])
                if c < NT - 1:
                    pS = kb.ps()
                    kb.op("pe", lambda e: e.matmul(pS[:, :], g.bt_tok[:, c, gg * 128:(gg + 1) * 128], xw[k][:, gg * 512:(gg + 1) * 512],
                                                   start=True, stop=True), R=[g.bt_tok, xw[k]], W=[pS])
                    if c == 0:
                        kb.op("dve", lambda e: e.tensor_copy(out=hst[gg][:], in_=pS[:, :]), R=[pS], W=[hst[gg]])
                    else:
                        kb.op("dve", lambda e: e.tensor_tensor(
                            out=htmp[gg][:, :].rearrange("p (h d) -> p h d", d=64), in0=hst[gg][:, :].rearrange("p (h d) -> p h d", d=64),
                            in1=cd[k][:, gg * 8:(gg + 1) * 8].unsqueeze(2).broadcast_to([128, 8, 64]), op=ALU.mult),
                            R=[hst[gg], cd[k]], W=[htmp[gg]])
                        kb.op("dve", lambda e: e.tensor_tensor(out=hst[gg][:], in0=htmp[gg][:], in1=pS[:, :], op=ALU.add),
                              R=[htmp[gg], pS], W=[hst[gg]])
                    pn = prevb[gg][c % 2]
                    kb.op("act", lambda e: e.activation(out=pn[:], in_=hst[gg][:], func=AF.Copy), R=[hst[gg]], W=[pn])

        def P3(c):
            k = c % NB
            tsl = slice(c * 128, (c + 1) * 128)
            kb.op("dve", lambda e: e.tensor_tensor(out=yz[k][:], in0=yy[k][:], in1=zst[k][:], op=ALU.mult), R=[yy[k], zst[k]], W=[yz[k]])
            kb.op("act", lambda e: e.activation(out=yn[k][:], in_=yz[k][:], func=AF.Square, accum_out=ss[k][:]), R=[yz[k]], W=[yn[k], ss[k]])
            kb.op("dve", lambda e: e.tensor_scalar(out=ss[k][:], in0=ss[k][:], scalar1=1.0 / 1024, scalar2=EPS, op0=ALU.mult, op1=ALU.add),
                  R=[ss[k]], W=[ss[k]])
            kb.op("act", lambda e: e.activation(out=ss[k][:], in_=ss[k][:], func=AF.Ln), R=[ss[k]], W=[ss[k]])
            kb.op("act", lambda e: e.activation(out=ss[k][:], in_=ss[k][:], func=AF.Exp, scale=-0.5), R=[ss[k]], W=[ss[k]])
            kb.op("dve", lambda e: e.scalar_tensor_tensor(out=yn[k][:], in0=yz[k][:], scalar=ss[k][:, 0:1], in1=nw_bc[:],
                                                          op0=ALU.mult, op1=ALU.mult), R=[yz[k], ss[k], nw_bc], W=[yn[k]])
            pt = kb.ps()
            ptb = pt[:, :].bitcast(BF16)
            for kc in range(KC):
                kb.op("pe", lambda e, kc=kc: e.transpose(ptb[:, kc * 128:(kc + 1) * 128], yn[k][:, kc * 128:(kc + 1) * 128], g.identb[:]),
                      R=[yn[k], g.identb], W=[pt])
            st = yst[(c // 4) % 2]
            kb.op("act", lambda e: e.activation(out=st[:, :, (c % 4) * 128:(c % 4 + 1) * 128],
                                                in_=ptb.rearrange("p (kc t) -> p kc t", kc=KC), func=AF.Copy), R=[pt], W=[st])
            if c % 4 == 3:
                tb = c // 4
                kb.dma("sp", S["yT"].t[0:8, :, tb * 512:(tb + 1) * 512].rearrange("kc p t -> p kc t"), st[:], R=[st])

        for step in range(-1, NT + 1):
            if 0 <= step + 1 < NT:
                P1(step + 1)
            if 0 <= step - 1 < NT:
                P3(step - 1)
            if 0 <= step < NT:
                P2(step)
        kb.barrier()


def phase_sb(g, l):
    kb, nc = g.kb, g.nc
    S = g.scr
    with ExitStack() as ph:
        def sb(name, shape, dt_=F32):
            return kb.sb(ph, name, shape, dt_)
        qT = sb("sbq", [128, 4, L], BF16)
        kT = sb("sbk", [128, 4, L], BF16)
        v = sb("sbv", [128, NT, 512], BF16)
        onesw = sb("onesw", [128, L], BF16)
        kb.op("dve", lambda e: e.memset(onesw[:], 1.0), W=[onesw])
        kb.dma("sp", qT[:], S["sbqT"].t.rearrange("c p t -> p c t"), W=[qT])
        kb.dma("sp", kT[:], S["sbkT"].t.rearrange("c p t -> p c t"), W=[kT])
        kb.dma("sp", v[:], S["sbv"].t.rearrange("(t p) f -> p t f", p=128), W=[v])
        e1b = [sb("e1", [128, L]) for _ in range(3)]
        spb = [sb("sp", [128, L]) for _ in range(2)]
        Fb = [sb("F", [128, L + 1]) for _ in range(2)]
        attb = [sb("att", [128, L], BF16) for _ in range(2)]
        attT = [sb("attT", [128, NT, 128], BF16) for _ in range(2)]
        ftn = [sb("ftn", [128, 1]) for _ in range(2)]
        yst = [sb("yst", [128, 128], BF16) for _ in range(4)]
        for F in Fb:
            kb.op("dve", lambda e, F=F: e.memset(F[:, 0:1], 0.0), W=[F])
        iters = [(qt, h) for qt in range(NT) for h in range(8)]

        def s1(i):
            qt, h = iters[i]
            W = 128 * (qt + 1)
            nch = (W + 511) // 512
            dsl = slice(qt * 128, (qt + 1) * 128)
            hp, hc = (h % 2) * 64, h // 2
            e1, sp_ = e1b[i % 3], spb[i % 2]
            for ch in range(nch):
                n = min(512, W - ch * 512)
                p = kb.ps()
                kb.op("pe", lambda e, p=p, ch=ch, n=n: e.matmul(p[:, 0:n], qT[hp:hp + 64, hc, dsl], kT[hp:hp + 64, hc, ch * 512:ch * 512 + n],
                                                                start=True, stop=True), R=[qT, kT], W=[p])
                kb.op("act", lambda e, p=p, ch=ch, n=n: e.activation(out=e1[:, ch * 512:ch * 512 + n], in_=p[:, 0:n], func=AF.Exp, scale=0.125),
                      R=[p], Wd=[e1])
            kb.op("act", lambda e: e.activation(out=sp_[:, 0:W], in_=e1[:, 0:W], func=AF.Ln, bias=1.0), R=[e1], W=[sp_])

        def s2(i):
            qt, h = iters[i]
            k = i % 2
            W = 128 * (qt + 1)
            dsl = slice(qt * 128, (qt + 1) * 128)
            sp_, F = spb[k], Fb[k]
            kb.op("dve", lambda e: e.tensor_tensor(out=sp_[:, dsl], in0=sp_[:, dsl], in1=g.strl[:], op=ALU.mult), R=[sp_, g.strl], W=[sp_])
            kb.op("dve", lambda e: e.memset(F[:, 0:1], 0.0), W=[F])
            kb.op("dve", lambda e: e.tensor_tensor_scan(out=F[:, 1:W + 1], data0=onesw[:, 0:W], data1=sp_[:, 0:W], initial=0.0,
                                                        op0=ALU.mult, op1=ALU.add), R=[onesw, sp_], W=[F])
            kb.op("dve", lambda e: e.tensor_scalar(out=ftn[k][:], in0=F[:, W:W + 1], scalar1=-1.0, scalar2=None, op0=ALU.mult),
                  R=[F], W=[ftn[k]])

        def s3a(i):
            qt, h = iters[i]
            k = i % 2
            W = 128 * (qt + 1)
            F = Fb[k]
            kb.op("act", lambda e: e.activation(out=F[:, 0:W], in_=F[:, 0:W], func=AF.Exp, bias=ftn[k][:, 0:1]), R=[F, ftn[k]], W=[F])

        def s3b(i):
            qt, h = iters[i]
            k = i % 2
            W = 128 * (qt + 1)
            dsl = slice(qt * 128, (qt + 1) * 128)
            e1, F, att = e1b[i % 3], Fb[k], attb[k]
            kb.op("dve", lambda e: e.tensor_tensor(out=att[:, 0:W], in0=e1[:, 0:W], in1=F[:, 0:W], op=ALU.mult), R=[e1, F], W=[att])
            kb.op("dve", lambda e: e.tensor_tensor(out=att[:, dsl], in0=att[:, dsl], in1=g.strlb[:], op=ALU.mult), R=[att, g.strlb], W=[att])

        def s3c(i):
            qt, h = iters[i]
            k = i % 2
            att, aT = attb[k], attT[k]
            for b0 in range(0, qt + 1, 8):
                nb = min(8, qt + 1 - b0)
                pt = kb.ps()
                ptb = pt[:, :].bitcast(BF16)
                for j in range(nb):
                    kb.op("pe", lambda e, j=j, b0=b0: e.transpose(ptb[:, j * 128:(j + 1) * 128], att[:, (b0 + j) * 128:(b0 + j + 1) * 128], g.identb[:]),
                          R=[att, g.identb], W=[pt])
                kb.op("act", lambda e, b0=b0, nb=nb: e.activation(out=aT[:, b0:b0 + nb, :], in_=ptb[:, 0:nb * 128].rearrange("p (j t) -> p j t", j=nb),
                                                                 func=AF.Copy), R=[pt], Wd=[aT])

        def s3d(i):
            qt, h = iters[i]
            k = i % 2
            dsl = slice(qt * 128, (qt + 1) * 128)
            hp, hc = (h % 2) * 64, h // 2
            aT = attT[k]
            py = kb.ps()
            for sbk in range(qt + 1):
                kb.op("pe", lambda e, sbk=sbk: e.matmul(py[0:64, 0:128], v[:, sbk, h * 64:(h + 1) * 64], aT[:, sbk, :],
                                                        start=(sbk == 0), stop=(sbk == qt)), R=[v, aT], W=[py])
            ys = yst[i % 4]
            kb.op("act", lambda e: e.activation(out=ys[hp:hp + 64, :], in_=py[0:64, 0:128], func=AF.Copy), R=[py], W=[ys])
            kb.dma("sp", S["yT"].t[8 + hc, hp:hp + 64, dsl], ys[hp:hp + 64, :], R=[ys])

        n_it = len(iters)

        def run(fn, i):
            if 0 <= i < n_it:
                fn(i)
        for step in range(n_it + 4):
            run(s3d, step - 4)
            run(s3a, step - 2)
            run(s1, step)
            run(s3c, step - 3)
            run(s2, step - 1)
            run(s3b, step - 2)
        kb.barrier()


NBIS = 16


def phase_dsa(g, l):
    kb, nc = g.kb, g.nc
    S = g.scr
    with ExitStack() as ph:
        def sb(name, shape, dt_=F32):
            return kb.sb(ph, name, shape, dt_)
        dk = sb("dk", [64, 2, L], BF16)
        ik = sb("ik", [64, L], BF16)
        vaug = sb("vaug", [128, NT, 2, 128], BF16)
        kb.dma("sp", dk[:], S["dkT"].t.rearrange("h d t -> d h t"), W=[dk])
        kb.dma("sp", ik[:], S["ikT"].t[0], W=[ik])
        kb.op("dve", lambda e: e.memset(vaug[:], 1.0), W=[vaug])
        for gg in range(2):
            kb.dma("sp", vaug[:, :, gg, 0:64], S["dv"].t[:, gg * 64:(gg + 1) * 64].rearrange("(t p) d -> p t d", p=128), W=[vaug])
        dqt = [sb("dqt", [64, 8, 128], BF16) for _ in range(2)]
        iqt = [sb("iqt", [64, 8, 128], BF16) for _ in range(2)]
        score = [sb("score", [128, L]) for _ in range(2)]
        rl = [sb("rl", [128, 512], BF16) for _ in range(4)]
        wabs = [sb("wabs", [128, 8]) for _ in range(2)]
        wsgn = [sb("wsgn", [128, 8]) for _ in range(2)]
        Dg = [sb("Dg", [128, 8, 128], BF16) for _ in range(2)]
        junk = sb("junk", [128, L], BF16)
        maskb = [sb("maskb", [128, L], BF16) for _ in range(2)]
        maskT = [sb("maskT", [128, NT, 128], BF16) for _ in range(2)]
        Eb = [sb("E", [128, 512], BF16) for _ in range(3)]
        M_ = [sb("M", [128, 1]) for _ in range(2)]
        A_ = [sb("A", [128, NBIS + 1]) for _ in range(2)]
        mid = [sb("mid", [128, 1]) for _ in range(2)]
        cnt = [sb("cnt", [128, 1]) for _ in range(2)]
        sela = [sb("sela", [128, 1]) for _ in range(2)]
        R0 = [sb("R0", [64, 512]) for _ in range(2)]
        ytmp = [sb("ytmp", [64, 512], BF16) for _ in range(2)]
        yraw = [sb("yraw", [64, 512]) for _ in range(2)]
        yst = [sb("dyst", [128, 4, 128], BF16) for _ in range(2)]
        cn = dict(rl=0, E=0)

        def stage_I(qt):
            k = qt % 2
            W = 128 * (qt + 1)
            dsl = slice(qt * 128, (qt + 1) * 128)
            if qt < 2:
                return
            kb.dma("sp", iqt[k][:], S["iqT"].t[:, :, dsl].rearrange("h d t -> d h t"), W=[iqt[k]])
            sc = score[k]
            nch = (W + 511) // 512
            wv = g.wix_tok[:, qt, :]
            kb.op("act", lambda e: e.activation(out=wabs[k][:], in_=wv, func=AF.Abs), R=[g.wix_tok], W=[wabs[k]])
            kb.op("dve", lambda e: e.tensor_scalar(out=wsgn[k][:], in0=wv, scalar1=0.0, scalar2=2.0, op0=ALU.is_ge, op1=ALU.mult),
                  R=[g.wix_tok], W=[wsgn[k]])
            kb.op("dve", lambda e: e.tensor_scalar(out=wsgn[k][:], in0=wsgn[k][:], scalar1=-1.0, scalar2=None, op0=ALU.add),
                  R=[wsgn[k]], W=[wsgn[k]])
            for h in range(8):
                kb.op("dve", lambda e, h=h: e.tensor_scalar(out=Dg[k][:, h, :], in0=g.identb[:], scalar1=wsgn[k][:, h:h + 1], scalar2=None,
                                                            op0=ALU.mult), R=[g.identb, wsgn[k]], Wd=[Dg[k]])
            items = [(ch, h) for ch in range(nch) for h in range(8)]
            pend = {}
            pscs = {}

            def mm(j):
                ch, h = items[j]
                n = min(512, W - ch * 512)
                p = kb.ps()
                kb.op("pe", lambda e: e.matmul(p[:, 0:n], iqt[k][0:64, h, :], ik[0:64, ch * 512:ch * 512 + n], start=True, stop=True),
                      R=[iqt[k], ik], W=[p])
                r = rl[cn["rl"] % 4]
                cn["rl"] += 1
                kb.op("act", lambda e: e.activation(out=r[:, 0:n], in_=p[:, 0:n], func=AF.Relu, scale=wabs[k][:, h:h + 1]),
                      R=[p, wabs[k]], W=[r])
                pend[j] = r

            def acc(j):
                ch, h = items[j]
                n = min(512, W - ch * 512)
                r = pend.pop(j)
                if h == 0:
                    pscs[ch] = kb.ps(pin=True)
                psc = pscs[ch]
                kb.op("pe", lambda e: e.matmul(psc[:, 0:n], Dg[k][:, h, :], r[:, 0:n], start=(h == 0), stop=(h == 7)),
                      R=[Dg[k], r], W=[psc])
                if h == 7:
                    kb.unpin(psc)
                    kb.op("act", lambda e: e.activation(out=sc[:, ch * 512:ch * 512 + n], in_=psc[:, 0:n], func=AF.Copy), R=[psc], Wd=[sc])

            for j in range(len(items) + 2):
                if j < len(items):
                    mm(j)
                if 0 <= j - 2 < len(items):
                    acc(j - 2)

        def stage_B(qt):
            k = qt % 2
            W = 128 * (qt + 1)
            dsl = slice(qt * 128, (qt + 1) * 128)
            mT = maskT[k]
            if qt < 2:
                if qt == 1:
                    kb.op("dve", lambda e: e.memset(mT[:, 0, :], 0.0), W=[mT])
                kb.op("dve", lambda e: e.tensor_copy(out=mT[:, qt, :], in_=g.negmb[:]), R=[g.negmb], W=[mT])
                return
            sc = score[k]
            kb.op("dve", lambda e: e.tensor_reduce(out=M_[k][:], in_=sc[:, 0:W], axis=AX.X, op=ALU.max, apply_absolute_value=True),
                  R=[sc], W=[M_[k]])
            kb.op("dve", lambda e: e.tensor_scalar(out=A_[k][:, 0:NBIS], in0=g.pow2[:, 0:NBIS], scalar1=M_[k][:, 0:1], scalar2=None, op0=ALU.mult),
                  R=[g.pow2, M_[k]], W=[A_[k]])
            kb.op("dve", lambda e: e.tensor_copy(out=A_[k][:, NBIS:NBIS + 1], in_=A_[k][:, NBIS - 1:NBIS]), R=[A_[k]], W=[A_[k]])
            kb.op("dve", lambda e: e.tensor_tensor(out=sc[:, dsl], in0=sc[:, dsl], in1=g.negup[:], op=ALU.add), R=[sc, g.negup], W=[sc])
            kb.op("dve", lambda e: e.memset(mid[k][:], 0.0), W=[mid[k]])
            for it in range(NBIS):
                kb.op("dve", lambda e: e.tensor_scalar(out=junk[:, 0:W], in0=sc[:, 0:W], scalar1=mid[k][:, 0:1], scalar2=0.0,
                                                       op0=ALU.is_ge, op1=ALU.add, accum_out=cnt[k][:]),
                      R=[sc, mid[k]], W=[junk, cnt[k]])
                kb.op("dve", lambda e, it=it: e.tensor_scalar(out=sela[k][:], in0=cnt[k][:], scalar1=255.5, scalar2=A_[k][:, it:it + 1],
                                                              op0=ALU.is_ge, op1=ALU.mult), R=[cnt[k], A_[k]], W=[sela[k]])
                kb.op("dve", lambda e, it=it: e.scalar_tensor_tensor(out=mid[k][:], in0=sela[k][:], scalar=A_[k][:, it + 1:it + 2], in1=mid[k][:],
                                                                     op0=ALU.subtract, op1=ALU.add), R=[sela[k], A_[k], mid[k]], W=[mid[k]])
            mb = maskb[k]
            kb.op("dve", lambda e: e.tensor_scalar(out=mb[:, 0:W], in0=sc[:, 0:W], scalar1=mid[k][:, 0:1], scalar2=-30000.0,
                                                   op0=ALU.is_lt, op1=ALU.mult), R=[sc, mid[k]], W=[mb])

        def stage_T(qt):
            if qt < 2:
                return
            k = qt % 2
            mb, mT = maskb[k], maskT[k]
            for b0 in range(0, qt + 1, 8):
                nb = min(8, qt + 1 - b0)
                pt = kb.ps()
                ptb = pt[:, :].bitcast(BF16)
                for j in range(nb):
                    kb.op("pe", lambda e, j=j, b0=b0: e.transpose(ptb[:, j * 128:(j + 1) * 128], mb[:, (b0 + j) * 128:(b0 + j + 1) * 128], g.identb[:]),
                          R=[mb, g.identb], W=[pt])
                kb.op("act", lambda e, b0=b0, nb=nb: e.activation(out=mT[:, b0:b0 + nb, :], in_=ptb[:, 0:nb * 128].rearrange("p (j t) -> p j t", j=nb),
                                                                 func=AF.Copy), R=[pt], Wd=[mT])

        def stage_A(qt):
            k = qt % 2
            dsl = slice(qt * 128, (qt + 1) * 128)
            mT = maskT[k]
            ys = yst[k]
            if qt + 1 < NT:
                kb.dma("sp", dqt[1 - k][:], S["dqT"].t[:, :, (qt + 1) * 128:(qt + 2) * 128].rearrange("h d t -> d h t"), W=[dqt[1 - k]])
            for gq in range(2):
                pO = kb.ps(pin=True)
                Es = {}

                def qk(sbk):
                    pS = kb.ps()
                    kb.op("pe", lambda e: e.matmul(pS[:, :], dk[0:64, gq, sbk * 128:(sbk + 1) * 128], dqt[k][0:64, 4 * gq:4 * gq + 4, :],
                                                   start=True, stop=False), R=[dk, dqt[k]], W=[pS])
                    kb.op("pe", lambda e: e.matmul(pS[:, :], g.identb[:], mT[:, sbk, :].unsqueeze(1).broadcast_to([128, 4, 128]),
                                                   start=False, stop=True), R=[g.identb, mT], W=[pS])
                    E = Eb[cn["E"] % 3]
                    cn["E"] += 1
                    kb.op("act", lambda e: e.activation(out=E[:], in_=pS[:, :], func=AF.Exp, scale=0.125), R=[pS], W=[E])
                    Es[sbk] = E

                def av(sbk):
                    E = Es.pop(sbk)
                    kb.op("pe", lambda e: e.matmul(pO[:, :], vaug[:, sbk, gq, :], E[:], start=(sbk == 0), stop=(sbk == qt)),
                          R=[vaug, E], W=[pO])

                for sbk in range(qt + 2):
                    if sbk <= qt:
                        qk(sbk)
                    if sbk >= 1:
                        av(sbk - 1)
                kb.unpin(pO)
                r0, yt = R0[gq], ytmp[gq]
                kb.op("act", lambda e: e.activation(out=r0[:], in_=pO[64:128, :], func=AF.Ln), R=[pO], W=[r0])
                kb.op("act", lambda e: e.activation(out=r0[:], in_=r0[:], func=AF.Exp, scale=-1.0), R=[r0], W=[r0])
                yr = yraw[gq]
                kb.op("act", lambda e: e.activation(out=yr[:], in_=pO[0:64, :], func=AF.Copy), R=[pO], W=[yr])
                kb.op("pool", lambda e: e.tensor_tensor(out=yt[:], in0=yr[:], in1=r0[:], op=ALU.mult), R=[yr, r0], W=[yt])
                for hh in range(4):
                    h = 4 * gq + hh
                    hp, hc = (h % 2) * 64, h // 2
                    kb.op("act", lambda e, hh=hh, hp=hp, hc=hc: e.activation(out=ys[hp:hp + 64, hc, :], in_=yt[:, hh * 128:(hh + 1) * 128], func=AF.Copy),
                          R=[yt], Wd=[ys])
            kb.dma("sp", S["yT"].t[12:16, :, dsl].rearrange("c p t -> p c t"), ys[:], R=[ys])

        kb.dma("sp", dqt[0][:], S["dqT"].t[:, :, 0:128].rearrange("h d t -> d h t"), W=[dqt[0]])
        stage_I(0)
        stage_B(0)
        stage_T(0)
        stage_I(1)
        modw = [sb("adaw", [128, KC, 512], BF16) for _ in range(2)] if g.dsa_hook is not None else None
        for qt in range(NT):
            if qt + 2 < NT:
                stage_I(qt + 2)
            stage_A(qt)
            if modw is not None and 2 <= qt < 14:
                mod_piece(g, g.dsa_hook, qt - 2, modw[qt % 2])
            if qt + 1 < NT:
                stage_B(qt + 1)
                stage_T(qt + 1)
        kb.barrier()


def phase_merge(g, l):
    kb, nc = g.kb, g.nc
    S = g.scr
    with ExitStack() as ph:
        def sb(name, shape, dt_=F32):
            return kb.sb(ph, name, shape, dt_)
        yT = sb("yTall", [128, 16, L], BF16)
        yTb = [kb.buf() for _ in range(4)]
        for q in range(4):
            kb.dma("sp", yT[:, q * 4:(q + 1) * 4, :], S["yT"].t[q * 4:(q + 1) * 4].rearrange("c p t -> p c t"), W=[yTb[q]])
        mT = sb("mergedT", [128, KC, L], BF16)
        mTb = [[kb.buf() for tb in range(4)] for c in range(KC)]
        wbr = [sb("wbr", [128, 16, 256], BF16) for _ in range(2)]
        gt = [sb("gt", [128, 3, 512], BF16) for _ in range(2)]
        gbr = [sb("gbr", [128, 3, 512], BF16) for _ in range(2)]
        it = 0
        for c2 in range(4):
            w = wbr[c2 % 2]
            csl = slice(c2 * 256, (c2 + 1) * 256)
            kb.dma("pool", w[:, 0:8, :], g.inp["w_br_ssd"].t[l][:, csl].rearrange("(kc p) n -> p kc n", p=128), W=[w])
            kb.dma("pool", w[:, 8:12, :], g.inp["w_br_sb"].t[l][:, csl].rearrange("(kc p) n -> p kc n", p=128), W=[w])
            kb.dma("pool", w[:, 12:16, :], g.inp["w_br_dsa"].t[l][:, csl].rearrange("(kc p) n -> p kc n", p=128), W=[w])
            for cl in range(2):
                c = c2 * 2 + cl
                for tb in range(4):
                    k = it % 2
                    it += 1
                    tsl = slice(tb * 512, (tb + 1) * 512)
                    for br in range(3):
                        kb.dma("sp", gt[k][:, br, :], S["gT"].t[br * 8 + c, :, tsl], W=[gt[k]])
                    pbr = []
                    for br, (k0, k1) in enumerate(((0, 8), (8, 12), (12, 16))):
                        p = kb.ps()
                        for kc in range(k0, k1):
                            kb.op("pe", lambda e, p=p, kc=kc, k0=k0, k1=k1: e.matmul(
                                p[:, :], w[:, kc, cl * 128:(cl + 1) * 128], yT[:, kc, tsl], start=(kc == k0), stop=(kc == k1 - 1)),
                                R=[w, yTb[kc // 4]], W=[p])
                        pbr.append(p)
                    gb = gbr[k]
                    for br in range(3):
                        kb.op("dve", lambda e, br=br: e.tensor_tensor(out=gb[:, br, :], in0=pbr[br][:, :], in1=gt[k][:, br, :], op=ALU.mult),
                              R=[pbr[br], gt[k]], Wd=[gb])
                    pm = kb.ps()
                    for br in range(3):
                        kb.op("pe", lambda e, br=br: e.matmul(pm[:, :], g.identb[:], gb[:, br, :], start=(br == 0), stop=(br == 2)),
                              R=[g.identb, gb], W=[pm])
                    kb.op("act", lambda e: e.activation(out=mT[:, c, tsl], in_=pm[:, :], func=AF.Copy), R=[pm], W=[mTb[c][tb]])
        wo = [sb("wo", [128, KC, 256], BF16) for _ in range(2)]
        for c2 in range(4):
            w = wo[c2 % 2]
            kb.dma("pool", w[:], g.inp["w_out"].t[l][:, c2 * 256:(c2 + 1) * 256].rearrange("(kc p) n -> p kc n", p=128), W=[w])
            for cl in range(2):
                c = c2 * 2 + cl
                for tb in range(4):
                    tsl = slice(tb * 512, (tb + 1) * 512)
                    p = kb.ps()
                    for kc in range(KC):
                        kb.op("pe", lambda e, p=p, kc=kc: e.matmul(p[:, :], w[:, kc, cl * 128:(cl + 1) * 128], mT[:, kc, tsl],
                                                                   start=(kc == 0), stop=(kc == KC - 1)), R=[w, mTb[kc][tb]], W=[p])
                    xs = g.xT[:, c, tsl]
                    kb.op("dve", lambda e, p=p, c=c, xs=xs: e.scalar_tensor_tensor(
                        out=xs, in0=p[:, :], scalar=g.modT[:, 16 + c:17 + c], in1=xs, op0=ALU.mult, op1=ALU.add),
                        R=[p, g.modT, g.xTb[c][tb]], W=[g.xTb[c][tb]])
        kb.barrier()


def phase_mlp(g, l):
    kb = g.kb
    with ExitStack() as ph:
        wup = [kb.sb(ph, f"wup{i}", [128, KC, 512], BF16) for i in range(2)]
        wdn = [kb.sb(ph, f"wdn{i}", [128, 4, D], BF16) for i in range(2)]
        act = [kb.sb(ph, f"mact{i}", [128, 4, L], BF16) for i in range(2)]
        actb = [[[kb.buf() for tb in range(4)] for hc in range(4)] for i in range(2)]
        rl = [kb.sb(ph, f"mrl{i}", [128, 512], F32) for i in range(2)]
        nrl = 0
        for gi in range(8):
            wu, wd, a, ab = wup[gi % 2], wdn[gi % 2], act[gi % 2], actb[gi % 2]
            load_w(g, wu, g.inp["w_up"].t[l][:, gi * 512:(gi + 1) * 512])
            load_w(g, wd, g.inp["w_down"].t[l][gi * 512:(gi + 1) * 512, :])
            for tb in range(4):
                for hc in range(4):
                    p = kb.ps()
                    for kc in range(KC):
                        kb.op("pe", lambda e, p=p, kc=kc, hc=hc: e.matmul(
                            p[:, :], wu[:, kc, hc * 128:(hc + 1) * 128], g.hT[:, kc, tb * 512:(tb + 1) * 512],
                            start=(kc == 0), stop=(kc == KC - 1)), R=[wu, g.hTb[kc][tb]], W=[p])
                    r = rl[nrl % 2]
                    nrl += 1
                    kb.op("act", lambda e, p=p, r=r: e.activation(out=r[:], in_=p[:, :], func=AF.Relu), R=[p], W=[r])
                    kb.op("act", lambda e, r=r, hc=hc: e.activation(out=a[:, hc, tb * 512:(tb + 1) * 512], in_=r[:], func=AF.Square),
                          R=[r], W=[ab[hc][tb]])
            import os
            if os.environ.get("MLP_UP_ONLY"):
                continue
            for tb in range(4):
                for c in range(KC):
                    p = kb.ps()
                    for hc in range(4):
                        kb.op("pe", lambda e, p=p, hc=hc, c=c: e.matmul(
                            p[:, :], wd[:, hc, c * 128:(c + 1) * 128], a[:, hc, tb * 512:(tb + 1) * 512],
                            start=(hc == 0), stop=(hc == 3)), R=[wd, ab[hc][tb]], W=[p])
                    xs = g.xT[:, c, tb * 512:(tb + 1) * 512]
                    kb.op("dve", lambda e, p=p, c=c, xs=xs: e.scalar_tensor_tensor(
                        out=xs, in0=p[:, :], scalar=g.modT[:, 40 + c:41 + c], in1=xs, op0=ALU.mult, op1=ALU.add),
                        R=[p, g.modT, g.xTb[c][tb]], W=[g.xTb[c][tb]])
        kb.barrier()


def rstd_bcast(kb, ph, xT, xTb, tb, onesb, sq, rs):
    p = kb.ps()
    for c in range(KC):
        s = sq[c % 2]
        kb.op("act", lambda e, s=s, c=c: e.activation(out=s[:], in_=xT[:, c, tb * 512:(tb + 1) * 512], func=AF.Square),
              R=[xTb[c][tb]], W=[s])
        kb.op("pe", lambda e, s=s, c=c, p=p: e.matmul(p[:, :], onesb[:], s[:], start=(c == 0), stop=(c == KC - 1)),
              R=[s, onesb], W=[p])
    kb.op("dve", lambda e: e.tensor_scalar(out=rs[:], in0=p[:, :], scalar1=1.0 / D, scalar2=EPS, op0=ALU.mult, op1=ALU.add),
          R=[p], W=[rs])
    kb.op("act", lambda e: e.activation(out=rs[:], in_=rs[:], func=AF.Ln), R=[rs], W=[rs])
    kb.op("act", lambda e: e.activation(out=rs[:], in_=rs[:], func=AF.Exp, scale=-0.5), R=[rs], W=[rs])


def final_norm(kb, nc, xT, xTb, fnw, onesb, ident, out_d):
    with ExitStack() as ph:
        sq = [kb.sb(ph, f"fsq{i}", [128, 512], BF16) for i in range(2)]
        rs = kb.sb(ph, "frs", [128, 512], F32)
        yT = [kb.sb(ph, f"fyT{i}", [128, 512], F32) for i in range(2)]
        ost = [kb.sb(ph, f"fost{i}", [128, 4, D], F32) for i in range(2)]
        for tb in range(4):
            rstd_bcast(kb, ph, xT, xTb, tb, onesb, sq, rs)
            o = ost[tb % 2]
            for c in range(KC):
                y = yT[c % 2]
                kb.op("dve", lambda e, y=y, c=c: e.scalar_tensor_tensor(
                    out=y[:], in0=xT[:, c, tb * 512:(tb + 1) * 512], scalar=fnw[:, c:c + 1], in1=rs[:],
                    op0=ALU.mult, op1=ALU.mult), R=[xTb[c][tb], fnw, rs], W=[y])
                p = kb.ps()
                for j in range(4):
                    kb.op("pe", lambda e, y=y, j=j, p=p: e.transpose(
                        p[:, j * 128:(j + 1) * 128], y[:, j * 128:(j + 1) * 128], ident[:]), R=[y, ident], W=[p])
                src = p[:, :].rearrange("p (j f) -> p j f", j=4)
                dst = o[:, :, c * 128:(c + 1) * 128]
                kb.op("act", lambda e, dst=dst, src=src: e.activation(out=dst, in_=src, func=AF.Copy), R=[p], Wd=[o])
            kb.dma("sp", out_d.t[tb * 512:(tb + 1) * 512, :].rearrange("(j p) d -> p j d", p=128), o[:], R=[o], W=[out_d])
        kb.barrier()


_NC_CACHE = {}


def make_in_maps(inputs):
    nb = inputs["x"].shape[0]
    inputs = dict(inputs)
    inputs.update(host_consts())
    shared = {n: np.ascontiguousarray(np.asarray(inputs[n], dtype=np.float32)) for n in IN_SHAPES if n not in ("x", "c")}
    in_maps = []
    for b in range(nb):
        m = dict(shared)
        m["x"] = np.ascontiguousarray(inputs["x"][b])
        m["c"] = np.ascontiguousarray(inputs["c"][b])
        in_maps.append(m)
    return in_maps


def kernel(**inputs):
    nb = inputs["x"].shape[0]
    if "nc" not in _NC_CACHE:
        _NC_CACHE["nc"] = build()[0]
    nc = _NC_CACHE["nc"]
    res = run_bass_kernel_spmd(nc, make_in_maps(inputs), core_ids=list(range(nb)))
    return np.stack([r["out"] for r in res.results], axis=0)
```

```python
import math
from contextlib import ExitStack

import numpy as np
import concourse.bass as bass
import concourse.mybir as mybir
from concourse.bass_utils import run_bass_kernel_spmd

F32 = mybir.dt.float32
BF16 = mybir.dt.bfloat16
I32 = mybir.dt.int32
AF = mybir.ActivationFunctionType
ALU = mybir.AluOpType
AX = mybir.AxisListType

D = 1024
L = 2048
DEPTH = 2
NT = L // 128
KC = D // 128
EPS = 1e-6
import os
NDS = int(os.environ.get("NDS", "12"))
STRICT = bool(int(os.environ.get("KSTRICT", "1")))


class Buf:
    __slots__ = ("name", "w", "r", "excl")

    def __init__(self, name):
        self.name = name
        self.excl = False
        self.w = None
        self.r = {}


class T:
    def __init__(self, t, buf):
        self.t = t
        self.b = buf

    def __getitem__(self, k):
        return self.t[k]


class KB:
    def __init__(self, nc):
        self.nc = nc
        self.es = ExitStack()
        self.sems = {}
        self.engs = {}
        for name, e in (("pe", nc.tensor), ("act", nc.scalar), ("dve", nc.vector),
                        ("pool", nc.gpsimd), ("sp", nc.sync)):
            key = "s_" + name
            self.sems[key] = self.es.enter_context(nc.semaphore(key))
            self.engs[name] = dict(e=e, key=key, cnt=0, seen={})
        self.dq = {}
        for q in ("sp", "pool"):
            keys = []
            for i in range(NDS):
                key = f"d_{q}{i}"
                self.sems[key] = self.es.enter_context(nc.semaphore(key))
                keys.append(key)
            self.dq[q] = dict(keys=keys, cnt=[0] * NDS, nxt=0)
        self.nbuf = 0
        self.psum = []
        self.ps_next = 0
        self.pinned = set()

    def buf(self, name=None):
        self.nbuf += 1
        return Buf(name or f"b{self.nbuf}")

    def sb(self, es, name, shape, dtype):
        self.nbuf += 1
        name = f"{name}_{self.nbuf}"
        t = es.enter_context(self.nc.sbuf_tensor(name, list(shape), dtype))
        return T(t, self.buf(name))

    def dram(self, name, shape, dtype, kind="Internal"):
        t = self.nc.dram_tensor(name, list(shape), dtype, kind=kind)
        return T(t.ap(), self.buf(name))

    def init_psum(self):
        for i in range(8):
            t = self.es.enter_context(self.nc.psum_tensor(f"ps{i}", [128, 512], F32))
            self.psum.append(T(t, self.buf(f"ps{i}")))
            self.psum[-1].b.excl = True

    def ps(self, pin=False):
        while True:
            i = self.ps_next
            self.ps_next = (self.ps_next + 1) % 8
            if i not in self.pinned:
                break
        if pin:
            self.pinned.add(i)
        return self.psum[i]

    def unpin(self, p):
        self.pinned.discard(self.psum.index(p))

    def _deps(self, own_key, R, W, same_raw, Wd=()):
        deps = {}

        def add(ev):
            if ev is None:
                return
            k, v = ev
            if deps.get(k, 0) < v:
                deps[k] = v

        for t in R:
            b = t.b if isinstance(t, T) else t
            if b.w is not None and (b.w[0] != own_key or same_raw):
                add(b.w)
            if b.excl:
                for k, v in b.r.items():
                    if k != own_key:
                        add((k, v))
        for t in W:
            b = t.b if isinstance(t, T) else t
            if b.w is not None and (b.w[0] != own_key or (STRICT and same_raw)):
                add(b.w)
            for k, v in b.r.items():
                if k != own_key or (STRICT and same_raw):
                    add((k, v))
        for t in Wd:
            b = t.b if isinstance(t, T) else t
            if b.w is not None and b.w[0] != own_key:
                add(b.w)
            for k, v in b.r.items():
                if k != own_key or (STRICT and same_raw):
                    add((k, v))
        return deps

    def _mark(self, ev, R, W):
        k, v = ev
        for t in R:
            b = t.b if isinstance(t, T) else t
            if b.r.get(k, 0) < v:
                b.r[k] = v
        for t in W:
            b = t.b if isinstance(t, T) else t
            b.w = ev
            b.r = {}

    def _wait(self, eng, deps):
        for k, v in deps.items():
            if eng["seen"].get(k, 0) < v:
                eng["e"].wait_ge(self.sems[k], v)
                eng["seen"][k] = v

    def op(self, engname, fn, R=(), W=(), Wd=()):
        eng = self.engs[engname]
        deps = self._deps(eng["key"], R, W, same_raw=(engname != "pe"), Wd=Wd)
        W = list(W) + list(Wd)
        self._wait(eng, deps)
        ins = fn(eng["e"])
        eng["cnt"] += 1
        ins.then_inc(self.sems[eng["key"]], 1)
        self._mark((eng["key"], eng["cnt"]), R, W)
        return ins

    def dma(self, q, out, in_, R=(), W=(), **kw):
        eng = self.engs[q]
        dq = self.dq[q]
        deps = self._deps(None, R, W, same_raw=True)
        i = dq["nxt"]
        dq["nxt"] = (i + 1) % NDS
        key = dq["keys"][i]
        if dq["cnt"][i] > 0:
            deps[key] = max(deps.get(key, 0), dq["cnt"][i])
        self._wait(eng, deps)
        dq["cnt"][i] += 16
        eng["e"].dma_start(out=out, in_=in_, **kw).then_inc(self.sems[key], 16)
        self._mark((key, dq["cnt"][i]), R, W)

    def barrier(self):
        allev = {}
        for name, eng in self.engs.items():
            if eng["cnt"] > 0:
                allev[eng["key"]] = eng["cnt"]
        for q, dq in self.dq.items():
            for key, c in zip(dq["keys"], dq["cnt"]):
                if c > 0:
                    allev[key] = c
        for name, eng in self.engs.items():
            deps = {k: v for k, v in allev.items() if k != eng["key"]}
            self._wait(eng, deps)

    def finish(self, out_bufs):
        eng = self.engs["sp"]
        deps = {}
        for t in out_bufs:
            b = t.b if isinstance(t, T) else t
            if b.w is not None:
                deps[b.w[0]] = max(deps.get(b.w[0], 0), b.w[1])
        self._wait(eng, deps)
        self.barrier()


IN_SHAPES = {
    "x": [L, D], "c": [D], "norm1_w": [DEPTH, D], "ada_w": [DEPTH, D, 6 * D], "ada_b": [DEPTH, 6 * D],
    "w_in": [DEPTH, D, 8536], "conv_w": [DEPTH, 4, 1536], "conv_b": [DEPTH, 1536], "dt_bias": [DEPTH, 16],
    "a_log": [DEPTH, 16], "d_skip": [DEPTH, 16], "ssd_norm_w": [DEPTH, D], "w_br_ssd": [DEPTH, D, D],
    "w_br_sb": [DEPTH, 512, D], "w_br_dsa": [DEPTH, 512, D], "w_out": [DEPTH, D, D], "norm2_w": [DEPTH, D],
    "w_up": [DEPTH, D, 4 * D], "w_down": [DEPTH, 4 * D, D], "final_norm_w": [D],
    "rope_cs": [L, 16],
}


def host_consts():
    half = 8
    inv_freq = np.exp(np.arange(half, dtype=np.float32) * np.float32(-2.0 * math.log(500000.0) / 16)).astype(np.float32)
    ang = np.arange(L, dtype=np.float32)[:, None] * inv_freq[None, :]
    return {"rope_cs": np.concatenate([np.cos(ang), np.sin(ang)], axis=1).astype(np.float32)}


class G:
    pass


def build(nlayers=DEPTH, dbg=(), skip_mixer=False, phases=("mod", "inproj", "ssd", "sb", "dsa", "merge", "norm2", "mlp")):
    nc = bass.Bass("TRN2", target_bir_lowering=False)
    kb = KB(nc)
    es = kb.es
    kb.init_psum()
    g = G()
    g.nc, g.kb, g.es, g.dbg = nc, kb, es, set(dbg)
    g.inp = {n: kb.dram(n, shp, F32, kind="ExternalInput") for n, shp in IN_SHAPES.items()}
    out_d = kb.dram("out", [L, D], F32, kind="ExternalOutput")
    g.dbg_out = {}

    g.ident = ident = kb.sb(es, "ident", [128, 128], F32)
    g.identb = identb = kb.sb(es, "identb", [128, 128], BF16)
    g.onesb = onesb = kb.sb(es, "onesb", [128, 128], BF16)
    g.onesf = kb.sb(es, "onesf", [128, 128], F32)
    g.triu = kb.sb(es, "triu", [128, 128], F32)
    g.negm = kb.sb(es, "negm", [128, 128], F32)
    g.triub = kb.sb(es, "triub", [128, 128], BF16)
    g.strl = kb.sb(es, "strl", [128, 128], F32)
    g.strlb = kb.sb(es, "strlb", [128, 128], BF16)
    g.trilb = kb.sb(es, "trilb", [128, 128], BF16)
    g.negup = kb.sb(es, "negup", [128, 128], F32)
    g.negmb = kb.sb(es, "negmb", [128, 128], BF16)
    g.pow2 = kb.sb(es, "pow2", [128, 24], F32)
    with ExitStack() as tmp:
        coli = kb.sb(tmp, "coli", [128, 128], I32)
        rowi = kb.sb(tmp, "rowi", [128, 1], I32)
        colf = kb.sb(tmp, "colf", [128, 128], F32)
        rowf = kb.sb(tmp, "rowf", [128, 1], F32)
        kb.op("pool", lambda e: e.iota(coli[:], [[1, 128]], base=0, channel_multiplier=0), W=[coli])
        kb.op("pool", lambda e: e.iota(rowi[:], [[0, 1]], base=0, channel_multiplier=1), W=[rowi])
        kb.op("dve", lambda e: e.tensor_copy(out=colf[:], in_=coli[:]), R=[coli], W=[colf])
        kb.op("dve", lambda e: e.tensor_copy(out=rowf[:], in_=rowi[:]), R=[rowi], W=[rowf])
        kb.op("dve", lambda e: e.tensor_scalar(out=ident[:], in0=colf[:], scalar1=rowf[:, 0:1], scalar2=None,
                                               op0=ALU.is_equal), R=[colf, rowf], W=[ident])
        kb.op("dve", lambda e: e.tensor_copy(out=identb[:], in_=ident[:]), R=[ident], W=[identb])
        kb.op("dve", lambda e: e.memset(onesb[:], 1.0), W=[onesb])
        kb.op("dve", lambda e: e.memset(g.onesf[:], 1.0), W=[g.onesf])
        kb.op("dve", lambda e: e.tensor_scalar(out=g.triu[:], in0=colf[:], scalar1=rowf[:, 0:1], scalar2=None, op0=ALU.is_ge),
              R=[colf, rowf], W=[g.triu])
        kb.op("dve", lambda e: e.tensor_scalar(out=g.negm[:], in0=g.triu[:], scalar1=-1.0, scalar2=30000.0, op0=ALU.add, op1=ALU.mult),
              R=[g.triu], W=[g.negm])
        kb.op("dve", lambda e: e.tensor_copy(out=g.triub[:], in_=g.triu[:]), R=[g.triu], W=[g.triub])
        kb.op("dve", lambda e: e.tensor_copy(out=g.negmb[:], in_=g.negm[:]), R=[g.negm], W=[g.negmb])
        kb.op("dve", lambda e: e.tensor_scalar(out=g.strl[:], in0=colf[:], scalar1=rowf[:, 0:1], scalar2=None, op0=ALU.is_lt),
              R=[colf, rowf], W=[g.strl])
        kb.op("dve", lambda e: e.tensor_copy(out=g.strlb[:], in_=g.strl[:]), R=[g.strl], W=[g.strlb])
        kb.op("dve", lambda e: e.tensor_scalar(out=g.trilb[:], in0=colf[:], scalar1=rowf[:, 0:1], scalar2=None, op0=ALU.is_le),
              R=[colf, rowf], W=[g.trilb])
        kb.op("dve", lambda e: e.tensor_scalar(out=g.negup[:], in0=g.trilb[:], scalar1=-1.0, scalar2=1.0e9, op0=ALU.add, op1=ALU.mult),
              R=[g.trilb], W=[g.negup])
        for kk_ in range(24):
            kb.op("dve", lambda e, kk_=kk_: e.memset(g.pow2[:, kk_:kk_ + 1], float(2.0 ** (-kk_))), Wd=[g.pow2])
        kb.barrier()

    g.xT = xT = kb.sb(es, "xT", [128, KC, L], F32)
    g.xTb = xTb = [[kb.buf(f"xT{c}_{tb}") for tb in range(4)] for c in range(KC)]
    g.hTb = [[kb.buf(f"hT{c}_{tb}") for tb in range(4)] for c in range(KC)]

    fnw = load_cols(g, es, "fnw", g.inp["final_norm_w"].t, KC)
    g.ccol = load_cols(g, es, "ccol", g.inp["c"].t, KC)
    g.csilu = kb.sb(es, "csilu", [128, KC], BF16)
    kb.op("act", lambda e: e.activation(out=g.csilu[:], in_=g.ccol[:], func=AF.Silu), R=[g.ccol], W=[g.csilu])
    g.modT_l = [kb.sb(es, "modT", [128, 48], F32) for _ in range(DEPTH)]
    g.modraw = [kb.sb(es, "modraw", [128, 48], F32) for _ in range(DEPTH)]
    g.A1_l = [kb.sb(es, "A1", [128, KC], F32) for _ in range(DEPTH)]
    g.A2_l = [kb.sb(es, "A2", [128, KC], F32) for _ in range(DEPTH)]
    g.dsa_hook = None

    x_in = g.inp["x"]
    with ExitStack() as ph:
        xin = [kb.sb(ph, f"xin{i}", [128, D], F32) for i in range(2)]
        for tt in range(NT):
            xi = xin[tt % 2]
            kb.dma("sp", xi[:], x_in.t[tt * 128:(tt + 1) * 128, :], W=[xi])
            for half in range(2):
                p = kb.ps()
                for j in range(4):
                    c = half * 4 + j
                    kb.op("pe", lambda e, c=c, j=j, p=p, xi=xi: e.transpose(
                        p[:, j * 128:(j + 1) * 128], xi[:, c * 128:(c + 1) * 128], ident[:]),
                        R=[xi, ident], W=[p])
                tb = tt // 4
                wb = [xTb[half * 4 + j][tb] for j in range(4)]
                dst = xT[:, half * 4:half * 4 + 4, tt * 128:(tt + 1) * 128]
                src = p[:, :].rearrange("p (j t) -> p j t", j=4)
                if half == 0:
                    kb.op("act", lambda e, dst=dst, src=src: e.activation(out=dst, in_=src, func=AF.Copy), R=[p], W=wb)
                else:
                    kb.op("dve", lambda e, dst=dst, src=src: e.tensor_copy(out=dst, in_=src), R=[p], W=wb)
        kb.barrier()

    def scr(name, shape, dtype=BF16):
        return kb.dram("scr_" + name, shape, dtype, kind=("ExternalOutput" if name in g.dbg else "Internal"))
    g.scr = dict(zs=scr("zs", [L, 1024]), sbv=scr("sbv", [L, 512]), dv=scr("dv", [L, 128]),
                 dqT=scr("dqT", [8, 64, L]), dkT=scr("dkT", [2, 64, L]), iqT=scr("iqT", [8, 64, L]), ikT=scr("ikT", [1, 64, L]),
                 yT=scr("yT", [16, 128, L]), sbqT=scr("sbqT", [4, 128, L]), sbkT=scr("sbkT", [4, 128, L]), gT=scr("gT", [24, 128, L]))
    for name in g.scr:
        if name in g.dbg:
            g.dbg_out["scr_" + name] = g.scr[name]
    g.ropecs = kb.sb(es, "ropecs", [128, NT, 16], F32)
    kb.dma("sp", g.ropecs[:], g.inp["rope_cs"].t.rearrange("(t p) c -> p t c", p=128), W=[g.ropecs])
    g.dt_tok = kb.sb(es, "dt_tok", [128, NT, 16], F32)
    g.wix_tok = kb.sb(es, "wix_tok", [128, NT, 8], F32)
    g.dtb_bc = kb.sb(es, "dtb_bc", [128, 16], F32)
    g.cb = kb.sb(es, "cb", [128, 12], F32)
    g.cw = [kb.sb(es, f"cw{k}", [128, 12], F32) for k in range(4)]

    for l in range(nlayers):
        g.modT, g.A1, g.A2 = g.modT_l[l], g.A1_l[l], g.A2_l[l]
        if "mod" in phases:
            if l == 0:
                phase_mod(g, l)
            else:
                mod_finish(g, l)
        g.dsa_hook = (l + 1) if (l + 1 < nlayers and "mod" in phases) else None
        dump_sb(g, f"mod{l}", g.modT, [128, 48])
        if not skip_mixer:
            with ExitStack() as ssd_scope:
                g.xs_tok = kb.sb(ssd_scope, "xs_tok", [128, NT, 1024], BF16)
                g.bt_tok = kb.sb(ssd_scope, "bt_tok", [128, NT, 256], BF16)
                g.bcT = kb.sb(ssd_scope, "bcT", [128, 4, L], BF16)
                with nc.allow_non_contiguous_dma(reason="tiny vectors"):
                    kb.dma("sp", g.dtb_bc[:], g.inp["dt_bias"].t[l].partition_broadcast(128), W=[g.dtb_bc])
                    kb.dma("sp", g.cb[:], g.inp["conv_b"].t[l].rearrange("(c p) -> p c", p=128), W=[g.cb])
                    for k in range(4):
                        kb.dma("sp", g.cw[k][:], g.inp["conv_w"].t[l][k].rearrange("(c p) -> p c", p=128), W=[g.cw[k]])
                with ExitStack() as hsc:
                    g.hT = kb.sb(hsc, "hT", [128, KC, L], BF16)
                    phase_norm(g, l, which=1)
                    dump_hT(g, f"h{l}")
                    if "inproj" in phases:
                        phase_inproj(g, l)
                dump_sb(g, f"xs_tok{l}", g.xs_tok, [128, NT, 1024], BF16)
                dump_sb(g, f"bt_tok{l}", g.bt_tok, [128, NT, 256], BF16)
                dump_sb(g, f"bcT{l}", g.bcT, [128, 4, L], BF16)
                dump_sb(g, f"dt_tok{l}", g.dt_tok, [128, NT, 16], F32)
                dump_sb(g, f"wix_tok{l}", g.wix_tok, [128, NT, 8], F32)
                kb.barrier()
                if "ssd" in phases:
                    phase_ssd(g, l)
            if "sb" in phases:
                phase_sb(g, l)
            if "dsa" in phases:
                phase_dsa(g, l)
            if "merge" in phases:
                phase_merge(g, l)
            dump_xT(g, f"xmix{l}")
        with ExitStack() as hsc:
            g.hT = kb.sb(hsc, "hT", [128, KC, L], BF16)
            if "norm2" in phases:
                phase_norm(g, l, which=2)
            if "mlp" in phases:
                phase_mlp(g, l)
        dump_xT(g, f"xout{l}")

    final_norm(kb, nc, xT, xTb, fnw, onesb, ident, out_d)
    kb.finish([out_d] + list(g.dbg_out.values()))
    return nc, sorted(g.dbg_out.keys())


def load_cols(g, es_, name, src_ap, n):
    kb = g.kb
    t = kb.sb(es_, name, [128, n], F32)
    with g.nc.allow_non_contiguous_dma(reason="tiny per-feature vector load"):
        for c0 in range(0, n, 8):
            c1 = min(n, c0 + 8)
            kb.dma("sp", t[:, c0:c1], src_ap[c0 * 128:c1 * 128].rearrange("(c p) -> p c", p=128), W=[t])
    return t


def dump_sb(g, name, t, shape, dtype=F32):
    if name not in g.dbg:
        return
    d = g.kb.dram("dbg_" + name, shape, dtype, kind="ExternalOutput")
    g.kb.barrier()
    g.kb.dma("sp", d.t, t[:], R=[t], W=[d])
    g.dbg_out["dbg_" + name] = d


def dump_xT(g, name):
    if name not in g.dbg:
        return
    d = g.kb.dram("dbg_" + name, [128, KC, L], F32, kind="ExternalOutput")
    g.kb.barrier()
    g.kb.dma("sp", d.t, g.xT[:], R=[b for row in g.xTb for b in row], W=[d])
    g.dbg_out["dbg_" + name] = d


def dump_hT(g, name):
    if name not in g.dbg:
        return
    d = g.kb.dram("dbg_" + name, [128, KC, L], BF16, kind="ExternalOutput")
    g.kb.barrier()
    g.kb.dma("sp", d.t, g.hT[:], R=[b for row in g.hTb for b in row], W=[d])
    g.dbg_out["dbg_" + name] = d


def load_w(g, t, src_rows_ap):
    g.kb.dma("pool", t[:], src_rows_ap.rearrange("(kc p) n -> p kc n", p=128), W=[t])


def mod_piece_load(g, l, j4, w):
    g.kb.dma("pool", w[:], g.inp["ada_w"].t[l][:, j4 * 512:(j4 + 1) * 512].rearrange("(kc p) n -> p kc n", p=128), W=[w])


def mod_piece_mm(g, l, j4, w):
    kb = g.kb
    p = kb.ps()
    for jl in range(4):
        for kc in range(KC):
            kb.op("pe", lambda e, jl=jl, kc=kc: e.matmul(p[:, jl:jl + 1], w[:, kc, jl * 128:(jl + 1) * 128], g.csilu[:, kc:kc + 1],
                                                         start=(kc == 0), stop=(kc == KC - 1)), R=[w, g.csilu], W=[p])
    raw = g.modraw[l]
    kb.op("act", lambda e: e.activation(out=raw[:, j4 * 4:(j4 + 1) * 4], in_=p[:, 0:4], func=AF.Copy), R=[p], Wd=[raw])


def mod_piece(g, l, j4, w):
    mod_piece_load(g, l, j4, w)
    mod_piece_mm(g, l, j4, w)


def mod_finish(g, l):
    kb = g.kb
    with ExitStack() as ph:
        adab = load_cols(g, ph, "adab", g.inp["ada_b"].t[l], 48)
        n1w = load_cols(g, ph, "n1w", g.inp["norm1_w"].t[l], KC)
        n2w = load_cols(g, ph, "n2w", g.inp["norm2_w"].t[l], KC)
        modT, A1, A2 = g.modT_l[l], g.A1_l[l], g.A2_l[l]
        kb.op("dve", lambda e: e.tensor_tensor(out=modT[:], in0=g.modraw[l][:], in1=adab[:], op=ALU.add),
              R=[g.modraw[l], adab], W=[modT])
        kb.op("dve", lambda e: e.scalar_tensor_tensor(out=A1[:], in0=modT[:, 8:16], scalar=1.0, in1=n1w[:],
                                                      op0=ALU.add, op1=ALU.mult), R=[modT, n1w], W=[A1])
        kb.op("dve", lambda e: e.scalar_tensor_tensor(out=A2[:], in0=modT[:, 32:40], scalar=1.0, in1=n2w[:],
                                                      op0=ALU.add, op1=ALU.mult), R=[modT, n2w], W=[A2])
        kb.barrier()


def phase_mod(g, l):
    kb = g.kb
    with ExitStack() as ph:
        wt = [kb.sb(ph, f"adaw{i}", [128, KC, 512], BF16) for i in range(3)]
        for j4 in range(12):
            mod_piece(g, l, j4, wt[j4 % 3])
        kb.barrier()
    mod_finish(g, l)


def phase_norm(g, l, which):
    kb = g.kb
    A = g.A1 if which == 1 else g.A2
    sh0 = 0 if which == 1 else 24
    with ExitStack() as ph:
        sq = [kb.sb(ph, f"nsq{i}", [128, 512], BF16) for i in range(2)]
        rs = kb.sb(ph, "nrs", [128, 512], F32)
        tmp = [kb.sb(ph, f"ntmp{i}", [128, 512], F32) for i in range(2)]
        for tb in range(4):
            rstd_bcast(kb, ph, g.xT, g.xTb, tb, g.onesb, sq, rs)
            for c in range(KC):
                t = tmp[c % 2]
                kb.op("dve", lambda e, t=t, c=c: e.tensor_tensor(out=t[:], in0=g.xT[:, c, tb * 512:(tb + 1) * 512], in1=rs[:],
                                                                 op=ALU.mult), R=[g.xTb[c][tb], rs], W=[t])
                kb.op("act", lambda e, t=t, c=c: e.activation(
                    out=g.hT[:, c, tb * 512:(tb + 1) * 512], in_=t[:], func=AF.Identity,
                    scale=A[:, c:c + 1], bias=g.modT[:, sh0 + c:sh0 + c + 1]), R=[t, A, g.modT], W=[g.hTb[c][tb]])
        kb.barrier()


OFF = dict(z=0, xbc=1024, dt=2560, sbq=2576, sbk=3088, sbv=3600, dsq=4112, dsk=4624, dsv=4752,
           ixq=4880, ixk=5392, ixw=5456, gate=5464)


def rope(g, f3, nh, tt, rt):
    kb = g.kb
    fb = f3._buf
    cos = g.ropecs[:, tt:tt + 1, 0:8].broadcast_to([128, nh, 8])
    sin = g.ropecs[:, tt:tt + 1, 8:16].broadcast_to([128, nh, 8])
    x1 = f3.ap[:, :, 0:8]
    x2 = f3.ap[:, :, 8:16]
    t = [rt[:, i, 0:nh, :] for i in range(4)]
    kb.op("dve", lambda e: e.tensor_tensor(out=t[0], in0=x1, in1=cos, op=ALU.mult), R=[fb, g.ropecs], W=[rt])
    kb.op("dve", lambda e: e.tensor_tensor(out=t[1], in0=x2, in1=sin, op=ALU.mult), R=[fb, g.ropecs], W=[rt])
    kb.op("dve", lambda e: e.tensor_tensor(out=t[2], in0=x2, in1=cos, op=ALU.mult), R=[fb, g.ropecs], W=[rt])
    kb.op("dve", lambda e: e.tensor_tensor(out=t[3], in0=x1, in1=sin, op=ALU.mult), R=[fb, g.ropecs], W=[rt])
    kb.op("dve", lambda e: e.tensor_tensor(out=x1, in0=t[0], in1=t[1], op=ALU.subtract), R=[rt], W=[fb])
    kb.op("dve", lambda e: e.tensor_tensor(out=x2, in0=t[2], in1=t[3], op=ALU.add), R=[rt], W=[fb])


class V:
    def __init__(self, ap, buf):
        self.ap = ap
        self._buf = buf


def phase_inproj(g, l):
    kb, nc = g.kb, g.nc
    win = g.inp["w_in"].t[l]
    S = g.scr
    with ExitStack() as ph:
        wt = [kb.sb(ph, "winw", [128, KC, 512], BF16) for _ in range(2)]
        nw = [0]

        def getw(n0, N):
            w = wt[nw[0] % 2]
            nw[0] += 1
            kb.dma("pool", w[:, :, 0:N], win[:, n0:n0 + N].rearrange("(kc p) n -> p kc n", p=128), W=[w])
            return w

        stgb = [kb.sb(ph, "stgb", [128, 512], BF16) for _ in range(3)]
        stgf = [kb.sb(ph, "stgf", [128, 512], F32) for _ in range(2)]
        rt = kb.sb(ph, "ropetmp", [128, 4, 16, 8], F32)
        tst = [kb.sb(ph, "tst", [64, 8, 512], BF16) for _ in range(2)]
        cnt = dict(b=0, f=0, t=0)

        def nxt(lst, k):
            r = lst[cnt[k] % len(lst)]
            cnt[k] += 1
            return r

        def tok_mm(w, N, tt):
            p = kb.ps()
            for kc in range(KC):
                kb.op("pe", lambda e, kc=kc: e.matmul(p[:, 0:N], g.hT[:, kc, tt * 128:(tt + 1) * 128], w[:, kc, 0:N],
                                                      start=(kc == 0), stop=(kc == KC - 1)),
                      R=[g.hTb[kc][tt // 4], w], W=[p])
            return p

        def feat_mm(w, ci, tb):
            p = kb.ps()
            for kc in range(KC):
                kb.op("pe", lambda e, kc=kc: e.matmul(p[:, :], w[:, kc, ci * 128:(ci + 1) * 128],
                                                      g.hT[:, kc, tb * 512:(tb + 1) * 512],
                                                      start=(kc == 0), stop=(kc == KC - 1)),
                      R=[g.hTb[kc][tb], w], W=[p])
            return p

        for zi in range(2):
            w = getw(OFF["z"] + zi * 512, 512)
            for tt in range(NT):
                p = tok_mm(w, 512, tt)
                sb_ = nxt(stgb, "b")
                kb.op("act", lambda e: e.activation(out=sb_[:], in_=p[:, :], func=AF.Silu), R=[p], W=[sb_])
                kb.dma("sp", S["zs"].t[tt * 128:(tt + 1) * 128, zi * 512:(zi + 1) * 512], sb_[:], R=[sb_])
        w = getw(OFF["dt"], 16)
        for tt in range(NT):
            p = tok_mm(w, 16, tt)
            f = nxt(stgf, "f")
            kb.op("dve", lambda e: e.tensor_tensor(out=f[:, 0:16], in0=p[:, 0:16], in1=g.dtb_bc[:], op=ALU.add),
                  R=[p, g.dtb_bc], W=[f])
            kb.op("act", lambda e: e.activation(out=f[:, 0:16], in_=f[:, 0:16], func=AF.Exp), R=[f], W=[f])
            kb.op("act", lambda e: e.activation(out=g.dt_tok[:, tt, :], in_=f[:, 0:16], func=AF.Ln, bias=1.0),
                  R=[f], W=[g.dt_tok])
        w = getw(OFF["sbv"], 512)
        for tt in range(NT):
            p = tok_mm(w, 512, tt)
            sb_ = nxt(stgb, "b")
            kb.op("act", lambda e: e.activation(out=sb_[:], in_=p[:, :], func=AF.Copy), R=[p], W=[sb_])
            kb.dma("sp", S["sbv"].t[tt * 128:(tt + 1) * 128, :], sb_[:], R=[sb_])

        def roped_group(n0, N, nh, dstT, extra=None):
            w = getw(n0, N)
            fs, bs, sts = {}, {}, {}

            def stA(tt):
                p = tok_mm(w, N, tt)
                f = nxt(stgf, "f")
                kb.op("act", lambda e: e.activation(out=f[:, 0:N], in_=p[:, 0:N], func=AF.Copy), R=[p], W=[f])
                fs[tt] = f

            def stB(tt):
                f = fs.pop(tt)
                rope(g, V(f[:, 0:nh * 64].rearrange("p (h d) -> p h d", d=64), f), nh, tt, rt)
                b = nxt(stgb, "b")
                kb.op("act", lambda e: e.activation(out=b[:, 0:N], in_=f[:, 0:N], func=AF.Copy), R=[f], W=[b])
                if extra is not None:
                    extra(tt, f, b)
                bs[tt] = b

            def stC(tt):
                b = bs.pop(tt)
                pt = kb.ps()
                ptb = pt[:, :].bitcast(BF16)
                for h in range(nh):
                    kb.op("pe", lambda e, h=h: e.transpose(ptb[0:64, h * 128:(h + 1) * 128], b[:, h * 64:(h + 1) * 64], g.identb[:]),
                          R=[b, g.identb], W=[pt])
                if tt % 4 == 0:
                    sts[tt // 4] = nxt(tst, "t")
                st = sts[tt // 4]
                kb.op("act", lambda e: e.activation(
                    out=st[0:64, 0:nh, (tt % 4) * 128:(tt % 4 + 1) * 128],
                    in_=ptb[0:64, 0:nh * 128].rearrange("p (h t) -> p h t", h=nh), func=AF.Copy), R=[pt], Wd=[st])
                if tt % 4 == 3:
                    tb = tt // 4
                    kb.dma("sp", dstT.t[:, :, tb * 512:(tb + 1) * 512].rearrange("h d t -> d h t"), st[0:64, 0:nh, :], R=[st])

            for step in range(NT + 2):
                if step < NT:
                    stA(step)
                if 0 <= step - 1 < NT:
                    stB(step - 1)
                if 0 <= step - 2 < NT:
                    stC(step - 2)

        roped_group(OFF["dsq"], 512, 8, S["dqT"])

        def dsv_extra(tt, f, b):
            kb.dma("sp", S["dv"].t[tt * 128:(tt + 1) * 128, :], b[:, 128:256], R=[b])
        roped_group(OFF["dsk"], 256, 2, S["dkT"], dsv_extra)
        roped_group(OFF["ixq"], 512, 8, S["iqT"])

        def ixw_extra(tt, f, b):
            kb.op("dve", lambda e: e.tensor_copy(out=g.wix_tok[:, tt, :], in_=f[:, 64:72]), R=[f], W=[g.wix_tok])
        roped_group(OFF["ixk"], 72, 1, S["ikT"], ixw_extra)

        def feat_group(n0, dstT, c0, func):
            w = getw(n0, 512)
            for ci in range(4):
                for tb in range(4):
                    p = feat_mm(w, ci, tb)
                    sb_ = nxt(stgb, "b")
                    kb.op("act", lambda e: e.activation(out=sb_[:], in_=p[:, :], func=func), R=[p], W=[sb_])
                    kb.dma("sp", dstT.t[c0 + ci, :, tb * 512:(tb + 1) * 512], sb_[:], R=[sb_])

        feat_group(OFF["sbq"], S["sbqT"], 0, AF.Copy)
        feat_group(OFF["sbk"], S["sbkT"], 0, AF.Copy)
        for gi in range(6):
            feat_group(OFF["gate"] + gi * 512, S["gT"], gi * 4, AF.Sigmoid)

        kb.barrier()

    with ExitStack() as ph:
        wt = [kb.sb(ph, "winw", [128, KC, 512], BF16) for _ in range(2)]
        nw = [0]
        xpad = [kb.sb(ph, "xpad", [128, 3 + L], F32) for _ in range(1)]
        cva = [kb.sb(ph, "cva", [128, L], F32) for _ in range(1)]
        cvo = [kb.sb(ph, "cvo", [128, L], BF16) for _ in range(2)]
        for xp in xpad:
            kb.op("dve", lambda e, xp=xp: e.memset(xp[:, 0:3], 0.0), W=[xp])
        for gi in range(3):
            w = getw(OFF["xbc"] + gi * 512, 512)
            for ci in range(4):
                cidx = gi * 4 + ci
                xp, acc = xpad[0], cva[0]
                for tb in range(4):
                    p = feat_mm(w, ci, tb)
                    kb.op("act", lambda e: e.activation(out=xp[:, 3 + tb * 512:3 + (tb + 1) * 512], in_=p[:, :], func=AF.Copy),
                          R=[p], Wd=[xp])
                kb.op("act", lambda e: e.activation(out=acc[:], in_=xp[:, 3:3 + L], func=AF.Identity,
                                                    scale=g.cw[3][:, cidx:cidx + 1], bias=g.cb[:, cidx:cidx + 1]),
                      R=[xp, g.cw[3], g.cb], W=[acc])
                for k in (2, 1, 0):
                    kb.op("dve", lambda e, k=k: e.scalar_tensor_tensor(out=acc[:], in0=xp[:, k:k + L], scalar=g.cw[k][:, cidx:cidx + 1],
                                                                       in1=acc[:], op0=ALU.mult, op1=ALU.add),
                          R=[xp, g.cw[k], acc], W=[acc])
                if cidx < 8:
                    o = cvo[cidx % 2]
                    ov = o[:, :]
                else:
                    o = g.bcT
                    ov = g.bcT[:, cidx - 8, :]
                kb.op("act", lambda e: e.activation(out=ov, in_=acc[:], func=AF.Silu), R=[acc], W=[o])
                if cidx < 10:
                    for half in range(2):
                        pt = kb.ps()
                        ptb = pt[:, :].bitcast(BF16)
                        for j in range(8):
                            tt = half * 8 + j
                            kb.op("pe", lambda e, j=j, tt=tt: e.transpose(ptb[:, j * 128:(j + 1) * 128], ov[:, tt * 128:(tt + 1) * 128],
                                                                          g.identb[:]), R=[o, g.identb], W=[pt])
                        if cidx < 8:
                            dst, dt_ = g.xs_tok[:, half * 8:(half + 1) * 8, cidx * 128:(cidx + 1) * 128], g.xs_tok
                        else:
                            dst, dt_ = g.bt_tok[:, half * 8:(half + 1) * 8, (cidx - 8) * 128:(cidx - 7) * 128], g.bt_tok
                        kb.op("dve", lambda e: e.tensor_copy(out=dst, in_=ptb.rearrange("p (j f) -> p j f", j=8)), R=[pt], W=[dt_])
        kb.barrier()


def phase_ssd(g, l):
    kb, nc = g.kb, g.nc
    S = g.scr
    with ExitStack() as ph:
        def sb(name, shape, dt_=F32):
            return kb.sb(ph, name, shape, dt_)
        a_bc = sb("a_bc", [128, 16])
        dsk_bc = sb("dsk_bc", [128, 16])
        nw_bc = sb("nw_bc", [128, 1024])
        Dmat = sb("Dmat", [128, 16, 128], BF16)
        with nc.allow_non_contiguous_dma(reason="tiny vectors"):
            kb.dma("sp", a_bc[:], g.inp["a_log"].t[l].partition_broadcast(128), W=[a_bc])
            kb.dma("sp", dsk_bc[:], g.inp["d_skip"].t[l].partition_broadcast(128), W=[dsk_bc])
            kb.dma("sp", nw_bc[:], g.inp["ssd_norm_w"].t[l].partition_broadcast(128), W=[nw_bc])
        kb.op("act", lambda e: e.activation(out=a_bc[:], in_=a_bc[:], func=AF.Exp), R=[a_bc], W=[a_bc])
        kb.op("dve", lambda e: e.tensor_scalar(out=a_bc[:], in0=a_bc[:], scalar1=-1.0, scalar2=None, op0=ALU.mult), R=[a_bc], W=[a_bc])
        for h in range(16):
            kb.op("dve", lambda e, h=h: e.tensor_scalar(out=Dmat[:, h, :], in0=g.ident[:], scalar1=dsk_bc[:, h:h + 1], scalar2=None,
                                                        op0=ALU.mult), R=[g.ident, dsk_bc], Wd=[Dmat])
        NB = 2
        dA = [sb("dA", [128, 16]) for _ in range(NB)]
        ac = [sb("ac", [128, 16]) for _ in range(NB)]
        eac = [sb("eac", [128, 16]) for _ in range(NB)]
        wgt = [sb("wgt", [128, 16]) for _ in range(NB)]
        cd = [sb("cd", [128, 16]) for _ in range(NB)]
        X = [sb("X", [128, 2, 16, 128], BF16)] * 2
        dAs = [sb("dAs", [128, 2, 16], BF16) for _ in range(NB)]
        acs = [sb("acs", [128, 2, 16], BF16) for _ in range(NB)]
        dd = [sb("dd", [128, 16, 128])] * 2
        Mt = [sb("Mt", [128, 16, 128], BF16) for _ in range(NB)]
        cbT = [sb("cbT", [128, 2, 128]) for _ in range(NB)]
        xw = [sb("xw", [128, 1024], BF16) for _ in range(NB)]
        zst = [sb("zst", [128, 1024], BF16) for _ in range(NB)]
        yy = [sb("yy", [128, 1024])] * 2
        yz = [sb("yz", [128, 1024])] * 2
        ss = [sb("ss", [128, 1]) for _ in range(NB)]
        yn = [sb("yn", [128, 1024], BF16) for _ in range(NB)]
        yst = [sb("yst", [128, KC, 512], BF16)] * 2
        hst = [sb("hst", [128, 512]) for _ in range(2)]
        htmp = [sb("htmp", [128, 512]) for _ in range(2)]
        prevb = [[sb("prevb", [128, 512], BF16) for _ in range(2)] for _ in range(2)]
        ytmp = [sb("ytmp", [128, 512]) for _ in range(2)]

        def P1(c):
            k = c % NB
            tsl = slice(c * 128, (c + 1) * 128)
            kb.op("dve", lambda e: e.tensor_tensor(out=dA[k][:], in0=g.dt_tok[:, c, :], in1=a_bc[:], op=ALU.mult),
                  R=[g.dt_tok, a_bc], W=[dA[k]])
            kb.op("dve", lambda e: e.tensor_copy(out=dAs[k][:, 0, :], in_=dA[k][:]), R=[dA[k]], W=[dAs[k]])
            kb.op("dve", lambda e: e.tensor_tensor(out=dAs[k][:, 1, :], in0=dA[k][:], in1=dAs[k][:, 0, :], op=ALU.subtract),
                  R=[dA[k], dAs[k]], W=[dAs[k]])
            p_ac = kb.ps()
            for hl_ in range(2):
                kb.op("pe", lambda e, hl_=hl_: e.matmul(p_ac[:, 0:16], g.triub[:], dAs[k][:, hl_, :], start=(hl_ == 0), stop=(hl_ == 1)),
                      R=[g.triub, dAs[k]], W=[p_ac])
            for hl_ in range(2):
                kb.op("pe", lambda e, hl_=hl_: e.matmul(p_ac[:, 16:32], g.onesb[:], dAs[k][:, hl_, :], start=(hl_ == 0), stop=(hl_ == 1)),
                      R=[g.onesb, dAs[k]], W=[p_ac])
            kb.op("dve", lambda e: e.tensor_copy(out=ac[k][:], in_=p_ac[:, 0:16]), R=[p_ac], W=[ac[k]])
            kb.op("act", lambda e: e.activation(out=eac[k][:], in_=p_ac[:, 0:16], func=AF.Exp), R=[p_ac], W=[eac[k]])
            kb.op("act", lambda e: e.activation(out=cd[k][:], in_=p_ac[:, 16:32], func=AF.Exp), R=[p_ac], W=[cd[k]])
            kb.op("dve", lambda e: e.tensor_tensor(out=wgt[k][:], in0=p_ac[:, 16:32], in1=ac[k][:], op=ALU.subtract),
                  R=[p_ac, ac[k]], W=[wgt[k]])
            kb.op("act", lambda e: e.activation(out=wgt[k][:], in_=wgt[k][:], func=AF.Exp), R=[wgt[k]], W=[wgt[k]])
            kb.op("dve", lambda e: e.tensor_tensor(out=wgt[k][:], in0=wgt[k][:], in1=g.dt_tok[:, c, :], op=ALU.mult),
                  R=[wgt[k], g.dt_tok], W=[wgt[k]])
            kb.op("dve", lambda e: e.tensor_tensor(
                out=xw[k][:, :].rearrange("p (h d) -> p h d", d=64), in0=g.xs_tok[:, c, :].rearrange("p (h d) -> p h d", d=64),
                in1=wgt[k][:, :].unsqueeze(2).broadcast_to([128, 16, 64]), op=ALU.mult), R=[g.xs_tok, wgt[k]], W=[xw[k]])
            kb.op("dve", lambda e: e.tensor_copy(out=acs[k][:, 0, :], in_=ac[k][:]), R=[ac[k]], W=[acs[k]])
            kb.op("dve", lambda e: e.tensor_tensor(out=acs[k][:, 1, :], in0=ac[k][:], in1=acs[k][:, 0, :], op=ALU.subtract),
                  R=[ac[k], acs[k]], W=[acs[k]])
            for hl_ in range(2):
                kb.op("dve", lambda e, hl_=hl_: e.tensor_tensor(
                    out=X[k][:, hl_, :, :], in0=g.identb[:, :].unsqueeze(1).broadcast_to([128, 16, 128]),
                    in1=acs[k][:, hl_, :].unsqueeze(2).broadcast_to([128, 16, 128]), op=ALU.mult), R=[g.identb, acs[k]], Wd=[X[k]])
            for q4 in range(4):
                pb = kb.ps()
                for hl_ in range(2):
                    kb.op("pe", lambda e, hl_=hl_: e.matmul(pb[:, :], g.onesb[:], X[k][:, hl_, q4 * 4:(q4 + 1) * 4, :],
                                                            start=(hl_ == 0), stop=(hl_ == 1)), R=[g.onesb, X[k]], W=[pb])
                for hh in range(4):
                    h = q4 * 4 + hh
                    kb.op("dve", lambda e, h=h, hh=hh: e.scalar_tensor_tensor(
                        out=dd[k][:, h, :], in0=pb[:, hh * 128:(hh + 1) * 128], scalar=ac[k][:, h:h + 1], in1=g.negm[:],
                        op0=ALU.subtract, op1=ALU.add), R=[pb, ac[k], g.negm], Wd=[dd[k]])
            kb.op("act", lambda e: e.activation(out=dd[k][:], in_=dd[k][:], func=AF.Exp), R=[dd[k]], W=[dd[k]])
            pcb = kb.ps()
            for gg in range(2):
                kb.op("pe", lambda e, gg=gg: e.matmul(pcb[:, gg * 128:(gg + 1) * 128], g.bcT[:, gg, tsl], g.bcT[:, 2 + gg, tsl],
                                                      start=True, stop=True), R=[g.bcT], W=[pcb])
            kb.op("act", lambda e: e.activation(out=cbT[k][:], in_=pcb[:, 0:256].rearrange("p (g i) -> p g i", g=2), func=AF.Copy),
                  R=[pcb], W=[cbT[k]])
            for h in range(16):
                kb.op("dve", lambda e, h=h: e.scalar_tensor_tensor(
                    out=Mt[k][:, h, :], in0=cbT[k][:, h // 8, :], scalar=g.dt_tok[:, c, h:h + 1], in1=dd[k][:, h, :],
                    op0=ALU.mult, op1=ALU.mult), R=[cbT[k], g.dt_tok, dd[k]], Wd=[Mt[k]])

        def P2(c):
            k = c % NB
            tsl = slice(c * 128, (c + 1) * 128)
            kb.dma("sp", zst[k][:], S["zs"].t[tsl, :], W=[zst[k]])
            for gg in range(2):
                pA = kb.ps()
                for hl in range(8):
                    h = gg * 8 + hl
                    xsl = g.xs_tok[:, c, h * 64:(h + 1) * 64]
                    kb.op("pe", lambda e, h=h, hl=hl, xsl=xsl: e.matmul(pA[:, hl * 64:(hl + 1) * 64], Mt[k][:, h, :], xsl, start=True, stop=False),
                          R=[Mt[k], g.xs_tok], W=[pA])
                    kb.op("pe", lambda e, h=h, hl=hl, xsl=xsl: e.matmul(pA[:, hl * 64:(hl + 1) * 64], Dmat[:, h, :], xsl, start=False, stop=True),
                          R=[Dmat, g.xs_tok], W=[pA])
                ysl = yy[k][:, gg * 512:(gg + 1) * 512]
                if c > 0:
                    pB = kb.ps()
                    pv = prevb[gg][(c - 1) % 2]
                    kb.op("pe", lambda e: e.matmul(pB[:, :], g.bcT[:, 2 + gg, tsl], pv[:], start=True, stop=True), R=[g.bcT, pv], W=[pB])
                    yt = ytmp[gg]
                    kb.op("dve", lambda e: e.tensor_tensor(
                        out=yt[:, :].rearrange("p (h d) -> p h d", d=64), in0=pB[:, :].rearrange("p (h d) -> p h d", d=64),
                        in1=eac[k][:, gg * 8:(gg + 1) * 8].unsqueeze(2).broadcast_to([128, 8, 64]), op=ALU.mult),
                        R=[pB, eac[k]], W=[yt])
                    kb.op("dve", lambda e: e.tensor_tensor(out=ysl, in0=yt[:], in1=pA[:, :], op=ALU.add), R=[yt, pA], W=[yy[k]])
                else:
                    kb.op("act", lambda e: e.activation(out=ysl, in_=pA[:, :], func=AF.Copy), R=[pA], W=[yy[k]])
                if c < NT - 1:
                    pS = kb.ps()
                    kb.op("pe", lambda e: e.matmul(pS[:, :], g.bt_tok[:, c, gg * 128:(gg + 1) * 128], xw[k][:, gg * 512:(gg + 1) * 512],
                                                   start=True, stop=True), R=[g.bt_tok, xw[k]], W=[pS])
                    if c == 0:
                        kb.op("dve", lambda e: e.tensor_copy(out=hst[gg][:], in_=pS[:, :]), R=[pS], W=[hst[gg]])
                    else:
                        kb.op("dve", lambda e: e.tensor_tensor(
                            out=htmp[gg][:, :].rearrange("p (h d) -> p h d", d=64), in0=hst[gg][:, :].rearrange("p (h d) -> p h d", d=64),
                            in1=cd[k][:, gg * 8:(gg + 1) * 8].unsqueeze(2).broadcast_to([128, 8, 64]), op=ALU.mult),
                            R=[hst[gg], cd[k]], W=[htmp[gg]])
                        kb.op("dve", lambda e: e.tensor_tensor(out=hst[gg][:], in0=htmp[gg][:], in1=pS[:, :], op=ALU.add),
                              R=[htmp[gg], pS], W=[hst[gg]])
                    pn = prevb[gg][c % 2]
                    kb.op("act", lambda e: e.activation(out=pn[:], in_=hst[gg][:], func=AF.Copy), R=[hst[gg]], W=[pn])

        def P3(c):
            k = c % NB
            tsl = slice(c * 128, (c + 1) * 128)
            kb.op("dve", lambda e: e.tensor_tensor(out=yz[k][:], in0=yy[k][:], in1=zst[k][:], op=ALU.mult), R=[yy[k], zst[k]], W=[yz[k]])
            kb.op("act", lambda e: e.activation(out=yn[k][:], in_=yz[k][:], func=AF.Square, accum_out=ss[k][:]), R=[yz[k]], W=[yn[k], ss[k]])
            kb.op("dve", lambda e: e.tensor_scalar(out=ss[k][:], in0=ss[k][:], scalar1=1.0 / 1024, scalar2=EPS, op0=ALU.mult, op1=ALU.add),
                  R=[ss[k]], W=[ss[k]])
            kb.op("act", lambda e: e.activation(out=ss[k][:], in_=ss[k][:], func=AF.Ln), R=[ss[k]], W=[ss[k]])
            kb.op("act", lambda e: e.activation(out=ss[k][:], in_=ss[k][:], func=AF.Exp, scale=-0.5), R=[ss[k]], W=[ss[k]])
            kb.op("dve", lambda e: e.scalar_tensor_tensor(out=yn[k][:], in0=yz[k][:], scalar=ss[k][:, 0:1], in1=nw_bc[:],
                                                          op0=ALU.mult, op1=ALU.mult), R=[yz[k], ss[k], nw_bc], W=[yn[k]])
            pt = kb.ps()
            ptb = pt[:, :].bitcast(BF16)
            for kc in range(KC):
                kb.op("pe", lambda e, kc=kc: e.transpose(ptb[:, kc * 128:(kc + 1) * 128], yn[k][:, kc * 128:(kc + 1) * 128], g.identb[:]),
                      R=[yn[k], g.identb], W=[pt])
            st = yst[(c // 4) % 2]
            kb.op("act", lambda e: e.activation(out=st[:, :, (c % 4) * 128:(c % 4 + 1) * 128],
                                                in_=ptb.rearrange("p (kc t) -> p kc t", kc=KC), func=AF.Copy), R=[pt], W=[st])
            if c % 4 == 3:
                tb = c // 4
                kb.dma("sp", S["yT"].t[0:8, :, tb * 512:(tb + 1) * 512].rearrange("kc p t -> p kc t"), st[:], R=[st])

        for step in range(-1, NT + 1):
            if 0 <= step + 1 < NT:
                P1(step + 1)
            if 0 <= step - 1 < NT:
                P3(step - 1)
            if 0 <= step < NT:
                P2(step)
        kb.barrier()


def phase_sb(g, l):
    kb, nc = g.kb, g.nc
    S = g.scr
    with ExitStack() as ph:
        def sb(name, shape, dt_=F32):
            return kb.sb(ph, name, shape, dt_)
        qT = sb("sbq", [128, 4, L], BF16)
        kT = sb("sbk", [128, 4, L], BF16)
        v = sb("sbv", [128, NT, 512], BF16)
        onesw = sb("onesw", [128, L], BF16)
        kb.op("dve", lambda e: e.memset(onesw[:], 1.0), W=[onesw])
        kb.dma("sp", qT[:], S["sbqT"].t.rearrange("c p t -> p c t"), W=[qT])
        kb.dma("sp", kT[:], S["sbkT"].t.rearrange("c p t -> p c t"), W=[kT])
        kb.dma("sp", v[:], S["sbv"].t.rearrange("(t p) f -> p t f", p=128), W=[v])
        e1b = [sb("e1", [128, L]) for _ in range(3)]
        spb = [sb("sp", [128, L]) for _ in range(2)]
        Fb = [sb("F", [128, L + 1]) for _ in range(2)]
        attb = [sb("att", [128, L], BF16) for _ in range(2)]
        attT = [sb("attT", [128, NT, 128], BF16) for _ in range(2)]
        ftn = [sb("ftn", [128, 1]) for _ in range(2)]
        yst = [sb("yst", [128, 128], BF16) for _ in range(4)]
        for F in Fb:
            kb.op("dve", lambda e, F=F: e.memset(F[:, 0:1], 0.0), W=[F])
        iters = [(qt, h) for qt in range(NT) for h in range(8)]

        def s1(i):
            qt, h = iters[i]
            W = 128 * (qt + 1)
            nch = (W + 511) // 512
            dsl = slice(qt * 128, (qt + 1) * 128)
            hp, hc = (h % 2) * 64, h // 2
            e1, sp_ = e1b[i % 3], spb[i % 2]
            for ch in range(nch):
                n = min(512, W - ch * 512)
                p = kb.ps()
                kb.op("pe", lambda e, p=p, ch=ch, n=n: e.matmul(p[:, 0:n], qT[hp:hp + 64, hc, dsl], kT[hp:hp + 64, hc, ch * 512:ch * 512 + n],
                                                                start=True, stop=True), R=[qT, kT], W=[p])
                kb.op("act", lambda e, p=p, ch=ch, n=n: e.activation(out=e1[:, ch * 512:ch * 512 + n], in_=p[:, 0:n], func=AF.Exp, scale=0.125),
                      R=[p], Wd=[e1])
            kb.op("act", lambda e: e.activation(out=sp_[:, 0:W], in_=e1[:, 0:W], func=AF.Ln, bias=1.0), R=[e1], W=[sp_])

        def s2(i):
            qt, h = iters[i]
            k = i % 2
            W = 128 * (qt + 1)
            dsl = slice(qt * 128, (qt + 1) * 128)
            sp_, F = spb[k], Fb[k]
            kb.op("dve", lambda e: e.tensor_tensor(out=sp_[:, dsl], in0=sp_[:, dsl], in1=g.strl[:], op=ALU.mult), R=[sp_, g.strl], W=[sp_])
            kb.op("dve", lambda e: e.memset(F[:, 0:1], 0.0), W=[F])
            kb.op("dve", lambda e: e.tensor_tensor_scan(out=F[:, 1:W + 1], data0=onesw[:, 0:W], data1=sp_[:, 0:W], initial=0.0,
                                                        op0=ALU.mult, op1=ALU.add), R=[onesw, sp_], W=[F])
            kb.op("dve", lambda e: e.tensor_scalar(out=ftn[k][:], in0=F[:, W:W + 1], scalar1=-1.0, scalar2=None, op0=ALU.mult),
                  R=[F], W=[ftn[k]])

        def s3a(i):
            qt, h = iters[i]
            k = i % 2
            W = 128 * (qt + 1)
            F = Fb[k]
            kb.op("act", lambda e: e.activation(out=F[:, 0:W], in_=F[:, 0:W], func=AF.Exp, bias=ftn[k][:, 0:1]), R=[F, ftn[k]], W=[F])

        def s3b(i):
            qt, h = iters[i]
            k = i % 2
            W = 128 * (qt + 1)
            dsl = slice(qt * 128, (qt + 1) * 128)
            e1, F, att = e1b[i % 3], Fb[k], attb[k]
            kb.op("dve", lambda e: e.tensor_tensor(out=att[:, 0:W], in0=e1[:, 0:W], in1=F[:, 0:W], op=ALU.mult), R=[e1, F], W=[att])
            kb.op("dve", lambda e: e.tensor_tensor(out=att[:, dsl], in0=att[:, dsl], in1=g.strlb[:], op=ALU.mult), R=[att, g.strlb], W=[att])

        def s3c(i):
            qt, h = iters[i]
            k = i % 2
            att, aT = attb[k], attT[k]
            for b0 in range(0, qt + 1, 8):
                nb = min(8, qt + 1 - b0)
                pt = kb.ps()
                ptb = pt[:, :].bitcast(BF16)
                for j in range(nb):
                    kb.op("pe", lambda e, j=j, b0=b0: e.transpose(ptb[:, j * 128:(j + 1) * 128], att[:, (b0 + j) * 128:(b0 + j + 1) * 128], g.identb[:]),
                          R=[att, g.identb], W=[pt])
                kb.op("act", lambda e, b0=b0, nb=nb: e.activation(out=aT[:, b0:b0 + nb, :], in_=ptb[:, 0:nb * 128].rearrange("p (j t) -> p j t", j=nb),
                                                                 func=AF.Copy), R=[pt], Wd=[aT])

        def s3d(i):
            qt, h = iters[i]
            k = i % 2
            dsl = slice(qt * 128, (qt + 1) * 128)
            hp, hc = (h % 2) * 64, h // 2
            aT = attT[k]
            py = kb.ps()
            for sbk in range(qt + 1):
                kb.op("pe", lambda e, sbk=sbk: e.matmul(py[0:64, 0:128], v[:, sbk, h * 64:(h + 1) * 64], aT[:, sbk, :],
                                                        start=(sbk == 0), stop=(sbk == qt)), R=[v, aT], W=[py])
            ys = yst[i % 4]
            kb.op("act", lambda e: e.activation(out=ys[hp:hp + 64, :], in_=py[0:64, 0:128], func=AF.Copy), R=[py], W=[ys])
            kb.dma("sp", S["yT"].t[8 + hc, hp:hp + 64, dsl], ys[hp:hp + 64, :], R=[ys])

        n_it = len(iters)

        def run(fn, i):
            if 0 <= i < n_it:
                fn(i)
        for step in range(n_it + 4):
            run(s3d, step - 4)
            run(s3a, step - 2)
            run(s1, step)
            run(s3c, step - 3)
            run(s2, step - 1)
            run(s3b, step - 2)
        kb.barrier()


NBIS = 16


def phase_dsa(g, l):
    kb, nc = g.kb, g.nc
    S = g.scr
    with ExitStack() as ph:
        def sb(name, shape, dt_=F32):
            return kb.sb(ph, name, shape, dt_)
        dk = sb("dk", [64, 2, L], BF16)
        ik = sb("ik", [64, L], BF16)
        vaug = sb("vaug", [128, NT, 2, 128], BF16)
        kb.dma("sp", dk[:], S["dkT"].t.rearrange("h d t -> d h t"), W=[dk])
        kb.dma("sp", ik[:], S["ikT"].t[0], W=[ik])
        kb.op("dve", lambda e: e.memset(vaug[:], 1.0), W=[vaug])
        for gg in range(2):
            kb.dma("sp", vaug[:, :, gg, 0:64], S["dv"].t[:, gg * 64:(gg + 1) * 64].rearrange("(t p) d -> p t d", p=128), W=[vaug])
        dqt = [sb("dqt", [64, 8, 128], BF16) for _ in range(2)]
        iqt = [sb("iqt", [64, 8, 128], BF16) for _ in range(2)]
        score = [sb("score", [128, L]) for _ in range(2)]
        rl = [sb("rl", [128, 512], BF16) for _ in range(4)]
        wabs = [sb("wabs", [128, 8]) for _ in range(2)]
        wsgn = [sb("wsgn", [128, 8]) for _ in range(2)]
        Dg = [sb("Dg", [128, 8, 128], BF16) for _ in range(2)]
        junk = sb("junk", [128, L], BF16)
        maskb = [sb("maskb", [128, L], BF16) for _ in range(2)]
        maskT = [sb("maskT", [128, NT, 128], BF16) for _ in range(2)]
        Eb = [sb("E", [128, 512], BF16) for _ in range(3)]
        M_ = [sb("M", [128, 1]) for _ in range(2)]
        A_ = [sb("A", [128, NBIS + 1]) for _ in range(2)]
        mid = [sb("mid", [128, 1]) for _ in range(2)]
        cnt = [sb("cnt", [128, 1]) for _ in range(2)]
        sela = [sb("sela", [128, 1]) for _ in range(2)]
        R0 = [sb("R0", [64, 512]) for _ in range(2)]
        ytmp = [sb("ytmp", [64, 512], BF16) for _ in range(2)]
        yraw = [sb("yraw", [64, 512]) for _ in range(2)]
        yst = [sb("dyst", [128, 4, 128], BF16) for _ in range(2)]
        cn = dict(rl=0, E=0)

        def stage_I(qt):
            k = qt % 2
            W = 128 * (qt + 1)
            dsl = slice(qt * 128, (qt + 1) * 128)
            if qt < 2:
                return
            kb.dma("sp", iqt[k][:], S["iqT"].t[:, :, dsl].rearrange("h d t -> d h t"), W=[iqt[k]])
            sc = score[k]
            nch = (W + 511) // 512
            wv = g.wix_tok[:, qt, :]
            kb.op("act", lambda e: e.activation(out=wabs[k][:], in_=wv, func=AF.Abs), R=[g.wix_tok], W=[wabs[k]])
            kb.op("dve", lambda e: e.tensor_scalar(out=wsgn[k][:], in0=wv, scalar1=0.0, scalar2=2.0, op0=ALU.is_ge, op1=ALU.mult),
                  R=[g.wix_tok], W=[wsgn[k]])
            kb.op("dve", lambda e: e.tensor_scalar(out=wsgn[k][:], in0=wsgn[k][:], scalar1=-1.0, scalar2=None, op0=ALU.add),
                  R=[wsgn[k]], W=[wsgn[k]])
            for h in range(8):
                kb.op("dve", lambda e, h=h: e.tensor_scalar(out=Dg[k][:, h, :], in0=g.identb[:], scalar1=wsgn[k][:, h:h + 1], scalar2=None,
                                                            op0=ALU.mult), R=[g.identb, wsgn[k]], Wd=[Dg[k]])
            items = [(ch, h) for ch in range(nch) for h in range(8)]
            pend = {}
            pscs = {}

            def mm(j):
                ch, h = items[j]
                n = min(512, W - ch * 512)
                p = kb.ps()
                kb.op("pe", lambda e: e.matmul(p[:, 0:n], iqt[k][0:64, h, :], ik[0:64, ch * 512:ch * 512 + n], start=True, stop=True),
                      R=[iqt[k], ik], W=[p])
                r = rl[cn["rl"] % 4]
                cn["rl"] += 1
                kb.op("act", lambda e: e.activation(out=r[:, 0:n], in_=p[:, 0:n], func=AF.Relu, scale=wabs[k][:, h:h + 1]),
                      R=[p, wabs[k]], W=[r])
                pend[j] = r

            def acc(j):
                ch, h = items[j]
                n = min(512, W - ch * 512)
                r = pend.pop(j)
                if h == 0:
                    pscs[ch] = kb.ps(pin=True)
                psc = pscs[ch]
                kb.op("pe", lambda e: e.matmul(psc[:, 0:n], Dg[k][:, h, :], r[:, 0:n], start=(h == 0), stop=(h == 7)),
                      R=[Dg[k], r], W=[psc])
                if h == 7:
                    kb.unpin(psc)
                    kb.op("act", lambda e: e.activation(out=sc[:, ch * 512:ch * 512 + n], in_=psc[:, 0:n], func=AF.Copy), R=[psc], Wd=[sc])

            for j in range(len(items) + 2):
                if j < len(items):
                    mm(j)
                if 0 <= j - 2 < len(items):
                    acc(j - 2)

        def stage_B(qt):
            k = qt % 2
            W = 128 * (qt + 1)
            dsl = slice(qt * 128, (qt + 1) * 128)
            mT = maskT[k]
            if qt < 2:
                if qt == 1:
                    kb.op("dve", lambda e: e.memset(mT[:, 0, :], 0.0), W=[mT])
                kb.op("dve", lambda e: e.tensor_copy(out=mT[:, qt, :], in_=g.negmb[:]), R=[g.negmb], W=[mT])
                return
            sc = score[k]
            kb.op("dve", lambda e: e.tensor_reduce(out=M_[k][:], in_=sc[:, 0:W], axis=AX.X, op=ALU.max, apply_absolute_value=True),
                  R=[sc], W=[M_[k]])
            kb.op("dve", lambda e: e.tensor_scalar(out=A_[k][:, 0:NBIS], in0=g.pow2[:, 0:NBIS], scalar1=M_[k][:, 0:1], scalar2=None, op0=ALU.mult),
                  R=[g.pow2, M_[k]], W=[A_[k]])
            kb.op("dve", lambda e: e.tensor_copy(out=A_[k][:, NBIS:NBIS + 1], in_=A_[k][:, NBIS - 1:NBIS]), R=[A_[k]], W=[A_[k]])
            kb.op("dve", lambda e: e.tensor_tensor(out=sc[:, dsl], in0=sc[:, dsl], in1=g.negup[:], op=ALU.add), R=[sc, g.negup], W=[sc])
            kb.op("dve", lambda e: e.memset(mid[k][:], 0.0), W=[mid[k]])
            for it in range(NBIS):
                kb.op("dve", lambda e: e.tensor_scalar(out=junk[:, 0:W], in0=sc[:, 0:W], scalar1=mid[k][:, 0:1], scalar2=0.0,
                                                       op0=ALU.is_ge, op1=ALU.add, accum_out=cnt[k][:]),
                      R=[sc, mid[k]], W=[junk, cnt[k]])
                kb.op("dve", lambda e, it=it: e.tensor_scalar(out=sela[k][:], in0=cnt[k][:], scalar1=255.5, scalar2=A_[k][:, it:it + 1],
                                                              op0=ALU.is_ge, op1=ALU.mult), R=[cnt[k], A_[k]], W=[sela[k]])
                kb.op("dve", lambda e, it=it: e.scalar_tensor_tensor(out=mid[k][:], in0=sela[k][:], scalar=A_[k][:, it + 1:it + 2], in1=mid[k][:],
                                                                     op0=ALU.subtract, op1=ALU.add), R=[sela[k], A_[k], mid[k]], W=[mid[k]])
            mb = maskb[k]
            kb.op("dve", lambda e: e.tensor_scalar(out=mb[:, 0:W], in0=sc[:, 0:W], scalar1=mid[k][:, 0:1], scalar2=-30000.0,
                                                   op0=ALU.is_lt, op1=ALU.mult), R=[sc, mid[k]], W=[mb])

        def stage_T(qt):
            if qt < 2:
                return
            k = qt % 2
            mb, mT = maskb[k], maskT[k]
            for b0 in range(0, qt + 1, 8):
                nb = min(8, qt + 1 - b0)
                pt = kb.ps()
                ptb = pt[:, :].bitcast(BF16)
                for j in range(nb):
                    kb.op("pe", lambda e, j=j, b0=b0: e.transpose(ptb[:, j * 128:(j + 1) * 128], mb[:, (b0 + j) * 128:(b0 + j + 1) * 128], g.identb[:]),
                          R=[mb, g.identb], W=[pt])
                kb.op("act", lambda e, b0=b0, nb=nb: e.activation(out=mT[:, b0:b0 + nb, :], in_=ptb[:, 0:nb * 128].rearrange("p (j t) -> p j t", j=nb),
                                                                 func=AF.Copy), R=[pt], Wd=[mT])

        def stage_A(qt):
            k = qt % 2
            dsl = slice(qt * 128, (qt + 1) * 128)
            mT = maskT[k]
            ys = yst[k]
            if qt + 1 < NT:
                kb.dma("sp", dqt[1 - k][:], S["dqT"].t[:, :, (qt + 1) * 128:(qt + 2) * 128].rearrange("h d t -> d h t"), W=[dqt[1 - k]])
            for gq in range(2):
                pO = kb.ps(pin=True)
                Es = {}

                def qk(sbk):
                    pS = kb.ps()
                    kb.op("pe", lambda e: e.matmul(pS[:, :], dk[0:64, gq, sbk * 128:(sbk + 1) * 128], dqt[k][0:64, 4 * gq:4 * gq + 4, :],
                                                   start=True, stop=False), R=[dk, dqt[k]], W=[pS])
                    kb.op("pe", lambda e: e.matmul(pS[:, :], g.identb[:], mT[:, sbk, :].unsqueeze(1).broadcast_to([128, 4, 128]),
                                                   start=False, stop=True), R=[g.identb, mT], W=[pS])
                    E = Eb[cn["E"] % 3]
                    cn["E"] += 1
                    kb.op("act", lambda e: e.activation(out=E[:], in_=pS[:, :], func=AF.Exp, scale=0.125), R=[pS], W=[E])
                    Es[sbk] = E

                def av(sbk):
                    E = Es.pop(sbk)
                    kb.op("pe", lambda e: e.matmul(pO[:, :], vaug[:, sbk, gq, :], E[:], start=(sbk == 0), stop=(sbk == qt)),
                          R=[vaug, E], W=[pO])

                for sbk in range(qt + 2):
                    if sbk <= qt:
                        qk(sbk)
                    if sbk >= 1:
                        av(sbk - 1)
                kb.unpin(pO)
                r0, yt = R0[gq], ytmp[gq]
                kb.op("act", lambda e: e.activation(out=r0[:], in_=pO[64:128, :], func=AF.Ln), R=[pO], W=[r0])
                kb.op("act", lambda e: e.activation(out=r0[:], in_=r0[:], func=AF.Exp, scale=-1.0), R=[r0], W=[r0])
                yr = yraw[gq]
                kb.op("act", lambda e: e.activation(out=yr[:], in_=pO[0:64, :], func=AF.Copy), R=[pO], W=[yr])
                kb.op("pool", lambda e: e.tensor_tensor(out=yt[:], in0=yr[:], in1=r0[:], op=ALU.mult), R=[yr, r0], W=[yt])
                for hh in range(4):
                    h = 4 * gq + hh
                    hp, hc = (h % 2) * 64, h // 2
                    kb.op("act", lambda e, hh=hh, hp=hp, hc=hc: e.activation(out=ys[hp:hp + 64, hc, :], in_=yt[:, hh * 128:(hh + 1) * 128], func=AF.Copy),
                          R=[yt], Wd=[ys])
            kb.dma("sp", S["yT"].t[12:16, :, dsl].rearrange("c p t -> p c t"), ys[:], R=[ys])

        kb.dma("sp", dqt[0][:], S["dqT"].t[:, :, 0:128].rearrange("h d t -> d h t"), W=[dqt[0]])
        stage_I(0)
        stage_B(0)
        stage_T(0)
        stage_I(1)
        modw = [sb("adaw", [128, KC, 512], BF16) for _ in range(2)] if g.dsa_hook is not None else None
        for qt in range(NT):
            if qt + 2 < NT:
                stage_I(qt + 2)
            stage_A(qt)
            if modw is not None and 1 <= qt < 13:
                mod_piece_load(g, g.dsa_hook, qt - 1, modw[(qt - 1) % 2])
            if modw is not None and 2 <= qt < 14:
                mod_piece_mm(g, g.dsa_hook, qt - 2, modw[(qt - 2) % 2])
            if qt + 1 < NT:
                stage_B(qt + 1)
                stage_T(qt + 1)
        kb.barrier()


def phase_merge(g, l):
    kb, nc = g.kb, g.nc
    S = g.scr
    with ExitStack() as ph:
        def sb(name, shape, dt_=F32):
            return kb.sb(ph, name, shape, dt_)
        yT = sb("yTall", [128, 16, L], BF16)
        yTb = [kb.buf() for _ in range(4)]
        for q in range(4):
            kb.dma("sp", yT[:, q * 4:(q + 1) * 4, :], S["yT"].t[q * 4:(q + 1) * 4].rearrange("c p t -> p c t"), W=[yTb[q]])
        mT = sb("mergedT", [128, KC, L], BF16)
        mTb = [[kb.buf() for tb in range(4)] for c in range(KC)]
        wbr = [sb("wbr", [128, 16, 256], BF16) for _ in range(2)]
        gt = [sb("gt", [128, 3, 512], BF16) for _ in range(2)]
        gbr = [sb("gbr", [128, 3, 512], BF16) for _ in range(2)]
        it = 0
        for c2 in range(4):
            w = wbr[c2 % 2]
            csl = slice(c2 * 256, (c2 + 1) * 256)
            kb.dma("pool", w[:, 0:8, :], g.inp["w_br_ssd"].t[l][:, csl].rearrange("(kc p) n -> p kc n", p=128), W=[w])
            kb.dma("pool", w[:, 8:12, :], g.inp["w_br_sb"].t[l][:, csl].rearrange("(kc p) n -> p kc n", p=128), W=[w])
            kb.dma("pool", w[:, 12:16, :], g.inp["w_br_dsa"].t[l][:, csl].rearrange("(kc p) n -> p kc n", p=128), W=[w])
            for cl in range(2):
                c = c2 * 2 + cl
                for tb in range(4):
                    k = it % 2
                    it += 1
                    tsl = slice(tb * 512, (tb + 1) * 512)
                    for br in range(3):
                        kb.dma("sp", gt[k][:, br, :], S["gT"].t[br * 8 + c, :, tsl], W=[gt[k]])
                    pbr = []
                    for br, (k0, k1) in enumerate(((0, 8), (8, 12), (12, 16))):
                        p = kb.ps()
                        for kc in range(k0, k1):
                            kb.op("pe", lambda e, p=p, kc=kc, k0=k0, k1=k1: e.matmul(
                                p[:, :], w[:, kc, cl * 128:(cl + 1) * 128], yT[:, kc, tsl], start=(kc == k0), stop=(kc == k1 - 1)),
                                R=[w, yTb[kc // 4]], W=[p])
                        pbr.append(p)
                    gb = gbr[k]
                    for br in range(3):
                        kb.op("dve", lambda e, br=br: e.tensor_tensor(out=gb[:, br, :], in0=pbr[br][:, :], in1=gt[k][:, br, :], op=ALU.mult),
                              R=[pbr[br], gt[k]], Wd=[gb])
                    pm = kb.ps()
                    for br in range(3):
                        kb.op("pe", lambda e, br=br: e.matmul(pm[:, :], g.identb[:], gb[:, br, :], start=(br == 0), stop=(br == 2)),
                              R=[g.identb, gb], W=[pm])
                    kb.op("act", lambda e: e.activation(out=mT[:, c, tsl], in_=pm[:, :], func=AF.Copy), R=[pm], W=[mTb[c][tb]])
        wo = [sb("wo", [128, KC, 256], BF16) for _ in range(2)]
        for c2 in range(4):
            w = wo[c2 % 2]
            kb.dma("pool", w[:], g.inp["w_out"].t[l][:, c2 * 256:(c2 + 1) * 256].rearrange("(kc p) n -> p kc n", p=128), W=[w])
            for cl in range(2):
                c = c2 * 2 + cl
                for tb in range(4):
                    tsl = slice(tb * 512, (tb + 1) * 512)
                    p = kb.ps()
                    for kc in range(KC):
                        kb.op("pe", lambda e, p=p, kc=kc: e.matmul(p[:, :], w[:, kc, cl * 128:(cl + 1) * 128], mT[:, kc, tsl],
                                                                   start=(kc == 0), stop=(kc == KC - 1)), R=[w, mTb[kc][tb]], W=[p])
                    xs = g.xT[:, c, tsl]
                    kb.op("dve", lambda e, p=p, c=c, xs=xs: e.scalar_tensor_tensor(
                        out=xs, in0=p[:, :], scalar=g.modT[:, 16 + c:17 + c], in1=xs, op0=ALU.mult, op1=ALU.add),
                        R=[p, g.modT, g.xTb[c][tb]], W=[g.xTb[c][tb]])
        kb.barrier()


def phase_mlp(g, l):
    kb = g.kb
    with ExitStack() as ph:
        wup = [kb.sb(ph, f"wup{i}", [128, KC, 512], BF16) for i in range(2)]
        wdn = [kb.sb(ph, f"wdn{i}", [128, 4, D], BF16) for i in range(2)]
        act = [kb.sb(ph, f"mact{i}", [128, 4, L], BF16) for i in range(2)]
        actb = [[[kb.buf() for tb in range(4)] for hc in range(4)] for i in range(2)]
        rl = [kb.sb(ph, f"mrl{i}", [128, 512], F32) for i in range(2)]
        nrl = 0
        for gi in range(8):
            wu, wd, a, ab = wup[gi % 2], wdn[gi % 2], act[gi % 2], actb[gi % 2]
            load_w(g, wu, g.inp["w_up"].t[l][:, gi * 512:(gi + 1) * 512])
            load_w(g, wd, g.inp["w_down"].t[l][gi * 512:(gi + 1) * 512, :])
            for tb in range(4):
                for hc in range(4):
                    p = kb.ps()
                    for kc in range(KC):
                        kb.op("pe", lambda e, p=p, kc=kc, hc=hc: e.matmul(
                            p[:, :], wu[:, kc, hc * 128:(hc + 1) * 128], g.hT[:, kc, tb * 512:(tb + 1) * 512],
                            start=(kc == 0), stop=(kc == KC - 1)), R=[wu, g.hTb[kc][tb]], W=[p])
                    r = rl[nrl % 2]
                    nrl += 1
                    kb.op("act", lambda e, p=p, r=r: e.activation(out=r[:], in_=p[:, :], func=AF.Relu), R=[p], W=[r])
                    kb.op("act", lambda e, r=r, hc=hc: e.activation(out=a[:, hc, tb * 512:(tb + 1) * 512], in_=r[:], func=AF.Square),
                          R=[r], W=[ab[hc][tb]])
            import os
            if os.environ.get("MLP_UP_ONLY"):
                continue
            for tb in range(4):
                for c in range(KC):
                    p = kb.ps()
                    for hc in range(4):
                        kb.op("pe", lambda e, p=p, hc=hc, c=c: e.matmul(
                            p[:, :], wd[:, hc, c * 128:(c + 1) * 128], a[:, hc, tb * 512:(tb + 1) * 512],
                            start=(hc == 0), stop=(hc == 3)), R=[wd, ab[hc][tb]], W=[p])
                    xs = g.xT[:, c, tb * 512:(tb + 1) * 512]
                    kb.op("dve", lambda e, p=p, c=c, xs=xs: e.scalar_tensor_tensor(
                        out=xs, in0=p[:, :], scalar=g.modT[:, 40 + c:41 + c], in1=xs, op0=ALU.mult, op1=ALU.add),
                        R=[p, g.modT, g.xTb[c][tb]], W=[g.xTb[c][tb]])
        kb.barrier()


def rstd_bcast(kb, ph, xT, xTb, tb, onesb, sq, rs):
    p = kb.ps()
    for c in range(KC):
        s = sq[c % 2]
        kb.op("act", lambda e, s=s, c=c: e.activation(out=s[:], in_=xT[:, c, tb * 512:(tb + 1) * 512], func=AF.Square),
              R=[xTb[c][tb]], W=[s])
        kb.op("pe", lambda e, s=s, c=c, p=p: e.matmul(p[:, :], onesb[:], s[:], start=(c == 0), stop=(c == KC - 1)),
              R=[s, onesb], W=[p])
    kb.op("dve", lambda e: e.tensor_scalar(out=rs[:], in0=p[:, :], scalar1=1.0 / D, scalar2=EPS, op0=ALU.mult, op1=ALU.add),
          R=[p], W=[rs])
    kb.op("act", lambda e: e.activation(out=rs[:], in_=rs[:], func=AF.Ln), R=[rs], W=[rs])
    kb.op("act", lambda e: e.activation(out=rs[:], in_=rs[:], func=AF.Exp, scale=-0.5), R=[rs], W=[rs])


def final_norm(kb, nc, xT, xTb, fnw, onesb, ident, out_d):
    with ExitStack() as ph:
        sq = [kb.sb(ph, f"fsq{i}", [128, 512], BF16) for i in range(2)]
        rs = kb.sb(ph, "frs", [128, 512], F32)
        yT = [kb.sb(ph, f"fyT{i}", [128, 512], F32) for i in range(2)]
        ost = [kb.sb(ph, f"fost{i}", [128, 4, D], F32) for i in range(2)]
        for tb in range(4):
            rstd_bcast(kb, ph, xT, xTb, tb, onesb, sq, rs)
            o = ost[tb % 2]
            for c in range(KC):
                y = yT[c % 2]
                kb.op("dve", lambda e, y=y, c=c: e.scalar_tensor_tensor(
                    out=y[:], in0=xT[:, c, tb * 512:(tb + 1) * 512], scalar=fnw[:, c:c + 1], in1=rs[:],
                    op0=ALU.mult, op1=ALU.mult), R=[xTb[c][tb], fnw, rs], W=[y])
                p = kb.ps()
                for j in range(4):
                    kb.op("pe", lambda e, y=y, j=j, p=p: e.transpose(
                        p[:, j * 128:(j + 1) * 128], y[:, j * 128:(j + 1) * 128], ident[:]), R=[y, ident], W=[p])
                src = p[:, :].rearrange("p (j f) -> p j f", j=4)
                dst = o[:, :, c * 128:(c + 1) * 128]
                kb.op("act", lambda e, dst=dst, src=src: e.activation(out=dst, in_=src, func=AF.Copy), R=[p], Wd=[o])
            kb.dma("sp", out_d.t[tb * 512:(tb + 1) * 512, :].rearrange("(j p) d -> p j d", p=128), o[:], R=[o], W=[out_d])
        kb.barrier()


_NC_CACHE = {}


def make_in_maps(inputs):
    nb = inputs["x"].shape[0]
    inputs = dict(inputs)
    inputs.update(host_consts())
    shared = {n: np.ascontiguousarray(np.asarray(inputs[n], dtype=np.float32)) for n in IN_SHAPES if n not in ("x", "c")}
    in_maps = []
    for b in range(nb):
        m = dict(shared)
        m["x"] = np.ascontiguousarray(inputs["x"][b])
        m["c"] = np.ascontiguousarray(inputs["c"][b])
        in_maps.append(m)
    return in_maps


def kernel(**inputs):
    nb = inputs["x"].shape[0]
    if "nc" not in _NC_CACHE:
        _NC_CACHE["nc"] = build()[0]
    nc = _NC_CACHE["nc"]
    res = run_bass_kernel_spmd(nc, make_in_maps(inputs), core_ids=list(range(nb)))
    return np.stack([r["out"] for r in res.results], axis=0)
```

```python
import math
from contextlib import ExitStack

import numpy as np
import concourse.bass as bass
import concourse.mybir as mybir
from concourse.bass_utils import run_bass_kernel_spmd

F32 = mybir.dt.float32
BF16 = mybir.dt.bfloat16
I32 = mybir.dt.int32
AF = mybir.ActivationFunctionType
ALU = mybir.AluOpType
AX = mybir.AxisListType

D = 1024
L = 2048
DEPTH = 2
NT = L // 128
KC = D // 128
EPS = 1e-6
import os
NDS = int(os.environ.get("NDS", "12"))
STRICT = bool(int(os.environ.get("KSTRICT", "1")))


class Buf:
    __slots__ = ("name", "w", "r", "excl")

    def __init__(self, name):
        self.name = name
        self.excl = False
        self.w = None
        self.r = {}


class T:
    def __init__(self, t, buf):
        self.t = t
        self.b = buf

    def __getitem__(self, k):
        return self.t[k]


class KB:
    def __init__(self, nc):
        self.nc = nc
        self.es = ExitStack()
        self.sems = {}
        self.engs = {}
        for name, e in (("pe", nc.tensor), ("act", nc.scalar), ("dve", nc.vector),
                        ("pool", nc.gpsimd), ("sp", nc.sync)):
            key = "s_" + name
            self.sems[key] = self.es.enter_context(nc.semaphore(key))
            self.engs[name] = dict(e=e, key=key, cnt=0, seen={})
        self.dq = {}
        for q in ("sp", "pool"):
            keys = []
            for i in range(NDS):
                key = f"d_{q}{i}"
                self.sems[key] = self.es.enter_context(nc.semaphore(key))
                keys.append(key)
            self.dq[q] = dict(keys=keys, cnt=[0] * NDS, nxt=0)
        self.nbuf = 0
        self.psum = []
        self.ps_next = 0
        self.pinned = set()

    def buf(self, name=None):
        self.nbuf += 1
        return Buf(name or f"b{self.nbuf}")

    def sb(self, es, name, shape, dtype):
        self.nbuf += 1
        name = f"{name}_{self.nbuf}"
        t = es.enter_context(self.nc.sbuf_tensor(name, list(shape), dtype))
        return T(t, self.buf(name))

    def dram(self, name, shape, dtype, kind="Internal"):
        t = self.nc.dram_tensor(name, list(shape), dtype, kind=kind)
        return T(t.ap(), self.buf(name))

    def init_psum(self):
        for i in range(8):
            t = self.es.enter_context(self.nc.psum_tensor(f"ps{i}", [128, 512], F32))
            self.psum.append(T(t, self.buf(f"ps{i}")))
            self.psum[-1].b.excl = True

    def ps(self, pin=False):
        while True:
            i = self.ps_next
            self.ps_next = (self.ps_next + 1) % 8
            if i not in self.pinned:
                break
        if pin:
            self.pinned.add(i)
        return self.psum[i]

    def unpin(self, p):
        self.pinned.discard(self.psum.index(p))

    def _deps(self, own_key, R, W, same_raw, Wd=()):
        deps = {}

        def add(ev):
            if ev is None:
                return
            k, v = ev
            if deps.get(k, 0) < v:
                deps[k] = v

        for t in R:
            b = t.b if isinstance(t, T) else t
            if b.w is not None and (b.w[0] != own_key or same_raw):
                add(b.w)
            if b.excl:
                for k, v in b.r.items():
                    if k != own_key:
                        add((k, v))
        for t in W:
            b = t.b if isinstance(t, T) else t
            if b.w is not None and (b.w[0] != own_key or (STRICT and same_raw)):
                add(b.w)
            for k, v in b.r.items():
                if k != own_key or (STRICT and same_raw):
                    add((k, v))
        for t in Wd:
            b = t.b if isinstance(t, T) else t
            if b.w is not None and b.w[0] != own_key:
                add(b.w)
            for k, v in b.r.items():
                if k != own_key or (STRICT and same_raw):
                    add((k, v))
        return deps

    def _mark(self, ev, R, W):
        k, v = ev
        for t in R:
            b = t.b if isinstance(t, T) else t
            if b.r.get(k, 0) < v:
                b.r[k] = v
        for t in W:
            b = t.b if isinstance(t, T) else t
            b.w = ev
            b.r = {}

    def _wait(self, eng, deps):
        for k, v in deps.items():
            if eng["seen"].get(k, 0) < v:
                eng["e"].wait_ge(self.sems[k], v)
                eng["seen"][k] = v

    def op(self, engname, fn, R=(), W=(), Wd=()):
        eng = self.engs[engname]
        deps = self._deps(eng["key"], R, W, same_raw=(engname != "pe"), Wd=Wd)
        W = list(W) + list(Wd)
        self._wait(eng, deps)
        ins = fn(eng["e"])
        eng["cnt"] += 1
        ins.then_inc(self.sems[eng["key"]], 1)
        self._mark((eng["key"], eng["cnt"]), R, W)
        return ins

    def dma(self, q, out, in_, R=(), W=(), **kw):
        eng = self.engs[q]
        dq = self.dq[q]
        deps = self._deps(None, R, W, same_raw=True)
        i = dq["nxt"]
        dq["nxt"] = (i + 1) % NDS
        key = dq["keys"][i]
        if dq["cnt"][i] > 0:
            deps[key] = max(deps.get(key, 0), dq["cnt"][i])
        self._wait(eng, deps)
        dq["cnt"][i] += 16
        eng["e"].dma_start(out=out, in_=in_, **kw).then_inc(self.sems[key], 16)
        self._mark((key, dq["cnt"][i]), R, W)

    def barrier(self):
        allev = {}
        for name, eng in self.engs.items():
            if eng["cnt"] > 0:
                allev[eng["key"]] = eng["cnt"]
        for q, dq in self.dq.items():
            for key, c in zip(dq["keys"], dq["cnt"]):
                if c > 0:
                    allev[key] = c
        for name, eng in self.engs.items():
            deps = {k: v for k, v in allev.items() if k != eng["key"]}
            self._wait(eng, deps)

    def finish(self, out_bufs):
        eng = self.engs["sp"]
        deps = {}
        for t in out_bufs:
            b = t.b if isinstance(t, T) else t
            if b.w is not None:
                deps[b.w[0]] = max(deps.get(b.w[0], 0), b.w[1])
        self._wait(eng, deps)
        self.barrier()


IN_SHAPES = {
    "x": [L, D], "c": [D], "norm1_w": [DEPTH, D], "ada_w": [DEPTH, D, 6 * D], "ada_b": [DEPTH, 6 * D],
    "w_in": [DEPTH, D, 8536], "conv_w": [DEPTH, 4, 1536], "conv_b": [DEPTH, 1536], "dt_bias": [DEPTH, 16],
    "a_log": [DEPTH, 16], "d_skip": [DEPTH, 16], "ssd_norm_w": [DEPTH, D], "w_br_ssd": [DEPTH, D, D],
    "w_br_sb": [DEPTH, 512, D], "w_br_dsa": [DEPTH, 512, D], "w_out": [DEPTH, D, D], "norm2_w": [DEPTH, D],
    "w_up": [DEPTH, D, 4 * D], "w_down": [DEPTH, 4 * D, D], "final_norm_w": [D],
    "rope_cs": [L, 16],
}


def host_consts():
    half = 8
    inv_freq = np.exp(np.arange(half, dtype=np.float32) * np.float32(-2.0 * math.log(500000.0) / 16)).astype(np.float32)
    ang = np.arange(L, dtype=np.float32)[:, None] * inv_freq[None, :]
    return {"rope_cs": np.concatenate([np.cos(ang), np.sin(ang)], axis=1).astype(np.float32)}


class G:
    pass


def build(nlayers=DEPTH, dbg=(), skip_mixer=False, phases=("mod", "inproj", "ssd", "sb", "dsa", "merge", "norm2", "mlp")):
    nc = bass.Bass("TRN2", target_bir_lowering=False)
    kb = KB(nc)
    es = kb.es
    kb.init_psum()
    g = G()
    g.nc, g.kb, g.es, g.dbg = nc, kb, es, set(dbg)
    g.inp = {n: kb.dram(n, shp, F32, kind="ExternalInput") for n, shp in IN_SHAPES.items()}
    out_d = kb.dram("out", [L, D], F32, kind="ExternalOutput")
    g.dbg_out = {}

    g.ident = ident = kb.sb(es, "ident", [128, 128], F32)
    g.identb = identb = kb.sb(es, "identb", [128, 128], BF16)
    g.onesb = onesb = kb.sb(es, "onesb", [128, 128], BF16)
    g.onesf = kb.sb(es, "onesf", [128, 128], F32)
    g.triu = kb.sb(es, "triu", [128, 128], F32)
    g.negm = kb.sb(es, "negm", [128, 128], F32)
    g.triub = kb.sb(es, "triub", [128, 128], BF16)
    g.strl = kb.sb(es, "strl", [128, 128], F32)
    g.strlb = kb.sb(es, "strlb", [128, 128], BF16)
    g.trilb = kb.sb(es, "trilb", [128, 128], BF16)
    g.negup = kb.sb(es, "negup", [128, 128], F32)
    g.negmb = kb.sb(es, "negmb", [128, 128], BF16)
    g.pow2 = kb.sb(es, "pow2", [128, 24], F32)
    with ExitStack() as tmp:
        coli = kb.sb(tmp, "coli", [128, 128], I32)
        rowi = kb.sb(tmp, "rowi", [128, 1], I32)
        colf = kb.sb(tmp, "colf", [128, 128], F32)
        rowf = kb.sb(tmp, "rowf", [128, 1], F32)
        kb.op("pool", lambda e: e.iota(coli[:], [[1, 128]], base=0, channel_multiplier=0), W=[coli])
        kb.op("pool", lambda e: e.iota(rowi[:], [[0, 1]], base=0, channel_multiplier=1), W=[rowi])
        kb.op("dve", lambda e: e.tensor_copy(out=colf[:], in_=coli[:]), R=[coli], W=[colf])
        kb.op("dve", lambda e: e.tensor_copy(out=rowf[:], in_=rowi[:]), R=[rowi], W=[rowf])
        kb.op("dve", lambda e: e.tensor_scalar(out=ident[:], in0=colf[:], scalar1=rowf[:, 0:1], scalar2=None,
                                               op0=ALU.is_equal), R=[colf, rowf], W=[ident])
        kb.op("dve", lambda e: e.tensor_copy(out=identb[:], in_=ident[:]), R=[ident], W=[identb])
        kb.op("dve", lambda e: e.memset(onesb[:], 1.0), W=[onesb])
        kb.op("dve", lambda e: e.memset(g.onesf[:], 1.0), W=[g.onesf])
        kb.op("dve", lambda e: e.tensor_scalar(out=g.triu[:], in0=colf[:], scalar1=rowf[:, 0:1], scalar2=None, op0=ALU.is_ge),
              R=[colf, rowf], W=[g.triu])
        kb.op("dve", lambda e: e.tensor_scalar(out=g.negm[:], in0=g.triu[:], scalar1=-1.0, scalar2=30000.0, op0=ALU.add, op1=ALU.mult),
              R=[g.triu], W=[g.negm])
        kb.op("dve", lambda e: e.tensor_copy(out=g.triub[:], in_=g.triu[:]), R=[g.triu], W=[g.triub])
        kb.op("dve", lambda e: e.tensor_copy(out=g.negmb[:], in_=g.negm[:]), R=[g.negm], W=[g.negmb])
        kb.op("dve", lambda e: e.tensor_scalar(out=g.strl[:], in0=colf[:], scalar1=rowf[:, 0:1], scalar2=None, op0=ALU.is_lt),
              R=[colf, rowf], W=[g.strl])
        kb.op("dve", lambda e: e.tensor_copy(out=g.strlb[:], in_=g.strl[:]), R=[g.strl], W=[g.strlb])
        kb.op("dve", lambda e: e.tensor_scalar(out=g.trilb[:], in0=colf[:], scalar1=rowf[:, 0:1], scalar2=None, op0=ALU.is_le),
              R=[colf, rowf], W=[g.trilb])
        kb.op("dve", lambda e: e.tensor_scalar(out=g.negup[:], in0=g.trilb[:], scalar1=-1.0, scalar2=1.0e9, op0=ALU.add, op1=ALU.mult),
              R=[g.trilb], W=[g.negup])
        for kk_ in range(24):
            kb.op("dve", lambda e, kk_=kk_: e.memset(g.pow2[:, kk_:kk_ + 1], float(2.0 ** (-kk_))), Wd=[g.pow2])
        kb.barrier()

    g.xT = xT = kb.sb(es, "xT", [128, KC, L], F32)
    g.xTb = xTb = [[kb.buf(f"xT{c}_{tb}") for tb in range(4)] for c in range(KC)]
    g.hTb = [[kb.buf(f"hT{c}_{tb}") for tb in range(4)] for c in range(KC)]

    fnw = load_cols(g, es, "fnw", g.inp["final_norm_w"].t, KC)
    g.ccol = load_cols(g, es, "ccol", g.inp["c"].t, KC)
    g.csilu = kb.sb(es, "csilu", [128, KC], BF16)
    kb.op("act", lambda e: e.activation(out=g.csilu[:], in_=g.ccol[:], func=AF.Silu), R=[g.ccol], W=[g.csilu])
    g.modT_l = [kb.sb(es, "modT", [128, 48], F32) for _ in range(DEPTH)]
    g.modraw = [kb.sb(es, "modraw", [128, 48], F32) for _ in range(DEPTH)]
    g.A1_l = [kb.sb(es, "A1", [128, KC], F32) for _ in range(DEPTH)]
    g.A2_l = [kb.sb(es, "A2", [128, KC], F32) for _ in range(DEPTH)]
    g.dsa_hook = None

    x_in = g.inp["x"]
    with ExitStack() as ph:
        xin = [kb.sb(ph, f"xin{i}", [128, D], F32) for i in range(2)]
        for tt in range(NT):
            xi = xin[tt % 2]
            kb.dma("sp", xi[:], x_in.t[tt * 128:(tt + 1) * 128, :], W=[xi])
            for half in range(2):
                p = kb.ps()
                for j in range(4):
                    c = half * 4 + j
                    kb.op("pe", lambda e, c=c, j=j, p=p, xi=xi: e.transpose(
                        p[:, j * 128:(j + 1) * 128], xi[:, c * 128:(c + 1) * 128], ident[:]),
                        R=[xi, ident], W=[p])
                tb = tt // 4
                wb = [xTb[half * 4 + j][tb] for j in range(4)]
                dst = xT[:, half * 4:half * 4 + 4, tt * 128:(tt + 1) * 128]
                src = p[:, :].rearrange("p (j t) -> p j t", j=4)
                if half == 0:
                    kb.op("act", lambda e, dst=dst, src=src: e.activation(out=dst, in_=src, func=AF.Copy), R=[p], W=wb)
                else:
                    kb.op("dve", lambda e, dst=dst, src=src: e.tensor_copy(out=dst, in_=src), R=[p], W=wb)
        kb.barrier()

    def scr(name, shape, dtype=BF16):
        return kb.dram("scr_" + name, shape, dtype, kind=("ExternalOutput" if name in g.dbg else "Internal"))
    g.scr = dict(zs=scr("zs", [L, 1024]), sbv=scr("sbv", [L, 512]), dv=scr("dv", [L, 128]),
                 dqT=scr("dqT", [8, 64, L]), dkT=scr("dkT", [2, 64, L]), iqT=scr("iqT", [8, 64, L]), ikT=scr("ikT", [1, 64, L]),
                 yT=scr("yT", [16, 128, L]), sbqT=scr("sbqT", [4, 128, L]), sbkT=scr("sbkT", [4, 128, L]), gT=scr("gT", [24, 128, L]))
    for name in g.scr:
        if name in g.dbg:
            g.dbg_out["scr_" + name] = g.scr[name]
    g.ropecs = kb.sb(es, "ropecs", [128, NT, 16], F32)
    kb.dma("sp", g.ropecs[:], g.inp["rope_cs"].t.rearrange("(t p) c -> p t c", p=128), W=[g.ropecs])
    g.dt_tok = kb.sb(es, "dt_tok", [128, NT, 16], F32)
    g.wix_tok = kb.sb(es, "wix_tok", [128, NT, 8], F32)
    g.dtb_bc = kb.sb(es, "dtb_bc", [128, 16], F32)
    g.cb = kb.sb(es, "cb", [128, 12], F32)
    g.cw = [kb.sb(es, f"cw{k}", [128, 12], F32) for k in range(4)]

    for l in range(nlayers):
        g.modT, g.A1, g.A2 = g.modT_l[l], g.A1_l[l], g.A2_l[l]
        if "mod" in phases:
            if l == 0:
                phase_mod(g, l)
            else:
                mod_finish(g, l)
        g.dsa_hook = (l + 1) if (l + 1 < nlayers and "mod" in phases) else None
        dump_sb(g, f"mod{l}", g.modT, [128, 48])
        if not skip_mixer:
            with ExitStack() as ssd_scope:
                g.xs_tok = kb.sb(ssd_scope, "xs_tok", [128, NT, 1024], BF16)
                g.bt_tok = kb.sb(ssd_scope, "bt_tok", [128, NT, 256], BF16)
                g.bcT = kb.sb(ssd_scope, "bcT", [128, 4, L], BF16)
                with nc.allow_non_contiguous_dma(reason="tiny vectors"):
                    kb.dma("sp", g.dtb_bc[:], g.inp["dt_bias"].t[l].partition_broadcast(128), W=[g.dtb_bc])
                    kb.dma("sp", g.cb[:], g.inp["conv_b"].t[l].rearrange("(c p) -> p c", p=128), W=[g.cb])
                    for k in range(4):
                        kb.dma("sp", g.cw[k][:], g.inp["conv_w"].t[l][k].rearrange("(c p) -> p c", p=128), W=[g.cw[k]])
                with ExitStack() as hsc:
                    g.hT = kb.sb(hsc, "hT", [128, KC, L], BF16)
                    phase_norm(g, l, which=1)
                    dump_hT(g, f"h{l}")
                    if "inproj" in phases:
                        phase_inproj(g, l)
                dump_sb(g, f"xs_tok{l}", g.xs_tok, [128, NT, 1024], BF16)
                dump_sb(g, f"bt_tok{l}", g.bt_tok, [128, NT, 256], BF16)
                dump_sb(g, f"bcT{l}", g.bcT, [128, 4, L], BF16)
                dump_sb(g, f"dt_tok{l}", g.dt_tok, [128, NT, 16], F32)
                dump_sb(g, f"wix_tok{l}", g.wix_tok, [128, NT, 8], F32)
                kb.barrier()
                if "ssd" in phases:
                    phase_ssd(g, l)
            if "sb" in phases:
                phase_sb(g, l)
            if "dsa" in phases:
                phase_dsa(g, l)
            if "merge" in phases:
                phase_merge(g, l)
            dump_xT(g, f"xmix{l}")
        with ExitStack() as hsc:
            g.hT = kb.sb(hsc, "hT", [128, KC, L], BF16)
            if "norm2" in phases:
                phase_norm(g, l, which=2)
            if "mlp" in phases:
                phase_mlp(g, l)
        dump_xT(g, f"xout{l}")

    final_norm(kb, nc, xT, xTb, fnw, onesb, ident, out_d)
    kb.finish([out_d] + list(g.dbg_out.values()))
    return nc, sorted(g.dbg_out.keys())


def load_cols(g, es_, name, src_ap, n):
    kb = g.kb
    t = kb.sb(es_, name, [128, n], F32)
    with g.nc.allow_non_contiguous_dma(reason="tiny per-feature vector load"):
        for c0 in range(0, n, 8):
            c1 = min(n, c0 + 8)
            kb.dma("sp", t[:, c0:c1], src_ap[c0 * 128:c1 * 128].rearrange("(c p) -> p c", p=128), W=[t])
    return t


def dump_sb(g, name, t, shape, dtype=F32):
    if name not in g.dbg:
        return
    d = g.kb.dram("dbg_" + name, shape, dtype, kind="ExternalOutput")
    g.kb.barrier()
    g.kb.dma("sp", d.t, t[:], R=[t], W=[d])
    g.dbg_out["dbg_" + name] = d


def dump_xT(g, name):
    if name not in g.dbg:
        return
    d = g.kb.dram("dbg_" + name, [128, KC, L], F32, kind="ExternalOutput")
    g.kb.barrier()
    g.kb.dma("sp", d.t, g.xT[:], R=[b for row in g.xTb for b in row], W=[d])
    g.dbg_out["dbg_" + name] = d


def dump_hT(g, name):
    if name not in g.dbg:
        return
    d = g.kb.dram("dbg_" + name, [128, KC, L], BF16, kind="ExternalOutput")
    g.kb.barrier()
    g.kb.dma("sp", d.t, g.hT[:], R=[b for row in g.hTb for b in row], W=[d])
    g.dbg_out["dbg_" + name] = d


def load_w(g, t, src_rows_ap):
    g.kb.dma("pool", t[:], src_rows_ap.rearrange("(kc p) n -> p kc n", p=128), W=[t])


def mod_piece_load(g, l, j4, w):
    g.kb.dma("pool", w[:], g.inp["ada_w"].t[l][:, j4 * 512:(j4 + 1) * 512].rearrange("(kc p) n -> p kc n", p=128), W=[w])


def mod_piece_mm(g, l, j4, w):
    kb = g.kb
    p = kb.ps()
    for jl in range(4):
        for kc in range(KC):
            kb.op("pe", lambda e, jl=jl, kc=kc: e.matmul(p[:, jl:jl + 1], w[:, kc, jl * 128:(jl + 1) * 128], g.csilu[:, kc:kc + 1],
                                                         start=(kc == 0), stop=(kc == KC - 1)), R=[w, g.csilu], W=[p])
    raw = g.modraw[l]
    kb.op("act", lambda e: e.activation(out=raw[:, j4 * 4:(j4 + 1) * 4], in_=p[:, 0:4], func=AF.Copy), R=[p], Wd=[raw])


def mod_piece(g, l, j4, w):
    mod_piece_load(g, l, j4, w)
    mod_piece_mm(g, l, j4, w)


def mod_finish(g, l):
    kb = g.kb
    with ExitStack() as ph:
        adab = load_cols(g, ph, "adab", g.inp["ada_b"].t[l], 48)
        n1w = load_cols(g, ph, "n1w", g.inp["norm1_w"].t[l], KC)
        n2w = load_cols(g, ph, "n2w", g.inp["norm2_w"].t[l], KC)
        modT, A1, A2 = g.modT_l[l], g.A1_l[l], g.A2_l[l]
        kb.op("dve", lambda e: e.tensor_tensor(out=modT[:], in0=g.modraw[l][:], in1=adab[:], op=ALU.add),
              R=[g.modraw[l], adab], W=[modT])
        kb.op("dve", lambda e: e.scalar_tensor_tensor(out=A1[:], in0=modT[:, 8:16], scalar=1.0, in1=n1w[:],
                                                      op0=ALU.add, op1=ALU.mult), R=[modT, n1w], W=[A1])
        kb.op("dve", lambda e: e.scalar_tensor_tensor(out=A2[:], in0=modT[:, 32:40], scalar=1.0, in1=n2w[:],
                                                      op0=ALU.add, op1=ALU.mult), R=[modT, n2w], W=[A2])
        kb.barrier()


def phase_mod(g, l):
    kb = g.kb
    with ExitStack() as ph:
        wt = [kb.sb(ph, f"adaw{i}", [128, KC, 512], BF16) for i in range(3)]
        for j4 in range(12):
            mod_piece(g, l, j4, wt[j4 % 3])
        kb.barrier()
    mod_finish(g, l)


def phase_norm(g, l, which):
    kb = g.kb
    A = g.A1 if which == 1 else g.A2
    sh0 = 0 if which == 1 else 24
    with ExitStack() as ph:
        sq = [kb.sb(ph, f"nsq{i}", [128, 512], BF16) for i in range(2)]
        rs = kb.sb(ph, "nrs", [128, 512], F32)
        tmp = [kb.sb(ph, f"ntmp{i}", [128, 512], F32) for i in range(2)]
        for tb in range(4):
            rstd_bcast(kb, ph, g.xT, g.xTb, tb, g.onesb, sq, rs)
            for c in range(KC):
                t = tmp[c % 2]
                kb.op("dve", lambda e, t=t, c=c: e.tensor_tensor(out=t[:], in0=g.xT[:, c, tb * 512:(tb + 1) * 512], in1=rs[:],
                                                                 op=ALU.mult), R=[g.xTb[c][tb], rs], W=[t])
                kb.op("act", lambda e, t=t, c=c: e.activation(
                    out=g.hT[:, c, tb * 512:(tb + 1) * 512], in_=t[:], func=AF.Identity,
                    scale=A[:, c:c + 1], bias=g.modT[:, sh0 + c:sh0 + c + 1]), R=[t, A, g.modT], W=[g.hTb[c][tb]])
        kb.barrier()


OFF = dict(z=0, xbc=1024, dt=2560, sbq=2576, sbk=3088, sbv=3600, dsq=4112, dsk=4624, dsv=4752,
           ixq=4880, ixk=5392, ixw=5456, gate=5464)


def rope(g, f3, nh, tt, rt):
    kb = g.kb
    fb = f3._buf
    cos = g.ropecs[:, tt:tt + 1, 0:8].broadcast_to([128, nh, 8])
    sin = g.ropecs[:, tt:tt + 1, 8:16].broadcast_to([128, nh, 8])
    x1 = f3.ap[:, :, 0:8]
    x2 = f3.ap[:, :, 8:16]
    t = [rt[:, i, 0:nh, :] for i in range(4)]
    kb.op("dve", lambda e: e.tensor_tensor(out=t[0], in0=x1, in1=cos, op=ALU.mult), R=[fb, g.ropecs], W=[rt])
    kb.op("dve", lambda e: e.tensor_tensor(out=t[1], in0=x2, in1=sin, op=ALU.mult), R=[fb, g.ropecs], W=[rt])
    kb.op("dve", lambda e: e.tensor_tensor(out=t[2], in0=x2, in1=cos, op=ALU.mult), R=[fb, g.ropecs], W=[rt])
    kb.op("dve", lambda e: e.tensor_tensor(out=t[3], in0=x1, in1=sin, op=ALU.mult), R=[fb, g.ropecs], W=[rt])
    kb.op("dve", lambda e: e.tensor_tensor(out=x1, in0=t[0], in1=t[1], op=ALU.subtract), R=[rt], W=[fb])
    kb.op("dve", lambda e: e.tensor_tensor(out=x2, in0=t[2], in1=t[3], op=ALU.add), R=[rt], W=[fb])


class V:
    def __init__(self, ap, buf):
        self.ap = ap
        self._buf = buf


def phase_inproj(g, l):
    kb, nc = g.kb, g.nc
    win = g.inp["w_in"].t[l]
    S = g.scr
    with ExitStack() as ph:
        wt = [kb.sb(ph, "winw", [128, KC, 512], BF16) for _ in range(2)]
        nw = [0]

        def getw(n0, N):
            w = wt[nw[0] % 2]
            nw[0] += 1
            kb.dma("pool", w[:, :, 0:N], win[:, n0:n0 + N].rearrange("(kc p) n -> p kc n", p=128), W=[w])
            return w

        stgb = [kb.sb(ph, "stgb", [128, 512], BF16) for _ in range(3)]
        stgf = [kb.sb(ph, "stgf", [128, 512], F32) for _ in range(2)]
        rt = kb.sb(ph, "ropetmp", [128, 4, 16, 8], F32)
        tst = [kb.sb(ph, "tst", [64, 8, 512], BF16) for _ in range(2)]
        cnt = dict(b=0, f=0, t=0)

        def nxt(lst, k):
            r = lst[cnt[k] % len(lst)]
            cnt[k] += 1
            return r

        def tok_mm(w, N, tt):
            p = kb.ps()
            for kc in range(KC):
                kb.op("pe", lambda e, kc=kc: e.matmul(p[:, 0:N], g.hT[:, kc, tt * 128:(tt + 1) * 128], w[:, kc, 0:N],
                                                      start=(kc == 0), stop=(kc == KC - 1)),
                      R=[g.hTb[kc][tt // 4], w], W=[p])
            return p

        def feat_mm(w, ci, tb):
            p = kb.ps()
            for kc in range(KC):
                kb.op("pe", lambda e, kc=kc: e.matmul(p[:, :], w[:, kc, ci * 128:(ci + 1) * 128],
                                                      g.hT[:, kc, tb * 512:(tb + 1) * 512],
                                                      start=(kc == 0), stop=(kc == KC - 1)),
                      R=[g.hTb[kc][tb], w], W=[p])
            return p

        for zi in range(2):
            w = getw(OFF["z"] + zi * 512, 512)
            for tt in range(NT):
                p = tok_mm(w, 512, tt)
                sb_ = nxt(stgb, "b")
                kb.op("act", lambda e: e.activation(out=sb_[:], in_=p[:, :], func=AF.Silu), R=[p], W=[sb_])
                kb.dma("sp", S["zs"].t[tt * 128:(tt + 1) * 128, zi * 512:(zi + 1) * 512], sb_[:], R=[sb_])
        w = getw(OFF["dt"], 16)
        for tt in range(NT):
            p = tok_mm(w, 16, tt)
            f = nxt(stgf, "f")
            kb.op("dve", lambda e: e.tensor_tensor(out=f[:, 0:16], in0=p[:, 0:16], in1=g.dtb_bc[:], op=ALU.add),
                  R=[p, g.dtb_bc], W=[f])
            kb.op("act", lambda e: e.activation(out=f[:, 0:16], in_=f[:, 0:16], func=AF.Exp), R=[f], W=[f])
            kb.op("act", lambda e: e.activation(out=g.dt_tok[:, tt, :], in_=f[:, 0:16], func=AF.Ln, bias=1.0),
                  R=[f], W=[g.dt_tok])
        w = getw(OFF["sbv"], 512)
        for tt in range(NT):
            p = tok_mm(w, 512, tt)
            sb_ = nxt(stgb, "b")
            kb.op("act", lambda e: e.activation(out=sb_[:], in_=p[:, :], func=AF.Copy), R=[p], W=[sb_])
            kb.dma("sp", S["sbv"].t[tt * 128:(tt + 1) * 128, :], sb_[:], R=[sb_])

        def roped_group(n0, N, nh, dstT, extra=None):
            w = getw(n0, N)
            fs, bs, sts = {}, {}, {}

            def stA(tt):
                p = tok_mm(w, N, tt)
                f = nxt(stgf, "f")
                kb.op("act", lambda e: e.activation(out=f[:, 0:N], in_=p[:, 0:N], func=AF.Copy), R=[p], W=[f])
                fs[tt] = f

            def stB(tt):
                f = fs.pop(tt)
                rope(g, V(f[:, 0:nh * 64].rearrange("p (h d) -> p h d", d=64), f), nh, tt, rt)
                b = nxt(stgb, "b")
                kb.op("act", lambda e: e.activation(out=b[:, 0:N], in_=f[:, 0:N], func=AF.Copy), R=[f], W=[b])
                if extra is not None:
                    extra(tt, f, b)
                bs[tt] = b

            def stC(tt):
                b = bs.pop(tt)
                pt = kb.ps()
                ptb = pt[:, :].bitcast(BF16)
                for h in range(nh):
                    kb.op("pe", lambda e, h=h: e.transpose(ptb[0:64, h * 128:(h + 1) * 128], b[:, h * 64:(h + 1) * 64], g.identb[:]),
                          R=[b, g.identb], W=[pt])
                if tt % 4 == 0:
                    sts[tt // 4] = nxt(tst, "t")
                st = sts[tt // 4]
                kb.op("act", lambda e: e.activation(
                    out=st[0:64, 0:nh, (tt % 4) * 128:(tt % 4 + 1) * 128],
                    in_=ptb[0:64, 0:nh * 128].rearrange("p (h t) -> p h t", h=nh), func=AF.Copy), R=[pt], Wd=[st])
                if tt % 4 == 3:
                    tb = tt // 4
                    kb.dma("sp", dstT.t[:, :, tb * 512:(tb + 1) * 512].rearrange("h d t -> d h t"), st[0:64, 0:nh, :], R=[st])

            for step in range(NT + 2):
                if step < NT:
                    stA(step)
                if 0 <= step - 1 < NT:
                    stB(step - 1)
                if 0 <= step - 2 < NT:
                    stC(step - 2)

        roped_group(OFF["dsq"], 512, 8, S["dqT"])

        def dsv_extra(tt, f, b):
            kb.dma("sp", S["dv"].t[tt * 128:(tt + 1) * 128, :], b[:, 128:256], R=[b])
        roped_group(OFF["dsk"], 256, 2, S["dkT"], dsv_extra)
        roped_group(OFF["ixq"], 512, 8, S["iqT"])

        def ixw_extra(tt, f, b):
            kb.op("dve", lambda e: e.tensor_copy(out=g.wix_tok[:, tt, :], in_=f[:, 64:72]), R=[f], W=[g.wix_tok])
        roped_group(OFF["ixk"], 72, 1, S["ikT"], ixw_extra)

        def feat_group(n0, dstT, c0, func):
            w = getw(n0, 512)
            for ci in range(4):
                for tb in range(4):
                    p = feat_mm(w, ci, tb)
                    sb_ = nxt(stgb, "b")
                    kb.op("act", lambda e: e.activation(out=sb_[:], in_=p[:, :], func=func), R=[p], W=[sb_])
                    kb.dma("sp", dstT.t[c0 + ci, :, tb * 512:(tb + 1) * 512], sb_[:], R=[sb_])

        feat_group(OFF["sbq"], S["sbqT"], 0, AF.Copy)
        feat_group(OFF["sbk"], S["sbkT"], 0, AF.Copy)
        for gi in range(6):
            feat_group(OFF["gate"] + gi * 512, S["gT"], gi * 4, AF.Sigmoid)

        kb.barrier()

    with ExitStack() as ph:
        wt = [kb.sb(ph, "winw", [128, KC, 512], BF16) for _ in range(2)]
        nw = [0]
        xpad = [kb.sb(ph, "xpad", [128, 3 + L], F32) for _ in range(1)]
        cva = [kb.sb(ph, "cva", [128, L], F32) for _ in range(1)]
        cvo = [kb.sb(ph, "cvo", [128, L], BF16) for _ in range(2)]
        for xp in xpad:
            kb.op("dve", lambda e, xp=xp: e.memset(xp[:, 0:3], 0.0), W=[xp])
        for gi in range(3):
            w = getw(OFF["xbc"] + gi * 512, 512)
            for ci in range(4):
                cidx = gi * 4 + ci
                xp, acc = xpad[0], cva[0]
                for tb in range(4):
                    p = feat_mm(w, ci, tb)
                    kb.op("act", lambda e: e.activation(out=xp[:, 3 + tb * 512:3 + (tb + 1) * 512], in_=p[:, :], func=AF.Copy),
                          R=[p], Wd=[xp])
                kb.op("act", lambda e: e.activation(out=acc[:], in_=xp[:, 3:3 + L], func=AF.Identity,
                                                    scale=g.cw[3][:, cidx:cidx + 1], bias=g.cb[:, cidx:cidx + 1]),
                      R=[xp, g.cw[3], g.cb], W=[acc])
                for k in (2, 1, 0):
                    kb.op("dve", lambda e, k=k: e.scalar_tensor_tensor(out=acc[:], in0=xp[:, k:k + L], scalar=g.cw[k][:, cidx:cidx + 1],
                                                                       in1=acc[:], op0=ALU.mult, op1=ALU.add),
                          R=[xp, g.cw[k], acc], W=[acc])
                if cidx < 8:
                    o = cvo[cidx % 2]
                    ov = o[:, :]
                else:
                    o = g.bcT
                    ov = g.bcT[:, cidx - 8, :]
                kb.op("act", lambda e: e.activation(out=ov, in_=acc[:], func=AF.Silu), R=[acc], W=[o])
                if cidx < 10:
                    for half in range(2):
                        pt = kb.ps()
                        ptb = pt[:, :].bitcast(BF16)
                        for j in range(8):
                            tt = half * 8 + j
                            kb.op("pe", lambda e, j=j, tt=tt: e.transpose(ptb[:, j * 128:(j + 1) * 128], ov[:, tt * 128:(tt + 1) * 128],
                                                                          g.identb[:]), R=[o, g.identb], W=[pt])
                        if cidx < 8:
                            dst, dt_ = g.xs_tok[:, half * 8:(half + 1) * 8, cidx * 128:(cidx + 1) * 128], g.xs_tok
                        else:
                            dst, dt_ = g.bt_tok[:, half * 8:(half + 1) * 8, (cidx - 8) * 128:(cidx - 7) * 128], g.bt_tok
                        kb.op("dve", lambda e: e.tensor_copy(out=dst, in_=ptb.rearrange("p (j f) -> p j f", j=8)), R=[pt], W=[dt_])
        kb.barrier()


def phase_ssd(g, l):
    kb, nc = g.kb, g.nc
    S = g.scr
    with ExitStack() as ph:
        def sb(name, shape, dt_=F32):
            return kb.sb(ph, name, shape, dt_)
        a_bc = sb("a_bc", [128, 16])
        dsk_bc = sb("dsk_bc", [128, 16])
        nw_bc = sb("nw_bc", [128, 1024])
        Dmat = sb("Dmat", [128, 16, 128], BF16)
        with nc.allow_non_contiguous_dma(reason="tiny vectors"):
            kb.dma("sp", a_bc[:], g.inp["a_log"].t[l].partition_broadcast(128), W=[a_bc])
            kb.dma("sp", dsk_bc[:], g.inp["d_skip"].t[l].partition_broadcast(128), W=[dsk_bc])
            kb.dma("sp", nw_bc[:], g.inp["ssd_norm_w"].t[l].partition_broadcast(128), W=[nw_bc])
        kb.op("act", lambda e: e.activation(out=a_bc[:], in_=a_bc[:], func=AF.Exp), R=[a_bc], W=[a_bc])
        kb.op("dve", lambda e: e.tensor_scalar(out=a_bc[:], in0=a_bc[:], scalar1=-1.0, scalar2=None, op0=ALU.mult), R=[a_bc], W=[a_bc])
        for h in range(16):
            kb.op("dve", lambda e, h=h: e.tensor_scalar(out=Dmat[:, h, :], in0=g.ident[:], scalar1=dsk_bc[:, h:h + 1], scalar2=None,
                                                        op0=ALU.mult), R=[g.ident, dsk_bc], Wd=[Dmat])
        NB = 2
        X = [sb("X", [128, 2, 16, 128], BF16)] * 2
        dd = [sb("dd", [128, 16, 128])] * 2
        Mt = [sb("Mt", [128, 16, 128], BF16) for _ in range(NB)]
        xw = [sb("xw", [128, 1024], BF16) for _ in range(NB)]
        zst = [sb("zst", [128, 1024], BF16) for _ in range(NB)]
        yy = [sb("yy", [128, 1024])] * 2
        yz = yy
        ss = [sb("ss", [128, 1]) for _ in range(NB)]
        yn = [sb("yn", [128, 1024], BF16) for _ in range(NB)]
        yst = [sb("yst", [128, KC, 256], BF16)] * 2
        hst = [sb("hst", [128, 512]) for _ in range(2)]
        htmp = [sb("htmp", [128, 512]) for _ in range(2)]
        prevb = [[sb("prevb", [128, 512], BF16) for _ in range(2)] for _ in range(2)]
        ytmp = [sb("ytmp", [128, 512]) for _ in range(2)]

        dA_all = sb("dA_all", [128, 256])
        dAs_all = sb("dAs_all", [128, 2, 256], BF16)
        ac_all = sb("ac_all", [128, 256])
        eac_all = sb("eac_all", [128, 256])
        cd_all = sb("cd_all", [128, 256])
        wgt_all = sb("wgt_all", [128, 256])
        v_all = sb("v_all", [128, 256])
        vs_all = sb("vs_all", [128, 2, 256], BF16)
        acs_all = sb("acs_all", [128, 2, 256], BF16)
        dtf = g.dt_tok[:, :, :].rearrange("p c h -> p (c h)")
        kb.op("dve", lambda e: e.tensor_tensor(out=dA_all[:, :].rearrange("p (c h) -> p c h", h=16), in0=g.dt_tok[:, :, :],
                                               in1=a_bc[:, :].unsqueeze(1).broadcast_to([128, NT, 16]), op=ALU.mult),
              R=[g.dt_tok, a_bc], W=[dA_all])
        kb.op("dve", lambda e: e.tensor_copy(out=dAs_all[:, 0, :], in_=dA_all[:]), R=[dA_all], W=[dAs_all])
        kb.op("dve", lambda e: e.tensor_tensor(out=dAs_all[:, 1, :], in0=dA_all[:], in1=dAs_all[:, 0, :], op=ALU.subtract),
              R=[dA_all, dAs_all], W=[dAs_all])
        pac = kb.ps(pin=True)
        for c in range(NT):
            for half, lhs in ((0, g.triub), (1, g.onesb)):
                for hl_ in range(2):
                    kb.op("pe", lambda e, c=c, half=half, lhs=lhs, hl_=hl_: e.matmul(
                        pac[:, half * 256 + c * 16:half * 256 + (c + 1) * 16], lhs[:], dAs_all[:, hl_, c * 16:(c + 1) * 16],
                        start=(hl_ == 0), stop=(hl_ == 1)), R=[lhs, dAs_all], W=[pac])
        kb.unpin(pac)
        kb.op("dve", lambda e: e.tensor_copy(out=ac_all[:], in_=pac[:, 0:256]), R=[pac], W=[ac_all])
        kb.op("act", lambda e: e.activation(out=eac_all[:], in_=pac[:, 0:256], func=AF.Exp), R=[pac], W=[eac_all])
        kb.op("act", lambda e: e.activation(out=cd_all[:], in_=pac[:, 256:512], func=AF.Exp), R=[pac], W=[cd_all])
        kb.op("dve", lambda e: e.tensor_tensor(out=wgt_all[:], in0=pac[:, 256:512], in1=ac_all[:], op=ALU.subtract), R=[pac, ac_all], W=[wgt_all])
        kb.op("act", lambda e: e.activation(out=wgt_all[:], in_=wgt_all[:], func=AF.Exp), R=[wgt_all], W=[wgt_all])
        kb.op("dve", lambda e: e.tensor_tensor(out=wgt_all[:], in0=wgt_all[:], in1=dtf, op=ALU.mult), R=[wgt_all, g.dt_tok], W=[wgt_all])
        kb.op("act", lambda e: e.activation(out=v_all[:], in_=dtf, func=AF.Ln), R=[g.dt_tok], W=[v_all])
        kb.op("dve", lambda e: e.tensor_tensor(out=v_all[:], in0=v_all[:], in1=ac_all[:], op=ALU.subtract), R=[v_all, ac_all], W=[v_all])
        kb.op("dve", lambda e: e.tensor_copy(out=vs_all[:, 0, :], in_=v_all[:]), R=[v_all], W=[vs_all])
        kb.op("dve", lambda e: e.tensor_tensor(out=vs_all[:, 1, :], in0=v_all[:], in1=vs_all[:, 0, :], op=ALU.subtract), R=[v_all, vs_all], W=[vs_all])
        kb.op("dve", lambda e: e.tensor_copy(out=acs_all[:, 0, :], in_=ac_all[:]), R=[ac_all], W=[acs_all])
        kb.op("dve", lambda e: e.tensor_tensor(out=acs_all[:, 1, :], in0=ac_all[:], in1=acs_all[:, 0, :], op=ALU.subtract), R=[ac_all, acs_all], W=[acs_all])

        pcbs = {}

        def P1(c):
            k = c % NB
            tsl = slice(c * 128, (c + 1) * 128)
            c16 = slice(c * 16, (c + 1) * 16)
            kb.op("dve", lambda e: e.tensor_tensor(
                out=xw[k][:, :].rearrange("p (h d) -> p h d", d=64), in0=g.xs_tok[:, c, :].rearrange("p (h d) -> p h d", d=64),
                in1=wgt_all[:, c16].unsqueeze(2).broadcast_to([128, 16, 64]), op=ALU.mult), R=[g.xs_tok, wgt_all], W=[xw[k]])
            for hl_ in range(2):
                kb.op("dve", lambda e, hl_=hl_: e.tensor_tensor(
                    out=X[k][:, hl_, :, :], in0=g.identb[:, :].unsqueeze(1).broadcast_to([128, 16, 128]),
                    in1=acs_all[:, hl_, c16].unsqueeze(2).broadcast_to([128, 16, 128]), op=ALU.mult), R=[g.identb, acs_all], Wd=[X[k]])
            for q4 in range(4):
                pb = kb.ps()
                h4 = slice(c * 16 + q4 * 4, c * 16 + q4 * 4 + 4)
                for hl_ in range(2):
                    kb.op("pe", lambda e, hl_=hl_: e.matmul(pb[:, :], g.onesb[:], X[k][:, hl_, q4 * 4:(q4 + 1) * 4, :],
                                                            start=(hl_ == 0), stop=False), R=[g.onesb, X[k]], W=[pb])
                for hl_ in range(2):
                    kb.op("pe", lambda e, hl_=hl_: e.matmul(pb[:, :], g.identb[:], vs_all[:, hl_, h4].unsqueeze(2).broadcast_to([128, 4, 128]),
                                                            start=False, stop=False), R=[g.identb, vs_all], W=[pb])
                kb.op("pe", lambda e: e.matmul(pb[:, :], g.identb[:], g.negmb[:, :].unsqueeze(1).broadcast_to([128, 4, 128]),
                                               start=False, stop=True), R=[g.identb, g.negmb], W=[pb])
                kb.op("act", lambda e, q4=q4, pb=pb: e.activation(out=dd[k][:, q4 * 4:(q4 + 1) * 4, :],
                                                                  in_=pb[:, :].rearrange("p (h i) -> p h i", h=4), func=AF.Exp),
                      R=[pb], Wd=[dd[k]])
            pcb = kb.ps()
            for gg in range(2):
                kb.op("pe", lambda e, gg=gg: e.matmul(pcb[:, gg * 128:(gg + 1) * 128], g.bcT[:, gg, tsl], g.bcT[:, 2 + gg, tsl],
                                                      start=True, stop=True), R=[g.bcT], W=[pcb])
            pcbs[c] = pcb

        def P1b(c):
            k = c % NB
            pcb = pcbs.pop(c)
            for gg in range(2):
                kb.op("dve", lambda e, gg=gg: e.tensor_tensor(
                    out=Mt[k][:, gg * 8:(gg + 1) * 8, :], in0=dd[k][:, gg * 8:(gg + 1) * 8, :],
                    in1=pcb[:, gg * 128:(gg + 1) * 128].unsqueeze(1).broadcast_to([128, 8, 128]), op=ALU.mult),
                    R=[dd[k], pcb], Wd=[Mt[k]])

        def P2(c):
            k = c % NB
            tsl = slice(c * 128, (c + 1) * 128)
            kb.dma("sp", zst[k][:], S["zs"].t[tsl, :], W=[zst[k]])
            for gg in range(2):
                pA = kb.ps()
                for hl in range(8):
                    h = gg * 8 + hl
                    xsl = g.xs_tok[:, c, h * 64:(h + 1) * 64]
                    kb.op("pe", lambda e, h=h, hl=hl, xsl=xsl: e.matmul(pA[:, hl * 64:(hl + 1) * 64], Mt[k][:, h, :], xsl, start=True, stop=False),
                          R=[Mt[k], g.xs_tok], W=[pA])
                    kb.op("pe", lambda e, h=h, hl=hl, xsl=xsl: e.matmul(pA[:, hl * 64:(hl + 1) * 64], Dmat[:, h, :], xsl, start=False, stop=True),
                          R=[Dmat, g.xs_tok], W=[pA])
                ysl = yy[k][:, gg * 512:(gg + 1) * 512]
                if c > 0:
                    pB = kb.ps()
                    pv = prevb[gg][(c - 1) % 2]
                    kb.op("pe", lambda e: e.matmul(pB[:, :], g.bcT[:, 2 + gg, tsl], pv[:], start=True, stop=True), R=[g.bcT, pv], W=[pB])
                    yt = ytmp[gg]
                    kb.op("dve", lambda e: e.tensor_tensor(
                        out=yt[:, :].rearrange("p (h d) -> p h d", d=64), in0=pB[:, :].rearrange("p (h d) -> p h d", d=64),
                        in1=eac_all[:, c * 16 + gg * 8:c * 16 + (gg + 1) * 8].unsqueeze(2).broadcast_to([128, 8, 64]), op=ALU.mult),
                        R=[pB, eac_all], W=[yt])
                    kb.op("dve", lambda e: e.tensor_tensor(out=ysl, in0=yt[:], in1=pA[:, :], op=ALU.add), R=[yt, pA], W=[yy[k]])
                else:
                    kb.op("act", lambda e: e.activation(out=ysl, in_=pA[:, :], func=AF.Copy), R=[pA], W=[yy[k]])
                if c < NT - 1:
                    pS = kb.ps()
                    kb.op("pe", lambda e: e.matmul(pS[:, :], g.bt_tok[:, c, gg * 128:(gg + 1) * 128], xw[k][:, gg * 512:(gg + 1) * 512],
                                                   start=True, stop=True), R=[g.bt_tok, xw[k]], W=[pS])
                    if c == 0:
                        kb.op("dve", lambda e: e.tensor_copy(out=hst[gg][:], in_=pS[:, :]), R=[pS], W=[hst[gg]])
                    else:
                        kb.op("dve", lambda e: e.tensor_tensor(
                            out=htmp[gg][:, :].rearrange("p (h d) -> p h d", d=64), in0=hst[gg][:, :].rearrange("p (h d) -> p h d", d=64),
                            in1=cd_all[:, c * 16 + gg * 8:c * 16 + (gg + 1) * 8].unsqueeze(2).broadcast_to([128, 8, 64]), op=ALU.mult),
                            R=[hst[gg], cd_all], W=[htmp[gg]])
                        kb.op("dve", lambda e: e.tensor_tensor(out=hst[gg][:], in0=htmp[gg][:], in1=pS[:, :], op=ALU.add),
                              R=[htmp[gg], pS], W=[hst[gg]])
                    pn = prevb[gg][c % 2]
                    kb.op("act", lambda e: e.activation(out=pn[:], in_=hst[gg][:], func=AF.Copy), R=[hst[gg]], W=[pn])

        def P3(c):
            k = c % NB
            tsl = slice(c * 128, (c + 1) * 128)
            kb.op("dve", lambda e: e.tensor_tensor(out=yz[k][:], in0=yy[k][:], in1=zst[k][:], op=ALU.mult), R=[yy[k], zst[k]], W=[yz[k]])
            kb.op("act", lambda e: e.activation(out=yn[k][:], in_=yz[k][:], func=AF.Square, accum_out=ss[k][:]), R=[yz[k]], W=[yn[k], ss[k]])
            kb.op("dve", lambda e: e.tensor_scalar(out=ss[k][:], in0=ss[k][:], scalar1=1.0 / 1024, scalar2=EPS, op0=ALU.mult, op1=ALU.add),
                  R=[ss[k]], W=[ss[k]])
            kb.op("act", lambda e: e.activation(out=ss[k][:], in_=ss[k][:], func=AF.Ln), R=[ss[k]], W=[ss[k]])
            kb.op("act", lambda e: e.activation(out=ss[k][:], in_=ss[k][:], func=AF.Exp, scale=-0.5), R=[ss[k]], W=[ss[k]])
            kb.op("dve", lambda e: e.scalar_tensor_tensor(out=yn[k][:], in0=yz[k][:], scalar=ss[k][:, 0:1], in1=nw_bc[:],
                                                          op0=ALU.mult, op1=ALU.mult), R=[yz[k], ss[k], nw_bc], W=[yn[k]])
            pt = kb.ps()
            ptb = pt[:, :].bitcast(BF16)
            for kc in range(KC):
                kb.op("pe", lambda e, kc=kc: e.transpose(ptb[:, kc * 128:(kc + 1) * 128], yn[k][:, kc * 128:(kc + 1) * 128], g.identb[:]),
                      R=[yn[k], g.identb], W=[pt])
            st = yst[(c // 2) % 2]
            kb.op("act", lambda e: e.activation(out=st[:, :, (c % 2) * 128:(c % 2 + 1) * 128],
                                                in_=ptb.rearrange("p (kc t) -> p kc t", kc=KC), func=AF.Copy), R=[pt], Wd=[st])
            if c % 2 == 1:
                t2 = c // 2
                kb.dma("sp", S["yT"].t[0:8, :, t2 * 256:(t2 + 1) * 256].rearrange("kc p t -> p kc t"), st[:], R=[st])

        for step in range(-1, NT + 1):
            if 0 <= step + 1 < NT:
                P1(step + 1)
            if 0 <= step - 1 < NT:
                P3(step - 1)
            if 0 <= step + 1 < NT:
                P1b(step + 1)
            if 0 <= step < NT:
                P2(step)
        kb.barrier()


def phase_sb(g, l):
    kb, nc = g.kb, g.nc
    S = g.scr
    with ExitStack() as ph:
        def sb(name, shape, dt_=F32):
            return kb.sb(ph, name, shape, dt_)
        qT = sb("sbq", [128, 4, L], BF16)
        kT = sb("sbk", [128, 4, L], BF16)
        v = sb("sbv", [128, NT, 512], BF16)
        onesw = sb("onesw", [128, L], BF16)
        kb.op("dve", lambda e: e.memset(onesw[:], 1.0), W=[onesw])
        kb.dma("sp", qT[:], S["sbqT"].t.rearrange("c p t -> p c t"), W=[qT])
        kb.dma("sp", kT[:], S["sbkT"].t.rearrange("c p t -> p c t"), W=[kT])
        kb.dma("sp", v[:], S["sbv"].t.rearrange("(t p) f -> p t f", p=128), W=[v])
        e1b = [sb("e1", [128, L]) for _ in range(3)]
        spb = [sb("sp", [128, L]) for _ in range(2)]
        Fb = [sb("F", [128, L + 1]) for _ in range(2)]
        attb = [sb("att", [128, L], BF16) for _ in range(2)]
        attT = [sb("attT", [128, NT, 128], BF16) for _ in range(2)]
        ftn = [sb("ftn", [128, 1]) for _ in range(2)]
        yst = [sb("yst", [128, 128], BF16) for _ in range(4)]
        for F in Fb:
            kb.op("dve", lambda e, F=F: e.memset(F[:, 0:1], 0.0), W=[F])
        iters = [(qt, h) for qt in range(NT) for h in range(8)]

        def s1(i):
            qt, h = iters[i]
            W = 128 * (qt + 1)
            nch = (W + 511) // 512
            dsl = slice(qt * 128, (qt + 1) * 128)
            hp, hc = (h % 2) * 64, h // 2
            e1, sp_ = e1b[i % 3], spb[i % 2]
            for ch in range(nch):
                n = min(512, W - ch * 512)
                p = kb.ps()
                kb.op("pe", lambda e, p=p, ch=ch, n=n: e.matmul(p[:, 0:n], qT[hp:hp + 64, hc, dsl], kT[hp:hp + 64, hc, ch * 512:ch * 512 + n],
                                                                start=True, stop=True), R=[qT, kT], W=[p])
                kb.op("act", lambda e, p=p, ch=ch, n=n: e.activation(out=e1[:, ch * 512:ch * 512 + n], in_=p[:, 0:n], func=AF.Exp, scale=0.125),
                      R=[p], Wd=[e1])
            kb.op("act", lambda e: e.activation(out=sp_[:, 0:W], in_=e1[:, 0:W], func=AF.Ln, bias=1.0), R=[e1], W=[sp_])

        def s2(i):
            qt, h = iters[i]
            k = i % 2
            W = 128 * (qt + 1)
            dsl = slice(qt * 128, (qt + 1) * 128)
            sp_, F = spb[k], Fb[k]
            kb.op("dve", lambda e: e.tensor_tensor(out=sp_[:, dsl], in0=sp_[:, dsl], in1=g.strl[:], op=ALU.mult), R=[sp_, g.strl], W=[sp_])
            kb.op("dve", lambda e: e.memset(F[:, 0:1], 0.0), W=[F])
            kb.op("dve", lambda e: e.tensor_tensor_scan(out=F[:, 1:W + 1], data0=onesw[:, 0:W], data1=sp_[:, 0:W], initial=0.0,
                                                        op0=ALU.mult, op1=ALU.add), R=[onesw, sp_], W=[F])
            kb.op("dve", lambda e: e.tensor_scalar(out=ftn[k][:], in0=F[:, W:W + 1], scalar1=-1.0, scalar2=None, op0=ALU.mult),
                  R=[F], W=[ftn[k]])

        def s3a(i):
            qt, h = iters[i]
            k = i % 2
            W = 128 * (qt + 1)
            F = Fb[k]
            kb.op("act", lambda e: e.activation(out=F[:, 0:W], in_=F[:, 0:W], func=AF.Exp, bias=ftn[k][:, 0:1]), R=[F, ftn[k]], W=[F])

        def s3b(i):
            qt, h = iters[i]
            k = i % 2
            W = 128 * (qt + 1)
            dsl = slice(qt * 128, (qt + 1) * 128)
            e1, F, att = e1b[i % 3], Fb[k], attb[k]
            kb.op("dve", lambda e: e.tensor_tensor(out=att[:, 0:W], in0=e1[:, 0:W], in1=F[:, 0:W], op=ALU.mult), R=[e1, F], W=[att])
            kb.op("dve", lambda e: e.tensor_tensor(out=att[:, dsl], in0=att[:, dsl], in1=g.strlb[:], op=ALU.mult), R=[att, g.strlb], W=[att])

        def s3c(i):
            qt, h = iters[i]
            k = i % 2
            att, aT = attb[k], attT[k]
            for b0 in range(0, qt + 1, 8):
                nb = min(8, qt + 1 - b0)
                pt = kb.ps()
                ptb = pt[:, :].bitcast(BF16)
                for j in range(nb):
                    kb.op("pe", lambda e, j=j, b0=b0: e.transpose(ptb[:, j * 128:(j + 1) * 128], att[:, (b0 + j) * 128:(b0 + j + 1) * 128], g.identb[:]),
                          R=[att, g.identb], W=[pt])
                kb.op("act", lambda e, b0=b0, nb=nb: e.activation(out=aT[:, b0:b0 + nb, :], in_=ptb[:, 0:nb * 128].rearrange("p (j t) -> p j t", j=nb),
                                                                 func=AF.Copy), R=[pt], Wd=[aT])

        def s3d(i):
            qt, h = iters[i]
            k = i % 2
            dsl = slice(qt * 128, (qt + 1) * 128)
            hp, hc = (h % 2) * 64, h // 2
            aT = attT[k]
            py = kb.ps()
            for sbk in range(qt + 1):
                kb.op("pe", lambda e, sbk=sbk: e.matmul(py[0:64, 0:128], v[:, sbk, h * 64:(h + 1) * 64], aT[:, sbk, :],
                                                        start=(sbk == 0), stop=(sbk == qt)), R=[v, aT], W=[py])
            ys = yst[i % 4]
            kb.op("act", lambda e: e.activation(out=ys[hp:hp + 64, :], in_=py[0:64, 0:128], func=AF.Copy), R=[py], W=[ys])
            kb.dma("sp", S["yT"].t[8 + hc, hp:hp + 64, dsl], ys[hp:hp + 64, :], R=[ys])

        n_it = len(iters)

        def run(fn, i):
            if 0 <= i < n_it:
                fn(i)
        for step in range(n_it + 4):
            run(s3d, step - 4)
            run(s3a, step - 2)
            run(s1, step)
            run(s3c, step - 3)
            run(s2, step - 1)
            run(s3b, step - 2)
        kb.barrier()


NBIS = 16


def phase_dsa(g, l):
    kb, nc = g.kb, g.nc
    S = g.scr
    with ExitStack() as ph:
        def sb(name, shape, dt_=F32):
            return kb.sb(ph, name, shape, dt_)
        dk = sb("dk", [64, 2, L], BF16)
        ik = sb("ik", [64, L], BF16)
        vaug = sb("vaug", [128, NT, 2, 128], BF16)
        kb.dma("sp", dk[:], S["dkT"].t.rearrange("h d t -> d h t"), W=[dk])
        kb.dma("sp", ik[:], S["ikT"].t[0], W=[ik])
        kb.op("dve", lambda e: e.memset(vaug[:], 1.0), W=[vaug])
        for gg in range(2):
            kb.dma("sp", vaug[:, :, gg, 0:64], S["dv"].t[:, gg * 64:(gg + 1) * 64].rearrange("(t p) d -> p t d", p=128), W=[vaug])
        dqt = [sb("dqt", [64, 8, 128], BF16) for _ in range(2)]
        iqt = [sb("iqt", [64, 8, 128], BF16) for _ in range(2)]
        score = [sb("score", [128, L]) for _ in range(2)]
        rl = [sb("rl", [128, 512], BF16) for _ in range(4)]
        wabs = [sb("wabs", [128, 8]) for _ in range(2)]
        wsgn = [sb("wsgn", [128, 8]) for _ in range(2)]
        Dg = [sb("Dg", [128, 8, 128], BF16) for _ in range(2)]
        junk = sb("junk", [128, L], BF16)
        maskb = [sb("maskb", [128, L], BF16) for _ in range(2)]
        maskT = [sb("maskT", [128, NT, 128], BF16) for _ in range(2)]
        Eb = [sb("E", [128, 512], BF16) for _ in range(3)]
        M_ = [sb("M", [128, 1]) for _ in range(2)]
        A_ = [sb("A", [128, NBIS + 1]) for _ in range(2)]
        mid = [sb("mid", [128, 1]) for _ in range(2)]
        cnt = [sb("cnt", [128, 1]) for _ in range(2)]
        sela = [sb("sela", [128, 1]) for _ in range(2)]
        R0 = [sb("R0", [64, 512]) for _ in range(2)]
        ytmp = [sb("ytmp", [64, 512], BF16) for _ in range(2)]
        yraw = [sb("yraw", [64, 512]) for _ in range(2)]
        yst = [sb("dyst", [128, 4, 128], BF16) for _ in range(2)]
        cn = dict(rl=0, E=0)

        def stage_I(qt):
            k = qt % 2
            W = 128 * (qt + 1)
            dsl = slice(qt * 128, (qt + 1) * 128)
            if qt < 2:
                return
            kb.dma("sp", iqt[k][:], S["iqT"].t[:, :, dsl].rearrange("h d t -> d h t"), W=[iqt[k]])
            sc = score[k]
            nch = (W + 511) // 512
            wv = g.wix_tok[:, qt, :]
            kb.op("act", lambda e: e.activation(out=wabs[k][:], in_=wv, func=AF.Abs), R=[g.wix_tok], W=[wabs[k]])
            kb.op("dve", lambda e: e.tensor_scalar(out=wsgn[k][:], in0=wv, scalar1=0.0, scalar2=2.0, op0=ALU.is_ge, op1=ALU.mult),
                  R=[g.wix_tok], W=[wsgn[k]])
            kb.op("dve", lambda e: e.tensor_scalar(out=wsgn[k][:], in0=wsgn[k][:], scalar1=-1.0, scalar2=None, op0=ALU.add),
                  R=[wsgn[k]], W=[wsgn[k]])
            for h in range(8):
                kb.op("dve", lambda e, h=h: e.tensor_scalar(out=Dg[k][:, h, :], in0=g.identb[:], scalar1=wsgn[k][:, h:h + 1], scalar2=None,
                                                            op0=ALU.mult), R=[g.identb, wsgn[k]], Wd=[Dg[k]])
            items = [(ch, h) for ch in range(nch) for h in range(8)]
            pend = {}
            pscs = {}

            def mm(j):
                ch, h = items[j]
                n = min(512, W - ch * 512)
                p = kb.ps()
                kb.op("pe", lambda e: e.matmul(p[:, 0:n], iqt[k][0:64, h, :], ik[0:64, ch * 512:ch * 512 + n], start=True, stop=True),
                      R=[iqt[k], ik], W=[p])
                r = rl[cn["rl"] % 4]
                cn["rl"] += 1
                kb.op("act", lambda e: e.activation(out=r[:, 0:n], in_=p[:, 0:n], func=AF.Relu, scale=wabs[k][:, h:h + 1]),
                      R=[p, wabs[k]], W=[r])
                pend[j] = r

            def acc(j):
                ch, h = items[j]
                n = min(512, W - ch * 512)
                r = pend.pop(j)
                if h == 0:
                    pscs[ch] = kb.ps(pin=True)
                psc = pscs[ch]
                kb.op("pe", lambda e: e.matmul(psc[:, 0:n], Dg[k][:, h, :], r[:, 0:n], start=(h == 0), stop=(h == 7)),
                      R=[Dg[k], r], W=[psc])
                if h == 7:
                    kb.unpin(psc)
                    kb.op("act", lambda e: e.activation(out=sc[:, ch * 512:ch * 512 + n], in_=psc[:, 0:n], func=AF.Copy), R=[psc], Wd=[sc])

            for j in range(len(items) + 2):
                if j < len(items):
                    mm(j)
                if 0 <= j - 2 < len(items):
                    acc(j - 2)

        def stage_B(qt):
            k = qt % 2
            W = 128 * (qt + 1)
            dsl = slice(qt * 128, (qt + 1) * 128)
            mT = maskT[k]
            if qt < 2:
                if qt == 1:
                    kb.op("dve", lambda e: e.memset(mT[:, 0, :], 0.0), W=[mT])
                kb.op("dve", lambda e: e.tensor_copy(out=mT[:, qt, :], in_=g.negmb[:]), R=[g.negmb], W=[mT])
                return
            sc = score[k]
            kb.op("dve", lambda e: e.tensor_reduce(out=M_[k][:], in_=sc[:, 0:W], axis=AX.X, op=ALU.max, apply_absolute_value=True),
                  R=[sc], W=[M_[k]])
            kb.op("dve", lambda e: e.tensor_scalar(out=A_[k][:, 0:NBIS], in0=g.pow2[:, 0:NBIS], scalar1=M_[k][:, 0:1], scalar2=None, op0=ALU.mult),
                  R=[g.pow2, M_[k]], W=[A_[k]])
            kb.op("dve", lambda e: e.tensor_copy(out=A_[k][:, NBIS:NBIS + 1], in_=A_[k][:, NBIS - 1:NBIS]), R=[A_[k]], W=[A_[k]])
            kb.op("dve", lambda e: e.tensor_tensor(out=sc[:, dsl], in0=sc[:, dsl], in1=g.negup[:], op=ALU.add), R=[sc, g.negup], W=[sc])
            kb.op("dve", lambda e: e.memset(mid[k][:], 0.0), W=[mid[k]])
            for it in range(NBIS):
                kb.op("dve", lambda e: e.tensor_scalar(out=junk[:, 0:W], in0=sc[:, 0:W], scalar1=mid[k][:, 0:1], scalar2=0.0,
                                                       op0=ALU.is_ge, op1=ALU.add, accum_out=cnt[k][:]),
                      R=[sc, mid[k]], W=[junk, cnt[k]])
                kb.op("dve", lambda e, it=it: e.tensor_scalar(out=sela[k][:], in0=cnt[k][:], scalar1=255.5, scalar2=A_[k][:, it:it + 1],
                                                              op0=ALU.is_ge, op1=ALU.mult), R=[cnt[k], A_[k]], W=[sela[k]])
                kb.op("dve", lambda e, it=it: e.scalar_tensor_tensor(out=mid[k][:], in0=sela[k][:], scalar=A_[k][:, it + 1:it + 2], in1=mid[k][:],
                                                                     op0=ALU.subtract, op1=ALU.add), R=[sela[k], A_[k], mid[k]], W=[mid[k]])
            mb = maskb[k]
            kb.op("dve", lambda e: e.tensor_scalar(out=mb[:, 0:W], in0=sc[:, 0:W], scalar1=mid[k][:, 0:1], scalar2=-30000.0,
                                                   op0=ALU.is_lt, op1=ALU.mult), R=[sc, mid[k]], W=[mb])

        def stage_T(qt):
            if qt < 2:
                return
            k = qt % 2
            mb, mT = maskb[k], maskT[k]
            for b0 in range(0, qt + 1, 8):
                nb = min(8, qt + 1 - b0)
                pt = kb.ps()
                ptb = pt[:, :].bitcast(BF16)
                for j in range(nb):
                    kb.op("pe", lambda e, j=j, b0=b0: e.transpose(ptb[:, j * 128:(j + 1) * 128], mb[:, (b0 + j) * 128:(b0 + j + 1) * 128], g.identb[:]),
                          R=[mb, g.identb], W=[pt])
                kb.op("act", lambda e, b0=b0, nb=nb: e.activation(out=mT[:, b0:b0 + nb, :], in_=ptb[:, 0:nb * 128].rearrange("p (j t) -> p j t", j=nb),
                                                                 func=AF.Copy), R=[pt], Wd=[mT])

        def stage_A(qt):
            k = qt % 2
            dsl = slice(qt * 128, (qt + 1) * 128)
            mT = maskT[k]
            ys = yst[k]
            if qt + 1 < NT:
                kb.dma("sp", dqt[1 - k][:], S["dqT"].t[:, :, (qt + 1) * 128:(qt + 2) * 128].rearrange("h d t -> d h t"), W=[dqt[1 - k]])
            for gq in range(2):
                pO = kb.ps(pin=True)
                Es = {}

                def qk(sbk):
                    pS = kb.ps()
                    kb.op("pe", lambda e: e.matmul(pS[:, :], dk[0:64, gq, sbk * 128:(sbk + 1) * 128], dqt[k][0:64, 4 * gq:4 * gq + 4, :],
                                                   start=True, stop=False), R=[dk, dqt[k]], W=[pS])
                    kb.op("pe", lambda e: e.matmul(pS[:, :], g.identb[:], mT[:, sbk, :].unsqueeze(1).broadcast_to([128, 4, 128]),
                                                   start=False, stop=True), R=[g.identb, mT], W=[pS])
                    E = Eb[cn["E"] % 3]
                    cn["E"] += 1
                    kb.op("act", lambda e: e.activation(out=E[:], in_=pS[:, :], func=AF.Exp, scale=0.125), R=[pS], W=[E])
                    Es[sbk] = E

                def av(sbk):
                    E = Es.pop(sbk)
                    kb.op("pe", lambda e: e.matmul(pO[:, :], vaug[:, sbk, gq, :], E[:], start=(sbk == 0), stop=(sbk == qt)),
                          R=[vaug, E], W=[pO])

                for sbk in range(qt + 2):
                    if sbk <= qt:
                        qk(sbk)
                    if sbk >= 1:
                        av(sbk - 1)
                kb.unpin(pO)
                r0, yt = R0[gq], ytmp[gq]
                kb.op("act", lambda e: e.activation(out=r0[:], in_=pO[64:128, :], func=AF.Ln), R=[pO], W=[r0])
                kb.op("act", lambda e: e.activation(out=r0[:], in_=r0[:], func=AF.Exp, scale=-1.0), R=[r0], W=[r0])
                yr = yraw[gq]
                kb.op("act", lambda e: e.activation(out=yr[:], in_=pO[0:64, :], func=AF.Copy), R=[pO], W=[yr])
                kb.op("pool", lambda e: e.tensor_tensor(out=yt[:], in0=yr[:], in1=r0[:], op=ALU.mult), R=[yr, r0], W=[yt])
                for hh in range(4):
                    h = 4 * gq + hh
                    hp, hc = (h % 2) * 64, h // 2
                    kb.op("act", lambda e, hh=hh, hp=hp, hc=hc: e.activation(out=ys[hp:hp + 64, hc, :], in_=yt[:, hh * 128:(hh + 1) * 128], func=AF.Copy),
                          R=[yt], Wd=[ys])
            kb.dma("sp", S["yT"].t[12:16, :, dsl].rearrange("c p t -> p c t"), ys[:], R=[ys])

        kb.dma("sp", dqt[0][:], S["dqT"].t[:, :, 0:128].rearrange("h d t -> d h t"), W=[dqt[0]])
        stage_I(0)
        stage_B(0)
        stage_T(0)
        stage_I(1)
        modw = [sb("adaw", [128, KC, 512], BF16) for _ in range(2)] if g.dsa_hook is not None else None
        for qt in range(NT):
            if qt + 2 < NT:
                stage_I(qt + 2)
            stage_A(qt)
            if modw is not None and 1 <= qt < 13:
                mod_piece_load(g, g.dsa_hook, qt - 1, modw[(qt - 1) % 2])
            if modw is not None and 2 <= qt < 14:
                mod_piece_mm(g, g.dsa_hook, qt - 2, modw[(qt - 2) % 2])
            if qt + 1 < NT:
                stage_B(qt + 1)
                stage_T(qt + 1)
        kb.barrier()


def phase_merge(g, l):
    kb, nc = g.kb, g.nc
    S = g.scr
    with ExitStack() as ph:
        def sb(name, shape, dt_=F32):
            return kb.sb(ph, name, shape, dt_)
        yT = sb("yTall", [128, 16, L], BF16)
        yTb = [kb.buf() for _ in range(4)]
        for q in range(4):
            kb.dma("sp", yT[:, q * 4:(q + 1) * 4, :], S["yT"].t[q * 4:(q + 1) * 4].rearrange("c p t -> p c t"), W=[yTb[q]])
        mT = sb("mergedT", [128, KC, L], BF16)
        mTb = [[kb.buf() for tb in range(4)] for c in range(KC)]
        wbr = [sb("wbr", [128, 16, 256], BF16) for _ in range(2)]
        gt = [sb("gt", [128, 3, 512], BF16) for _ in range(2)]
        gbr = [sb("gbr", [128, 3, 512], BF16) for _ in range(2)]
        it = 0
        for c2 in range(4):
            w = wbr[c2 % 2]
            csl = slice(c2 * 256, (c2 + 1) * 256)
            kb.dma("pool", w[:, 0:8, :], g.inp["w_br_ssd"].t[l][:, csl].rearrange("(kc p) n -> p kc n", p=128), W=[w])
            kb.dma("pool", w[:, 8:12, :], g.inp["w_br_sb"].t[l][:, csl].rearrange("(kc p) n -> p kc n", p=128), W=[w])
            kb.dma("pool", w[:, 12:16, :], g.inp["w_br_dsa"].t[l][:, csl].rearrange("(kc p) n -> p kc n", p=128), W=[w])
            for cl in range(2):
                c = c2 * 2 + cl
                for tb in range(4):
                    k = it % 2
                    it += 1
                    tsl = slice(tb * 512, (tb + 1) * 512)
                    for br in range(3):
                        kb.dma("sp", gt[k][:, br, :], S["gT"].t[br * 8 + c, :, tsl], W=[gt[k]])
                    pbr = []
                    for br, (k0, k1) in enumerate(((0, 8), (8, 12), (12, 16))):
                        p = kb.ps()
                        for kc in range(k0, k1):
                            kb.op("pe", lambda e, p=p, kc=kc, k0=k0, k1=k1: e.matmul(
                                p[:, :], w[:, kc, cl * 128:(cl + 1) * 128], yT[:, kc, tsl], start=(kc == k0), stop=(kc == k1 - 1)),
                                R=[w, yTb[kc // 4]], W=[p])
                        pbr.append(p)
                    gb = gbr[k]
                    for br in range(3):
                        kb.op("dve", lambda e, br=br: e.tensor_tensor(out=gb[:, br, :], in0=pbr[br][:, :], in1=gt[k][:, br, :], op=ALU.mult),
                              R=[pbr[br], gt[k]], Wd=[gb])
                    pm = kb.ps()
                    for br in range(3):
                        kb.op("pe", lambda e, br=br: e.matmul(pm[:, :], g.identb[:], gb[:, br, :], start=(br == 0), stop=(br == 2)),
                              R=[g.identb, gb], W=[pm])
                    kb.op("act", lambda e: e.activation(out=mT[:, c, tsl], in_=pm[:, :], func=AF.Copy), R=[pm], W=[mTb[c][tb]])
        wo = [sb("wo", [128, KC, 256], BF16) for _ in range(2)]
        for c2 in range(4):
            w = wo[c2 % 2]
            kb.dma("pool", w[:], g.inp["w_out"].t[l][:, c2 * 256:(c2 + 1) * 256].rearrange("(kc p) n -> p kc n", p=128), W=[w])
            for cl in range(2):
                c = c2 * 2 + cl
                for tb in range(4):
                    tsl = slice(tb * 512, (tb + 1) * 512)
                    p = kb.ps()
                    for kc in range(KC):
                        kb.op("pe", lambda e, p=p, kc=kc: e.matmul(p[:, :], w[:, kc, cl * 128:(cl + 1) * 128], mT[:, kc, tsl],
                                                                   start=(kc == 0), stop=(kc == KC - 1)), R=[w, mTb[kc][tb]], W=[p])
                    xs = g.xT[:, c, tsl]
                    kb.op("dve", lambda e, p=p, c=c, xs=xs: e.scalar_tensor_tensor(
                        out=xs, in0=p[:, :], scalar=g.modT[:, 16 + c:17 + c], in1=xs, op0=ALU.mult, op1=ALU.add),
                        R=[p, g.modT, g.xTb[c][tb]], W=[g.xTb[c][tb]])
        kb.barrier()


def phase_mlp(g, l):
    kb = g.kb
    with ExitStack() as ph:
        wup = [kb.sb(ph, f"wup{i}", [128, KC, 512], BF16) for i in range(2)]
        wdn = [kb.sb(ph, f"wdn{i}", [128, 4, D], BF16) for i in range(2)]
        act = [kb.sb(ph, f"mact{i}", [128, 4, L], BF16) for i in range(2)]
        actb = [[[kb.buf() for tb in range(4)] for hc in range(4)] for i in range(2)]
        rl = [kb.sb(ph, f"mrl{i}", [128, 512], F32) for i in range(2)]
        nrl = 0
        for gi in range(8):
            wu, wd, a, ab = wup[gi % 2], wdn[gi % 2], act[gi % 2], actb[gi % 2]
            load_w(g, wu, g.inp["w_up"].t[l][:, gi * 512:(gi + 1) * 512])
            load_w(g, wd, g.inp["w_down"].t[l][gi * 512:(gi + 1) * 512, :])
            for tb in range(4):
                for hc in range(4):
                    p = kb.ps()
                    for kc in range(KC):
                        kb.op("pe", lambda e, p=p, kc=kc, hc=hc: e.matmul(
                            p[:, :], wu[:, kc, hc * 128:(hc + 1) * 128], g.hT[:, kc, tb * 512:(tb + 1) * 512],
                            start=(kc == 0), stop=(kc == KC - 1)), R=[wu, g.hTb[kc][tb]], W=[p])
                    r = rl[nrl % 2]
                    nrl += 1
                    kb.op("act", lambda e, p=p, r=r: e.activation(out=r[:], in_=p[:, :], func=AF.Relu), R=[p], W=[r])
                    kb.op("act", lambda e, r=r, hc=hc: e.activation(out=a[:, hc, tb * 512:(tb + 1) * 512], in_=r[:], func=AF.Square),
                          R=[r], W=[ab[hc][tb]])
            import os
            if os.environ.get("MLP_UP_ONLY"):
                continue
            for tb in range(4):
                for c in range(KC):
                    p = kb.ps()
                    for hc in range(4):
                        kb.op("pe", lambda e, p=p, hc=hc, c=c: e.matmul(
                            p[:, :], wd[:, hc, c * 128:(c + 1) * 128], a[:, hc, tb * 512:(tb + 1) * 512],
                            start=(hc == 0), stop=(hc == 3)), R=[wd, ab[hc][tb]], W=[p])
                    xs = g.xT[:, c, tb * 512:(tb + 1) * 512]
                    kb.op("dve", lambda e, p=p, c=c, xs=xs: e.scalar_tensor_tensor(
                        out=xs, in0=p[:, :], scalar=g.modT[:, 40 + c:41 + c], in1=xs, op0=ALU.mult, op1=ALU.add),
                        R=[p, g.modT, g.xTb[c][tb]], W=[g.xTb[c][tb]])
        kb.barrier()


def rstd_bcast(kb, ph, xT, xTb, tb, onesb, sq, rs):
    p = kb.ps()
    for c in range(KC):
        s = sq[c % 2]
        kb.op("act", lambda e, s=s, c=c: e.activation(out=s[:], in_=xT[:, c, tb * 512:(tb + 1) * 512], func=AF.Square),
              R=[xTb[c][tb]], W=[s])
        kb.op("pe", lambda e, s=s, c=c, p=p: e.matmul(p[:, :], onesb[:], s[:], start=(c == 0), stop=(c == KC - 1)),
              R=[s, onesb], W=[p])
    kb.op("dve", lambda e: e.tensor_scalar(out=rs[:], in0=p[:, :], scalar1=1.0 / D, scalar2=EPS, op0=ALU.mult, op1=ALU.add),
          R=[p], W=[rs])
    kb.op("act", lambda e: e.activation(out=rs[:], in_=rs[:], func=AF.Ln), R=[rs], W=[rs])
    kb.op("act", lambda e: e.activation(out=rs[:], in_=rs[:], func=AF.Exp, scale=-0.5), R=[rs], W=[rs])


def final_norm(kb, nc, xT, xTb, fnw, onesb, ident, out_d):
    with ExitStack() as ph:
        sq = [kb.sb(ph, f"fsq{i}", [128, 512], BF16) for i in range(2)]
        rs = kb.sb(ph, "frs", [128, 512], F32)
        yT = [kb.sb(ph, f"fyT{i}", [128, 512], F32) for i in range(2)]
        ost = [kb.sb(ph, f"fost{i}", [128, 4, D], F32) for i in range(2)]
        for tb in range(4):
            rstd_bcast(kb, ph, xT, xTb, tb, onesb, sq, rs)
            o = ost[tb % 2]
            for c in range(KC):
                y = yT[c % 2]
                kb.op("dve", lambda e, y=y, c=c: e.scalar_tensor_tensor(
                    out=y[:], in0=xT[:, c, tb * 512:(tb + 1) * 512], scalar=fnw[:, c:c + 1], in1=rs[:],
                    op0=ALU.mult, op1=ALU.mult), R=[xTb[c][tb], fnw, rs], W=[y])
                p = kb.ps()
                for j in range(4):
                    kb.op("pe", lambda e, y=y, j=j, p=p: e.transpose(
                        p[:, j * 128:(j + 1) * 128], y[:, j * 128:(j + 1) * 128], ident[:]), R=[y, ident], W=[p])
                src = p[:, :].rearrange("p (j f) -> p j f", j=4)
                dst = o[:, :, c * 128:(c + 1) * 128]
                kb.op("act", lambda e, dst=dst, src=src: e.activation(out=dst, in_=src, func=AF.Copy), R=[p], Wd=[o])
            kb.dma("sp", out_d.t[tb * 512:(tb + 1) * 512, :].rearrange("(j p) d -> p j d", p=128), o[:], R=[o], W=[out_d])
        kb.barrier()


_NC_CACHE = {}


def make_in_maps(inputs):
    nb = inputs["x"].shape[0]
    inputs = dict(inputs)
    inputs.update(host_consts())
    shared = {n: np.ascontiguousarray(np.asarray(inputs[n], dtype=np.float32)) for n in IN_SHAPES if n not in ("x", "c")}
    in_maps = []
    for b in range(nb):
        m = dict(shared)
        m["x"] = np.ascontiguousarray(inputs["x"][b])
        m["c"] = np.ascontiguousarray(inputs["c"][b])
        in_maps.append(m)
    return in_maps


def kernel(**inputs):
    nb = inputs["x"].shape[0]
    if "nc" not in _NC_CACHE:
        _NC_CACHE["nc"] = build()[0]
    nc = _NC_CACHE["nc"]
    res = run_bass_kernel_spmd(nc, make_in_maps(inputs), core_ids=list(range(nb)))
    return np.stack([r["out"] for r in res.results], axis=0)
```

```python
import math
from contextlib import ExitStack

import numpy as np
import concourse.bass as bass
import concourse.mybir as mybir
from concourse.bass_utils import run_bass_kernel_spmd

F32 = mybir.dt.float32
BF16 = mybir.dt.bfloat16
I32 = mybir.dt.int32
AF = mybir.ActivationFunctionType
ALU = mybir.AluOpType
AX = mybir.AxisListType

D = 1024
L = 2048
DEPTH = 2
NT = L // 128
KC = D // 128
EPS = 1e-6
import os
NDS = int(os.environ.get("NDS", "12"))
STRICT = bool(int(os.environ.get("KSTRICT", "1")))


class Buf:
    __slots__ = ("name", "w", "r", "excl")

    def __init__(self, name):
        self.name = name
        self.excl = False
        self.w = None
        self.r = {}


class T:
    def __init__(self, t, buf):
        self.t = t
        self.b = buf

    def __getitem__(self, k):
        return self.t[k]


class KB:
    def __init__(self, nc):
        self.nc = nc
        self.es = ExitStack()
        self.sems = {}
        self.engs = {}
        for name, e in (("pe", nc.tensor), ("act", nc.scalar), ("dve", nc.vector),
                        ("pool", nc.gpsimd), ("sp", nc.sync)):
            key = "s_" + name
            self.sems[key] = self.es.enter_context(nc.semaphore(key))
            self.engs[name] = dict(e=e, key=key, cnt=0, seen={})
        self.dq = {}
        for q in ("sp", "pool"):
            keys = []
            for i in range(NDS):
                key = f"d_{q}{i}"
                self.sems[key] = self.es.enter_context(nc.semaphore(key))
                keys.append(key)
            self.dq[q] = dict(keys=keys, cnt=[0] * NDS, nxt=0)
        self.nbuf = 0
        self.psum = []
        self.ps_next = 0
        self.pinned = set()

    def buf(self, name=None):
        self.nbuf += 1
        return Buf(name or f"b{self.nbuf}")

    def sb(self, es, name, shape, dtype):
        self.nbuf += 1
        name = f"{name}_{self.nbuf}"
        t = es.enter_context(self.nc.sbuf_tensor(name, list(shape), dtype))
        return T(t, self.buf(name))

    def dram(self, name, shape, dtype, kind="Internal"):
        t = self.nc.dram_tensor(name, list(shape), dtype, kind=kind)
        return T(t.ap(), self.buf(name))

    def init_psum(self):
        for i in range(8):
            t = self.es.enter_context(self.nc.psum_tensor(f"ps{i}", [128, 512], F32))
            self.psum.append(T(t, self.buf(f"ps{i}")))
            self.psum[-1].b.excl = True

    def ps(self, pin=False):
        while True:
            i = self.ps_next
            self.ps_next = (self.ps_next + 1) % 8
            if i not in self.pinned:
                break
        if pin:
            self.pinned.add(i)
        return self.psum[i]

    def unpin(self, p):
        self.pinned.discard(self.psum.index(p))

    def _deps(self, own_key, R, W, same_raw, Wd=()):
        deps = {}

        def add(ev):
            if ev is None:
                return
            k, v = ev
            if deps.get(k, 0) < v:
                deps[k] = v

        for t in R:
            b = t.b if isinstance(t, T) else t
            if b.w is not None and (b.w[0] != own_key or same_raw):
                add(b.w)
            if b.excl:
                for k, v in b.r.items():
                    if k != own_key:
                        add((k, v))
        for t in W:
            b = t.b if isinstance(t, T) else t
            if b.w is not None and (b.w[0] != own_key or (STRICT and same_raw)):
                add(b.w)
            for k, v in b.r.items():
                if k != own_key or (STRICT and same_raw):
                    add((k, v))
        for t in Wd:
            b = t.b if isinstance(t, T) else t
            if b.w is not None and b.w[0] != own_key:
                add(b.w)
            for k, v in b.r.items():
                if k != own_key or (STRICT and same_raw):
                    add((k, v))
        return deps

    def _mark(self, ev, R, W):
        k, v = ev
        for t in R:
            b = t.b if isinstance(t, T) else t
            if b.r.get(k, 0) < v:
                b.r[k] = v
        for t in W:
            b = t.b if isinstance(t, T) else t
            b.w = ev
            b.r = {}

    def _wait(self, eng, deps):
        for k, v in deps.items():
            if eng["seen"].get(k, 0) < v:
                eng["e"].wait_ge(self.sems[k], v)
                eng["seen"][k] = v

    def op(self, engname, fn, R=(), W=(), Wd=()):
        eng = self.engs[engname]
        deps = self._deps(eng["key"], R, W, same_raw=(engname != "pe"), Wd=Wd)
        W = list(W) + list(Wd)
        self._wait(eng, deps)
        ins = fn(eng["e"])
        eng["cnt"] += 1
        ins.then_inc(self.sems[eng["key"]], 1)
        self._mark((eng["key"], eng["cnt"]), R, W)
        return ins

    def dma(self, q, out, in_, R=(), W=(), **kw):
        eng = self.engs[q]
        dq = self.dq[q]
        deps = self._deps(None, R, W, same_raw=True)
        i = dq["nxt"]
        dq["nxt"] = (i + 1) % NDS
        key = dq["keys"][i]
        if dq["cnt"][i] > 0:
            deps[key] = max(deps.get(key, 0), dq["cnt"][i])
        self._wait(eng, deps)
        dq["cnt"][i] += 16
        eng["e"].dma_start(out=out, in_=in_, **kw).then_inc(self.sems[key], 16)
        self._mark((key, dq["cnt"][i]), R, W)

    def barrier(self):
        allev = {}
        for name, eng in self.engs.items():
            if eng["cnt"] > 0:
                allev[eng["key"]] = eng["cnt"]
        for q, dq in self.dq.items():
            for key, c in zip(dq["keys"], dq["cnt"]):
                if c > 0:
                    allev[key] = c
        for name, eng in self.engs.items():
            deps = {k: v for k, v in allev.items() if k != eng["key"]}
            self._wait(eng, deps)

    def finish(self, out_bufs):
        eng = self.engs["sp"]
        deps = {}
        for t in out_bufs:
            b = t.b if isinstance(t, T) else t
            if b.w is not None:
                deps[b.w[0]] = max(deps.get(b.w[0], 0), b.w[1])
        self._wait(eng, deps)
        self.barrier()


IN_SHAPES = {
    "x": [L, D], "c": [D], "norm1_w": [DEPTH, D], "ada_w": [DEPTH, D, 6 * D], "ada_b": [DEPTH, 6 * D],
    "w_in": [DEPTH, D, 8536], "conv_w": [DEPTH, 4, 1536], "conv_b": [DEPTH, 1536], "dt_bias": [DEPTH, 16],
    "a_log": [DEPTH, 16], "d_skip": [DEPTH, 16], "ssd_norm_w": [DEPTH, D], "w_br_ssd": [DEPTH, D, D],
    "w_br_sb": [DEPTH, 512, D], "w_br_dsa": [DEPTH, 512, D], "w_out": [DEPTH, D, D], "norm2_w": [DEPTH, D],
    "w_up": [DEPTH, D, 4 * D], "w_down": [DEPTH, 4 * D, D], "final_norm_w": [D],
    "rope_cs": [L, 16],
}


def host_consts():
    half = 8
    inv_freq = np.exp(np.arange(half, dtype=np.float32) * np.float32(-2.0 * math.log(500000.0) / 16)).astype(np.float32)
    ang = np.arange(L, dtype=np.float32)[:, None] * inv_freq[None, :]
    return {"rope_cs": np.concatenate([np.cos(ang), np.sin(ang)], axis=1).astype(np.float32)}


class G:
    pass


def build(nlayers=DEPTH, dbg=(), skip_mixer=False, phases=("mod", "inproj", "ssd", "sb", "dsa", "merge", "norm2", "mlp")):
    nc = bass.Bass("TRN2", target_bir_lowering=False)
    kb = KB(nc)
    es = kb.es
    kb.init_psum()
    g = G()
    g.nc, g.kb, g.es, g.dbg = nc, kb, es, set(dbg)
    g.inp = {n: kb.dram(n, shp, F32, kind="ExternalInput") for n, shp in IN_SHAPES.items()}
    out_d = kb.dram("out", [L, D], F32, kind="ExternalOutput")
    g.dbg_out = {}

    g.ident = ident = kb.sb(es, "ident", [128, 128], F32)
    g.identb = identb = kb.sb(es, "identb", [128, 128], BF16)
    g.onesb = onesb = kb.sb(es, "onesb", [128, 128], BF16)
    g.onesf = kb.sb(es, "onesf", [128, 128], F32)
    g.triu = kb.sb(es, "triu", [128, 128], F32)
    g.negm = kb.sb(es, "negm", [128, 128], F32)
    g.triub = kb.sb(es, "triub", [128, 128], BF16)
    g.strl = kb.sb(es, "strl", [128, 128], F32)
    g.strlb = kb.sb(es, "strlb", [128, 128], BF16)
    g.trilb = kb.sb(es, "trilb", [128, 128], BF16)
    g.negup = kb.sb(es, "negup", [128, 128], F32)
    g.negmb = kb.sb(es, "negmb", [128, 128], BF16)
    g.pow2 = kb.sb(es, "pow2", [128, 24], F32)
    with ExitStack() as tmp:
        coli = kb.sb(tmp, "coli", [128, 128], I32)
        rowi = kb.sb(tmp, "rowi", [128, 1], I32)
        colf = kb.sb(tmp, "colf", [128, 128], F32)
        rowf = kb.sb(tmp, "rowf", [128, 1], F32)
        kb.op("pool", lambda e: e.iota(coli[:], [[1, 128]], base=0, channel_multiplier=0), W=[coli])
        kb.op("pool", lambda e: e.iota(rowi[:], [[0, 1]], base=0, channel_multiplier=1), W=[rowi])
        kb.op("dve", lambda e: e.tensor_copy(out=colf[:], in_=coli[:]), R=[coli], W=[colf])
        kb.op("dve", lambda e: e.tensor_copy(out=rowf[:], in_=rowi[:]), R=[rowi], W=[rowf])
        kb.op("dve", lambda e: e.tensor_scalar(out=ident[:], in0=colf[:], scalar1=rowf[:, 0:1], scalar2=None,
                                               op0=ALU.is_equal), R=[colf, rowf], W=[ident])
        kb.op("dve", lambda e: e.tensor_copy(out=identb[:], in_=ident[:]), R=[ident], W=[identb])
        kb.op("dve", lambda e: e.memset(onesb[:], 1.0), W=[onesb])
        kb.op("dve", lambda e: e.memset(g.onesf[:], 1.0), W=[g.onesf])
        kb.op("dve", lambda e: e.tensor_scalar(out=g.triu[:], in0=colf[:], scalar1=rowf[:, 0:1], scalar2=None, op0=ALU.is_ge),
              R=[colf, rowf], W=[g.triu])
        kb.op("dve", lambda e: e.tensor_scalar(out=g.negm[:], in0=g.triu[:], scalar1=-1.0, scalar2=30000.0, op0=ALU.add, op1=ALU.mult),
              R=[g.triu], W=[g.negm])
        kb.op("dve", lambda e: e.tensor_copy(out=g.triub[:], in_=g.triu[:]), R=[g.triu], W=[g.triub])
        kb.op("dve", lambda e: e.tensor_copy(out=g.negmb[:], in_=g.negm[:]), R=[g.negm], W=[g.negmb])
        kb.op("dve", lambda e: e.tensor_scalar(out=g.strl[:], in0=colf[:], scalar1=rowf[:, 0:1], scalar2=None, op0=ALU.is_lt),
              R=[colf, rowf], W=[g.strl])
        kb.op("dve", lambda e: e.tensor_copy(out=g.strlb[:], in_=g.strl[:]), R=[g.strl], W=[g.strlb])
        kb.op("dve", lambda e: e.tensor_scalar(out=g.trilb[:], in0=colf[:], scalar1=rowf[:, 0:1], scalar2=None, op0=ALU.is_le),
              R=[colf, rowf], W=[g.trilb])
        kb.op("dve", lambda e: e.tensor_scalar(out=g.negup[:], in0=g.trilb[:], scalar1=-1.0, scalar2=1.0e9, op0=ALU.add, op1=ALU.mult),
              R=[g.trilb], W=[g.negup])
        for kk_ in range(24):
            kb.op("dve", lambda e, kk_=kk_: e.memset(g.pow2[:, kk_:kk_ + 1], float(2.0 ** (-kk_))), Wd=[g.pow2])
        kb.barrier()

    g.xT = xT = kb.sb(es, "xT", [128, KC, L], F32)
    g.xTb = xTb = [[kb.buf(f"xT{c}_{tb}") for tb in range(4)] for c in range(KC)]
    g.hTb = [[kb.buf(f"hT{c}_{tb}") for tb in range(4)] for c in range(KC)]

    fnw = load_cols(g, es, "fnw", g.inp["final_norm_w"].t, KC)
    g.ccol = load_cols(g, es, "ccol", g.inp["c"].t, KC)
    g.csilu = kb.sb(es, "csilu", [128, KC], BF16)
    kb.op("act", lambda e: e.activation(out=g.csilu[:], in_=g.ccol[:], func=AF.Silu), R=[g.ccol], W=[g.csilu])
    g.modT_l = [kb.sb(es, "modT", [128, 48], F32) for _ in range(DEPTH)]
    g.modraw = [kb.sb(es, "modraw", [128, 48], F32) for _ in range(DEPTH)]
    g.A1_l = [kb.sb(es, "A1", [128, KC], F32) for _ in range(DEPTH)]
    g.A2_l = [kb.sb(es, "A2", [128, KC], F32) for _ in range(DEPTH)]
    g.dsa_hook = None
    g.adab_l = [load_cols(g, es, "adab", g.inp["ada_b"].t[l_], 48) for l_ in range(DEPTH)]
    g.n1w_l = [load_cols(g, es, "n1w", g.inp["norm1_w"].t[l_], KC) for l_ in range(DEPTH)]
    g.n2w_l = [load_cols(g, es, "n2w", g.inp["norm2_w"].t[l_], KC) for l_ in range(DEPTH)]

    x_in = g.inp["x"]
    with ExitStack() as ph:
        g.mod0_w = [kb.sb(ph, f"adaw{i}", [128, KC, 512], BF16) for i in range(3)] if "mod" in phases else None
        if g.mod0_w is not None:
            for j4 in range(3):
                mod_piece_load(g, 0, j4, g.mod0_w[j4])
        xin = [kb.sb(ph, f"xin{i}", [128, D], F32) for i in range(2)]
        for tt in range(NT):
            xi = xin[tt % 2]
            kb.dma("sp", xi[:], x_in.t[tt * 128:(tt + 1) * 128, :], W=[xi])
            for half in range(2):
                p = kb.ps()
                for j in range(4):
                    c = half * 4 + j
                    kb.op("pe", lambda e, c=c, j=j, p=p, xi=xi: e.transpose(
                        p[:, j * 128:(j + 1) * 128], xi[:, c * 128:(c + 1) * 128], ident[:]),
                        R=[xi, ident], W=[p])
                tb = tt // 4
                wb = [xTb[half * 4 + j][tb] for j in range(4)]
                dst = xT[:, half * 4:half * 4 + 4, tt * 128:(tt + 1) * 128]
                src = p[:, :].rearrange("p (j t) -> p j t", j=4)
                if half == 0:
                    kb.op("act", lambda e, dst=dst, src=src: e.activation(out=dst, in_=src, func=AF.Copy), R=[p], W=wb)
                else:
                    kb.op("dve", lambda e, dst=dst, src=src: e.tensor_copy(out=dst, in_=src), R=[p], W=wb)
        if g.mod0_w is not None:
            for j4 in range(12):
                if j4 >= 3:
                    mod_piece_load(g, 0, j4, g.mod0_w[j4 % 3])
                mod_piece_mm(g, 0, j4, g.mod0_w[j4 % 3])
        kb.barrier()

    def scr(name, shape, dtype=BF16):
        return kb.dram("scr_" + name, shape, dtype, kind=("ExternalOutput" if name in g.dbg else "Internal"))
    g.scr = dict(zs=scr("zs", [L, 1024]), sbv=scr("sbv", [L, 512]), dv=scr("dv", [L, 128]),
                 dqT=scr("dqT", [8, 64, L]), dkT=scr("dkT", [2, 64, L]), iqT=scr("iqT", [8, 64, L]), ikT=scr("ikT", [1, 64, L]),
                 yT=scr("yT", [16, 128, L]), sbqT=scr("sbqT", [4, 128, L]), sbkT=scr("sbkT", [4, 128, L]), gT=scr("gT", [24, 128, L]))
    for name in g.scr:
        if name in g.dbg:
            g.dbg_out["scr_" + name] = g.scr[name]
    g.ropecs = kb.sb(es, "ropecs", [128, NT, 16], F32)
    kb.dma("sp", g.ropecs[:], g.inp["rope_cs"].t.rearrange("(t p) c -> p t c", p=128), W=[g.ropecs])
    g.dt_tok = kb.sb(es, "dt_tok", [128, NT, 16], F32)
    g.wix_tok = kb.sb(es, "wix_tok", [128, NT, 8], F32)
    g.dtb_bc = kb.sb(es, "dtb_bc", [128, 16], F32)
    g.cb = kb.sb(es, "cb", [128, 12], F32)
    g.cw = [kb.sb(es, f"cw{k}", [128, 12], F32) for k in range(4)]

    for l in range(nlayers):
        g.modT, g.A1, g.A2 = g.modT_l[l], g.A1_l[l], g.A2_l[l]
        if "mod" in phases:
            mod_finish(g, l)
        g.dsa_hook = (l + 1) if (l + 1 < nlayers and "mod" in phases) else None
        dump_sb(g, f"mod{l}", g.modT, [128, 48])
        if not skip_mixer:
            with ExitStack() as ssd_scope:
                g.xs_tok = kb.sb(ssd_scope, "xs_tok", [128, NT, 1024], BF16)
                g.bt_tok = kb.sb(ssd_scope, "bt_tok", [128, NT, 256], BF16)
                g.bcT = kb.sb(ssd_scope, "bcT", [128, 4, L], BF16)
                with nc.allow_non_contiguous_dma(reason="tiny vectors"):
                    kb.dma("sp", g.dtb_bc[:], g.inp["dt_bias"].t[l].partition_broadcast(128), W=[g.dtb_bc])
                    kb.dma("sp", g.cb[:], g.inp["conv_b"].t[l].rearrange("(c p) -> p c", p=128), W=[g.cb])
                    for k in range(4):
                        kb.dma("sp", g.cw[k][:], g.inp["conv_w"].t[l][k].rearrange("(c p) -> p c", p=128), W=[g.cw[k]])
                with ExitStack() as hsc:
                    g.hT = kb.sb(hsc, "hT", [128, KC, L], BF16)
                    phase_norm(g, l, which=1)
                    dump_hT(g, f"h{l}")
                    if "inproj" in phases:
                        phase_inproj(g, l)
                dump_sb(g, f"xs_tok{l}", g.xs_tok, [128, NT, 1024], BF16)
                dump_sb(g, f"bt_tok{l}", g.bt_tok, [128, NT, 256], BF16)
                dump_sb(g, f"bcT{l}", g.bcT, [128, 4, L], BF16)
                dump_sb(g, f"dt_tok{l}", g.dt_tok, [128, NT, 16], F32)
                dump_sb(g, f"wix_tok{l}", g.wix_tok, [128, NT, 8], F32)
                kb.barrier()
                if "ssd" in phases:
                    phase_ssd(g, l)
            if "sb" in phases:
                phase_sb(g, l)
            if "dsa" in phases:
                phase_dsa(g, l)
            if "merge" in phases:
                phase_merge(g, l)
            dump_xT(g, f"xmix{l}")
        with ExitStack() as hsc:
            g.hT = kb.sb(hsc, "hT", [128, KC, L], BF16)
            if "norm2" in phases:
                phase_norm(g, l, which=2)
            if "mlp" in phases:
                phase_mlp(g, l)
        dump_xT(g, f"xout{l}")

    final_norm(kb, nc, xT, xTb, fnw, onesb, ident, out_d)
    kb.finish([out_d] + list(g.dbg_out.values()))
    return nc, sorted(g.dbg_out.keys())


def load_cols(g, es_, name, src_ap, n):
    kb = g.kb
    t = kb.sb(es_, name, [128, n], F32)
    with g.nc.allow_non_contiguous_dma(reason="tiny per-feature vector load"):
        for c0 in range(0, n, 8):
            c1 = min(n, c0 + 8)
            kb.dma("sp", t[:, c0:c1], src_ap[c0 * 128:c1 * 128].rearrange("(c p) -> p c", p=128), W=[t])
    return t


def dump_sb(g, name, t, shape, dtype=F32):
    if name not in g.dbg:
        return
    d = g.kb.dram("dbg_" + name, shape, dtype, kind="ExternalOutput")
    g.kb.barrier()
    g.kb.dma("sp", d.t, t[:], R=[t], W=[d])
    g.dbg_out["dbg_" + name] = d


def dump_xT(g, name):
    if name not in g.dbg:
        return
    d = g.kb.dram("dbg_" + name, [128, KC, L], F32, kind="ExternalOutput")
    g.kb.barrier()
    g.kb.dma("sp", d.t, g.xT[:], R=[b for row in g.xTb for b in row], W=[d])
    g.dbg_out["dbg_" + name] = d


def dump_hT(g, name):
    if name not in g.dbg:
        return
    d = g.kb.dram("dbg_" + name, [128, KC, L], BF16, kind="ExternalOutput")
    g.kb.barrier()
    g.kb.dma("sp", d.t, g.hT[:], R=[b for row in g.hTb for b in row], W=[d])
    g.dbg_out["dbg_" + name] = d


def load_w(g, t, src_rows_ap):
    g.kb.dma("pool", t[:], src_rows_ap.rearrange("(kc p) n -> p kc n", p=128), W=[t])


def mod_piece_load(g, l, j4, w):
    g.kb.dma("pool", w[:], g.inp["ada_w"].t[l][:, j4 * 512:(j4 + 1) * 512].rearrange("(kc p) n -> p kc n", p=128), W=[w])


def mod_piece_mm(g, l, j4, w):
    kb = g.kb
    p = kb.ps()
    for jl in range(4):
        for kc in range(KC):
            kb.op("pe", lambda e, jl=jl, kc=kc: e.matmul(p[:, jl:jl + 1], w[:, kc, jl * 128:(jl + 1) * 128], g.csilu[:, kc:kc + 1],
                                                         start=(kc == 0), stop=(kc == KC - 1)), R=[w, g.csilu], W=[p])
    raw = g.modraw[l]
    kb.op("act", lambda e: e.activation(out=raw[:, j4 * 4:(j4 + 1) * 4], in_=p[:, 0:4], func=AF.Copy), R=[p], Wd=[raw])


def mod_piece(g, l, j4, w):
    mod_piece_load(g, l, j4, w)
    mod_piece_mm(g, l, j4, w)


def mod_finish(g, l):
    kb = g.kb
    with ExitStack() as ph:
        adab, n1w, n2w = g.adab_l[l], g.n1w_l[l], g.n2w_l[l]
        modT, A1, A2 = g.modT_l[l], g.A1_l[l], g.A2_l[l]
        kb.op("dve", lambda e: e.tensor_tensor(out=modT[:], in0=g.modraw[l][:], in1=adab[:], op=ALU.add),
              R=[g.modraw[l], adab], W=[modT])
        kb.op("dve", lambda e: e.scalar_tensor_tensor(out=A1[:], in0=modT[:, 8:16], scalar=1.0, in1=n1w[:],
                                                      op0=ALU.add, op1=ALU.mult), R=[modT, n1w], W=[A1])
        kb.op("dve", lambda e: e.scalar_tensor_tensor(out=A2[:], in0=modT[:, 32:40], scalar=1.0, in1=n2w[:],
                                                      op0=ALU.add, op1=ALU.mult), R=[modT, n2w], W=[A2])
        kb.barrier()


def phase_mod(g, l):
    kb = g.kb
    with ExitStack() as ph:
        wt = [kb.sb(ph, f"adaw{i}", [128, KC, 512], BF16) for i in range(3)]
        for j4 in range(12):
            mod_piece(g, l, j4, wt[j4 % 3])
        kb.barrier()
    mod_finish(g, l)


def phase_norm(g, l, which):
    kb = g.kb
    A = g.A1 if which == 1 else g.A2
    sh0 = 0 if which == 1 else 24
    with ExitStack() as ph:
        sq = [kb.sb(ph, f"nsq{i}", [128, 512], BF16) for i in range(2)]
        rs = kb.sb(ph, "nrs", [128, 512], F32)
        tmp = [kb.sb(ph, f"ntmp{i}", [128, 512], F32) for i in range(2)]
        for tb in range(4):
            rstd_bcast(kb, ph, g.xT, g.xTb, tb, g.onesb, sq, rs)
            for c in range(KC):
                t = tmp[c % 2]
                kb.op("dve", lambda e, t=t, c=c: e.tensor_tensor(out=t[:], in0=g.xT[:, c, tb * 512:(tb + 1) * 512], in1=rs[:],
                                                                 op=ALU.mult), R=[g.xTb[c][tb], rs], W=[t])
                kb.op("act", lambda e, t=t, c=c: e.activation(
                    out=g.hT[:, c, tb * 512:(tb + 1) * 512], in_=t[:], func=AF.Identity,
                    scale=A[:, c:c + 1], bias=g.modT[:, sh0 + c:sh0 + c + 1]), R=[t, A, g.modT], W=[g.hTb[c][tb]])
        kb.barrier()


OFF = dict(z=0, xbc=1024, dt=2560, sbq=2576, sbk=3088, sbv=3600, dsq=4112, dsk=4624, dsv=4752,
           ixq=4880, ixk=5392, ixw=5456, gate=5464)


def rope(g, f3, nh, tt, rt):
    kb = g.kb
    fb = f3._buf
    cos = g.ropecs[:, tt:tt + 1, 0:8].broadcast_to([128, nh, 8])
    sin = g.ropecs[:, tt:tt + 1, 8:16].broadcast_to([128, nh, 8])
    x1 = f3.ap[:, :, 0:8]
    x2 = f3.ap[:, :, 8:16]
    t = [rt[:, i, 0:nh, :] for i in range(4)]
    kb.op("dve", lambda e: e.tensor_tensor(out=t[0], in0=x1, in1=cos, op=ALU.mult), R=[fb, g.ropecs], W=[rt])
    kb.op("dve", lambda e: e.tensor_tensor(out=t[1], in0=x2, in1=sin, op=ALU.mult), R=[fb, g.ropecs], W=[rt])
    kb.op("dve", lambda e: e.tensor_tensor(out=t[2], in0=x2, in1=cos, op=ALU.mult), R=[fb, g.ropecs], W=[rt])
    kb.op("dve", lambda e: e.tensor_tensor(out=t[3], in0=x1, in1=sin, op=ALU.mult), R=[fb, g.ropecs], W=[rt])
    kb.op("dve", lambda e: e.tensor_tensor(out=x1, in0=t[0], in1=t[1], op=ALU.subtract), R=[rt], W=[fb])
    kb.op("dve", lambda e: e.tensor_tensor(out=x2, in0=t[2], in1=t[3], op=ALU.add), R=[rt], W=[fb])


class V:
    def __init__(self, ap, buf):
        self.ap = ap
        self._buf = buf


def phase_inproj(g, l):
    kb, nc = g.kb, g.nc
    win = g.inp["w_in"].t[l]
    S = g.scr
    with ExitStack() as ph:
        wt = [kb.sb(ph, "winw", [128, KC, 512], BF16) for _ in range(2)]
        nw = [0]

        def getw(n0, N):
            w = wt[nw[0] % 2]
            nw[0] += 1
            kb.dma("pool", w[:, :, 0:N], win[:, n0:n0 + N].rearrange("(kc p) n -> p kc n", p=128), W=[w])
            return w

        stgb = [kb.sb(ph, "stgb", [128, 512], BF16) for _ in range(3)]
        stgf = [kb.sb(ph, "stgf", [128, 512], F32) for _ in range(2)]
        rt = kb.sb(ph, "ropetmp", [128, 4, 16, 8], F32)
        tst = [kb.sb(ph, "tst", [64, 8, 512], BF16) for _ in range(2)]
        cnt = dict(b=0, f=0, t=0)

        def nxt(lst, k):
            r = lst[cnt[k] % len(lst)]
            cnt[k] += 1
            return r

        def tok_mm(w, N, tt):
            p = kb.ps()
            for kc in range(KC):
                kb.op("pe", lambda e, kc=kc: e.matmul(p[:, 0:N], g.hT[:, kc, tt * 128:(tt + 1) * 128], w[:, kc, 0:N],
                                                      start=(kc == 0), stop=(kc == KC - 1)),
                      R=[g.hTb[kc][tt // 4], w], W=[p])
            return p

        def feat_mm(w, ci, tb):
            p = kb.ps()
            for kc in range(KC):
                kb.op("pe", lambda e, kc=kc: e.matmul(p[:, :], w[:, kc, ci * 128:(ci + 1) * 128],
                                                      g.hT[:, kc, tb * 512:(tb + 1) * 512],
                                                      start=(kc == 0), stop=(kc == KC - 1)),
                      R=[g.hTb[kc][tb], w], W=[p])
            return p

        for zi in range(2):
            w = getw(OFF["z"] + zi * 512, 512)
            for tt in range(NT):
                p = tok_mm(w, 512, tt)
                sb_ = nxt(stgb, "b")
                kb.op("act", lambda e: e.activation(out=sb_[:], in_=p[:, :], func=AF.Silu), R=[p], W=[sb_])
                kb.dma("sp", S["zs"].t[tt * 128:(tt + 1) * 128, zi * 512:(zi + 1) * 512], sb_[:], R=[sb_])
        w = getw(OFF["dt"], 16)
        for tt in range(NT):
            p = tok_mm(w, 16, tt)
            f = nxt(stgf, "f")
            kb.op("dve", lambda e: e.tensor_tensor(out=f[:, 0:16], in0=p[:, 0:16], in1=g.dtb_bc[:], op=ALU.add),
                  R=[p, g.dtb_bc], W=[f])
            kb.op("act", lambda e: e.activation(out=f[:, 0:16], in_=f[:, 0:16], func=AF.Exp), R=[f], W=[f])
            kb.op("act", lambda e: e.activation(out=g.dt_tok[:, tt, :], in_=f[:, 0:16], func=AF.Ln, bias=1.0),
                  R=[f], W=[g.dt_tok])
        w = getw(OFF["sbv"], 512)
        for tt in range(NT):
            p = tok_mm(w, 512, tt)
            sb_ = nxt(stgb, "b")
            kb.op("act", lambda e: e.activation(out=sb_[:], in_=p[:, :], func=AF.Copy), R=[p], W=[sb_])
            kb.dma("sp", S["sbv"].t[tt * 128:(tt + 1) * 128, :], sb_[:], R=[sb_])

        def roped_group(n0, N, nh, dstT, extra=None):
            w = getw(n0, N)
            fs, bs, sts = {}, {}, {}

            def stA(tt):
                p = tok_mm(w, N, tt)
                f = nxt(stgf, "f")
                kb.op("act", lambda e: e.activation(out=f[:, 0:N], in_=p[:, 0:N], func=AF.Copy), R=[p], W=[f])
                fs[tt] = f

            def stB(tt):
                f = fs.pop(tt)
                rope(g, V(f[:, 0:nh * 64].rearrange("p (h d) -> p h d", d=64), f), nh, tt, rt)
                b = nxt(stgb, "b")
                kb.op("act", lambda e: e.activation(out=b[:, 0:N], in_=f[:, 0:N], func=AF.Copy), R=[f], W=[b])
                if extra is not None:
                    extra(tt, f, b)
                bs[tt] = b

            def stC(tt):
                b = bs.pop(tt)
                pt = kb.ps()
                ptb = pt[:, :].bitcast(BF16)
                for h in range(nh):
                    kb.op("pe", lambda e, h=h: e.transpose(ptb[0:64, h * 128:(h + 1) * 128], b[:, h * 64:(h + 1) * 64], g.identb[:]),
                          R=[b, g.identb], W=[pt])
                if tt % 4 == 0:
                    sts[tt // 4] = nxt(tst, "t")
                st = sts[tt // 4]
                kb.op("act", lambda e: e.activation(
                    out=st[0:64, 0:nh, (tt % 4) * 128:(tt % 4 + 1) * 128],
                    in_=ptb[0:64, 0:nh * 128].rearrange("p (h t) -> p h t", h=nh), func=AF.Copy), R=[pt], Wd=[st])
                if tt % 4 == 3:
                    tb = tt // 4
                    kb.dma("sp", dstT.t[:, :, tb * 512:(tb + 1) * 512].rearrange("h d t -> d h t"), st[0:64, 0:nh, :], R=[st])

            for step in range(NT + 2):
                if step < NT:
                    stA(step)
                if 0 <= step - 1 < NT:
                    stB(step - 1)
                if 0 <= step - 2 < NT:
                    stC(step - 2)

        roped_group(OFF["dsq"], 512, 8, S["dqT"])

        def dsv_extra(tt, f, b):
            kb.dma("sp", S["dv"].t[tt * 128:(tt + 1) * 128, :], b[:, 128:256], R=[b])
        roped_group(OFF["dsk"], 256, 2, S["dkT"], dsv_extra)
        roped_group(OFF["ixq"], 512, 8, S["iqT"])

        def ixw_extra(tt, f, b):
            kb.op("dve", lambda e: e.tensor_copy(out=g.wix_tok[:, tt, :], in_=f[:, 64:72]), R=[f], W=[g.wix_tok])
        roped_group(OFF["ixk"], 72, 1, S["ikT"], ixw_extra)

        def feat_group(n0, dstT, c0, func):
            w = getw(n0, 512)
            for ci in range(4):
                for tb in range(4):
                    p = feat_mm(w, ci, tb)
                    sb_ = nxt(stgb, "b")
                    kb.op("act", lambda e: e.activation(out=sb_[:], in_=p[:, :], func=func), R=[p], W=[sb_])
                    kb.dma("sp", dstT.t[c0 + ci, :, tb * 512:(tb + 1) * 512], sb_[:], R=[sb_])

        feat_group(OFF["sbq"], S["sbqT"], 0, AF.Copy)
        feat_group(OFF["sbk"], S["sbkT"], 0, AF.Copy)
        for gi in range(6):
            feat_group(OFF["gate"] + gi * 512, S["gT"], gi * 4, AF.Sigmoid)

        kb.barrier()

    with ExitStack() as ph:
        wt = [kb.sb(ph, "winw", [128, KC, 512], BF16) for _ in range(2)]
        nw = [0]
        xpad = [kb.sb(ph, "xpad", [128, 3 + L], F32) for _ in range(1)]
        cva = [kb.sb(ph, "cva", [128, L], F32) for _ in range(1)]
        cvo = [kb.sb(ph, "cvo", [128, L], BF16) for _ in range(2)]
        for xp in xpad:
            kb.op("dve", lambda e, xp=xp: e.memset(xp[:, 0:3], 0.0), W=[xp])
        for gi in range(3):
            w = getw(OFF["xbc"] + gi * 512, 512)
            for ci in range(4):
                cidx = gi * 4 + ci
                xp, acc = xpad[0], cva[0]
                for tb in range(4):
                    p = feat_mm(w, ci, tb)
                    kb.op("act", lambda e: e.activation(out=xp[:, 3 + tb * 512:3 + (tb + 1) * 512], in_=p[:, :], func=AF.Copy),
                          R=[p], Wd=[xp])
                kb.op("act", lambda e: e.activation(out=acc[:], in_=xp[:, 3:3 + L], func=AF.Identity,
                                                    scale=g.cw[3][:, cidx:cidx + 1], bias=g.cb[:, cidx:cidx + 1]),
                      R=[xp, g.cw[3], g.cb], W=[acc])
                for k in (2, 1, 0):
                    kb.op("dve", lambda e, k=k: e.scalar_tensor_tensor(out=acc[:], in0=xp[:, k:k + L], scalar=g.cw[k][:, cidx:cidx + 1],
                                                                       in1=acc[:], op0=ALU.mult, op1=ALU.add),
                          R=[xp, g.cw[k], acc], W=[acc])
                if cidx < 8:
                    o = cvo[cidx % 2]
                    ov = o[:, :]
                else:
                    o = g.bcT
                    ov = g.bcT[:, cidx - 8, :]
                kb.op("act", lambda e: e.activation(out=ov, in_=acc[:], func=AF.Silu), R=[acc], W=[o])
                if cidx < 10:
                    for half in range(2):
                        pt = kb.ps()
                        ptb = pt[:, :].bitcast(BF16)
                        for j in range(8):
                            tt = half * 8 + j
                            kb.op("pe", lambda e, j=j, tt=tt: e.transpose(ptb[:, j * 128:(j + 1) * 128], ov[:, tt * 128:(tt + 1) * 128],
                                                                          g.identb[:]), R=[o, g.identb], W=[pt])
                        if cidx < 8:
                            dst, dt_ = g.xs_tok[:, half * 8:(half + 1) * 8, cidx * 128:(cidx + 1) * 128], g.xs_tok
                        else:
                            dst, dt_ = g.bt_tok[:, half * 8:(half + 1) * 8, (cidx - 8) * 128:(cidx - 7) * 128], g.bt_tok
                        kb.op("dve", lambda e: e.tensor_copy(out=dst, in_=ptb.rearrange("p (j f) -> p j f", j=8)), R=[pt], W=[dt_])
        kb.barrier()


def phase_ssd(g, l):
    kb, nc = g.kb, g.nc
    S = g.scr
    with ExitStack() as ph:
        def sb(name, shape, dt_=F32):
            return kb.sb(ph, name, shape, dt_)
        a_bc = sb("a_bc", [128, 16])
        dsk_bc = sb("dsk_bc", [128, 16])
        nw_bc = sb("nw_bc", [128, 1024])
        Dmat = sb("Dmat", [128, 16, 128], BF16)
        with nc.allow_non_contiguous_dma(reason="tiny vectors"):
            kb.dma("sp", a_bc[:], g.inp["a_log"].t[l].partition_broadcast(128), W=[a_bc])
            kb.dma("sp", dsk_bc[:], g.inp["d_skip"].t[l].partition_broadcast(128), W=[dsk_bc])
            kb.dma("sp", nw_bc[:], g.inp["ssd_norm_w"].t[l].partition_broadcast(128), W=[nw_bc])
        kb.op("act", lambda e: e.activation(out=a_bc[:], in_=a_bc[:], func=AF.Exp), R=[a_bc], W=[a_bc])
        kb.op("dve", lambda e: e.tensor_scalar(out=a_bc[:], in0=a_bc[:], scalar1=-1.0, scalar2=None, op0=ALU.mult), R=[a_bc], W=[a_bc])
        for h in range(16):
            kb.op("dve", lambda e, h=h: e.tensor_scalar(out=Dmat[:, h, :], in0=g.ident[:], scalar1=dsk_bc[:, h:h + 1], scalar2=None,
                                                        op0=ALU.mult), R=[g.ident, dsk_bc], Wd=[Dmat])
        NB = 2
        X = [sb("X", [128, 2, 16, 128], BF16)] * 2
        dd = [sb("dd", [128, 16, 128])] * 2
        Mt = [sb("Mt", [128, 16, 128], BF16) for _ in range(NB)]
        xw = [sb("xw", [128, 1024], BF16) for _ in range(NB)]
        zst = [sb("zst", [128, 1024], BF16) for _ in range(NB)]
        yy = [sb("yy", [128, 1024])] * 2
        yz = yy
        ss = [sb("ss", [128, 1]) for _ in range(NB)]
        yn = [sb("yn", [128, 1024], BF16) for _ in range(NB)]
        yst = [sb("yst", [128, KC, 256], BF16)] * 2
        hst = [sb("hst", [128, 512]) for _ in range(2)]
        htmp = [sb("htmp", [128, 512]) for _ in range(2)]
        prevb = [[sb("prevb", [128, 512], BF16) for _ in range(2)] for _ in range(2)]
        ytmp = [sb("ytmp", [128, 512]) for _ in range(2)]

        dA_all = sb("dA_all", [128, 256])
        dAs_all = sb("dAs_all", [128, 2, 256], BF16)
        ac_all = sb("ac_all", [128, 256])
        eac_all = sb("eac_all", [128, 256])
        cd_all = sb("cd_all", [128, 256])
        wgt_all = sb("wgt_all", [128, 256])
        v_all = sb("v_all", [128, 256])
        vs_all = sb("vs_all", [128, 2, 256], BF16)
        acs_all = sb("acs_all", [128, 2, 256], BF16)
        dtf = g.dt_tok[:, :, :].rearrange("p c h -> p (c h)")
        kb.op("dve", lambda e: e.tensor_tensor(out=dA_all[:, :].rearrange("p (c h) -> p c h", h=16), in0=g.dt_tok[:, :, :],
                                               in1=a_bc[:, :].unsqueeze(1).broadcast_to([128, NT, 16]), op=ALU.mult),
              R=[g.dt_tok, a_bc], W=[dA_all])
        kb.op("dve", lambda e: e.tensor_copy(out=dAs_all[:, 0, :], in_=dA_all[:]), R=[dA_all], W=[dAs_all])
        kb.op("dve", lambda e: e.tensor_tensor(out=dAs_all[:, 1, :], in0=dA_all[:], in1=dAs_all[:, 0, :], op=ALU.subtract),
              R=[dA_all, dAs_all], W=[dAs_all])
        pac = kb.ps(pin=True)
        for c in range(NT):
            for half, lhs in ((0, g.triub), (1, g.onesb)):
                for hl_ in range(2):
                    kb.op("pe", lambda e, c=c, half=half, lhs=lhs, hl_=hl_: e.matmul(
                        pac[:, half * 256 + c * 16:half * 256 + (c + 1) * 16], lhs[:], dAs_all[:, hl_, c * 16:(c + 1) * 16],
                        start=(hl_ == 0), stop=(hl_ == 1)), R=[lhs, dAs_all], W=[pac])
        kb.unpin(pac)
        kb.op("dve", lambda e: e.tensor_copy(out=ac_all[:], in_=pac[:, 0:256]), R=[pac], W=[ac_all])
        kb.op("act", lambda e: e.activation(out=eac_all[:], in_=pac[:, 0:256], func=AF.Exp), R=[pac], W=[eac_all])
        kb.op("act", lambda e: e.activation(out=cd_all[:], in_=pac[:, 256:512], func=AF.Exp), R=[pac], W=[cd_all])
        kb.op("dve", lambda e: e.tensor_tensor(out=wgt_all[:], in0=pac[:, 256:512], in1=ac_all[:], op=ALU.subtract), R=[pac, ac_all], W=[wgt_all])
        kb.op("act", lambda e: e.activation(out=wgt_all[:], in_=wgt_all[:], func=AF.Exp), R=[wgt_all], W=[wgt_all])
        kb.op("dve", lambda e: e.tensor_tensor(out=wgt_all[:], in0=wgt_all[:], in1=dtf, op=ALU.mult), R=[wgt_all, g.dt_tok], W=[wgt_all])
        kb.op("act", lambda e: e.activation(out=v_all[:], in_=dtf, func=AF.Ln), R=[g.dt_tok], W=[v_all])
        kb.op("dve", lambda e: e.tensor_tensor(out=v_all[:], in0=v_all[:], in1=ac_all[:], op=ALU.subtract), R=[v_all, ac_all], W=[v_all])
        kb.op("dve", lambda e: e.tensor_copy(out=vs_all[:, 0, :], in_=v_all[:]), R=[v_all], W=[vs_all])
        kb.op("dve", lambda e: e.tensor_tensor(out=vs_all[:, 1, :], in0=v_all[:], in1=vs_all[:, 0, :], op=ALU.subtract), R=[v_all, vs_all], W=[vs_all])
        kb.op("dve", lambda e: e.tensor_copy(out=acs_all[:, 0, :], in_=ac_all[:]), R=[ac_all], W=[acs_all])
        kb.op("dve", lambda e: e.tensor_tensor(out=acs_all[:, 1, :], in0=ac_all[:], in1=acs_all[:, 0, :], op=ALU.subtract), R=[ac_all, acs_all], W=[acs_all])

        pcbs = {}

        def P1(c):
            k = c % NB
            tsl = slice(c * 128, (c + 1) * 128)
            c16 = slice(c * 16, (c + 1) * 16)
            kb.op("dve", lambda e: e.tensor_tensor(
                out=xw[k][:, :].rearrange("p (h d) -> p h d", d=64), in0=g.xs_tok[:, c, :].rearrange("p (h d) -> p h d", d=64),
                in1=wgt_all[:, c16].unsqueeze(2).broadcast_to([128, 16, 64]), op=ALU.mult), R=[g.xs_tok, wgt_all], W=[xw[k]])
            for hl_ in range(2):
                kb.op("dve", lambda e, hl_=hl_: e.tensor_tensor(
                    out=X[k][:, hl_, :, :], in0=g.identb[:, :].unsqueeze(1).broadcast_to([128, 16, 128]),
                    in1=acs_all[:, hl_, c16].unsqueeze(2).broadcast_to([128, 16, 128]), op=ALU.mult), R=[g.identb, acs_all], Wd=[X[k]])
            for q4 in range(4):
                pb = kb.ps()
                h4 = slice(c * 16 + q4 * 4, c * 16 + q4 * 4 + 4)
                for hl_ in range(2):
                    kb.op("pe", lambda e, hl_=hl_: e.matmul(pb[:, :], g.onesb[:], X[k][:, hl_, q4 * 4:(q4 + 1) * 4, :],
                                                            start=(hl_ == 0), stop=False), R=[g.onesb, X[k]], W=[pb])
                for hl_ in range(2):
                    kb.op("pe", lambda e, hl_=hl_: e.matmul(pb[:, :], g.identb[:], vs_all[:, hl_, h4].unsqueeze(2).broadcast_to([128, 4, 128]),
                                                            start=False, stop=False), R=[g.identb, vs_all], W=[pb])
                kb.op("pe", lambda e: e.matmul(pb[:, :], g.identb[:], g.negmb[:, :].unsqueeze(1).broadcast_to([128, 4, 128]),
                                               start=False, stop=True), R=[g.identb, g.negmb], W=[pb])
                kb.op("act", lambda e, q4=q4, pb=pb: e.activation(out=dd[k][:, q4 * 4:(q4 + 1) * 4, :],
                                                                  in_=pb[:, :].rearrange("p (h i) -> p h i", h=4), func=AF.Exp),
                      R=[pb], Wd=[dd[k]])
            pcb = kb.ps()
            for gg in range(2):
                kb.op("pe", lambda e, gg=gg: e.matmul(pcb[:, gg * 128:(gg + 1) * 128], g.bcT[:, gg, tsl], g.bcT[:, 2 + gg, tsl],
                                                      start=True, stop=True), R=[g.bcT], W=[pcb])
            pcbs[c] = pcb

        def P1b(c):
            k = c % NB
            pcb = pcbs.pop(c)
            for gg in range(2):
                kb.op("dve", lambda e, gg=gg: e.tensor_tensor(
                    out=Mt[k][:, gg * 8:(gg + 1) * 8, :], in0=dd[k][:, gg * 8:(gg + 1) * 8, :],
                    in1=pcb[:, gg * 128:(gg + 1) * 128].unsqueeze(1).broadcast_to([128, 8, 128]), op=ALU.mult),
                    R=[dd[k], pcb], Wd=[Mt[k]])

        def P2(c):
            k = c % NB
            tsl = slice(c * 128, (c + 1) * 128)
            kb.dma("sp", zst[k][:], S["zs"].t[tsl, :], W=[zst[k]])
            for gg in range(2):
                pA = kb.ps()
                for hl in range(8):
                    h = gg * 8 + hl
                    xsl = g.xs_tok[:, c, h * 64:(h + 1) * 64]
                    kb.op("pe", lambda e, h=h, hl=hl, xsl=xsl: e.matmul(pA[:, hl * 64:(hl + 1) * 64], Mt[k][:, h, :], xsl, start=True, stop=False),
                          R=[Mt[k], g.xs_tok], W=[pA])
                    kb.op("pe", lambda e, h=h, hl=hl, xsl=xsl: e.matmul(pA[:, hl * 64:(hl + 1) * 64], Dmat[:, h, :], xsl, start=False, stop=True),
                          R=[Dmat, g.xs_tok], W=[pA])
                ysl = yy[k][:, gg * 512:(gg + 1) * 512]
                if c > 0:
                    pB = kb.ps()
                    pv = prevb[gg][(c - 1) % 2]
                    kb.op("pe", lambda e: e.matmul(pB[:, :], g.bcT[:, 2 + gg, tsl], pv[:], start=True, stop=True), R=[g.bcT, pv], W=[pB])
                    yt = ytmp[gg]
                    kb.op("dve", lambda e: e.tensor_tensor(
                        out=yt[:, :].rearrange("p (h d) -> p h d", d=64), in0=pB[:, :].rearrange("p (h d) -> p h d", d=64),
                        in1=eac_all[:, c * 16 + gg * 8:c * 16 + (gg + 1) * 8].unsqueeze(2).broadcast_to([128, 8, 64]), op=ALU.mult),
                        R=[pB, eac_all], W=[yt])
                    kb.op("dve", lambda e: e.tensor_tensor(out=ysl, in0=yt[:], in1=pA[:, :], op=ALU.add), R=[yt, pA], W=[yy[k]])
                else:
                    kb.op("act", lambda e: e.activation(out=ysl, in_=pA[:, :], func=AF.Copy), R=[pA], W=[yy[k]])
                if c < NT - 1:
                    pS = kb.ps()
                    kb.op("pe", lambda e: e.matmul(pS[:, :], g.bt_tok[:, c, gg * 128:(gg + 1) * 128], xw[k][:, gg * 512:(gg + 1) * 512],
                                                   start=True, stop=True), R=[g.bt_tok, xw[k]], W=[pS])
                    if c == 0:
                        kb.op("dve", lambda e: e.tensor_copy(out=hst[gg][:], in_=pS[:, :]), R=[pS], W=[hst[gg]])
                    else:
                        kb.op("dve", lambda e: e.tensor_tensor(
                            out=htmp[gg][:, :].rearrange("p (h d) -> p h d", d=64), in0=hst[gg][:, :].rearrange("p (h d) -> p h d", d=64),
                            in1=cd_all[:, c * 16 + gg * 8:c * 16 + (gg + 1) * 8].unsqueeze(2).broadcast_to([128, 8, 64]), op=ALU.mult),
                            R=[hst[gg], cd_all], W=[htmp[gg]])
                        kb.op("dve", lambda e: e.tensor_tensor(out=hst[gg][:], in0=htmp[gg][:], in1=pS[:, :], op=ALU.add),
                              R=[htmp[gg], pS], W=[hst[gg]])
                    pn = prevb[gg][c % 2]
                    kb.op("act", lambda e: e.activation(out=pn[:], in_=hst[gg][:], func=AF.Copy), R=[hst[gg]], W=[pn])

        def P3(c):
            k = c % NB
            tsl = slice(c * 128, (c + 1) * 128)
            kb.op("dve", lambda e: e.tensor_tensor(out=yz[k][:], in0=yy[k][:], in1=zst[k][:], op=ALU.mult), R=[yy[k], zst[k]], W=[yz[k]])
            kb.op("act", lambda e: e.activation(out=yn[k][:], in_=yz[k][:], func=AF.Square, accum_out=ss[k][:]), R=[yz[k]], W=[yn[k], ss[k]])
            kb.op("dve", lambda e: e.tensor_scalar(out=ss[k][:], in0=ss[k][:], scalar1=1.0 / 1024, scalar2=EPS, op0=ALU.mult, op1=ALU.add),
                  R=[ss[k]], W=[ss[k]])
            kb.op("act", lambda e: e.activation(out=ss[k][:], in_=ss[k][:], func=AF.Ln), R=[ss[k]], W=[ss[k]])
            kb.op("act", lambda e: e.activation(out=ss[k][:], in_=ss[k][:], func=AF.Exp, scale=-0.5), R=[ss[k]], W=[ss[k]])
            kb.op("dve", lambda e: e.scalar_tensor_tensor(out=yn[k][:], in0=yz[k][:], scalar=ss[k][:, 0:1], in1=nw_bc[:],
                                                          op0=ALU.mult, op1=ALU.mult), R=[yz[k], ss[k], nw_bc], W=[yn[k]])
            pt = kb.ps()
            ptb = pt[:, :].bitcast(BF16)
            for kc in range(KC):
                kb.op("pe", lambda e, kc=kc: e.transpose(ptb[:, kc * 128:(kc + 1) * 128], yn[k][:, kc * 128:(kc + 1) * 128], g.identb[:]),
                      R=[yn[k], g.identb], W=[pt])
            st = yst[(c // 2) % 2]
            kb.op("act", lambda e: e.activation(out=st[:, :, (c % 2) * 128:(c % 2 + 1) * 128],
                                                in_=ptb.rearrange("p (kc t) -> p kc t", kc=KC), func=AF.Copy), R=[pt], Wd=[st])
            if c % 2 == 1:
                t2 = c // 2
                kb.dma("sp", S["yT"].t[0:8, :, t2 * 256:(t2 + 1) * 256].rearrange("kc p t -> p kc t"), st[:], R=[st])

        for step in range(-1, NT + 1):
            if 0 <= step + 1 < NT:
                P1(step + 1)
            if 0 <= step - 1 < NT:
                P3(step - 1)
            if 0 <= step + 1 < NT:
                P1b(step + 1)
            if 0 <= step < NT:
                P2(step)
        kb.barrier()


def phase_sb(g, l):
    kb, nc = g.kb, g.nc
    S = g.scr
    with ExitStack() as ph:
        def sb(name, shape, dt_=F32):
            return kb.sb(ph, name, shape, dt_)
        qT = sb("sbq", [128, 4, L], BF16)
        kT = sb("sbk", [128, 4, L], BF16)
        v = sb("sbv", [128, NT, 512], BF16)
        onesw = sb("onesw", [128, L], BF16)
        kb.op("dve", lambda e: e.memset(onesw[:], 1.0), W=[onesw])
        kb.dma("sp", qT[:], S["sbqT"].t.rearrange("c p t -> p c t"), W=[qT])
        kb.dma("sp", kT[:], S["sbkT"].t.rearrange("c p t -> p c t"), W=[kT])
        kb.dma("sp", v[:], S["sbv"].t.rearrange("(t p) f -> p t f", p=128), W=[v])
        e1b = [sb("e1", [128, L]) for _ in range(3)]
        spb = [sb("sp", [128, L]) for _ in range(2)]
        Fb = [sb("F", [128, L + 1]) for _ in range(2)]
        attb = [sb("att", [128, L], BF16) for _ in range(2)]
        attT = [sb("attT", [128, NT, 128], BF16) for _ in range(2)]
        ftn = [sb("ftn", [128, 1]) for _ in range(2)]
        yst = [sb("yst", [128, 128], BF16) for _ in range(4)]
        for F in Fb:
            kb.op("dve", lambda e, F=F: e.memset(F[:, 0:1], 0.0), W=[F])
        iters = [(qt, h) for qt in range(NT) for h in range(8)]

        def s1(i):
            qt, h = iters[i]
            W = 128 * (qt + 1)
            nch = (W + 511) // 512
            dsl = slice(qt * 128, (qt + 1) * 128)
            hp, hc = (h % 2) * 64, h // 2
            e1, sp_ = e1b[i % 3], spb[i % 2]
            for ch in range(nch):
                n = min(512, W - ch * 512)
                p = kb.ps()
                kb.op("pe", lambda e, p=p, ch=ch, n=n: e.matmul(p[:, 0:n], qT[hp:hp + 64, hc, dsl], kT[hp:hp + 64, hc, ch * 512:ch * 512 + n],
                                                                start=True, stop=True), R=[qT, kT], W=[p])
                kb.op("act", lambda e, p=p, ch=ch, n=n: e.activation(out=e1[:, ch * 512:ch * 512 + n], in_=p[:, 0:n], func=AF.Exp, scale=0.125),
                      R=[p], Wd=[e1])
            kb.op("act", lambda e: e.activation(out=sp_[:, 0:W], in_=e1[:, 0:W], func=AF.Ln, bias=1.0), R=[e1], W=[sp_])

        def s2(i):
            qt, h = iters[i]
            k = i % 2
            W = 128 * (qt + 1)
            dsl = slice(qt * 128, (qt + 1) * 128)
            sp_, F = spb[k], Fb[k]
            kb.op("dve", lambda e: e.tensor_tensor(out=sp_[:, dsl], in0=sp_[:, dsl], in1=g.strl[:], op=ALU.mult), R=[sp_, g.strl], W=[sp_])
            kb.op("dve", lambda e: e.memset(F[:, 0:1], 0.0), W=[F])
            kb.op("dve", lambda e: e.tensor_tensor_scan(out=F[:, 1:W + 1], data0=onesw[:, 0:W], data1=sp_[:, 0:W], initial=0.0,
                                                        op0=ALU.mult, op1=ALU.add), R=[onesw, sp_], W=[F])
            kb.op("dve", lambda e: e.tensor_scalar(out=ftn[k][:], in0=F[:, W:W + 1], scalar1=-1.0, scalar2=None, op0=ALU.mult),
                  R=[F], W=[ftn[k]])

        def s3a(i):
            qt, h = iters[i]
            k = i % 2
            W = 128 * (qt + 1)
            F = Fb[k]
            kb.op("act", lambda e: e.activation(out=F[:, 0:W], in_=F[:, 0:W], func=AF.Exp, bias=ftn[k][:, 0:1]), R=[F, ftn[k]], W=[F])

        def s3b(i):
            qt, h = iters[i]
            k = i % 2
            W = 128 * (qt + 1)
            dsl = slice(qt * 128, (qt + 1) * 128)
            e1, F, att = e1b[i % 3], Fb[k], attb[k]
            kb.op("dve", lambda e: e.tensor_tensor(out=att[:, 0:W], in0=e1[:, 0:W], in1=F[:, 0:W], op=ALU.mult), R=[e1, F], W=[att])
            kb.op("dve", lambda e: e.tensor_tensor(out=att[:, dsl], in0=att[:, dsl], in1=g.strlb[:], op=ALU.mult), R=[att, g.strlb], W=[att])

        def s3c(i):
            qt, h = iters[i]
            k = i % 2
            att, aT = attb[k], attT[k]
            for b0 in range(0, qt + 1, 8):
                nb = min(8, qt + 1 - b0)
                pt = kb.ps()
                ptb = pt[:, :].bitcast(BF16)
                for j in range(nb):
                    kb.op("pe", lambda e, j=j, b0=b0: e.transpose(ptb[:, j * 128:(j + 1) * 128], att[:, (b0 + j) * 128:(b0 + j + 1) * 128], g.identb[:]),
                          R=[att, g.identb], W=[pt])
                kb.op("act", lambda e, b0=b0, nb=nb: e.activation(out=aT[:, b0:b0 + nb, :], in_=ptb[:, 0:nb * 128].rearrange("p (j t) -> p j t", j=nb),
                                                                 func=AF.Copy), R=[pt], Wd=[aT])

        def s3d(i):
            qt, h = iters[i]
            k = i % 2
            dsl = slice(qt * 128, (qt + 1) * 128)
            hp, hc = (h % 2) * 64, h // 2
            aT = attT[k]
            py = kb.ps()
            for sbk in range(qt + 1):
                kb.op("pe", lambda e, sbk=sbk: e.matmul(py[0:64, 0:128], v[:, sbk, h * 64:(h + 1) * 64], aT[:, sbk, :],
                                                        start=(sbk == 0), stop=(sbk == qt)), R=[v, aT], W=[py])
            ys = yst[i % 4]
            kb.op("act", lambda e: e.activation(out=ys[hp:hp + 64, :], in_=py[0:64, 0:128], func=AF.Copy), R=[py], W=[ys])
            kb.dma("sp", S["yT"].t[8 + hc, hp:hp + 64, dsl], ys[hp:hp + 64, :], R=[ys])

        n_it = len(iters)

        def run(fn, i):
            if 0 <= i < n_it:
                fn(i)
        for step in range(n_it + 4):
            run(s3d, step - 4)
            run(s3a, step - 2)
            run(s1, step)
            run(s3c, step - 3)
            run(s2, step - 1)
            run(s3b, step - 2)
        kb.barrier()


NBIS = 16


def phase_dsa(g, l):
    kb, nc = g.kb, g.nc
    S = g.scr
    with ExitStack() as ph:
        def sb(name, shape, dt_=F32):
            return kb.sb(ph, name, shape, dt_)
        dk = sb("dk", [64, 2, L], BF16)
        ik = sb("ik", [64, L], BF16)
        vaug = sb("vaug", [128, NT, 2, 128], BF16)
        kb.dma("sp", dk[:], S["dkT"].t.rearrange("h d t -> d h t"), W=[dk])
        kb.dma("sp", ik[:], S["ikT"].t[0], W=[ik])
        kb.op("dve", lambda e: e.memset(vaug[:], 1.0), W=[vaug])
        for gg in range(2):
            kb.dma("sp", vaug[:, :, gg, 0:64], S["dv"].t[:, gg * 64:(gg + 1) * 64].rearrange("(t p) d -> p t d", p=128), W=[vaug])
        dqt = [sb("dqt", [64, 8, 128], BF16) for _ in range(2)]
        iqt = [sb("iqt", [64, 8, 128], BF16) for _ in range(2)]
        score = [sb("score", [128, L]) for _ in range(2)]
        rl = [sb("rl", [128, 512], BF16) for _ in range(4)]
        wabs = [sb("wabs", [128, 8]) for _ in range(2)]
        wsgn = [sb("wsgn", [128, 8]) for _ in range(2)]
        Dg = [sb("Dg", [128, 8, 128], BF16) for _ in range(2)]
        junk = sb("junk", [128, L], BF16)
        maskb = [sb("maskb", [128, L], BF16) for _ in range(2)]
        maskT = [sb("maskT", [128, NT, 128], BF16) for _ in range(2)]
        Eb = [sb("E", [128, 512], BF16) for _ in range(3)]
        M_ = [sb("M", [128, 1]) for _ in range(2)]
        A_ = [sb("A", [128, NBIS + 1]) for _ in range(2)]
        mid = [sb("mid", [128, 1]) for _ in range(2)]
        cnt = [sb("cnt", [128, 1]) for _ in range(2)]
        sela = [sb("sela", [128, 1]) for _ in range(2)]
        R0 = [sb("R0", [64, 512]) for _ in range(2)]
        ytmp = [sb("ytmp", [64, 512], BF16) for _ in range(2)]
        yraw = [sb("yraw", [64, 512]) for _ in range(2)]
        yst = [sb("dyst", [128, 4, 128], BF16) for _ in range(2)]
        cn = dict(rl=0, E=0)

        def stage_I(qt):
            k = qt % 2
            W = 128 * (qt + 1)
            dsl = slice(qt * 128, (qt + 1) * 128)
            if qt < 2:
                return
            kb.dma("sp", iqt[k][:], S["iqT"].t[:, :, dsl].rearrange("h d t -> d h t"), W=[iqt[k]])
            sc = score[k]
            nch = (W + 511) // 512
            wv = g.wix_tok[:, qt, :]
            kb.op("act", lambda e: e.activation(out=wabs[k][:], in_=wv, func=AF.Abs), R=[g.wix_tok], W=[wabs[k]])
            kb.op("dve", lambda e: e.tensor_scalar(out=wsgn[k][:], in0=wv, scalar1=0.0, scalar2=2.0, op0=ALU.is_ge, op1=ALU.mult),
                  R=[g.wix_tok], W=[wsgn[k]])
            kb.op("dve", lambda e: e.tensor_scalar(out=wsgn[k][:], in0=wsgn[k][:], scalar1=-1.0, scalar2=None, op0=ALU.add),
                  R=[wsgn[k]], W=[wsgn[k]])
            for h in range(8):
                kb.op("dve", lambda e, h=h: e.tensor_scalar(out=Dg[k][:, h, :], in0=g.identb[:], scalar1=wsgn[k][:, h:h + 1], scalar2=None,
                                                            op0=ALU.mult), R=[g.identb, wsgn[k]], Wd=[Dg[k]])
            items = [(ch, h) for ch in range(nch) for h in range(8)]
            pend = {}
            pscs = {}

            def mm(j):
                ch, h = items[j]
                n = min(512, W - ch * 512)
                p = kb.ps()
                kb.op("pe", lambda e: e.matmul(p[:, 0:n], iqt[k][0:64, h, :], ik[0:64, ch * 512:ch * 512 + n], start=True, stop=True),
                      R=[iqt[k], ik], W=[p])
                r = rl[cn["rl"] % 4]
                cn["rl"] += 1
                kb.op("act", lambda e: e.activation(out=r[:, 0:n], in_=p[:, 0:n], func=AF.Relu, scale=wabs[k][:, h:h + 1]),
                      R=[p, wabs[k]], W=[r])
                pend[j] = r

            def acc(j):
                ch, h = items[j]
                n = min(512, W - ch * 512)
                r = pend.pop(j)
                if h == 0:
                    pscs[ch] = kb.ps(pin=True)
                psc = pscs[ch]
                kb.op("pe", lambda e: e.matmul(psc[:, 0:n], Dg[k][:, h, :], r[:, 0:n], start=(h == 0), stop=(h == 7)),
                      R=[Dg[k], r], W=[psc])
                if h == 7:
                    kb.unpin(psc)
                    kb.op("act", lambda e: e.activation(out=sc[:, ch * 512:ch * 512 + n], in_=psc[:, 0:n], func=AF.Copy), R=[psc], Wd=[sc])

            for j in range(len(items) + 2):
                if j < len(items):
                    mm(j)
                if 0 <= j - 2 < len(items):
                    acc(j - 2)

        def stage_B(qt):
            k = qt % 2
            W = 128 * (qt + 1)
            dsl = slice(qt * 128, (qt + 1) * 128)
            mT = maskT[k]
            if qt < 2:
                if qt == 1:
                    kb.op("dve", lambda e: e.memset(mT[:, 0, :], 0.0), W=[mT])
                kb.op("dve", lambda e: e.tensor_copy(out=mT[:, qt, :], in_=g.negmb[:]), R=[g.negmb], W=[mT])
                return
            sc = score[k]
            kb.op("dve", lambda e: e.tensor_reduce(out=M_[k][:], in_=sc[:, 0:W], axis=AX.X, op=ALU.max, apply_absolute_value=True),
                  R=[sc], W=[M_[k]])
            kb.op("dve", lambda e: e.tensor_scalar(out=A_[k][:, 0:NBIS], in0=g.pow2[:, 0:NBIS], scalar1=M_[k][:, 0:1], scalar2=None, op0=ALU.mult),
                  R=[g.pow2, M_[k]], W=[A_[k]])
            kb.op("dve", lambda e: e.tensor_copy(out=A_[k][:, NBIS:NBIS + 1], in_=A_[k][:, NBIS - 1:NBIS]), R=[A_[k]], W=[A_[k]])
            kb.op("dve", lambda e: e.tensor_tensor(out=sc[:, dsl], in0=sc[:, dsl], in1=g.negup[:], op=ALU.add), R=[sc, g.negup], W=[sc])
            kb.op("dve", lambda e: e.memset(mid[k][:], 0.0), W=[mid[k]])
            for it in range(NBIS):
                kb.op("dve", lambda e: e.tensor_scalar(out=junk[:, 0:W], in0=sc[:, 0:W], scalar1=mid[k][:, 0:1], scalar2=0.0,
                                                       op0=ALU.is_ge, op1=ALU.add, accum_out=cnt[k][:]),
                      R=[sc, mid[k]], W=[junk, cnt[k]])
                kb.op("dve", lambda e, it=it: e.tensor_scalar(out=sela[k][:], in0=cnt[k][:], scalar1=255.5, scalar2=A_[k][:, it:it + 1],
                                                              op0=ALU.is_ge, op1=ALU.mult), R=[cnt[k], A_[k]], W=[sela[k]])
                kb.op("dve", lambda e, it=it: e.scalar_tensor_tensor(out=mid[k][:], in0=sela[k][:], scalar=A_[k][:, it + 1:it + 2], in1=mid[k][:],
                                                                     op0=ALU.subtract, op1=ALU.add), R=[sela[k], A_[k], mid[k]], W=[mid[k]])
            mb = maskb[k]
            kb.op("dve", lambda e: e.tensor_scalar(out=mb[:, 0:W], in0=sc[:, 0:W], scalar1=mid[k][:, 0:1], scalar2=-30000.0,
                                                   op0=ALU.is_lt, op1=ALU.mult), R=[sc, mid[k]], W=[mb])

        def stage_T(qt):
            if qt < 2:
                return
            k = qt % 2
            mb, mT = maskb[k], maskT[k]
            for b0 in range(0, qt + 1, 8):
                nb = min(8, qt + 1 - b0)
                pt = kb.ps()
                ptb = pt[:, :].bitcast(BF16)
                for j in range(nb):
                    kb.op("pe", lambda e, j=j, b0=b0: e.transpose(ptb[:, j * 128:(j + 1) * 128], mb[:, (b0 + j) * 128:(b0 + j + 1) * 128], g.identb[:]),
                          R=[mb, g.identb], W=[pt])
                kb.op("act", lambda e, b0=b0, nb=nb: e.activation(out=mT[:, b0:b0 + nb, :], in_=ptb[:, 0:nb * 128].rearrange("p (j t) -> p j t", j=nb),
                                                                 func=AF.Copy), R=[pt], Wd=[mT])

        def stage_A(qt):
            k = qt % 2
            dsl = slice(qt * 128, (qt + 1) * 128)
            mT = maskT[k]
            ys = yst[k]
            if qt + 1 < NT:
                kb.dma("sp", dqt[1 - k][:], S["dqT"].t[:, :, (qt + 1) * 128:(qt + 2) * 128].rearrange("h d t -> d h t"), W=[dqt[1 - k]])
            for gq in range(2):
                pO = kb.ps(pin=True)
                Es = {}

                def qk(sbk):
                    pS = kb.ps()
                    kb.op("pe", lambda e: e.matmul(pS[:, :], dk[0:64, gq, sbk * 128:(sbk + 1) * 128], dqt[k][0:64, 4 * gq:4 * gq + 4, :],
                                                   start=True, stop=False), R=[dk, dqt[k]], W=[pS])
                    kb.op("pe", lambda e: e.matmul(pS[:, :], g.identb[:], mT[:, sbk, :].unsqueeze(1).broadcast_to([128, 4, 128]),
                                                   start=False, stop=True), R=[g.identb, mT], W=[pS])
                    E = Eb[cn["E"] % 3]
                    cn["E"] += 1
                    kb.op("act", lambda e: e.activation(out=E[:], in_=pS[:, :], func=AF.Exp, scale=0.125), R=[pS], W=[E])
                    Es[sbk] = E

                def av(sbk):
                    E = Es.pop(sbk)
                    kb.op("pe", lambda e: e.matmul(pO[:, :], vaug[:, sbk, gq, :], E[:], start=(sbk == 0), stop=(sbk == qt)),
                          R=[vaug, E], W=[pO])

                for sbk in range(qt + 2):
                    if sbk <= qt:
                        qk(sbk)
                    if sbk >= 1:
                        av(sbk - 1)
                kb.unpin(pO)
                r0, yt = R0[gq], ytmp[gq]
                kb.op("act", lambda e: e.activation(out=r0[:], in_=pO[64:128, :], func=AF.Ln), R=[pO], W=[r0])
                kb.op("act", lambda e: e.activation(out=r0[:], in_=r0[:], func=AF.Exp, scale=-1.0), R=[r0], W=[r0])
                yr = yraw[gq]
                kb.op("act", lambda e: e.activation(out=yr[:], in_=pO[0:64, :], func=AF.Copy), R=[pO], W=[yr])
                kb.op("pool", lambda e: e.tensor_tensor(out=yt[:], in0=yr[:], in1=r0[:], op=ALU.mult), R=[yr, r0], W=[yt])
                for hh in range(4):
                    h = 4 * gq + hh
                    hp, hc = (h % 2) * 64, h // 2
                    kb.op("act", lambda e, hh=hh, hp=hp, hc=hc: e.activation(out=ys[hp:hp + 64, hc, :], in_=yt[:, hh * 128:(hh + 1) * 128], func=AF.Copy),
                          R=[yt], Wd=[ys])
            kb.dma("sp", S["yT"].t[12:16, :, dsl].rearrange("c p t -> p c t"), ys[:], R=[ys])

        kb.dma("sp", dqt[0][:], S["dqT"].t[:, :, 0:128].rearrange("h d t -> d h t"), W=[dqt[0]])
        stage_I(0)
        stage_B(0)
        stage_T(0)
        stage_I(1)
        modw = [sb("adaw", [128, KC, 512], BF16) for _ in range(2)] if g.dsa_hook is not None else None
        for qt in range(NT):
            if qt + 2 < NT:
                stage_I(qt + 2)
            stage_A(qt)
            if modw is not None and 1 <= qt < 13:
                mod_piece_load(g, g.dsa_hook, qt - 1, modw[(qt - 1) % 2])
            if modw is not None and 2 <= qt < 14:
                mod_piece_mm(g, g.dsa_hook, qt - 2, modw[(qt - 2) % 2])
            if qt + 1 < NT:
                stage_B(qt + 1)
                stage_T(qt + 1)
        kb.barrier()


def phase_merge(g, l):
    kb, nc = g.kb, g.nc
    S = g.scr
    with ExitStack() as ph:
        def sb(name, shape, dt_=F32):
            return kb.sb(ph, name, shape, dt_)
        yT = sb("yTall", [128, 16, L], BF16)
        yTb = [kb.buf() for _ in range(4)]
        for q in range(4):
            kb.dma("sp", yT[:, q * 4:(q + 1) * 4, :], S["yT"].t[q * 4:(q + 1) * 4].rearrange("c p t -> p c t"), W=[yTb[q]])
        mT = sb("mergedT", [128, KC, L], BF16)
        mTb = [[kb.buf() for tb in range(4)] for c in range(KC)]
        wbr = [sb("wbr", [128, 16, 256], BF16) for _ in range(2)]
        gt = [sb("gt", [128, 3, 512], BF16) for _ in range(2)]
        gbr = [sb("gbr", [128, 3, 512], BF16) for _ in range(2)]
        it = 0
        for c2 in range(4):
            w = wbr[c2 % 2]
            csl = slice(c2 * 256, (c2 + 1) * 256)
            kb.dma("pool", w[:, 0:8, :], g.inp["w_br_ssd"].t[l][:, csl].rearrange("(kc p) n -> p kc n", p=128), W=[w])
            kb.dma("pool", w[:, 8:12, :], g.inp["w_br_sb"].t[l][:, csl].rearrange("(kc p) n -> p kc n", p=128), W=[w])
            kb.dma("pool", w[:, 12:16, :], g.inp["w_br_dsa"].t[l][:, csl].rearrange("(kc p) n -> p kc n", p=128), W=[w])
            for cl in range(2):
                c = c2 * 2 + cl
                for tb in range(4):
                    k = it % 2
                    it += 1
                    tsl = slice(tb * 512, (tb + 1) * 512)
                    for br in range(3):
                        kb.dma("sp", gt[k][:, br, :], S["gT"].t[br * 8 + c, :, tsl], W=[gt[k]])
                    pbr = []
                    for br, (k0, k1) in enumerate(((0, 8), (8, 12), (12, 16))):
                        p = kb.ps()
                        for kc in range(k0, k1):
                            kb.op("pe", lambda e, p=p, kc=kc, k0=k0, k1=k1: e.matmul(
                                p[:, :], w[:, kc, cl * 128:(cl + 1) * 128], yT[:, kc, tsl], start=(kc == k0), stop=(kc == k1 - 1)),
                                R=[w, yTb[kc // 4]], W=[p])
                        pbr.append(p)
                    gb = gbr[k]
                    for br in range(3):
                        kb.op("dve", lambda e, br=br: e.tensor_tensor(out=gb[:, br, :], in0=pbr[br][:, :], in1=gt[k][:, br, :], op=ALU.mult),
                              R=[pbr[br], gt[k]], Wd=[gb])
                    pm = kb.ps()
                    for br in range(3):
                        kb.op("pe", lambda e, br=br: e.matmul(pm[:, :], g.identb[:], gb[:, br, :], start=(br == 0), stop=(br == 2)),
                              R=[g.identb, gb], W=[pm])
                    kb.op("act", lambda e: e.activation(out=mT[:, c, tsl], in_=pm[:, :], func=AF.Copy), R=[pm], W=[mTb[c][tb]])
        wo = [sb("wo", [128, KC, 256], BF16) for _ in range(2)]
        for c2 in range(4):
            w = wo[c2 % 2]
            kb.dma("pool", w[:], g.inp["w_out"].t[l][:, c2 * 256:(c2 + 1) * 256].rearrange("(kc p) n -> p kc n", p=128), W=[w])
            for cl in range(2):
                c = c2 * 2 + cl
                for tb in range(4):
                    tsl = slice(tb * 512, (tb + 1) * 512)
                    p = kb.ps()
                    for kc in range(KC):
                        kb.op("pe", lambda e, p=p, kc=kc: e.matmul(p[:, :], w[:, kc, cl * 128:(cl + 1) * 128], mT[:, kc, tsl],
                                                                   start=(kc == 0), stop=(kc == KC - 1)), R=[w, mTb[kc][tb]], W=[p])
                    xs = g.xT[:, c, tsl]
                    kb.op("dve", lambda e, p=p, c=c, xs=xs: e.scalar_tensor_tensor(
                        out=xs, in0=p[:, :], scalar=g.modT[:, 16 + c:17 + c], in1=xs, op0=ALU.mult, op1=ALU.add),
                        R=[p, g.modT, g.xTb[c][tb]], W=[g.xTb[c][tb]])
        kb.barrier()


def phase_mlp(g, l):
    kb = g.kb
    with ExitStack() as ph:
        wup = [kb.sb(ph, f"wup{i}", [128, KC, 512], BF16) for i in range(2)]
        wdn = [kb.sb(ph, f"wdn{i}", [128, 4, D], BF16) for i in range(2)]
        act = [kb.sb(ph, f"mact{i}", [128, 4, L], BF16) for i in range(2)]
        actb = [[[kb.buf() for tb in range(4)] for hc in range(4)] for i in range(2)]
        rl = [kb.sb(ph, f"mrl{i}", [128, 512], F32) for i in range(2)]
        nrl = 0
        for gi in range(8):
            wu, wd, a, ab = wup[gi % 2], wdn[gi % 2], act[gi % 2], actb[gi % 2]
            load_w(g, wu, g.inp["w_up"].t[l][:, gi * 512:(gi + 1) * 512])
            load_w(g, wd, g.inp["w_down"].t[l][gi * 512:(gi + 1) * 512, :])
            for tb in range(4):
                for hc in range(4):
                    p = kb.ps()
                    for kc in range(KC):
                        kb.op("pe", lambda e, p=p, kc=kc, hc=hc: e.matmul(
                            p[:, :], wu[:, kc, hc * 128:(hc + 1) * 128], g.hT[:, kc, tb * 512:(tb + 1) * 512],
                            start=(kc == 0), stop=(kc == KC - 1)), R=[wu, g.hTb[kc][tb]], W=[p])
                    r = rl[nrl % 2]
                    nrl += 1
                    kb.op("act", lambda e, p=p, r=r: e.activation(out=r[:], in_=p[:, :], func=AF.Relu), R=[p], W=[r])
                    kb.op("act", lambda e, r=r, hc=hc: e.activation(out=a[:, hc, tb * 512:(tb + 1) * 512], in_=r[:], func=AF.Square),
                          R=[r], W=[ab[hc][tb]])
            import os
            if os.environ.get("MLP_UP_ONLY"):
                continue
            for tb in range(4):
                for c in range(KC):
                    p = kb.ps()
                    for hc in range(4):
                        kb.op("pe", lambda e, p=p, hc=hc, c=c: e.matmul(
                            p[:, :], wd[:, hc, c * 128:(c + 1) * 128], a[:, hc, tb * 512:(tb + 1) * 512],
                            start=(hc == 0), stop=(hc == 3)), R=[wd, ab[hc][tb]], W=[p])
                    xs = g.xT[:, c, tb * 512:(tb + 1) * 512]
                    kb.op("dve", lambda e, p=p, c=c, xs=xs: e.scalar_tensor_tensor(
                        out=xs, in0=p[:, :], scalar=g.modT[:, 40 + c:41 + c], in1=xs, op0=ALU.mult, op1=ALU.add),
                        R=[p, g.modT, g.xTb[c][tb]], W=[g.xTb[c][tb]])
        kb.barrier()


def rstd_bcast(kb, ph, xT, xTb, tb, onesb, sq, rs):
    p = kb.ps()
    for c in range(KC):
        s = sq[c % 2]
        kb.op("act", lambda e, s=s, c=c: e.activation(out=s[:], in_=xT[:, c, tb * 512:(tb + 1) * 512], func=AF.Square),
              R=[xTb[c][tb]], W=[s])
        kb.op("pe", lambda e, s=s, c=c, p=p: e.matmul(p[:, :], onesb[:], s[:], start=(c == 0), stop=(c == KC - 1)),
              R=[s, onesb], W=[p])
    kb.op("dve", lambda e: e.tensor_scalar(out=rs[:], in0=p[:, :], scalar1=1.0 / D, scalar2=EPS, op0=ALU.mult, op1=ALU.add),
          R=[p], W=[rs])
    kb.op("act", lambda e: e.activation(out=rs[:], in_=rs[:], func=AF.Ln), R=[rs], W=[rs])
    kb.op("act", lambda e: e.activation(out=rs[:], in_=rs[:], func=AF.Exp, scale=-0.5), R=[rs], W=[rs])


def final_norm(kb, nc, xT, xTb, fnw, onesb, ident, out_d):
    with ExitStack() as ph:
        sq = [kb.sb(ph, f"fsq{i}", [128, 512], BF16) for i in range(2)]
        rs = kb.sb(ph, "frs", [128, 512], F32)
        yT = [kb.sb(ph, f"fyT{i}", [128, 512], F32) for i in range(2)]
        ost = [kb.sb(ph, f"fost{i}", [128, 4, D], F32) for i in range(2)]
        for tb in range(4):
            rstd_bcast(kb, ph, xT, xTb, tb, onesb, sq, rs)
            o = ost[tb % 2]
            for c in range(KC):
                y = yT[c % 2]
                kb.op("dve", lambda e, y=y, c=c: e.scalar_tensor_tensor(
                    out=y[:], in0=xT[:, c, tb * 512:(tb + 1) * 512], scalar=fnw[:, c:c + 1], in1=rs[:],
                    op0=ALU.mult, op1=ALU.mult), R=[xTb[c][tb], fnw, rs], W=[y])
                p = kb.ps()
                for j in range(4):
                    kb.op("pe", lambda e, y=y, j=j, p=p: e.transpose(
                        p[:, j * 128:(j + 1) * 128], y[:, j * 128:(j + 1) * 128], ident[:]), R=[y, ident], W=[p])
                src = p[:, :].rearrange("p (j f) -> p j f", j=4)
                dst = o[:, :, c * 128:(c + 1) * 128]
                kb.op("act", lambda e, dst=dst, src=src: e.activation(out=dst, in_=src, func=AF.Copy), R=[p], Wd=[o])
            kb.dma("sp", out_d.t[tb * 512:(tb + 1) * 512, :].rearrange("(j p) d -> p j d", p=128), o[:], R=[o], W=[out_d])
        kb.barrier()


_NC_CACHE = {}


def make_in_maps(inputs):
    nb = inputs["x"].shape[0]
    inputs = dict(inputs)
    inputs.update(host_consts())
    shared = {n: np.ascontiguousarray(np.asarray(inputs[n], dtype=np.float32)) for n in IN_SHAPES if n not in ("x", "c")}
    in_maps = []
    for b in range(nb):
        m = dict(shared)
        m["x"] = np.ascontiguousarray(inputs["x"][b])
        m["c"] = np.ascontiguousarray(inputs["c"][b])
        in_maps.append(m)
    return in_maps


def kernel(**inputs):
    nb = inputs["x"].shape[0]
    if "nc" not in _NC_CACHE:
        _NC_CACHE["nc"] = build()[0]
    nc = _NC_CACHE["nc"]
    res = run_bass_kernel_spmd(nc, make_in_maps(inputs), core_ids=list(range(nb)))
    return np.stack([r["out"] for r in res.results], axis=0)
```

```python
import math
from contextlib import ExitStack

import numpy as np
import concourse.bass as bass
import concourse.mybir as mybir
from concourse.bass_utils import run_bass_kernel_spmd

F32 = mybir.dt.float32
BF16 = mybir.dt.bfloat16
I32 = mybir.dt.int32
AF = mybir.ActivationFunctionType
ALU = mybir.AluOpType
AX = mybir.AxisListType

D = 1024
L = 2048
DEPTH = 2
NT = L // 128
KC = D // 128
EPS = 1e-6
import os
NDS = int(os.environ.get("NDS", "12"))
STRICT = bool(int(os.environ.get("KSTRICT", "1")))


class Buf:
    __slots__ = ("name", "w", "r", "excl")

    def __init__(self, name):
        self.name = name
        self.excl = False
        self.w = None
        self.r = {}


class T:
    def __init__(self, t, buf):
        self.t = t
        self.b = buf

    def __getitem__(self, k):
        return self.t[k]


class KB:
    def __init__(self, nc):
        self.nc = nc
        self.es = ExitStack()
        self.sems = {}
        self.engs = {}
        for name, e in (("pe", nc.tensor), ("act", nc.scalar), ("dve", nc.vector),
                        ("pool", nc.gpsimd), ("sp", nc.sync)):
            key = "s_" + name
            self.sems[key] = self.es.enter_context(nc.semaphore(key))
            self.engs[name] = dict(e=e, key=key, cnt=0, seen={})
        self.dq = {}
        for q in ("sp", "pool"):
            keys = []
            for i in range(NDS):
                key = f"d_{q}{i}"
                self.sems[key] = self.es.enter_context(nc.semaphore(key))
                keys.append(key)
            self.dq[q] = dict(keys=keys, cnt=[0] * NDS, nxt=0)
        self.nbuf = 0
        self.psum = []
        self.ps_next = 0
        self.pinned = set()

    def buf(self, name=None):
        self.nbuf += 1
        return Buf(name or f"b{self.nbuf}")

    def sb(self, es, name, shape, dtype):
        self.nbuf += 1
        name = f"{name}_{self.nbuf}"
        t = es.enter_context(self.nc.sbuf_tensor(name, list(shape), dtype))
        return T(t, self.buf(name))

    def dram(self, name, shape, dtype, kind="Internal"):
        t = self.nc.dram_tensor(name, list(shape), dtype, kind=kind)
        return T(t.ap(), self.buf(name))

    def init_psum(self):
        for i in range(8):
            t = self.es.enter_context(self.nc.psum_tensor(f"ps{i}", [128, 512], F32))
            self.psum.append(T(t, self.buf(f"ps{i}")))
            self.psum[-1].b.excl = True

    def ps(self, pin=False):
        while True:
            i = self.ps_next
            self.ps_next = (self.ps_next + 1) % 8
            if i not in self.pinned:
                break
        if pin:
            self.pinned.add(i)
        return self.psum[i]

    def unpin(self, p):
        self.pinned.discard(self.psum.index(p))

    def _deps(self, own_key, R, W, same_raw, Wd=()):
        deps = {}

        def add(ev):
            if ev is None:
                return
            k, v = ev
            if deps.get(k, 0) < v:
                deps[k] = v

        for t in R:
            b = t.b if isinstance(t, T) else t
            if b.w is not None and (b.w[0] != own_key or same_raw):
                add(b.w)
            if b.excl:
                for k, v in b.r.items():
                    if k != own_key:
                        add((k, v))
        for t in W:
            b = t.b if isinstance(t, T) else t
            if b.w is not None and (b.w[0] != own_key or (STRICT and same_raw)):
                add(b.w)
            for k, v in b.r.items():
                if k != own_key or (STRICT and same_raw):
                    add((k, v))
        for t in Wd:
            b = t.b if isinstance(t, T) else t
            if b.w is not None and b.w[0] != own_key:
                add(b.w)
            for k, v in b.r.items():
                if k != own_key or (STRICT and same_raw):
                    add((k, v))
        return deps

    def _mark(self, ev, R, W):
        k, v = ev
        for t in R:
            b = t.b if isinstance(t, T) else t
            if b.r.get(k, 0) < v:
                b.r[k] = v
        for t in W:
            b = t.b if isinstance(t, T) else t
            b.w = ev
            b.r = {}

    def _wait(self, eng, deps):
        for k, v in deps.items():
            if eng["seen"].get(k, 0) < v:
                eng["e"].wait_ge(self.sems[k], v)
                eng["seen"][k] = v

    def op(self, engname, fn, R=(), W=(), Wd=()):
        eng = self.engs[engname]
        deps = self._deps(eng["key"], R, W, same_raw=(engname != "pe"), Wd=Wd)
        W = list(W) + list(Wd)
        self._wait(eng, deps)
        ins = fn(eng["e"])
        eng["cnt"] += 1
        ins.then_inc(self.sems[eng["key"]], 1)
        self._mark((eng["key"], eng["cnt"]), R, W)
        return ins

    def dma(self, q, out, in_, R=(), W=(), **kw):
        eng = self.engs[q]
        dq = self.dq[q]
        deps = self._deps(None, R, W, same_raw=True)
        i = dq["nxt"]
        dq["nxt"] = (i + 1) % NDS
        key = dq["keys"][i]
        if dq["cnt"][i] > 0:
            deps[key] = max(deps.get(key, 0), dq["cnt"][i])
        self._wait(eng, deps)
        dq["cnt"][i] += 16
        eng["e"].dma_start(out=out, in_=in_, **kw).then_inc(self.sems[key], 16)
        self._mark((key, dq["cnt"][i]), R, W)

    def barrier(self):
        allev = {}
        for name, eng in self.engs.items():
            if eng["cnt"] > 0:
                allev[eng["key"]] = eng["cnt"]
        for q, dq in self.dq.items():
            for key, c in zip(dq["keys"], dq["cnt"]):
                if c > 0:
                    allev[key] = c
        for name, eng in self.engs.items():
            deps = {k: v for k, v in allev.items() if k != eng["key"]}
            self._wait(eng, deps)

    def finish(self, out_bufs):
        eng = self.engs["sp"]
        deps = {}
        for t in out_bufs:
            b = t.b if isinstance(t, T) else t
            if b.w is not None:
                deps[b.w[0]] = max(deps.get(b.w[0], 0), b.w[1])
        self._wait(eng, deps)
        self.barrier()


IN_SHAPES = {
    "x": [L, D], "c": [D], "norm1_w": [DEPTH, D], "ada_w": [DEPTH, D, 6 * D], "ada_b": [DEPTH, 6 * D],
    "w_in": [DEPTH, D, 8536], "conv_w": [DEPTH, 4, 1536], "conv_b": [DEPTH, 1536], "dt_bias": [DEPTH, 16],
    "a_log": [DEPTH, 16], "d_skip": [DEPTH, 16], "ssd_norm_w": [DEPTH, D], "w_br_ssd": [DEPTH, D, D],
    "w_br_sb": [DEPTH, 512, D], "w_br_dsa": [DEPTH, 512, D], "w_out": [DEPTH, D, D], "norm2_w": [DEPTH, D],
    "w_up": [DEPTH, D, 4 * D], "w_down": [DEPTH, 4 * D, D], "final_norm_w": [D],
    "rope_cs": [L, 16],
}


def host_consts():
    half = 8
    inv_freq = np.exp(np.arange(half, dtype=np.float32) * np.float32(-2.0 * math.log(500000.0) / 16)).astype(np.float32)
    ang = np.arange(L, dtype=np.float32)[:, None] * inv_freq[None, :]
    return {"rope_cs": np.concatenate([np.cos(ang), np.sin(ang)], axis=1).astype(np.float32)}


class G:
    pass


def build(nlayers=DEPTH, dbg=(), skip_mixer=False, phases=("mod", "inproj", "ssd", "sb", "dsa", "merge", "norm2", "mlp")):
    nc = bass.Bass("TRN2", target_bir_lowering=False)
    kb = KB(nc)
    es = kb.es
    kb.init_psum()
    g = G()
    g.nc, g.kb, g.es, g.dbg = nc, kb, es, set(dbg)
    g.inp = {n: kb.dram(n, shp, F32, kind="ExternalInput") for n, shp in IN_SHAPES.items()}
    out_d = kb.dram("out", [L, D], F32, kind="ExternalOutput")
    g.dbg_out = {}

    g.ident = ident = kb.sb(es, "ident", [128, 128], F32)
    g.identb = identb = kb.sb(es, "identb", [128, 128], BF16)
    g.onesb = onesb = kb.sb(es, "onesb", [128, 128], BF16)
    g.onesf = kb.sb(es, "onesf", [128, 128], F32)
    g.triu = kb.sb(es, "triu", [128, 128], F32)
    g.negm = kb.sb(es, "negm", [128, 128], F32)
    g.triub = kb.sb(es, "triub", [128, 128], BF16)
    g.strl = kb.sb(es, "strl", [128, 128], F32)
    g.strlb = kb.sb(es, "strlb", [128, 128], BF16)
    g.trilb = kb.sb(es, "trilb", [128, 128], BF16)
    g.negup = kb.sb(es, "negup", [128, 128], F32)
    g.negmb = kb.sb(es, "negmb", [128, 128], BF16)
    g.pow2 = kb.sb(es, "pow2", [128, 24], F32)
    with ExitStack() as tmp:
        coli = kb.sb(tmp, "coli", [128, 128], I32)
        rowi = kb.sb(tmp, "rowi", [128, 1], I32)
        colf = kb.sb(tmp, "colf", [128, 128], F32)
        rowf = kb.sb(tmp, "rowf", [128, 1], F32)
        kb.op("pool", lambda e: e.iota(coli[:], [[1, 128]], base=0, channel_multiplier=0), W=[coli])
        kb.op("pool", lambda e: e.iota(rowi[:], [[0, 1]], base=0, channel_multiplier=1), W=[rowi])
        kb.op("dve", lambda e: e.tensor_copy(out=colf[:], in_=coli[:]), R=[coli], W=[colf])
        kb.op("dve", lambda e: e.tensor_copy(out=rowf[:], in_=rowi[:]), R=[rowi], W=[rowf])
        kb.op("dve", lambda e: e.tensor_scalar(out=ident[:], in0=colf[:], scalar1=rowf[:, 0:1], scalar2=None,
                                               op0=ALU.is_equal), R=[colf, rowf], W=[ident])
        kb.op("dve", lambda e: e.tensor_copy(out=identb[:], in_=ident[:]), R=[ident], W=[identb])
        kb.op("dve", lambda e: e.memset(onesb[:], 1.0), W=[onesb])
        kb.op("dve", lambda e: e.memset(g.onesf[:], 1.0), W=[g.onesf])
        kb.op("dve", lambda e: e.tensor_scalar(out=g.triu[:], in0=colf[:], scalar1=rowf[:, 0:1], scalar2=None, op0=ALU.is_ge),
              R=[colf, rowf], W=[g.triu])
        kb.op("dve", lambda e: e.tensor_scalar(out=g.negm[:], in0=g.triu[:], scalar1=-1.0, scalar2=30000.0, op0=ALU.add, op1=ALU.mult),
              R=[g.triu], W=[g.negm])
        kb.op("dve", lambda e: e.tensor_copy(out=g.triub[:], in_=g.triu[:]), R=[g.triu], W=[g.triub])
        kb.op("dve", lambda e: e.tensor_copy(out=g.negmb[:], in_=g.negm[:]), R=[g.negm], W=[g.negmb])
        kb.op("dve", lambda e: e.tensor_scalar(out=g.strl[:], in0=colf[:], scalar1=rowf[:, 0:1], scalar2=None, op0=ALU.is_lt),
              R=[colf, rowf], W=[g.strl])
        kb.op("dve", lambda e: e.tensor_copy(out=g.strlb[:], in_=g.strl[:]), R=[g.strl], W=[g.strlb])
        kb.op("dve", lambda e: e.tensor_scalar(out=g.trilb[:], in0=colf[:], scalar1=rowf[:, 0:1], scalar2=None, op0=ALU.is_le),
              R=[colf, rowf], W=[g.trilb])
        kb.op("dve", lambda e: e.tensor_scalar(out=g.negup[:], in0=g.trilb[:], scalar1=-1.0, scalar2=1.0e9, op0=ALU.add, op1=ALU.mult),
              R=[g.trilb], W=[g.negup])
        for kk_ in range(24):
            kb.op("dve", lambda e, kk_=kk_: e.memset(g.pow2[:, kk_:kk_ + 1], float(2.0 ** (-kk_))), Wd=[g.pow2])
        kb.barrier()

    g.xT = xT = kb.sb(es, "xT", [128, KC, L], F32)
    g.xTb = xTb = [[kb.buf(f"xT{c}_{tb}") for tb in range(4)] for c in range(KC)]
    g.hTb = [[kb.buf(f"hT{c}_{tb}") for tb in range(4)] for c in range(KC)]

    fnw = load_cols(g, es, "fnw", g.inp["final_norm_w"].t, KC)
    g.ccol = load_cols(g, es, "ccol", g.inp["c"].t, KC)
    g.csilu = kb.sb(es, "csilu", [128, KC], BF16)
    kb.op("act", lambda e: e.activation(out=g.csilu[:], in_=g.ccol[:], func=AF.Silu), R=[g.ccol], W=[g.csilu])
    g.modT_l = [kb.sb(es, "modT", [128, 48], F32) for _ in range(DEPTH)]
    g.modraw = [kb.sb(es, "modraw", [128, 48], F32) for _ in range(DEPTH)]
    g.A1_l = [kb.sb(es, "A1", [128, KC], F32) for _ in range(DEPTH)]
    g.A2_l = [kb.sb(es, "A2", [128, KC], F32) for _ in range(DEPTH)]
    g.dsa_hook = None

    x_in = g.inp["x"]
    with ExitStack() as ph:
        xin = [kb.sb(ph, f"xin{i}", [128, D], F32) for i in range(2)]
        for tt in range(NT):
            xi = xin[tt % 2]
            kb.dma("sp", xi[:], x_in.t[tt * 128:(tt + 1) * 128, :], W=[xi])
            for half in range(2):
                p = kb.ps()
                for j in range(4):
                    c = half * 4 + j
                    kb.op("pe", lambda e, c=c, j=j, p=p, xi=xi: e.transpose(
                        p[:, j * 128:(j + 1) * 128], xi[:, c * 128:(c + 1) * 128], ident[:]),
                        R=[xi, ident], W=[p])
                tb = tt // 4
                wb = [xTb[half * 4 + j][tb] for j in range(4)]
                dst = xT[:, half * 4:half * 4 + 4, tt * 128:(tt + 1) * 128]
                src = p[:, :].rearrange("p (j t) -> p j t", j=4)
                if half == 0:
                    kb.op("act", lambda e, dst=dst, src=src: e.activation(out=dst, in_=src, func=AF.Copy), R=[p], W=wb)
                else:
                    kb.op("dve", lambda e, dst=dst, src=src: e.tensor_copy(out=dst, in_=src), R=[p], W=wb)
        kb.barrier()

    def scr(name, shape, dtype=BF16):
        return kb.dram("scr_" + name, shape, dtype, kind=("ExternalOutput" if name in g.dbg else "Internal"))
    g.scr = dict(zs=scr("zs", [L, 1024]), sbv=scr("sbv", [L, 512]), dv=scr("dv", [L, 128]),
                 dqT=scr("dqT", [8, 64, L]), dkT=scr("dkT", [2, 64, L]), iqT=scr("iqT", [8, 64, L]), ikT=scr("ikT", [1, 64, L]),
                 yT=scr("yT", [16, 128, L]), sbqT=scr("sbqT", [4, 128, L]), sbkT=scr("sbkT", [4, 128, L]), gT=scr("gT", [24, 128, L]))
    for name in g.scr:
        if name in g.dbg:
            g.dbg_out["scr_" + name] = g.scr[name]
    g.ropecs = kb.sb(es, "ropecs", [128, NT, 16], F32)
    kb.dma("sp", g.ropecs[:], g.inp["rope_cs"].t.rearrange("(t p) c -> p t c", p=128), W=[g.ropecs])
    g.dt_tok = kb.sb(es, "dt_tok", [128, NT, 16], F32)
    g.wix_tok = kb.sb(es, "wix_tok", [128, NT, 8], F32)
    g.dtb_bc = kb.sb(es, "dtb_bc", [128, 16], F32)
    g.cb = kb.sb(es, "cb", [128, 12], F32)
    g.cw = [kb.sb(es, f"cw{k}", [128, 12], F32) for k in range(4)]

    for l in range(nlayers):
        g.modT, g.A1, g.A2 = g.modT_l[l], g.A1_l[l], g.A2_l[l]
        if "mod" in phases:
            if l == 0:
                phase_mod(g, l)
            else:
                mod_finish(g, l)
        g.dsa_hook = (l + 1) if (l + 1 < nlayers and "mod" in phases) else None
        dump_sb(g, f"mod{l}", g.modT, [128, 48])
        if not skip_mixer:
            with ExitStack() as ssd_scope:
                g.xs_tok = kb.sb(ssd_scope, "xs_tok", [128, NT, 1024], BF16)
                g.bt_tok = kb.sb(ssd_scope, "bt_tok", [128, NT, 256], BF16)
                g.bcT = kb.sb(ssd_scope, "bcT", [128, 4, L], BF16)
                with nc.allow_non_contiguous_dma(reason="tiny vectors"):
                    kb.dma("sp", g.dtb_bc[:], g.inp["dt_bias"].t[l].partition_broadcast(128), W=[g.dtb_bc])
                    kb.dma("sp", g.cb[:], g.inp["conv_b"].t[l].rearrange("(c p) -> p c", p=128), W=[g.cb])
                    for k in range(4):
                        kb.dma("sp", g.cw[k][:], g.inp["conv_w"].t[l][k].rearrange("(c p) -> p c", p=128), W=[g.cw[k]])
                with ExitStack() as hsc:
                    g.hT = kb.sb(hsc, "hT", [128, KC, L], BF16)
                    phase_norm(g, l, which=1)
                    dump_hT(g, f"h{l}")
                    if "inproj" in phases:
                        phase_inproj(g, l)
                dump_sb(g, f"xs_tok{l}", g.xs_tok, [128, NT, 1024], BF16)
                dump_sb(g, f"bt_tok{l}", g.bt_tok, [128, NT, 256], BF16)
                dump_sb(g, f"bcT{l}", g.bcT, [128, 4, L], BF16)
                dump_sb(g, f"dt_tok{l}", g.dt_tok, [128, NT, 16], F32)
                dump_sb(g, f"wix_tok{l}", g.wix_tok, [128, NT, 8], F32)
                kb.barrier()
                if "ssd" in phases:
                    phase_ssd(g, l)
            if "sb" in phases:
                phase_sb(g, l)
            if "dsa" in phases:
                phase_dsa(g, l)
            if "merge" in phases:
                phase_merge(g, l)
            dump_xT(g, f"xmix{l}")
        with ExitStack() as hsc:
            g.hT = kb.sb(hsc, "hT", [128, KC, L], BF16)
            if "norm2" in phases:
                phase_norm(g, l, which=2)
            if "mlp" in phases:
                phase_mlp(g, l)
        dump_xT(g, f"xout{l}")

    final_norm(kb, nc, xT, xTb, fnw, onesb, ident, out_d)
    kb.finish([out_d] + list(g.dbg_out.values()))
    return nc, sorted(g.dbg_out.keys())


def load_cols(g, es_, name, src_ap, n):
    kb = g.kb
    t = kb.sb(es_, name, [128, n], F32)
    with g.nc.allow_non_contiguous_dma(reason="tiny per-feature vector load"):
        for c0 in range(0, n, 8):
            c1 = min(n, c0 + 8)
            kb.dma("sp", t[:, c0:c1], src_ap[c0 * 128:c1 * 128].rearrange("(c p) -> p c", p=128), W=[t])
    return t


def dump_sb(g, name, t, shape, dtype=F32):
    if name not in g.dbg:
        return
    d = g.kb.dram("dbg_" + name, shape, dtype, kind="ExternalOutput")
    g.kb.barrier()
    g.kb.dma("sp", d.t, t[:], R=[t], W=[d])
    g.dbg_out["dbg_" + name] = d


def dump_xT(g, name):
    if name not in g.dbg:
        return
    d = g.kb.dram("dbg_" + name, [128, KC, L], F32, kind="ExternalOutput")
    g.kb.barrier()
    g.kb.dma("sp", d.t, g.xT[:], R=[b for row in g.xTb for b in row], W=[d])
    g.dbg_out["dbg_" + name] = d


def dump_hT(g, name):
    if name not in g.dbg:
        return
    d = g.kb.dram("dbg_" + name, [128, KC, L], BF16, kind="ExternalOutput")
    g.kb.barrier()
    g.kb.dma("sp", d.t, g.hT[:], R=[b for row in g.hTb for b in row], W=[d])
    g.dbg_out["dbg_" + name] = d


def load_w(g, t, src_rows_ap):
    g.kb.dma("pool", t[:], src_rows_ap.rearrange("(kc p) n -> p kc n", p=128), W=[t])


def mod_piece_load(g, l, j4, w):
    g.kb.dma("pool", w[:], g.inp["ada_w"].t[l][:, j4 * 512:(j4 + 1) * 512].rearrange("(kc p) n -> p kc n", p=128), W=[w])


def mod_piece_mm(g, l, j4, w):
    kb = g.kb
    p = kb.ps()
    for jl in range(4):
        for kc in range(KC):
            kb.op("pe", lambda e, jl=jl, kc=kc: e.matmul(p[:, jl:jl + 1], w[:, kc, jl * 128:(jl + 1) * 128], g.csilu[:, kc:kc + 1],
                                                         start=(kc == 0), stop=(kc == KC - 1)), R=[w, g.csilu], W=[p])
    raw = g.modraw[l]
    kb.op("act", lambda e: e.activation(out=raw[:, j4 * 4:(j4 + 1) * 4], in_=p[:, 0:4], func=AF.Copy), R=[p], Wd=[raw])


def mod_piece(g, l, j4, w):
    mod_piece_load(g, l, j4, w)
    mod_piece_mm(g, l, j4, w)


def mod_finish(g, l):
    kb = g.kb
    with ExitStack() as ph:
        adab = load_cols(g, ph, "adab", g.inp["ada_b"].t[l], 48)
        n1w = load_cols(g, ph, "n1w", g.inp["norm1_w"].t[l], KC)
        n2w = load_cols(g, ph, "n2w", g.inp["norm2_w"].t[l], KC)
        modT, A1, A2 = g.modT_l[l], g.A1_l[l], g.A2_l[l]
        kb.op("dve", lambda e: e.tensor_tensor(out=modT[:], in0=g.modraw[l][:], in1=adab[:], op=ALU.add),
              R=[g.modraw[l], adab], W=[modT])
        kb.op("dve", lambda e: e.scalar_tensor_tensor(out=A1[:], in0=modT[:, 8:16], scalar=1.0, in1=n1w[:],
                                                      op0=ALU.add, op1=ALU.mult), R=[modT, n1w], W=[A1])
        kb.op("dve", lambda e: e.scalar_tensor_tensor(out=A2[:], in0=modT[:, 32:40], scalar=1.0, in1=n2w[:],
                                                      op0=ALU.add, op1=ALU.mult), R=[modT, n2w], W=[A2])
        kb.barrier()


def phase_mod(g, l):
    kb = g.kb
    with ExitStack() as ph:
        wt = [kb.sb(ph, f"adaw{i}", [128, KC, 512], BF16) for i in range(3)]
        for j4 in range(12):
            mod_piece(g, l, j4, wt[j4 % 3])
        kb.barrier()
    mod_finish(g, l)


def phase_norm(g, l, which):
    kb = g.kb
    A = g.A1 if which == 1 else g.A2
    sh0 = 0 if which == 1 else 24
    with ExitStack() as ph:
        sq = [kb.sb(ph, f"nsq{i}", [128, 512], BF16) for i in range(2)]
        rs = kb.sb(ph, "nrs", [128, 512], F32)
        tmp = [kb.sb(ph, f"ntmp{i}", [128, 512], F32) for i in range(2)]
        for tb in range(4):
            rstd_bcast(kb, ph, g.xT, g.xTb, tb, g.onesb, sq, rs)
            for c in range(KC):
                t = tmp[c % 2]
                kb.op("dve", lambda e, t=t, c=c: e.tensor_tensor(out=t[:], in0=g.xT[:, c, tb * 512:(tb + 1) * 512], in1=rs[:],
                                                                 op=ALU.mult), R=[g.xTb[c][tb], rs], W=[t])
                kb.op("act", lambda e, t=t, c=c: e.activation(
                    out=g.hT[:, c, tb * 512:(tb + 1) * 512], in_=t[:], func=AF.Identity,
                    scale=A[:, c:c + 1], bias=g.modT[:, sh0 + c:sh0 + c + 1]), R=[t, A, g.modT], W=[g.hTb[c][tb]])
        kb.barrier()


OFF = dict(z=0, xbc=1024, dt=2560, sbq=2576, sbk=3088, sbv=3600, dsq=4112, dsk=4624, dsv=4752,
           ixq=4880, ixk=5392, ixw=5456, gate=5464)


def rope(g, f3, nh, tt, rt):
    kb = g.kb
    fb = f3._buf
    cos = g.ropecs[:, tt:tt + 1, 0:8].broadcast_to([128, nh, 8])
    sin = g.ropecs[:, tt:tt + 1, 8:16].broadcast_to([128, nh, 8])
    x1 = f3.ap[:, :, 0:8]
    x2 = f3.ap[:, :, 8:16]
    t = [rt[:, i, 0:nh, :] for i in range(4)]
    kb.op("dve", lambda e: e.tensor_tensor(out=t[0], in0=x1, in1=cos, op=ALU.mult), R=[fb, g.ropecs], W=[rt])
    kb.op("dve", lambda e: e.tensor_tensor(out=t[1], in0=x2, in1=sin, op=ALU.mult), R=[fb, g.ropecs], W=[rt])
    kb.op("dve", lambda e: e.tensor_tensor(out=t[2], in0=x2, in1=cos, op=ALU.mult), R=[fb, g.ropecs], W=[rt])
    kb.op("dve", lambda e: e.tensor_tensor(out=t[3], in0=x1, in1=sin, op=ALU.mult), R=[fb, g.ropecs], W=[rt])
    kb.op("dve", lambda e: e.tensor_tensor(out=x1, in0=t[0], in1=t[1], op=ALU.subtract), R=[rt], W=[fb])
    kb.op("dve", lambda e: e.tensor_tensor(out=x2, in0=t[2], in1=t[3], op=ALU.add), R=[rt], W=[fb])


class V:
    def __init__(self, ap, buf):
        self.ap = ap
        self._buf = buf


def phase_inproj(g, l):
    kb, nc = g.kb, g.nc
    win = g.inp["w_in"].t[l]
    S = g.scr
    with ExitStack() as ph:
        wt = [kb.sb(ph, "winw", [128, KC, 512], BF16) for _ in range(2)]
        nw = [0]

        def getw(n0, N):
            w = wt[nw[0] % 2]
            nw[0] += 1
            kb.dma("pool", w[:, :, 0:N], win[:, n0:n0 + N].rearrange("(kc p) n -> p kc n", p=128), W=[w])
            return w

        stgb = [kb.sb(ph, "stgb", [128, 512], BF16) for _ in range(3)]
        stgf = [kb.sb(ph, "stgf", [128, 512], F32) for _ in range(2)]
        rt = kb.sb(ph, "ropetmp", [128, 4, 16, 8], F32)
        tst = [kb.sb(ph, "tst", [64, 8, 512], BF16) for _ in range(2)]
        cnt = dict(b=0, f=0, t=0)

        def nxt(lst, k):
            r = lst[cnt[k] % len(lst)]
            cnt[k] += 1
            return r

        def tok_mm(w, N, tt):
            p = kb.ps()
            for kc in range(KC):
                kb.op("pe", lambda e, kc=kc: e.matmul(p[:, 0:N], g.hT[:, kc, tt * 128:(tt + 1) * 128], w[:, kc, 0:N],
                                                      start=(kc == 0), stop=(kc == KC - 1)),
                      R=[g.hTb[kc][tt // 4], w], W=[p])
            return p

        def feat_mm(w, ci, tb):
            p = kb.ps()
            for kc in range(KC):
                kb.op("pe", lambda e, kc=kc: e.matmul(p[:, :], w[:, kc, ci * 128:(ci + 1) * 128],
                                                      g.hT[:, kc, tb * 512:(tb + 1) * 512],
                                                      start=(kc == 0), stop=(kc == KC - 1)),
                      R=[g.hTb[kc][tb], w], W=[p])
            return p

        for zi in range(2):
            w = getw(OFF["z"] + zi * 512, 512)
            for tt in range(NT):
                p = tok_mm(w, 512, tt)
                sb_ = nxt(stgb, "b")
                kb.op("act", lambda e: e.activation(out=sb_[:], in_=p[:, :], func=AF.Silu), R=[p], W=[sb_])
                kb.dma("sp", S["zs"].t[tt * 128:(tt + 1) * 128, zi * 512:(zi + 1) * 512], sb_[:], R=[sb_])
        w = getw(OFF["dt"], 16)
        for tt in range(NT):
            p = tok_mm(w, 16, tt)
            f = nxt(stgf, "f")
            kb.op("dve", lambda e: e.tensor_tensor(out=f[:, 0:16], in0=p[:, 0:16], in1=g.dtb_bc[:], op=ALU.add),
                  R=[p, g.dtb_bc], W=[f])
            kb.op("act", lambda e: e.activation(out=f[:, 0:16], in_=f[:, 0:16], func=AF.Exp), R=[f], W=[f])
            kb.op("act", lambda e: e.activation(out=g.dt_tok[:, tt, :], in_=f[:, 0:16], func=AF.Ln, bias=1.0),
                  R=[f], W=[g.dt_tok])
        w = getw(OFF["sbv"], 512)
        for tt in range(NT):
            p = tok_mm(w, 512, tt)
            sb_ = nxt(stgb, "b")
            kb.op("act", lambda e: e.activation(out=sb_[:], in_=p[:, :], func=AF.Copy), R=[p], W=[sb_])
            kb.dma("sp", S["sbv"].t[tt * 128:(tt + 1) * 128, :], sb_[:], R=[sb_])

        def roped_group(n0, N, nh, dstT, extra=None):
            w = getw(n0, N)
            fs, bs, sts = {}, {}, {}

            def stA(tt):
                p = tok_mm(w, N, tt)
                f = nxt(stgf, "f")
                kb.op("act", lambda e: e.activation(out=f[:, 0:N], in_=p[:, 0:N], func=AF.Copy), R=[p], W=[f])
                fs[tt] = f

            def stB(tt):
                f = fs.pop(tt)
                rope(g, V(f[:, 0:nh * 64].rearrange("p (h d) -> p h d", d=64), f), nh, tt, rt)
                b = nxt(stgb, "b")
                kb.op("act", lambda e: e.activation(out=b[:, 0:N], in_=f[:, 0:N], func=AF.Copy), R=[f], W=[b])
                if extra is not None:
                    extra(tt, f, b)
                bs[tt] = b

            def stC(tt):
                b = bs.pop(tt)
                pt = kb.ps()
                ptb = pt[:, :].bitcast(BF16)
                for h in range(nh):
                    kb.op("pe", lambda e, h=h: e.transpose(ptb[0:64, h * 128:(h + 1) * 128], b[:, h * 64:(h + 1) * 64], g.identb[:]),
                          R=[b, g.identb], W=[pt])
                if tt % 4 == 0:
                    sts[tt // 4] = nxt(tst, "t")
                st = sts[tt // 4]
                kb.op("act", lambda e: e.activation(
                    out=st[0:64, 0:nh, (tt % 4) * 128:(tt % 4 + 1) * 128],
                    in_=ptb[0:64, 0:nh * 128].rearrange("p (h t) -> p h t", h=nh), func=AF.Copy), R=[pt], Wd=[st])
                if tt % 4 == 3:
                    tb = tt // 4
                    kb.dma("sp", dstT.t[:, :, tb * 512:(tb + 1) * 512].rearrange("h d t -> d h t"), st[0:64, 0:nh, :], R=[st])

            for step in range(NT + 2):
                if step < NT:
                    stA(step)
                if 0 <= step - 1 < NT:
                    stB(step - 1)
                if 0 <= step - 2 < NT:
                    stC(step - 2)

        roped_group(OFF["dsq"], 512, 8, S["dqT"])

        def dsv_extra(tt, f, b):
            kb.dma("sp", S["dv"].t[tt * 128:(tt + 1) * 128, :], b[:, 128:256], R=[b])
        roped_group(OFF["dsk"], 256, 2, S["dkT"], dsv_extra)
        roped_group(OFF["ixq"], 512, 8, S["iqT"])

        def ixw_extra(tt, f, b):
            kb.op("dve", lambda e: e.tensor_copy(out=g.wix_tok[:, tt, :], in_=f[:, 64:72]), R=[f], W=[g.wix_tok])
        roped_group(OFF["ixk"], 72, 1, S["ikT"], ixw_extra)

        def feat_group(n0, dstT, c0, func):
            w = getw(n0, 512)
            for ci in range(4):
                for tb in range(4):
                    p = feat_mm(w, ci, tb)
                    sb_ = nxt(stgb, "b")
                    kb.op("act", lambda e: e.activation(out=sb_[:], in_=p[:, :], func=func), R=[p], W=[sb_])
                    kb.dma("sp", dstT.t[c0 + ci, :, tb * 512:(tb + 1) * 512], sb_[:], R=[sb_])

        feat_group(OFF["sbq"], S["sbqT"], 0, AF.Copy)
        feat_group(OFF["sbk"], S["sbkT"], 0, AF.Copy)
        for gi in range(6):
            feat_group(OFF["gate"] + gi * 512, S["gT"], gi * 4, AF.Sigmoid)

        kb.barrier()

    with ExitStack() as ph:
        wt = [kb.sb(ph, "winw", [128, KC, 512], BF16) for _ in range(2)]
        nw = [0]
        xpad = [kb.sb(ph, "xpad", [128, 3 + L], F32) for _ in range(1)]
        cva = [kb.sb(ph, "cva", [128, L], F32) for _ in range(1)]
        cvo = [kb.sb(ph, "cvo", [128, L], BF16) for _ in range(2)]
        for xp in xpad:
            kb.op("dve", lambda e, xp=xp: e.memset(xp[:, 0:3], 0.0), W=[xp])
        for gi in range(3):
            w = getw(OFF["xbc"] + gi * 512, 512)
            for ci in range(4):
                cidx = gi * 4 + ci
                xp, acc = xpad[0], cva[0]
                for tb in range(4):
                    p = feat_mm(w, ci, tb)
                    kb.op("act", lambda e: e.activation(out=xp[:, 3 + tb * 512:3 + (tb + 1) * 512], in_=p[:, :], func=AF.Copy),
                          R=[p], Wd=[xp])
                kb.op("act", lambda e: e.activation(out=acc[:], in_=xp[:, 3:3 + L], func=AF.Identity,
                                                    scale=g.cw[3][:, cidx:cidx + 1], bias=g.cb[:, cidx:cidx + 1]),
                      R=[xp, g.cw[3], g.cb], W=[acc])
                for k in (2, 1, 0):
                    kb.op("dve", lambda e, k=k: e.scalar_tensor_tensor(out=acc[:], in0=xp[:, k:k + L], scalar=g.cw[k][:, cidx:cidx + 1],
                                                                       in1=acc[:], op0=ALU.mult, op1=ALU.add),
                          R=[xp, g.cw[k], acc], W=[acc])
                if cidx < 8:
                    o = cvo[cidx % 2]
                    ov = o[:, :]
                else:
                    o = g.bcT
                    ov = g.bcT[:, cidx - 8, :]
                kb.op("act", lambda e: e.activation(out=ov, in_=acc[:], func=AF.Silu), R=[acc], W=[o])
                if cidx < 10:
                    for half in range(2):
                        pt = kb.ps()
                        ptb = pt[:, :].bitcast(BF16)
                        for j in range(8):
                            tt = half * 8 + j
                            kb.op("pe", lambda e, j=j, tt=tt: e.transpose(ptb[:, j * 128:(j + 1) * 128], ov[:, tt * 128:(tt + 1) * 128],
                                                                          g.identb[:]), R=[o, g.identb], W=[pt])
                        if cidx < 8:
                            dst, dt_ = g.xs_tok[:, half * 8:(half + 1) * 8, cidx * 128:(cidx + 1) * 128], g.xs_tok
                        else:
                            dst, dt_ = g.bt_tok[:, half * 8:(half + 1) * 8, (cidx - 8) * 128:(cidx - 7) * 128], g.bt_tok
                        kb.op("dve", lambda e: e.tensor_copy(out=dst, in_=ptb.rearrange("p (j f) -> p j f", j=8)), R=[pt], W=[dt_])
        kb.barrier()


def phase_ssd(g, l):
    kb, nc = g.kb, g.nc
    S = g.scr
    with ExitStack() as ph:
        def sb(name, shape, dt_=F32):
            return kb.sb(ph, name, shape, dt_)
        a_bc = sb("a_bc", [128, 16])
        dsk_bc = sb("dsk_bc", [128, 16])
        nw_bc = sb("nw_bc", [128, 1024])
        Dmat = sb("Dmat", [128, 16, 128], BF16)
        with nc.allow_non_contiguous_dma(reason="tiny vectors"):
            kb.dma("sp", a_bc[:], g.inp["a_log"].t[l].partition_broadcast(128), W=[a_bc])
            kb.dma("sp", dsk_bc[:], g.inp["d_skip"].t[l].partition_broadcast(128), W=[dsk_bc])
            kb.dma("sp", nw_bc[:], g.inp["ssd_norm_w"].t[l].partition_broadcast(128), W=[nw_bc])
        kb.op("act", lambda e: e.activation(out=a_bc[:], in_=a_bc[:], func=AF.Exp), R=[a_bc], W=[a_bc])
        kb.op("dve", lambda e: e.tensor_scalar(out=a_bc[:], in0=a_bc[:], scalar1=-1.0, scalar2=None, op0=ALU.mult), R=[a_bc], W=[a_bc])
        for h in range(16):
            kb.op("dve", lambda e, h=h: e.tensor_scalar(out=Dmat[:, h, :], in0=g.ident[:], scalar1=dsk_bc[:, h:h + 1], scalar2=None,
                                                        op0=ALU.mult), R=[g.ident, dsk_bc], Wd=[Dmat])
        NB = 2
        X = [sb("X", [128, 2, 16, 128], BF16)] * 2
        dd = [sb("dd", [128, 16, 128])] * 2
        Mt = [sb("Mt", [128, 16, 128], BF16) for _ in range(NB)]
        xw = [sb("xw", [128, 1024], BF16) for _ in range(NB)]
        zst = [sb("zst", [128, 1024], BF16) for _ in range(NB)]
        yy = [sb("yy", [128, 1024])] * 2
        yz = yy
        ss = [sb("ss", [128, 1]) for _ in range(NB)]
        yn = [sb("yn", [128, 1024], BF16) for _ in range(NB)]
        yst = [sb("yst", [128, KC, 256], BF16)] * 2
        hst = [sb("hst", [128, 512]) for _ in range(2)]
        htmp = [sb("htmp", [128, 512]) for _ in range(2)]
        prevb = [[sb("prevb", [128, 512], BF16) for _ in range(2)] for _ in range(2)]
        ytmp = [sb("ytmp", [128, 512]) for _ in range(2)]

        dA_all = sb("dA_all", [128, 256])
        dAs_all = sb("dAs_all", [128, 2, 256], BF16)
        ac_all = sb("ac_all", [128, 256])
        eac_all = sb("eac_all", [128, 256])
        cd_all = sb("cd_all", [128, 256])
        wgt_all = sb("wgt_all", [128, 256])
        v_all = sb("v_all", [128, 256])
        vs_all = sb("vs_all", [128, 2, 256], BF16)
        acs_all = sb("acs_all", [128, 2, 256], BF16)
        dtf = g.dt_tok[:, :, :].rearrange("p c h -> p (c h)")
        kb.op("dve", lambda e: e.tensor_tensor(out=dA_all[:, :].rearrange("p (c h) -> p c h", h=16), in0=g.dt_tok[:, :, :],
                                               in1=a_bc[:, :].unsqueeze(1).broadcast_to([128, NT, 16]), op=ALU.mult),
              R=[g.dt_tok, a_bc], W=[dA_all])
        kb.op("dve", lambda e: e.tensor_copy(out=dAs_all[:, 0, :], in_=dA_all[:]), R=[dA_all], W=[dAs_all])
        kb.op("dve", lambda e: e.tensor_tensor(out=dAs_all[:, 1, :], in0=dA_all[:], in1=dAs_all[:, 0, :], op=ALU.subtract),
              R=[dA_all, dAs_all], W=[dAs_all])
        pac = kb.ps(pin=True)
        for c in range(NT):
            for half, lhs in ((0, g.triub), (1, g.onesb)):
                for hl_ in range(2):
                    kb.op("pe", lambda e, c=c, half=half, lhs=lhs, hl_=hl_: e.matmul(
                        pac[:, half * 256 + c * 16:half * 256 + (c + 1) * 16], lhs[:], dAs_all[:, hl_, c * 16:(c + 1) * 16],
                        start=(hl_ == 0), stop=(hl_ == 1)), R=[lhs, dAs_all], W=[pac])
        kb.unpin(pac)
        kb.op("dve", lambda e: e.tensor_copy(out=ac_all[:], in_=pac[:, 0:256]), R=[pac], W=[ac_all])
        kb.op("act", lambda e: e.activation(out=eac_all[:], in_=pac[:, 0:256], func=AF.Exp), R=[pac], W=[eac_all])
        kb.op("act", lambda e: e.activation(out=cd_all[:], in_=pac[:, 256:512], func=AF.Exp), R=[pac], W=[cd_all])
        kb.op("dve", lambda e: e.tensor_tensor(out=wgt_all[:], in0=pac[:, 256:512], in1=ac_all[:], op=ALU.subtract), R=[pac, ac_all], W=[wgt_all])
        kb.op("act", lambda e: e.activation(out=wgt_all[:], in_=wgt_all[:], func=AF.Exp), R=[wgt_all], W=[wgt_all])
        kb.op("dve", lambda e: e.tensor_tensor(out=wgt_all[:], in0=wgt_all[:], in1=dtf, op=ALU.mult), R=[wgt_all, g.dt_tok], W=[wgt_all])
        kb.op("act", lambda e: e.activation(out=v_all[:], in_=dtf, func=AF.Ln), R=[g.dt_tok], W=[v_all])
        kb.op("dve", lambda e: e.tensor_tensor(out=v_all[:], in0=v_all[:], in1=ac_all[:], op=ALU.subtract), R=[v_all, ac_all], W=[v_all])
        kb.op("dve", lambda e: e.tensor_copy(out=vs_all[:, 0, :], in_=v_all[:]), R=[v_all], W=[vs_all])
        kb.op("dve", lambda e: e.tensor_tensor(out=vs_all[:, 1, :], in0=v_all[:], in1=vs_all[:, 0, :], op=ALU.subtract), R=[v_all, vs_all], W=[vs_all])
        kb.op("dve", lambda e: e.tensor_copy(out=acs_all[:, 0, :], in_=ac_all[:]), R=[ac_all], W=[acs_all])
        kb.op("dve", lambda e: e.tensor_tensor(out=acs_all[:, 1, :], in0=ac_all[:], in1=acs_all[:, 0, :], op=ALU.subtract), R=[ac_all, acs_all], W=[acs_all])

        pcbs = {}

        def P1(c):
            k = c % NB
            tsl = slice(c * 128, (c + 1) * 128)
            c16 = slice(c * 16, (c + 1) * 16)
            kb.op("dve", lambda e: e.tensor_tensor(
                out=xw[k][:, :].rearrange("p (h d) -> p h d", d=64), in0=g.xs_tok[:, c, :].rearrange("p (h d) -> p h d", d=64),
                in1=wgt_all[:, c16].unsqueeze(2).broadcast_to([128, 16, 64]), op=ALU.mult), R=[g.xs_tok, wgt_all], W=[xw[k]])
            for hl_ in range(2):
                kb.op("dve", lambda e, hl_=hl_: e.tensor_tensor(
                    out=X[k][:, hl_, :, :], in0=g.identb[:, :].unsqueeze(1).broadcast_to([128, 16, 128]),
                    in1=acs_all[:, hl_, c16].unsqueeze(2).broadcast_to([128, 16, 128]), op=ALU.mult), R=[g.identb, acs_all], Wd=[X[k]])
            for q4 in range(4):
                pb = kb.ps()
                h4 = slice(c * 16 + q4 * 4, c * 16 + q4 * 4 + 4)
                for hl_ in range(2):
                    kb.op("pe", lambda e, hl_=hl_: e.matmul(pb[:, :], g.onesb[:], X[k][:, hl_, q4 * 4:(q4 + 1) * 4, :],
                                                            start=(hl_ == 0), stop=False), R=[g.onesb, X[k]], W=[pb])
                for hl_ in range(2):
                    kb.op("pe", lambda e, hl_=hl_: e.matmul(pb[:, :], g.identb[:], vs_all[:, hl_, h4].unsqueeze(2).broadcast_to([128, 4, 128]),
                                                            start=False, stop=False), R=[g.identb, vs_all], W=[pb])
                kb.op("pe", lambda e: e.matmul(pb[:, :], g.identb[:], g.negmb[:, :].unsqueeze(1).broadcast_to([128, 4, 128]),
                                               start=False, stop=True), R=[g.identb, g.negmb], W=[pb])
                kb.op("act", lambda e, q4=q4, pb=pb: e.activation(out=dd[k][:, q4 * 4:(q4 + 1) * 4, :],
                                                                  in_=pb[:, :].rearrange("p (h i) -> p h i", h=4), func=AF.Exp),
                      R=[pb], Wd=[dd[k]])
            pcb = kb.ps()
            for gg in range(2):
                kb.op("pe", lambda e, gg=gg: e.matmul(pcb[:, gg * 128:(gg + 1) * 128], g.bcT[:, gg, tsl], g.bcT[:, 2 + gg, tsl],
                                                      start=True, stop=True), R=[g.bcT], W=[pcb])
            pcbs[c] = pcb

        def P1b(c):
            k = c % NB
            pcb = pcbs.pop(c)
            for gg in range(2):
                kb.op("dve", lambda e, gg=gg: e.tensor_tensor(
                    out=Mt[k][:, gg * 8:(gg + 1) * 8, :], in0=dd[k][:, gg * 8:(gg + 1) * 8, :],
                    in1=pcb[:, gg * 128:(gg + 1) * 128].unsqueeze(1).broadcast_to([128, 8, 128]), op=ALU.mult),
                    R=[dd[k], pcb], Wd=[Mt[k]])

        def P2(c):
            k = c % NB
            tsl = slice(c * 128, (c + 1) * 128)
            kb.dma("sp", zst[k][:], S["zs"].t[tsl, :], W=[zst[k]])
            for gg in range(2):
                pA = kb.ps()
                for hl in range(8):
                    h = gg * 8 + hl
                    xsl = g.xs_tok[:, c, h * 64:(h + 1) * 64]
                    kb.op("pe", lambda e, h=h, hl=hl, xsl=xsl: e.matmul(pA[:, hl * 64:(hl + 1) * 64], Mt[k][:, h, :], xsl, start=True, stop=False),
                          R=[Mt[k], g.xs_tok], W=[pA])
                    kb.op("pe", lambda e, h=h, hl=hl, xsl=xsl: e.matmul(pA[:, hl * 64:(hl + 1) * 64], Dmat[:, h, :], xsl, start=False, stop=True),
                          R=[Dmat, g.xs_tok], W=[pA])
                ysl = yy[k][:, gg * 512:(gg + 1) * 512]
                if c > 0:
                    pB = kb.ps()
                    pv = prevb[gg][(c - 1) % 2]
                    kb.op("pe", lambda e: e.matmul(pB[:, :], g.bcT[:, 2 + gg, tsl], pv[:], start=True, stop=True), R=[g.bcT, pv], W=[pB])
                    yt = ytmp[gg]
                    kb.op("dve", lambda e: e.tensor_tensor(
                        out=yt[:, :].rearrange("p (h d) -> p h d", d=64), in0=pB[:, :].rearrange("p (h d) -> p h d", d=64),
                        in1=eac_all[:, c * 16 + gg * 8:c * 16 + (gg + 1) * 8].unsqueeze(2).broadcast_to([128, 8, 64]), op=ALU.mult),
                        R=[pB, eac_all], W=[yt])
                    kb.op("dve", lambda e: e.tensor_tensor(out=ysl, in0=yt[:], in1=pA[:, :], op=ALU.add), R=[yt, pA], W=[yy[k]])
                else:
                    kb.op("act", lambda e: e.activation(out=ysl, in_=pA[:, :], func=AF.Copy), R=[pA], W=[yy[k]])
                if c < NT - 1:
                    pS = kb.ps()
                    kb.op("pe", lambda e: e.matmul(pS[:, :], g.bt_tok[:, c, gg * 128:(gg + 1) * 128], xw[k][:, gg * 512:(gg + 1) * 512],
                                                   start=True, stop=True), R=[g.bt_tok, xw[k]], W=[pS])
                    if c == 0:
                        kb.op("dve", lambda e: e.tensor_copy(out=hst[gg][:], in_=pS[:, :]), R=[pS], W=[hst[gg]])
                    else:
                        kb.op("dve", lambda e: e.tensor_tensor(
                            out=htmp[gg][:, :].rearrange("p (h d) -> p h d", d=64), in0=hst[gg][:, :].rearrange("p (h d) -> p h d", d=64),
                            in1=cd_all[:, c * 16 + gg * 8:c * 16 + (gg + 1) * 8].unsqueeze(2).broadcast_to([128, 8, 64]), op=ALU.mult),
                            R=[hst[gg], cd_all], W=[htmp[gg]])
                        kb.op("dve", lambda e: e.tensor_tensor(out=hst[gg][:], in0=htmp[gg][:], in1=pS[:, :], op=ALU.add),
                              R=[htmp[gg], pS], W=[hst[gg]])
                    pn = prevb[gg][c % 2]
                    kb.op("act", lambda e: e.activation(out=pn[:], in_=hst[gg][:], func=AF.Copy), R=[hst[gg]], W=[pn])

        def P3(c):
            k = c % NB
            tsl = slice(c * 128, (c + 1) * 128)
            kb.op("dve", lambda e: e.tensor_tensor(out=yz[k][:], in0=yy[k][:], in1=zst[k][:], op=ALU.mult), R=[yy[k], zst[k]], W=[yz[k]])
            kb.op("act", lambda e: e.activation(out=yn[k][:], in_=yz[k][:], func=AF.Square, accum_out=ss[k][:]), R=[yz[k]], W=[yn[k], ss[k]])
            kb.op("dve", lambda e: e.tensor_scalar(out=ss[k][:], in0=ss[k][:], scalar1=1.0 / 1024, scalar2=EPS, op0=ALU.mult, op1=ALU.add),
                  R=[ss[k]], W=[ss[k]])
            kb.op("act", lambda e: e.activation(out=ss[k][:], in_=ss[k][:], func=AF.Ln), R=[ss[k]], W=[ss[k]])
            kb.op("act", lambda e: e.activation(out=ss[k][:], in_=ss[k][:], func=AF.Exp, scale=-0.5), R=[ss[k]], W=[ss[k]])
            kb.op("dve", lambda e: e.scalar_tensor_tensor(out=yn[k][:], in0=yz[k][:], scalar=ss[k][:, 0:1], in1=nw_bc[:],
                                                          op0=ALU.mult, op1=ALU.mult), R=[yz[k], ss[k], nw_bc], W=[yn[k]])
            pt = kb.ps()
            ptb = pt[:, :].bitcast(BF16)
            for kc in range(KC):
                kb.op("pe", lambda e, kc=kc: e.transpose(ptb[:, kc * 128:(kc + 1) * 128], yn[k][:, kc * 128:(kc + 1) * 128], g.identb[:]),
                      R=[yn[k], g.identb], W=[pt])
            st = yst[(c // 2) % 2]
            kb.op("act", lambda e: e.activation(out=st[:, :, (c % 2) * 128:(c % 2 + 1) * 128],
                                                in_=ptb.rearrange("p (kc t) -> p kc t", kc=KC), func=AF.Copy), R=[pt], Wd=[st])
            if c % 2 == 1:
                t2 = c // 2
                kb.dma("sp", S["yT"].t[0:8, :, t2 * 256:(t2 + 1) * 256].rearrange("kc p t -> p kc t"), st[:], R=[st])

        for step in range(-1, NT + 1):
            if 0 <= step + 1 < NT:
                P1(step + 1)
            if 0 <= step - 1 < NT:
                P3(step - 1)
            if 0 <= step + 1 < NT:
                P1b(step + 1)
            if 0 <= step < NT:
                P2(step)
        kb.barrier()


def phase_sb(g, l):
    kb, nc = g.kb, g.nc
    S = g.scr
    with ExitStack() as ph:
        def sb(name, shape, dt_=F32):
            return kb.sb(ph, name, shape, dt_)
        qT = sb("sbq", [128, 4, L], BF16)
        kT = sb("sbk", [128, 4, L], BF16)
        v = sb("sbv", [128, NT, 512], BF16)
        onesw = sb("onesw", [128, L], BF16)
        kb.op("dve", lambda e: e.memset(onesw[:], 1.0), W=[onesw])
        kb.dma("sp", qT[:], S["sbqT"].t.rearrange("c p t -> p c t"), W=[qT])
        kb.dma("sp", kT[:], S["sbkT"].t.rearrange("c p t -> p c t"), W=[kT])
        kb.dma("sp", v[:], S["sbv"].t.rearrange("(t p) f -> p t f", p=128), W=[v])
        e1b = [sb("e1", [128, L]) for _ in range(3)]
        spb = [sb("sp", [128, L]) for _ in range(2)]
        Fb = [sb("F", [128, L + 1]) for _ in range(2)]
        attb = [sb("att", [128, L], BF16) for _ in range(2)]
        attT = [sb("attT", [128, NT, 128], BF16) for _ in range(2)]
        ftn = [sb("ftn", [128, 1]) for _ in range(2)]
        yst = [sb("yst", [128, 128], BF16) for _ in range(4)]
        for F in Fb:
            kb.op("dve", lambda e, F=F: e.memset(F[:, 0:1], 0.0), W=[F])
        iters = [(qt, h) for qt in range(NT) for h in range(8)]

        def s1(i):
            qt, h = iters[i]
            W = 128 * (qt + 1)
            nch = (W + 511) // 512
            dsl = slice(qt * 128, (qt + 1) * 128)
            hp, hc = (h % 2) * 64, h // 2
            e1, sp_ = e1b[i % 3], spb[i % 2]
            for ch in range(nch):
                n = min(512, W - ch * 512)
                p = kb.ps()
                kb.op("pe", lambda e, p=p, ch=ch, n=n: e.matmul(p[:, 0:n], qT[hp:hp + 64, hc, dsl], kT[hp:hp + 64, hc, ch * 512:ch * 512 + n],
                                                                start=True, stop=True), R=[qT, kT], W=[p])
                kb.op("act", lambda e, p=p, ch=ch, n=n: e.activation(out=e1[:, ch * 512:ch * 512 + n], in_=p[:, 0:n], func=AF.Exp, scale=0.125),
                      R=[p], Wd=[e1])
            kb.op("act", lambda e: e.activation(out=sp_[:, 0:W], in_=e1[:, 0:W], func=AF.Ln, bias=1.0), R=[e1], W=[sp_])

        def s2(i):
            qt, h = iters[i]
            k = i % 2
            W = 128 * (qt + 1)
            dsl = slice(qt * 128, (qt + 1) * 128)
            sp_, F = spb[k], Fb[k]
            kb.op("dve", lambda e: e.tensor_tensor(out=sp_[:, dsl], in0=sp_[:, dsl], in1=g.strl[:], op=ALU.mult), R=[sp_, g.strl], W=[sp_])
            kb.op("dve", lambda e: e.memset(F[:, 0:1], 0.0), W=[F])
            kb.op("dve", lambda e: e.tensor_tensor_scan(out=F[:, 1:W + 1], data0=onesw[:, 0:W], data1=sp_[:, 0:W], initial=0.0,
                                                        op0=ALU.mult, op1=ALU.add), R=[onesw, sp_], W=[F])
            kb.op("dve", lambda e: e.tensor_scalar(out=ftn[k][:], in0=F[:, W:W + 1], scalar1=-1.0, scalar2=None, op0=ALU.mult),
                  R=[F], W=[ftn[k]])

        def s3a(i):
            qt, h = iters[i]
            k = i % 2
            W = 128 * (qt + 1)
            F = Fb[k]
            kb.op("act", lambda e: e.activation(out=F[:, 0:W], in_=F[:, 0:W], func=AF.Exp, bias=ftn[k][:, 0:1]), R=[F, ftn[k]], W=[F])

        def s3b(i):
            qt, h = iters[i]
            k = i % 2
            W = 128 * (qt + 1)
            dsl = slice(qt * 128, (qt + 1) * 128)
            e1, F, att = e1b[i % 3], Fb[k], attb[k]
            kb.op("dve", lambda e: e.tensor_tensor(out=att[:, 0:W], in0=e1[:, 0:W], in1=F[:, 0:W], op=ALU.mult), R=[e1, F], W=[att])
            kb.op("dve", lambda e: e.tensor_tensor(out=att[:, dsl], in0=att[:, dsl], in1=g.strlb[:], op=ALU.mult), R=[att, g.strlb], W=[att])

        def s3c(i):
            qt, h = iters[i]
            k = i % 2
            att, aT = attb[k], attT[k]
            for b0 in range(0, qt + 1, 8):
                nb = min(8, qt + 1 - b0)
                pt = kb.ps()
                ptb = pt[:, :].bitcast(BF16)
                for j in range(nb):
                    kb.op("pe", lambda e, j=j, b0=b0: e.transpose(ptb[:, j * 128:(j + 1) * 128], att[:, (b0 + j) * 128:(b0 + j + 1) * 128], g.identb[:]),
                          R=[att, g.identb], W=[pt])
                kb.op("act", lambda e, b0=b0, nb=nb: e.activation(out=aT[:, b0:b0 + nb, :], in_=ptb[:, 0:nb * 128].rearrange("p (j t) -> p j t", j=nb),
                                                                 func=AF.Copy), R=[pt], Wd=[aT])

        def s3d(i):
            qt, h = iters[i]
            k = i % 2
            dsl = slice(qt * 128, (qt + 1) * 128)
            hp, hc = (h % 2) * 64, h // 2
            aT = attT[k]
            py = kb.ps()
            for sbk in range(qt + 1):
                kb.op("pe", lambda e, sbk=sbk: e.matmul(py[0:64, 0:128], v[:, sbk, h * 64:(h + 1) * 64], aT[:, sbk, :],
                                                        start=(sbk == 0), stop=(sbk == qt)), R=[v, aT], W=[py])
            ys = yst[i % 4]
            kb.op("dve", lambda e: e.tensor_copy(out=ys[hp:hp + 64, :], in_=py[0:64, 0:128]), R=[py], W=[ys])
            kb.dma("sp", S["yT"].t[8 + hc, hp:hp + 64, dsl], ys[hp:hp + 64, :], R=[ys])

        n_it = len(iters)

        def run(fn, i):
            if 0 <= i < n_it:
                fn(i)
        for step in range(n_it + 4):
            run(s3d, step - 4)
            run(s3a, step - 2)
            run(s1, step)
            run(s3c, step - 3)
            run(s2, step - 1)
            run(s3b, step - 2)
        kb.barrier()


NBIS = 16


def phase_dsa(g, l):
    kb, nc = g.kb, g.nc
    S = g.scr
    with ExitStack() as ph:
        def sb(name, shape, dt_=F32):
            return kb.sb(ph, name, shape, dt_)
        dk = sb("dk", [64, 2, L], BF16)
        ik = sb("ik", [64, L], BF16)
        vaug = sb("vaug", [128, NT, 2, 128], BF16)
        kb.dma("sp", dk[:], S["dkT"].t.rearrange("h d t -> d h t"), W=[dk])
        kb.dma("sp", ik[:], S["ikT"].t[0], W=[ik])
        kb.op("dve", lambda e: e.memset(vaug[:], 1.0), W=[vaug])
        for gg in range(2):
            kb.dma("sp", vaug[:, :, gg, 0:64], S["dv"].t[:, gg * 64:(gg + 1) * 64].rearrange("(t p) d -> p t d", p=128), W=[vaug])
        dqt = [sb("dqt", [64, 8, 128], BF16) for _ in range(2)]
        iqt = [sb("iqt", [64, 8, 128], BF16) for _ in range(2)]
        score = [sb("score", [128, L]) for _ in range(2)]
        rl = [sb("rl", [128, 512], BF16) for _ in range(4)]
        wabs = [sb("wabs", [128, 8]) for _ in range(2)]
        wsgn = [sb("wsgn", [128, 8]) for _ in range(2)]
        Dg = [sb("Dg", [128, 8, 128], BF16) for _ in range(2)]
        junk = sb("junk", [128, L], BF16)
        maskb = [sb("maskb", [128, L], BF16) for _ in range(2)]
        maskT = [sb("maskT", [128, NT, 128], BF16) for _ in range(2)]
        Eb = [sb("E", [128, 512], BF16) for _ in range(3)]
        M_ = [sb("M", [128, 1]) for _ in range(2)]
        A_ = [sb("A", [128, NBIS + 1]) for _ in range(2)]
        mid = [sb("mid", [128, 1]) for _ in range(2)]
        cnt = [sb("cnt", [128, 1]) for _ in range(2)]
        sela = [sb("sela", [128, 1]) for _ in range(2)]
        R0 = [sb("R0", [64, 512]) for _ in range(2)]
        ytmp = [sb("ytmp", [64, 512], BF16) for _ in range(2)]
        yraw = [sb("yraw", [64, 512]) for _ in range(2)]
        yst = [sb("dyst", [128, 4, 128], BF16) for _ in range(2)]
        cn = dict(rl=0, E=0)

        def stage_I(qt):
            k = qt % 2
            W = 128 * (qt + 1)
            dsl = slice(qt * 128, (qt + 1) * 128)
            if qt < 2:
                return
            kb.dma("sp", iqt[k][:], S["iqT"].t[:, :, dsl].rearrange("h d t -> d h t"), W=[iqt[k]])
            sc = score[k]
            nch = (W + 511) // 512
            wv = g.wix_tok[:, qt, :]
            kb.op("act", lambda e: e.activation(out=wabs[k][:], in_=wv, func=AF.Abs), R=[g.wix_tok], W=[wabs[k]])
            kb.op("dve", lambda e: e.tensor_scalar(out=wsgn[k][:], in0=wv, scalar1=0.0, scalar2=2.0, op0=ALU.is_ge, op1=ALU.mult),
                  R=[g.wix_tok], W=[wsgn[k]])
            kb.op("dve", lambda e: e.tensor_scalar(out=wsgn[k][:], in0=wsgn[k][:], scalar1=-1.0, scalar2=None, op0=ALU.add),
                  R=[wsgn[k]], W=[wsgn[k]])
            for h in range(8):
                kb.op("dve", lambda e, h=h: e.tensor_scalar(out=Dg[k][:, h, :], in0=g.identb[:], scalar1=wsgn[k][:, h:h + 1], scalar2=None,
                                                            op0=ALU.mult), R=[g.identb, wsgn[k]], Wd=[Dg[k]])
            items = [(ch, h) for ch in range(nch) for h in range(8)]
            pend = {}
            pscs = {}

            def mm(j):
                ch, h = items[j]
                n = min(512, W - ch * 512)
                p = kb.ps()
                kb.op("pe", lambda e: e.matmul(p[:, 0:n], iqt[k][0:64, h, :], ik[0:64, ch * 512:ch * 512 + n], start=True, stop=True),
                      R=[iqt[k], ik], W=[p])
                r = rl[cn["rl"] % 4]
                cn["rl"] += 1
                kb.op("act", lambda e: e.activation(out=r[:, 0:n], in_=p[:, 0:n], func=AF.Relu, scale=wabs[k][:, h:h + 1]),
                      R=[p, wabs[k]], W=[r])
                pend[j] = r

            def acc(j):
                ch, h = items[j]
                n = min(512, W - ch * 512)
                r = pend.pop(j)
                if h == 0:
                    pscs[ch] = kb.ps(pin=True)
                psc = pscs[ch]
                kb.op("pe", lambda e: e.matmul(psc[:, 0:n], Dg[k][:, h, :], r[:, 0:n], start=(h == 0), stop=(h == 7)),
                      R=[Dg[k], r], W=[psc])
                if h == 7:
                    kb.unpin(psc)
                    kb.op("act", lambda e: e.activation(out=sc[:, ch * 512:ch * 512 + n], in_=psc[:, 0:n], func=AF.Copy), R=[psc], Wd=[sc])

            for j in range(len(items) + 2):
                if j < len(items):
                    mm(j)
                if 0 <= j - 2 < len(items):
                    acc(j - 2)

        def stage_B(qt):
            k = qt % 2
            W = 128 * (qt + 1)
            dsl = slice(qt * 128, (qt + 1) * 128)
            mT = maskT[k]
            if qt < 2:
                if qt == 1:
                    kb.op("dve", lambda e: e.memset(mT[:, 0, :], 0.0), W=[mT])
                kb.op("dve", lambda e: e.tensor_copy(out=mT[:, qt, :], in_=g.negmb[:]), R=[g.negmb], W=[mT])
                return
            sc = score[k]
            kb.op("dve", lambda e: e.tensor_reduce(out=M_[k][:], in_=sc[:, 0:W], axis=AX.X, op=ALU.max, apply_absolute_value=True),
                  R=[sc], W=[M_[k]])
            kb.op("dve", lambda e: e.tensor_scalar(out=A_[k][:, 0:NBIS], in0=g.pow2[:, 0:NBIS], scalar1=M_[k][:, 0:1], scalar2=None, op0=ALU.mult),
                  R=[g.pow2, M_[k]], W=[A_[k]])
            kb.op("dve", lambda e: e.tensor_copy(out=A_[k][:, NBIS:NBIS + 1], in_=A_[k][:, NBIS - 1:NBIS]), R=[A_[k]], W=[A_[k]])
            kb.op("dve", lambda e: e.tensor_tensor(out=sc[:, dsl], in0=sc[:, dsl], in1=g.negup[:], op=ALU.add), R=[sc, g.negup], W=[sc])
            kb.op("dve", lambda e: e.memset(mid[k][:], 0.0), W=[mid[k]])
            for it in range(NBIS):
                kb.op("dve", lambda e: e.tensor_scalar(out=junk[:, 0:W], in0=sc[:, 0:W], scalar1=mid[k][:, 0:1], scalar2=0.0,
                                                       op0=ALU.is_ge, op1=ALU.add, accum_out=cnt[k][:]),
                      R=[sc, mid[k]], W=[junk, cnt[k]])
                kb.op("dve", lambda e, it=it: e.tensor_scalar(out=sela[k][:], in0=cnt[k][:], scalar1=255.5, scalar2=A_[k][:, it:it + 1],
                                                              op0=ALU.is_ge, op1=ALU.mult), R=[cnt[k], A_[k]], W=[sela[k]])
                kb.op("dve", lambda e, it=it: e.scalar_tensor_tensor(out=mid[k][:], in0=sela[k][:], scalar=A_[k][:, it + 1:it + 2], in1=mid[k][:],
                                                                     op0=ALU.subtract, op1=ALU.add), R=[sela[k], A_[k], mid[k]], W=[mid[k]])
            mb = maskb[k]
            kb.op("dve", lambda e: e.tensor_scalar(out=mb[:, 0:W], in0=sc[:, 0:W], scalar1=mid[k][:, 0:1], scalar2=-30000.0,
                                                   op0=ALU.is_lt, op1=ALU.mult), R=[sc, mid[k]], W=[mb])

        def stage_T(qt):
            if qt < 2:
                return
            k = qt % 2
            mb, mT = maskb[k], maskT[k]
            for b0 in range(0, qt + 1, 8):
                nb = min(8, qt + 1 - b0)
                pt = kb.ps()
                ptb = pt[:, :].bitcast(BF16)
                for j in range(nb):
                    kb.op("pe", lambda e, j=j, b0=b0: e.transpose(ptb[:, j * 128:(j + 1) * 128], mb[:, (b0 + j) * 128:(b0 + j + 1) * 128], g.identb[:]),
                          R=[mb, g.identb], W=[pt])
                kb.op("act", lambda e, b0=b0, nb=nb: e.activation(out=mT[:, b0:b0 + nb, :], in_=ptb[:, 0:nb * 128].rearrange("p (j t) -> p j t", j=nb),
                                                                 func=AF.Copy), R=[pt], Wd=[mT])

        def stage_A(qt):
            k = qt % 2
            dsl = slice(qt * 128, (qt + 1) * 128)
            mT = maskT[k]
            ys = yst[k]
            if qt + 1 < NT:
                kb.dma("sp", dqt[1 - k][:], S["dqT"].t[:, :, (qt + 1) * 128:(qt + 2) * 128].rearrange("h d t -> d h t"), W=[dqt[1 - k]])
            for gq in range(2):
                pO = kb.ps(pin=True)
                Es = {}

                def qk(sbk):
                    pS = kb.ps()
                    kb.op("pe", lambda e: e.matmul(pS[:, :], dk[0:64, gq, sbk * 128:(sbk + 1) * 128], dqt[k][0:64, 4 * gq:4 * gq + 4, :],
                                                   start=True, stop=False), R=[dk, dqt[k]], W=[pS])
                    kb.op("pe", lambda e: e.matmul(pS[:, :], g.identb[:], mT[:, sbk, :].unsqueeze(1).broadcast_to([128, 4, 128]),
                                                   start=False, stop=True), R=[g.identb, mT], W=[pS])
                    E = Eb[cn["E"] % 3]
                    cn["E"] += 1
                    kb.op("act", lambda e: e.activation(out=E[:], in_=pS[:, :], func=AF.Exp, scale=0.125), R=[pS], W=[E])
                    Es[sbk] = E

                def av(sbk):
                    E = Es.pop(sbk)
                    kb.op("pe", lambda e: e.matmul(pO[:, :], vaug[:, sbk, gq, :], E[:], start=(sbk == 0), stop=(sbk == qt)),
                          R=[vaug, E], W=[pO])

                for sbk in range(qt + 2):
                    if sbk <= qt:
                        qk(sbk)
                    if sbk >= 1:
                        av(sbk - 1)
                kb.unpin(pO)
                r0, yt = R0[gq], ytmp[gq]
                kb.op("act", lambda e: e.activation(out=r0[:], in_=pO[64:128, :], func=AF.Ln), R=[pO], W=[r0])
                kb.op("act", lambda e: e.activation(out=r0[:], in_=r0[:], func=AF.Exp, scale=-1.0), R=[r0], W=[r0])
                yr = yraw[gq]
                kb.op("act", lambda e: e.activation(out=yr[:], in_=pO[0:64, :], func=AF.Copy), R=[pO], W=[yr])
                kb.op("pool", lambda e: e.tensor_tensor(out=yt[:], in0=yr[:], in1=r0[:], op=ALU.mult), R=[yr, r0], W=[yt])
                for hh in range(4):
                    h = 4 * gq + hh
                    hp, hc = (h % 2) * 64, h // 2
                    kb.op("act", lambda e, hh=hh, hp=hp, hc=hc: e.activation(out=ys[hp:hp + 64, hc, :], in_=yt[:, hh * 128:(hh + 1) * 128], func=AF.Copy),
                          R=[yt], Wd=[ys])
            kb.dma("sp", S["yT"].t[12:16, :, dsl].rearrange("c p t -> p c t"), ys[:], R=[ys])

        kb.dma("sp", dqt[0][:], S["dqT"].t[:, :, 0:128].rearrange("h d t -> d h t"), W=[dqt[0]])
        stage_I(0)
        stage_B(0)
        stage_T(0)
        stage_I(1)
        modw = [sb("adaw", [128, KC, 512], BF16) for _ in range(2)] if g.dsa_hook is not None else None
        for qt in range(NT):
            if qt + 2 < NT:
                stage_I(qt + 2)
            stage_A(qt)
            if modw is not None and 1 <= qt < 13:
                mod_piece_load(g, g.dsa_hook, qt - 1, modw[(qt - 1) % 2])
            if modw is not None and 2 <= qt < 14:
                mod_piece_mm(g, g.dsa_hook, qt - 2, modw[(qt - 2) % 2])
            if qt + 1 < NT:
                stage_B(qt + 1)
                stage_T(qt + 1)
        kb.barrier()


def phase_merge(g, l):
    kb, nc = g.kb, g.nc
    S = g.scr
    with ExitStack() as ph:
        def sb(name, shape, dt_=F32):
            return kb.sb(ph, name, shape, dt_)
        yT = sb("yTall", [128, 16, L], BF16)
        yTb = [kb.buf() for _ in range(4)]
        for q in range(4):
            kb.dma("sp", yT[:, q * 4:(q + 1) * 4, :], S["yT"].t[q * 4:(q + 1) * 4].rearrange("c p t -> p c t"), W=[yTb[q]])
        mT = sb("mergedT", [128, KC, L], BF16)
        mTb = [[kb.buf() for tb in range(4)] for c in range(KC)]
        wbr = [sb("wbr", [128, 16, 256], BF16) for _ in range(2)]
        gt = [sb("gt", [128, 3, 512], BF16) for _ in range(2)]
        gbr = [sb("gbr", [128, 3, 512], BF16) for _ in range(2)]
        it = 0
        for c2 in range(4):
            w = wbr[c2 % 2]
            csl = slice(c2 * 256, (c2 + 1) * 256)
            kb.dma("pool", w[:, 0:8, :], g.inp["w_br_ssd"].t[l][:, csl].rearrange("(kc p) n -> p kc n", p=128), W=[w])
            kb.dma("pool", w[:, 8:12, :], g.inp["w_br_sb"].t[l][:, csl].rearrange("(kc p) n -> p kc n", p=128), W=[w])
            kb.dma("pool", w[:, 12:16, :], g.inp["w_br_dsa"].t[l][:, csl].rearrange("(kc p) n -> p kc n", p=128), W=[w])
            for cl in range(2):
                c = c2 * 2 + cl
                for tb in range(4):
                    k = it % 2
                    it += 1
                    tsl = slice(tb * 512, (tb + 1) * 512)
                    for br in range(3):
                        kb.dma("sp", gt[k][:, br, :], S["gT"].t[br * 8 + c, :, tsl], W=[gt[k]])
                    pbr = []
                    for br, (k0, k1) in enumerate(((0, 8), (8, 12), (12, 16))):
                        p = kb.ps()
                        for kc in range(k0, k1):
                            kb.op("pe", lambda e, p=p, kc=kc, k0=k0, k1=k1: e.matmul(
                                p[:, :], w[:, kc, cl * 128:(cl + 1) * 128], yT[:, kc, tsl], start=(kc == k0), stop=(kc == k1 - 1)),
                                R=[w, yTb[kc // 4]], W=[p])
                        pbr.append(p)
                    gb = gbr[k]
                    for br in range(3):
                        kb.op("dve", lambda e, br=br: e.tensor_tensor(out=gb[:, br, :], in0=pbr[br][:, :], in1=gt[k][:, br, :], op=ALU.mult),
                              R=[pbr[br], gt[k]], Wd=[gb])
                    pm = kb.ps()
                    for br in range(3):
                        kb.op("pe", lambda e, br=br: e.matmul(pm[:, :], g.identb[:], gb[:, br, :], start=(br == 0), stop=(br == 2)),
                              R=[g.identb, gb], W=[pm])
                    kb.op("act", lambda e: e.activation(out=mT[:, c, tsl], in_=pm[:, :], func=AF.Copy), R=[pm], W=[mTb[c][tb]])
        wo = [sb("wo", [128, KC, 256], BF16) for _ in range(2)]
        for c2 in range(4):
            w = wo[c2 % 2]
            kb.dma("pool", w[:], g.inp["w_out"].t[l][:, c2 * 256:(c2 + 1) * 256].rearrange("(kc p) n -> p kc n", p=128), W=[w])
            for cl in range(2):
                c = c2 * 2 + cl
                for tb in range(4):
                    tsl = slice(tb * 512, (tb + 1) * 512)
                    p = kb.ps()
                    for kc in range(KC):
                        kb.op("pe", lambda e, p=p, kc=kc: e.matmul(p[:, :], w[:, kc, cl * 128:(cl + 1) * 128], mT[:, kc, tsl],
                                                                   start=(kc == 0), stop=(kc == KC - 1)), R=[w, mTb[kc][tb]], W=[p])
                    xs = g.xT[:, c, tsl]
                    kb.op("dve", lambda e, p=p, c=c, xs=xs: e.scalar_tensor_tensor(
                        out=xs, in0=p[:, :], scalar=g.modT[:, 16 + c:17 + c], in1=xs, op0=ALU.mult, op1=ALU.add),
                        R=[p, g.modT, g.xTb[c][tb]], W=[g.xTb[c][tb]])
        kb.barrier()


def phase_mlp(g, l):
    kb = g.kb
    with ExitStack() as ph:
        wup = [kb.sb(ph, f"wup{i}", [128, KC, 512], BF16) for i in range(2)]
        wdn = [kb.sb(ph, f"wdn{i}", [128, 4, D], BF16) for i in range(2)]
        act = [kb.sb(ph, f"mact{i}", [128, 4, L], BF16) for i in range(2)]
        actb = [[[kb.buf() for tb in range(4)] for hc in range(4)] for i in range(2)]
        rl = [kb.sb(ph, f"mrl{i}", [128, 512], F32) for i in range(2)]
        nrl = 0
        for gi in range(8):
            wu, wd, a, ab = wup[gi % 2], wdn[gi % 2], act[gi % 2], actb[gi % 2]
            load_w(g, wu, g.inp["w_up"].t[l][:, gi * 512:(gi + 1) * 512])
            load_w(g, wd, g.inp["w_down"].t[l][gi * 512:(gi + 1) * 512, :])
            for tb in range(4):
                for hc in range(4):
                    p = kb.ps()
                    for kc in range(KC):
                        kb.op("pe", lambda e, p=p, kc=kc, hc=hc: e.matmul(
                            p[:, :], wu[:, kc, hc * 128:(hc + 1) * 128], g.hT[:, kc, tb * 512:(tb + 1) * 512],
                            start=(kc == 0), stop=(kc == KC - 1)), R=[wu, g.hTb[kc][tb]], W=[p])
                    r = rl[nrl % 2]
                    nrl += 1
                    kb.op("act", lambda e, p=p, r=r: e.activation(out=r[:], in_=p[:, :], func=AF.Relu), R=[p], W=[r])
                    kb.op("act", lambda e, r=r, hc=hc: e.activation(out=a[:, hc, tb * 512:(tb + 1) * 512], in_=r[:], func=AF.Square),
                          R=[r], W=[ab[hc][tb]])
            import os
            if os.environ.get("MLP_UP_ONLY"):
                continue
            for tb in range(4):
                for c in range(KC):
                    p = kb.ps()
                    for hc in range(4):
                        kb.op("pe", lambda e, p=p, hc=hc, c=c: e.matmul(
                            p[:, :], wd[:, hc, c * 128:(c + 1) * 128], a[:, hc, tb * 512:(tb + 1) * 512],
                            start=(hc == 0), stop=(hc == 3)), R=[wd, ab[hc][tb]], W=[p])
                    xs = g.xT[:, c, tb * 512:(tb + 1) * 512]
                    kb.op("dve", lambda e, p=p, c=c, xs=xs: e.scalar_tensor_tensor(
                        out=xs, in0=p[:, :], scalar=g.modT[:, 40 + c:41 + c], in1=xs, op0=ALU.mult, op1=ALU.add),
                        R=[p, g.modT, g.xTb[c][tb]], W=[g.xTb[c][tb]])
        kb.barrier()


def rstd_bcast(kb, ph, xT, xTb, tb, onesb, sq, rs):
    p = kb.ps()
    for c in range(KC):
        s = sq[c % 2]
        kb.op("act", lambda e, s=s, c=c: e.activation(out=s[:], in_=xT[:, c, tb * 512:(tb + 1) * 512], func=AF.Square),
              R=[xTb[c][tb]], W=[s])
        kb.op("pe", lambda e, s=s, c=c, p=p: e.matmul(p[:, :], onesb[:], s[:], start=(c == 0), stop=(c == KC - 1)),
              R=[s, onesb], W=[p])
    kb.op("dve", lambda e: e.tensor_scalar(out=rs[:], in0=p[:, :], scalar1=1.0 / D, scalar2=EPS, op0=ALU.mult, op1=ALU.add),
          R=[p], W=[rs])
    kb.op("act", lambda e: e.activation(out=rs[:], in_=rs[:], func=AF.Ln), R=[rs], W=[rs])
    kb.op("act", lambda e: e.activation(out=rs[:], in_=rs[:], func=AF.Exp, scale=-0.5), R=[rs], W=[rs])


def final_norm(kb, nc, xT, xTb, fnw, onesb, ident, out_d):
    with ExitStack() as ph:
        sq = [kb.sb(ph, f"fsq{i}", [128, 512], BF16) for i in range(2)]
        rs = kb.sb(ph, "frs", [128, 512], F32)
        yT = [kb.sb(ph, f"fyT{i}", [128, 512], F32) for i in range(2)]
        ost = [kb.sb(ph, f"fost{i}", [128, 4, D], F32) for i in range(2)]
        for tb in range(4):
            rstd_bcast(kb, ph, xT, xTb, tb, onesb, sq, rs)
            o = ost[tb % 2]
            for c in range(KC):
                y = yT[c % 2]
                kb.op("dve", lambda e, y=y, c=c: e.scalar_tensor_tensor(
                    out=y[:], in0=xT[:, c, tb * 512:(tb + 1) * 512], scalar=fnw[:, c:c + 1], in1=rs[:],
                    op0=ALU.mult, op1=ALU.mult), R=[xTb[c][tb], fnw, rs], W=[y])
                p = kb.ps()
                for j in range(4):
                    kb.op("pe", lambda e, y=y, j=j, p=p: e.transpose(
                        p[:, j * 128:(j + 1) * 128], y[:, j * 128:(j + 1) * 128], ident[:]), R=[y, ident], W=[p])
                src = p[:, :].rearrange("p (j f) -> p j f", j=4)
                dst = o[:, :, c * 128:(c + 1) * 128]
                kb.op("act", lambda e, dst=dst, src=src: e.activation(out=dst, in_=src, func=AF.Copy), R=[p], Wd=[o])
            kb.dma("sp", out_d.t[tb * 512:(tb + 1) * 512, :].rearrange("(j p) d -> p j d", p=128), o[:], R=[o], W=[out_d])
        kb.barrier()


_NC_CACHE = {}


def make_in_maps(inputs):
    nb = inputs["x"].shape[0]
    inputs = dict(inputs)
    inputs.update(host_consts())
    shared = {n: np.ascontiguousarray(np.asarray(inputs[n], dtype=np.float32)) for n in IN_SHAPES if n not in ("x", "c")}
    in_maps = []
    for b in range(nb):
        m = dict(shared)
        m["x"] = np.ascontiguousarray(inputs["x"][b])
        m["c"] = np.ascontiguousarray(inputs["c"][b])
        in_maps.append(m)
    return in_maps


def kernel(**inputs):
    nb = inputs["x"].shape[0]
    if "nc" not in _NC_CACHE:
        _NC_CACHE["nc"] = build()[0]
    nc = _NC_CACHE["nc"]
    res = run_bass_kernel_spmd(nc, make_in_maps(inputs), core_ids=list(range(nb)))
    return np.stack([r["out"] for r in res.results], axis=0)
```

```python
import math
from contextlib import ExitStack

import numpy as np
import concourse.bass as bass
import concourse.mybir as mybir
from concourse.bass_utils import run_bass_kernel_spmd

F32 = mybir.dt.float32
BF16 = mybir.dt.bfloat16
I32 = mybir.dt.int32
AF = mybir.ActivationFunctionType
ALU = mybir.AluOpType
AX = mybir.AxisListType

D = 1024
L = 2048
DEPTH = 2
NT = L // 128
KC = D // 128
EPS = 1e-6
import os
NDS = int(os.environ.get("NDS", "12"))
STRICT = bool(int(os.environ.get("KSTRICT", "1")))


class Buf:
    __slots__ = ("name", "w", "r", "excl")

    def __init__(self, name):
        self.name = name
        self.excl = False
        self.w = None
        self.r = {}


class T:
    def __init__(self, t, buf):
        self.t = t
        self.b = buf

    def __getitem__(self, k):
        return self.t[k]


class KB:
    def __init__(self, nc):
        self.nc = nc
        self.es = ExitStack()
        self.sems = {}
        self.engs = {}
        for name, e in (("pe", nc.tensor), ("act", nc.scalar), ("dve", nc.vector),
                        ("pool", nc.gpsimd), ("sp", nc.sync)):
            key = "s_" + name
            self.sems[key] = self.es.enter_context(nc.semaphore(key))
            self.engs[name] = dict(e=e, key=key, cnt=0, seen={})
        self.dq = {}
        for q in ("sp", "pool"):
            keys = []
            for i in range(NDS):
                key = f"d_{q}{i}"
                self.sems[key] = self.es.enter_context(nc.semaphore(key))
                keys.append(key)
            self.dq[q] = dict(keys=keys, cnt=[0] * NDS, nxt=0)
        self.nbuf = 0
        self.psum = []
        self.ps_next = 0
        self.pinned = set()

    def buf(self, name=None):
        self.nbuf += 1
        return Buf(name or f"b{self.nbuf}")

    def sb(self, es, name, shape, dtype):
        self.nbuf += 1
        name = f"{name}_{self.nbuf}"
        t = es.enter_context(self.nc.sbuf_tensor(name, list(shape), dtype))
        return T(t, self.buf(name))

    def dram(self, name, shape, dtype, kind="Internal"):
        t = self.nc.dram_tensor(name, list(shape), dtype, kind=kind)
        return T(t.ap(), self.buf(name))

    def init_psum(self):
        for i in range(8):
            t = self.es.enter_context(self.nc.psum_tensor(f"ps{i}", [128, 512], F32))
            self.psum.append(T(t, self.buf(f"ps{i}")))
            self.psum[-1].b.excl = True

    def ps(self, pin=False):
        while True:
            i = self.ps_next
            self.ps_next = (self.ps_next + 1) % 8
            if i not in self.pinned:
                break
        if pin:
            self.pinned.add(i)
        return self.psum[i]

    def unpin(self, p):
        self.pinned.discard(self.psum.index(p))

    def _deps(self, own_key, R, W, same_raw, Wd=()):
        deps = {}

        def add(ev):
            if ev is None:
                return
            k, v = ev
            if deps.get(k, 0) < v:
                deps[k] = v

        for t in R:
            b = t.b if isinstance(t, T) else t
            if b.w is not None and (b.w[0] != own_key or same_raw):
                add(b.w)
            if b.excl:
                for k, v in b.r.items():
                    if k != own_key:
                        add((k, v))
        for t in W:
            b = t.b if isinstance(t, T) else t
            if b.w is not None and (b.w[0] != own_key or (STRICT and same_raw)):
                add(b.w)
            for k, v in b.r.items():
                if k != own_key or (STRICT and same_raw):
                    add((k, v))
        for t in Wd:
            b = t.b if isinstance(t, T) else t
            if b.w is not None and b.w[0] != own_key:
                add(b.w)
            for k, v in b.r.items():
                if k != own_key or (STRICT and same_raw):
                    add((k, v))
        return deps

    def _mark(self, ev, R, W):
        k, v = ev
        for t in R:
            b = t.b if isinstance(t, T) else t
            if b.r.get(k, 0) < v:
                b.r[k] = v
        for t in W:
            b = t.b if isinstance(t, T) else t
            b.w = ev
            b.r = {}

    def _wait(self, eng, deps):
        for k, v in deps.items():
            if eng["seen"].get(k, 0) < v:
                eng["e"].wait_ge(self.sems[k], v)
                eng["seen"][k] = v

    def op(self, engname, fn, R=(), W=(), Wd=()):
        eng = self.engs[engname]
        deps = self._deps(eng["key"], R, W, same_raw=(engname != "pe"), Wd=Wd)
        W = list(W) + list(Wd)
        self._wait(eng, deps)
        ins = fn(eng["e"])
        eng["cnt"] += 1
        ins.then_inc(self.sems[eng["key"]], 1)
        self._mark((eng["key"], eng["cnt"]), R, W)
        return ins

    def dma(self, q, out, in_, R=(), W=(), **kw):
        eng = self.engs[q]
        dq = self.dq[q]
        deps = self._deps(None, R, W, same_raw=True)
        i = dq["nxt"]
        dq["nxt"] = (i + 1) % NDS
        key = dq["keys"][i]
        if dq["cnt"][i] > 0:
            deps[key] = max(deps.get(key, 0), dq["cnt"][i])
        self._wait(eng, deps)
        dq["cnt"][i] += 16
        eng["e"].dma_start(out=out, in_=in_, **kw).then_inc(self.sems[key], 16)
        self._mark((key, dq["cnt"][i]), R, W)

    def barrier(self):
        allev = {}
        for name, eng in self.engs.items():
            if eng["cnt"] > 0:
                allev[eng["key"]] = eng["cnt"]
        for q, dq in self.dq.items():
            for key, c in zip(dq["keys"], dq["cnt"]):
                if c > 0:
                    allev[key] = c
        for name, eng in self.engs.items():
            deps = {k: v for k, v in allev.items() if k != eng["key"]}
            self._wait(eng, deps)

    def finish(self, out_bufs):
        eng = self.engs["sp"]
        deps = {}
        for t in out_bufs:
            b = t.b if isinstance(t, T) else t
            if b.w is not None:
                deps[b.w[0]] = max(deps.get(b.w[0], 0), b.w[1])
        self._wait(eng, deps)
        self.barrier()


IN_SHAPES = {
    "x": [L, D], "c": [D], "norm1_w": [DEPTH, D], "ada_w": [DEPTH, D, 6 * D], "ada_b": [DEPTH, 6 * D],
    "w_in": [DEPTH, D, 8536], "conv_w": [DEPTH, 4, 1536], "conv_b": [DEPTH, 1536], "dt_bias": [DEPTH, 16],
    "a_log": [DEPTH, 16], "d_skip": [DEPTH, 16], "ssd_norm_w": [DEPTH, D], "w_br_ssd": [DEPTH, D, D],
    "w_br_sb": [DEPTH, 512, D], "w_br_dsa": [DEPTH, 512, D], "w_out": [DEPTH, D, D], "norm2_w": [DEPTH, D],
    "w_up": [DEPTH, D, 4 * D], "w_down": [DEPTH, 4 * D, D], "final_norm_w": [D],
    "rope_cs": [L, 16],
}


def host_consts():
    half = 8
    inv_freq = np.exp(np.arange(half, dtype=np.float32) * np.float32(-2.0 * math.log(500000.0) / 16)).astype(np.float32)
    ang = np.arange(L, dtype=np.float32)[:, None] * inv_freq[None, :]
    return {"rope_cs": np.concatenate([np.cos(ang), np.sin(ang)], axis=1).astype(np.float32)}


class G:
    pass


def build(nlayers=DEPTH, dbg=(), skip_mixer=False, phases=("mod", "inproj", "ssd", "sb", "dsa", "merge", "norm2", "mlp")):
    nc = bass.Bass("TRN2", target_bir_lowering=False)
    kb = KB(nc)
    es = kb.es
    kb.init_psum()
    g = G()
    g.nc, g.kb, g.es, g.dbg = nc, kb, es, set(dbg)
    g.inp = {n: kb.dram(n, shp, F32, kind="ExternalInput") for n, shp in IN_SHAPES.items()}
    out_d = kb.dram("out", [L, D], F32, kind="ExternalOutput")
    g.dbg_out = {}

    g.ident = ident = kb.sb(es, "ident", [128, 128], F32)
    g.identb = identb = kb.sb(es, "identb", [128, 128], BF16)
    g.onesb = onesb = kb.sb(es, "onesb", [128, 128], BF16)
    g.onesf = kb.sb(es, "onesf", [128, 128], F32)
    g.triu = kb.sb(es, "triu", [128, 128], F32)
    g.negm = kb.sb(es, "negm", [128, 128], F32)
    g.triub = kb.sb(es, "triub", [128, 128], BF16)
    g.strl = kb.sb(es, "strl", [128, 128], F32)
    g.strlb = kb.sb(es, "strlb", [128, 128], BF16)
    g.trilb = kb.sb(es, "trilb", [128, 128], BF16)
    g.negup = kb.sb(es, "negup", [128, 128], F32)
    g.negmb = kb.sb(es, "negmb", [128, 128], BF16)
    g.pow2 = kb.sb(es, "pow2", [128, 24], F32)
    with ExitStack() as tmp:
        coli = kb.sb(tmp, "coli", [128, 128], I32)
        rowi = kb.sb(tmp, "rowi", [128, 1], I32)
        colf = kb.sb(tmp, "colf", [128, 128], F32)
        rowf = kb.sb(tmp, "rowf", [128, 1], F32)
        kb.op("pool", lambda e: e.iota(coli[:], [[1, 128]], base=0, channel_multiplier=0), W=[coli])
        kb.op("pool", lambda e: e.iota(rowi[:], [[0, 1]], base=0, channel_multiplier=1), W=[rowi])
        kb.op("dve", lambda e: e.tensor_copy(out=colf[:], in_=coli[:]), R=[coli], W=[colf])
        kb.op("dve", lambda e: e.tensor_copy(out=rowf[:], in_=rowi[:]), R=[rowi], W=[rowf])
        kb.op("dve", lambda e: e.tensor_scalar(out=ident[:], in0=colf[:], scalar1=rowf[:, 0:1], scalar2=None,
                                               op0=ALU.is_equal), R=[colf, rowf], W=[ident])
        kb.op("dve", lambda e: e.tensor_copy(out=identb[:], in_=ident[:]), R=[ident], W=[identb])
        kb.op("dve", lambda e: e.memset(onesb[:], 1.0), W=[onesb])
        kb.op("dve", lambda e: e.memset(g.onesf[:], 1.0), W=[g.onesf])
        kb.op("dve", lambda e: e.tensor_scalar(out=g.triu[:], in0=colf[:], scalar1=rowf[:, 0:1], scalar2=None, op0=ALU.is_ge),
              R=[colf, rowf], W=[g.triu])
        kb.op("dve", lambda e: e.tensor_scalar(out=g.negm[:], in0=g.triu[:], scalar1=-1.0, scalar2=30000.0, op0=ALU.add, op1=ALU.mult),
              R=[g.triu], W=[g.negm])
        kb.op("dve", lambda e: e.tensor_copy(out=g.triub[:], in_=g.triu[:]), R=[g.triu], W=[g.triub])
        kb.op("dve", lambda e: e.tensor_copy(out=g.negmb[:], in_=g.negm[:]), R=[g.negm], W=[g.negmb])
        kb.op("dve", lambda e: e.tensor_scalar(out=g.strl[:], in0=colf[:], scalar1=rowf[:, 0:1], scalar2=None, op0=ALU.is_lt),
              R=[colf, rowf], W=[g.strl])
        kb.op("dve", lambda e: e.tensor_copy(out=g.strlb[:], in_=g.strl[:]), R=[g.strl], W=[g.strlb])
        kb.op("dve", lambda e: e.tensor_scalar(out=g.trilb[:], in0=colf[:], scalar1=rowf[:, 0:1], scalar2=None, op0=ALU.is_le),
              R=[colf, rowf], W=[g.trilb])
        kb.op("dve", lambda e: e.tensor_scalar(out=g.negup[:], in0=g.trilb[:], scalar1=-1.0, scalar2=1.0e9, op0=ALU.add, op1=ALU.mult),
              R=[g.trilb], W=[g.negup])
        for kk_ in range(24):
            kb.op("dve", lambda e, kk_=kk_: e.memset(g.pow2[:, kk_:kk_ + 1], float(2.0 ** (-kk_))), Wd=[g.pow2])
        kb.barrier()

    g.xT = xT = kb.sb(es, "xT", [128, KC, L], F32)
    g.xTb = xTb = [[kb.buf(f"xT{c}_{tb}") for tb in range(4)] for c in range(KC)]
    g.hTb = [[kb.buf(f"hT{c}_{tb}") for tb in range(4)] for c in range(KC)]

    fnw = load_cols(g, es, "fnw", g.inp["final_norm_w"].t, KC)
    g.ccol = load_cols(g, es, "ccol", g.inp["c"].t, KC)
    g.csilu = kb.sb(es, "csilu", [128, KC], BF16)
    kb.op("act", lambda e: e.activation(out=g.csilu[:], in_=g.ccol[:], func=AF.Silu), R=[g.ccol], W=[g.csilu])
    g.modT_l = [kb.sb(es, "modT", [128, 48], F32) for _ in range(DEPTH)]
    g.modraw = [kb.sb(es, "modraw", [128, 48], F32) for _ in range(DEPTH)]
    g.A1_l = [kb.sb(es, "A1", [128, KC], F32) for _ in range(DEPTH)]
    g.A2_l = [kb.sb(es, "A2", [128, KC], F32) for _ in range(DEPTH)]
    g.dsa_hook = None

    x_in = g.inp["x"]
    with ExitStack() as ph:
        xin = [kb.sb(ph, f"xin{i}", [128, D], F32) for i in range(2)]
        for tt in range(NT):
            xi = xin[tt % 2]
            kb.dma("sp", xi[:], x_in.t[tt * 128:(tt + 1) * 128, :], W=[xi])
            for half in range(2):
                p = kb.ps()
                for j in range(4):
                    c = half * 4 + j
                    kb.op("pe", lambda e, c=c, j=j, p=p, xi=xi: e.transpose(
                        p[:, j * 128:(j + 1) * 128], xi[:, c * 128:(c + 1) * 128], ident[:]),
                        R=[xi, ident], W=[p])
                tb = tt // 4
                wb = [xTb[half * 4 + j][tb] for j in range(4)]
                dst = xT[:, half * 4:half * 4 + 4, tt * 128:(tt + 1) * 128]
                src = p[:, :].rearrange("p (j t) -> p j t", j=4)
                if half == 0:
                    kb.op("act", lambda e, dst=dst, src=src: e.activation(out=dst, in_=src, func=AF.Copy), R=[p], W=wb)
                else:
                    kb.op("dve", lambda e, dst=dst, src=src: e.tensor_copy(out=dst, in_=src), R=[p], W=wb)
        kb.barrier()

    def scr(name, shape, dtype=BF16):
        return kb.dram("scr_" + name, shape, dtype, kind=("ExternalOutput" if name in g.dbg else "Internal"))
    g.scr = dict(zs=scr("zs", [L, 1024]), sbv=scr("sbv", [L, 512]), dv=scr("dv", [L, 128]),
                 dqT=scr("dqT", [8, 64, L]), dkT=scr("dkT", [2, 64, L]), iqT=scr("iqT", [8, 64, L]), ikT=scr("ikT", [1, 64, L]),
                 yT=scr("yT", [16, 128, L]), sbqT=scr("sbqT", [4, 128, L]), sbkT=scr("sbkT", [4, 128, L]), gT=scr("gT", [24, 128, L]))
    for name in g.scr:
        if name in g.dbg:
            g.dbg_out["scr_" + name] = g.scr[name]
    g.ropecs = kb.sb(es, "ropecs", [128, NT, 16], F32)
    kb.dma("sp", g.ropecs[:], g.inp["rope_cs"].t.rearrange("(t p) c -> p t c", p=128), W=[g.ropecs])
    g.dt_tok = kb.sb(es, "dt_tok", [128, NT, 16], F32)
    g.wix_tok = kb.sb(es, "wix_tok", [128, NT, 8], F32)
    g.dtb_bc = kb.sb(es, "dtb_bc", [128, 16], F32)
    g.cb = kb.sb(es, "cb", [128, 12], F32)
    g.cw = [kb.sb(es, f"cw{k}", [128, 12], F32) for k in range(4)]

    for l in range(nlayers):
        g.modT, g.A1, g.A2 = g.modT_l[l], g.A1_l[l], g.A2_l[l]
        if "mod" in phases:
            if l == 0:
                phase_mod(g, l)
            else:
                mod_finish(g, l)
        g.dsa_hook = (l + 1) if (l + 1 < nlayers and "mod" in phases) else None
        dump_sb(g, f"mod{l}", g.modT, [128, 48])
        if not skip_mixer:
            with ExitStack() as ssd_scope:
                g.xs_tok = kb.sb(ssd_scope, "xs_tok", [128, NT, 1024], BF16)
                g.bt_tok = kb.sb(ssd_scope, "bt_tok", [128, NT, 256], BF16)
                g.bcT = kb.sb(ssd_scope, "bcT", [128, 4, L], BF16)
                with nc.allow_non_contiguous_dma(reason="tiny vectors"):
                    kb.dma("sp", g.dtb_bc[:], g.inp["dt_bias"].t[l].partition_broadcast(128), W=[g.dtb_bc])
                    kb.dma("sp", g.cb[:], g.inp["conv_b"].t[l].rearrange("(c p) -> p c", p=128), W=[g.cb])
                    for k in range(4):
                        kb.dma("sp", g.cw[k][:], g.inp["conv_w"].t[l][k].rearrange("(c p) -> p c", p=128), W=[g.cw[k]])
                with ExitStack() as hsc:
                    g.hT = kb.sb(hsc, "hT", [128, KC, L], BF16)
                    phase_norm(g, l, which=1)
                    dump_hT(g, f"h{l}")
                    if "inproj" in phases:
                        phase_inproj(g, l)
                dump_sb(g, f"xs_tok{l}", g.xs_tok, [128, NT, 1024], BF16)
                dump_sb(g, f"bt_tok{l}", g.bt_tok, [128, NT, 256], BF16)
                dump_sb(g, f"bcT{l}", g.bcT, [128, 4, L], BF16)
                dump_sb(g, f"dt_tok{l}", g.dt_tok, [128, NT, 16], F32)
                dump_sb(g, f"wix_tok{l}", g.wix_tok, [128, NT, 8], F32)
                kb.barrier()
                if "ssd" in phases:
                    phase_ssd(g, l)
            if "sb" in phases:
                phase_sb(g, l)
            if "dsa" in phases:
                phase_dsa(g, l)
            if "merge" in phases:
                phase_merge(g, l)
            dump_xT(g, f"xmix{l}")
        with ExitStack() as hsc:
            g.hT = kb.sb(hsc, "hT", [128, KC, L], BF16)
            if "norm2" in phases:
                phase_norm(g, l, which=2)
            if "mlp" in phases:
                phase_mlp(g, l)
        dump_xT(g, f"xout{l}")

    final_norm(kb, nc, xT, xTb, fnw, onesb, ident, out_d)
    kb.finish([out_d] + list(g.dbg_out.values()))
    return nc, sorted(g.dbg_out.keys())


def load_cols(g, es_, name, src_ap, n):
    kb = g.kb
    t = kb.sb(es_, name, [128, n], F32)
    with g.nc.allow_non_contiguous_dma(reason="tiny per-feature vector load"):
        for c0 in range(0, n, 8):
            c1 = min(n, c0 + 8)
            kb.dma("sp", t[:, c0:c1], src_ap[c0 * 128:c1 * 128].rearrange("(c p) -> p c", p=128), W=[t])
    return t


def dump_sb(g, name, t, shape, dtype=F32):
    if name not in g.dbg:
        return
    d = g.kb.dram("dbg_" + name, shape, dtype, kind="ExternalOutput")
    g.kb.barrier()
    g.kb.dma("sp", d.t, t[:], R=[t], W=[d])
    g.dbg_out["dbg_" + name] = d


def dump_xT(g, name):
    if name not in g.dbg:
        return
    d = g.kb.dram("dbg_" + name, [128, KC, L], F32, kind="ExternalOutput")
    g.kb.barrier()
    g.kb.dma("sp", d.t, g.xT[:], R=[b for row in g.xTb for b in row], W=[d])
    g.dbg_out["dbg_" + name] = d


def dump_hT(g, name):
    if name not in g.dbg:
        return
    d = g.kb.dram("dbg_" + name, [128, KC, L], BF16, kind="ExternalOutput")
    g.kb.barrier()
    g.kb.dma("sp", d.t, g.hT[:], R=[b for row in g.hTb for b in row], W=[d])
    g.dbg_out["dbg_" + name] = d


def load_w(g, t, src_rows_ap):
    g.kb.dma("pool", t[:], src_rows_ap.rearrange("(kc p) n -> p kc n", p=128), W=[t])


def mod_piece_load(g, l, j4, w):
    g.kb.dma("pool", w[:], g.inp["ada_w"].t[l][:, j4 * 512:(j4 + 1) * 512].rearrange("(kc p) n -> p kc n", p=128), W=[w])


def mod_piece_mm(g, l, j4, w):
    kb = g.kb
    p = kb.ps()
    for jl in range(4):
        for kc in range(KC):
            kb.op("pe", lambda e, jl=jl, kc=kc: e.matmul(p[:, jl:jl + 1], w[:, kc, jl * 128:(jl + 1) * 128], g.csilu[:, kc:kc + 1],
                                                         start=(kc == 0), stop=(kc == KC - 1)), R=[w, g.csilu], W=[p])
    raw = g.modraw[l]
    kb.op("act", lambda e: e.activation(out=raw[:, j4 * 4:(j4 + 1) * 4], in_=p[:, 0:4], func=AF.Copy), R=[p], Wd=[raw])


def mod_piece(g, l, j4, w):
    mod_piece_load(g, l, j4, w)
    mod_piece_mm(g, l, j4, w)


def mod_finish(g, l):
    kb = g.kb
    with ExitStack() as ph:
        adab = load_cols(g, ph, "adab", g.inp["ada_b"].t[l], 48)
        n1w = load_cols(g, ph, "n1w", g.inp["norm1_w"].t[l], KC)
        n2w = load_cols(g, ph, "n2w", g.inp["norm2_w"].t[l], KC)
        modT, A1, A2 = g.modT_l[l], g.A1_l[l], g.A2_l[l]
        kb.op("dve", lambda e: e.tensor_tensor(out=modT[:], in0=g.modraw[l][:], in1=adab[:], op=ALU.add),
              R=[g.modraw[l], adab], W=[modT])
        kb.op("dve", lambda e: e.scalar_tensor_tensor(out=A1[:], in0=modT[:, 8:16], scalar=1.0, in1=n1w[:],
                                                      op0=ALU.add, op1=ALU.mult), R=[modT, n1w], W=[A1])
        kb.op("dve", lambda e: e.scalar_tensor_tensor(out=A2[:], in0=modT[:, 32:40], scalar=1.0, in1=n2w[:],
                                                      op0=ALU.add, op1=ALU.mult), R=[modT, n2w], W=[A2])
        kb.barrier()


def phase_mod(g, l):
    kb = g.kb
    with ExitStack() as ph:
        wt = [kb.sb(ph, f"adaw{i}", [128, KC, 512], BF16) for i in range(3)]
        for j4 in range(12):
            mod_piece(g, l, j4, wt[j4 % 3])
        kb.barrier()
    mod_finish(g, l)


def phase_norm(g, l, which):
    kb = g.kb
    A = g.A1 if which == 1 else g.A2
    sh0 = 0 if which == 1 else 24
    with ExitStack() as ph:
        sq = [kb.sb(ph, f"nsq{i}", [128, 512], BF16) for i in range(2)]
        rs = kb.sb(ph, "nrs", [128, 512], F32)
        tmp = [kb.sb(ph, f"ntmp{i}", [128, 512], F32) for i in range(2)]
        for tb in range(4):
            rstd_bcast(kb, ph, g.xT, g.xTb, tb, g.onesb, sq, rs)
            for c in range(KC):
                t = tmp[c % 2]
                kb.op("dve", lambda e, t=t, c=c: e.tensor_tensor(out=t[:], in0=g.xT[:, c, tb * 512:(tb + 1) * 512], in1=rs[:],
                                                                 op=ALU.mult), R=[g.xTb[c][tb], rs], W=[t])
                kb.op("act", lambda e, t=t, c=c: e.activation(
                    out=g.hT[:, c, tb * 512:(tb + 1) * 512], in_=t[:], func=AF.Identity,
                    scale=A[:, c:c + 1], bias=g.modT[:, sh0 + c:sh0 + c + 1]), R=[t, A, g.modT], W=[g.hTb[c][tb]])
        kb.barrier()


OFF = dict(z=0, xbc=1024, dt=2560, sbq=2576, sbk=3088, sbv=3600, dsq=4112, dsk=4624, dsv=4752,
           ixq=4880, ixk=5392, ixw=5456, gate=5464)


def rope(g, f3, nh, tt, rt):
    kb = g.kb
    fb = f3._buf
    cos = g.ropecs[:, tt:tt + 1, 0:8].broadcast_to([128, nh, 8])
    sin = g.ropecs[:, tt:tt + 1, 8:16].broadcast_to([128, nh, 8])
    x1 = f3.ap[:, :, 0:8]
    x2 = f3.ap[:, :, 8:16]
    t = [rt[:, i, 0:nh, :] for i in range(4)]
    kb.op("dve", lambda e: e.tensor_tensor(out=t[0], in0=x1, in1=cos, op=ALU.mult), R=[fb, g.ropecs], W=[rt])
    kb.op("dve", lambda e: e.tensor_tensor(out=t[1], in0=x2, in1=sin, op=ALU.mult), R=[fb, g.ropecs], W=[rt])
    kb.op("dve", lambda e: e.tensor_tensor(out=t[2], in0=x2, in1=cos, op=ALU.mult), R=[fb, g.ropecs], W=[rt])
    kb.op("dve", lambda e: e.tensor_tensor(out=t[3], in0=x1, in1=sin, op=ALU.mult), R=[fb, g.ropecs], W=[rt])
    kb.op("dve", lambda e: e.tensor_tensor(out=x1, in0=t[0], in1=t[1], op=ALU.subtract), R=[rt], W=[fb])
    kb.op("dve", lambda e: e.tensor_tensor(out=x2, in0=t[2], in1=t[3], op=ALU.add), R=[rt], W=[fb])


class V:
    def __init__(self, ap, buf):
        self.ap = ap
        self._buf = buf


def phase_inproj(g, l):
    kb, nc = g.kb, g.nc
    win = g.inp["w_in"].t[l]
    S = g.scr
    with ExitStack() as ph:
        wt = [kb.sb(ph, "winw", [128, KC, 512], BF16) for _ in range(2)]
        nw = [0]

        def getw(n0, N):
            w = wt[nw[0] % 2]
            nw[0] += 1
            kb.dma("pool", w[:, :, 0:N], win[:, n0:n0 + N].rearrange("(kc p) n -> p kc n", p=128), W=[w])
            return w

        stgb = [kb.sb(ph, "stgb", [128, 512], BF16) for _ in range(3)]
        stgf = [kb.sb(ph, "stgf", [128, 512], F32) for _ in range(2)]
        rt = kb.sb(ph, "ropetmp", [128, 4, 16, 8], F32)
        tst = [kb.sb(ph, "tst", [64, 8, 512], BF16) for _ in range(2)]
        cnt = dict(b=0, f=0, t=0)

        def nxt(lst, k):
            r = lst[cnt[k] % len(lst)]
            cnt[k] += 1
            return r

        def tok_mm(w, N, tt):
            p = kb.ps()
            for kc in range(KC):
                kb.op("pe", lambda e, kc=kc: e.matmul(p[:, 0:N], g.hT[:, kc, tt * 128:(tt + 1) * 128], w[:, kc, 0:N],
                                                      start=(kc == 0), stop=(kc == KC - 1)),
                      R=[g.hTb[kc][tt // 4], w], W=[p])
            return p

        def feat_mm(w, ci, tb):
            p = kb.ps()
            for kc in range(KC):
                kb.op("pe", lambda e, kc=kc: e.matmul(p[:, :], w[:, kc, ci * 128:(ci + 1) * 128],
                                                      g.hT[:, kc, tb * 512:(tb + 1) * 512],
                                                      start=(kc == 0), stop=(kc == KC - 1)),
                      R=[g.hTb[kc][tb], w], W=[p])
            return p

        for zi in range(2):
            w = getw(OFF["z"] + zi * 512, 512)
            for tt in range(NT):
                p = tok_mm(w, 512, tt)
                sb_ = nxt(stgb, "b")
                kb.op("act", lambda e: e.activation(out=sb_[:], in_=p[:, :], func=AF.Silu), R=[p], W=[sb_])
                kb.dma("sp", S["zs"].t[tt * 128:(tt + 1) * 128, zi * 512:(zi + 1) * 512], sb_[:], R=[sb_])
        w = getw(OFF["dt"], 16)
        for tt in range(NT):
            p = tok_mm(w, 16, tt)
            f = nxt(stgf, "f")
            kb.op("dve", lambda e: e.tensor_tensor(out=f[:, 0:16], in0=p[:, 0:16], in1=g.dtb_bc[:], op=ALU.add),
                  R=[p, g.dtb_bc], W=[f])
            kb.op("act", lambda e: e.activation(out=f[:, 0:16], in_=f[:, 0:16], func=AF.Exp), R=[f], W=[f])
            kb.op("act", lambda e: e.activation(out=g.dt_tok[:, tt, :], in_=f[:, 0:16], func=AF.Ln, bias=1.0),
                  R=[f], W=[g.dt_tok])
        w = getw(OFF["sbv"], 512)
        for tt in range(NT):
            p = tok_mm(w, 512, tt)
            sb_ = nxt(stgb, "b")
            kb.op("act", lambda e: e.activation(out=sb_[:], in_=p[:, :], func=AF.Copy), R=[p], W=[sb_])
            kb.dma("sp", S["sbv"].t[tt * 128:(tt + 1) * 128, :], sb_[:], R=[sb_])

        def roped_group(n0, N, nh, dstT, extra=None):
            w = getw(n0, N)
            fs, bs, sts = {}, {}, {}

            def stA(tt):
                p = tok_mm(w, N, tt)
                f = nxt(stgf, "f")
                kb.op("act", lambda e: e.activation(out=f[:, 0:N], in_=p[:, 0:N], func=AF.Copy), R=[p], W=[f])
                fs[tt] = f

            def stB(tt):
                f = fs.pop(tt)
                rope(g, V(f[:, 0:nh * 64].rearrange("p (h d) -> p h d", d=64), f), nh, tt, rt)
                b = nxt(stgb, "b")
                kb.op("act", lambda e: e.activation(out=b[:, 0:N], in_=f[:, 0:N], func=AF.Copy), R=[f], W=[b])
                if extra is not None:
                    extra(tt, f, b)
                bs[tt] = b

            def stC(tt):
                b = bs.pop(tt)
                pt = kb.ps()
                ptb = pt[:, :].bitcast(BF16)
                for h in range(nh):
                    kb.op("pe", lambda e, h=h: e.transpose(ptb[0:64, h * 128:(h + 1) * 128], b[:, h * 64:(h + 1) * 64], g.identb[:]),
                          R=[b, g.identb], W=[pt])
                if tt % 4 == 0:
                    sts[tt // 4] = nxt(tst, "t")
                st = sts[tt // 4]
                kb.op("act", lambda e: e.activation(
                    out=st[0:64, 0:nh, (tt % 4) * 128:(tt % 4 + 1) * 128],
                    in_=ptb[0:64, 0:nh * 128].rearrange("p (h t) -> p h t", h=nh), func=AF.Copy), R=[pt], Wd=[st])
                if tt % 4 == 3:
                    tb = tt // 4
                    kb.dma("sp", dstT.t[:, :, tb * 512:(tb + 1) * 512].rearrange("h d t -> d h t"), st[0:64, 0:nh, :], R=[st])

            for step in range(NT + 2):
                if step < NT:
                    stA(step)
                if 0 <= step - 1 < NT:
                    stB(step - 1)
                if 0 <= step - 2 < NT:
                    stC(step - 2)

        roped_group(OFF["dsq"], 512, 8, S["dqT"])

        def dsv_extra(tt, f, b):
            kb.dma("sp", S["dv"].t[tt * 128:(tt + 1) * 128, :], b[:, 128:256], R=[b])
        roped_group(OFF["dsk"], 256, 2, S["dkT"], dsv_extra)
        roped_group(OFF["ixq"], 512, 8, S["iqT"])

        def ixw_extra(tt, f, b):
            kb.op("dve", lambda e: e.tensor_copy(out=g.wix_tok[:, tt, :], in_=f[:, 64:72]), R=[f], W=[g.wix_tok])
        roped_group(OFF["ixk"], 72, 1, S["ikT"], ixw_extra)

        def feat_group(n0, dstT, c0, func):
            w = getw(n0, 512)
            for ci in range(4):
                for tb in range(4):
                    p = feat_mm(w, ci, tb)
                    sb_ = nxt(stgb, "b")
                    kb.op("act", lambda e: e.activation(out=sb_[:], in_=p[:, :], func=func), R=[p], W=[sb_])
                    kb.dma("sp", dstT.t[c0 + ci, :, tb * 512:(tb + 1) * 512], sb_[:], R=[sb_])

        feat_group(OFF["sbq"], S["sbqT"], 0, AF.Copy)
        feat_group(OFF["sbk"], S["sbkT"], 0, AF.Copy)
        for gi in range(6):
            feat_group(OFF["gate"] + gi * 512, S["gT"], gi * 4, AF.Sigmoid)

        kb.barrier()

    with ExitStack() as ph:
        wt = [kb.sb(ph, "winw", [128, KC, 512], BF16) for _ in range(2)]
        nw = [0]
        xpad = [kb.sb(ph, "xpad", [128, 3 + L], F32) for _ in range(1)]
        cva = [kb.sb(ph, "cva", [128, L], F32) for _ in range(1)]
        cvo = [kb.sb(ph, "cvo", [128, L], BF16) for _ in range(2)]
        for xp in xpad:
            kb.op("dve", lambda e, xp=xp: e.memset(xp[:, 0:3], 0.0), W=[xp])
        for gi in range(3):
            w = getw(OFF["xbc"] + gi * 512, 512)
            for ci in range(4):
                cidx = gi * 4 + ci
                xp, acc = xpad[0], cva[0]
                for tb in range(4):
                    p = feat_mm(w, ci, tb)
                    kb.op("act", lambda e: e.activation(out=xp[:, 3 + tb * 512:3 + (tb + 1) * 512], in_=p[:, :], func=AF.Copy),
                          R=[p], Wd=[xp])
                kb.op("act", lambda e: e.activation(out=acc[:], in_=xp[:, 3:3 + L], func=AF.Identity,
                                                    scale=g.cw[3][:, cidx:cidx + 1], bias=g.cb[:, cidx:cidx + 1]),
                      R=[xp, g.cw[3], g.cb], W=[acc])
                for k in (2, 1, 0):
                    kb.op("dve", lambda e, k=k: e.scalar_tensor_tensor(out=acc[:], in0=xp[:, k:k + L], scalar=g.cw[k][:, cidx:cidx + 1],
                                                                       in1=acc[:], op0=ALU.mult, op1=ALU.add),
                          R=[xp, g.cw[k], acc], W=[acc])
                if cidx < 8:
                    o = cvo[cidx % 2]
                    ov = o[:, :]
                else:
                    o = g.bcT
                    ov = g.bcT[:, cidx - 8, :]
                kb.op("act", lambda e: e.activation(out=ov, in_=acc[:], func=AF.Silu), R=[acc], W=[o])
                if cidx < 10:
                    for half in range(2):
                        pt = kb.ps()
                        ptb = pt[:, :].bitcast(BF16)
                        for j in range(8):
                            tt = half * 8 + j
                            kb.op("pe", lambda e, j=j, tt=tt: e.transpose(ptb[:, j * 128:(j + 1) * 128], ov[:, tt * 128:(tt + 1) * 128],
                                                                          g.identb[:]), R=[o, g.identb], W=[pt])
                        if cidx < 8:
                            dst, dt_ = g.xs_tok[:, half * 8:(half + 1) * 8, cidx * 128:(cidx + 1) * 128], g.xs_tok
                        else:
                            dst, dt_ = g.bt_tok[:, half * 8:(half + 1) * 8, (cidx - 8) * 128:(cidx - 7) * 128], g.bt_tok
                        kb.op("dve", lambda e: e.tensor_copy(out=dst, in_=ptb.rearrange("p (j f) -> p j f", j=8)), R=[pt], W=[dt_])
        kb.barrier()


def phase_ssd(g, l):
    kb, nc = g.kb, g.nc
    S = g.scr
    with ExitStack() as ph:
        def sb(name, shape, dt_=F32):
            return kb.sb(ph, name, shape, dt_)
        a_bc = sb("a_bc", [128, 16])
        dsk_bc = sb("dsk_bc", [128, 16])
        nw_bc = sb("nw_bc", [128, 1024])
        Dmat = sb("Dmat", [128, 16, 128], BF16)
        with nc.allow_non_contiguous_dma(reason="tiny vectors"):
            kb.dma("sp", a_bc[:], g.inp["a_log"].t[l].partition_broadcast(128), W=[a_bc])
            kb.dma("sp", dsk_bc[:], g.inp["d_skip"].t[l].partition_broadcast(128), W=[dsk_bc])
            kb.dma("sp", nw_bc[:], g.inp["ssd_norm_w"].t[l].partition_broadcast(128), W=[nw_bc])
        kb.op("act", lambda e: e.activation(out=a_bc[:], in_=a_bc[:], func=AF.Exp), R=[a_bc], W=[a_bc])
        kb.op("dve", lambda e: e.tensor_scalar(out=a_bc[:], in0=a_bc[:], scalar1=-1.0, scalar2=None, op0=ALU.mult), R=[a_bc], W=[a_bc])
        for h in range(16):
            kb.op("dve", lambda e, h=h: e.tensor_scalar(out=Dmat[:, h, :], in0=g.ident[:], scalar1=dsk_bc[:, h:h + 1], scalar2=None,
                                                        op0=ALU.mult), R=[g.ident, dsk_bc], Wd=[Dmat])
        NB = 2
        X = [sb("X", [128, 2, 16, 128], BF16)] * 2
        dd = [sb("dd", [128, 16, 128])] * 2
        Mt = [sb("Mt", [128, 16, 128], BF16) for _ in range(NB)]
        xw = [sb("xw", [128, 1024], BF16) for _ in range(NB)]
        zst = [sb("zst", [128, 1024], BF16) for _ in range(NB)]
        yy = [sb("yy", [128, 1024])] * 2
        yz = yy
        ss = [sb("ss", [128, 1]) for _ in range(NB)]
        epsc = sb("epsc", [128, 1])
        kb.op("dve", lambda e: e.memset(epsc[:], EPS), W=[epsc])
        yn = [sb("yn", [128, 1024], BF16) for _ in range(NB)]
        yst = [sb("yst", [128, KC, 256], BF16)] * 2
        hst = [sb("hst", [128, 512]) for _ in range(2)]
        htmp = [sb("htmp", [128, 512]) for _ in range(2)]
        prevb = [[sb("prevb", [128, 512], BF16) for _ in range(2)] for _ in range(2)]
        ytmp = [sb("ytmp", [128, 512]) for _ in range(2)]

        dA_all = sb("dA_all", [128, 256])
        dAs_all = sb("dAs_all", [128, 2, 256], BF16)
        ac_all = sb("ac_all", [128, 256])
        eac_all = sb("eac_all", [128, 256])
        cd_all = sb("cd_all", [128, 256])
        wgt_all = sb("wgt_all", [128, 256])
        v_all = sb("v_all", [128, 256])
        vs_all = sb("vs_all", [128, 2, 256], BF16)
        acs_all = sb("acs_all", [128, 2, 256], BF16)
        dtf = g.dt_tok[:, :, :].rearrange("p c h -> p (c h)")
        kb.op("dve", lambda e: e.tensor_tensor(out=dA_all[:, :].rearrange("p (c h) -> p c h", h=16), in0=g.dt_tok[:, :, :],
                                               in1=a_bc[:, :].unsqueeze(1).broadcast_to([128, NT, 16]), op=ALU.mult),
              R=[g.dt_tok, a_bc], W=[dA_all])
        kb.op("dve", lambda e: e.tensor_copy(out=dAs_all[:, 0, :], in_=dA_all[:]), R=[dA_all], W=[dAs_all])
        kb.op("dve", lambda e: e.tensor_tensor(out=dAs_all[:, 1, :], in0=dA_all[:], in1=dAs_all[:, 0, :], op=ALU.subtract),
              R=[dA_all, dAs_all], W=[dAs_all])
        pac = kb.ps(pin=True)
        for c in range(NT):
            for half, lhs in ((0, g.triub), (1, g.onesb)):
                for hl_ in range(2):
                    kb.op("pe", lambda e, c=c, half=half, lhs=lhs, hl_=hl_: e.matmul(
                        pac[:, half * 256 + c * 16:half * 256 + (c + 1) * 16], lhs[:], dAs_all[:, hl_, c * 16:(c + 1) * 16],
                        start=(hl_ == 0), stop=(hl_ == 1)), R=[lhs, dAs_all], W=[pac])
        kb.unpin(pac)
        kb.op("dve", lambda e: e.tensor_copy(out=ac_all[:], in_=pac[:, 0:256]), R=[pac], W=[ac_all])
        kb.op("act", lambda e: e.activation(out=eac_all[:], in_=pac[:, 0:256], func=AF.Exp), R=[pac], W=[eac_all])
        kb.op("act", lambda e: e.activation(out=cd_all[:], in_=pac[:, 256:512], func=AF.Exp), R=[pac], W=[cd_all])
        kb.op("dve", lambda e: e.tensor_tensor(out=wgt_all[:], in0=pac[:, 256:512], in1=ac_all[:], op=ALU.subtract), R=[pac, ac_all], W=[wgt_all])
        kb.op("act", lambda e: e.activation(out=wgt_all[:], in_=wgt_all[:], func=AF.Exp), R=[wgt_all], W=[wgt_all])
        kb.op("dve", lambda e: e.tensor_tensor(out=wgt_all[:], in0=wgt_all[:], in1=dtf, op=ALU.mult), R=[wgt_all, g.dt_tok], W=[wgt_all])
        kb.op("act", lambda e: e.activation(out=v_all[:], in_=dtf, func=AF.Ln), R=[g.dt_tok], W=[v_all])
        kb.op("dve", lambda e: e.tensor_tensor(out=v_all[:], in0=v_all[:], in1=ac_all[:], op=ALU.subtract), R=[v_all, ac_all], W=[v_all])
        kb.op("dve", lambda e: e.tensor_copy(out=vs_all[:, 0, :], in_=v_all[:]), R=[v_all], W=[vs_all])
        kb.op("dve", lambda e: e.tensor_tensor(out=vs_all[:, 1, :], in0=v_all[:], in1=vs_all[:, 0, :], op=ALU.subtract), R=[v_all, vs_all], W=[vs_all])
        kb.op("dve", lambda e: e.tensor_copy(out=acs_all[:, 0, :], in_=ac_all[:]), R=[ac_all], W=[acs_all])
        kb.op("dve", lambda e: e.tensor_tensor(out=acs_all[:, 1, :], in0=ac_all[:], in1=acs_all[:, 0, :], op=ALU.subtract), R=[ac_all, acs_all], W=[acs_all])

        pcbs = {}

        def P1(c):
            k = c % NB
            tsl = slice(c * 128, (c + 1) * 128)
            c16 = slice(c * 16, (c + 1) * 16)
            kb.op("dve", lambda e: e.tensor_tensor(
                out=xw[k][:, :].rearrange("p (h d) -> p h d", d=64), in0=g.xs_tok[:, c, :].rearrange("p (h d) -> p h d", d=64),
                in1=wgt_all[:, c16].unsqueeze(2).broadcast_to([128, 16, 64]), op=ALU.mult), R=[g.xs_tok, wgt_all], W=[xw[k]])
            for hl_ in range(2):
                kb.op("dve", lambda e, hl_=hl_: e.tensor_tensor(
                    out=X[k][:, hl_, :, :], in0=g.identb[:, :].unsqueeze(1).broadcast_to([128, 16, 128]),
                    in1=acs_all[:, hl_, c16].unsqueeze(2).broadcast_to([128, 16, 128]), op=ALU.mult), R=[g.identb, acs_all], Wd=[X[k]])
            for q4 in range(4):
                pb = kb.ps()
                h4 = slice(c * 16 + q4 * 4, c * 16 + q4 * 4 + 4)
                for hl_ in range(2):
                    kb.op("pe", lambda e, hl_=hl_: e.matmul(pb[:, :], g.onesb[:], X[k][:, hl_, q4 * 4:(q4 + 1) * 4, :],
                                                            start=(hl_ == 0), stop=False), R=[g.onesb, X[k]], W=[pb])
                for hl_ in range(2):
                    kb.op("pe", lambda e, hl_=hl_: e.matmul(pb[:, :], g.identb[:], vs_all[:, hl_, h4].unsqueeze(2).broadcast_to([128, 4, 128]),
                                                            start=False, stop=False), R=[g.identb, vs_all], W=[pb])
                kb.op("pe", lambda e: e.matmul(pb[:, :], g.identb[:], g.negmb[:, :].unsqueeze(1).broadcast_to([128, 4, 128]),
                                               start=False, stop=True), R=[g.identb, g.negmb], W=[pb])
                kb.op("act", lambda e, q4=q4, pb=pb: e.activation(out=dd[k][:, q4 * 4:(q4 + 1) * 4, :],
                                                                  in_=pb[:, :].rearrange("p (h i) -> p h i", h=4), func=AF.Exp),
                      R=[pb], Wd=[dd[k]])
            pcb = kb.ps(pin=True)
            for gg in range(2):
                kb.op("pe", lambda e, gg=gg: e.matmul(pcb[:, gg * 128:(gg + 1) * 128], g.bcT[:, gg, tsl], g.bcT[:, 2 + gg, tsl],
                                                      start=True, stop=True), R=[g.bcT], W=[pcb])
            pcbs[c] = pcb

        def P1b(c):
            k = c % NB
            pcb = pcbs.pop(c)
            kb.unpin(pcb)
            for gg in range(2):
                kb.op("dve", lambda e, gg=gg: e.tensor_tensor(
                    out=Mt[k][:, gg * 8:(gg + 1) * 8, :], in0=dd[k][:, gg * 8:(gg + 1) * 8, :],
                    in1=pcb[:, gg * 128:(gg + 1) * 128].unsqueeze(1).broadcast_to([128, 8, 128]), op=ALU.mult),
                    R=[dd[k], pcb], Wd=[Mt[k]])

        pend2 = {}

        def P2a(c):
            k = c % NB
            tsl = slice(c * 128, (c + 1) * 128)
            kb.dma("sp", zst[k][:], S["zs"].t[tsl, :], W=[zst[k]])
            rec = []
            for gg in range(2):
                pA = kb.ps()
                for hl in range(8):
                    h = gg * 8 + hl
                    xsl = g.xs_tok[:, c, h * 64:(h + 1) * 64]
                    kb.op("pe", lambda e, h=h, hl=hl, xsl=xsl: e.matmul(pA[:, hl * 64:(hl + 1) * 64], Mt[k][:, h, :], xsl, start=True, stop=False),
                          R=[Mt[k], g.xs_tok], W=[pA])
                    kb.op("pe", lambda e, h=h, hl=hl, xsl=xsl: e.matmul(pA[:, hl * 64:(hl + 1) * 64], Dmat[:, h, :], xsl, start=False, stop=True),
                          R=[Dmat, g.xs_tok], W=[pA])
                pB = pS = None
                if c > 0:
                    pB = kb.ps()
                    pv = prevb[gg][(c - 1) % 2]
                    kb.op("pe", lambda e, pB=pB, pv=pv, gg=gg: e.matmul(pB[:, :], g.bcT[:, 2 + gg, tsl], pv[:], start=True, stop=True),
                          R=[g.bcT, pv], W=[pB])
                if c < NT - 1:
                    pS = kb.ps()
                    kb.op("pe", lambda e, pS=pS, gg=gg: e.matmul(pS[:, :], g.bt_tok[:, c, gg * 128:(gg + 1) * 128], xw[k][:, gg * 512:(gg + 1) * 512],
                                                                start=True, stop=True), R=[g.bt_tok, xw[k]], W=[pS])
                rec.append((pA, pB, pS))
            pend2[c] = rec

        def P2b(c):
            k = c % NB
            rec = pend2.pop(c)
            for gg in range(2):
                pA, pB, pS = rec[gg]
                ysl = yy[k][:, gg * 512:(gg + 1) * 512]
                if c > 0:
                    yt = ytmp[gg]
                    kb.op("dve", lambda e: e.tensor_tensor(
                        out=yt[:, :].rearrange("p (h d) -> p h d", d=64), in0=pB[:, :].rearrange("p (h d) -> p h d", d=64),
                        in1=eac_all[:, c * 16 + gg * 8:c * 16 + (gg + 1) * 8].unsqueeze(2).broadcast_to([128, 8, 64]), op=ALU.mult),
                        R=[pB, eac_all], W=[yt])
                    kb.op("dve", lambda e: e.tensor_tensor(out=ysl, in0=yt[:], in1=pA[:, :], op=ALU.add), R=[yt, pA], W=[yy[k]])
                else:
                    kb.op("act", lambda e: e.activation(out=ysl, in_=pA[:, :], func=AF.Copy), R=[pA], W=[yy[k]])
                if c < NT - 1:
                    if c == 0:
                        kb.op("dve", lambda e: e.tensor_copy(out=hst[gg][:], in_=pS[:, :]), R=[pS], W=[hst[gg]])
                    else:
                        kb.op("dve", lambda e: e.tensor_tensor(
                            out=htmp[gg][:, :].rearrange("p (h d) -> p h d", d=64), in0=hst[gg][:, :].rearrange("p (h d) -> p h d", d=64),
                            in1=cd_all[:, c * 16 + gg * 8:c * 16 + (gg + 1) * 8].unsqueeze(2).broadcast_to([128, 8, 64]), op=ALU.mult),
                            R=[hst[gg], cd_all], W=[htmp[gg]])
                        kb.op("dve", lambda e: e.tensor_tensor(out=hst[gg][:], in0=htmp[gg][:], in1=pS[:, :], op=ALU.add),
                              R=[htmp[gg], pS], W=[hst[gg]])
                    pn = prevb[gg][c % 2]
                    kb.op("act", lambda e: e.activation(out=pn[:], in_=hst[gg][:], func=AF.Copy), R=[hst[gg]], W=[pn])

        def P3(c):
            k = c % NB
            tsl = slice(c * 128, (c + 1) * 128)
            kb.op("dve", lambda e: e.tensor_tensor(out=yz[k][:], in0=yy[k][:], in1=zst[k][:], op=ALU.mult), R=[yy[k], zst[k]], W=[yz[k]])
            kb.op("act", lambda e: e.activation(out=yn[k][:], in_=yz[k][:], func=AF.Square, accum_out=ss[k][:]), R=[yz[k]], W=[yn[k], ss[k]])
            kb.op("act", lambda e: e.activation(out=ss[k][:], in_=ss[k][:], func=AF.Ln, scale=1.0 / 1024, bias=epsc[:, 0:1]), R=[ss[k], epsc], W=[ss[k]])
            kb.op("act", lambda e: e.activation(out=ss[k][:], in_=ss[k][:], func=AF.Exp, scale=-0.5), R=[ss[k]], W=[ss[k]])
            kb.op("dve", lambda e: e.scalar_tensor_tensor(out=yn[k][:], in0=yz[k][:], scalar=ss[k][:, 0:1], in1=nw_bc[:],
                                                          op0=ALU.mult, op1=ALU.mult), R=[yz[k], ss[k], nw_bc], W=[yn[k]])
            pt = kb.ps()
            ptb = pt[:, :].bitcast(BF16)
            for kc in range(KC):
                kb.op("pe", lambda e, kc=kc: e.transpose(ptb[:, kc * 128:(kc + 1) * 128], yn[k][:, kc * 128:(kc + 1) * 128], g.identb[:]),
                      R=[yn[k], g.identb], W=[pt])
            st = yst[(c // 2) % 2]
            kb.op("act", lambda e: e.activation(out=st[:, :, (c % 2) * 128:(c % 2 + 1) * 128],
                                                in_=ptb.rearrange("p (kc t) -> p kc t", kc=KC), func=AF.Copy), R=[pt], Wd=[st])
            if c % 2 == 1:
                t2 = c // 2
                kb.dma("sp", S["yT"].t[0:8, :, t2 * 256:(t2 + 1) * 256].rearrange("kc p t -> p kc t"), st[:], R=[st])

        for step in range(-1, NT + 1):
            if 0 <= step + 1 < NT:
                P1(step + 1)
            if 0 <= step < NT:
                P2a(step)
            if 0 <= step - 1 < NT:
                P3(step - 1)
            if 0 <= step + 1 < NT:
                P1b(step + 1)
            if 0 <= step < NT:
                P2b(step)
        kb.barrier()


def phase_sb(g, l):
    kb, nc = g.kb, g.nc
    S = g.scr
    with ExitStack() as ph:
        def sb(name, shape, dt_=F32):
            return kb.sb(ph, name, shape, dt_)
        qT = sb("sbq", [128, 4, L], BF16)
        kT = sb("sbk", [128, 4, L], BF16)
        v = sb("sbv", [128, NT, 512], BF16)
        onesw = sb("onesw", [128, L], BF16)
        kb.op("dve", lambda e: e.memset(onesw[:], 1.0), W=[onesw])
        kb.dma("sp", qT[:], S["sbqT"].t.rearrange("c p t -> p c t"), W=[qT])
        kb.dma("sp", kT[:], S["sbkT"].t.rearrange("c p t -> p c t"), W=[kT])
        kb.dma("sp", v[:], S["sbv"].t.rearrange("(t p) f -> p t f", p=128), W=[v])
        e1b = [sb("e1", [128, L]) for _ in range(3)]
        spb = [sb("sp", [128, L]) for _ in range(2)]
        Fb = [sb("F", [128, L + 1]) for _ in range(2)]
        attb = [sb("att", [128, L], BF16) for _ in range(2)]
        attT = [sb("attT", [128, NT, 128], BF16) for _ in range(2)]
        ftn = [sb("ftn", [128, 1]) for _ in range(2)]
        yst = [sb("yst", [128, 128], BF16) for _ in range(4)]
        for F in Fb:
            kb.op("dve", lambda e, F=F: e.memset(F[:, 0:1], 0.0), W=[F])
        iters = [(qt, h) for qt in range(NT) for h in range(8)]

        def s1(i):
            qt, h = iters[i]
            W = 128 * (qt + 1)
            nch = (W + 511) // 512
            dsl = slice(qt * 128, (qt + 1) * 128)
            hp, hc = (h % 2) * 64, h // 2
            e1, sp_ = e1b[i % 3], spb[i % 2]
            for ch in range(nch):
                n = min(512, W - ch * 512)
                p = kb.ps()
                kb.op("pe", lambda e, p=p, ch=ch, n=n: e.matmul(p[:, 0:n], qT[hp:hp + 64, hc, dsl], kT[hp:hp + 64, hc, ch * 512:ch * 512 + n],
                                                                start=True, stop=True), R=[qT, kT], W=[p])
                kb.op("act", lambda e, p=p, ch=ch, n=n: e.activation(out=e1[:, ch * 512:ch * 512 + n], in_=p[:, 0:n], func=AF.Exp, scale=0.125),
                      R=[p], Wd=[e1])
            kb.op("act", lambda e: e.activation(out=sp_[:, 0:W], in_=e1[:, 0:W], func=AF.Ln, bias=1.0), R=[e1], W=[sp_])

        def s2(i):
            qt, h = iters[i]
            k = i % 2
            W = 128 * (qt + 1)
            dsl = slice(qt * 128, (qt + 1) * 128)
            sp_, F = spb[k], Fb[k]
            kb.op("dve", lambda e: e.tensor_tensor(out=sp_[:, dsl], in0=sp_[:, dsl], in1=g.strl[:], op=ALU.mult), R=[sp_, g.strl], W=[sp_])
            kb.op("dve", lambda e: e.memset(F[:, 0:1], 0.0), W=[F])
            kb.op("dve", lambda e: e.tensor_tensor_scan(out=F[:, 1:W + 1], data0=onesw[:, 0:W], data1=sp_[:, 0:W], initial=0.0,
                                                        op0=ALU.mult, op1=ALU.add), R=[onesw, sp_], W=[F])
            kb.op("dve", lambda e: e.tensor_scalar(out=ftn[k][:], in0=F[:, W:W + 1], scalar1=-1.0, scalar2=None, op0=ALU.mult),
                  R=[F], W=[ftn[k]])

        def s3a(i):
            qt, h = iters[i]
            k = i % 2
            W = 128 * (qt + 1)
            F = Fb[k]
            kb.op("act", lambda e: e.activation(out=F[:, 0:W], in_=F[:, 0:W], func=AF.Exp, bias=ftn[k][:, 0:1]), R=[F, ftn[k]], W=[F])

        def s3b(i):
            qt, h = iters[i]
            k = i % 2
            W = 128 * (qt + 1)
            dsl = slice(qt * 128, (qt + 1) * 128)
            e1, F, att = e1b[i % 3], Fb[k], attb[k]
            kb.op("dve", lambda e: e.tensor_tensor(out=att[:, 0:W], in0=e1[:, 0:W], in1=F[:, 0:W], op=ALU.mult), R=[e1, F], W=[att])
            kb.op("dve", lambda e: e.tensor_tensor(out=att[:, dsl], in0=att[:, dsl], in1=g.strlb[:], op=ALU.mult), R=[att, g.strlb], W=[att])

        def s3c(i):
            qt, h = iters[i]
            k = i % 2
            att, aT = attb[k], attT[k]
            for b0 in range(0, qt + 1, 8):
                nb = min(8, qt + 1 - b0)
                pt = kb.ps()
                ptb = pt[:, :].bitcast(BF16)
                for j in range(nb):
                    kb.op("pe", lambda e, j=j, b0=b0: e.transpose(ptb[:, j * 128:(j + 1) * 128], att[:, (b0 + j) * 128:(b0 + j + 1) * 128], g.identb[:]),
                          R=[att, g.identb], W=[pt])
                kb.op("act", lambda e, b0=b0, nb=nb: e.activation(out=aT[:, b0:b0 + nb, :], in_=ptb[:, 0:nb * 128].rearrange("p (j t) -> p j t", j=nb),
                                                                 func=AF.Copy), R=[pt], Wd=[aT])

        def s3d(i):
            qt, h = iters[i]
            k = i % 2
            dsl = slice(qt * 128, (qt + 1) * 128)
            hp, hc = (h % 2) * 64, h // 2
            aT = attT[k]
            py = kb.ps()
            for sbk in range(qt + 1):
                kb.op("pe", lambda e, sbk=sbk: e.matmul(py[0:64, 0:128], v[:, sbk, h * 64:(h + 1) * 64], aT[:, sbk, :],
                                                        start=(sbk == 0), stop=(sbk == qt)), R=[v, aT], W=[py])
            pys[i] = py

        def s3e(i):
            qt, h = iters[i]
            dsl = slice(qt * 128, (qt + 1) * 128)
            hp, hc = (h % 2) * 64, h // 2
            py = pys.pop(i)
            ys = yst[i % 4]
            kb.op("dve", lambda e: e.tensor_copy(out=ys[hp:hp + 64, :], in_=py[0:64, 0:128]), R=[py], W=[ys])
            kb.dma("sp", S["yT"].t[8 + hc, hp:hp + 64, dsl], ys[hp:hp + 64, :], R=[ys])

        n_it = len(iters)

        def run(fn, i):
            if 0 <= i < n_it:
                fn(i)
        pys = {}
        for step in range(n_it + 4):
            run(s3d, step - 4)
            run(s3a, step - 2)
            run(s1, step)
            run(s3c, step - 3)
            run(s2, step - 1)
            run(s3e, step - 4)
            run(s3b, step - 2)
        kb.barrier()


NBIS = 16


def phase_dsa(g, l):
    kb, nc = g.kb, g.nc
    S = g.scr
    with ExitStack() as ph:
        def sb(name, shape, dt_=F32):
            return kb.sb(ph, name, shape, dt_)
        dk = sb("dk", [64, 2, L], BF16)
        ik = sb("ik", [64, L], BF16)
        vaug = sb("vaug", [128, NT, 2, 128], BF16)
        kb.dma("sp", dk[:], S["dkT"].t.rearrange("h d t -> d h t"), W=[dk])
        kb.dma("sp", ik[:], S["ikT"].t[0], W=[ik])
        kb.op("dve", lambda e: e.memset(vaug[:], 1.0), W=[vaug])
        for gg in range(2):
            kb.dma("sp", vaug[:, :, gg, 0:64], S["dv"].t[:, gg * 64:(gg + 1) * 64].rearrange("(t p) d -> p t d", p=128), W=[vaug])
        dqt = [sb("dqt", [64, 8, 128], BF16) for _ in range(2)]
        iqt = [sb("iqt", [64, 8, 128], BF16) for _ in range(2)]
        score = [sb("score", [128, L]) for _ in range(2)]
        rl = [sb("rl", [128, 512], BF16) for _ in range(4)]
        wabs = [sb("wabs", [128, 8]) for _ in range(2)]
        wsgn = [sb("wsgn", [128, 8]) for _ in range(2)]
        Dg = [sb("Dg", [128, 8, 128], BF16) for _ in range(2)]
        junk = sb("junk", [128, L], BF16)
        maskb = [sb("maskb", [128, L], BF16) for _ in range(2)]
        maskT = [sb("maskT", [128, NT, 128], BF16) for _ in range(2)]
        Eb = [sb("E", [128, 512], BF16) for _ in range(3)]
        M_ = [sb("M", [128, 1]) for _ in range(2)]
        A_ = [sb("A", [128, NBIS + 1]) for _ in range(2)]
        mid = [sb("mid", [128, 1]) for _ in range(2)]
        cnt = [sb("cnt", [128, 1]) for _ in range(2)]
        sela = [sb("sela", [128, 1]) for _ in range(2)]
        R0 = [sb("R0", [64, 512]) for _ in range(2)]
        ytmp = [sb("ytmp", [64, 512], BF16) for _ in range(2)]
        yraw = [sb("yraw", [64, 512]) for _ in range(2)]
        yst = [sb("dyst", [128, 4, 128], BF16) for _ in range(2)]
        cn = dict(rl=0, E=0)

        def stage_I(qt):
            k = qt % 2
            W = 128 * (qt + 1)
            dsl = slice(qt * 128, (qt + 1) * 128)
            if qt < 2:
                return
            kb.dma("sp", iqt[k][:], S["iqT"].t[:, :, dsl].rearrange("h d t -> d h t"), W=[iqt[k]])
            sc = score[k]
            nch = (W + 511) // 512
            wv = g.wix_tok[:, qt, :]
            kb.op("act", lambda e: e.activation(out=wabs[k][:], in_=wv, func=AF.Abs), R=[g.wix_tok], W=[wabs[k]])
            kb.op("dve", lambda e: e.tensor_scalar(out=wsgn[k][:], in0=wv, scalar1=0.0, scalar2=2.0, op0=ALU.is_ge, op1=ALU.mult),
                  R=[g.wix_tok], W=[wsgn[k]])
            kb.op("dve", lambda e: e.tensor_scalar(out=wsgn[k][:], in0=wsgn[k][:], scalar1=-1.0, scalar2=None, op0=ALU.add),
                  R=[wsgn[k]], W=[wsgn[k]])
            for h in range(8):
                kb.op("dve", lambda e, h=h: e.tensor_scalar(out=Dg[k][:, h, :], in0=g.identb[:], scalar1=wsgn[k][:, h:h + 1], scalar2=None,
                                                            op0=ALU.mult), R=[g.identb, wsgn[k]], Wd=[Dg[k]])
            items = [(ch, h) for ch in range(nch) for h in range(8)]
            pend = {}
            pscs = {}

            def mm(j):
                ch, h = items[j]
                n = min(512, W - ch * 512)
                p = kb.ps()
                kb.op("pe", lambda e: e.matmul(p[:, 0:n], iqt[k][0:64, h, :], ik[0:64, ch * 512:ch * 512 + n], start=True, stop=True),
                      R=[iqt[k], ik], W=[p])
                r = rl[cn["rl"] % 4]
                cn["rl"] += 1
                kb.op("act", lambda e: e.activation(out=r[:, 0:n], in_=p[:, 0:n], func=AF.Relu, scale=wabs[k][:, h:h + 1]),
                      R=[p, wabs[k]], W=[r])
                pend[j] = r

            def acc(j):
                ch, h = items[j]
                n = min(512, W - ch * 512)
                r = pend.pop(j)
                if h == 0:
                    pscs[ch] = kb.ps(pin=True)
                psc = pscs[ch]
                kb.op("pe", lambda e: e.matmul(psc[:, 0:n], Dg[k][:, h, :], r[:, 0:n], start=(h == 0), stop=(h == 7)),
                      R=[Dg[k], r], W=[psc])
                if h == 7:
                    kb.unpin(psc)
                    kb.op("act", lambda e: e.activation(out=sc[:, ch * 512:ch * 512 + n], in_=psc[:, 0:n], func=AF.Copy), R=[psc], Wd=[sc])

            for j in range(len(items) + 2):
                if j < len(items):
                    mm(j)
                if 0 <= j - 2 < len(items):
                    acc(j - 2)

        def stage_B(qt):
            k = qt % 2
            W = 128 * (qt + 1)
            dsl = slice(qt * 128, (qt + 1) * 128)
            mT = maskT[k]
            if qt < 2:
                if qt == 1:
                    kb.op("dve", lambda e: e.memset(mT[:, 0, :], 0.0), W=[mT])
                kb.op("dve", lambda e: e.tensor_copy(out=mT[:, qt, :], in_=g.negmb[:]), R=[g.negmb], W=[mT])
                return
            sc = score[k]
            kb.op("dve", lambda e: e.tensor_reduce(out=M_[k][:], in_=sc[:, 0:W], axis=AX.X, op=ALU.max, apply_absolute_value=True),
                  R=[sc], W=[M_[k]])
            kb.op("dve", lambda e: e.tensor_scalar(out=A_[k][:, 0:NBIS], in0=g.pow2[:, 0:NBIS], scalar1=M_[k][:, 0:1], scalar2=None, op0=ALU.mult),
                  R=[g.pow2, M_[k]], W=[A_[k]])
            kb.op("dve", lambda e: e.tensor_copy(out=A_[k][:, NBIS:NBIS + 1], in_=A_[k][:, NBIS - 1:NBIS]), R=[A_[k]], W=[A_[k]])
            kb.op("dve", lambda e: e.tensor_tensor(out=sc[:, dsl], in0=sc[:, dsl], in1=g.negup[:], op=ALU.add), R=[sc, g.negup], W=[sc])
            kb.op("dve", lambda e: e.memset(mid[k][:], 0.0), W=[mid[k]])
            for it in range(NBIS):
                kb.op("dve", lambda e: e.tensor_scalar(out=junk[:, 0:W], in0=sc[:, 0:W], scalar1=mid[k][:, 0:1], scalar2=0.0,
                                                       op0=ALU.is_ge, op1=ALU.add, accum_out=cnt[k][:]),
                      R=[sc, mid[k]], W=[junk, cnt[k]])
                kb.op("dve", lambda e, it=it: e.tensor_scalar(out=sela[k][:], in0=cnt[k][:], scalar1=255.5, scalar2=A_[k][:, it:it + 1],
                                                              op0=ALU.is_ge, op1=ALU.mult), R=[cnt[k], A_[k]], W=[sela[k]])
                kb.op("dve", lambda e, it=it: e.scalar_tensor_tensor(out=mid[k][:], in0=sela[k][:], scalar=A_[k][:, it + 1:it + 2], in1=mid[k][:],
                                                                     op0=ALU.subtract, op1=ALU.add), R=[sela[k], A_[k], mid[k]], W=[mid[k]])
            mb = maskb[k]
            kb.op("dve", lambda e: e.tensor_scalar(out=mb[:, 0:W], in0=sc[:, 0:W], scalar1=mid[k][:, 0:1], scalar2=-30000.0,
                                                   op0=ALU.is_lt, op1=ALU.mult), R=[sc, mid[k]], W=[mb])

        def stage_T(qt):
            if qt < 2:
                return
            k = qt % 2
            mb, mT = maskb[k], maskT[k]
            for b0 in range(0, qt + 1, 8):
                nb = min(8, qt + 1 - b0)
                pt = kb.ps()
                ptb = pt[:, :].bitcast(BF16)
                for j in range(nb):
                    kb.op("pe", lambda e, j=j, b0=b0: e.transpose(ptb[:, j * 128:(j + 1) * 128], mb[:, (b0 + j) * 128:(b0 + j + 1) * 128], g.identb[:]),
                          R=[mb, g.identb], W=[pt])
                kb.op("act", lambda e, b0=b0, nb=nb: e.activation(out=mT[:, b0:b0 + nb, :], in_=ptb[:, 0:nb * 128].rearrange("p (j t) -> p j t", j=nb),
                                                                 func=AF.Copy), R=[pt], Wd=[mT])

        def stage_A(qt):
            k = qt % 2
            dsl = slice(qt * 128, (qt + 1) * 128)
            mT = maskT[k]
            ys = yst[k]
            if qt + 1 < NT:
                kb.dma("sp", dqt[1 - k][:], S["dqT"].t[:, :, (qt + 1) * 128:(qt + 2) * 128].rearrange("h d t -> d h t"), W=[dqt[1 - k]])
            for gq in range(2):
                pO = kb.ps(pin=True)
                Es = {}

                def qk(sbk):
                    pS = kb.ps()
                    kb.op("pe", lambda e: e.matmul(pS[:, :], dk[0:64, gq, sbk * 128:(sbk + 1) * 128], dqt[k][0:64, 4 * gq:4 * gq + 4, :],
                                                   start=True, stop=False), R=[dk, dqt[k]], W=[pS])
                    kb.op("pe", lambda e: e.matmul(pS[:, :], g.identb[:], mT[:, sbk, :].unsqueeze(1).broadcast_to([128, 4, 128]),
                                                   start=False, stop=True), R=[g.identb, mT], W=[pS])
                    E = Eb[cn["E"] % 3]
                    cn["E"] += 1
                    kb.op("act", lambda e: e.activation(out=E[:], in_=pS[:, :], func=AF.Exp, scale=0.125), R=[pS], W=[E])
                    Es[sbk] = E

                def av(sbk):
                    E = Es.pop(sbk)
                    kb.op("pe", lambda e: e.matmul(pO[:, :], vaug[:, sbk, gq, :], E[:], start=(sbk == 0), stop=(sbk == qt)),
                          R=[vaug, E], W=[pO])

                for sbk in range(qt + 2):
                    if sbk <= qt:
                        qk(sbk)
                    if sbk >= 1:
                        av(sbk - 1)
                def norm(pO=pO, gq=gq, ys=ys, dsl=dsl, last=(gq == 1)):
                    kb.unpin(pO)
                    r0, yt = R0[gq], ytmp[gq]
                    kb.op("act", lambda e: e.activation(out=r0[:], in_=pO[64:128, :], func=AF.Ln), R=[pO], W=[r0])
                    kb.op("act", lambda e: e.activation(out=r0[:], in_=r0[:], func=AF.Exp, scale=-1.0), R=[r0], W=[r0])
                    yr = yraw[gq]
                    kb.op("act", lambda e: e.activation(out=yr[:], in_=pO[0:64, :], func=AF.Copy), R=[pO], W=[yr])
                    kb.op("pool", lambda e: e.tensor_tensor(out=yt[:], in0=yr[:], in1=r0[:], op=ALU.mult), R=[yr, r0], W=[yt])
                    for hh in range(4):
                        h = 4 * gq + hh
                        hp, hc = (h % 2) * 64, h // 2
                        kb.op("act", lambda e, hh=hh, hp=hp, hc=hc: e.activation(out=ys[hp:hp + 64, hc, :], in_=yt[:, hh * 128:(hh + 1) * 128], func=AF.Copy),
                              R=[yt], Wd=[ys])
                    if last:
                        kb.dma("sp", S["yT"].t[12:16, :, dsl].rearrange("c p t -> p c t"), ys[:], R=[ys])
                if pending:
                    pending.pop(0)()
                pending.append(norm)

        pending = []
        kb.dma("sp", dqt[0][:], S["dqT"].t[:, :, 0:128].rearrange("h d t -> d h t"), W=[dqt[0]])
        stage_I(0)
        stage_B(0)
        stage_T(0)
        stage_I(1)
        modw = [sb("adaw", [128, KC, 512], BF16) for _ in range(2)] if g.dsa_hook is not None else None
        for qt in range(NT):
            if qt + 2 < NT:
                stage_I(qt + 2)
            stage_A(qt)
            if modw is not None and 1 <= qt < 13:
                mod_piece_load(g, g.dsa_hook, qt - 1, modw[(qt - 1) % 2])
            if modw is not None and 2 <= qt < 14:
                mod_piece_mm(g, g.dsa_hook, qt - 2, modw[(qt - 2) % 2])
            if qt + 1 < NT:
                stage_B(qt + 1)
                stage_T(qt + 1)
        while pending:
            pending.pop(0)()
        kb.barrier()


def phase_merge(g, l):
    kb, nc = g.kb, g.nc
    S = g.scr
    with ExitStack() as ph:
        def sb(name, shape, dt_=F32):
            return kb.sb(ph, name, shape, dt_)
        yT = sb("yTall", [128, 16, L], BF16)
        yTb = [kb.buf() for _ in range(4)]
        for q in range(4):
            kb.dma("sp", yT[:, q * 4:(q + 1) * 4, :], S["yT"].t[q * 4:(q + 1) * 4].rearrange("c p t -> p c t"), W=[yTb[q]])
        mT = sb("mergedT", [128, KC, L], BF16)
        mTb = [[kb.buf() for tb in range(4)] for c in range(KC)]
        wbr = [sb("wbr", [128, 16, 256], BF16) for _ in range(2)]
        gt = [sb("gt", [128, 3, 512], BF16) for _ in range(2)]
        gbr = [sb("gbr", [128, 3, 512], BF16) for _ in range(2)]
        it = 0
        for c2 in range(4):
            w = wbr[c2 % 2]
            csl = slice(c2 * 256, (c2 + 1) * 256)
            kb.dma("pool", w[:, 0:8, :], g.inp["w_br_ssd"].t[l][:, csl].rearrange("(kc p) n -> p kc n", p=128), W=[w])
            kb.dma("pool", w[:, 8:12, :], g.inp["w_br_sb"].t[l][:, csl].rearrange("(kc p) n -> p kc n", p=128), W=[w])
            kb.dma("pool", w[:, 12:16, :], g.inp["w_br_dsa"].t[l][:, csl].rearrange("(kc p) n -> p kc n", p=128), W=[w])
            for cl in range(2):
                c = c2 * 2 + cl
                for tb in range(4):
                    k = it % 2
                    it += 1
                    tsl = slice(tb * 512, (tb + 1) * 512)
                    for br in range(3):
                        kb.dma("sp", gt[k][:, br, :], S["gT"].t[br * 8 + c, :, tsl], W=[gt[k]])
                    pbr = []
                    for br, (k0, k1) in enumerate(((0, 8), (8, 12), (12, 16))):
                        p = kb.ps()
                        for kc in range(k0, k1):
                            kb.op("pe", lambda e, p=p, kc=kc, k0=k0, k1=k1: e.matmul(
                                p[:, :], w[:, kc, cl * 128:(cl + 1) * 128], yT[:, kc, tsl], start=(kc == k0), stop=(kc == k1 - 1)),
                                R=[w, yTb[kc // 4]], W=[p])
                        pbr.append(p)
                    gb = gbr[k]
                    for br in range(3):
                        kb.op("dve", lambda e, br=br: e.tensor_tensor(out=gb[:, br, :], in0=pbr[br][:, :], in1=gt[k][:, br, :], op=ALU.mult),
                              R=[pbr[br], gt[k]], Wd=[gb])
                    pm = kb.ps()
                    for br in range(3):
                        kb.op("pe", lambda e, br=br: e.matmul(pm[:, :], g.identb[:], gb[:, br, :], start=(br == 0), stop=(br == 2)),
                              R=[g.identb, gb], W=[pm])
                    kb.op("act", lambda e: e.activation(out=mT[:, c, tsl], in_=pm[:, :], func=AF.Copy), R=[pm], W=[mTb[c][tb]])
        wo = [sb("wo", [128, KC, 256], BF16) for _ in range(2)]
        for c2 in range(4):
            w = wo[c2 % 2]
            kb.dma("pool", w[:], g.inp["w_out"].t[l][:, c2 * 256:(c2 + 1) * 256].rearrange("(kc p) n -> p kc n", p=128), W=[w])
            for cl in range(2):
                c = c2 * 2 + cl
                for tb in range(4):
                    tsl = slice(tb * 512, (tb + 1) * 512)
                    p = kb.ps()
                    for kc in range(KC):
                        kb.op("pe", lambda e, p=p, kc=kc: e.matmul(p[:, :], w[:, kc, cl * 128:(cl + 1) * 128], mT[:, kc, tsl],
                                                                   start=(kc == 0), stop=(kc == KC - 1)), R=[w, mTb[kc][tb]], W=[p])
                    xs = g.xT[:, c, tsl]
                    kb.op("dve", lambda e, p=p, c=c, xs=xs: e.scalar_tensor_tensor(
                        out=xs, in0=p[:, :], scalar=g.modT[:, 16 + c:17 + c], in1=xs, op0=ALU.mult, op1=ALU.add),
                        R=[p, g.modT, g.xTb[c][tb]], W=[g.xTb[c][tb]])
        kb.barrier()


def phase_mlp(g, l):
    kb = g.kb
    with ExitStack() as ph:
        wup = [kb.sb(ph, f"wup{i}", [128, KC, 512], BF16) for i in range(2)]
        wdn = [kb.sb(ph, f"wdn{i}", [128, 4, D], BF16) for i in range(2)]
        act = [kb.sb(ph, f"mact{i}", [128, 4, L], BF16) for i in range(2)]
        actb = [[[kb.buf() for tb in range(4)] for hc in range(4)] for i in range(2)]
        rl = [kb.sb(ph, f"mrl{i}", [128, 512], F32) for i in range(2)]
        nrl = 0
        for gi in range(8):
            wu, wd, a, ab = wup[gi % 2], wdn[gi % 2], act[gi % 2], actb[gi % 2]
            load_w(g, wu, g.inp["w_up"].t[l][:, gi * 512:(gi + 1) * 512])
            load_w(g, wd, g.inp["w_down"].t[l][gi * 512:(gi + 1) * 512, :])
            for tb in range(4):
                for hc in range(4):
                    p = kb.ps()
                    for kc in range(KC):
                        kb.op("pe", lambda e, p=p, kc=kc, hc=hc: e.matmul(
                            p[:, :], wu[:, kc, hc * 128:(hc + 1) * 128], g.hT[:, kc, tb * 512:(tb + 1) * 512],
                            start=(kc == 0), stop=(kc == KC - 1)), R=[wu, g.hTb[kc][tb]], W=[p])
                    r = rl[nrl % 2]
                    nrl += 1
                    kb.op("act", lambda e, p=p, r=r: e.activation(out=r[:], in_=p[:, :], func=AF.Relu), R=[p], W=[r])
                    kb.op("act", lambda e, r=r, hc=hc: e.activation(out=a[:, hc, tb * 512:(tb + 1) * 512], in_=r[:], func=AF.Square),
                          R=[r], W=[ab[hc][tb]])
            import os
            if os.environ.get("MLP_UP_ONLY"):
                continue
            for tb in range(4):
                for c in range(KC):
                    p = kb.ps()
                    for hc in range(4):
                        kb.op("pe", lambda e, p=p, hc=hc, c=c: e.matmul(
                            p[:, :], wd[:, hc, c * 128:(c + 1) * 128], a[:, hc, tb * 512:(tb + 1) * 512],
                            start=(hc == 0), stop=(hc == 3)), R=[wd, ab[hc][tb]], W=[p])
                    xs = g.xT[:, c, tb * 512:(tb + 1) * 512]
                    kb.op("dve", lambda e, p=p, c=c, xs=xs: e.scalar_tensor_tensor(
                        out=xs, in0=p[:, :], scalar=g.modT[:, 40 + c:41 + c], in1=xs, op0=ALU.mult, op1=ALU.add),
                        R=[p, g.modT, g.xTb[c][tb]], W=[g.xTb[c][tb]])
        kb.barrier()


def rstd_bcast(kb, ph, xT, xTb, tb, onesb, sq, rs):
    p = kb.ps()
    for c in range(KC):
        s = sq[c % 2]
        kb.op("act", lambda e, s=s, c=c: e.activation(out=s[:], in_=xT[:, c, tb * 512:(tb + 1) * 512], func=AF.Square),
              R=[xTb[c][tb]], W=[s])
        kb.op("pe", lambda e, s=s, c=c, p=p: e.matmul(p[:, :], onesb[:], s[:], start=(c == 0), stop=(c == KC - 1)),
              R=[s, onesb], W=[p])
    kb.op("dve", lambda e: e.tensor_scalar(out=rs[:], in0=p[:, :], scalar1=1.0 / D, scalar2=EPS, op0=ALU.mult, op1=ALU.add),
          R=[p], W=[rs])
    kb.op("act", lambda e: e.activation(out=rs[:], in_=rs[:], func=AF.Ln), R=[rs], W=[rs])
    kb.op("act", lambda e: e.activation(out=rs[:], in_=rs[:], func=AF.Exp, scale=-0.5), R=[rs], W=[rs])


def final_norm(kb, nc, xT, xTb, fnw, onesb, ident, out_d):
    with ExitStack() as ph:
        sq = [kb.sb(ph, f"fsq{i}", [128, 512], BF16) for i in range(2)]
        rs = kb.sb(ph, "frs", [128, 512], F32)
        yT = [kb.sb(ph, f"fyT{i}", [128, 512], F32) for i in range(2)]
        ost = [kb.sb(ph, f"fost{i}", [128, 4, D], F32) for i in range(2)]
        for tb in range(4):
            rstd_bcast(kb, ph, xT, xTb, tb, onesb, sq, rs)
            o = ost[tb % 2]
            for c in range(KC):
                y = yT[c % 2]
                kb.op("dve", lambda e, y=y, c=c: e.scalar_tensor_tensor(
                    out=y[:], in0=xT[:, c, tb * 512:(tb + 1) * 512], scalar=fnw[:, c:c + 1], in1=rs[:],
                    op0=ALU.mult, op1=ALU.mult), R=[xTb[c][tb], fnw, rs], W=[y])
                p = kb.ps()
                for j in range(4):
                    kb.op("pe", lambda e, y=y, j=j, p=p: e.transpose(
                        p[:, j * 128:(j + 1) * 128], y[:, j * 128:(j + 1) * 128], ident[:]), R=[y, ident], W=[p])
                src = p[:, :].rearrange("p (j f) -> p j f", j=4)
                dst = o[:, :, c * 128:(c + 1) * 128]
                kb.op("act", lambda e, dst=dst, src=src: e.activation(out=dst, in_=src, func=AF.Copy), R=[p], Wd=[o])
            kb.dma("sp", out_d.t[tb * 512:(tb + 1) * 512, :].rearrange("(j p) d -> p j d", p=128), o[:], R=[o], W=[out_d])
        kb.barrier()


_NC_CACHE = {}


def make_in_maps(inputs):
    nb = inputs["x"].shape[0]
    inputs = dict(inputs)
    inputs.update(host_consts())
    shared = {n: np.ascontiguousarray(np.asarray(inputs[n], dtype=np.float32)) for n in IN_SHAPES if n not in ("x", "c")}
    in_maps = []
    for b in range(nb):
        m = dict(shared)
        m["x"] = np.ascontiguousarray(inputs["x"][b])
        m["c"] = np.ascontiguousarray(inputs["c"][b])
        in_maps.append(m)
    return in_maps


def kernel(**inputs):
    nb = inputs["x"].shape[0]
    if "nc" not in _NC_CACHE:
        _NC_CACHE["nc"] = build()[0]
    nc = _NC_CACHE["nc"]
    res = run_bass_kernel_spmd(nc, make_in_maps(inputs), core_ids=list(range(nb)))
    return np.stack([r["out"] for r in res.results], axis=0)
```

```python
import math
from contextlib import ExitStack

import numpy as np
import concourse.bass as bass
import concourse.mybir as mybir
from concourse.bass_utils import run_bass_kernel_spmd

F32 = mybir.dt.float32
BF16 = mybir.dt.bfloat16
I32 = mybir.dt.int32
AF = mybir.ActivationFunctionType
ALU = mybir.AluOpType
AX = mybir.AxisListType

D = 1024
L = 2048
DEPTH = 2
NT = L // 128
KC = D // 128
EPS = 1e-6
import os
NDS = int(os.environ.get("NDS", "12"))
STRICT = bool(int(os.environ.get("KSTRICT", "1")))


class Buf:
    __slots__ = ("name", "w", "r", "excl")

    def __init__(self, name):
        self.name = name
        self.excl = False
        self.w = None
        self.r = {}


class T:
    def __init__(self, t, buf):
        self.t = t
        self.b = buf

    def __getitem__(self, k):
        return self.t[k]


class KB:
    def __init__(self, nc):
        self.nc = nc
        self.es = ExitStack()
        self.sems = {}
        self.engs = {}
        for name, e in (("pe", nc.tensor), ("act", nc.scalar), ("dve", nc.vector),
                        ("pool", nc.gpsimd), ("sp", nc.sync)):
            key = "s_" + name
            self.sems[key] = self.es.enter_context(nc.semaphore(key))
            self.engs[name] = dict(e=e, key=key, cnt=0, seen={})
        self.dq = {}
        for q in ("sp", "pool"):
            keys = []
            for i in range(NDS):
                key = f"d_{q}{i}"
                self.sems[key] = self.es.enter_context(nc.semaphore(key))
                keys.append(key)
            self.dq[q] = dict(keys=keys, cnt=[0] * NDS, nxt=0)
        self.nbuf = 0
        self.psum = []
        self.ps_next = 0
        self.pinned = set()

    def buf(self, name=None):
        self.nbuf += 1
        return Buf(name or f"b{self.nbuf}")

    def sb(self, es, name, shape, dtype):
        self.nbuf += 1
        name = f"{name}_{self.nbuf}"
        t = es.enter_context(self.nc.sbuf_tensor(name, list(shape), dtype))
        return T(t, self.buf(name))

    def dram(self, name, shape, dtype, kind="Internal"):
        t = self.nc.dram_tensor(name, list(shape), dtype, kind=kind)
        return T(t.ap(), self.buf(name))

    def init_psum(self):
        for i in range(8):
            t = self.es.enter_context(self.nc.psum_tensor(f"ps{i}", [128, 512], F32))
            self.psum.append(T(t, self.buf(f"ps{i}")))
            self.psum[-1].b.excl = True

    def ps(self, pin=False):
        while True:
            i = self.ps_next
            self.ps_next = (self.ps_next + 1) % 8
            if i not in self.pinned:
                break
        if pin:
            self.pinned.add(i)
        return self.psum[i]

    def unpin(self, p):
        self.pinned.discard(self.psum.index(p))

    def _deps(self, own_key, R, W, same_raw, Wd=()):
        deps = {}

        def add(ev):
            if ev is None:
                return
            k, v = ev
            if deps.get(k, 0) < v:
                deps[k] = v

        for t in R:
            b = t.b if isinstance(t, T) else t
            if b.w is not None and (b.w[0] != own_key or same_raw):
                add(b.w)
            if b.excl:
                for k, v in b.r.items():
                    if k != own_key:
                        add((k, v))
        for t in W:
            b = t.b if isinstance(t, T) else t
            if b.w is not None and (b.w[0] != own_key or (STRICT and same_raw)):
                add(b.w)
            for k, v in b.r.items():
                if k != own_key or (STRICT and same_raw):
                    add((k, v))
        for t in Wd:
            b = t.b if isinstance(t, T) else t
            if b.w is not None and b.w[0] != own_key:
                add(b.w)
            for k, v in b.r.items():
                if k != own_key or (STRICT and same_raw):
                    add((k, v))
        return deps

    def _mark(self, ev, R, W):
        k, v = ev
        for t in R:
            b = t.b if isinstance(t, T) else t
            if b.r.get(k, 0) < v:
                b.r[k] = v
        for t in W:
            b = t.b if isinstance(t, T) else t
            b.w = ev
            b.r = {}

    def _wait(self, eng, deps):
        for k, v in deps.items():
            if eng["seen"].get(k, 0) < v:
                eng["e"].wait_ge(self.sems[k], v)
                eng["seen"][k] = v

    def op(self, engname, fn, R=(), W=(), Wd=()):
        eng = self.engs[engname]
        deps = self._deps(eng["key"], R, W, same_raw=(engname != "pe"), Wd=Wd)
        W = list(W) + list(Wd)
        self._wait(eng, deps)
        ins = fn(eng["e"])
        eng["cnt"] += 1
        ins.then_inc(self.sems[eng["key"]], 1)
        self._mark((eng["key"], eng["cnt"]), R, W)
        return ins

    def dma(self, q, out, in_, R=(), W=(), **kw):
        eng = self.engs[q]
        dq = self.dq[q]
        deps = self._deps(None, R, W, same_raw=True)
        i = dq["nxt"]
        dq["nxt"] = (i + 1) % NDS
        key = dq["keys"][i]
        if dq["cnt"][i] > 0:
            deps[key] = max(deps.get(key, 0), dq["cnt"][i])
        self._wait(eng, deps)
        dq["cnt"][i] += 16
        eng["e"].dma_start(out=out, in_=in_, **kw).then_inc(self.sems[key], 16)
        self._mark((key, dq["cnt"][i]), R, W)

    def barrier(self):
        allev = {}
        for name, eng in self.engs.items():
            if eng["cnt"] > 0:
                allev[eng["key"]] = eng["cnt"]
        for q, dq in self.dq.items():
            for key, c in zip(dq["keys"], dq["cnt"]):
                if c > 0:
                    allev[key] = c
        for name, eng in self.engs.items():
            deps = {k: v for k, v in allev.items() if k != eng["key"]}
            self._wait(eng, deps)

    def finish(self, out_bufs):
        eng = self.engs["sp"]
        deps = {}
        for t in out_bufs:
            b = t.b if isinstance(t, T) else t
            if b.w is not None:
                deps[b.w[0]] = max(deps.get(b.w[0], 0), b.w[1])
        self._wait(eng, deps)
        self.barrier()


IN_SHAPES = {
    "x": [L, D], "c": [D], "norm1_w": [DEPTH, D], "ada_w": [DEPTH, D, 6 * D], "ada_b": [DEPTH, 6 * D],
    "w_in": [DEPTH, D, 8536], "conv_w": [DEPTH, 4, 1536], "conv_b": [DEPTH, 1536], "dt_bias": [DEPTH, 16],
    "a_log": [DEPTH, 16], "d_skip": [DEPTH, 16], "ssd_norm_w": [DEPTH, D], "w_br_ssd": [DEPTH, D, D],
    "w_br_sb": [DEPTH, 512, D], "w_br_dsa": [DEPTH, 512, D], "w_out": [DEPTH, D, D], "norm2_w": [DEPTH, D],
    "w_up": [DEPTH, D, 4 * D], "w_down": [DEPTH, 4 * D, D], "final_norm_w": [D],
    "rope_cs": [L, 16],
}


def host_consts():
    half = 8
    inv_freq = np.exp(np.arange(half, dtype=np.float32) * np.float32(-2.0 * math.log(500000.0) / 16)).astype(np.float32)
    ang = np.arange(L, dtype=np.float32)[:, None] * inv_freq[None, :]
    return {"rope_cs": np.concatenate([np.cos(ang), np.sin(ang)], axis=1).astype(np.float32)}


class G:
    pass


def build(nlayers=DEPTH, dbg=(), skip_mixer=False, phases=("mod", "inproj", "ssd", "sb", "dsa", "merge", "norm2", "mlp")):
    nc = bass.Bass("TRN2", target_bir_lowering=False)
    kb = KB(nc)
    es = kb.es
    kb.init_psum()
    g = G()
    g.nc, g.kb, g.es, g.dbg = nc, kb, es, set(dbg)
    g.inp = {n: kb.dram(n, shp, F32, kind="ExternalInput") for n, shp in IN_SHAPES.items()}
    out_d = kb.dram("out", [L, D], F32, kind="ExternalOutput")
    g.dbg_out = {}

    g.ident = ident = kb.sb(es, "ident", [128, 128], F32)
    g.identb = identb = kb.sb(es, "identb", [128, 128], BF16)
    g.onesb = onesb = kb.sb(es, "onesb", [128, 128], BF16)
    g.onesf = kb.sb(es, "onesf", [128, 128], F32)
    g.triu = kb.sb(es, "triu", [128, 128], F32)
    g.negm = kb.sb(es, "negm", [128, 128], F32)
    g.triub = kb.sb(es, "triub", [128, 128], BF16)
    g.strl = kb.sb(es, "strl", [128, 128], F32)
    g.strlb = kb.sb(es, "strlb", [128, 128], BF16)
    g.trilb = kb.sb(es, "trilb", [128, 128], BF16)
    g.negup = kb.sb(es, "negup", [128, 128], F32)
    g.negmb = kb.sb(es, "negmb", [128, 128], BF16)
    g.pow2 = kb.sb(es, "pow2", [128, 24], F32)
    with ExitStack() as tmp:
        coli = kb.sb(tmp, "coli", [128, 128], I32)
        rowi = kb.sb(tmp, "rowi", [128, 1], I32)
        colf = kb.sb(tmp, "colf", [128, 128], F32)
        rowf = kb.sb(tmp, "rowf", [128, 1], F32)
        kb.op("pool", lambda e: e.iota(coli[:], [[1, 128]], base=0, channel_multiplier=0), W=[coli])
        kb.op("pool", lambda e: e.iota(rowi[:], [[0, 1]], base=0, channel_multiplier=1), W=[rowi])
        kb.op("dve", lambda e: e.tensor_copy(out=colf[:], in_=coli[:]), R=[coli], W=[colf])
        kb.op("dve", lambda e: e.tensor_copy(out=rowf[:], in_=rowi[:]), R=[rowi], W=[rowf])
        kb.op("dve", lambda e: e.tensor_scalar(out=ident[:], in0=colf[:], scalar1=rowf[:, 0:1], scalar2=None,
                                               op0=ALU.is_equal), R=[colf, rowf], W=[ident])
        kb.op("dve", lambda e: e.tensor_copy(out=identb[:], in_=ident[:]), R=[ident], W=[identb])
        kb.op("dve", lambda e: e.memset(onesb[:], 1.0), W=[onesb])
        kb.op("dve", lambda e: e.memset(g.onesf[:], 1.0), W=[g.onesf])
        kb.op("dve", lambda e: e.tensor_scalar(out=g.triu[:], in0=colf[:], scalar1=rowf[:, 0:1], scalar2=None, op0=ALU.is_ge),
              R=[colf, rowf], W=[g.triu])
        kb.op("dve", lambda e: e.tensor_scalar(out=g.negm[:], in0=g.triu[:], scalar1=-1.0, scalar2=30000.0, op0=ALU.add, op1=ALU.mult),
              R=[g.triu], W=[g.negm])
        kb.op("dve", lambda e: e.tensor_copy(out=g.triub[:], in_=g.triu[:]), R=[g.triu], W=[g.triub])
        kb.op("dve", lambda e: e.tensor_copy(out=g.negmb[:], in_=g.negm[:]), R=[g.negm], W=[g.negmb])
        kb.op("dve", lambda e: e.tensor_scalar(out=g.strl[:], in0=colf[:], scalar1=rowf[:, 0:1], scalar2=None, op0=ALU.is_lt),
              R=[colf, rowf], W=[g.strl])
        kb.op("dve", lambda e: e.tensor_copy(out=g.strlb[:], in_=g.strl[:]), R=[g.strl], W=[g.strlb])
        kb.op("dve", lambda e: e.tensor_scalar(out=g.trilb[:], in0=colf[:], scalar1=rowf[:, 0:1], scalar2=None, op0=ALU.is_le),
              R=[colf, rowf], W=[g.trilb])
        kb.op("dve", lambda e: e.tensor_scalar(out=g.negup[:], in0=g.trilb[:], scalar1=-1.0, scalar2=1.0e9, op0=ALU.add, op1=ALU.mult),
              R=[g.trilb], W=[g.negup])
        for kk_ in range(24):
            kb.op("dve", lambda e, kk_=kk_: e.memset(g.pow2[:, kk_:kk_ + 1], float(2.0 ** (-kk_))), Wd=[g.pow2])
        kb.barrier()

    g.xT = xT = kb.sb(es, "xT", [128, KC, L], F32)
    g.xTb = xTb = [[kb.buf(f"xT{c}_{tb}") for tb in range(4)] for c in range(KC)]
    g.hTb = [[kb.buf(f"hT{c}_{tb}") for tb in range(4)] for c in range(KC)]

    fnw = load_cols(g, es, "fnw", g.inp["final_norm_w"].t, KC)
    g.ccol = load_cols(g, es, "ccol", g.inp["c"].t, KC)
    g.csilu = kb.sb(es, "csilu", [128, KC], BF16)
    kb.op("act", lambda e: e.activation(out=g.csilu[:], in_=g.ccol[:], func=AF.Silu), R=[g.ccol], W=[g.csilu])
    g.modT_l = [kb.sb(es, "modT", [128, 48], F32) for _ in range(DEPTH)]
    g.modraw = [kb.sb(es, "modraw", [128, 48], F32) for _ in range(DEPTH)]
    g.A1_l = [kb.sb(es, "A1", [128, KC], F32) for _ in range(DEPTH)]
    g.A2_l = [kb.sb(es, "A2", [128, KC], F32) for _ in range(DEPTH)]
    g.dsa_hook = None

    x_in = g.inp["x"]
    with ExitStack() as ph:
        xin = [kb.sb(ph, f"xin{i}", [128, D], F32) for i in range(2)]
        for tt in range(NT):
            xi = xin[tt % 2]
            kb.dma("sp", xi[:], x_in.t[tt * 128:(tt + 1) * 128, :], W=[xi])
            for half in range(2):
                p = kb.ps()
                for j in range(4):
                    c = half * 4 + j
                    kb.op("pe", lambda e, c=c, j=j, p=p, xi=xi: e.transpose(
                        p[:, j * 128:(j + 1) * 128], xi[:, c * 128:(c + 1) * 128], ident[:]),
                        R=[xi, ident], W=[p])
                tb = tt // 4
                wb = [xTb[half * 4 + j][tb] for j in range(4)]
                dst = xT[:, half * 4:half * 4 + 4, tt * 128:(tt + 1) * 128]
                src = p[:, :].rearrange("p (j t) -> p j t", j=4)
                if half == 0:
                    kb.op("act", lambda e, dst=dst, src=src: e.activation(out=dst, in_=src, func=AF.Copy), R=[p], W=wb)
                else:
                    kb.op("dve", lambda e, dst=dst, src=src: e.tensor_copy(out=dst, in_=src), R=[p], W=wb)
        kb.barrier()

    def scr(name, shape, dtype=BF16):
        return kb.dram("scr_" + name, shape, dtype, kind=("ExternalOutput" if name in g.dbg else "Internal"))
    g.scr = dict(zs=scr("zs", [L, 1024]), sbv=scr("sbv", [L, 512]), dv=scr("dv", [L, 128]),
                 dqT=scr("dqT", [8, 64, L]), dkT=scr("dkT", [2, 64, L]), iqT=scr("iqT", [8, 64, L]), ikT=scr("ikT", [1, 64, L]),
                 yT=scr("yT", [16, 128, L]), sbqT=scr("sbqT", [4, 128, L]), sbkT=scr("sbkT", [4, 128, L]), gT=scr("gT", [24, 128, L]))
    for name in g.scr:
        if name in g.dbg:
            g.dbg_out["scr_" + name] = g.scr[name]
    g.ropecs = kb.sb(es, "ropecs", [128, NT, 16], F32)
    kb.dma("sp", g.ropecs[:], g.inp["rope_cs"].t.rearrange("(t p) c -> p t c", p=128), W=[g.ropecs])
    g.dt_tok = kb.sb(es, "dt_tok", [128, NT, 16], F32)
    g.wix_tok = kb.sb(es, "wix_tok", [128, NT, 8], F32)
    g.dtb_bc = kb.sb(es, "dtb_bc", [128, 16], F32)
    g.cb = kb.sb(es, "cb", [128, 12], F32)
    g.cw = [kb.sb(es, f"cw{k}", [128, 12], F32) for k in range(4)]

    for l in range(nlayers):
        g.modT, g.A1, g.A2 = g.modT_l[l], g.A1_l[l], g.A2_l[l]
        if "mod" in phases:
            if l == 0:
                phase_mod(g, l)
            else:
                mod_finish(g, l)
        g.dsa_hook = (l + 1) if (l + 1 < nlayers and "mod" in phases) else None
        dump_sb(g, f"mod{l}", g.modT, [128, 48])
        if not skip_mixer:
            with ExitStack() as ssd_scope:
                g.xs_tok = kb.sb(ssd_scope, "xs_tok", [128, NT, 1024], BF16)
                g.bt_tok = kb.sb(ssd_scope, "bt_tok", [128, NT, 256], BF16)
                g.bcT = kb.sb(ssd_scope, "bcT", [128, 4, L], BF16)
                with nc.allow_non_contiguous_dma(reason="tiny vectors"):
                    kb.dma("sp", g.dtb_bc[:], g.inp["dt_bias"].t[l].partition_broadcast(128), W=[g.dtb_bc])
                    kb.dma("sp", g.cb[:], g.inp["conv_b"].t[l].rearrange("(c p) -> p c", p=128), W=[g.cb])
                    for k in range(4):
                        kb.dma("sp", g.cw[k][:], g.inp["conv_w"].t[l][k].rearrange("(c p) -> p c", p=128), W=[g.cw[k]])
                with ExitStack() as hsc:
                    g.hT = kb.sb(hsc, "hT", [128, KC, L], BF16)
                    phase_norm(g, l, which=1)
                    dump_hT(g, f"h{l}")
                    if "inproj" in phases:
                        phase_inproj(g, l)
                dump_sb(g, f"xs_tok{l}", g.xs_tok, [128, NT, 1024], BF16)
                dump_sb(g, f"bt_tok{l}", g.bt_tok, [128, NT, 256], BF16)
                dump_sb(g, f"bcT{l}", g.bcT, [128, 4, L], BF16)
                dump_sb(g, f"dt_tok{l}", g.dt_tok, [128, NT, 16], F32)
                dump_sb(g, f"wix_tok{l}", g.wix_tok, [128, NT, 8], F32)
                kb.barrier()
                if "ssd" in phases:
                    phase_ssd(g, l)
            if "sb" in phases:
                phase_sb(g, l)
            if "dsa" in phases:
                phase_dsa(g, l)
            if "merge" in phases:
                phase_merge(g, l)
            dump_xT(g, f"xmix{l}")
        with ExitStack() as hsc:
            g.hT = kb.sb(hsc, "hT", [128, KC, L], BF16)
            if "norm2" in phases:
                phase_norm(g, l, which=2)
            if "mlp" in phases:
                phase_mlp(g, l)
        dump_xT(g, f"xout{l}")

    final_norm(kb, nc, xT, xTb, fnw, onesb, ident, out_d)
    kb.finish([out_d] + list(g.dbg_out.values()))
    return nc, sorted(g.dbg_out.keys())


def load_cols(g, es_, name, src_ap, n):
    kb = g.kb
    t = kb.sb(es_, name, [128, n], F32)
    with g.nc.allow_non_contiguous_dma(reason="tiny per-feature vector load"):
        for c0 in range(0, n, 8):
            c1 = min(n, c0 + 8)
            kb.dma("sp", t[:, c0:c1], src_ap[c0 * 128:c1 * 128].rearrange("(c p) -> p c", p=128), W=[t])
    return t


def dump_sb(g, name, t, shape, dtype=F32):
    if name not in g.dbg:
        return
    d = g.kb.dram("dbg_" + name, shape, dtype, kind="ExternalOutput")
    g.kb.barrier()
    g.kb.dma("sp", d.t, t[:], R=[t], W=[d])
    g.dbg_out["dbg_" + name] = d


def dump_xT(g, name):
    if name not in g.dbg:
        return
    d = g.kb.dram("dbg_" + name, [128, KC, L], F32, kind="ExternalOutput")
    g.kb.barrier()
    g.kb.dma("sp", d.t, g.xT[:], R=[b for row in g.xTb for b in row], W=[d])
    g.dbg_out["dbg_" + name] = d


def dump_hT(g, name):
    if name not in g.dbg:
        return
    d = g.kb.dram("dbg_" + name, [128, KC, L], BF16, kind="ExternalOutput")
    g.kb.barrier()
    g.kb.dma("sp", d.t, g.hT[:], R=[b for row in g.hTb for b in row], W=[d])
    g.dbg_out["dbg_" + name] = d


def load_w(g, t, src_rows_ap):
    g.kb.dma("pool", t[:], src_rows_ap.rearrange("(kc p) n -> p kc n", p=128), W=[t])


def mod_piece_load(g, l, j4, w):
    g.kb.dma("pool", w[:], g.inp["ada_w"].t[l][:, j4 * 512:(j4 + 1) * 512].rearrange("(kc p) n -> p kc n", p=128), W=[w])


def mod_piece_mm(g, l, j4, w):
    kb = g.kb
    p = kb.ps()
    for jl in range(4):
        for kc in range(KC):
            kb.op("pe", lambda e, jl=jl, kc=kc: e.matmul(p[:, jl:jl + 1], w[:, kc, jl * 128:(jl + 1) * 128], g.csilu[:, kc:kc + 1],
                                                         start=(kc == 0), stop=(kc == KC - 1)), R=[w, g.csilu], W=[p])
    raw = g.modraw[l]
    kb.op("act", lambda e: e.activation(out=raw[:, j4 * 4:(j4 + 1) * 4], in_=p[:, 0:4], func=AF.Copy), R=[p], Wd=[raw])


def mod_piece(g, l, j4, w):
    mod_piece_load(g, l, j4, w)
    mod_piece_mm(g, l, j4, w)


def mod_finish(g, l):
    kb = g.kb
    with ExitStack() as ph:
        adab = load_cols(g, ph, "adab", g.inp["ada_b"].t[l], 48)
        n1w = load_cols(g, ph, "n1w", g.inp["norm1_w"].t[l], KC)
        n2w = load_cols(g, ph, "n2w", g.inp["norm2_w"].t[l], KC)
        modT, A1, A2 = g.modT_l[l], g.A1_l[l], g.A2_l[l]
        kb.op("dve", lambda e: e.tensor_tensor(out=modT[:], in0=g.modraw[l][:], in1=adab[:], op=ALU.add),
              R=[g.modraw[l], adab], W=[modT])
        kb.op("dve", lambda e: e.scalar_tensor_tensor(out=A1[:], in0=modT[:, 8:16], scalar=1.0, in1=n1w[:],
                                                      op0=ALU.add, op1=ALU.mult), R=[modT, n1w], W=[A1])
        kb.op("dve", lambda e: e.scalar_tensor_tensor(out=A2[:], in0=modT[:, 32:40], scalar=1.0, in1=n2w[:],
                                                      op0=ALU.add, op1=ALU.mult), R=[modT, n2w], W=[A2])
        kb.barrier()


def phase_mod(g, l):
    kb = g.kb
    with ExitStack() as ph:
        wt = [kb.sb(ph, f"adaw{i}", [128, KC, 512], BF16) for i in range(3)]
        for j4 in range(12):
            mod_piece(g, l, j4, wt[j4 % 3])
        kb.barrier()
    mod_finish(g, l)


def phase_norm(g, l, which):
    kb = g.kb
    A = g.A1 if which == 1 else g.A2
    sh0 = 0 if which == 1 else 24
    with ExitStack() as ph:
        sq = [kb.sb(ph, f"nsq{i}", [128, 512], BF16) for i in range(2)]
        rs = kb.sb(ph, "nrs", [128, 512], F32)
        tmp = [kb.sb(ph, f"ntmp{i}", [128, 512], F32) for i in range(2)]
        for tb in range(4):
            rstd_bcast(kb, ph, g.xT, g.xTb, tb, g.onesb, sq, rs)
            for c in range(KC):
                t = tmp[c % 2]
                kb.op("dve", lambda e, t=t, c=c: e.tensor_tensor(out=t[:], in0=g.xT[:, c, tb * 512:(tb + 1) * 512], in1=rs[:],
                                                                 op=ALU.mult), R=[g.xTb[c][tb], rs], W=[t])
                kb.op("act", lambda e, t=t, c=c: e.activation(
                    out=g.hT[:, c, tb * 512:(tb + 1) * 512], in_=t[:], func=AF.Identity,
                    scale=A[:, c:c + 1], bias=g.modT[:, sh0 + c:sh0 + c + 1]), R=[t, A, g.modT], W=[g.hTb[c][tb]])
        kb.barrier()


OFF = dict(z=0, xbc=1024, dt=2560, sbq=2576, sbk=3088, sbv=3600, dsq=4112, dsk=4624, dsv=4752,
           ixq=4880, ixk=5392, ixw=5456, gate=5464)


def rope(g, f3, nh, tt, rt):
    kb = g.kb
    fb = f3._buf
    cos = g.ropecs[:, tt:tt + 1, 0:8].broadcast_to([128, nh, 8])
    sin = g.ropecs[:, tt:tt + 1, 8:16].broadcast_to([128, nh, 8])
    x1 = f3.ap[:, :, 0:8]
    x2 = f3.ap[:, :, 8:16]
    t = [rt[:, i, 0:nh, :] for i in range(4)]
    kb.op("dve", lambda e: e.tensor_tensor(out=t[0], in0=x1, in1=cos, op=ALU.mult), R=[fb, g.ropecs], W=[rt])
    kb.op("dve", lambda e: e.tensor_tensor(out=t[1], in0=x2, in1=sin, op=ALU.mult), R=[fb, g.ropecs], W=[rt])
    kb.op("dve", lambda e: e.tensor_tensor(out=t[2], in0=x2, in1=cos, op=ALU.mult), R=[fb, g.ropecs], W=[rt])
    kb.op("dve", lambda e: e.tensor_tensor(out=t[3], in0=x1, in1=sin, op=ALU.mult), R=[fb, g.ropecs], W=[rt])
    kb.op("dve", lambda e: e.tensor_tensor(out=x1, in0=t[0], in1=t[1], op=ALU.subtract), R=[rt], W=[fb])
    kb.op("dve", lambda e: e.tensor_tensor(out=x2, in0=t[2], in1=t[3], op=ALU.add), R=[rt], W=[fb])


class V:
    def __init__(self, ap, buf):
        self.ap = ap
        self._buf = buf


def phase_inproj(g, l):
    kb, nc = g.kb, g.nc
    win = g.inp["w_in"].t[l]
    S = g.scr
    with ExitStack() as ph:
        wt = [kb.sb(ph, "winw", [128, KC, 512], BF16) for _ in range(2)]
        nw = [0]

        def getw(n0, N):
            w = wt[nw[0] % 2]
            nw[0] += 1
            kb.dma("pool", w[:, :, 0:N], win[:, n0:n0 + N].rearrange("(kc p) n -> p kc n", p=128), W=[w])
            return w

        stgb = [kb.sb(ph, "stgb", [128, 512], BF16) for _ in range(3)]
        stgf = [kb.sb(ph, "stgf", [128, 512], F32) for _ in range(2)]
        rt = kb.sb(ph, "ropetmp", [128, 4, 16, 8], F32)
        tst = [kb.sb(ph, "tst", [64, 8, 512], BF16) for _ in range(2)]
        cnt = dict(b=0, f=0, t=0)

        def nxt(lst, k):
            r = lst[cnt[k] % len(lst)]
            cnt[k] += 1
            return r

        def tok_mm(w, N, tt):
            p = kb.ps()
            for kc in range(KC):
                kb.op("pe", lambda e, kc=kc: e.matmul(p[:, 0:N], g.hT[:, kc, tt * 128:(tt + 1) * 128], w[:, kc, 0:N],
                                                      start=(kc == 0), stop=(kc == KC - 1)),
                      R=[g.hTb[kc][tt // 4], w], W=[p])
            return p

        def feat_mm(w, ci, tb):
            p = kb.ps()
            for kc in range(KC):
                kb.op("pe", lambda e, kc=kc: e.matmul(p[:, :], w[:, kc, ci * 128:(ci + 1) * 128],
                                                      g.hT[:, kc, tb * 512:(tb + 1) * 512],
                                                      start=(kc == 0), stop=(kc == KC - 1)),
                      R=[g.hTb[kc][tb], w], W=[p])
            return p

        for zi in range(2):
            w = getw(OFF["z"] + zi * 512, 512)
            for tt in range(NT):
                p = tok_mm(w, 512, tt)
                sb_ = nxt(stgb, "b")
                kb.op("act", lambda e: e.activation(out=sb_[:], in_=p[:, :], func=AF.Silu), R=[p], W=[sb_])
                kb.dma("sp", S["zs"].t[tt * 128:(tt + 1) * 128, zi * 512:(zi + 1) * 512], sb_[:], R=[sb_])
        w = getw(OFF["dt"], 16)
        for tt in range(NT):
            p = tok_mm(w, 16, tt)
            f = nxt(stgf, "f")
            kb.op("dve", lambda e: e.tensor_tensor(out=f[:, 0:16], in0=p[:, 0:16], in1=g.dtb_bc[:], op=ALU.add),
                  R=[p, g.dtb_bc], W=[f])
            kb.op("act", lambda e: e.activation(out=f[:, 0:16], in_=f[:, 0:16], func=AF.Exp), R=[f], W=[f])
            kb.op("act", lambda e: e.activation(out=g.dt_tok[:, tt, :], in_=f[:, 0:16], func=AF.Ln, bias=1.0),
                  R=[f], W=[g.dt_tok])
        w = getw(OFF["sbv"], 512)
        for tt in range(NT):
            p = tok_mm(w, 512, tt)
            sb_ = nxt(stgb, "b")
            kb.op("act", lambda e: e.activation(out=sb_[:], in_=p[:, :], func=AF.Copy), R=[p], W=[sb_])
            kb.dma("sp", S["sbv"].t[tt * 128:(tt + 1) * 128, :], sb_[:], R=[sb_])

        def roped_group(n0, N, nh, dstT, extra=None):
            w = getw(n0, N)
            fs, bs, sts = {}, {}, {}

            def stA(tt):
                p = tok_mm(w, N, tt)
                f = nxt(stgf, "f")
                kb.op("act", lambda e: e.activation(out=f[:, 0:N], in_=p[:, 0:N], func=AF.Copy), R=[p], W=[f])
                fs[tt] = f

            def stB(tt):
                f = fs.pop(tt)
                rope(g, V(f[:, 0:nh * 64].rearrange("p (h d) -> p h d", d=64), f), nh, tt, rt)
                b = nxt(stgb, "b")
                kb.op("act", lambda e: e.activation(out=b[:, 0:N], in_=f[:, 0:N], func=AF.Copy), R=[f], W=[b])
                if extra is not None:
                    extra(tt, f, b)
                bs[tt] = b

            def stC(tt):
                b = bs.pop(tt)
                pt = kb.ps()
                ptb = pt[:, :].bitcast(BF16)
                for h in range(nh):
                    kb.op("pe", lambda e, h=h: e.transpose(ptb[0:64, h * 128:(h + 1) * 128], b[:, h * 64:(h + 1) * 64], g.identb[:]),
                          R=[b, g.identb], W=[pt])
                if tt % 4 == 0:
                    sts[tt // 4] = nxt(tst, "t")
                st = sts[tt // 4]
                kb.op("act", lambda e: e.activation(
                    out=st[0:64, 0:nh, (tt % 4) * 128:(tt % 4 + 1) * 128],
                    in_=ptb[0:64, 0:nh * 128].rearrange("p (h t) -> p h t", h=nh), func=AF.Copy), R=[pt], Wd=[st])
                if tt % 4 == 3:
                    tb = tt // 4
                    kb.dma("sp", dstT.t[:, :, tb * 512:(tb + 1) * 512].rearrange("h d t -> d h t"), st[0:64, 0:nh, :], R=[st])

            for step in range(NT + 2):
                if step < NT:
                    stA(step)
                if 0 <= step - 1 < NT:
                    stB(step - 1)
                if 0 <= step - 2 < NT:
                    stC(step - 2)

        roped_group(OFF["dsq"], 512, 8, S["dqT"])

        def dsv_extra(tt, f, b):
            kb.dma("sp", S["dv"].t[tt * 128:(tt + 1) * 128, :], b[:, 128:256], R=[b])
        roped_group(OFF["dsk"], 256, 2, S["dkT"], dsv_extra)
        roped_group(OFF["ixq"], 512, 8, S["iqT"])

        def ixw_extra(tt, f, b):
            kb.op("dve", lambda e: e.tensor_copy(out=g.wix_tok[:, tt, :], in_=f[:, 64:72]), R=[f], W=[g.wix_tok])
        roped_group(OFF["ixk"], 72, 1, S["ikT"], ixw_extra)

        def feat_group(n0, dstT, c0, func):
            w = getw(n0, 512)
            for ci in range(4):
                for tb in range(4):
                    p = feat_mm(w, ci, tb)
                    sb_ = nxt(stgb, "b")
                    kb.op("act", lambda e: e.activation(out=sb_[:], in_=p[:, :], func=func), R=[p], W=[sb_])
                    kb.dma("sp", dstT.t[c0 + ci, :, tb * 512:(tb + 1) * 512], sb_[:], R=[sb_])

        feat_group(OFF["sbq"], S["sbqT"], 0, AF.Copy)
        feat_group(OFF["sbk"], S["sbkT"], 0, AF.Copy)
        for gi in range(6):
            feat_group(OFF["gate"] + gi * 512, S["gT"], gi * 4, AF.Sigmoid)

        kb.barrier()

    with ExitStack() as ph:
        wt = [kb.sb(ph, "winw", [128, KC, 512], BF16) for _ in range(2)]
        nw = [0]
        xpad = [kb.sb(ph, "xpad", [128, 3 + L], F32) for _ in range(1)]
        cva = [kb.sb(ph, "cva", [128, L], F32) for _ in range(1)]
        cvo = [kb.sb(ph, "cvo", [128, L], BF16) for _ in range(2)]
        for xp in xpad:
            kb.op("dve", lambda e, xp=xp: e.memset(xp[:, 0:3], 0.0), W=[xp])
        for gi in range(3):
            w = getw(OFF["xbc"] + gi * 512, 512)
            for ci in range(4):
                cidx = gi * 4 + ci
                xp, acc = xpad[0], cva[0]
                for tb in range(4):
                    p = feat_mm(w, ci, tb)
                    kb.op("act", lambda e: e.activation(out=xp[:, 3 + tb * 512:3 + (tb + 1) * 512], in_=p[:, :], func=AF.Copy),
                          R=[p], Wd=[xp])
                kb.op("act", lambda e: e.activation(out=acc[:], in_=xp[:, 3:3 + L], func=AF.Identity,
                                                    scale=g.cw[3][:, cidx:cidx + 1], bias=g.cb[:, cidx:cidx + 1]),
                      R=[xp, g.cw[3], g.cb], W=[acc])
                for k in (2, 1, 0):
                    kb.op("dve", lambda e, k=k: e.scalar_tensor_tensor(out=acc[:], in0=xp[:, k:k + L], scalar=g.cw[k][:, cidx:cidx + 1],
                                                                       in1=acc[:], op0=ALU.mult, op1=ALU.add),
                          R=[xp, g.cw[k], acc], W=[acc])
                if cidx < 8:
                    o = cvo[cidx % 2]
                    ov = o[:, :]
                else:
                    o = g.bcT
                    ov = g.bcT[:, cidx - 8, :]
                kb.op("act", lambda e: e.activation(out=ov, in_=acc[:], func=AF.Silu), R=[acc], W=[o])
                if cidx < 10:
                    for half in range(2):
                        pt = kb.ps()
                        ptb = pt[:, :].bitcast(BF16)
                        for j in range(8):
                            tt = half * 8 + j
                            kb.op("pe", lambda e, j=j, tt=tt: e.transpose(ptb[:, j * 128:(j + 1) * 128], ov[:, tt * 128:(tt + 1) * 128],
                                                                          g.identb[:]), R=[o, g.identb], W=[pt])
                        if cidx < 8:
                            dst, dt_ = g.xs_tok[:, half * 8:(half + 1) * 8, cidx * 128:(cidx + 1) * 128], g.xs_tok
                        else:
                            dst, dt_ = g.bt_tok[:, half * 8:(half + 1) * 8, (cidx - 8) * 128:(cidx - 7) * 128], g.bt_tok
                        kb.op("dve", lambda e: e.tensor_copy(out=dst, in_=ptb.rearrange("p (j f) -> p j f", j=8)), R=[pt], W=[dt_])
        kb.barrier()


def phase_ssd(g, l):
    kb, nc = g.kb, g.nc
    S = g.scr
    with ExitStack() as ph:
        def sb(name, shape, dt_=F32):
            return kb.sb(ph, name, shape, dt_)
        a_bc = sb("a_bc", [128, 16])
        dsk_bc = sb("dsk_bc", [128, 16])
        nw_bc = sb("nw_bc", [128, 1024])
        Dmat = sb("Dmat", [128, 16, 128], BF16)
        with nc.allow_non_contiguous_dma(reason="tiny vectors"):
            kb.dma("sp", a_bc[:], g.inp["a_log"].t[l].partition_broadcast(128), W=[a_bc])
            kb.dma("sp", dsk_bc[:], g.inp["d_skip"].t[l].partition_broadcast(128), W=[dsk_bc])
            kb.dma("sp", nw_bc[:], g.inp["ssd_norm_w"].t[l].partition_broadcast(128), W=[nw_bc])
        kb.op("act", lambda e: e.activation(out=a_bc[:], in_=a_bc[:], func=AF.Exp), R=[a_bc], W=[a_bc])
        kb.op("dve", lambda e: e.tensor_scalar(out=a_bc[:], in0=a_bc[:], scalar1=-1.0, scalar2=None, op0=ALU.mult), R=[a_bc], W=[a_bc])
        for h in range(16):
            kb.op("dve", lambda e, h=h: e.tensor_scalar(out=Dmat[:, h, :], in0=g.ident[:], scalar1=dsk_bc[:, h:h + 1], scalar2=None,
                                                        op0=ALU.mult), R=[g.ident, dsk_bc], Wd=[Dmat])
        NB = 2
        X = [sb("X", [128, 2, 16, 128], BF16)] * 2
        dd = [sb("dd", [128, 16, 128])] * 2
        Mt = [sb("Mt", [128, 16, 128], BF16) for _ in range(NB)]
        xw = [sb("xw", [128, 1024], BF16) for _ in range(NB)]
        zst = [sb("zst", [128, 1024], BF16) for _ in range(NB)]
        yy = [sb("yy", [128, 1024])] * 2
        yz = yy
        ss = [sb("ss", [128, 1]) for _ in range(NB)]
        epsc = sb("epsc", [128, 1])
        kb.op("dve", lambda e: e.memset(epsc[:], EPS), W=[epsc])
        yn = [sb("yn", [128, 1024], BF16) for _ in range(NB)]
        yst = [sb("yst", [128, KC, 256], BF16)] * 2
        hst = [sb("hst", [128, 512]) for _ in range(2)]
        htmp = [sb("htmp", [128, 512]) for _ in range(2)]
        prevb = [[sb("prevb", [128, 512], BF16) for _ in range(2)] for _ in range(2)]
        ytmp = [sb("ytmp", [128, 512]) for _ in range(2)]

        dA_all = sb("dA_all", [128, 256])
        dAs_all = sb("dAs_all", [128, 2, 256], BF16)
        ac_all = sb("ac_all", [128, 256])
        eac_all = sb("eac_all", [128, 256])
        cd_all = sb("cd_all", [128, 256])
        wgt_all = sb("wgt_all", [128, 256])
        v_all = sb("v_all", [128, 256])
        vs_all = sb("vs_all", [128, 2, 256], BF16)
        acs_all = sb("acs_all", [128, 2, 256], BF16)
        dtf = g.dt_tok[:, :, :].rearrange("p c h -> p (c h)")
        kb.op("dve", lambda e: e.tensor_tensor(out=dA_all[:, :].rearrange("p (c h) -> p c h", h=16), in0=g.dt_tok[:, :, :],
                                               in1=a_bc[:, :].unsqueeze(1).broadcast_to([128, NT, 16]), op=ALU.mult),
              R=[g.dt_tok, a_bc], W=[dA_all])
        kb.op("dve", lambda e: e.tensor_copy(out=dAs_all[:, 0, :], in_=dA_all[:]), R=[dA_all], W=[dAs_all])
        kb.op("dve", lambda e: e.tensor_tensor(out=dAs_all[:, 1, :], in0=dA_all[:], in1=dAs_all[:, 0, :], op=ALU.subtract),
              R=[dA_all, dAs_all], W=[dAs_all])
        pac = kb.ps(pin=True)
        for c in range(NT):
            for half, lhs in ((0, g.triub), (1, g.onesb)):
                for hl_ in range(2):
                    kb.op("pe", lambda e, c=c, half=half, lhs=lhs, hl_=hl_: e.matmul(
                        pac[:, half * 256 + c * 16:half * 256 + (c + 1) * 16], lhs[:], dAs_all[:, hl_, c * 16:(c + 1) * 16],
                        start=(hl_ == 0), stop=(hl_ == 1)), R=[lhs, dAs_all], W=[pac])
        kb.unpin(pac)
        kb.op("dve", lambda e: e.tensor_copy(out=ac_all[:], in_=pac[:, 0:256]), R=[pac], W=[ac_all])
        kb.op("act", lambda e: e.activation(out=eac_all[:], in_=pac[:, 0:256], func=AF.Exp), R=[pac], W=[eac_all])
        kb.op("act", lambda e: e.activation(out=cd_all[:], in_=pac[:, 256:512], func=AF.Exp), R=[pac], W=[cd_all])
        kb.op("dve", lambda e: e.tensor_tensor(out=wgt_all[:], in0=pac[:, 256:512], in1=ac_all[:], op=ALU.subtract), R=[pac, ac_all], W=[wgt_all])
        kb.op("act", lambda e: e.activation(out=wgt_all[:], in_=wgt_all[:], func=AF.Exp), R=[wgt_all], W=[wgt_all])
        kb.op("dve", lambda e: e.tensor_tensor(out=wgt_all[:], in0=wgt_all[:], in1=dtf, op=ALU.mult), R=[wgt_all, g.dt_tok], W=[wgt_all])
        kb.op("act", lambda e: e.activation(out=v_all[:], in_=dtf, func=AF.Ln), R=[g.dt_tok], W=[v_all])
        kb.op("dve", lambda e: e.tensor_tensor(out=v_all[:], in0=v_all[:], in1=ac_all[:], op=ALU.subtract), R=[v_all, ac_all], W=[v_all])
        kb.op("dve", lambda e: e.tensor_copy(out=vs_all[:, 0, :], in_=v_all[:]), R=[v_all], W=[vs_all])
        kb.op("dve", lambda e: e.tensor_tensor(out=vs_all[:, 1, :], in0=v_all[:], in1=vs_all[:, 0, :], op=ALU.subtract), R=[v_all, vs_all], W=[vs_all])
        kb.op("dve", lambda e: e.tensor_copy(out=acs_all[:, 0, :], in_=ac_all[:]), R=[ac_all], W=[acs_all])
        kb.op("dve", lambda e: e.tensor_tensor(out=acs_all[:, 1, :], in0=ac_all[:], in1=acs_all[:, 0, :], op=ALU.subtract), R=[ac_all, acs_all], W=[acs_all])

        pcbs = {}

        def P1(c):
            k = c % NB
            tsl = slice(c * 128, (c + 1) * 128)
            c16 = slice(c * 16, (c + 1) * 16)
            kb.op("dve", lambda e: e.tensor_tensor(
                out=xw[k][:, :].rearrange("p (h d) -> p h d", d=64), in0=g.xs_tok[:, c, :].rearrange("p (h d) -> p h d", d=64),
                in1=wgt_all[:, c16].unsqueeze(2).broadcast_to([128, 16, 64]), op=ALU.mult), R=[g.xs_tok, wgt_all], W=[xw[k]])
            for hl_ in range(2):
                kb.op("dve", lambda e, hl_=hl_: e.tensor_tensor(
                    out=X[k][:, hl_, :, :], in0=g.identb[:, :].unsqueeze(1).broadcast_to([128, 16, 128]),
                    in1=acs_all[:, hl_, c16].unsqueeze(2).broadcast_to([128, 16, 128]), op=ALU.mult), R=[g.identb, acs_all], Wd=[X[k]])
            for q4 in range(4):
                pb = kb.ps()
                h4 = slice(c * 16 + q4 * 4, c * 16 + q4 * 4 + 4)
                for hl_ in range(2):
                    kb.op("pe", lambda e, hl_=hl_: e.matmul(pb[:, :], g.onesb[:], X[k][:, hl_, q4 * 4:(q4 + 1) * 4, :],
                                                            start=(hl_ == 0), stop=False), R=[g.onesb, X[k]], W=[pb])
                for hl_ in range(2):
                    kb.op("pe", lambda e, hl_=hl_: e.matmul(pb[:, :], g.identb[:], vs_all[:, hl_, h4].unsqueeze(2).broadcast_to([128, 4, 128]),
                                                            start=False, stop=False), R=[g.identb, vs_all], W=[pb])
                kb.op("pe", lambda e: e.matmul(pb[:, :], g.identb[:], g.negmb[:, :].unsqueeze(1).broadcast_to([128, 4, 128]),
                                               start=False, stop=True), R=[g.identb, g.negmb], W=[pb])
                kb.op("act", lambda e, q4=q4, pb=pb: e.activation(out=dd[k][:, q4 * 4:(q4 + 1) * 4, :],
                                                                  in_=pb[:, :].rearrange("p (h i) -> p h i", h=4), func=AF.Exp),
                      R=[pb], Wd=[dd[k]])
            pcb = kb.ps(pin=True)
            for gg in range(2):
                kb.op("pe", lambda e, gg=gg: e.matmul(pcb[:, gg * 128:(gg + 1) * 128], g.bcT[:, gg, tsl], g.bcT[:, 2 + gg, tsl],
                                                      start=True, stop=True), R=[g.bcT], W=[pcb])
            pcbs[c] = pcb

        def P1b(c):
            k = c % NB
            pcb = pcbs.pop(c)
            kb.unpin(pcb)
            for gg in range(2):
                kb.op("dve", lambda e, gg=gg: e.tensor_tensor(
                    out=Mt[k][:, gg * 8:(gg + 1) * 8, :], in0=dd[k][:, gg * 8:(gg + 1) * 8, :],
                    in1=pcb[:, gg * 128:(gg + 1) * 128].unsqueeze(1).broadcast_to([128, 8, 128]), op=ALU.mult),
                    R=[dd[k], pcb], Wd=[Mt[k]])

        pend2 = {}

        def P2a(c):
            k = c % NB
            tsl = slice(c * 128, (c + 1) * 128)
            kb.dma("sp", zst[k][:], S["zs"].t[tsl, :], W=[zst[k]])
            rec = []
            for gg in range(2):
                pA = kb.ps()
                for hl in range(8):
                    h = gg * 8 + hl
                    xsl = g.xs_tok[:, c, h * 64:(h + 1) * 64]
                    kb.op("pe", lambda e, h=h, hl=hl, xsl=xsl: e.matmul(pA[:, hl * 64:(hl + 1) * 64], Mt[k][:, h, :], xsl, start=True, stop=False),
                          R=[Mt[k], g.xs_tok], W=[pA])
                    kb.op("pe", lambda e, h=h, hl=hl, xsl=xsl: e.matmul(pA[:, hl * 64:(hl + 1) * 64], Dmat[:, h, :], xsl, start=False, stop=True),
                          R=[Dmat, g.xs_tok], W=[pA])
                pB = pS = None
                if c > 0:
                    pB = kb.ps()
                    pv = prevb[gg][(c - 1) % 2]
                    kb.op("pe", lambda e, pB=pB, pv=pv, gg=gg: e.matmul(pB[:, :], g.bcT[:, 2 + gg, tsl], pv[:], start=True, stop=True),
                          R=[g.bcT, pv], W=[pB])
                if c < NT - 1:
                    pS = kb.ps()
                    kb.op("pe", lambda e, pS=pS, gg=gg: e.matmul(pS[:, :], g.bt_tok[:, c, gg * 128:(gg + 1) * 128], xw[k][:, gg * 512:(gg + 1) * 512],
                                                                start=True, stop=True), R=[g.bt_tok, xw[k]], W=[pS])
                rec.append((pA, pB, pS))
            pend2[c] = rec

        def P2b(c):
            k = c % NB
            rec = pend2.pop(c)
            for gg in range(2):
                pA, pB, pS = rec[gg]
                ysl = yy[k][:, gg * 512:(gg + 1) * 512]
                if c > 0:
                    yt = ytmp[gg]
                    kb.op("dve", lambda e: e.tensor_tensor(
                        out=yt[:, :].rearrange("p (h d) -> p h d", d=64), in0=pB[:, :].rearrange("p (h d) -> p h d", d=64),
                        in1=eac_all[:, c * 16 + gg * 8:c * 16 + (gg + 1) * 8].unsqueeze(2).broadcast_to([128, 8, 64]), op=ALU.mult),
                        R=[pB, eac_all], W=[yt])
                    kb.op("dve", lambda e: e.tensor_tensor(out=ysl, in0=yt[:], in1=pA[:, :], op=ALU.add), R=[yt, pA], W=[yy[k]])
                else:
                    kb.op("act", lambda e: e.activation(out=ysl, in_=pA[:, :], func=AF.Copy), R=[pA], W=[yy[k]])
                if c < NT - 1:
                    if c == 0:
                        kb.op("dve", lambda e: e.tensor_copy(out=hst[gg][:], in_=pS[:, :]), R=[pS], W=[hst[gg]])
                    else:
                        kb.op("dve", lambda e: e.tensor_tensor(
                            out=htmp[gg][:, :].rearrange("p (h d) -> p h d", d=64), in0=hst[gg][:, :].rearrange("p (h d) -> p h d", d=64),
                            in1=cd_all[:, c * 16 + gg * 8:c * 16 + (gg + 1) * 8].unsqueeze(2).broadcast_to([128, 8, 64]), op=ALU.mult),
                            R=[hst[gg], cd_all], W=[htmp[gg]])
                        kb.op("dve", lambda e: e.tensor_tensor(out=hst[gg][:], in0=htmp[gg][:], in1=pS[:, :], op=ALU.add),
                              R=[htmp[gg], pS], W=[hst[gg]])
                    pn = prevb[gg][c % 2]
                    kb.op("act", lambda e: e.activation(out=pn[:], in_=hst[gg][:], func=AF.Copy), R=[hst[gg]], W=[pn])

        def P3(c):
            k = c % NB
            tsl = slice(c * 128, (c + 1) * 128)
            kb.op("dve", lambda e: e.tensor_tensor(out=yz[k][:], in0=yy[k][:], in1=zst[k][:], op=ALU.mult), R=[yy[k], zst[k]], W=[yz[k]])
            kb.op("act", lambda e: e.activation(out=yn[k][:], in_=yz[k][:], func=AF.Square, accum_out=ss[k][:]), R=[yz[k]], W=[yn[k], ss[k]])
            kb.op("act", lambda e: e.activation(out=ss[k][:], in_=ss[k][:], func=AF.Ln, scale=1.0 / 1024, bias=epsc[:, 0:1]), R=[ss[k], epsc], W=[ss[k]])
            kb.op("act", lambda e: e.activation(out=ss[k][:], in_=ss[k][:], func=AF.Exp, scale=-0.5), R=[ss[k]], W=[ss[k]])
            kb.op("dve", lambda e: e.scalar_tensor_tensor(out=yn[k][:], in0=yz[k][:], scalar=ss[k][:, 0:1], in1=nw_bc[:],
                                                          op0=ALU.mult, op1=ALU.mult), R=[yz[k], ss[k], nw_bc], W=[yn[k]])
            pt = kb.ps()
            ptb = pt[:, :].bitcast(BF16)
            for kc in range(KC):
                kb.op("pe", lambda e, kc=kc: e.transpose(ptb[:, kc * 128:(kc + 1) * 128], yn[k][:, kc * 128:(kc + 1) * 128], g.identb[:]),
                      R=[yn[k], g.identb], W=[pt])
            st = yst[(c // 2) % 2]
            kb.op("act", lambda e: e.activation(out=st[:, :, (c % 2) * 128:(c % 2 + 1) * 128],
                                                in_=ptb.rearrange("p (kc t) -> p kc t", kc=KC), func=AF.Copy), R=[pt], Wd=[st])
            if c % 2 == 1:
                t2 = c // 2
                kb.dma("sp", S["yT"].t[0:8, :, t2 * 256:(t2 + 1) * 256].rearrange("kc p t -> p kc t"), st[:], R=[st])

        for step in range(-1, NT + 1):
            if 0 <= step + 1 < NT:
                P1(step + 1)
            if 0 <= step < NT:
                P2a(step)
            if 0 <= step - 1 < NT:
                P3(step - 1)
            if 0 <= step + 1 < NT:
                P1b(step + 1)
            if 0 <= step < NT:
                P2b(step)
        kb.barrier()


def phase_sb(g, l):
    kb, nc = g.kb, g.nc
    S = g.scr
    with ExitStack() as ph:
        def sb(name, shape, dt_=F32):
            return kb.sb(ph, name, shape, dt_)
        qT = sb("sbq", [128, 4, L], BF16)
        kT = sb("sbk", [128, 4, L], BF16)
        v = sb("sbv", [128, NT, 512], BF16)
        onesw = sb("onesw", [128, L], BF16)
        kb.op("dve", lambda e: e.memset(onesw[:], 1.0), W=[onesw])
        kb.dma("sp", qT[:], S["sbqT"].t.rearrange("c p t -> p c t"), W=[qT])
        kb.dma("sp", kT[:], S["sbkT"].t.rearrange("c p t -> p c t"), W=[kT])
        kb.dma("sp", v[:], S["sbv"].t.rearrange("(t p) f -> p t f", p=128), W=[v])
        e1b = [sb("e1", [128, L]) for _ in range(3)]
        spb = [sb("sp", [128, L]) for _ in range(2)]
        Fb = [sb("F", [128, L + 1]) for _ in range(2)]
        attb = [sb("att", [128, L], BF16) for _ in range(2)]
        attT = [sb("attT", [128, NT, 128], BF16) for _ in range(2)]
        ftn = [sb("ftn", [128, 1]) for _ in range(2)]
        yst = [sb("yst", [128, 128], BF16) for _ in range(4)]
        for F in Fb:
            kb.op("dve", lambda e, F=F: e.memset(F[:, 0:1], 0.0), W=[F])
        iters = [(qt, h) for qt in range(NT) for h in range(8)]

        def s1(i):
            qt, h = iters[i]
            W = 128 * (qt + 1)
            nch = (W + 511) // 512
            dsl = slice(qt * 128, (qt + 1) * 128)
            hp, hc = (h % 2) * 64, h // 2
            e1, sp_ = e1b[i % 3], spb[i % 2]
            for ch in range(nch):
                n = min(512, W - ch * 512)
                p = kb.ps()
                kb.op("pe", lambda e, p=p, ch=ch, n=n: e.matmul(p[:, 0:n], qT[hp:hp + 64, hc, dsl], kT[hp:hp + 64, hc, ch * 512:ch * 512 + n],
                                                                start=True, stop=True), R=[qT, kT], W=[p])
                kb.op("act", lambda e, p=p, ch=ch, n=n: e.activation(out=e1[:, ch * 512:ch * 512 + n], in_=p[:, 0:n], func=AF.Exp, scale=0.125),
                      R=[p], Wd=[e1])
            kb.op("act", lambda e: e.activation(out=sp_[:, 0:W], in_=e1[:, 0:W], func=AF.Ln, bias=1.0), R=[e1], W=[sp_])

        def s2(i):
            qt, h = iters[i]
            k = i % 2
            W = 128 * (qt + 1)
            dsl = slice(qt * 128, (qt + 1) * 128)
            sp_, F = spb[k], Fb[k]
            kb.op("dve", lambda e: e.tensor_tensor(out=sp_[:, dsl], in0=sp_[:, dsl], in1=g.strl[:], op=ALU.mult), R=[sp_, g.strl], W=[sp_])
            kb.op("dve", lambda e: e.memset(F[:, 0:1], 0.0), W=[F])
            kb.op("dve", lambda e: e.tensor_tensor_scan(out=F[:, 1:W + 1], data0=onesw[:, 0:W], data1=sp_[:, 0:W], initial=0.0,
                                                        op0=ALU.mult, op1=ALU.add), R=[onesw, sp_], W=[F])
            kb.op("dve", lambda e: e.tensor_scalar(out=ftn[k][:], in0=F[:, W:W + 1], scalar1=-1.0, scalar2=None, op0=ALU.mult),
                  R=[F], W=[ftn[k]])

        def s3a(i):
            qt, h = iters[i]
            k = i % 2
            W = 128 * (qt + 1)
            F = Fb[k]
            kb.op("act", lambda e: e.activation(out=F[:, 0:W], in_=F[:, 0:W], func=AF.Exp, bias=ftn[k][:, 0:1]), R=[F, ftn[k]], W=[F])

        def s3b(i):
            qt, h = iters[i]
            k = i % 2
            W = 128 * (qt + 1)
            dsl = slice(qt * 128, (qt + 1) * 128)
            e1, F, att = e1b[i % 3], Fb[k], attb[k]
            kb.op("dve", lambda e: e.tensor_tensor(out=att[:, 0:W], in0=e1[:, 0:W], in1=F[:, 0:W], op=ALU.mult), R=[e1, F], W=[att])
            kb.op("dve", lambda e: e.tensor_tensor(out=att[:, dsl], in0=att[:, dsl], in1=g.strlb[:], op=ALU.mult), R=[att, g.strlb], W=[att])

        def s3c(i):
            qt, h = iters[i]
            k = i % 2
            att, aT = attb[k], attT[k]
            for b0 in range(0, qt + 1, 8):
                nb = min(8, qt + 1 - b0)
                pt = kb.ps()
                ptb = pt[:, :].bitcast(BF16)
                for j in range(nb):
                    kb.op("pe", lambda e, j=j, b0=b0: e.transpose(ptb[:, j * 128:(j + 1) * 128], att[:, (b0 + j) * 128:(b0 + j + 1) * 128], g.identb[:]),
                          R=[att, g.identb], W=[pt])
                kb.op("act", lambda e, b0=b0, nb=nb: e.activation(out=aT[:, b0:b0 + nb, :], in_=ptb[:, 0:nb * 128].rearrange("p (j t) -> p j t", j=nb),
                                                                 func=AF.Copy), R=[pt], Wd=[aT])

        def s3d(i):
            qt, h = iters[i]
            k = i % 2
            dsl = slice(qt * 128, (qt + 1) * 128)
            hp, hc = (h % 2) * 64, h // 2
            aT = attT[k]
            py = kb.ps()
            for sbk in range(qt + 1):
                kb.op("pe", lambda e, sbk=sbk: e.matmul(py[0:64, 0:128], v[:, sbk, h * 64:(h + 1) * 64], aT[:, sbk, :],
                                                        start=(sbk == 0), stop=(sbk == qt)), R=[v, aT], W=[py])
            pys[i] = py

        def s3e(i):
            qt, h = iters[i]
            dsl = slice(qt * 128, (qt + 1) * 128)
            hp, hc = (h % 2) * 64, h // 2
            py = pys.pop(i)
            ys = yst[i % 4]
            kb.op("dve", lambda e: e.tensor_copy(out=ys[hp:hp + 64, :], in_=py[0:64, 0:128]), R=[py], W=[ys])
            kb.dma("sp", S["yT"].t[8 + hc, hp:hp + 64, dsl], ys[hp:hp + 64, :], R=[ys])

        n_it = len(iters)

        def run(fn, i):
            if 0 <= i < n_it:
                fn(i)
        pys = {}
        for step in range(n_it + 4):
            run(s3d, step - 4)
            run(s3a, step - 2)
            run(s1, step)
            run(s3c, step - 3)
            run(s2, step - 1)
            run(s3e, step - 4)
            run(s3b, step - 2)
        kb.barrier()


NBIS = 16


def phase_dsa(g, l):
    kb, nc = g.kb, g.nc
    S = g.scr
    with ExitStack() as ph:
        def sb(name, shape, dt_=F32):
            return kb.sb(ph, name, shape, dt_)
        dk = sb("dk", [64, 2, L], BF16)
        ik = sb("ik", [128, L], BF16)
        vaug = sb("vaug", [128, NT, 2, 128], BF16)
        kb.dma("sp", dk[:], S["dkT"].t.rearrange("h d t -> d h t"), W=[dk])
        kb.dma("sp", ik[0:64, :], S["ikT"].t[0], W=[ik])
        kb.dma("sp", ik[64:128, :], S["ikT"].t[0], W=[ik])
        kb.op("dve", lambda e: e.memset(vaug[:], 1.0), W=[vaug])
        for gg in range(2):
            kb.dma("sp", vaug[:, :, gg, 0:64], S["dv"].t[:, gg * 64:(gg + 1) * 64].rearrange("(t p) d -> p t d", p=128), W=[vaug])
        dqt = [sb("dqt", [64, 8, 128], BF16) for _ in range(2)]
        iqt = [sb("iqt", [128, 4, 128], BF16) for _ in range(2)]
        score = [sb("score", [128, L]) for _ in range(2)]
        rl = [sb("rl", [128, 512], BF16) for _ in range(4)]
        wabs = [sb("wabs", [128, 8]) for _ in range(2)]
        wsgn = [sb("wsgn", [128, 8]) for _ in range(2)]
        Dg = [sb("Dg", [128, 8, 128], BF16) for _ in range(2)]
        junk = sb("junk", [128, L], BF16)
        maskb = [sb("maskb", [128, L], BF16) for _ in range(2)]
        maskT = [sb("maskT", [128, NT, 128], BF16) for _ in range(2)]
        Eb = [sb("E", [128, 512], BF16) for _ in range(3)]
        M_ = [sb("M", [128, 1]) for _ in range(2)]
        A_ = [sb("A", [128, NBIS + 1]) for _ in range(2)]
        mid = [sb("mid", [128, 1]) for _ in range(2)]
        cnt = [sb("cnt", [128, 1]) for _ in range(2)]
        sela = [sb("sela", [128, 1]) for _ in range(2)]
        R0 = [sb("R0", [64, 512]) for _ in range(2)]
        ytmp = [sb("ytmp", [64, 512], BF16) for _ in range(2)]
        yraw = [sb("yraw", [64, 512]) for _ in range(2)]
        yst = [sb("dyst", [128, 4, 128], BF16) for _ in range(2)]
        cn = dict(rl=0, E=0)

        def stage_I(qt):
            k = qt % 2
            W = 128 * (qt + 1)
            dsl = slice(qt * 128, (qt + 1) * 128)
            if qt < 2:
                return
            iq_src = S["iqT"].t.rearrange("(hp e) d t -> e d hp t", e=2)
            for e_ in range(2):
                kb.dma("sp", iqt[k][e_ * 64:(e_ + 1) * 64, :, :], iq_src[e_][:, :, dsl], W=[iqt[k]])
            sc = score[k]
            nch = (W + 511) // 512
            wv = g.wix_tok[:, qt, :]
            kb.op("act", lambda e: e.activation(out=wabs[k][:], in_=wv, func=AF.Abs), R=[g.wix_tok], W=[wabs[k]])
            kb.op("dve", lambda e: e.tensor_scalar(out=wsgn[k][:], in0=wv, scalar1=0.0, scalar2=2.0, op0=ALU.is_ge, op1=ALU.mult),
                  R=[g.wix_tok], W=[wsgn[k]])
            kb.op("dve", lambda e: e.tensor_scalar(out=wsgn[k][:], in0=wsgn[k][:], scalar1=-1.0, scalar2=None, op0=ALU.add),
                  R=[wsgn[k]], W=[wsgn[k]])
            for h in range(8):
                kb.op("dve", lambda e, h=h: e.tensor_scalar(out=Dg[k][:, h, :], in0=g.identb[:], scalar1=wsgn[k][:, h:h + 1], scalar2=None,
                                                            op0=ALU.mult), R=[g.identb, wsgn[k]], Wd=[Dg[k]])
            items = [(ch, h) for ch in range(nch) for h in range(8)]
            pend = {}
            pscs = {}

            def mm(j):
                ch, h = items[j]
                n = min(512, W - ch * 512)
                p = kb.ps()
                hp_ = (h % 2) * 64
                kb.op("pe", lambda e: e.matmul(p[:, 0:n], iqt[k][hp_:hp_ + 64, h // 2, :], ik[hp_:hp_ + 64, ch * 512:ch * 512 + n],
                                               start=True, stop=True), R=[iqt[k], ik], W=[p])
                r = rl[cn["rl"] % 4]
                cn["rl"] += 1
                kb.op("act", lambda e: e.activation(out=r[:, 0:n], in_=p[:, 0:n], func=AF.Relu, scale=wabs[k][:, h:h + 1]),
                      R=[p, wabs[k]], W=[r])
                pend[j] = r

            def acc(j):
                ch, h = items[j]
                n = min(512, W - ch * 512)
                r = pend.pop(j)
                if h == 0:
                    pscs[ch] = kb.ps(pin=True)
                psc = pscs[ch]
                kb.op("pe", lambda e: e.matmul(psc[:, 0:n], Dg[k][:, h, :], r[:, 0:n], start=(h == 0), stop=(h == 7)),
                      R=[Dg[k], r], W=[psc])
                if h == 7:
                    kb.unpin(psc)
                    kb.op("act", lambda e: e.activation(out=sc[:, ch * 512:ch * 512 + n], in_=psc[:, 0:n], func=AF.Copy), R=[psc], Wd=[sc])

            for j in range(0, len(items) + 2, 2):
                for jj in (j, j + 1):
                    if jj < len(items):
                        mm(jj)
                for jj in (j - 2, j - 1):
                    if 0 <= jj < len(items):
                        acc(jj)

        def stage_B(qt):
            k = qt % 2
            W = 128 * (qt + 1)
            dsl = slice(qt * 128, (qt + 1) * 128)
            mT = maskT[k]
            if qt < 2:
                if qt == 1:
                    kb.op("dve", lambda e: e.memset(mT[:, 0, :], 0.0), W=[mT])
                kb.op("dve", lambda e: e.tensor_copy(out=mT[:, qt, :], in_=g.negmb[:]), R=[g.negmb], W=[mT])
                return
            sc = score[k]
            kb.op("dve", lambda e: e.tensor_reduce(out=M_[k][:], in_=sc[:, 0:W], axis=AX.X, op=ALU.max, apply_absolute_value=True),
                  R=[sc], W=[M_[k]])
            kb.op("dve", lambda e: e.tensor_scalar(out=A_[k][:, 0:NBIS], in0=g.pow2[:, 0:NBIS], scalar1=M_[k][:, 0:1], scalar2=None, op0=ALU.mult),
                  R=[g.pow2, M_[k]], W=[A_[k]])
            kb.op("dve", lambda e: e.tensor_copy(out=A_[k][:, NBIS:NBIS + 1], in_=A_[k][:, NBIS - 1:NBIS]), R=[A_[k]], W=[A_[k]])
            kb.op("dve", lambda e: e.tensor_tensor(out=sc[:, dsl], in0=sc[:, dsl], in1=g.negup[:], op=ALU.add), R=[sc, g.negup], W=[sc])
            kb.op("dve", lambda e: e.memset(mid[k][:], 0.0), W=[mid[k]])
            for it in range(NBIS):
                kb.op("dve", lambda e: e.tensor_scalar(out=junk[:, 0:W], in0=sc[:, 0:W], scalar1=mid[k][:, 0:1], scalar2=0.0,
                                                       op0=ALU.is_ge, op1=ALU.add, accum_out=cnt[k][:]),
                      R=[sc, mid[k]], W=[junk, cnt[k]])
                kb.op("dve", lambda e, it=it: e.tensor_scalar(out=sela[k][:], in0=cnt[k][:], scalar1=255.5, scalar2=A_[k][:, it:it + 1],
                                                              op0=ALU.is_ge, op1=ALU.mult), R=[cnt[k], A_[k]], W=[sela[k]])
                kb.op("dve", lambda e, it=it: e.scalar_tensor_tensor(out=mid[k][:], in0=sela[k][:], scalar=A_[k][:, it + 1:it + 2], in1=mid[k][:],
                                                                     op0=ALU.subtract, op1=ALU.add), R=[sela[k], A_[k], mid[k]], W=[mid[k]])
            mb = maskb[k]
            kb.op("dve", lambda e: e.tensor_scalar(out=mb[:, 0:W], in0=sc[:, 0:W], scalar1=mid[k][:, 0:1], scalar2=-30000.0,
                                                   op0=ALU.is_lt, op1=ALU.mult), R=[sc, mid[k]], W=[mb])

        def stage_T(qt):
            if qt < 2:
                return
            k = qt % 2
            mb, mT = maskb[k], maskT[k]
            for b0 in range(0, qt + 1, 8):
                nb = min(8, qt + 1 - b0)
                pt = kb.ps()
                ptb = pt[:, :].bitcast(BF16)
                for j in range(nb):
                    kb.op("pe", lambda e, j=j, b0=b0: e.transpose(ptb[:, j * 128:(j + 1) * 128], mb[:, (b0 + j) * 128:(b0 + j + 1) * 128], g.identb[:]),
                          R=[mb, g.identb], W=[pt])
                kb.op("act", lambda e, b0=b0, nb=nb: e.activation(out=mT[:, b0:b0 + nb, :], in_=ptb[:, 0:nb * 128].rearrange("p (j t) -> p j t", j=nb),
                                                                 func=AF.Copy), R=[pt], Wd=[mT])

        def stage_A(qt):
            k = qt % 2
            dsl = slice(qt * 128, (qt + 1) * 128)
            mT = maskT[k]
            ys = yst[k]
            if qt + 1 < NT:
                kb.dma("sp", dqt[1 - k][:], S["dqT"].t[:, :, (qt + 1) * 128:(qt + 2) * 128].rearrange("h d t -> d h t"), W=[dqt[1 - k]])
            for gq in range(2):
                pO = kb.ps(pin=True)
                Es = {}

                def qk(sbk):
                    pS = kb.ps()
                    kb.op("pe", lambda e: e.matmul(pS[:, :], dk[0:64, gq, sbk * 128:(sbk + 1) * 128], dqt[k][0:64, 4 * gq:4 * gq + 4, :],
                                                   start=True, stop=False), R=[dk, dqt[k]], W=[pS])
                    kb.op("pe", lambda e: e.matmul(pS[:, :], g.identb[:], mT[:, sbk, :].unsqueeze(1).broadcast_to([128, 4, 128]),
                                                   start=False, stop=True), R=[g.identb, mT], W=[pS])
                    E = Eb[cn["E"] % 3]
                    cn["E"] += 1
                    kb.op("act", lambda e: e.activation(out=E[:], in_=pS[:, :], func=AF.Exp, scale=0.125), R=[pS], W=[E])
                    Es[sbk] = E

                def av(sbk):
                    E = Es.pop(sbk)
                    kb.op("pe", lambda e: e.matmul(pO[:, :], vaug[:, sbk, gq, :], E[:], start=(sbk == 0), stop=(sbk == qt)),
                          R=[vaug, E], W=[pO])

                for sbk in range(qt + 2):
                    if sbk <= qt:
                        qk(sbk)
                    if sbk >= 1:
                        av(sbk - 1)
                def norm(pO=pO, gq=gq, ys=ys, dsl=dsl, last=(gq == 1)):
                    kb.unpin(pO)
                    r0, yt = R0[gq], ytmp[gq]
                    kb.op("act", lambda e: e.activation(out=r0[:], in_=pO[64:128, :], func=AF.Ln), R=[pO], W=[r0])
                    kb.op("act", lambda e: e.activation(out=r0[:], in_=r0[:], func=AF.Exp, scale=-1.0), R=[r0], W=[r0])
                    yr = yraw[gq]
                    kb.op("act", lambda e: e.activation(out=yr[:], in_=pO[0:64, :], func=AF.Copy), R=[pO], W=[yr])
                    kb.op("pool", lambda e: e.tensor_tensor(out=yt[:], in0=yr[:], in1=r0[:], op=ALU.mult), R=[yr, r0], W=[yt])
                    for hh in range(4):
                        h = 4 * gq + hh
                        hp, hc = (h % 2) * 64, h // 2
                        kb.op("act", lambda e, hh=hh, hp=hp, hc=hc: e.activation(out=ys[hp:hp + 64, hc, :], in_=yt[:, hh * 128:(hh + 1) * 128], func=AF.Copy),
                              R=[yt], Wd=[ys])
                    if last:
                        kb.dma("sp", S["yT"].t[12:16, :, dsl].rearrange("c p t -> p c t"), ys[:], R=[ys])
                if pending:
                    pending.pop(0)()
                pending.append(norm)

        pending = []
        kb.dma("sp", dqt[0][:], S["dqT"].t[:, :, 0:128].rearrange("h d t -> d h t"), W=[dqt[0]])
        stage_I(0)
        stage_B(0)
        stage_T(0)
        stage_I(1)
        modw = [sb("adaw", [128, KC, 512], BF16) for _ in range(2)] if g.dsa_hook is not None else None
        for qt in range(NT):
            if qt + 2 < NT:
                stage_I(qt + 2)
            stage_A(qt)
            if modw is not None and 1 <= qt < 13:
                mod_piece_load(g, g.dsa_hook, qt - 1, modw[(qt - 1) % 2])
            if modw is not None and 2 <= qt < 14:
                mod_piece_mm(g, g.dsa_hook, qt - 2, modw[(qt - 2) % 2])
            if qt + 1 < NT:
                stage_B(qt + 1)
                stage_T(qt + 1)
        while pending:
            pending.pop(0)()
        kb.barrier()


def phase_merge(g, l):
    kb, nc = g.kb, g.nc
    S = g.scr
    with ExitStack() as ph:
        def sb(name, shape, dt_=F32):
            return kb.sb(ph, name, shape, dt_)
        yT = sb("yTall", [128, 16, L], BF16)
        yTb = [kb.buf() for _ in range(4)]
        for q in range(4):
            kb.dma("sp", yT[:, q * 4:(q + 1) * 4, :], S["yT"].t[q * 4:(q + 1) * 4].rearrange("c p t -> p c t"), W=[yTb[q]])
        mT = sb("mergedT", [128, KC, L], BF16)
        mTb = [[kb.buf() for tb in range(4)] for c in range(KC)]
        wbr = [sb("wbr", [128, 16, 256], BF16) for _ in range(2)]
        gt = [sb("gt", [128, 3, 512], BF16) for _ in range(2)]
        gbr = [sb("gbr", [128, 3, 512], BF16) for _ in range(2)]
        it = 0
        for c2 in range(4):
            w = wbr[c2 % 2]
            csl = slice(c2 * 256, (c2 + 1) * 256)
            kb.dma("pool", w[:, 0:8, :], g.inp["w_br_ssd"].t[l][:, csl].rearrange("(kc p) n -> p kc n", p=128), W=[w])
            kb.dma("pool", w[:, 8:12, :], g.inp["w_br_sb"].t[l][:, csl].rearrange("(kc p) n -> p kc n", p=128), W=[w])
            kb.dma("pool", w[:, 12:16, :], g.inp["w_br_dsa"].t[l][:, csl].rearrange("(kc p) n -> p kc n", p=128), W=[w])
            for cl in range(2):
                c = c2 * 2 + cl
                for tb in range(4):
                    k = it % 2
                    it += 1
                    tsl = slice(tb * 512, (tb + 1) * 512)
                    for br in range(3):
                        kb.dma("sp", gt[k][:, br, :], S["gT"].t[br * 8 + c, :, tsl], W=[gt[k]])
                    pbr = []
                    for br, (k0, k1) in enumerate(((0, 8), (8, 12), (12, 16))):
                        p = kb.ps()
                        for kc in range(k0, k1):
                            kb.op("pe", lambda e, p=p, kc=kc, k0=k0, k1=k1: e.matmul(
                                p[:, :], w[:, kc, cl * 128:(cl + 1) * 128], yT[:, kc, tsl], start=(kc == k0), stop=(kc == k1 - 1)),
                                R=[w, yTb[kc // 4]], W=[p])
                        pbr.append(p)
                    gb = gbr[k]
                    for br in range(3):
                        kb.op("dve", lambda e, br=br: e.tensor_tensor(out=gb[:, br, :], in0=pbr[br][:, :], in1=gt[k][:, br, :], op=ALU.mult),
                              R=[pbr[br], gt[k]], Wd=[gb])
                    pm = kb.ps()
                    for br in range(3):
                        kb.op("pe", lambda e, br=br: e.matmul(pm[:, :], g.identb[:], gb[:, br, :], start=(br == 0), stop=(br == 2)),
                              R=[g.identb, gb], W=[pm])
                    kb.op("act", lambda e: e.activation(out=mT[:, c, tsl], in_=pm[:, :], func=AF.Copy), R=[pm], W=[mTb[c][tb]])
        wo = [sb("wo", [128, KC, 256], BF16) for _ in range(2)]
        for c2 in range(4):
            w = wo[c2 % 2]
            kb.dma("pool", w[:], g.inp["w_out"].t[l][:, c2 * 256:(c2 + 1) * 256].rearrange("(kc p) n -> p kc n", p=128), W=[w])
            for cl in range(2):
                c = c2 * 2 + cl
                for tb in range(4):
                    tsl = slice(tb * 512, (tb + 1) * 512)
                    p = kb.ps()
                    for kc in range(KC):
                        kb.op("pe", lambda e, p=p, kc=kc: e.matmul(p[:, :], w[:, kc, cl * 128:(cl + 1) * 128], mT[:, kc, tsl],
                                                                   start=(kc == 0), stop=(kc == KC - 1)), R=[w, mTb[kc][tb]], W=[p])
                    xs = g.xT[:, c, tsl]
                    kb.op("dve", lambda e, p=p, c=c, xs=xs: e.scalar_tensor_tensor(
                        out=xs, in0=p[:, :], scalar=g.modT[:, 16 + c:17 + c], in1=xs, op0=ALU.mult, op1=ALU.add),
                        R=[p, g.modT, g.xTb[c][tb]], W=[g.xTb[c][tb]])
        kb.barrier()


def phase_mlp(g, l):
    kb = g.kb
    with ExitStack() as ph:
        wup = [kb.sb(ph, f"wup{i}", [128, KC, 512], BF16) for i in range(2)]
        wdn = [kb.sb(ph, f"wdn{i}", [128, 4, D], BF16) for i in range(2)]
        act = [kb.sb(ph, f"mact{i}", [128, 4, L], BF16) for i in range(2)]
        actb = [[[kb.buf() for tb in range(4)] for hc in range(4)] for i in range(2)]
        rl = [kb.sb(ph, f"mrl{i}", [128, 512], F32) for i in range(2)]
        nrl = 0
        for gi in range(8):
            wu, wd, a, ab = wup[gi % 2], wdn[gi % 2], act[gi % 2], actb[gi % 2]
            load_w(g, wu, g.inp["w_up"].t[l][:, gi * 512:(gi + 1) * 512])
            load_w(g, wd, g.inp["w_down"].t[l][gi * 512:(gi + 1) * 512, :])
            for tb in range(4):
                for hc in range(4):
                    p = kb.ps()
                    for kc in range(KC):
                        kb.op("pe", lambda e, p=p, kc=kc, hc=hc: e.matmul(
                            p[:, :], wu[:, kc, hc * 128:(hc + 1) * 128], g.hT[:, kc, tb * 512:(tb + 1) * 512],
                            start=(kc == 0), stop=(kc == KC - 1)), R=[wu, g.hTb[kc][tb]], W=[p])
                    r = rl[nrl % 2]
                    nrl += 1
                    kb.op("act", lambda e, p=p, r=r: e.activation(out=r[:], in_=p[:, :], func=AF.Relu), R=[p], W=[r])
                    kb.op("act", lambda e, r=r, hc=hc: e.activation(out=a[:, hc, tb * 512:(tb + 1) * 512], in_=r[:], func=AF.Square),
                          R=[r], W=[ab[hc][tb]])
            import os
            if os.environ.get("MLP_UP_ONLY"):
                continue
            for tb in range(4):
                for c in range(KC):
                    p = kb.ps()
                    for hc in range(4):
                        kb.op("pe", lambda e, p=p, hc=hc, c=c: e.matmul(
                            p[:, :], wd[:, hc, c * 128:(c + 1) * 128], a[:, hc, tb * 512:(tb + 1) * 512],
                            start=(hc == 0), stop=(hc == 3)), R=[wd, ab[hc][tb]], W=[p])
                    xs = g.xT[:, c, tb * 512:(tb + 1) * 512]
                    kb.op("dve", lambda e, p=p, c=c, xs=xs: e.scalar_tensor_tensor(
                        out=xs, in0=p[:, :], scalar=g.modT[:, 40 + c:41 + c], in1=xs, op0=ALU.mult, op1=ALU.add),
                        R=[p, g.modT, g.xTb[c][tb]], W=[g.xTb[c][tb]])
        kb.barrier()


def rstd_bcast(kb, ph, xT, xTb, tb, onesb, sq, rs):
    p = kb.ps()
    for c in range(KC):
        s = sq[c % 2]
        kb.op("act", lambda e, s=s, c=c: e.activation(out=s[:], in_=xT[:, c, tb * 512:(tb + 1) * 512], func=AF.Square),
              R=[xTb[c][tb]], W=[s])
        kb.op("pe", lambda e, s=s, c=c, p=p: e.matmul(p[:, :], onesb[:], s[:], start=(c == 0), stop=(c == KC - 1)),
              R=[s, onesb], W=[p])
    kb.op("dve", lambda e: e.tensor_scalar(out=rs[:], in0=p[:, :], scalar1=1.0 / D, scalar2=EPS, op0=ALU.mult, op1=ALU.add),
          R=[p], W=[rs])
    kb.op("act", lambda e: e.activation(out=rs[:], in_=rs[:], func=AF.Ln), R=[rs], W=[rs])
    kb.op("act", lambda e: e.activation(out=rs[:], in_=rs[:], func=AF.Exp, scale=-0.5), R=[rs], W=[rs])


def final_norm(kb, nc, xT, xTb, fnw, onesb, ident, out_d):
    with ExitStack() as ph:
        sq = [kb.sb(ph, f"fsq{i}", [128, 512], BF16) for i in range(2)]
        rs = kb.sb(ph, "frs", [128, 512], F32)
        yT = [kb.sb(ph, f"fyT{i}", [128, 512], F32) for i in range(2)]
        ost = [kb.sb(ph, f"fost{i}", [128, 4, D], F32) for i in range(2)]
        for tb in range(4):
            rstd_bcast(kb, ph, xT, xTb, tb, onesb, sq, rs)
            o = ost[tb % 2]
            for c in range(KC):
                y = yT[c % 2]
                kb.op("dve", lambda e, y=y, c=c: e.scalar_tensor_tensor(
                    out=y[:], in0=xT[:, c, tb * 512:(tb + 1) * 512], scalar=fnw[:, c:c + 1], in1=rs[:],
                    op0=ALU.mult, op1=ALU.mult), R=[xTb[c][tb], fnw, rs], W=[y])
                p = kb.ps()
                for j in range(4):
                    kb.op("pe", lambda e, y=y, j=j, p=p: e.transpose(
                        p[:, j * 128:(j + 1) * 128], y[:, j * 128:(j + 1) * 128], ident[:]), R=[y, ident], W=[p])
                src = p[:, :].rearrange("p (j f) -> p j f", j=4)
                dst = o[:, :, c * 128:(c + 1) * 128]
                kb.op("act", lambda e, dst=dst, src=src: e.activation(out=dst, in_=src, func=AF.Copy), R=[p], Wd=[o])
            kb.dma("sp", out_d.t[tb * 512:(tb + 1) * 512, :].rearrange("(j p) d -> p j d", p=128), o[:], R=[o], W=[out_d])
        kb.barrier()


_NC_CACHE = {}


def make_in_maps(inputs):
    nb = inputs["x"].shape[0]
    inputs = dict(inputs)
    inputs.update(host_consts())
    shared = {n: np.ascontiguousarray(np.asarray(inputs[n], dtype=np.float32)) for n in IN_SHAPES if n not in ("x", "c")}
    in_maps = []
    for b in range(nb):
        m = dict(shared)
        m["x"] = np.ascontiguousarray(inputs["x"][b])
        m["c"] = np.ascontiguousarray(inputs["c"][b])
        in_maps.append(m)
    return in_maps


def kernel(**inputs):
    nb = inputs["x"].shape[0]
    if "nc" not in _NC_CACHE:
        _NC_CACHE["nc"] = build()[0]
    nc = _NC_CACHE["nc"]
    res = run_bass_kernel_spmd(nc, make_in_maps(inputs), core_ids=list(range(nb)))
    return np.stack([r["out"] for r in res.results], axis=0)
```

```python
import math
from contextlib import ExitStack

import numpy as np
import concourse.bass as bass
import concourse.mybir as mybir
from concourse.bass_utils import run_bass_kernel_spmd

F32 = mybir.dt.float32
BF16 = mybir.dt.bfloat16
I32 = mybir.dt.int32
AF = mybir.ActivationFunctionType
ALU = mybir.AluOpType
AX = mybir.AxisListType

D = 1024
L = 2048
DEPTH = 2
NT = L // 128
KC = D // 128
EPS = 1e-6
import os
NDS = int(os.environ.get("NDS", "12"))
STRICT = bool(int(os.environ.get("KSTRICT", "1")))


class Buf:
    __slots__ = ("name", "w", "r", "excl")

    def __init__(self, name):
        self.name = name
        self.excl = False
        self.w = None
        self.r = {}


class T:
    def __init__(self, t, buf):
        self.t = t
        self.b = buf

    def __getitem__(self, k):
        return self.t[k]


class KB:
    def __init__(self, nc):
        self.nc = nc
        self.es = ExitStack()
        self.sems = {}
        self.engs = {}
        for name, e in (("pe", nc.tensor), ("act", nc.scalar), ("dve", nc.vector),
                        ("pool", nc.gpsimd), ("sp", nc.sync)):
            key = "s_" + name
            self.sems[key] = self.es.enter_context(nc.semaphore(key))
            self.engs[name] = dict(e=e, key=key, cnt=0, seen={})
        self.dq = {}
        for q in ("sp", "pool"):
            keys = []
            for i in range(NDS):
                key = f"d_{q}{i}"
                self.sems[key] = self.es.enter_context(nc.semaphore(key))
                keys.append(key)
            self.dq[q] = dict(keys=keys, cnt=[0] * NDS, nxt=0)
        self.nbuf = 0
        self.psum = []
        self.ps_next = 0
        self.pinned = set()

    def buf(self, name=None):
        self.nbuf += 1
        return Buf(name or f"b{self.nbuf}")

    def sb(self, es, name, shape, dtype):
        self.nbuf += 1
        name = f"{name}_{self.nbuf}"
        t = es.enter_context(self.nc.sbuf_tensor(name, list(shape), dtype))
        return T(t, self.buf(name))

    def dram(self, name, shape, dtype, kind="Internal"):
        t = self.nc.dram_tensor(name, list(shape), dtype, kind=kind)
        return T(t.ap(), self.buf(name))

    def init_psum(self):
        for i in range(8):
            t = self.es.enter_context(self.nc.psum_tensor(f"ps{i}", [128, 512], F32))
            self.psum.append(T(t, self.buf(f"ps{i}")))
            self.psum[-1].b.excl = True

    def ps(self, pin=False):
        while True:
            i = self.ps_next
            self.ps_next = (self.ps_next + 1) % 8
            if i not in self.pinned:
                break
        if pin:
            self.pinned.add(i)
        return self.psum[i]

    def unpin(self, p):
        self.pinned.discard(self.psum.index(p))

    def _deps(self, own_key, R, W, same_raw, Wd=()):
        deps = {}

        def add(ev):
            if ev is None:
                return
            k, v = ev
            if deps.get(k, 0) < v:
                deps[k] = v

        for t in R:
            b = t.b if isinstance(t, T) else t
            if b.w is not None and (b.w[0] != own_key or same_raw):
                add(b.w)
            if b.excl:
                for k, v in b.r.items():
                    if k != own_key:
                        add((k, v))
        for t in W:
            b = t.b if isinstance(t, T) else t
            if b.w is not None and (b.w[0] != own_key or (STRICT and same_raw)):
                add(b.w)
            for k, v in b.r.items():
                if k != own_key or (STRICT and same_raw):
                    add((k, v))
        for t in Wd:
            b = t.b if isinstance(t, T) else t
            if b.w is not None and b.w[0] != own_key:
                add(b.w)
            for k, v in b.r.items():
                if k != own_key or (STRICT and same_raw):
                    add((k, v))
        return deps

    def _mark(self, ev, R, W):
        k, v = ev
        for t in R:
            b = t.b if isinstance(t, T) else t
            if b.r.get(k, 0) < v:
                b.r[k] = v
        for t in W:
            b = t.b if isinstance(t, T) else t
            b.w = ev
            b.r = {}

    def _wait(self, eng, deps):
        for k, v in deps.items():
            if eng["seen"].get(k, 0) < v:
                eng["e"].wait_ge(self.sems[k], v)
                eng["seen"][k] = v

    def op(self, engname, fn, R=(), W=(), Wd=()):
        eng = self.engs[engname]
        deps = self._deps(eng["key"], R, W, same_raw=(engname != "pe"), Wd=Wd)
        W = list(W) + list(Wd)
        self._wait(eng, deps)
        ins = fn(eng["e"])
        eng["cnt"] += 1
        ins.then_inc(self.sems[eng["key"]], 1)
        self._mark((eng["key"], eng["cnt"]), R, W)
        return ins

    def dma(self, q, out, in_, R=(), W=(), **kw):
        eng = self.engs[q]
        dq = self.dq[q]
        deps = self._deps(None, R, W, same_raw=True)
        i = dq["nxt"]
        dq["nxt"] = (i + 1) % NDS
        key = dq["keys"][i]
        if dq["cnt"][i] > 0:
            deps[key] = max(deps.get(key, 0), dq["cnt"][i])
        self._wait(eng, deps)
        dq["cnt"][i] += 16
        eng["e"].dma_start(out=out, in_=in_, **kw).then_inc(self.sems[key], 16)
        self._mark((key, dq["cnt"][i]), R, W)

    def barrier(self):
        allev = {}
        for name, eng in self.engs.items():
            if eng["cnt"] > 0:
                allev[eng["key"]] = eng["cnt"]
        for q, dq in self.dq.items():
            for key, c in zip(dq["keys"], dq["cnt"]):
                if c > 0:
                    allev[key] = c
        for name, eng in self.engs.items():
            deps = {k: v for k, v in allev.items() if k != eng["key"]}
            self._wait(eng, deps)

    def finish(self, out_bufs):
        eng = self.engs["sp"]
        deps = {}
        for t in out_bufs:
            b = t.b if isinstance(t, T) else t
            if b.w is not None:
                deps[b.w[0]] = max(deps.get(b.w[0], 0), b.w[1])
        self._wait(eng, deps)
        self.barrier()


IN_SHAPES = {
    "x": [L, D], "c": [D], "norm1_w": [DEPTH, D], "ada_w": [DEPTH, D, 6 * D], "ada_b": [DEPTH, 6 * D],
    "w_in": [DEPTH, D, 8536], "conv_w": [DEPTH, 4, 1536], "conv_b": [DEPTH, 1536], "dt_bias": [DEPTH, 16],
    "a_log": [DEPTH, 16], "d_skip": [DEPTH, 16], "ssd_norm_w": [DEPTH, D], "w_br_ssd": [DEPTH, D, D],
    "w_br_sb": [DEPTH, 512, D], "w_br_dsa": [DEPTH, 512, D], "w_out": [DEPTH, D, D], "norm2_w": [DEPTH, D],
    "w_up": [DEPTH, D, 4 * D], "w_down": [DEPTH, 4 * D, D], "final_norm_w": [D],
    "rope_cs": [L, 16],
}


def host_consts():
    half = 8
    inv_freq = np.exp(np.arange(half, dtype=np.float32) * np.float32(-2.0 * math.log(500000.0) / 16)).astype(np.float32)
    ang = np.arange(L, dtype=np.float32)[:, None] * inv_freq[None, :]
    return {"rope_cs": np.concatenate([np.cos(ang), np.sin(ang)], axis=1).astype(np.float32)}


class G:
    pass


def build(nlayers=DEPTH, dbg=(), skip_mixer=False, phases=("mod", "inproj", "ssd", "sb", "dsa", "merge", "norm2", "mlp")):
    nc = bass.Bass("TRN2", target_bir_lowering=False)
    kb = KB(nc)
    es = kb.es
    kb.init_psum()
    g = G()
    g.nc, g.kb, g.es, g.dbg = nc, kb, es, set(dbg)
    g.inp = {n: kb.dram(n, shp, F32, kind="ExternalInput") for n, shp in IN_SHAPES.items()}
    out_d = kb.dram("out", [L, D], F32, kind="ExternalOutput")
    g.dbg_out = {}

    g.ident = ident = kb.sb(es, "ident", [128, 128], F32)
    g.identb = identb = kb.sb(es, "identb", [128, 128], BF16)
    g.onesb = onesb = kb.sb(es, "onesb", [128, 128], BF16)
    g.onesf = kb.sb(es, "onesf", [128, 128], F32)
    g.triu = kb.sb(es, "triu", [128, 128], F32)
    g.negm = kb.sb(es, "negm", [128, 128], F32)
    g.triub = kb.sb(es, "triub", [128, 128], BF16)
    g.strl = kb.sb(es, "strl", [128, 128], F32)
    g.strlb = kb.sb(es, "strlb", [128, 128], BF16)
    g.trilb = kb.sb(es, "trilb", [128, 128], BF16)
    g.negup = kb.sb(es, "negup", [128, 128], F32)
    g.negmb = kb.sb(es, "negmb", [128, 128], BF16)
    g.pow2 = kb.sb(es, "pow2", [128, 24], F32)
    with ExitStack() as tmp:
        coli = kb.sb(tmp, "coli", [128, 128], I32)
        rowi = kb.sb(tmp, "rowi", [128, 1], I32)
        colf = kb.sb(tmp, "colf", [128, 128], F32)
        rowf = kb.sb(tmp, "rowf", [128, 1], F32)
        kb.op("pool", lambda e: e.iota(coli[:], [[1, 128]], base=0, channel_multiplier=0), W=[coli])
        kb.op("pool", lambda e: e.iota(rowi[:], [[0, 1]], base=0, channel_multiplier=1), W=[rowi])
        kb.op("dve", lambda e: e.tensor_copy(out=colf[:], in_=coli[:]), R=[coli], W=[colf])
        kb.op("dve", lambda e: e.tensor_copy(out=rowf[:], in_=rowi[:]), R=[rowi], W=[rowf])
        kb.op("dve", lambda e: e.tensor_scalar(out=ident[:], in0=colf[:], scalar1=rowf[:, 0:1], scalar2=None,
                                               op0=ALU.is_equal), R=[colf, rowf], W=[ident])
        kb.op("dve", lambda e: e.tensor_copy(out=identb[:], in_=ident[:]), R=[ident], W=[identb])
        kb.op("dve", lambda e: e.memset(onesb[:], 1.0), W=[onesb])
        kb.op("dve", lambda e: e.memset(g.onesf[:], 1.0), W=[g.onesf])
        kb.op("dve", lambda e: e.tensor_scalar(out=g.triu[:], in0=colf[:], scalar1=rowf[:, 0:1], scalar2=None, op0=ALU.is_ge),
              R=[colf, rowf], W=[g.triu])
        kb.op("dve", lambda e: e.tensor_scalar(out=g.negm[:], in0=g.triu[:], scalar1=-1.0, scalar2=30000.0, op0=ALU.add, op1=ALU.mult),
              R=[g.triu], W=[g.negm])
        kb.op("dve", lambda e: e.tensor_copy(out=g.triub[:], in_=g.triu[:]), R=[g.triu], W=[g.triub])
        kb.op("dve", lambda e: e.tensor_copy(out=g.negmb[:], in_=g.negm[:]), R=[g.negm], W=[g.negmb])
        kb.op("dve", lambda e: e.tensor_scalar(out=g.strl[:], in0=colf[:], scalar1=rowf[:, 0:1], scalar2=None, op0=ALU.is_lt),
              R=[colf, rowf], W=[g.strl])
        kb.op("dve", lambda e: e.tensor_copy(out=g.strlb[:], in_=g.strl[:]), R=[g.strl], W=[g.strlb])
        kb.op("dve", lambda e: e.tensor_scalar(out=g.trilb[:], in0=colf[:], scalar1=rowf[:, 0:1], scalar2=None, op0=ALU.is_le),
              R=[colf, rowf], W=[g.trilb])
        kb.op("dve", lambda e: e.tensor_scalar(out=g.negup[:], in0=g.trilb[:], scalar1=-1.0, scalar2=1.0e9, op0=ALU.add, op1=ALU.mult),
              R=[g.trilb], W=[g.negup])
        for kk_ in range(24):
            kb.op("dve", lambda e, kk_=kk_: e.memset(g.pow2[:, kk_:kk_ + 1], float(2.0 ** (-kk_))), Wd=[g.pow2])
        kb.barrier()

    g.xT = xT = kb.sb(es, "xT", [128, KC, L], F32)
    g.xTb = xTb = [[kb.buf(f"xT{c}_{tb}") for tb in range(4)] for c in range(KC)]
    g.hTb = [[kb.buf(f"hT{c}_{tb}") for tb in range(4)] for c in range(KC)]

    fnw = load_cols(g, es, "fnw", g.inp["final_norm_w"].t, KC)
    g.ccol = load_cols(g, es, "ccol", g.inp["c"].t, KC)
    g.csilu = kb.sb(es, "csilu", [128, KC], BF16)
    kb.op("act", lambda e: e.activation(out=g.csilu[:], in_=g.ccol[:], func=AF.Silu), R=[g.ccol], W=[g.csilu])
    g.modT_l = [kb.sb(es, "modT", [128, 48], F32) for _ in range(DEPTH)]
    g.modraw = [kb.sb(es, "modraw", [128, 48], F32) for _ in range(DEPTH)]
    g.A1_l = [kb.sb(es, "A1", [128, KC], F32) for _ in range(DEPTH)]
    g.A2_l = [kb.sb(es, "A2", [128, KC], F32) for _ in range(DEPTH)]
    g.dsa_hook = None

    x_in = g.inp["x"]
    with ExitStack() as ph:
        xin = [kb.sb(ph, f"xin{i}", [128, D], F32) for i in range(2)]
        for tt in range(NT):
            xi = xin[tt % 2]
            kb.dma("sp", xi[:], x_in.t[tt * 128:(tt + 1) * 128, :], W=[xi])
            for half in range(2):
                p = kb.ps()
                for j in range(4):
                    c = half * 4 + j
                    kb.op("pe", lambda e, c=c, j=j, p=p, xi=xi: e.transpose(
                        p[:, j * 128:(j + 1) * 128], xi[:, c * 128:(c + 1) * 128], ident[:]),
                        R=[xi, ident], W=[p])
                tb = tt // 4
                wb = [xTb[half * 4 + j][tb] for j in range(4)]
                dst = xT[:, half * 4:half * 4 + 4, tt * 128:(tt + 1) * 128]
                src = p[:, :].rearrange("p (j t) -> p j t", j=4)
                if half == 0:
                    kb.op("act", lambda e, dst=dst, src=src: e.activation(out=dst, in_=src, func=AF.Copy), R=[p], W=wb)
                else:
                    kb.op("dve", lambda e, dst=dst, src=src: e.tensor_copy(out=dst, in_=src), R=[p], W=wb)
        kb.barrier()

    def scr(name, shape, dtype=BF16):
        return kb.dram("scr_" + name, shape, dtype, kind=("ExternalOutput" if name in g.dbg else "Internal"))
    g.scr = dict(zs=scr("zs", [L, 1024]), sbv=scr("sbv", [L, 512]), dv=scr("dv", [L, 128]),
                 dqT=scr("dqT", [8, 64, L]), dkT=scr("dkT", [2, 64, L]), iqT=scr("iqT", [8, 64, L]), ikT=scr("ikT", [1, 64, L]),
                 yT=scr("yT", [16, 128, L]), sbqT=scr("sbqT", [4, 128, L]), sbkT=scr("sbkT", [4, 128, L]), gT=scr("gT", [24, 128, L]))
    for name in g.scr:
        if name in g.dbg:
            g.dbg_out["scr_" + name] = g.scr[name]
    g.ropecs = kb.sb(es, "ropecs", [128, NT, 16], F32)
    kb.dma("sp", g.ropecs[:], g.inp["rope_cs"].t.rearrange("(t p) c -> p t c", p=128), W=[g.ropecs])
    g.dt_tok = kb.sb(es, "dt_tok", [128, NT, 16], F32)
    g.wix_tok = kb.sb(es, "wix_tok", [128, NT, 8], F32)
    g.dtb_bc = kb.sb(es, "dtb_bc", [128, 16], F32)
    g.cb = kb.sb(es, "cb", [128, 12], F32)
    g.cw = [kb.sb(es, f"cw{k}", [128, 12], F32) for k in range(4)]

    for l in range(nlayers):
        g.modT, g.A1, g.A2 = g.modT_l[l], g.A1_l[l], g.A2_l[l]
        if "mod" in phases:
            if l == 0:
                phase_mod(g, l)
            else:
                mod_finish(g, l)
        g.dsa_hook = (l + 1) if (l + 1 < nlayers and "mod" in phases) else None
        dump_sb(g, f"mod{l}", g.modT, [128, 48])
        if not skip_mixer:
            with ExitStack() as ssd_scope:
                g.xs_tok = kb.sb(ssd_scope, "xs_tok", [128, NT, 1024], BF16)
                g.bt_tok = kb.sb(ssd_scope, "bt_tok", [128, NT, 256], BF16)
                g.bcT = kb.sb(ssd_scope, "bcT", [128, 4, L], BF16)
                with nc.allow_non_contiguous_dma(reason="tiny vectors"):
                    kb.dma("sp", g.dtb_bc[:], g.inp["dt_bias"].t[l].partition_broadcast(128), W=[g.dtb_bc])
                    kb.dma("sp", g.cb[:], g.inp["conv_b"].t[l].rearrange("(c p) -> p c", p=128), W=[g.cb])
                    for k in range(4):
                        kb.dma("sp", g.cw[k][:], g.inp["conv_w"].t[l][k].rearrange("(c p) -> p c", p=128), W=[g.cw[k]])
                with ExitStack() as hsc:
                    g.hT = kb.sb(hsc, "hT", [128, KC, L], BF16)
                    phase_norm(g, l, which=1)
                    dump_hT(g, f"h{l}")
                    if "inproj" in phases:
                        phase_inproj(g, l)
                dump_sb(g, f"xs_tok{l}", g.xs_tok, [128, NT, 1024], BF16)
                dump_sb(g, f"bt_tok{l}", g.bt_tok, [128, NT, 256], BF16)
                dump_sb(g, f"bcT{l}", g.bcT, [128, 4, L], BF16)
                dump_sb(g, f"dt_tok{l}", g.dt_tok, [128, NT, 16], F32)
                dump_sb(g, f"wix_tok{l}", g.wix_tok, [128, NT, 8], F32)
                kb.barrier()
                if "ssd" in phases:
                    phase_ssd(g, l)
            if "sb" in phases:
                phase_sb(g, l)
            if "dsa" in phases:
                phase_dsa(g, l)
            if "merge" in phases:
                phase_merge(g, l)
            dump_xT(g, f"xmix{l}")
        with ExitStack() as hsc:
            g.hT = kb.sb(hsc, "hT", [128, KC, L], BF16)
            if "norm2" in phases:
                phase_norm(g, l, which=2)
            if "mlp" in phases:
                phase_mlp(g, l)
        dump_xT(g, f"xout{l}")

    final_norm(kb, nc, xT, xTb, fnw, onesb, ident, out_d)
    kb.finish([out_d] + list(g.dbg_out.values()))
    return nc, sorted(g.dbg_out.keys())


def load_cols(g, es_, name, src_ap, n):
    kb = g.kb
    t = kb.sb(es_, name, [128, n], F32)
    with g.nc.allow_non_contiguous_dma(reason="tiny per-feature vector load"):
        for c0 in range(0, n, 8):
            c1 = min(n, c0 + 8)
            kb.dma("sp", t[:, c0:c1], src_ap[c0 * 128:c1 * 128].rearrange("(c p) -> p c", p=128), W=[t])
    return t


def dump_sb(g, name, t, shape, dtype=F32):
    if name not in g.dbg:
        return
    d = g.kb.dram("dbg_" + name, shape, dtype, kind="ExternalOutput")
    g.kb.barrier()
    g.kb.dma("sp", d.t, t[:], R=[t], W=[d])
    g.dbg_out["dbg_" + name] = d


def dump_xT(g, name):
    if name not in g.dbg:
        return
    d = g.kb.dram("dbg_" + name, [128, KC, L], F32, kind="ExternalOutput")
    g.kb.barrier()
    g.kb.dma("sp", d.t, g.xT[:], R=[b for row in g.xTb for b in row], W=[d])
    g.dbg_out["dbg_" + name] = d


def dump_hT(g, name):
    if name not in g.dbg:
        return
    d = g.kb.dram("dbg_" + name, [128, KC, L], BF16, kind="ExternalOutput")
    g.kb.barrier()
    g.kb.dma("sp", d.t, g.hT[:], R=[b for row in g.hTb for b in row], W=[d])
    g.dbg_out["dbg_" + name] = d


def load_w(g, t, src_rows_ap):
    g.kb.dma("pool", t[:], src_rows_ap.rearrange("(kc p) n -> p kc n", p=128), W=[t])


def mod_piece_load(g, l, j4, w):
    g.kb.dma("pool", w[:], g.inp["ada_w"].t[l][:, j4 * 512:(j4 + 1) * 512].rearrange("(kc p) n -> p kc n", p=128), W=[w])


def mod_piece_mm(g, l, j4, w):
    kb = g.kb
    p = kb.ps()
    for jl in range(4):
        for kc in range(KC):
            kb.op("pe", lambda e, jl=jl, kc=kc: e.matmul(p[:, jl:jl + 1], w[:, kc, jl * 128:(jl + 1) * 128], g.csilu[:, kc:kc + 1],
                                                         start=(kc == 0), stop=(kc == KC - 1)), R=[w, g.csilu], W=[p])
    raw = g.modraw[l]
    kb.op("act", lambda e: e.activation(out=raw[:, j4 * 4:(j4 + 1) * 4], in_=p[:, 0:4], func=AF.Copy), R=[p], Wd=[raw])


def mod_piece(g, l, j4, w):
    mod_piece_load(g, l, j4, w)
    mod_piece_mm(g, l, j4, w)


def mod_finish(g, l):
    kb = g.kb
    with ExitStack() as ph:
        adab = load_cols(g, ph, "adab", g.inp["ada_b"].t[l], 48)
        n1w = load_cols(g, ph, "n1w", g.inp["norm1_w"].t[l], KC)
        n2w = load_cols(g, ph, "n2w", g.inp["norm2_w"].t[l], KC)
        modT, A1, A2 = g.modT_l[l], g.A1_l[l], g.A2_l[l]
        kb.op("dve", lambda e: e.tensor_tensor(out=modT[:], in0=g.modraw[l][:], in1=adab[:], op=ALU.add),
              R=[g.modraw[l], adab], W=[modT])
        kb.op("dve", lambda e: e.scalar_tensor_tensor(out=A1[:], in0=modT[:, 8:16], scalar=1.0, in1=n1w[:],
                                                      op0=ALU.add, op1=ALU.mult), R=[modT, n1w], W=[A1])
        kb.op("dve", lambda e: e.scalar_tensor_tensor(out=A2[:], in0=modT[:, 32:40], scalar=1.0, in1=n2w[:],
                                                      op0=ALU.add, op1=ALU.mult), R=[modT, n2w], W=[A2])
        kb.barrier()


def phase_mod(g, l):
    kb = g.kb
    with ExitStack() as ph:
        wt = [kb.sb(ph, f"adaw{i}", [128, KC, 512], BF16) for i in range(3)]
        for j4 in range(12):
            mod_piece(g, l, j4, wt[j4 % 3])
        kb.barrier()
    mod_finish(g, l)


def phase_norm(g, l, which):
    kb = g.kb
    A = g.A1 if which == 1 else g.A2
    sh0 = 0 if which == 1 else 24
    with ExitStack() as ph:
        sq = [kb.sb(ph, f"nsq{i}", [128, 512], BF16) for i in range(2)]
        rs = kb.sb(ph, "nrs", [128, 512], F32)
        tmp = [kb.sb(ph, f"ntmp{i}", [128, 512], F32) for i in range(2)]
        for tb in range(4):
            rstd_bcast(kb, ph, g.xT, g.xTb, tb, g.onesb, sq, rs)
            for c in range(KC):
                t = tmp[c % 2]
                kb.op("dve", lambda e, t=t, c=c: e.tensor_tensor(out=t[:], in0=g.xT[:, c, tb * 512:(tb + 1) * 512], in1=rs[:],
                                                                 op=ALU.mult), R=[g.xTb[c][tb], rs], W=[t])
                kb.op("act", lambda e, t=t, c=c: e.activation(
                    out=g.hT[:, c, tb * 512:(tb + 1) * 512], in_=t[:], func=AF.Identity,
                    scale=A[:, c:c + 1], bias=g.modT[:, sh0 + c:sh0 + c + 1]), R=[t, A, g.modT], W=[g.hTb[c][tb]])
        kb.barrier()


OFF = dict(z=0, xbc=1024, dt=2560, sbq=2576, sbk=3088, sbv=3600, dsq=4112, dsk=4624, dsv=4752,
           ixq=4880, ixk=5392, ixw=5456, gate=5464)


def rope(g, f3, nh, tt, rt):
    kb = g.kb
    fb = f3._buf
    cos = g.ropecs[:, tt:tt + 1, 0:8].broadcast_to([128, nh, 8])
    sin = g.ropecs[:, tt:tt + 1, 8:16].broadcast_to([128, nh, 8])
    x1 = f3.ap[:, :, 0:8]
    x2 = f3.ap[:, :, 8:16]
    t = [rt[:, i, 0:nh, :] for i in range(4)]
    kb.op("dve", lambda e: e.tensor_tensor(out=t[0], in0=x1, in1=cos, op=ALU.mult), R=[fb, g.ropecs], W=[rt])
    kb.op("dve", lambda e: e.tensor_tensor(out=t[1], in0=x2, in1=sin, op=ALU.mult), R=[fb, g.ropecs], W=[rt])
    kb.op("dve", lambda e: e.tensor_tensor(out=t[2], in0=x2, in1=cos, op=ALU.mult), R=[fb, g.ropecs], W=[rt])
    kb.op("dve", lambda e: e.tensor_tensor(out=t[3], in0=x1, in1=sin, op=ALU.mult), R=[fb, g.ropecs], W=[rt])
    kb.op("dve", lambda e: e.tensor_tensor(out=x1, in0=t[0], in1=t[1], op=ALU.subtract), R=[rt], W=[fb])
    kb.op("dve", lambda e: e.tensor_tensor(out=x2, in0=t[2], in1=t[3], op=ALU.add), R=[rt], W=[fb])


class V:
    def __init__(self, ap, buf):
        self.ap = ap
        self._buf = buf


def phase_inproj(g, l):
    kb, nc = g.kb, g.nc
    win = g.inp["w_in"].t[l]
    S = g.scr
    with ExitStack() as ph:
        wt = [kb.sb(ph, "winw", [128, KC, 512], BF16) for _ in range(2)]
        nw = [0]

        def getw(n0, N):
            w = wt[nw[0] % 2]
            nw[0] += 1
            kb.dma("pool", w[:, :, 0:N], win[:, n0:n0 + N].rearrange("(kc p) n -> p kc n", p=128), W=[w])
            return w

        stgb = [kb.sb(ph, "stgb", [128, 512], BF16) for _ in range(3)]
        stgf = [kb.sb(ph, "stgf", [128, 512], F32) for _ in range(2)]
        rt = kb.sb(ph, "ropetmp", [128, 4, 16, 8], F32)
        tst = [kb.sb(ph, "tst", [64, 8, 512], BF16) for _ in range(2)]
        cnt = dict(b=0, f=0, t=0)

        def nxt(lst, k):
            r = lst[cnt[k] % len(lst)]
            cnt[k] += 1
            return r

        def tok_mm(w, N, tt):
            p = kb.ps()
            for kc in range(KC):
                kb.op("pe", lambda e, kc=kc: e.matmul(p[:, 0:N], g.hT[:, kc, tt * 128:(tt + 1) * 128], w[:, kc, 0:N],
                                                      start=(kc == 0), stop=(kc == KC - 1)),
                      R=[g.hTb[kc][tt // 4], w], W=[p])
            return p

        def feat_mm(w, ci, tb):
            p = kb.ps()
            for kc in range(KC):
                kb.op("pe", lambda e, kc=kc: e.matmul(p[:, :], w[:, kc, ci * 128:(ci + 1) * 128],
                                                      g.hT[:, kc, tb * 512:(tb + 1) * 512],
                                                      start=(kc == 0), stop=(kc == KC - 1)),
                      R=[g.hTb[kc][tb], w], W=[p])
            return p

        for zi in range(2):
            w = getw(OFF["z"] + zi * 512, 512)
            for tt in range(NT):
                p = tok_mm(w, 512, tt)
                sb_ = nxt(stgb, "b")
                kb.op("act", lambda e: e.activation(out=sb_[:], in_=p[:, :], func=AF.Silu), R=[p], W=[sb_])
                kb.dma("sp", S["zs"].t[tt * 128:(tt + 1) * 128, zi * 512:(zi + 1) * 512], sb_[:], R=[sb_])
        w = getw(OFF["dt"], 16)
        for tt in range(NT):
            p = tok_mm(w, 16, tt)
            f = nxt(stgf, "f")
            kb.op("dve", lambda e: e.tensor_tensor(out=f[:, 0:16], in0=p[:, 0:16], in1=g.dtb_bc[:], op=ALU.add),
                  R=[p, g.dtb_bc], W=[f])
            kb.op("act", lambda e: e.activation(out=f[:, 0:16], in_=f[:, 0:16], func=AF.Exp), R=[f], W=[f])
            kb.op("act", lambda e: e.activation(out=g.dt_tok[:, tt, :], in_=f[:, 0:16], func=AF.Ln, bias=1.0),
                  R=[f], W=[g.dt_tok])
        w = getw(OFF["sbv"], 512)
        for tt in range(NT):
            p = tok_mm(w, 512, tt)
            sb_ = nxt(stgb, "b")
            kb.op("act", lambda e: e.activation(out=sb_[:], in_=p[:, :], func=AF.Copy), R=[p], W=[sb_])
            kb.dma("sp", S["sbv"].t[tt * 128:(tt + 1) * 128, :], sb_[:], R=[sb_])

        def roped_group(n0, N, nh, dstT, extra=None):
            w = getw(n0, N)
            fs, bs, sts = {}, {}, {}

            def stA(tt):
                p = tok_mm(w, N, tt)
                f = nxt(stgf, "f")
                kb.op("act", lambda e: e.activation(out=f[:, 0:N], in_=p[:, 0:N], func=AF.Copy), R=[p], W=[f])
                fs[tt] = f

            def stB(tt):
                f = fs.pop(tt)
                rope(g, V(f[:, 0:nh * 64].rearrange("p (h d) -> p h d", d=64), f), nh, tt, rt)
                b = nxt(stgb, "b")
                kb.op("act", lambda e: e.activation(out=b[:, 0:N], in_=f[:, 0:N], func=AF.Copy), R=[f], W=[b])
                if extra is not None:
                    extra(tt, f, b)
                bs[tt] = b

            def stC(tt):
                b = bs.pop(tt)
                pt = kb.ps()
                ptb = pt[:, :].bitcast(BF16)
                for h in range(nh):
                    kb.op("pe", lambda e, h=h: e.transpose(ptb[0:64, h * 128:(h + 1) * 128], b[:, h * 64:(h + 1) * 64], g.identb[:]),
                          R=[b, g.identb], W=[pt])
                if tt % 4 == 0:
                    sts[tt // 4] = nxt(tst, "t")
                st = sts[tt // 4]
                kb.op("act", lambda e: e.activation(
                    out=st[0:64, 0:nh, (tt % 4) * 128:(tt % 4 + 1) * 128],
                    in_=ptb[0:64, 0:nh * 128].rearrange("p (h t) -> p h t", h=nh), func=AF.Copy), R=[pt], Wd=[st])
                if tt % 4 == 3:
                    tb = tt // 4
                    kb.dma("sp", dstT.t[:, :, tb * 512:(tb + 1) * 512].rearrange("h d t -> d h t"), st[0:64, 0:nh, :], R=[st])

            for step in range(NT + 2):
                if step < NT:
                    stA(step)
                if 0 <= step - 1 < NT:
                    stB(step - 1)
                if 0 <= step - 2 < NT:
                    stC(step - 2)

        roped_group(OFF["dsq"], 512, 8, S["dqT"])

        def dsv_extra(tt, f, b):
            kb.dma("sp", S["dv"].t[tt * 128:(tt + 1) * 128, :], b[:, 128:256], R=[b])
        roped_group(OFF["dsk"], 256, 2, S["dkT"], dsv_extra)
        roped_group(OFF["ixq"], 512, 8, S["iqT"])

        def ixw_extra(tt, f, b):
            kb.op("dve", lambda e: e.tensor_copy(out=g.wix_tok[:, tt, :], in_=f[:, 64:72]), R=[f], W=[g.wix_tok])
        roped_group(OFF["ixk"], 72, 1, S["ikT"], ixw_extra)

        def feat_group(n0, dstT, c0, func):
            w = getw(n0, 512)
            for ci in range(4):
                for tb in range(4):
                    p = feat_mm(w, ci, tb)
                    sb_ = nxt(stgb, "b")
                    kb.op("act", lambda e: e.activation(out=sb_[:], in_=p[:, :], func=func), R=[p], W=[sb_])
                    kb.dma("sp", dstT.t[c0 + ci, :, tb * 512:(tb + 1) * 512], sb_[:], R=[sb_])

        feat_group(OFF["sbq"], S["sbqT"], 0, AF.Copy)
        feat_group(OFF["sbk"], S["sbkT"], 0, AF.Copy)
        for gi in range(6):
            feat_group(OFF["gate"] + gi * 512, S["gT"], gi * 4, AF.Sigmoid)

        kb.barrier()

    with ExitStack() as ph:
        wt = [kb.sb(ph, "winw", [128, KC, 512], BF16) for _ in range(2)]
        nw = [0]
        xpad = [kb.sb(ph, "xpad", [128, 3 + L], F32) for _ in range(1)]
        cva = [kb.sb(ph, "cva", [128, L], F32) for _ in range(1)]
        cvo = [kb.sb(ph, "cvo", [128, L], BF16) for _ in range(2)]
        for xp in xpad:
            kb.op("dve", lambda e, xp=xp: e.memset(xp[:, 0:3], 0.0), W=[xp])
        for gi in range(3):
            w = getw(OFF["xbc"] + gi * 512, 512)
            for ci in range(4):
                cidx = gi * 4 + ci
                xp, acc = xpad[0], cva[0]
                for tb in range(4):
                    p = feat_mm(w, ci, tb)
                    kb.op("act", lambda e: e.activation(out=xp[:, 3 + tb * 512:3 + (tb + 1) * 512], in_=p[:, :], func=AF.Copy),
                          R=[p], Wd=[xp])
                kb.op("act", lambda e: e.activation(out=acc[:], in_=xp[:, 3:3 + L], func=AF.Identity,
                                                    scale=g.cw[3][:, cidx:cidx + 1], bias=g.cb[:, cidx:cidx + 1]),
                      R=[xp, g.cw[3], g.cb], W=[acc])
                for k in (2, 1, 0):
                    kb.op("dve", lambda e, k=k: e.scalar_tensor_tensor(out=acc[:], in0=xp[:, k:k + L], scalar=g.cw[k][:, cidx:cidx + 1],
                                                                       in1=acc[:], op0=ALU.mult, op1=ALU.add),
                          R=[xp, g.cw[k], acc], W=[acc])
                if cidx < 8:
                    o = cvo[cidx % 2]
                    ov = o[:, :]
                else:
                    o = g.bcT
                    ov = g.bcT[:, cidx - 8, :]
                kb.op("act", lambda e: e.activation(out=ov, in_=acc[:], func=AF.Silu), R=[acc], W=[o])
                if cidx < 10:
                    for half in range(2):
                        pt = kb.ps()
                        ptb = pt[:, :].bitcast(BF16)
                        for j in range(8):
                            tt = half * 8 + j
                            kb.op("pe", lambda e, j=j, tt=tt: e.transpose(ptb[:, j * 128:(j + 1) * 128], ov[:, tt * 128:(tt + 1) * 128],
                                                                          g.identb[:]), R=[o, g.identb], W=[pt])
                        if cidx < 8:
                            dst, dt_ = g.xs_tok[:, half * 8:(half + 1) * 8, cidx * 128:(cidx + 1) * 128], g.xs_tok
                        else:
                            dst, dt_ = g.bt_tok[:, half * 8:(half + 1) * 8, (cidx - 8) * 128:(cidx - 7) * 128], g.bt_tok
                        kb.op("dve", lambda e: e.tensor_copy(out=dst, in_=ptb.rearrange("p (j f) -> p j f", j=8)), R=[pt], W=[dt_])
        kb.barrier()


def phase_ssd(g, l):
    kb, nc = g.kb, g.nc
    S = g.scr
    with ExitStack() as ph:
        def sb(name, shape, dt_=F32):
            return kb.sb(ph, name, shape, dt_)
        a_bc = sb("a_bc", [128, 16])
        dsk_bc = sb("dsk_bc", [128, 16])
        nw_bc = sb("nw_bc", [128, 1024])
        Dmat = sb("Dmat", [128, 16, 128], BF16)
        with nc.allow_non_contiguous_dma(reason="tiny vectors"):
            kb.dma("sp", a_bc[:], g.inp["a_log"].t[l].partition_broadcast(128), W=[a_bc])
            kb.dma("sp", dsk_bc[:], g.inp["d_skip"].t[l].partition_broadcast(128), W=[dsk_bc])
            kb.dma("sp", nw_bc[:], g.inp["ssd_norm_w"].t[l].partition_broadcast(128), W=[nw_bc])
        kb.op("act", lambda e: e.activation(out=a_bc[:], in_=a_bc[:], func=AF.Exp), R=[a_bc], W=[a_bc])
        kb.op("dve", lambda e: e.tensor_scalar(out=a_bc[:], in0=a_bc[:], scalar1=-1.0, scalar2=None, op0=ALU.mult), R=[a_bc], W=[a_bc])
        for h in range(16):
            kb.op("dve", lambda e, h=h: e.tensor_scalar(out=Dmat[:, h, :], in0=g.ident[:], scalar1=dsk_bc[:, h:h + 1], scalar2=None,
                                                        op0=ALU.mult), R=[g.ident, dsk_bc], Wd=[Dmat])
        NB = 2
        X = [sb("X", [128, 2, 16, 128], BF16)] * 2
        dd = [sb("dd", [128, 16, 128])] * 2
        Mt = [sb("Mt", [128, 16, 128], BF16) for _ in range(NB)]
        xw = [sb("xw", [128, 1024], BF16) for _ in range(NB)]
        zst = [sb("zst", [128, 1024], BF16) for _ in range(NB)]
        yy = [sb("yy", [128, 1024])] * 2
        yz = yy
        ss = [sb("ss", [128, 1]) for _ in range(NB)]
        epsc = sb("epsc", [128, 1])
        kb.op("dve", lambda e: e.memset(epsc[:], EPS), W=[epsc])
        yn = [sb("yn", [128, 1024], BF16) for _ in range(NB)]
        yst = [sb("yst", [128, KC, 256], BF16)] * 2
        hst = [sb("hst", [128, 512]) for _ in range(2)]
        htmp = [sb("htmp", [128, 512]) for _ in range(2)]
        prevb = [[sb("prevb", [128, 512], BF16) for _ in range(2)] for _ in range(2)]
        ytmp = [sb("ytmp", [128, 512]) for _ in range(2)]

        dA_all = sb("dA_all", [128, 256])
        dAs_all = sb("dAs_all", [128, 2, 256], BF16)
        ac_all = sb("ac_all", [128, 256])
        eac_all = sb("eac_all", [128, 256])
        cd_all = sb("cd_all", [128, 256])
        wgt_all = sb("wgt_all", [128, 256])
        v_all = sb("v_all", [128, 256])
        vs_all = sb("vs_all", [128, 2, 256], BF16)
        acs_all = sb("acs_all", [128, 2, 256], BF16)
        dtf = g.dt_tok[:, :, :].rearrange("p c h -> p (c h)")
        kb.op("dve", lambda e: e.tensor_tensor(out=dA_all[:, :].rearrange("p (c h) -> p c h", h=16), in0=g.dt_tok[:, :, :],
                                               in1=a_bc[:, :].unsqueeze(1).broadcast_to([128, NT, 16]), op=ALU.mult),
              R=[g.dt_tok, a_bc], W=[dA_all])
        kb.op("dve", lambda e: e.tensor_copy(out=dAs_all[:, 0, :], in_=dA_all[:]), R=[dA_all], W=[dAs_all])
        kb.op("dve", lambda e: e.tensor_tensor(out=dAs_all[:, 1, :], in0=dA_all[:], in1=dAs_all[:, 0, :], op=ALU.subtract),
              R=[dA_all, dAs_all], W=[dAs_all])
        pac = kb.ps(pin=True)
        for c in range(NT):
            for half, lhs in ((0, g.triub), (1, g.onesb)):
                for hl_ in range(2):
                    kb.op("pe", lambda e, c=c, half=half, lhs=lhs, hl_=hl_: e.matmul(
                        pac[:, half * 256 + c * 16:half * 256 + (c + 1) * 16], lhs[:], dAs_all[:, hl_, c * 16:(c + 1) * 16],
                        start=(hl_ == 0), stop=(hl_ == 1)), R=[lhs, dAs_all], W=[pac])
        kb.unpin(pac)
        kb.op("dve", lambda e: e.tensor_copy(out=ac_all[:], in_=pac[:, 0:256]), R=[pac], W=[ac_all])
        kb.op("act", lambda e: e.activation(out=eac_all[:], in_=pac[:, 0:256], func=AF.Exp), R=[pac], W=[eac_all])
        kb.op("act", lambda e: e.activation(out=cd_all[:], in_=pac[:, 256:512], func=AF.Exp), R=[pac], W=[cd_all])
        kb.op("dve", lambda e: e.tensor_tensor(out=wgt_all[:], in0=pac[:, 256:512], in1=ac_all[:], op=ALU.subtract), R=[pac, ac_all], W=[wgt_all])
        kb.op("act", lambda e: e.activation(out=wgt_all[:], in_=wgt_all[:], func=AF.Exp), R=[wgt_all], W=[wgt_all])
        kb.op("dve", lambda e: e.tensor_tensor(out=wgt_all[:], in0=wgt_all[:], in1=dtf, op=ALU.mult), R=[wgt_all, g.dt_tok], W=[wgt_all])
        kb.op("act", lambda e: e.activation(out=v_all[:], in_=dtf, func=AF.Ln), R=[g.dt_tok], W=[v_all])
        kb.op("dve", lambda e: e.tensor_tensor(out=v_all[:], in0=v_all[:], in1=ac_all[:], op=ALU.subtract), R=[v_all, ac_all], W=[v_all])
        kb.op("dve", lambda e: e.tensor_copy(out=vs_all[:, 0, :], in_=v_all[:]), R=[v_all], W=[vs_all])
        kb.op("dve", lambda e: e.tensor_tensor(out=vs_all[:, 1, :], in0=v_all[:], in1=vs_all[:, 0, :], op=ALU.subtract), R=[v_all, vs_all], W=[vs_all])
        kb.op("dve", lambda e: e.tensor_copy(out=acs_all[:, 0, :], in_=ac_all[:]), R=[ac_all], W=[acs_all])
        kb.op("dve", lambda e: e.tensor_tensor(out=acs_all[:, 1, :], in0=ac_all[:], in1=acs_all[:, 0, :], op=ALU.subtract), R=[ac_all, acs_all], W=[acs_all])

        pcbs = {}

        def P1(c):
            k = c % NB
            tsl = slice(c * 128, (c + 1) * 128)
            c16 = slice(c * 16, (c + 1) * 16)
            kb.op("dve", lambda e: e.tensor_tensor(
                out=xw[k][:, :].rearrange("p (h d) -> p h d", d=64), in0=g.xs_tok[:, c, :].rearrange("p (h d) -> p h d", d=64),
                in1=wgt_all[:, c16].unsqueeze(2).broadcast_to([128, 16, 64]), op=ALU.mult), R=[g.xs_tok, wgt_all], W=[xw[k]])
            for hl_ in range(2):
                kb.op("dve", lambda e, hl_=hl_: e.tensor_tensor(
                    out=X[k][:, hl_, :, :], in0=g.identb[:, :].unsqueeze(1).broadcast_to([128, 16, 128]),
                    in1=acs_all[:, hl_, c16].unsqueeze(2).broadcast_to([128, 16, 128]), op=ALU.mult), R=[g.identb, acs_all], Wd=[X[k]])
            for q4 in range(4):
                pb = kb.ps()
                h4 = slice(c * 16 + q4 * 4, c * 16 + q4 * 4 + 4)
                for hl_ in range(2):
                    kb.op("pe", lambda e, hl_=hl_: e.matmul(pb[:, :], g.onesb[:], X[k][:, hl_, q4 * 4:(q4 + 1) * 4, :],
                                                            start=(hl_ == 0), stop=False), R=[g.onesb, X[k]], W=[pb])
                for hl_ in range(2):
                    kb.op("pe", lambda e, hl_=hl_: e.matmul(pb[:, :], g.identb[:], vs_all[:, hl_, h4].unsqueeze(2).broadcast_to([128, 4, 128]),
                                                            start=False, stop=False), R=[g.identb, vs_all], W=[pb])
                kb.op("pe", lambda e: e.matmul(pb[:, :], g.identb[:], g.negmb[:, :].unsqueeze(1).broadcast_to([128, 4, 128]),
                                               start=False, stop=True), R=[g.identb, g.negmb], W=[pb])
                kb.op("act", lambda e, q4=q4, pb=pb: e.activation(out=dd[k][:, q4 * 4:(q4 + 1) * 4, :],
                                                                  in_=pb[:, :].rearrange("p (h i) -> p h i", h=4), func=AF.Exp),
                      R=[pb], Wd=[dd[k]])
            pcb = kb.ps(pin=True)
            for gg in range(2):
                kb.op("pe", lambda e, gg=gg: e.matmul(pcb[:, gg * 128:(gg + 1) * 128], g.bcT[:, gg, tsl], g.bcT[:, 2 + gg, tsl],
                                                      start=True, stop=True), R=[g.bcT], W=[pcb])
            pcbs[c] = pcb

        def P1b(c):
            k = c % NB
            pcb = pcbs.pop(c)
            kb.unpin(pcb)
            for gg in range(2):
                kb.op("dve", lambda e, gg=gg: e.tensor_tensor(
                    out=Mt[k][:, gg * 8:(gg + 1) * 8, :], in0=dd[k][:, gg * 8:(gg + 1) * 8, :],
                    in1=pcb[:, gg * 128:(gg + 1) * 128].unsqueeze(1).broadcast_to([128, 8, 128]), op=ALU.mult),
                    R=[dd[k], pcb], Wd=[Mt[k]])

        pend2 = {}

        def P2a(c):
            k = c % NB
            tsl = slice(c * 128, (c + 1) * 128)
            kb.dma("sp", zst[k][:], S["zs"].t[tsl, :], W=[zst[k]])
            rec = []
            for gg in range(2):
                pA = kb.ps()
                for hl in range(8):
                    h = gg * 8 + hl
                    xsl = g.xs_tok[:, c, h * 64:(h + 1) * 64]
                    kb.op("pe", lambda e, h=h, hl=hl, xsl=xsl: e.matmul(pA[:, hl * 64:(hl + 1) * 64], Mt[k][:, h, :], xsl, start=True, stop=False),
                          R=[Mt[k], g.xs_tok], W=[pA])
                    kb.op("pe", lambda e, h=h, hl=hl, xsl=xsl: e.matmul(pA[:, hl * 64:(hl + 1) * 64], Dmat[:, h, :], xsl, start=False, stop=True),
                          R=[Dmat, g.xs_tok], W=[pA])
                pB = pS = None
                if c > 0:
                    pB = kb.ps()
                    pv = prevb[gg][(c - 1) % 2]
                    kb.op("pe", lambda e, pB=pB, pv=pv, gg=gg: e.matmul(pB[:, :], g.bcT[:, 2 + gg, tsl], pv[:], start=True, stop=True),
                          R=[g.bcT, pv], W=[pB])
                if c < NT - 1:
                    pS = kb.ps()
                    kb.op("pe", lambda e, pS=pS, gg=gg: e.matmul(pS[:, :], g.bt_tok[:, c, gg * 128:(gg + 1) * 128], xw[k][:, gg * 512:(gg + 1) * 512],
                                                                start=True, stop=True), R=[g.bt_tok, xw[k]], W=[pS])
                rec.append((pA, pB, pS))
            pend2[c] = rec

        def P2b(c):
            k = c % NB
            rec = pend2.pop(c)
            for gg in range(2):
                pA, pB, pS = rec[gg]
                ysl = yy[k][:, gg * 512:(gg + 1) * 512]
                if c > 0:
                    yt = ytmp[gg]
                    kb.op("dve", lambda e: e.tensor_tensor(
                        out=yt[:, :].rearrange("p (h d) -> p h d", d=64), in0=pB[:, :].rearrange("p (h d) -> p h d", d=64),
                        in1=eac_all[:, c * 16 + gg * 8:c * 16 + (gg + 1) * 8].unsqueeze(2).broadcast_to([128, 8, 64]), op=ALU.mult),
                        R=[pB, eac_all], W=[yt])
                    kb.op("dve", lambda e: e.tensor_tensor(out=ysl, in0=yt[:], in1=pA[:, :], op=ALU.add), R=[yt, pA], W=[yy[k]])
                else:
                    kb.op("act", lambda e: e.activation(out=ysl, in_=pA[:, :], func=AF.Copy), R=[pA], W=[yy[k]])
                if c < NT - 1:
                    if c == 0:
                        kb.op("dve", lambda e: e.tensor_copy(out=hst[gg][:], in_=pS[:, :]), R=[pS], W=[hst[gg]])
                    else:
                        kb.op("dve", lambda e: e.tensor_tensor(
                            out=htmp[gg][:, :].rearrange("p (h d) -> p h d", d=64), in0=hst[gg][:, :].rearrange("p (h d) -> p h d", d=64),
                            in1=cd_all[:, c * 16 + gg * 8:c * 16 + (gg + 1) * 8].unsqueeze(2).broadcast_to([128, 8, 64]), op=ALU.mult),
                            R=[hst[gg], cd_all], W=[htmp[gg]])
                        kb.op("dve", lambda e: e.tensor_tensor(out=hst[gg][:], in0=htmp[gg][:], in1=pS[:, :], op=ALU.add),
                              R=[htmp[gg], pS], W=[hst[gg]])
                    pn = prevb[gg][c % 2]
                    kb.op("act", lambda e: e.activation(out=pn[:], in_=hst[gg][:], func=AF.Copy), R=[hst[gg]], W=[pn])

        def P3(c):
            k = c % NB
            tsl = slice(c * 128, (c + 1) * 128)
            kb.op("dve", lambda e: e.tensor_tensor(out=yz[k][:], in0=yy[k][:], in1=zst[k][:], op=ALU.mult), R=[yy[k], zst[k]], W=[yz[k]])
            kb.op("act", lambda e: e.activation(out=yn[k][:], in_=yz[k][:], func=AF.Square, accum_out=ss[k][:]), R=[yz[k]], W=[yn[k], ss[k]])
            kb.op("act", lambda e: e.activation(out=ss[k][:], in_=ss[k][:], func=AF.Ln, scale=1.0 / 1024, bias=epsc[:, 0:1]), R=[ss[k], epsc], W=[ss[k]])
            kb.op("act", lambda e: e.activation(out=ss[k][:], in_=ss[k][:], func=AF.Exp, scale=-0.5), R=[ss[k]], W=[ss[k]])
            kb.op("dve", lambda e: e.scalar_tensor_tensor(out=yn[k][:], in0=yz[k][:], scalar=ss[k][:, 0:1], in1=nw_bc[:],
                                                          op0=ALU.mult, op1=ALU.mult), R=[yz[k], ss[k], nw_bc], W=[yn[k]])
            pt = kb.ps()
            ptb = pt[:, :].bitcast(BF16)
            for kc in range(KC):
                kb.op("pe", lambda e, kc=kc: e.transpose(ptb[:, kc * 128:(kc + 1) * 128], yn[k][:, kc * 128:(kc + 1) * 128], g.identb[:]),
                      R=[yn[k], g.identb], W=[pt])
            st = yst[(c // 2) % 2]
            kb.op("act", lambda e: e.activation(out=st[:, :, (c % 2) * 128:(c % 2 + 1) * 128],
                                                in_=ptb.rearrange("p (kc t) -> p kc t", kc=KC), func=AF.Copy), R=[pt], Wd=[st])
            if c % 2 == 1:
                t2 = c // 2
                kb.dma("sp", S["yT"].t[0:8, :, t2 * 256:(t2 + 1) * 256].rearrange("kc p t -> p kc t"), st[:], R=[st])

        for step in range(-1, NT + 1):
            if 0 <= step + 1 < NT:
                P1(step + 1)
            if 0 <= step < NT:
                P2a(step)
            if 0 <= step - 1 < NT:
                P3(step - 1)
            if 0 <= step + 1 < NT:
                P1b(step + 1)
            if 0 <= step < NT:
                P2b(step)
        kb.barrier()


def phase_sb(g, l):
    kb, nc = g.kb, g.nc
    S = g.scr
    with ExitStack() as ph:
        def sb(name, shape, dt_=F32):
            return kb.sb(ph, name, shape, dt_)
        qT = sb("sbq", [128, 4, L], BF16)
        kT = sb("sbk", [128, 4, L], BF16)
        v = sb("sbv", [128, NT, 512], BF16)
        onesw = sb("onesw", [128, L], BF16)
        kb.op("dve", lambda e: e.memset(onesw[:], 1.0), W=[onesw])
        kb.dma("sp", qT[:], S["sbqT"].t.rearrange("c p t -> p c t"), W=[qT])
        kb.dma("sp", kT[:], S["sbkT"].t.rearrange("c p t -> p c t"), W=[kT])
        kb.dma("sp", v[:], S["sbv"].t.rearrange("(t p) f -> p t f", p=128), W=[v])
        e1b = [sb("e1", [128, L]) for _ in range(3)]
        spb = [sb("sp", [128, L]) for _ in range(2)]
        Fb = [sb("F", [128, L + 1]) for _ in range(2)]
        attb = [sb("att", [128, L], BF16) for _ in range(2)]
        attT = [sb("attT", [128, NT, 128], BF16) for _ in range(2)]
        ftn = [sb("ftn", [128, 1]) for _ in range(2)]
        yst = [sb("yst", [128, 128], BF16) for _ in range(4)]
        for F in Fb:
            kb.op("dve", lambda e, F=F: e.memset(F[:, 0:1], 0.0), W=[F])
        iters = [(qt, h) for qt in range(NT) for h in range(8)]

        def s1(i):
            qt, h = iters[i]
            W = 128 * (qt + 1)
            nch = (W + 511) // 512
            dsl = slice(qt * 128, (qt + 1) * 128)
            hp, hc = (h % 2) * 64, h // 2
            e1, sp_ = e1b[i % 3], spb[i % 2]
            for ch in range(nch):
                n = min(512, W - ch * 512)
                p = kb.ps()
                kb.op("pe", lambda e, p=p, ch=ch, n=n: e.matmul(p[:, 0:n], qT[hp:hp + 64, hc, dsl], kT[hp:hp + 64, hc, ch * 512:ch * 512 + n],
                                                                start=True, stop=True), R=[qT, kT], W=[p])
                kb.op("act", lambda e, p=p, ch=ch, n=n: e.activation(out=e1[:, ch * 512:ch * 512 + n], in_=p[:, 0:n], func=AF.Exp, scale=0.125),
                      R=[p], Wd=[e1])
            kb.op("act", lambda e: e.activation(out=sp_[:, 0:W], in_=e1[:, 0:W], func=AF.Ln, bias=1.0), R=[e1], W=[sp_])

        def s2(i):
            qt, h = iters[i]
            k = i % 2
            W = 128 * (qt + 1)
            dsl = slice(qt * 128, (qt + 1) * 128)
            sp_, F = spb[k], Fb[k]
            kb.op("dve", lambda e: e.tensor_tensor(out=sp_[:, dsl], in0=sp_[:, dsl], in1=g.strl[:], op=ALU.mult), R=[sp_, g.strl], W=[sp_])
            kb.op("dve", lambda e: e.memset(F[:, 0:1], 0.0), W=[F])
            kb.op("dve", lambda e: e.tensor_tensor_scan(out=F[:, 1:W + 1], data0=onesw[:, 0:W], data1=sp_[:, 0:W], initial=0.0,
                                                        op0=ALU.mult, op1=ALU.add), R=[onesw, sp_], W=[F])
            kb.op("dve", lambda e: e.tensor_scalar(out=ftn[k][:], in0=F[:, W:W + 1], scalar1=-1.0, scalar2=None, op0=ALU.mult),
                  R=[F], W=[ftn[k]])

        def s3a(i):
            qt, h = iters[i]
            k = i % 2
            W = 128 * (qt + 1)
            F = Fb[k]
            kb.op("act", lambda e: e.activation(out=F[:, 0:W], in_=F[:, 0:W], func=AF.Exp, bias=ftn[k][:, 0:1]), R=[F, ftn[k]], W=[F])

        def s3b(i):
            qt, h = iters[i]
            k = i % 2
            W = 128 * (qt + 1)
            dsl = slice(qt * 128, (qt + 1) * 128)
            e1, F, att = e1b[i % 3], Fb[k], attb[k]
            kb.op("dve", lambda e: e.tensor_tensor(out=att[:, 0:W], in0=e1[:, 0:W], in1=F[:, 0:W], op=ALU.mult), R=[e1, F], W=[att])
            kb.op("dve", lambda e: e.tensor_tensor(out=att[:, dsl], in0=att[:, dsl], in1=g.strlb[:], op=ALU.mult), R=[att, g.strlb], W=[att])

        def s3c(i):
            qt, h = iters[i]
            k = i % 2
            att, aT = attb[k], attT[k]
            for b0 in range(0, qt + 1, 8):
                nb = min(8, qt + 1 - b0)
                pt = kb.ps()
                ptb = pt[:, :].bitcast(BF16)
                for j in range(nb):
                    kb.op("pe", lambda e, j=j, b0=b0: e.transpose(ptb[:, j * 128:(j + 1) * 128], att[:, (b0 + j) * 128:(b0 + j + 1) * 128], g.identb[:]),
                          R=[att, g.identb], W=[pt])
                kb.op("act", lambda e, b0=b0, nb=nb: e.activation(out=aT[:, b0:b0 + nb, :], in_=ptb[:, 0:nb * 128].rearrange("p (j t) -> p j t", j=nb),
                                                                 func=AF.Copy), R=[pt], Wd=[aT])

        def s3d(i):
            qt, h = iters[i]
            k = i % 2
            dsl = slice(qt * 128, (qt + 1) * 128)
            hp, hc = (h % 2) * 64, h // 2
            aT = attT[k]
            py = kb.ps()
            for sbk in range(qt + 1):
                kb.op("pe", lambda e, sbk=sbk: e.matmul(py[0:64, 0:128], v[:, sbk, h * 64:(h + 1) * 64], aT[:, sbk, :],
                                                        start=(sbk == 0), stop=(sbk == qt)), R=[v, aT], W=[py])
            pys[i] = py

        def s3e(i):
            qt, h = iters[i]
            dsl = slice(qt * 128, (qt + 1) * 128)
            hp, hc = (h % 2) * 64, h // 2
            py = pys.pop(i)
            ys = yst[i % 4]
            kb.op("dve", lambda e: e.tensor_copy(out=ys[hp:hp + 64, :], in_=py[0:64, 0:128]), R=[py], W=[ys])
            kb.dma("sp", S["yT"].t[8 + hc, hp:hp + 64, dsl], ys[hp:hp + 64, :], R=[ys])

        n_it = len(iters)

        def run(fn, i):
            if 0 <= i < n_it:
                fn(i)
        pys = {}
        for step in range(n_it + 4):
            run(s3d, step - 4)
            run(s3a, step - 2)
            run(s1, step)
            run(s3c, step - 3)
            run(s2, step - 1)
            run(s3e, step - 4)
            run(s3b, step - 2)
        kb.barrier()


NBIS = 16


def phase_dsa(g, l):
    kb, nc = g.kb, g.nc
    S = g.scr
    with ExitStack() as ph:
        def sb(name, shape, dt_=F32):
            return kb.sb(ph, name, shape, dt_)
        dk = sb("dk", [128, L], BF16)
        ik = sb("ik", [128, L], BF16)
        vaug = sb("vaug", [128, NT, 2, 128], BF16)
        kb.dma("sp", dk[:], S["dkT"].t.rearrange("h d t -> (h d) t"), W=[dk])
        kb.dma("sp", ik[0:64, :], S["ikT"].t[0], W=[ik])
        kb.dma("sp", ik[64:128, :], S["ikT"].t[0], W=[ik])
        kb.op("dve", lambda e: e.memset(vaug[:], 1.0), W=[vaug])
        for gg in range(2):
            kb.dma("sp", vaug[:, :, gg, 0:64], S["dv"].t[:, gg * 64:(gg + 1) * 64].rearrange("(t p) d -> p t d", p=128), W=[vaug])
        dqt = [sb("dqt", [128, 4, 128], BF16) for _ in range(2)]
        iqt = [sb("iqt", [128, 4, 128], BF16) for _ in range(2)]
        score = [sb("score", [128, L]) for _ in range(2)]
        rl = [sb("rl", [128, 512], BF16) for _ in range(4)]
        wabs = [sb("wabs", [128, 8]) for _ in range(2)]
        wsgn = [sb("wsgn", [128, 8]) for _ in range(2)]
        Dg = [sb("Dg", [128, 8, 128], BF16) for _ in range(2)]
        junk = sb("junk", [128, L], BF16)
        maskb = [sb("maskb", [128, L], BF16) for _ in range(2)]
        maskT = [sb("maskT", [128, NT, 128], BF16) for _ in range(2)]
        Eb = [sb("E", [128, 512], BF16) for _ in range(4)]
        M_ = [sb("M", [128, 1]) for _ in range(2)]
        A_ = [sb("A", [128, NBIS + 1]) for _ in range(2)]
        mid = [sb("mid", [128, 1]) for _ in range(2)]
        cnt = [sb("cnt", [128, 1]) for _ in range(2)]
        sela = [sb("sela", [128, 1]) for _ in range(2)]
        R0 = [sb("R0", [64, 512]) for _ in range(2)]
        ytmp = [sb("ytmp", [64, 512], BF16) for _ in range(2)]
        yraw = [sb("yraw", [64, 512]) for _ in range(2)]
        yst = [sb("dyst", [128, 4, 128], BF16) for _ in range(2)]
        cn = dict(rl=0, E=0)

        def stage_I(qt):
            k = qt % 2
            W = 128 * (qt + 1)
            dsl = slice(qt * 128, (qt + 1) * 128)
            if qt < 2:
                return
            iq_src = S["iqT"].t.rearrange("(hp e) d t -> e d hp t", e=2)
            for e_ in range(2):
                kb.dma("sp", iqt[k][e_ * 64:(e_ + 1) * 64, :, :], iq_src[e_][:, :, dsl], W=[iqt[k]])
            sc = score[k]
            nch = (W + 511) // 512
            wv = g.wix_tok[:, qt, :]
            kb.op("act", lambda e: e.activation(out=wabs[k][:], in_=wv, func=AF.Abs), R=[g.wix_tok], W=[wabs[k]])
            kb.op("dve", lambda e: e.tensor_scalar(out=wsgn[k][:], in0=wv, scalar1=0.0, scalar2=2.0, op0=ALU.is_ge, op1=ALU.mult),
                  R=[g.wix_tok], W=[wsgn[k]])
            kb.op("dve", lambda e: e.tensor_scalar(out=wsgn[k][:], in0=wsgn[k][:], scalar1=-1.0, scalar2=None, op0=ALU.add),
                  R=[wsgn[k]], W=[wsgn[k]])
            for h in range(8):
                kb.op("dve", lambda e, h=h: e.tensor_scalar(out=Dg[k][:, h, :], in0=g.identb[:], scalar1=wsgn[k][:, h:h + 1], scalar2=None,
                                                            op0=ALU.mult), R=[g.identb, wsgn[k]], Wd=[Dg[k]])
            items = [(ch, h) for ch in range(nch) for h in range(8)]
            pend = {}
            pscs = {}

            def mm(j):
                ch, h = items[j]
                n = min(512, W - ch * 512)
                p = kb.ps()
                hp_ = (h % 2) * 64
                kb.op("pe", lambda e: e.matmul(p[:, 0:n], iqt[k][hp_:hp_ + 64, h // 2, :], ik[hp_:hp_ + 64, ch * 512:ch * 512 + n],
                                               start=True, stop=True), R=[iqt[k], ik], W=[p])
                r = rl[cn["rl"] % 4]
                cn["rl"] += 1
                kb.op("act", lambda e: e.activation(out=r[:, 0:n], in_=p[:, 0:n], func=AF.Relu, scale=wabs[k][:, h:h + 1]),
                      R=[p, wabs[k]], W=[r])
                pend[j] = r

            def acc(j):
                ch, h = items[j]
                n = min(512, W - ch * 512)
                r = pend.pop(j)
                if h == 0:
                    pscs[ch] = kb.ps(pin=True)
                psc = pscs[ch]
                kb.op("pe", lambda e: e.matmul(psc[:, 0:n], Dg[k][:, h, :], r[:, 0:n], start=(h == 0), stop=(h == 7)),
                      R=[Dg[k], r], W=[psc])
                if h == 7:
                    kb.unpin(psc)
                    kb.op("act", lambda e: e.activation(out=sc[:, ch * 512:ch * 512 + n], in_=psc[:, 0:n], func=AF.Copy), R=[psc], Wd=[sc])

            for j in range(0, len(items) + 2, 2):
                for jj in (j, j + 1):
                    if jj < len(items):
                        mm(jj)
                for jj in (j - 2, j - 1):
                    if 0 <= jj < len(items):
                        acc(jj)

        def stage_B(qt):
            k = qt % 2
            W = 128 * (qt + 1)
            dsl = slice(qt * 128, (qt + 1) * 128)
            mT = maskT[k]
            if qt < 2:
                if qt == 1:
                    kb.op("dve", lambda e: e.memset(mT[:, 0, :], 0.0), W=[mT])
                kb.op("dve", lambda e: e.tensor_copy(out=mT[:, qt, :], in_=g.negmb[:]), R=[g.negmb], W=[mT])
                return
            sc = score[k]
            kb.op("dve", lambda e: e.tensor_reduce(out=M_[k][:], in_=sc[:, 0:W], axis=AX.X, op=ALU.max, apply_absolute_value=True),
                  R=[sc], W=[M_[k]])
            kb.op("dve", lambda e: e.tensor_scalar(out=A_[k][:, 0:NBIS], in0=g.pow2[:, 0:NBIS], scalar1=M_[k][:, 0:1], scalar2=None, op0=ALU.mult),
                  R=[g.pow2, M_[k]], W=[A_[k]])
            kb.op("dve", lambda e: e.tensor_copy(out=A_[k][:, NBIS:NBIS + 1], in_=A_[k][:, NBIS - 1:NBIS]), R=[A_[k]], W=[A_[k]])
            kb.op("dve", lambda e: e.tensor_tensor(out=sc[:, dsl], in0=sc[:, dsl], in1=g.negup[:], op=ALU.add), R=[sc, g.negup], W=[sc])
            kb.op("dve", lambda e: e.memset(mid[k][:], 0.0), W=[mid[k]])
            for it in range(NBIS):
                kb.op("dve", lambda e: e.tensor_scalar(out=junk[:, 0:W], in0=sc[:, 0:W], scalar1=mid[k][:, 0:1], scalar2=0.0,
                                                       op0=ALU.is_ge, op1=ALU.add, accum_out=cnt[k][:]),
                      R=[sc, mid[k]], W=[junk, cnt[k]])
                kb.op("dve", lambda e, it=it: e.tensor_scalar(out=sela[k][:], in0=cnt[k][:], scalar1=255.5, scalar2=A_[k][:, it:it + 1],
                                                              op0=ALU.is_ge, op1=ALU.mult), R=[cnt[k], A_[k]], W=[sela[k]])
                kb.op("dve", lambda e, it=it: e.scalar_tensor_tensor(out=mid[k][:], in0=sela[k][:], scalar=A_[k][:, it + 1:it + 2], in1=mid[k][:],
                                                                     op0=ALU.subtract, op1=ALU.add), R=[sela[k], A_[k], mid[k]], W=[mid[k]])
            mb = maskb[k]
            kb.op("dve", lambda e: e.tensor_scalar(out=mb[:, 0:W], in0=sc[:, 0:W], scalar1=mid[k][:, 0:1], scalar2=-30000.0,
                                                   op0=ALU.is_lt, op1=ALU.mult), R=[sc, mid[k]], W=[mb])

        def stage_T(qt):
            if qt < 2:
                return
            k = qt % 2
            mb, mT = maskb[k], maskT[k]
            for b0 in range(0, qt + 1, 8):
                nb = min(8, qt + 1 - b0)
                pt = kb.ps()
                ptb = pt[:, :].bitcast(BF16)
                for j in range(nb):
                    kb.op("pe", lambda e, j=j, b0=b0: e.transpose(ptb[:, j * 128:(j + 1) * 128], mb[:, (b0 + j) * 128:(b0 + j + 1) * 128], g.identb[:]),
                          R=[mb, g.identb], W=[pt])
                kb.op("act", lambda e, b0=b0, nb=nb: e.activation(out=mT[:, b0:b0 + nb, :], in_=ptb[:, 0:nb * 128].rearrange("p (j t) -> p j t", j=nb),
                                                                 func=AF.Copy), R=[pt], Wd=[mT])

        def load_dq(buf, qt_):
            src = S["dqT"].t.rearrange("(g hh) d t -> g d hh t", g=2)
            for g_ in range(2):
                kb.dma("sp", buf[g_ * 64:(g_ + 1) * 64, :, :], src[g_][:, :, qt_ * 128:(qt_ + 1) * 128], W=[buf])

        def stage_A(qt):
            k = qt % 2
            dsl = slice(qt * 128, (qt + 1) * 128)
            mT = maskT[k]
            ys = yst[k]
            if qt + 1 < NT:
                load_dq(dqt[1 - k], qt + 1)
            pOs = [kb.ps(pin=True) for _ in range(2)]
            Es = {}

            def qk(sbk):
                pSs = [kb.ps() for _ in range(2)]
                for gq in range(2):
                    kb.op("pe", lambda e, gq=gq: e.matmul(pSs[gq][:, :], dk[gq * 64:(gq + 1) * 64, sbk * 128:(sbk + 1) * 128],
                                                          dqt[k][gq * 64:(gq + 1) * 64, :, :], start=True, stop=False),
                          R=[dk, dqt[k]], W=[pSs[gq]])
                for gq in range(2):
                    kb.op("pe", lambda e, gq=gq: e.matmul(pSs[gq][:, :], g.identb[:], mT[:, sbk, :].unsqueeze(1).broadcast_to([128, 4, 128]),
                                                          start=False, stop=True), R=[g.identb, mT], W=[pSs[gq]])
                for gq in range(2):
                    E = Eb[cn["E"] % 4]
                    cn["E"] += 1
                    kb.op("act", lambda e, gq=gq, E=E: e.activation(out=E[:], in_=pSs[gq][:, :], func=AF.Exp, scale=0.125), R=[pSs[gq]], W=[E])
                    Es[(sbk, gq)] = E

            def av(sbk):
                for gq in range(2):
                    E = Es.pop((sbk, gq))
                    kb.op("pe", lambda e, gq=gq, E=E: e.matmul(pOs[gq][:, :], vaug[:, sbk, gq, :], E[:], start=(sbk == 0), stop=(sbk == qt)),
                          R=[vaug, E], W=[pOs[gq]])

            for sbk in range(qt + 2):
                if sbk <= qt:
                    qk(sbk)
                if sbk >= 1:
                    av(sbk - 1)
            for gq in range(2):
                def norm(pO=pOs[gq], gq=gq, ys=ys, dsl=dsl, last=(gq == 1)):
                    kb.unpin(pO)
                    r0, yt = R0[gq], ytmp[gq]
                    kb.op("act", lambda e: e.activation(out=r0[:], in_=pO[64:128, :], func=AF.Ln), R=[pO], W=[r0])
                    kb.op("act", lambda e: e.activation(out=r0[:], in_=r0[:], func=AF.Exp, scale=-1.0), R=[r0], W=[r0])
                    yr = yraw[gq]
                    kb.op("act", lambda e: e.activation(out=yr[:], in_=pO[0:64, :], func=AF.Copy), R=[pO], W=[yr])
                    kb.op("pool", lambda e: e.tensor_tensor(out=yt[:], in0=yr[:], in1=r0[:], op=ALU.mult), R=[yr, r0], W=[yt])
                    for hh in range(4):
                        h = 4 * gq + hh
                        hp, hc = (h % 2) * 64, h // 2
                        kb.op("act", lambda e, hh=hh, hp=hp, hc=hc: e.activation(out=ys[hp:hp + 64, hc, :], in_=yt[:, hh * 128:(hh + 1) * 128], func=AF.Copy),
                              R=[yt], Wd=[ys])
                    if last:
                        kb.dma("sp", S["yT"].t[12:16, :, dsl].rearrange("c p t -> p c t"), ys[:], R=[ys])
                pending.append(norm)

        pending = []
        load_dq(dqt[0], 0)
        stage_I(0)
        stage_B(0)
        stage_T(0)
        stage_I(1)
        modw = [sb("adaw", [128, KC, 512], BF16) for _ in range(2)] if g.dsa_hook is not None else None
        for qt in range(NT):
            if qt + 2 < NT:
                stage_I(qt + 2)
            prev_norms = pending[:]
            del pending[:]
            stage_A(qt)
            for nf in prev_norms:
                nf()
            if modw is not None and 1 <= qt < 13:
                mod_piece_load(g, g.dsa_hook, qt - 1, modw[(qt - 1) % 2])
            if modw is not None and 2 <= qt < 14:
                mod_piece_mm(g, g.dsa_hook, qt - 2, modw[(qt - 2) % 2])
            if qt + 1 < NT:
                stage_B(qt + 1)
                stage_T(qt + 1)
        while pending:
            pending.pop(0)()
        kb.barrier()


def phase_merge(g, l):
    kb, nc = g.kb, g.nc
    S = g.scr
    with ExitStack() as ph:
        def sb(name, shape, dt_=F32):
            return kb.sb(ph, name, shape, dt_)
        yT = sb("yTall", [128, 16, L], BF16)
        yTb = [kb.buf() for _ in range(4)]
        for q in range(4):
            kb.dma("sp", yT[:, q * 4:(q + 1) * 4, :], S["yT"].t[q * 4:(q + 1) * 4].rearrange("c p t -> p c t"), W=[yTb[q]])
        mT = sb("mergedT", [128, KC, L], BF16)
        mTb = [[kb.buf() for tb in range(4)] for c in range(KC)]
        wbr = [sb("wbr", [128, 16, 256], BF16) for _ in range(2)]
        gt = [sb("gt", [128, 3, 512], BF16) for _ in range(2)]
        gbr = [sb("gbr", [128, 3, 512], BF16) for _ in range(2)]
        it = 0
        for c2 in range(4):
            w = wbr[c2 % 2]
            csl = slice(c2 * 256, (c2 + 1) * 256)
            kb.dma("pool", w[:, 0:8, :], g.inp["w_br_ssd"].t[l][:, csl].rearrange("(kc p) n -> p kc n", p=128), W=[w])
            kb.dma("pool", w[:, 8:12, :], g.inp["w_br_sb"].t[l][:, csl].rearrange("(kc p) n -> p kc n", p=128), W=[w])
            kb.dma("pool", w[:, 12:16, :], g.inp["w_br_dsa"].t[l][:, csl].rearrange("(kc p) n -> p kc n", p=128), W=[w])
            for cl in range(2):
                c = c2 * 2 + cl
                for tb in range(4):
                    k = it % 2
                    it += 1
                    tsl = slice(tb * 512, (tb + 1) * 512)
                    for br in range(3):
                        kb.dma("sp", gt[k][:, br, :], S["gT"].t[br * 8 + c, :, tsl], W=[gt[k]])
                    pbr = []
                    for br, (k0, k1) in enumerate(((0, 8), (8, 12), (12, 16))):
                        p = kb.ps()
                        for kc in range(k0, k1):
                            kb.op("pe", lambda e, p=p, kc=kc, k0=k0, k1=k1: e.matmul(
                                p[:, :], w[:, kc, cl * 128:(cl + 1) * 128], yT[:, kc, tsl], start=(kc == k0), stop=(kc == k1 - 1)),
                                R=[w, yTb[kc // 4]], W=[p])
                        pbr.append(p)
                    gb = gbr[k]
                    for br in range(3):
                        kb.op("dve", lambda e, br=br: e.tensor_tensor(out=gb[:, br, :], in0=pbr[br][:, :], in1=gt[k][:, br, :], op=ALU.mult),
                              R=[pbr[br], gt[k]], Wd=[gb])
                    pm = kb.ps()
                    for br in range(3):
                        kb.op("pe", lambda e, br=br: e.matmul(pm[:, :], g.identb[:], gb[:, br, :], start=(br == 0), stop=(br == 2)),
                              R=[g.identb, gb], W=[pm])
                    kb.op("act", lambda e: e.activation(out=mT[:, c, tsl], in_=pm[:, :], func=AF.Copy), R=[pm], W=[mTb[c][tb]])
        wo = [sb("wo", [128, KC, 256], BF16) for _ in range(2)]
        for c2 in range(4):
            w = wo[c2 % 2]
            kb.dma("pool", w[:], g.inp["w_out"].t[l][:, c2 * 256:(c2 + 1) * 256].rearrange("(kc p) n -> p kc n", p=128), W=[w])
            for cl in range(2):
                c = c2 * 2 + cl
                for tb in range(4):
                    tsl = slice(tb * 512, (tb + 1) * 512)
                    p = kb.ps()
                    for kc in range(KC):
                        kb.op("pe", lambda e, p=p, kc=kc: e.matmul(p[:, :], w[:, kc, cl * 128:(cl + 1) * 128], mT[:, kc, tsl],
                                                                   start=(kc == 0), stop=(kc == KC - 1)), R=[w, mTb[kc][tb]], W=[p])
                    xs = g.xT[:, c, tsl]
                    kb.op("dve", lambda e, p=p, c=c, xs=xs: e.scalar_tensor_tensor(
                        out=xs, in0=p[:, :], scalar=g.modT[:, 16 + c:17 + c], in1=xs, op0=ALU.mult, op1=ALU.add),
                        R=[p, g.modT, g.xTb[c][tb]], W=[g.xTb[c][tb]])
        kb.barrier()


def phase_mlp(g, l):
    kb = g.kb
    with ExitStack() as ph:
        wup = [kb.sb(ph, f"wup{i}", [128, KC, 512], BF16) for i in range(2)]
        wdn = [kb.sb(ph, f"wdn{i}", [128, 4, D], BF16) for i in range(2)]
        act = [kb.sb(ph, f"mact{i}", [128, 4, L], BF16) for i in range(2)]
        actb = [[[kb.buf() for tb in range(4)] for hc in range(4)] for i in range(2)]
        rl = [kb.sb(ph, f"mrl{i}", [128, 512], F32) for i in range(2)]
        nrl = 0
        for gi in range(8):
            wu, wd, a, ab = wup[gi % 2], wdn[gi % 2], act[gi % 2], actb[gi % 2]
            load_w(g, wu, g.inp["w_up"].t[l][:, gi * 512:(gi + 1) * 512])
            load_w(g, wd, g.inp["w_down"].t[l][gi * 512:(gi + 1) * 512, :])
            for tb in range(4):
                for hc in range(4):
                    p = kb.ps()
                    for kc in range(KC):
                        kb.op("pe", lambda e, p=p, kc=kc, hc=hc: e.matmul(
                            p[:, :], wu[:, kc, hc * 128:(hc + 1) * 128], g.hT[:, kc, tb * 512:(tb + 1) * 512],
                            start=(kc == 0), stop=(kc == KC - 1)), R=[wu, g.hTb[kc][tb]], W=[p])
                    r = rl[nrl % 2]
                    nrl += 1
                    kb.op("act", lambda e, p=p, r=r: e.activation(out=r[:], in_=p[:, :], func=AF.Relu), R=[p], W=[r])
                    kb.op("act", lambda e, r=r, hc=hc: e.activation(out=a[:, hc, tb * 512:(tb + 1) * 512], in_=r[:], func=AF.Square),
                          R=[r], W=[ab[hc][tb]])
            import os
            if os.environ.get("MLP_UP_ONLY"):
                continue
            for tb in range(4):
                for c in range(KC):
                    p = kb.ps()
                    for hc in range(4):
                        kb.op("pe", lambda e, p=p, hc=hc, c=c: e.matmul(
                            p[:, :], wd[:, hc, c * 128:(c + 1) * 128], a[:, hc, tb * 512:(tb + 1) * 512],
                            start=(hc == 0), stop=(hc == 3)), R=[wd, ab[hc][tb]], W=[p])
                    xs = g.xT[:, c, tb * 512:(tb + 1) * 512]
                    kb.op("dve", lambda e, p=p, c=c, xs=xs: e.scalar_tensor_tensor(
                        out=xs, in0=p[:, :], scalar=g.modT[:, 40 + c:41 + c], in1=xs, op0=ALU.mult, op1=ALU.add),
                        R=[p, g.modT, g.xTb[c][tb]], W=[g.xTb[c][tb]])
        kb.barrier()


def rstd_bcast(kb, ph, xT, xTb, tb, onesb, sq, rs):
    p = kb.ps()
    for c in range(KC):
        s = sq[c % 2]
        kb.op("act", lambda e, s=s, c=c: e.activation(out=s[:], in_=xT[:, c, tb * 512:(tb + 1) * 512], func=AF.Square),
              R=[xTb[c][tb]], W=[s])
        kb.op("pe", lambda e, s=s, c=c, p=p: e.matmul(p[:, :], onesb[:], s[:], start=(c == 0), stop=(c == KC - 1)),
              R=[s, onesb], W=[p])
    kb.op("dve", lambda e: e.tensor_scalar(out=rs[:], in0=p[:, :], scalar1=1.0 / D, scalar2=EPS, op0=ALU.mult, op1=ALU.add),
          R=[p], W=[rs])
    kb.op("act", lambda e: e.activation(out=rs[:], in_=rs[:], func=AF.Ln), R=[rs], W=[rs])
    kb.op("act", lambda e: e.activation(out=rs[:], in_=rs[:], func=AF.Exp, scale=-0.5), R=[rs], W=[rs])


def final_norm(kb, nc, xT, xTb, fnw, onesb, ident, out_d):
    with ExitStack() as ph:
        sq = [kb.sb(ph, f"fsq{i}", [128, 512], BF16) for i in range(2)]
        rs = kb.sb(ph, "frs", [128, 512], F32)
        yT = [kb.sb(ph, f"fyT{i}", [128, 512], F32) for i in range(2)]
        ost = [kb.sb(ph, f"fost{i}", [128, 4, D], F32) for i in range(2)]
        for tb in range(4):
            rstd_bcast(kb, ph, xT, xTb, tb, onesb, sq, rs)
            o = ost[tb % 2]
            for c in range(KC):
                y = yT[c % 2]
                kb.op("dve", lambda e, y=y, c=c: e.scalar_tensor_tensor(
                    out=y[:], in0=xT[:, c, tb * 512:(tb + 1) * 512], scalar=fnw[:, c:c + 1], in1=rs[:],
                    op0=ALU.mult, op1=ALU.mult), R=[xTb[c][tb], fnw, rs], W=[y])
                p = kb.ps()
                for j in range(4):
                    kb.op("pe", lambda e, y=y, j=j, p=p: e.transpose(
                        p[:, j * 128:(j + 1) * 128], y[:, j * 128:(j + 1) * 128], ident[:]), R=[y, ident], W=[p])
                src = p[:, :].rearrange("p (j f) -> p j f", j=4)
                dst = o[:, :, c * 128:(c + 1) * 128]
                kb.op("act", lambda e, dst=dst, src=src: e.activation(out=dst, in_=src, func=AF.Copy), R=[p], Wd=[o])
            kb.dma("sp", out_d.t[tb * 512:(tb + 1) * 512, :].rearrange("(j p) d -> p j d", p=128), o[:], R=[o], W=[out_d])
        kb.barrier()


_NC_CACHE = {}


def make_in_maps(inputs):
    nb = inputs["x"].shape[0]
    inputs = dict(inputs)
    inputs.update(host_consts())
    shared = {n: np.ascontiguousarray(np.asarray(inputs[n], dtype=np.float32)) for n in IN_SHAPES if n not in ("x", "c")}
    in_maps = []
    for b in range(nb):
        m = dict(shared)
        m["x"] = np.ascontiguousarray(inputs["x"][b])
        m["c"] = np.ascontiguousarray(inputs["c"][b])
        in_maps.append(m)
    return in_maps


def kernel(**inputs):
    nb = inputs["x"].shape[0]
    if "nc" not in _NC_CACHE:
        _NC_CACHE["nc"] = build()[0]
    nc = _NC_CACHE["nc"]
    res = run_bass_kernel_spmd(nc, make_in_maps(inputs), core_ids=list(range(nb)))
    return np.stack([r["out"] for r in res.results], axis=0)
```

```python
import math
from contextlib import ExitStack

import numpy as np
import concourse.bass as bass
import concourse.mybir as mybir
from concourse.bass_utils import run_bass_kernel_spmd

F32 = mybir.dt.float32
BF16 = mybir.dt.bfloat16
I32 = mybir.dt.int32
AF = mybir.ActivationFunctionType
ALU = mybir.AluOpType
AX = mybir.AxisListType

D = 1024
L = 2048
DEPTH = 2
NT = L // 128
KC = D // 128
EPS = 1e-6
import os
NDS = int(os.environ.get("NDS", "12"))
STRICT = bool(int(os.environ.get("KSTRICT", "1")))


class Buf:
    __slots__ = ("name", "w", "r", "excl")

    def __init__(self, name):
        self.name = name
        self.excl = False
        self.w = None
        self.r = {}


class T:
    def __init__(self, t, buf):
        self.t = t
        self.b = buf

    def __getitem__(self, k):
        return self.t[k]


class KB:
    def __init__(self, nc):
        self.nc = nc
        self.es = ExitStack()
        self.sems = {}
        self.engs = {}
        for name, e in (("pe", nc.tensor), ("act", nc.scalar), ("dve", nc.vector),
                        ("pool", nc.gpsimd), ("sp", nc.sync)):
            key = "s_" + name
            self.sems[key] = self.es.enter_context(nc.semaphore(key))
            self.engs[name] = dict(e=e, key=key, cnt=0, seen={})
        self.dq = {}
        for q in ("sp", "pool"):
            keys = []
            for i in range(NDS):
                key = f"d_{q}{i}"
                self.sems[key] = self.es.enter_context(nc.semaphore(key))
                keys.append(key)
            self.dq[q] = dict(keys=keys, cnt=[0] * NDS, nxt=0)
        self.nbuf = 0
        self.psum = []
        self.ps_next = 0
        self.pinned = set()

    def buf(self, name=None):
        self.nbuf += 1
        return Buf(name or f"b{self.nbuf}")

    def sb(self, es, name, shape, dtype):
        self.nbuf += 1
        name = f"{name}_{self.nbuf}"
        t = es.enter_context(self.nc.sbuf_tensor(name, list(shape), dtype))
        return T(t, self.buf(name))

    def dram(self, name, shape, dtype, kind="Internal"):
        t = self.nc.dram_tensor(name, list(shape), dtype, kind=kind)
        return T(t.ap(), self.buf(name))

    def init_psum(self):
        for i in range(8):
            t = self.es.enter_context(self.nc.psum_tensor(f"ps{i}", [128, 512], F32))
            self.psum.append(T(t, self.buf(f"ps{i}")))
            self.psum[-1].b.excl = True

    def ps(self, pin=False):
        while True:
            i = self.ps_next
            self.ps_next = (self.ps_next + 1) % 8
            if i not in self.pinned:
                break
        if pin:
            self.pinned.add(i)
        return self.psum[i]

    def unpin(self, p):
        self.pinned.discard(self.psum.index(p))

    def _deps(self, own_key, R, W, same_raw, Wd=()):
        deps = {}

        def add(ev):
            if ev is None:
                return
            k, v = ev
            if deps.get(k, 0) < v:
                deps[k] = v

        for t in R:
            b = t.b if isinstance(t, T) else t
            if b.w is not None and (b.w[0] != own_key or same_raw):
                add(b.w)
            if b.excl:
                for k, v in b.r.items():
                    if k != own_key:
                        add((k, v))
        for t in W:
            b = t.b if isinstance(t, T) else t
            if b.w is not None and (b.w[0] != own_key or (STRICT and same_raw)):
                add(b.w)
            for k, v in b.r.items():
                if k != own_key or (STRICT and same_raw):
                    add((k, v))
        for t in Wd:
            b = t.b if isinstance(t, T) else t
            if b.w is not None and b.w[0] != own_key:
                add(b.w)
            for k, v in b.r.items():
                if k != own_key or (STRICT and same_raw):
                    add((k, v))
        return deps

    def _mark(self, ev, R, W):
        k, v = ev
        for t in R:
            b = t.b if isinstance(t, T) else t
            if b.r.get(k, 0) < v:
                b.r[k] = v
        for t in W:
            b = t.b if isinstance(t, T) else t
            b.w = ev
            b.r = {}

    def _wait(self, eng, deps):
        for k, v in deps.items():
            if eng["seen"].get(k, 0) < v:
                eng["e"].wait_ge(self.sems[k], v)
                eng["seen"][k] = v

    def op(self, engname, fn, R=(), W=(), Wd=()):
        eng = self.engs[engname]
        deps = self._deps(eng["key"], R, W, same_raw=(engname != "pe"), Wd=Wd)
        W = list(W) + list(Wd)
        self._wait(eng, deps)
        ins = fn(eng["e"])
        eng["cnt"] += 1
        ins.then_inc(self.sems[eng["key"]], 1)
        self._mark((eng["key"], eng["cnt"]), R, W)
        return ins

    def dma(self, q, out, in_, R=(), W=(), **kw):
        eng = self.engs[q]
        dq = self.dq[q]
        deps = self._deps(None, R, W, same_raw=True)
        i = dq["nxt"]
        dq["nxt"] = (i + 1) % NDS
        key = dq["keys"][i]
        if dq["cnt"][i] > 0:
            deps[key] = max(deps.get(key, 0), dq["cnt"][i])
        self._wait(eng, deps)
        dq["cnt"][i] += 16
        eng["e"].dma_start(out=out, in_=in_, **kw).then_inc(self.sems[key], 16)
        self._mark((key, dq["cnt"][i]), R, W)

    def barrier(self):
        allev = {}
        for name, eng in self.engs.items():
            if eng["cnt"] > 0:
                allev[eng["key"]] = eng["cnt"]
        for q, dq in self.dq.items():
            for key, c in zip(dq["keys"], dq["cnt"]):
                if c > 0:
                    allev[key] = c
        for name, eng in self.engs.items():
            deps = {k: v for k, v in allev.items() if k != eng["key"]}
            self._wait(eng, deps)

    def finish(self, out_bufs):
        eng = self.engs["sp"]
        deps = {}
        for t in out_bufs:
            b = t.b if isinstance(t, T) else t
            if b.w is not None:
                deps[b.w[0]] = max(deps.get(b.w[0], 0), b.w[1])
        self._wait(eng, deps)
        self.barrier()


IN_SHAPES = {
    "x": [L, D], "c": [D], "norm1_w": [DEPTH, D], "ada_w": [DEPTH, D, 6 * D], "ada_b": [DEPTH, 6 * D],
    "w_in": [DEPTH, D, 8536], "conv_w": [DEPTH, 4, 1536], "conv_b": [DEPTH, 1536], "dt_bias": [DEPTH, 16],
    "a_log": [DEPTH, 16], "d_skip": [DEPTH, 16], "ssd_norm_w": [DEPTH, D], "w_br_ssd": [DEPTH, D, D],
    "w_br_sb": [DEPTH, 512, D], "w_br_dsa": [DEPTH, 512, D], "w_out": [DEPTH, D, D], "norm2_w": [DEPTH, D],
    "w_up": [DEPTH, D, 4 * D], "w_down": [DEPTH, 4 * D, D], "final_norm_w": [D],
    "rope_cs": [L, 16],
}


def host_consts():
    half = 8
    inv_freq = np.exp(np.arange(half, dtype=np.float32) * np.float32(-2.0 * math.log(500000.0) / 16)).astype(np.float32)
    ang = np.arange(L, dtype=np.float32)[:, None] * inv_freq[None, :]
    return {"rope_cs": np.concatenate([np.cos(ang), np.sin(ang)], axis=1).astype(np.float32)}


class G:
    pass


def build(nlayers=DEPTH, dbg=(), skip_mixer=False, phases=("mod", "inproj", "ssd", "sb", "dsa", "merge", "norm2", "mlp")):
    nc = bass.Bass("TRN2", target_bir_lowering=False)
    kb = KB(nc)
    es = kb.es
    kb.init_psum()
    g = G()
    g.nc, g.kb, g.es, g.dbg = nc, kb, es, set(dbg)
    g.inp = {n: kb.dram(n, shp, F32, kind="ExternalInput") for n, shp in IN_SHAPES.items()}
    out_d = kb.dram("out", [L, D], F32, kind="ExternalOutput")
    g.dbg_out = {}

    g.ident = ident = kb.sb(es, "ident", [128, 128], F32)
    g.identb = identb = kb.sb(es, "identb", [128, 128], BF16)
    g.onesb = onesb = kb.sb(es, "onesb", [128, 128], BF16)
    g.onesf = kb.sb(es, "onesf", [128, 128], F32)
    g.triu = kb.sb(es, "triu", [128, 128], F32)
    g.negm = kb.sb(es, "negm", [128, 128], F32)
    g.triub = kb.sb(es, "triub", [128, 128], BF16)
    g.strl = kb.sb(es, "strl", [128, 128], F32)
    g.strlb = kb.sb(es, "strlb", [128, 128], BF16)
    g.trilb = kb.sb(es, "trilb", [128, 128], BF16)
    g.negup = kb.sb(es, "negup", [128, 128], F32)
    g.negmb = kb.sb(es, "negmb", [128, 128], BF16)
    g.pow2 = kb.sb(es, "pow2", [128, 24], F32)
    with ExitStack() as tmp:
        coli = kb.sb(tmp, "coli", [128, 128], I32)
        rowi = kb.sb(tmp, "rowi", [128, 1], I32)
        colf = kb.sb(tmp, "colf", [128, 128], F32)
        rowf = kb.sb(tmp, "rowf", [128, 1], F32)
        kb.op("pool", lambda e: e.iota(coli[:], [[1, 128]], base=0, channel_multiplier=0), W=[coli])
        kb.op("pool", lambda e: e.iota(rowi[:], [[0, 1]], base=0, channel_multiplier=1), W=[rowi])
        kb.op("dve", lambda e: e.tensor_copy(out=colf[:], in_=coli[:]), R=[coli], W=[colf])
        kb.op("dve", lambda e: e.tensor_copy(out=rowf[:], in_=rowi[:]), R=[rowi], W=[rowf])
        kb.op("dve", lambda e: e.tensor_scalar(out=ident[:], in0=colf[:], scalar1=rowf[:, 0:1], scalar2=None,
                                               op0=ALU.is_equal), R=[colf, rowf], W=[ident])
        kb.op("dve", lambda e: e.tensor_copy(out=identb[:], in_=ident[:]), R=[ident], W=[identb])
        kb.op("dve", lambda e: e.memset(onesb[:], 1.0), W=[onesb])
        kb.op("dve", lambda e: e.memset(g.onesf[:], 1.0), W=[g.onesf])
        kb.op("dve", lambda e: e.tensor_scalar(out=g.triu[:], in0=colf[:], scalar1=rowf[:, 0:1], scalar2=None, op0=ALU.is_ge),
              R=[colf, rowf], W=[g.triu])
        kb.op("dve", lambda e: e.tensor_scalar(out=g.negm[:], in0=g.triu[:], scalar1=-1.0, scalar2=30000.0, op0=ALU.add, op1=ALU.mult),
              R=[g.triu], W=[g.negm])
        kb.op("dve", lambda e: e.tensor_copy(out=g.triub[:], in_=g.triu[:]), R=[g.triu], W=[g.triub])
        kb.op("dve", lambda e: e.tensor_copy(out=g.negmb[:], in_=g.negm[:]), R=[g.negm], W=[g.negmb])
        kb.op("dve", lambda e: e.tensor_scalar(out=g.strl[:], in0=colf[:], scalar1=rowf[:, 0:1], scalar2=None, op0=ALU.is_lt),
              R=[colf, rowf], W=[g.strl])
        kb.op("dve", lambda e: e.tensor_copy(out=g.strlb[:], in_=g.strl[:]), R=[g.strl], W=[g.strlb])
        kb.op("dve", lambda e: e.tensor_scalar(out=g.trilb[:], in0=colf[:], scalar1=rowf[:, 0:1], scalar2=None, op0=ALU.is_le),
              R=[colf, rowf], W=[g.trilb])
        kb.op("dve", lambda e: e.tensor_scalar(out=g.negup[:], in0=g.trilb[:], scalar1=-1.0, scalar2=1.0e9, op0=ALU.add, op1=ALU.mult),
              R=[g.trilb], W=[g.negup])
        for kk_ in range(24):
            kb.op("dve", lambda e, kk_=kk_: e.memset(g.pow2[:, kk_:kk_ + 1], float(2.0 ** (-kk_))), Wd=[g.pow2])
        kb.barrier()

    g.xT = xT = kb.sb(es, "xT", [128, KC, L], F32)
    g.xTb = xTb = [[kb.buf(f"xT{c}_{tb}") for tb in range(4)] for c in range(KC)]
    g.hTb = [[kb.buf(f"hT{c}_{tb}") for tb in range(4)] for c in range(KC)]

    fnw = load_cols(g, es, "fnw", g.inp["final_norm_w"].t, KC)
    g.ccol = load_cols(g, es, "ccol", g.inp["c"].t, KC)
    g.csilu = kb.sb(es, "csilu", [128, KC], BF16)
    kb.op("act", lambda e: e.activation(out=g.csilu[:], in_=g.ccol[:], func=AF.Silu), R=[g.ccol], W=[g.csilu])
    g.modT_l = [kb.sb(es, "modT", [128, 48], F32) for _ in range(DEPTH)]
    g.modraw = [kb.sb(es, "modraw", [128, 48], F32) for _ in range(DEPTH)]
    g.A1_l = [kb.sb(es, "A1", [128, KC], F32) for _ in range(DEPTH)]
    g.A2_l = [kb.sb(es, "A2", [128, KC], F32) for _ in range(DEPTH)]
    g.dsa_hook = None

    x_in = g.inp["x"]
    with ExitStack() as ph:
        xin = [kb.sb(ph, f"xin{i}", [128, D], F32) for i in range(2)]
        for tt in range(NT):
            xi = xin[tt % 2]
            kb.dma("sp", xi[:], x_in.t[tt * 128:(tt + 1) * 128, :], W=[xi])
            for half in range(2):
                p = kb.ps()
                for j in range(4):
                    c = half * 4 + j
                    kb.op("pe", lambda e, c=c, j=j, p=p, xi=xi: e.transpose(
                        p[:, j * 128:(j + 1) * 128], xi[:, c * 128:(c + 1) * 128], ident[:]),
                        R=[xi, ident], W=[p])
                tb = tt // 4
                wb = [xTb[half * 4 + j][tb] for j in range(4)]
                dst = xT[:, half * 4:half * 4 + 4, tt * 128:(tt + 1) * 128]
                src = p[:, :].rearrange("p (j t) -> p j t", j=4)
                if half == 0:
                    kb.op("act", lambda e, dst=dst, src=src: e.activation(out=dst, in_=src, func=AF.Copy), R=[p], W=wb)
                else:
                    kb.op("dve", lambda e, dst=dst, src=src: e.tensor_copy(out=dst, in_=src), R=[p], W=wb)
        kb.barrier()

    def scr(name, shape, dtype=BF16):
        return kb.dram("scr_" + name, shape, dtype, kind=("ExternalOutput" if name in g.dbg else "Internal"))
    g.scr = dict(zs=scr("zs", [L, 1024]), sbv=scr("sbv", [L, 512]), dv=scr("dv", [L, 128]),
                 dqT=scr("dqT", [8, 64, L]), dkT=scr("dkT", [2, 64, L]), iqT=scr("iqT", [8, 64, L]), ikT=scr("ikT", [1, 64, L]),
                 yT=scr("yT", [16, 128, L]), sbqT=scr("sbqT", [4, 128, L]), sbkT=scr("sbkT", [4, 128, L]), gT=scr("gT", [24, 128, L]))
    for name in g.scr:
        if name in g.dbg:
            g.dbg_out["scr_" + name] = g.scr[name]
    g.ropecs = kb.sb(es, "ropecs", [128, NT, 16], F32)
    kb.dma("sp", g.ropecs[:], g.inp["rope_cs"].t.rearrange("(t p) c -> p t c", p=128), W=[g.ropecs])
    g.dt_tok = kb.sb(es, "dt_tok", [128, NT, 16], F32)
    g.wix_tok = kb.sb(es, "wix_tok", [128, NT, 8], F32)
    g.dtb_bc = kb.sb(es, "dtb_bc", [128, 16], F32)
    g.cb = kb.sb(es, "cb", [128, 12], F32)
    g.cw = [kb.sb(es, f"cw{k}", [128, 12], F32) for k in range(4)]

    for l in range(nlayers):
        g.modT, g.A1, g.A2 = g.modT_l[l], g.A1_l[l], g.A2_l[l]
        if "mod" in phases:
            if l == 0:
                phase_mod(g, l)
            else:
                mod_finish(g, l)
        g.dsa_hook = (l + 1) if (l + 1 < nlayers and "mod" in phases) else None
        dump_sb(g, f"mod{l}", g.modT, [128, 48])
        if not skip_mixer:
            with ExitStack() as ssd_scope:
                g.xs_tok = kb.sb(ssd_scope, "xs_tok", [128, NT, 1024], BF16)
                g.bt_tok = kb.sb(ssd_scope, "bt_tok", [128, NT, 256], BF16)
                g.bcT = kb.sb(ssd_scope, "bcT", [128, 4, L], BF16)
                with nc.allow_non_contiguous_dma(reason="tiny vectors"):
                    kb.dma("sp", g.dtb_bc[:], g.inp["dt_bias"].t[l].partition_broadcast(128), W=[g.dtb_bc])
                    kb.dma("sp", g.cb[:], g.inp["conv_b"].t[l].rearrange("(c p) -> p c", p=128), W=[g.cb])
                    for k in range(4):
                        kb.dma("sp", g.cw[k][:], g.inp["conv_w"].t[l][k].rearrange("(c p) -> p c", p=128), W=[g.cw[k]])
                with ExitStack() as hsc:
                    g.hT = kb.sb(hsc, "hT", [128, KC, L], BF16)
                    phase_norm(g, l, which=1)
                    dump_hT(g, f"h{l}")
                    if "inproj" in phases:
                        phase_inproj(g, l)
                dump_sb(g, f"xs_tok{l}", g.xs_tok, [128, NT, 1024], BF16)
                dump_sb(g, f"bt_tok{l}", g.bt_tok, [128, NT, 256], BF16)
                dump_sb(g, f"bcT{l}", g.bcT, [128, 4, L], BF16)
                dump_sb(g, f"dt_tok{l}", g.dt_tok, [128, NT, 16], F32)
                dump_sb(g, f"wix_tok{l}", g.wix_tok, [128, NT, 8], F32)
                kb.barrier()
                if "ssd" in phases:
                    phase_ssd(g, l)
            if "sb" in phases:
                phase_sb(g, l)
            if "dsa" in phases:
                phase_dsa(g, l)
            if "merge" in phases:
                phase_merge(g, l)
            dump_xT(g, f"xmix{l}")
        with ExitStack() as hsc:
            g.hT = kb.sb(hsc, "hT", [128, KC, L], BF16)
            if "norm2" in phases:
                phase_norm(g, l, which=2)
            if "mlp" in phases:
                phase_mlp(g, l)
        dump_xT(g, f"xout{l}")

    final_norm(kb, nc, xT, xTb, fnw, onesb, ident, out_d)
    kb.finish([out_d] + list(g.dbg_out.values()))
    return nc, sorted(g.dbg_out.keys())


def load_cols(g, es_, name, src_ap, n):
    kb = g.kb
    t = kb.sb(es_, name, [128, n], F32)
    with g.nc.allow_non_contiguous_dma(reason="tiny per-feature vector load"):
        for c0 in range(0, n, 8):
            c1 = min(n, c0 + 8)
            kb.dma("sp", t[:, c0:c1], src_ap[c0 * 128:c1 * 128].rearrange("(c p) -> p c", p=128), W=[t])
    return t


def dump_sb(g, name, t, shape, dtype=F32):
    if name not in g.dbg:
        return
    d = g.kb.dram("dbg_" + name, shape, dtype, kind="ExternalOutput")
    g.kb.barrier()
    g.kb.dma("sp", d.t, t[:], R=[t], W=[d])
    g.dbg_out["dbg_" + name] = d


def dump_xT(g, name):
    if name not in g.dbg:
        return
    d = g.kb.dram("dbg_" + name, [128, KC, L], F32, kind="ExternalOutput")
    g.kb.barrier()
    g.kb.dma("sp", d.t, g.xT[:], R=[b for row in g.xTb for b in row], W=[d])
    g.dbg_out["dbg_" + name] = d


def dump_hT(g, name):
    if name not in g.dbg:
        return
    d = g.kb.dram("dbg_" + name, [128, KC, L], BF16, kind="ExternalOutput")
    g.kb.barrier()
    g.kb.dma("sp", d.t, g.hT[:], R=[b for row in g.hTb for b in row], W=[d])
    g.dbg_out["dbg_" + name] = d


def load_w(g, t, src_rows_ap):
    g.kb.dma("pool", t[:], src_rows_ap.rearrange("(kc p) n -> p kc n", p=128), W=[t])


def mod_piece_load(g, l, j4, w):
    g.kb.dma("pool", w[:], g.inp["ada_w"].t[l][:, j4 * 512:(j4 + 1) * 512].rearrange("(kc p) n -> p kc n", p=128), W=[w])


def mod_piece_mm(g, l, j4, w):
    kb = g.kb
    p = kb.ps()
    for jl in range(4):
        for kc in range(KC):
            kb.op("pe", lambda e, jl=jl, kc=kc: e.matmul(p[:, jl:jl + 1], w[:, kc, jl * 128:(jl + 1) * 128], g.csilu[:, kc:kc + 1],
                                                         start=(kc == 0), stop=(kc == KC - 1)), R=[w, g.csilu], W=[p])
    raw = g.modraw[l]
    kb.op("act", lambda e: e.activation(out=raw[:, j4 * 4:(j4 + 1) * 4], in_=p[:, 0:4], func=AF.Copy), R=[p], Wd=[raw])


def mod_piece(g, l, j4, w):
    mod_piece_load(g, l, j4, w)
    mod_piece_mm(g, l, j4, w)


def mod_finish(g, l):
    kb = g.kb
    with ExitStack() as ph:
        adab = load_cols(g, ph, "adab", g.inp["ada_b"].t[l], 48)
        n1w = load_cols(g, ph, "n1w", g.inp["norm1_w"].t[l], KC)
        n2w = load_cols(g, ph, "n2w", g.inp["norm2_w"].t[l], KC)
        modT, A1, A2 = g.modT_l[l], g.A1_l[l], g.A2_l[l]
        kb.op("dve", lambda e: e.tensor_tensor(out=modT[:], in0=g.modraw[l][:], in1=adab[:], op=ALU.add),
              R=[g.modraw[l], adab], W=[modT])
        kb.op("dve", lambda e: e.scalar_tensor_tensor(out=A1[:], in0=modT[:, 8:16], scalar=1.0, in1=n1w[:],
                                                      op0=ALU.add, op1=ALU.mult), R=[modT, n1w], W=[A1])
        kb.op("dve", lambda e: e.scalar_tensor_tensor(out=A2[:], in0=modT[:, 32:40], scalar=1.0, in1=n2w[:],
                                                      op0=ALU.add, op1=ALU.mult), R=[modT, n2w], W=[A2])
        kb.barrier()


def phase_mod(g, l):
    kb = g.kb
    with ExitStack() as ph:
        wt = [kb.sb(ph, f"adaw{i}", [128, KC, 512], BF16) for i in range(3)]
        for j4 in range(12):
            mod_piece(g, l, j4, wt[j4 % 3])
        kb.barrier()
    mod_finish(g, l)


def phase_norm(g, l, which):
    kb = g.kb
    A = g.A1 if which == 1 else g.A2
    sh0 = 0 if which == 1 else 24
    with ExitStack() as ph:
        sq = [kb.sb(ph, f"nsq{i}", [128, 512], BF16) for i in range(2)]
        rs = kb.sb(ph, "nrs", [128, 512], F32)
        tmp = [kb.sb(ph, f"ntmp{i}", [128, 512], F32) for i in range(2)]
        for tb in range(4):
            rstd_bcast(kb, ph, g.xT, g.xTb, tb, g.onesb, sq, rs)
            for c in range(KC):
                t = tmp[c % 2]
                kb.op("dve", lambda e, t=t, c=c: e.tensor_tensor(out=t[:], in0=g.xT[:, c, tb * 512:(tb + 1) * 512], in1=rs[:],
                                                                 op=ALU.mult), R=[g.xTb[c][tb], rs], W=[t])
                kb.op("act", lambda e, t=t, c=c: e.activation(
                    out=g.hT[:, c, tb * 512:(tb + 1) * 512], in_=t[:], func=AF.Identity,
                    scale=A[:, c:c + 1], bias=g.modT[:, sh0 + c:sh0 + c + 1]), R=[t, A, g.modT], W=[g.hTb[c][tb]])
        kb.barrier()


OFF = dict(z=0, xbc=1024, dt=2560, sbq=2576, sbk=3088, sbv=3600, dsq=4112, dsk=4624, dsv=4752,
           ixq=4880, ixk=5392, ixw=5456, gate=5464)


def rope(g, f3, nh, tt, rt):
    kb = g.kb
    fb = f3._buf
    cos = g.ropecs[:, tt:tt + 1, 0:8].broadcast_to([128, nh, 8])
    sin = g.ropecs[:, tt:tt + 1, 8:16].broadcast_to([128, nh, 8])
    x1 = f3.ap[:, :, 0:8]
    x2 = f3.ap[:, :, 8:16]
    t = [rt[:, i, 0:nh, :] for i in range(4)]
    kb.op("dve", lambda e: e.tensor_tensor(out=t[0], in0=x1, in1=cos, op=ALU.mult), R=[fb, g.ropecs], W=[rt])
    kb.op("dve", lambda e: e.tensor_tensor(out=t[1], in0=x2, in1=sin, op=ALU.mult), R=[fb, g.ropecs], W=[rt])
    kb.op("dve", lambda e: e.tensor_tensor(out=t[2], in0=x2, in1=cos, op=ALU.mult), R=[fb, g.ropecs], W=[rt])
    kb.op("dve", lambda e: e.tensor_tensor(out=t[3], in0=x1, in1=sin, op=ALU.mult), R=[fb, g.ropecs], W=[rt])
    kb.op("dve", lambda e: e.tensor_tensor(out=x1, in0=t[0], in1=t[1], op=ALU.subtract), R=[rt], W=[fb])
    kb.op("dve", lambda e: e.tensor_tensor(out=x2, in0=t[2], in1=t[3], op=ALU.add), R=[rt], W=[fb])


class V:
    def __init__(self, ap, buf):
        self.ap = ap
        self._buf = buf


def phase_inproj(g, l):
    kb, nc = g.kb, g.nc
    win = g.inp["w_in"].t[l]
    S = g.scr
    with ExitStack() as ph:
        wt = [kb.sb(ph, "winw", [128, KC, 512], BF16) for _ in range(2)]
        nw = [0]

        def getw(n0, N):
            w = wt[nw[0] % 2]
            nw[0] += 1
            kb.dma("pool", w[:, :, 0:N], win[:, n0:n0 + N].rearrange("(kc p) n -> p kc n", p=128), W=[w])
            return w

        stgb = [kb.sb(ph, "stgb", [128, 512], BF16) for _ in range(3)]
        stgf = [kb.sb(ph, "stgf", [128, 512], F32) for _ in range(2)]
        rt = kb.sb(ph, "ropetmp", [128, 4, 16, 8], F32)
        tst = [kb.sb(ph, "tst", [64, 8, 512], BF16) for _ in range(2)]
        cnt = dict(b=0, f=0, t=0)

        def nxt(lst, k):
            r = lst[cnt[k] % len(lst)]
            cnt[k] += 1
            return r

        def tok_mm(w, N, tt):
            p = kb.ps()
            for kc in range(KC):
                kb.op("pe", lambda e, kc=kc: e.matmul(p[:, 0:N], g.hT[:, kc, tt * 128:(tt + 1) * 128], w[:, kc, 0:N],
                                                      start=(kc == 0), stop=(kc == KC - 1)),
                      R=[g.hTb[kc][tt // 4], w], W=[p])
            return p

        def feat_mm(w, ci, tb):
            p = kb.ps()
            for kc in range(KC):
                kb.op("pe", lambda e, kc=kc: e.matmul(p[:, :], w[:, kc, ci * 128:(ci + 1) * 128],
                                                      g.hT[:, kc, tb * 512:(tb + 1) * 512],
                                                      start=(kc == 0), stop=(kc == KC - 1)),
                      R=[g.hTb[kc][tb], w], W=[p])
            return p

        for zi in range(2):
            w = getw(OFF["z"] + zi * 512, 512)
            for tt in range(NT):
                p = tok_mm(w, 512, tt)
                sb_ = nxt(stgb, "b")
                kb.op("act", lambda e: e.activation(out=sb_[:], in_=p[:, :], func=AF.Silu), R=[p], W=[sb_])
                kb.dma("sp", S["zs"].t[tt * 128:(tt + 1) * 128, zi * 512:(zi + 1) * 512], sb_[:], R=[sb_])
        w = getw(OFF["dt"], 16)
        for tt in range(NT):
            p = tok_mm(w, 16, tt)
            f = nxt(stgf, "f")
            kb.op("dve", lambda e: e.tensor_tensor(out=f[:, 0:16], in0=p[:, 0:16], in1=g.dtb_bc[:], op=ALU.add),
                  R=[p, g.dtb_bc], W=[f])
            kb.op("act", lambda e: e.activation(out=f[:, 0:16], in_=f[:, 0:16], func=AF.Exp), R=[f], W=[f])
            kb.op("act", lambda e: e.activation(out=g.dt_tok[:, tt, :], in_=f[:, 0:16], func=AF.Ln, bias=1.0),
                  R=[f], W=[g.dt_tok])
        w = getw(OFF["sbv"], 512)
        for tt in range(NT):
            p = tok_mm(w, 512, tt)
            sb_ = nxt(stgb, "b")
            kb.op("act", lambda e: e.activation(out=sb_[:], in_=p[:, :], func=AF.Copy), R=[p], W=[sb_])
            kb.dma("sp", S["sbv"].t[tt * 128:(tt + 1) * 128, :], sb_[:], R=[sb_])

        def roped_group(n0, N, nh, dstT, extra=None):
            w = getw(n0, N)
            fs, bs, sts = {}, {}, {}

            def stA(tt):
                p = tok_mm(w, N, tt)
                f = nxt(stgf, "f")
                kb.op("act", lambda e: e.activation(out=f[:, 0:N], in_=p[:, 0:N], func=AF.Copy), R=[p], W=[f])
                fs[tt] = f

            def stB(tt):
                f = fs.pop(tt)
                rope(g, V(f[:, 0:nh * 64].rearrange("p (h d) -> p h d", d=64), f), nh, tt, rt)
                b = nxt(stgb, "b")
                kb.op("act", lambda e: e.activation(out=b[:, 0:N], in_=f[:, 0:N], func=AF.Copy), R=[f], W=[b])
                if extra is not None:
                    extra(tt, f, b)
                bs[tt] = b

            def stC(tt):
                b = bs.pop(tt)
                pt = kb.ps()
                ptb = pt[:, :].bitcast(BF16)
                for h in range(nh):
                    kb.op("pe", lambda e, h=h: e.transpose(ptb[0:64, h * 128:(h + 1) * 128], b[:, h * 64:(h + 1) * 64], g.identb[:]),
                          R=[b, g.identb], W=[pt])
                if tt % 4 == 0:
                    sts[tt // 4] = nxt(tst, "t")
                st = sts[tt // 4]
                kb.op("act", lambda e: e.activation(
                    out=st[0:64, 0:nh, (tt % 4) * 128:(tt % 4 + 1) * 128],
                    in_=ptb[0:64, 0:nh * 128].rearrange("p (h t) -> p h t", h=nh), func=AF.Copy), R=[pt], Wd=[st])
                if tt % 4 == 3:
                    tb = tt // 4
                    kb.dma("sp", dstT.t[:, :, tb * 512:(tb + 1) * 512].rearrange("h d t -> d h t"), st[0:64, 0:nh, :], R=[st])

            for step in range(NT + 2):
                if step < NT:
                    stA(step)
                if 0 <= step - 1 < NT:
                    stB(step - 1)
                if 0 <= step - 2 < NT:
                    stC(step - 2)

        roped_group(OFF["dsq"], 512, 8, S["dqT"])

        def dsv_extra(tt, f, b):
            kb.dma("sp", S["dv"].t[tt * 128:(tt + 1) * 128, :], b[:, 128:256], R=[b])
        roped_group(OFF["dsk"], 256, 2, S["dkT"], dsv_extra)
        roped_group(OFF["ixq"], 512, 8, S["iqT"])

        def ixw_extra(tt, f, b):
            kb.op("dve", lambda e: e.tensor_copy(out=g.wix_tok[:, tt, :], in_=f[:, 64:72]), R=[f], W=[g.wix_tok])
        roped_group(OFF["ixk"], 72, 1, S["ikT"], ixw_extra)

        def feat_group(n0, dstT, c0, func):
            w = getw(n0, 512)
            for ci in range(4):
                for tb in range(4):
                    p = feat_mm(w, ci, tb)
                    sb_ = nxt(stgb, "b")
                    kb.op("act", lambda e: e.activation(out=sb_[:], in_=p[:, :], func=func), R=[p], W=[sb_])
                    kb.dma("sp", dstT.t[c0 + ci, :, tb * 512:(tb + 1) * 512], sb_[:], R=[sb_])

        feat_group(OFF["sbq"], S["sbqT"], 0, AF.Copy)
        feat_group(OFF["sbk"], S["sbkT"], 0, AF.Copy)
        for gi in range(6):
            feat_group(OFF["gate"] + gi * 512, S["gT"], gi * 4, AF.Sigmoid)

        kb.barrier()

    with ExitStack() as ph:
        wt = [kb.sb(ph, "winw", [128, KC, 512], BF16) for _ in range(2)]
        nw = [0]
        xpad = [kb.sb(ph, "xpad", [128, 3 + L], F32) for _ in range(2)]
        cva = [kb.sb(ph, "cva", [128, L], F32) for _ in range(1)]
        cvo = [kb.sb(ph, "cvo", [128, L], BF16) for _ in range(1)]
        for xp in xpad:
            kb.op("dve", lambda e, xp=xp: e.memset(xp[:, 0:3], 0.0), W=[xp])
        ws = {}
        outs = {}

        def X1(cidx):
            gi, ci = cidx // 4, cidx % 4
            if ci == 0:
                ws[gi] = getw(OFF["xbc"] + gi * 512, 512)
            w = ws[gi]
            xp = xpad[cidx % 2]
            for tb in range(4):
                p = feat_mm(w, ci, tb)
                kb.op("act", lambda e, p=p, tb=tb: e.activation(out=xp[:, 3 + tb * 512:3 + (tb + 1) * 512], in_=p[:, :], func=AF.Copy),
                      R=[p], Wd=[xp])

        def X2(cidx):
            xp, acc = xpad[cidx % 2], cva[0]
            kb.op("act", lambda e: e.activation(out=acc[:], in_=xp[:, 3:3 + L], func=AF.Identity,
                                                scale=g.cw[3][:, cidx:cidx + 1], bias=g.cb[:, cidx:cidx + 1]),
                  R=[xp, g.cw[3], g.cb], W=[acc])
            for k in (2, 1, 0):
                kb.op("dve", lambda e, k=k: e.scalar_tensor_tensor(out=acc[:], in0=xp[:, k:k + L], scalar=g.cw[k][:, cidx:cidx + 1],
                                                                   in1=acc[:], op0=ALU.mult, op1=ALU.add),
                      R=[xp, g.cw[k], acc], W=[acc])
            if cidx < 8:
                o = cvo[0]
                ov = o[:, :]
            else:
                o = g.bcT
                ov = g.bcT[:, cidx - 8, :]
            kb.op("act", lambda e: e.activation(out=ov, in_=acc[:], func=AF.Silu), R=[acc], W=[o])
            outs[cidx] = (o, ov)

        def X3(cidx):
            o, ov = outs.pop(cidx)
            if cidx >= 10:
                return
            for half in range(2):
                pt = kb.ps()
                ptb = pt[:, :].bitcast(BF16)
                for j in range(8):
                    tt = half * 8 + j
                    kb.op("pe", lambda e, j=j, tt=tt: e.transpose(ptb[:, j * 128:(j + 1) * 128], ov[:, tt * 128:(tt + 1) * 128],
                                                                  g.identb[:]), R=[o, g.identb], W=[pt])
                if cidx < 8:
                    dst, dt_ = g.xs_tok[:, half * 8:(half + 1) * 8, cidx * 128:(cidx + 1) * 128], g.xs_tok
                else:
                    dst, dt_ = g.bt_tok[:, half * 8:(half + 1) * 8, (cidx - 8) * 128:(cidx - 7) * 128], g.bt_tok
                kb.op("dve", lambda e, dst=dst, ptb=ptb: e.tensor_copy(out=dst, in_=ptb.rearrange("p (j f) -> p j f", j=8)), R=[pt], Wd=[dt_])

        for step in range(12 + 2):
            if step < 12:
                X1(step)
            if 0 <= step - 2 < 12:
                X3(step - 2)
            if 0 <= step - 1 < 12:
                X2(step - 1)
        kb.barrier()


def phase_ssd(g, l):
    kb, nc = g.kb, g.nc
    S = g.scr
    with ExitStack() as ph:
        def sb(name, shape, dt_=F32):
            return kb.sb(ph, name, shape, dt_)
        a_bc = sb("a_bc", [128, 16])
        dsk_bc = sb("dsk_bc", [128, 16])
        nw_bc = sb("nw_bc", [128, 1024])
        Dmat = sb("Dmat", [128, 16, 128], BF16)
        with nc.allow_non_contiguous_dma(reason="tiny vectors"):
            kb.dma("sp", a_bc[:], g.inp["a_log"].t[l].partition_broadcast(128), W=[a_bc])
            kb.dma("sp", dsk_bc[:], g.inp["d_skip"].t[l].partition_broadcast(128), W=[dsk_bc])
            kb.dma("sp", nw_bc[:], g.inp["ssd_norm_w"].t[l].partition_broadcast(128), W=[nw_bc])
        kb.op("act", lambda e: e.activation(out=a_bc[:], in_=a_bc[:], func=AF.Exp), R=[a_bc], W=[a_bc])
        kb.op("dve", lambda e: e.tensor_scalar(out=a_bc[:], in0=a_bc[:], scalar1=-1.0, scalar2=None, op0=ALU.mult), R=[a_bc], W=[a_bc])
        for h in range(16):
            kb.op("dve", lambda e, h=h: e.tensor_scalar(out=Dmat[:, h, :], in0=g.ident[:], scalar1=dsk_bc[:, h:h + 1], scalar2=None,
                                                        op0=ALU.mult), R=[g.ident, dsk_bc], Wd=[Dmat])
        NB = 2
        X = [sb("X", [128, 2, 16, 128], BF16)] * 2
        dd = [sb("dd", [128, 16, 128])] * 2
        Mt = [sb("Mt", [128, 16, 128], BF16) for _ in range(NB)]
        xw = [sb("xw", [128, 1024], BF16) for _ in range(NB)]
        zst = [sb("zst", [128, 1024], BF16) for _ in range(NB)]
        yy = [sb("yy", [128, 1024])] * 2
        yz = yy
        ss = [sb("ss", [128, 1]) for _ in range(NB)]
        epsc = sb("epsc", [128, 1])
        kb.op("dve", lambda e: e.memset(epsc[:], EPS), W=[epsc])
        yn = [sb("yn", [128, 1024], BF16) for _ in range(NB)]
        yst = [sb("yst", [128, KC, 256], BF16)] * 2
        hst = [sb("hst", [128, 512]) for _ in range(2)]
        htmp = [sb("htmp", [128, 512]) for _ in range(2)]
        prevb = [[sb("prevb", [128, 512], BF16) for _ in range(2)] for _ in range(2)]
        ytmp = [sb("ytmp", [128, 512]) for _ in range(2)]

        dA_all = sb("dA_all", [128, 256])
        dAs_all = sb("dAs_all", [128, 2, 256], BF16)
        ac_all = sb("ac_all", [128, 256])
        eac_all = sb("eac_all", [128, 256])
        cd_all = sb("cd_all", [128, 256])
        wgt_all = sb("wgt_all", [128, 256])
        v_all = sb("v_all", [128, 256])
        vs_all = sb("vs_all", [128, 2, 256], BF16)
        acs_all = sb("acs_all", [128, 2, 256], BF16)
        dtf = g.dt_tok[:, :, :].rearrange("p c h -> p (c h)")
        kb.op("dve", lambda e: e.tensor_tensor(out=dA_all[:, :].rearrange("p (c h) -> p c h", h=16), in0=g.dt_tok[:, :, :],
                                               in1=a_bc[:, :].unsqueeze(1).broadcast_to([128, NT, 16]), op=ALU.mult),
              R=[g.dt_tok, a_bc], W=[dA_all])
        kb.op("dve", lambda e: e.tensor_copy(out=dAs_all[:, 0, :], in_=dA_all[:]), R=[dA_all], W=[dAs_all])
        kb.op("dve", lambda e: e.tensor_tensor(out=dAs_all[:, 1, :], in0=dA_all[:], in1=dAs_all[:, 0, :], op=ALU.subtract),
              R=[dA_all, dAs_all], W=[dAs_all])
        pac = kb.ps(pin=True)
        for c in range(NT):
            for half, lhs in ((0, g.triub), (1, g.onesb)):
                for hl_ in range(2):
                    kb.op("pe", lambda e, c=c, half=half, lhs=lhs, hl_=hl_: e.matmul(
                        pac[:, half * 256 + c * 16:half * 256 + (c + 1) * 16], lhs[:], dAs_all[:, hl_, c * 16:(c + 1) * 16],
                        start=(hl_ == 0), stop=(hl_ == 1)), R=[lhs, dAs_all], W=[pac])
        kb.unpin(pac)
        kb.op("dve", lambda e: e.tensor_copy(out=ac_all[:], in_=pac[:, 0:256]), R=[pac], W=[ac_all])
        kb.op("act", lambda e: e.activation(out=eac_all[:], in_=pac[:, 0:256], func=AF.Exp), R=[pac], W=[eac_all])
        kb.op("act", lambda e: e.activation(out=cd_all[:], in_=pac[:, 256:512], func=AF.Exp), R=[pac], W=[cd_all])
        kb.op("dve", lambda e: e.tensor_tensor(out=wgt_all[:], in0=pac[:, 256:512], in1=ac_all[:], op=ALU.subtract), R=[pac, ac_all], W=[wgt_all])
        kb.op("act", lambda e: e.activation(out=wgt_all[:], in_=wgt_all[:], func=AF.Exp), R=[wgt_all], W=[wgt_all])
        kb.op("dve", lambda e: e.tensor_tensor(out=wgt_all[:], in0=wgt_all[:], in1=dtf, op=ALU.mult), R=[wgt_all, g.dt_tok], W=[wgt_all])
        kb.op("act", lambda e: e.activation(out=v_all[:], in_=dtf, func=AF.Ln), R=[g.dt_tok], W=[v_all])
        kb.op("dve", lambda e: e.tensor_tensor(out=v_all[:], in0=v_all[:], in1=ac_all[:], op=ALU.subtract), R=[v_all, ac_all], W=[v_all])
        kb.op("dve", lambda e: e.tensor_copy(out=vs_all[:, 0, :], in_=v_all[:]), R=[v_all], W=[vs_all])
        kb.op("dve", lambda e: e.tensor_tensor(out=vs_all[:, 1, :], in0=v_all[:], in1=vs_all[:, 0, :], op=ALU.subtract), R=[v_all, vs_all], W=[vs_all])
        kb.op("dve", lambda e: e.tensor_copy(out=acs_all[:, 0, :], in_=ac_all[:]), R=[ac_all], W=[acs_all])
        kb.op("dve", lambda e: e.tensor_tensor(out=acs_all[:, 1, :], in0=ac_all[:], in1=acs_all[:, 0, :], op=ALU.subtract), R=[ac_all, acs_all], W=[acs_all])

        pcbs = {}

        def P1(c):
            k = c % NB
            tsl = slice(c * 128, (c + 1) * 128)
            c16 = slice(c * 16, (c + 1) * 16)
            kb.op("dve", lambda e: e.tensor_tensor(
                out=xw[k][:, :].rearrange("p (h d) -> p h d", d=64), in0=g.xs_tok[:, c, :].rearrange("p (h d) -> p h d", d=64),
                in1=wgt_all[:, c16].unsqueeze(2).broadcast_to([128, 16, 64]), op=ALU.mult), R=[g.xs_tok, wgt_all], W=[xw[k]])
            for hl_ in range(2):
                kb.op("dve", lambda e, hl_=hl_: e.tensor_tensor(
                    out=X[k][:, hl_, :, :], in0=g.identb[:, :].unsqueeze(1).broadcast_to([128, 16, 128]),
                    in1=acs_all[:, hl_, c16].unsqueeze(2).broadcast_to([128, 16, 128]), op=ALU.mult), R=[g.identb, acs_all], Wd=[X[k]])
            for q4 in range(4):
                pb = kb.ps()
                h4 = slice(c * 16 + q4 * 4, c * 16 + q4 * 4 + 4)
                for hl_ in range(2):
                    kb.op("pe", lambda e, hl_=hl_: e.matmul(pb[:, :], g.onesb[:], X[k][:, hl_, q4 * 4:(q4 + 1) * 4, :],
                                                            start=(hl_ == 0), stop=False), R=[g.onesb, X[k]], W=[pb])
                for hl_ in range(2):
                    kb.op("pe", lambda e, hl_=hl_: e.matmul(pb[:, :], g.identb[:], vs_all[:, hl_, h4].unsqueeze(2).broadcast_to([128, 4, 128]),
                                                            start=False, stop=False), R=[g.identb, vs_all], W=[pb])
                kb.op("pe", lambda e: e.matmul(pb[:, :], g.identb[:], g.negmb[:, :].unsqueeze(1).broadcast_to([128, 4, 128]),
                                               start=False, stop=True), R=[g.identb, g.negmb], W=[pb])
                kb.op("act", lambda e, q4=q4, pb=pb: e.activation(out=dd[k][:, q4 * 4:(q4 + 1) * 4, :],
                                                                  in_=pb[:, :].rearrange("p (h i) -> p h i", h=4), func=AF.Exp),
                      R=[pb], Wd=[dd[k]])
            pcb = kb.ps(pin=True)
            for gg in range(2):
                kb.op("pe", lambda e, gg=gg: e.matmul(pcb[:, gg * 128:(gg + 1) * 128], g.bcT[:, gg, tsl], g.bcT[:, 2 + gg, tsl],
                                                      start=True, stop=True), R=[g.bcT], W=[pcb])
            pcbs[c] = pcb

        def P1b(c):
            k = c % NB
            pcb = pcbs.pop(c)
            kb.unpin(pcb)
            for gg in range(2):
                kb.op("dve", lambda e, gg=gg: e.tensor_tensor(
                    out=Mt[k][:, gg * 8:(gg + 1) * 8, :], in0=dd[k][:, gg * 8:(gg + 1) * 8, :],
                    in1=pcb[:, gg * 128:(gg + 1) * 128].unsqueeze(1).broadcast_to([128, 8, 128]), op=ALU.mult),
                    R=[dd[k], pcb], Wd=[Mt[k]])

        pend2 = {}

        def P2a(c):
            k = c % NB
            tsl = slice(c * 128, (c + 1) * 128)
            kb.dma("sp", zst[k][:], S["zs"].t[tsl, :], W=[zst[k]])
            rec = []
            for gg in range(2):
                pA = kb.ps()
                for hl in range(8):
                    h = gg * 8 + hl
                    xsl = g.xs_tok[:, c, h * 64:(h + 1) * 64]
                    kb.op("pe", lambda e, h=h, hl=hl, xsl=xsl: e.matmul(pA[:, hl * 64:(hl + 1) * 64], Mt[k][:, h, :], xsl, start=True, stop=False),
                          R=[Mt[k], g.xs_tok], W=[pA])
                    kb.op("pe", lambda e, h=h, hl=hl, xsl=xsl: e.matmul(pA[:, hl * 64:(hl + 1) * 64], Dmat[:, h, :], xsl, start=False, stop=True),
                          R=[Dmat, g.xs_tok], W=[pA])
                pB = pS = None
                if c > 0:
                    pB = kb.ps()
                    pv = prevb[gg][(c - 1) % 2]
                    kb.op("pe", lambda e, pB=pB, pv=pv, gg=gg: e.matmul(pB[:, :], g.bcT[:, 2 + gg, tsl], pv[:], start=True, stop=True),
                          R=[g.bcT, pv], W=[pB])
                if c < NT - 1:
                    pS = kb.ps()
                    kb.op("pe", lambda e, pS=pS, gg=gg: e.matmul(pS[:, :], g.bt_tok[:, c, gg * 128:(gg + 1) * 128], xw[k][:, gg * 512:(gg + 1) * 512],
                                                                start=True, stop=True), R=[g.bt_tok, xw[k]], W=[pS])
                rec.append((pA, pB, pS))
            pend2[c] = rec

        def P2b(c):
            k = c % NB
            rec = pend2.pop(c)
            for gg in range(2):
                pA, pB, pS = rec[gg]
                ysl = yy[k][:, gg * 512:(gg + 1) * 512]
                if c > 0:
                    yt = ytmp[gg]
                    kb.op("dve", lambda e: e.tensor_tensor(
                        out=yt[:, :].rearrange("p (h d) -> p h d", d=64), in0=pB[:, :].rearrange("p (h d) -> p h d", d=64),
                        in1=eac_all[:, c * 16 + gg * 8:c * 16 + (gg + 1) * 8].unsqueeze(2).broadcast_to([128, 8, 64]), op=ALU.mult),
                        R=[pB, eac_all], W=[yt])
                    kb.op("dve", lambda e: e.tensor_tensor(out=ysl, in0=yt[:], in1=pA[:, :], op=ALU.add), R=[yt, pA], W=[yy[k]])
                else:
                    kb.op("act", lambda e: e.activation(out=ysl, in_=pA[:, :], func=AF.Copy), R=[pA], W=[yy[k]])
                if c < NT - 1:
                    if c == 0:
                        kb.op("dve", lambda e: e.tensor_copy(out=hst[gg][:], in_=pS[:, :]), R=[pS], W=[hst[gg]])
                    else:
                        kb.op("dve", lambda e: e.tensor_tensor(
                            out=htmp[gg][:, :].rearrange("p (h d) -> p h d", d=64), in0=hst[gg][:, :].rearrange("p (h d) -> p h d", d=64),
                            in1=cd_all[:, c * 16 + gg * 8:c * 16 + (gg + 1) * 8].unsqueeze(2).broadcast_to([128, 8, 64]), op=ALU.mult),
                            R=[hst[gg], cd_all], W=[htmp[gg]])
                        kb.op("dve", lambda e: e.tensor_tensor(out=hst[gg][:], in0=htmp[gg][:], in1=pS[:, :], op=ALU.add),
                              R=[htmp[gg], pS], W=[hst[gg]])
                    pn = prevb[gg][c % 2]
                    kb.op("act", lambda e: e.activation(out=pn[:], in_=hst[gg][:], func=AF.Copy), R=[hst[gg]], W=[pn])

        def P3(c):
            k = c % NB
            tsl = slice(c * 128, (c + 1) * 128)
            kb.op("dve", lambda e: e.tensor_tensor(out=yz[k][:], in0=yy[k][:], in1=zst[k][:], op=ALU.mult), R=[yy[k], zst[k]], W=[yz[k]])
            kb.op("act", lambda e: e.activation(out=yn[k][:], in_=yz[k][:], func=AF.Square, accum_out=ss[k][:]), R=[yz[k]], W=[yn[k], ss[k]])
            kb.op("act", lambda e: e.activation(out=ss[k][:], in_=ss[k][:], func=AF.Ln, scale=1.0 / 1024, bias=epsc[:, 0:1]), R=[ss[k], epsc], W=[ss[k]])
            kb.op("act", lambda e: e.activation(out=ss[k][:], in_=ss[k][:], func=AF.Exp, scale=-0.5), R=[ss[k]], W=[ss[k]])
            kb.op("dve", lambda e: e.scalar_tensor_tensor(out=yn[k][:], in0=yz[k][:], scalar=ss[k][:, 0:1], in1=nw_bc[:],
                                                          op0=ALU.mult, op1=ALU.mult), R=[yz[k], ss[k], nw_bc], W=[yn[k]])
            pt = kb.ps()
            ptb = pt[:, :].bitcast(BF16)
            for kc in range(KC):
                kb.op("pe", lambda e, kc=kc: e.transpose(ptb[:, kc * 128:(kc + 1) * 128], yn[k][:, kc * 128:(kc + 1) * 128], g.identb[:]),
                      R=[yn[k], g.identb], W=[pt])
            st = yst[(c // 2) % 2]
            kb.op("act", lambda e: e.activation(out=st[:, :, (c % 2) * 128:(c % 2 + 1) * 128],
                                                in_=ptb.rearrange("p (kc t) -> p kc t", kc=KC), func=AF.Copy), R=[pt], Wd=[st])
            if c % 2 == 1:
                t2 = c // 2
                kb.dma("sp", S["yT"].t[0:8, :, t2 * 256:(t2 + 1) * 256].rearrange("kc p t -> p kc t"), st[:], R=[st])

        for step in range(-1, NT + 1):
            if 0 <= step + 1 < NT:
                P1(step + 1)
            if 0 <= step < NT:
                P2a(step)
            if 0 <= step - 1 < NT:
                P3(step - 1)
            if 0 <= step + 1 < NT:
                P1b(step + 1)
            if 0 <= step < NT:
                P2b(step)
        kb.barrier()


def phase_sb(g, l):
    kb, nc = g.kb, g.nc
    S = g.scr
    with ExitStack() as ph:
        def sb(name, shape, dt_=F32):
            return kb.sb(ph, name, shape, dt_)
        qT = sb("sbq", [128, 4, L], BF16)
        kT = sb("sbk", [128, 4, L], BF16)
        v = sb("sbv", [128, NT, 512], BF16)
        onesw = sb("onesw", [128, L], BF16)
        kb.op("dve", lambda e: e.memset(onesw[:], 1.0), W=[onesw])
        kb.dma("sp", qT[:], S["sbqT"].t.rearrange("c p t -> p c t"), W=[qT])
        kb.dma("sp", kT[:], S["sbkT"].t.rearrange("c p t -> p c t"), W=[kT])
        kb.dma("sp", v[:], S["sbv"].t.rearrange("(t p) f -> p t f", p=128), W=[v])
        e1b = [sb("e1", [128, L]) for _ in range(3)]
        spb = [sb("sp", [128, L]) for _ in range(2)]
        Fb = [sb("F", [128, L + 1]) for _ in range(2)]
        attb = [sb("att", [128, L], BF16) for _ in range(2)]
        attT = [sb("attT", [128, NT, 128], BF16) for _ in range(2)]
        ftn = [sb("ftn", [128, 1]) for _ in range(2)]
        yst = [sb("yst", [128, 128], BF16) for _ in range(4)]
        for F in Fb:
            kb.op("dve", lambda e, F=F: e.memset(F[:, 0:1], 0.0), W=[F])
        iters = [(qt, h) for qt in range(NT) for h in range(8)]

        def s1(i):
            qt, h = iters[i]
            W = 128 * (qt + 1)
            nch = (W + 511) // 512
            dsl = slice(qt * 128, (qt + 1) * 128)
            hp, hc = (h % 2) * 64, h // 2
            e1, sp_ = e1b[i % 3], spb[i % 2]
            for ch in range(nch):
                n = min(512, W - ch * 512)
                p = kb.ps()
                kb.op("pe", lambda e, p=p, ch=ch, n=n: e.matmul(p[:, 0:n], qT[hp:hp + 64, hc, dsl], kT[hp:hp + 64, hc, ch * 512:ch * 512 + n],
                                                                start=True, stop=True), R=[qT, kT], W=[p])
                kb.op("act", lambda e, p=p, ch=ch, n=n: e.activation(out=e1[:, ch * 512:ch * 512 + n], in_=p[:, 0:n], func=AF.Exp, scale=0.125),
                      R=[p], Wd=[e1])
            kb.op("act", lambda e: e.activation(out=sp_[:, 0:W], in_=e1[:, 0:W], func=AF.Ln, bias=1.0), R=[e1], W=[sp_])

        def s2(i):
            qt, h = iters[i]
            k = i % 2
            W = 128 * (qt + 1)
            dsl = slice(qt * 128, (qt + 1) * 128)
            sp_, F = spb[k], Fb[k]
            kb.op("dve", lambda e: e.tensor_tensor(out=sp_[:, dsl], in0=sp_[:, dsl], in1=g.strl[:], op=ALU.mult), R=[sp_, g.strl], W=[sp_])
            kb.op("dve", lambda e: e.memset(F[:, 0:1], 0.0), W=[F])
            kb.op("dve", lambda e: e.tensor_tensor_scan(out=F[:, 1:W + 1], data0=onesw[:, 0:W], data1=sp_[:, 0:W], initial=0.0,
                                                        op0=ALU.mult, op1=ALU.add), R=[onesw, sp_], W=[F])
            kb.op("dve", lambda e: e.tensor_scalar(out=ftn[k][:], in0=F[:, W:W + 1], scalar1=-1.0, scalar2=None, op0=ALU.mult),
                  R=[F], W=[ftn[k]])

        def s3a(i):
            qt, h = iters[i]
            k = i % 2
            W = 128 * (qt + 1)
            F = Fb[k]
            kb.op("act", lambda e: e.activation(out=F[:, 0:W], in_=F[:, 0:W], func=AF.Exp, bias=ftn[k][:, 0:1]), R=[F, ftn[k]], W=[F])

        def s3b(i):
            qt, h = iters[i]
            k = i % 2
            W = 128 * (qt + 1)
            dsl = slice(qt * 128, (qt + 1) * 128)
            e1, F, att = e1b[i % 3], Fb[k], attb[k]
            kb.op("dve", lambda e: e.tensor_tensor(out=att[:, 0:W], in0=e1[:, 0:W], in1=F[:, 0:W], op=ALU.mult), R=[e1, F], W=[att])
            kb.op("dve", lambda e: e.tensor_tensor(out=att[:, dsl], in0=att[:, dsl], in1=g.strlb[:], op=ALU.mult), R=[att, g.strlb], W=[att])

        def s3c(i):
            qt, h = iters[i]
            k = i % 2
            att, aT = attb[k], attT[k]
            for b0 in range(0, qt + 1, 8):
                nb = min(8, qt + 1 - b0)
                pt = kb.ps()
                ptb = pt[:, :].bitcast(BF16)
                for j in range(nb):
                    kb.op("pe", lambda e, j=j, b0=b0: e.transpose(ptb[:, j * 128:(j + 1) * 128], att[:, (b0 + j) * 128:(b0 + j + 1) * 128], g.identb[:]),
                          R=[att, g.identb], W=[pt])
                kb.op("act", lambda e, b0=b0, nb=nb: e.activation(out=aT[:, b0:b0 + nb, :], in_=ptb[:, 0:nb * 128].rearrange("p (j t) -> p j t", j=nb),
                                                                 func=AF.Copy), R=[pt], Wd=[aT])

        def s3d(i):
            qt, h = iters[i]
            k = i % 2
            dsl = slice(qt * 128, (qt + 1) * 128)
            hp, hc = (h % 2) * 64, h // 2
            aT = attT[k]
            py = kb.ps()
            for sbk in range(qt + 1):
                kb.op("pe", lambda e, sbk=sbk: e.matmul(py[0:64, 0:128], v[:, sbk, h * 64:(h + 1) * 64], aT[:, sbk, :],
                                                        start=(sbk == 0), stop=(sbk == qt)), R=[v, aT], W=[py])
            pys[i] = py

        def s3e(i):
            qt, h = iters[i]
            dsl = slice(qt * 128, (qt + 1) * 128)
            hp, hc = (h % 2) * 64, h // 2
            py = pys.pop(i)
            ys = yst[i % 4]
            kb.op("dve", lambda e: e.tensor_copy(out=ys[hp:hp + 64, :], in_=py[0:64, 0:128]), R=[py], W=[ys])
            kb.dma("sp", S["yT"].t[8 + hc, hp:hp + 64, dsl], ys[hp:hp + 64, :], R=[ys])

        n_it = len(iters)

        def run(fn, i):
            if 0 <= i < n_it:
                fn(i)
        pys = {}
        for step in range(n_it + 4):
            run(s3d, step - 4)
            run(s3a, step - 2)
            run(s1, step)
            run(s3c, step - 3)
            run(s2, step - 1)
            run(s3e, step - 4)
            run(s3b, step - 2)
        kb.barrier()


NBIS = 16


def phase_dsa(g, l):
    kb, nc = g.kb, g.nc
    S = g.scr
    with ExitStack() as ph:
        def sb(name, shape, dt_=F32):
            return kb.sb(ph, name, shape, dt_)
        dk = sb("dk", [128, L], BF16)
        ik = sb("ik", [128, L], BF16)
        vaug = sb("vaug", [128, NT, 2, 128], BF16)
        kb.dma("sp", dk[:], S["dkT"].t.rearrange("h d t -> (h d) t"), W=[dk])
        kb.dma("sp", ik[0:64, :], S["ikT"].t[0], W=[ik])
        kb.dma("sp", ik[64:128, :], S["ikT"].t[0], W=[ik])
        kb.op("dve", lambda e: e.memset(vaug[:], 1.0), W=[vaug])
        for gg in range(2):
            kb.dma("sp", vaug[:, :, gg, 0:64], S["dv"].t[:, gg * 64:(gg + 1) * 64].rearrange("(t p) d -> p t d", p=128), W=[vaug])
        dqt = [sb("dqt", [128, 4, 128], BF16) for _ in range(2)]
        iqt = [sb("iqt", [128, 4, 128], BF16) for _ in range(2)]
        score = [sb("score", [128, L]) for _ in range(2)]
        rl = [sb("rl", [128, 512], BF16) for _ in range(4)]
        wabs = [sb("wabs", [128, 8]) for _ in range(2)]
        wsgn = [sb("wsgn", [128, 8]) for _ in range(2)]
        Dg = [sb("Dg", [128, 8, 128], BF16) for _ in range(2)]
        junk = sb("junk", [128, L], BF16)
        maskb = [sb("maskb", [128, L], BF16) for _ in range(2)]
        maskT = [sb("maskT", [128, NT, 128], BF16) for _ in range(2)]
        Eb = [sb("E", [128, 512], BF16) for _ in range(4)]
        M_ = [sb("M", [128, 1]) for _ in range(2)]
        A_ = [sb("A", [128, NBIS + 1]) for _ in range(2)]
        mid = [sb("mid", [128, 1]) for _ in range(2)]
        cnt = [sb("cnt", [128, 1]) for _ in range(2)]
        sela = [sb("sela", [128, 1]) for _ in range(2)]
        R0 = [sb("R0", [64, 512]) for _ in range(2)]
        ytmp = [sb("ytmp", [64, 512], BF16) for _ in range(2)]
        yraw = [sb("yraw", [64, 512]) for _ in range(2)]
        yst = [sb("dyst", [128, 4, 128], BF16) for _ in range(2)]
        cn = dict(rl=0, E=0)

        def stage_I(qt):
            k = qt % 2
            W = 128 * (qt + 1)
            dsl = slice(qt * 128, (qt + 1) * 128)
            if qt < 2:
                return
            iq_src = S["iqT"].t.rearrange("(hp e) d t -> e d hp t", e=2)
            for e_ in range(2):
                kb.dma("sp", iqt[k][e_ * 64:(e_ + 1) * 64, :, :], iq_src[e_][:, :, dsl], W=[iqt[k]])
            sc = score[k]
            nch = (W + 511) // 512
            wv = g.wix_tok[:, qt, :]
            kb.op("act", lambda e: e.activation(out=wabs[k][:], in_=wv, func=AF.Abs), R=[g.wix_tok], W=[wabs[k]])
            kb.op("dve", lambda e: e.tensor_scalar(out=wsgn[k][:], in0=wv, scalar1=0.0, scalar2=2.0, op0=ALU.is_ge, op1=ALU.mult),
                  R=[g.wix_tok], W=[wsgn[k]])
            kb.op("dve", lambda e: e.tensor_scalar(out=wsgn[k][:], in0=wsgn[k][:], scalar1=-1.0, scalar2=None, op0=ALU.add),
                  R=[wsgn[k]], W=[wsgn[k]])
            for h in range(8):
                kb.op("dve", lambda e, h=h: e.tensor_scalar(out=Dg[k][:, h, :], in0=g.identb[:], scalar1=wsgn[k][:, h:h + 1], scalar2=None,
                                                            op0=ALU.mult), R=[g.identb, wsgn[k]], Wd=[Dg[k]])
            items = [(ch, h) for ch in range(nch) for h in range(8)]
            pend = {}
            pscs = {}

            def mm(j):
                ch, h = items[j]
                n = min(512, W - ch * 512)
                p = kb.ps()
                hp_ = (h % 2) * 64
                kb.op("pe", lambda e: e.matmul(p[:, 0:n], iqt[k][hp_:hp_ + 64, h // 2, :], ik[hp_:hp_ + 64, ch * 512:ch * 512 + n],
                                               start=True, stop=True), R=[iqt[k], ik], W=[p])
                r = rl[cn["rl"] % 4]
                cn["rl"] += 1
                kb.op("act", lambda e: e.activation(out=r[:, 0:n], in_=p[:, 0:n], func=AF.Relu, scale=wabs[k][:, h:h + 1]),
                      R=[p, wabs[k]], W=[r])
                pend[j] = r

            def acc(j):
                ch, h = items[j]
                n = min(512, W - ch * 512)
                r = pend.pop(j)
                if h == 0:
                    pscs[ch] = kb.ps(pin=True)
                psc = pscs[ch]
                kb.op("pe", lambda e: e.matmul(psc[:, 0:n], Dg[k][:, h, :], r[:, 0:n], start=(h == 0), stop=(h == 7)),
                      R=[Dg[k], r], W=[psc])
                if h == 7:
                    kb.unpin(psc)
                    kb.op("act", lambda e: e.activation(out=sc[:, ch * 512:ch * 512 + n], in_=psc[:, 0:n], func=AF.Copy), R=[psc], Wd=[sc])

            for j in range(0, len(items) + 2, 2):
                for jj in (j, j + 1):
                    if jj < len(items):
                        mm(jj)
                for jj in (j - 2, j - 1):
                    if 0 <= jj < len(items):
                        acc(jj)

        def stage_B(qt):
            k = qt % 2
            W = 128 * (qt + 1)
            dsl = slice(qt * 128, (qt + 1) * 128)
            mT = maskT[k]
            if qt < 2:
                if qt == 1:
                    kb.op("dve", lambda e: e.memset(mT[:, 0, :], 0.0), W=[mT])
                kb.op("dve", lambda e: e.tensor_copy(out=mT[:, qt, :], in_=g.negmb[:]), R=[g.negmb], W=[mT])
                return
            sc = score[k]
            kb.op("dve", lambda e: e.tensor_reduce(out=M_[k][:], in_=sc[:, 0:W], axis=AX.X, op=ALU.max, apply_absolute_value=True),
                  R=[sc], W=[M_[k]])
            kb.op("dve", lambda e: e.tensor_scalar(out=A_[k][:, 0:NBIS], in0=g.pow2[:, 0:NBIS], scalar1=M_[k][:, 0:1], scalar2=None, op0=ALU.mult),
                  R=[g.pow2, M_[k]], W=[A_[k]])
            kb.op("dve", lambda e: e.tensor_copy(out=A_[k][:, NBIS:NBIS + 1], in_=A_[k][:, NBIS - 1:NBIS]), R=[A_[k]], W=[A_[k]])
            kb.op("dve", lambda e: e.tensor_tensor(out=sc[:, dsl], in0=sc[:, dsl], in1=g.negup[:], op=ALU.add), R=[sc, g.negup], W=[sc])
            kb.op("dve", lambda e: e.memset(mid[k][:], 0.0), W=[mid[k]])
            for it in range(NBIS):
                kb.op("dve", lambda e: e.tensor_scalar(out=junk[:, 0:W], in0=sc[:, 0:W], scalar1=mid[k][:, 0:1], scalar2=0.0,
                                                       op0=ALU.is_ge, op1=ALU.add, accum_out=cnt[k][:]),
                      R=[sc, mid[k]], W=[junk, cnt[k]])
                kb.op("dve", lambda e, it=it: e.tensor_scalar(out=sela[k][:], in0=cnt[k][:], scalar1=255.5, scalar2=A_[k][:, it:it + 1],
                                                              op0=ALU.is_ge, op1=ALU.mult), R=[cnt[k], A_[k]], W=[sela[k]])
                kb.op("dve", lambda e, it=it: e.scalar_tensor_tensor(out=mid[k][:], in0=sela[k][:], scalar=A_[k][:, it + 1:it + 2], in1=mid[k][:],
                                                                     op0=ALU.subtract, op1=ALU.add), R=[sela[k], A_[k], mid[k]], W=[mid[k]])
            mb = maskb[k]
            kb.op("dve", lambda e: e.tensor_scalar(out=mb[:, 0:W], in0=sc[:, 0:W], scalar1=mid[k][:, 0:1], scalar2=-30000.0,
                                                   op0=ALU.is_lt, op1=ALU.mult), R=[sc, mid[k]], W=[mb])

        def stage_T(qt):
            if qt < 2:
                return
            k = qt % 2
            mb, mT = maskb[k], maskT[k]
            for b0 in range(0, qt + 1, 8):
                nb = min(8, qt + 1 - b0)
                pt = kb.ps()
                ptb = pt[:, :].bitcast(BF16)
                for j in range(nb):
                    kb.op("pe", lambda e, j=j, b0=b0: e.transpose(ptb[:, j * 128:(j + 1) * 128], mb[:, (b0 + j) * 128:(b0 + j + 1) * 128], g.identb[:]),
                          R=[mb, g.identb], W=[pt])
                kb.op("act", lambda e, b0=b0, nb=nb: e.activation(out=mT[:, b0:b0 + nb, :], in_=ptb[:, 0:nb * 128].rearrange("p (j t) -> p j t", j=nb),
                                                                 func=AF.Copy), R=[pt], Wd=[mT])

        def load_dq(buf, qt_):
            src = S["dqT"].t.rearrange("(g hh) d t -> g d hh t", g=2)
            for g_ in range(2):
                kb.dma("sp", buf[g_ * 64:(g_ + 1) * 64, :, :], src[g_][:, :, qt_ * 128:(qt_ + 1) * 128], W=[buf])

        def stage_A(qt):
            k = qt % 2
            dsl = slice(qt * 128, (qt + 1) * 128)
            mT = maskT[k]
            ys = yst[k]
            if qt + 1 < NT:
                load_dq(dqt[1 - k], qt + 1)
            pOs = [kb.ps(pin=True) for _ in range(2)]
            Es = {}

            def qk(sbk):
                pSs = [kb.ps() for _ in range(2)]
                for gq in range(2):
                    kb.op("pe", lambda e, gq=gq: e.matmul(pSs[gq][:, :], dk[gq * 64:(gq + 1) * 64, sbk * 128:(sbk + 1) * 128],
                                                          dqt[k][gq * 64:(gq + 1) * 64, :, :], start=True, stop=False),
                          R=[dk, dqt[k]], W=[pSs[gq]])
                for gq in range(2):
                    kb.op("pe", lambda e, gq=gq: e.matmul(pSs[gq][:, :], g.identb[:], mT[:, sbk, :].unsqueeze(1).broadcast_to([128, 4, 128]),
                                                          start=False, stop=True), R=[g.identb, mT], W=[pSs[gq]])
                for gq in range(2):
                    E = Eb[cn["E"] % 4]
                    cn["E"] += 1
                    kb.op("act", lambda e, gq=gq, E=E: e.activation(out=E[:], in_=pSs[gq][:, :], func=AF.Exp, scale=0.125), R=[pSs[gq]], W=[E])
                    Es[(sbk, gq)] = E

            def av(sbk):
                for gq in range(2):
                    E = Es.pop((sbk, gq))
                    kb.op("pe", lambda e, gq=gq, E=E: e.matmul(pOs[gq][:, :], vaug[:, sbk, gq, :], E[:], start=(sbk == 0), stop=(sbk == qt)),
                          R=[vaug, E], W=[pOs[gq]])

            for sbk in range(qt + 2):
                if sbk <= qt:
                    qk(sbk)
                if sbk >= 1:
                    av(sbk - 1)
            for gq in range(2):
                def norm(pO=pOs[gq], gq=gq, ys=ys, dsl=dsl, last=(gq == 1)):
                    kb.unpin(pO)
                    r0, yt = R0[gq], ytmp[gq]
                    kb.op("act", lambda e: e.activation(out=r0[:], in_=pO[64:128, :], func=AF.Ln), R=[pO], W=[r0])
                    kb.op("act", lambda e: e.activation(out=r0[:], in_=r0[:], func=AF.Exp, scale=-1.0), R=[r0], W=[r0])
                    yr = yraw[gq]
                    kb.op("act", lambda e: e.activation(out=yr[:], in_=pO[0:64, :], func=AF.Copy), R=[pO], W=[yr])
                    kb.op("pool", lambda e: e.tensor_tensor(out=yt[:], in0=yr[:], in1=r0[:], op=ALU.mult), R=[yr, r0], W=[yt])
                    for hh in range(4):
                        h = 4 * gq + hh
                        hp, hc = (h % 2) * 64, h // 2
                        kb.op("act", lambda e, hh=hh, hp=hp, hc=hc: e.activation(out=ys[hp:hp + 64, hc, :], in_=yt[:, hh * 128:(hh + 1) * 128], func=AF.Copy),
                              R=[yt], Wd=[ys])
                    if last:
                        kb.dma("sp", S["yT"].t[12:16, :, dsl].rearrange("c p t -> p c t"), ys[:], R=[ys])
                pending.append(norm)

        pending = []
        load_dq(dqt[0], 0)
        stage_I(0)
        stage_B(0)
        stage_T(0)
        stage_I(1)
        modw = [sb("adaw", [128, KC, 512], BF16) for _ in range(2)] if g.dsa_hook is not None else None
        for qt in range(NT):
            if qt + 2 < NT:
                stage_I(qt + 2)
            prev_norms = pending[:]
            del pending[:]
            stage_A(qt)
            for nf in prev_norms:
                nf()
            if modw is not None and 1 <= qt < 13:
                mod_piece_load(g, g.dsa_hook, qt - 1, modw[(qt - 1) % 2])
            if modw is not None and 2 <= qt < 14:
                mod_piece_mm(g, g.dsa_hook, qt - 2, modw[(qt - 2) % 2])
            if qt + 1 < NT:
                stage_B(qt + 1)
                stage_T(qt + 1)
        while pending:
            pending.pop(0)()
        kb.barrier()


def phase_merge(g, l):
    kb, nc = g.kb, g.nc
    S = g.scr
    with ExitStack() as ph:
        def sb(name, shape, dt_=F32):
            return kb.sb(ph, name, shape, dt_)
        yT = sb("yTall", [128, 16, L], BF16)
        yTb = [kb.buf() for _ in range(4)]
        for q in range(4):
            kb.dma("sp", yT[:, q * 4:(q + 1) * 4, :], S["yT"].t[q * 4:(q + 1) * 4].rearrange("c p t -> p c t"), W=[yTb[q]])
        mT = sb("mergedT", [128, KC, L], BF16)
        mTb = [[kb.buf() for tb in range(4)] for c in range(KC)]
        wbr = [sb("wbr", [128, 16, 256], BF16) for _ in range(2)]
        gt = [sb("gt", [128, 3, 512], BF16) for _ in range(2)]
        gbr = [sb("gbr", [128, 3, 512], BF16) for _ in range(2)]
        it = 0
        for c2 in range(4):
            w = wbr[c2 % 2]
            csl = slice(c2 * 256, (c2 + 1) * 256)
            kb.dma("pool", w[:, 0:8, :], g.inp["w_br_ssd"].t[l][:, csl].rearrange("(kc p) n -> p kc n", p=128), W=[w])
            kb.dma("pool", w[:, 8:12, :], g.inp["w_br_sb"].t[l][:, csl].rearrange("(kc p) n -> p kc n", p=128), W=[w])
            kb.dma("pool", w[:, 12:16, :], g.inp["w_br_dsa"].t[l][:, csl].rearrange("(kc p) n -> p kc n", p=128), W=[w])
            for cl in range(2):
                c = c2 * 2 + cl
                for tb in range(4):
                    k = it % 2
                    it += 1
                    tsl = slice(tb * 512, (tb + 1) * 512)
                    for br in range(3):
                        kb.dma("sp", gt[k][:, br, :], S["gT"].t[br * 8 + c, :, tsl], W=[gt[k]])
                    pbr = []
                    for br, (k0, k1) in enumerate(((0, 8), (8, 12), (12, 16))):
                        p = kb.ps()
                        for kc in range(k0, k1):
                            kb.op("pe", lambda e, p=p, kc=kc, k0=k0, k1=k1: e.matmul(
                                p[:, :], w[:, kc, cl * 128:(cl + 1) * 128], yT[:, kc, tsl], start=(kc == k0), stop=(kc == k1 - 1)),
                                R=[w, yTb[kc // 4]], W=[p])
                        pbr.append(p)
                    gb = gbr[k]
                    for br in range(3):
                        kb.op("dve", lambda e, br=br: e.tensor_tensor(out=gb[:, br, :], in0=pbr[br][:, :], in1=gt[k][:, br, :], op=ALU.mult),
                              R=[pbr[br], gt[k]], Wd=[gb])
                    pm = kb.ps()
                    for br in range(3):
                        kb.op("pe", lambda e, br=br: e.matmul(pm[:, :], g.identb[:], gb[:, br, :], start=(br == 0), stop=(br == 2)),
                              R=[g.identb, gb], W=[pm])
                    kb.op("act", lambda e: e.activation(out=mT[:, c, tsl], in_=pm[:, :], func=AF.Copy), R=[pm], W=[mTb[c][tb]])
        wo = [sb("wo", [128, KC, 256], BF16) for _ in range(2)]
        for c2 in range(4):
            w = wo[c2 % 2]
            kb.dma("pool", w[:], g.inp["w_out"].t[l][:, c2 * 256:(c2 + 1) * 256].rearrange("(kc p) n -> p kc n", p=128), W=[w])
            for cl in range(2):
                c = c2 * 2 + cl
                for tb in range(4):
                    tsl = slice(tb * 512, (tb + 1) * 512)
                    p = kb.ps()
                    for kc in range(KC):
                        kb.op("pe", lambda e, p=p, kc=kc: e.matmul(p[:, :], w[:, kc, cl * 128:(cl + 1) * 128], mT[:, kc, tsl],
                                                                   start=(kc == 0), stop=(kc == KC - 1)), R=[w, mTb[kc][tb]], W=[p])
                    xs = g.xT[:, c, tsl]
                    kb.op("dve", lambda e, p=p, c=c, xs=xs: e.scalar_tensor_tensor(
                        out=xs, in0=p[:, :], scalar=g.modT[:, 16 + c:17 + c], in1=xs, op0=ALU.mult, op1=ALU.add),
                        R=[p, g.modT, g.xTb[c][tb]], W=[g.xTb[c][tb]])
        kb.barrier()


def phase_mlp(g, l):
    kb = g.kb
    with ExitStack() as ph:
        wup = [kb.sb(ph, f"wup{i}", [128, KC, 512], BF16) for i in range(2)]
        wdn = [kb.sb(ph, f"wdn{i}", [128, 4, D], BF16) for i in range(2)]
        act = [kb.sb(ph, f"mact{i}", [128, 4, L], BF16) for i in range(2)]
        actb = [[[kb.buf() for tb in range(4)] for hc in range(4)] for i in range(2)]
        rl = [kb.sb(ph, f"mrl{i}", [128, 512], F32) for i in range(2)]
        nrl = 0
        for gi in range(8):
            wu, wd, a, ab = wup[gi % 2], wdn[gi % 2], act[gi % 2], actb[gi % 2]
            load_w(g, wu, g.inp["w_up"].t[l][:, gi * 512:(gi + 1) * 512])
            load_w(g, wd, g.inp["w_down"].t[l][gi * 512:(gi + 1) * 512, :])
            for tb in range(4):
                for hc in range(4):
                    p = kb.ps()
                    for kc in range(KC):
                        kb.op("pe", lambda e, p=p, kc=kc, hc=hc: e.matmul(
                            p[:, :], wu[:, kc, hc * 128:(hc + 1) * 128], g.hT[:, kc, tb * 512:(tb + 1) * 512],
                            start=(kc == 0), stop=(kc == KC - 1)), R=[wu, g.hTb[kc][tb]], W=[p])
                    r = rl[nrl % 2]
                    nrl += 1
                    kb.op("act", lambda e, p=p, r=r: e.activation(out=r[:], in_=p[:, :], func=AF.Relu), R=[p], W=[r])
                    kb.op("act", lambda e, r=r, hc=hc: e.activation(out=a[:, hc, tb * 512:(tb + 1) * 512], in_=r[:], func=AF.Square),
                          R=[r], W=[ab[hc][tb]])
            import os
            if os.environ.get("MLP_UP_ONLY"):
                continue
            for tb in range(4):
                for c in range(KC):
                    p = kb.ps()
                    for hc in range(4):
                        kb.op("pe", lambda e, p=p, hc=hc, c=c: e.matmul(
                            p[:, :], wd[:, hc, c * 128:(c + 1) * 128], a[:, hc, tb * 512:(tb + 1) * 512],
                            start=(hc == 0), stop=(hc == 3)), R=[wd, ab[hc][tb]], W=[p])
                    xs = g.xT[:, c, tb * 512:(tb + 1) * 512]
                    kb.op("dve", lambda e, p=p, c=c, xs=xs: e.scalar_tensor_tensor(
                        out=xs, in0=p[:, :], scalar=g.modT[:, 40 + c:41 + c], in1=xs, op0=ALU.mult, op1=ALU.add),
                        R=[p, g.modT, g.xTb[c][tb]], W=[g.xTb[c][tb]])
        kb.barrier()


def rstd_bcast(kb, ph, xT, xTb, tb, onesb, sq, rs):
    p = kb.ps()
    for c in range(KC):
        s = sq[c % 2]
        kb.op("act", lambda e, s=s, c=c: e.activation(out=s[:], in_=xT[:, c, tb * 512:(tb + 1) * 512], func=AF.Square),
              R=[xTb[c][tb]], W=[s])
        kb.op("pe", lambda e, s=s, c=c, p=p: e.matmul(p[:, :], onesb[:], s[:], start=(c == 0), stop=(c == KC - 1)),
              R=[s, onesb], W=[p])
    kb.op("dve", lambda e: e.tensor_scalar(out=rs[:], in0=p[:, :], scalar1=1.0 / D, scalar2=EPS, op0=ALU.mult, op1=ALU.add),
          R=[p], W=[rs])
    kb.op("act", lambda e: e.activation(out=rs[:], in_=rs[:], func=AF.Ln), R=[rs], W=[rs])
    kb.op("act", lambda e: e.activation(out=rs[:], in_=rs[:], func=AF.Exp, scale=-0.5), R=[rs], W=[rs])


def final_norm(kb, nc, xT, xTb, fnw, onesb, ident, out_d):
    with ExitStack() as ph:
        sq = [kb.sb(ph, f"fsq{i}", [128, 512], BF16) for i in range(2)]
        rs = kb.sb(ph, "frs", [128, 512], F32)
        yT = [kb.sb(ph, f"fyT{i}", [128, 512], F32) for i in range(2)]
        ost = [kb.sb(ph, f"fost{i}", [128, 4, D], F32) for i in range(2)]
        for tb in range(4):
            rstd_bcast(kb, ph, xT, xTb, tb, onesb, sq, rs)
            o = ost[tb % 2]
            for c in range(KC):
                y = yT[c % 2]
                kb.op("dve", lambda e, y=y, c=c: e.scalar_tensor_tensor(
                    out=y[:], in0=xT[:, c, tb * 512:(tb + 1) * 512], scalar=fnw[:, c:c + 1], in1=rs[:],
                    op0=ALU.mult, op1=ALU.mult), R=[xTb[c][tb], fnw, rs], W=[y])
                p = kb.ps()
                for j in range(4):
                    kb.op("pe", lambda e, y=y, j=j, p=p: e.transpose(
                        p[:, j * 128:(j + 1) * 128], y[:, j * 128:(j + 1) * 128], ident[:]), R=[y, ident], W=[p])
                src = p[:, :].rearrange("p (j f) -> p j f", j=4)
                dst = o[:, :, c * 128:(c + 1) * 128]
                kb.op("act", lambda e, dst=dst, src=src: e.activation(out=dst, in_=src, func=AF.Copy), R=[p], Wd=[o])
            kb.dma("sp", out_d.t[tb * 512:(tb + 1) * 512, :].rearrange("(j p) d -> p j d", p=128), o[:], R=[o], W=[out_d])
        kb.barrier()


_NC_CACHE = {}


def make_in_maps(inputs):
    nb = inputs["x"].shape[0]
    inputs = dict(inputs)
    inputs.update(host_consts())
    shared = {n: np.ascontiguousarray(np.asarray(inputs[n], dtype=np.float32)) for n in IN_SHAPES if n not in ("x", "c")}
    in_maps = []
    for b in range(nb):
        m = dict(shared)
        m["x"] = np.ascontiguousarray(inputs["x"][b])
        m["c"] = np.ascontiguousarray(inputs["c"][b])
        in_maps.append(m)
    return in_maps


def kernel(**inputs):
    nb = inputs["x"].shape[0]
    if "nc" not in _NC_CACHE:
        _NC_CACHE["nc"] = build()[0]
    nc = _NC_CACHE["nc"]
    res = run_bass_kernel_spmd(nc, make_in_maps(inputs), core_ids=list(range(nb)))
    return np.stack([r["out"] for r in res.results], axis=0)
```

```python
import math
from contextlib import ExitStack

import numpy as np
import concourse.bass as bass
import concourse.mybir as mybir
from concourse.bass_utils import run_bass_kernel_spmd

F32 = mybir.dt.float32
BF16 = mybir.dt.bfloat16
I32 = mybir.dt.int32
AF = mybir.ActivationFunctionType
ALU = mybir.AluOpType
AX = mybir.AxisListType

D = 1024
L = 2048
DEPTH = 2
NT = L // 128
KC = D // 128
EPS = 1e-6
import os
NDS = int(os.environ.get("NDS", "12"))
STRICT = bool(int(os.environ.get("KSTRICT", "1")))


class Buf:
    __slots__ = ("name", "w", "r", "excl")

    def __init__(self, name):
        self.name = name
        self.excl = False
        self.w = None
        self.r = {}


class T:
    def __init__(self, t, buf):
        self.t = t
        self.b = buf

    def __getitem__(self, k):
        return self.t[k]


class KB:
    def __init__(self, nc):
        self.nc = nc
        self.es = ExitStack()
        self.sems = {}
        self.engs = {}
        for name, e in (("pe", nc.tensor), ("act", nc.scalar), ("dve", nc.vector),
                        ("pool", nc.gpsimd), ("sp", nc.sync)):
            key = "s_" + name
            self.sems[key] = self.es.enter_context(nc.semaphore(key))
            self.engs[name] = dict(e=e, key=key, cnt=0, seen={})
        self.dq = {}
        for q in ("sp", "pool"):
            keys = []
            for i in range(NDS):
                key = f"d_{q}{i}"
                self.sems[key] = self.es.enter_context(nc.semaphore(key))
                keys.append(key)
            self.dq[q] = dict(keys=keys, cnt=[0] * NDS, nxt=0)
        self.nbuf = 0
        self.psum = []
        self.ps_next = 0
        self.pinned = set()

    def buf(self, name=None):
        self.nbuf += 1
        return Buf(name or f"b{self.nbuf}")

    def sb(self, es, name, shape, dtype):
        self.nbuf += 1
        name = f"{name}_{self.nbuf}"
        t = es.enter_context(self.nc.sbuf_tensor(name, list(shape), dtype))
        return T(t, self.buf(name))

    def dram(self, name, shape, dtype, kind="Internal"):
        t = self.nc.dram_tensor(name, list(shape), dtype, kind=kind)
        return T(t.ap(), self.buf(name))

    def init_psum(self):
        for i in range(8):
            t = self.es.enter_context(self.nc.psum_tensor(f"ps{i}", [128, 512], F32))
            self.psum.append(T(t, self.buf(f"ps{i}")))
            self.psum[-1].b.excl = True

    def ps(self, pin=False):
        while True:
            i = self.ps_next
            self.ps_next = (self.ps_next + 1) % 8
            if i not in self.pinned:
                break
        if pin:
            self.pinned.add(i)
        return self.psum[i]

    def unpin(self, p):
        self.pinned.discard(self.psum.index(p))

    def _deps(self, own_key, R, W, same_raw, Wd=()):
        deps = {}

        def add(ev):
            if ev is None:
                return
            k, v = ev
            if deps.get(k, 0) < v:
                deps[k] = v

        for t in R:
            b = t.b if isinstance(t, T) else t
            if b.w is not None and (b.w[0] != own_key or same_raw):
                add(b.w)
            if b.excl:
                for k, v in b.r.items():
                    if k != own_key:
                        add((k, v))
        for t in W:
            b = t.b if isinstance(t, T) else t
            if b.w is not None and (b.w[0] != own_key or (STRICT and same_raw)):
                add(b.w)
            for k, v in b.r.items():
                if k != own_key or (STRICT and same_raw):
                    add((k, v))
        for t in Wd:
            b = t.b if isinstance(t, T) else t
            if b.w is not None and b.w[0] != own_key:
                add(b.w)
            for k, v in b.r.items():
                if k != own_key or (STRICT and same_raw):
                    add((k, v))
        return deps

    def _mark(self, ev, R, W):
        k, v = ev
        for t in R:
            b = t.b if isinstance(t, T) else t
            if b.r.get(k, 0) < v:
                b.r[k] = v
        for t in W:
            b = t.b if isinstance(t, T) else t
            b.w = ev
            b.r = {}

    def _wait(self, eng, deps):
        for k, v in deps.items():
            if eng["seen"].get(k, 0) < v:
                eng["e"].wait_ge(self.sems[k], v)
                eng["seen"][k] = v

    def op(self, engname, fn, R=(), W=(), Wd=()):
        eng = self.engs[engname]
        deps = self._deps(eng["key"], R, W, same_raw=(engname != "pe"), Wd=Wd)
        W = list(W) + list(Wd)
        self._wait(eng, deps)
        ins = fn(eng["e"])
        eng["cnt"] += 1
        ins.then_inc(self.sems[eng["key"]], 1)
        self._mark((eng["key"], eng["cnt"]), R, W)
        return ins

    def dma(self, q, out, in_, R=(), W=(), **kw):
        eng = self.engs[q]
        dq = self.dq[q]
        deps = self._deps(None, R, W, same_raw=True)
        i = dq["nxt"]
        dq["nxt"] = (i + 1) % NDS
        key = dq["keys"][i]
        if dq["cnt"][i] > 0:
            deps[key] = max(deps.get(key, 0), dq["cnt"][i])
        self._wait(eng, deps)
        dq["cnt"][i] += 16
        eng["e"].dma_start(out=out, in_=in_, **kw).then_inc(self.sems[key], 16)
        self._mark((key, dq["cnt"][i]), R, W)

    def barrier(self):
        allev = {}
        for name, eng in self.engs.items():
            if eng["cnt"] > 0:
                allev[eng["key"]] = eng["cnt"]
        for q, dq in self.dq.items():
            for key, c in zip(dq["keys"], dq["cnt"]):
                if c > 0:
                    allev[key] = c
        for name, eng in self.engs.items():
            deps = {k: v for k, v in allev.items() if k != eng["key"]}
            self._wait(eng, deps)

    def finish(self, out_bufs):
        eng = self.engs["sp"]
        deps = {}
        for t in out_bufs:
            b = t.b if isinstance(t, T) else t
            if b.w is not None:
                deps[b.w[0]] = max(deps.get(b.w[0], 0), b.w[1])
        self._wait(eng, deps)
        self.barrier()


IN_SHAPES = {
    "x": [L, D], "c": [D], "norm1_w": [DEPTH, D], "ada_w": [DEPTH, D, 6 * D], "ada_b": [DEPTH, 6 * D],
    "w_in": [DEPTH, D, 8536], "conv_w": [DEPTH, 4, 1536], "conv_b": [DEPTH, 1536], "dt_bias": [DEPTH, 16],
    "a_log": [DEPTH, 16], "d_skip": [DEPTH, 16], "ssd_norm_w": [DEPTH, D], "w_br_ssd": [DEPTH, D, D],
    "w_br_sb": [DEPTH, 512, D], "w_br_dsa": [DEPTH, 512, D], "w_out": [DEPTH, D, D], "norm2_w": [DEPTH, D],
    "w_up": [DEPTH, D, 4 * D], "w_down": [DEPTH, 4 * D, D], "final_norm_w": [D],
    "rope_cs": [L, 16],
}


def host_consts():
    half = 8
    inv_freq = np.exp(np.arange(half, dtype=np.float32) * np.float32(-2.0 * math.log(500000.0) / 16)).astype(np.float32)
    ang = np.arange(L, dtype=np.float32)[:, None] * inv_freq[None, :]
    return {"rope_cs": np.concatenate([np.cos(ang), np.sin(ang)], axis=1).astype(np.float32)}


class G:
    pass


def build(nlayers=DEPTH, dbg=(), skip_mixer=False, phases=("mod", "inproj", "ssd", "sb", "dsa", "merge", "norm2", "mlp")):
    nc = bass.Bass("TRN2", target_bir_lowering=False)
    kb = KB(nc)
    es = kb.es
    kb.init_psum()
    g = G()
    g.nc, g.kb, g.es, g.dbg = nc, kb, es, set(dbg)
    g.inp = {n: kb.dram(n, shp, F32, kind="ExternalInput") for n, shp in IN_SHAPES.items()}
    out_d = kb.dram("out", [L, D], F32, kind="ExternalOutput")
    g.dbg_out = {}

    g.ident = ident = kb.sb(es, "ident", [128, 128], F32)
    g.identb = identb = kb.sb(es, "identb", [128, 128], BF16)
    g.onesb = onesb = kb.sb(es, "onesb", [128, 128], BF16)
    g.onesf = kb.sb(es, "onesf", [128, 128], F32)
    g.triu = kb.sb(es, "triu", [128, 128], F32)
    g.negm = kb.sb(es, "negm", [128, 128], F32)
    g.triub = kb.sb(es, "triub", [128, 128], BF16)
    g.strl = kb.sb(es, "strl", [128, 128], F32)
    g.strlb = kb.sb(es, "strlb", [128, 128], BF16)
    g.trilb = kb.sb(es, "trilb", [128, 128], BF16)
    g.negup = kb.sb(es, "negup", [128, 128], F32)
    g.negmb = kb.sb(es, "negmb", [128, 128], BF16)
    g.pow2 = kb.sb(es, "pow2", [128, 24], F32)
    with ExitStack() as tmp:
        coli = kb.sb(tmp, "coli", [128, 128], I32)
        rowi = kb.sb(tmp, "rowi", [128, 1], I32)
        colf = kb.sb(tmp, "colf", [128, 128], F32)
        rowf = kb.sb(tmp, "rowf", [128, 1], F32)
        kb.op("pool", lambda e: e.iota(coli[:], [[1, 128]], base=0, channel_multiplier=0), W=[coli])
        kb.op("pool", lambda e: e.iota(rowi[:], [[0, 1]], base=0, channel_multiplier=1), W=[rowi])
        kb.op("dve", lambda e: e.tensor_copy(out=colf[:], in_=coli[:]), R=[coli], W=[colf])
        kb.op("dve", lambda e: e.tensor_copy(out=rowf[:], in_=rowi[:]), R=[rowi], W=[rowf])
        kb.op("dve", lambda e: e.tensor_scalar(out=ident[:], in0=colf[:], scalar1=rowf[:, 0:1], scalar2=None,
                                               op0=ALU.is_equal), R=[colf, rowf], W=[ident])
        kb.op("dve", lambda e: e.tensor_copy(out=identb[:], in_=ident[:]), R=[ident], W=[identb])
        kb.op("dve", lambda e: e.memset(onesb[:], 1.0), W=[onesb])
        kb.op("dve", lambda e: e.memset(g.onesf[:], 1.0), W=[g.onesf])
        kb.op("dve", lambda e: e.tensor_scalar(out=g.triu[:], in0=colf[:], scalar1=rowf[:, 0:1], scalar2=None, op0=ALU.is_ge),
              R=[colf, rowf], W=[g.triu])
        kb.op("dve", lambda e: e.tensor_scalar(out=g.negm[:], in0=g.triu[:], scalar1=-1.0, scalar2=30000.0, op0=ALU.add, op1=ALU.mult),
              R=[g.triu], W=[g.negm])
        kb.op("dve", lambda e: e.tensor_copy(out=g.triub[:], in_=g.triu[:]), R=[g.triu], W=[g.triub])
        kb.op("dve", lambda e: e.tensor_copy(out=g.negmb[:], in_=g.negm[:]), R=[g.negm], W=[g.negmb])
        kb.op("dve", lambda e: e.tensor_scalar(out=g.strl[:], in0=colf[:], scalar1=rowf[:, 0:1], scalar2=None, op0=ALU.is_lt),
              R=[colf, rowf], W=[g.strl])
        kb.op("dve", lambda e: e.tensor_copy(out=g.strlb[:], in_=g.strl[:]), R=[g.strl], W=[g.strlb])
        kb.op("dve", lambda e: e.tensor_scalar(out=g.trilb[:], in0=colf[:], scalar1=rowf[:, 0:1], scalar2=None, op0=ALU.is_le),
              R=[colf, rowf], W=[g.trilb])
        kb.op("dve", lambda e: e.tensor_scalar(out=g.negup[:], in0=g.trilb[:], scalar1=-1.0, scalar2=1.0e9, op0=ALU.add, op1=ALU.mult),
              R=[g.trilb], W=[g.negup])
        for kk_ in range(24):
            kb.op("dve", lambda e, kk_=kk_: e.memset(g.pow2[:, kk_:kk_ + 1], float(2.0 ** (-kk_))), Wd=[g.pow2])
        kb.barrier()

    g.xT = xT = kb.sb(es, "xT", [128, KC, L], F32)
    g.xTb = xTb = [[kb.buf(f"xT{c}_{tb}") for tb in range(4)] for c in range(KC)]
    g.hTb = [[kb.buf(f"hT{c}_{tb}") for tb in range(4)] for c in range(KC)]

    fnw = load_cols(g, es, "fnw", g.inp["final_norm_w"].t, KC)
    g.ccol = load_cols(g, es, "ccol", g.inp["c"].t, KC)
    g.csilu = kb.sb(es, "csilu", [128, KC], BF16)
    kb.op("act", lambda e: e.activation(out=g.csilu[:], in_=g.ccol[:], func=AF.Silu), R=[g.ccol], W=[g.csilu])
    g.modT_l = [kb.sb(es, "modT", [128, 48], F32) for _ in range(DEPTH)]
    g.modraw = [kb.sb(es, "modraw", [128, 48], F32) for _ in range(DEPTH)]
    g.A1_l = [kb.sb(es, "A1", [128, KC], F32) for _ in range(DEPTH)]
    g.A2_l = [kb.sb(es, "A2", [128, KC], F32) for _ in range(DEPTH)]
    g.dsa_hook = None

    x_in = g.inp["x"]
    with ExitStack() as ph:
        xin = [kb.sb(ph, f"xin{i}", [128, D], F32) for i in range(2)]
        for tt in range(NT):
            xi = xin[tt % 2]
            kb.dma("sp", xi[:], x_in.t[tt * 128:(tt + 1) * 128, :], W=[xi])
            for half in range(2):
                p = kb.ps()
                for j in range(4):
                    c = half * 4 + j
                    kb.op("pe", lambda e, c=c, j=j, p=p, xi=xi: e.transpose(
                        p[:, j * 128:(j + 1) * 128], xi[:, c * 128:(c + 1) * 128], ident[:]),
                        R=[xi, ident], W=[p])
                tb = tt // 4
                wb = [xTb[half * 4 + j][tb] for j in range(4)]
                dst = xT[:, half * 4:half * 4 + 4, tt * 128:(tt + 1) * 128]
                src = p[:, :].rearrange("p (j t) -> p j t", j=4)
                if half == 0:
                    kb.op("act", lambda e, dst=dst, src=src: e.activation(out=dst, in_=src, func=AF.Copy), R=[p], W=wb)
                else:
                    kb.op("dve", lambda e, dst=dst, src=src: e.tensor_copy(out=dst, in_=src), R=[p], W=wb)
        kb.barrier()

    def scr(name, shape, dtype=BF16):
        return kb.dram("scr_" + name, shape, dtype, kind=("ExternalOutput" if name in g.dbg else "Internal"))
    g.scr = dict(zs=scr("zs", [L, 1024]), sbv=scr("sbv", [L, 512]), dv=scr("dv", [L, 128]),
                 dqT=scr("dqT", [8, 64, L]), dkT=scr("dkT", [2, 64, L]), iqT=scr("iqT", [8, 64, L]), ikT=scr("ikT", [1, 64, L]),
                 yT=scr("yT", [16, 128, L]), sbqT=scr("sbqT", [4, 128, L]), sbkT=scr("sbkT", [4, 128, L]), gT=scr("gT", [24, 128, L]))
    for name in g.scr:
        if name in g.dbg:
            g.dbg_out["scr_" + name] = g.scr[name]
    g.ropecs = kb.sb(es, "ropecs", [128, NT, 16], F32)
    kb.dma("sp", g.ropecs[:], g.inp["rope_cs"].t.rearrange("(t p) c -> p t c", p=128), W=[g.ropecs])
    g.dt_tok = kb.sb(es, "dt_tok", [128, NT, 16], F32)
    g.wix_tok = kb.sb(es, "wix_tok", [128, NT, 8], F32)
    g.dtb_bc = kb.sb(es, "dtb_bc", [128, 16], F32)
    g.cb = kb.sb(es, "cb", [128, 12], F32)
    g.cw = [kb.sb(es, f"cw{k}", [128, 12], F32) for k in range(4)]

    for l in range(nlayers):
        g.modT, g.A1, g.A2 = g.modT_l[l], g.A1_l[l], g.A2_l[l]
        if "mod" in phases:
            if l == 0:
                phase_mod(g, l)
            else:
                mod_finish(g, l)
        g.dsa_hook = (l + 1) if (l + 1 < nlayers and "mod" in phases) else None
        dump_sb(g, f"mod{l}", g.modT, [128, 48])
        if not skip_mixer:
            with ExitStack() as ssd_scope:
                g.xs_tok = kb.sb(ssd_scope, "xs_tok", [128, NT, 1024], BF16)
                g.bt_tok = kb.sb(ssd_scope, "bt_tok", [128, NT, 256], BF16)
                g.bcT = kb.sb(ssd_scope, "bcT", [128, 4, L], BF16)
                with nc.allow_non_contiguous_dma(reason="tiny vectors"):
                    kb.dma("sp", g.dtb_bc[:], g.inp["dt_bias"].t[l].partition_broadcast(128), W=[g.dtb_bc])
                    kb.dma("sp", g.cb[:], g.inp["conv_b"].t[l].rearrange("(c p) -> p c", p=128), W=[g.cb])
                    for k in range(4):
                        kb.dma("sp", g.cw[k][:], g.inp["conv_w"].t[l][k].rearrange("(c p) -> p c", p=128), W=[g.cw[k]])
                with ExitStack() as hsc:
                    g.hT = kb.sb(hsc, "hT", [128, KC, L], BF16)
                    phase_norm(g, l, which=1)
                    dump_hT(g, f"h{l}")
                    if "inproj" in phases:
                        phase_inproj(g, l)
                dump_sb(g, f"xs_tok{l}", g.xs_tok, [128, NT, 1024], BF16)
                dump_sb(g, f"bt_tok{l}", g.bt_tok, [128, NT, 256], BF16)
                dump_sb(g, f"bcT{l}", g.bcT, [128, 4, L], BF16)
                dump_sb(g, f"dt_tok{l}", g.dt_tok, [128, NT, 16], F32)
                dump_sb(g, f"wix_tok{l}", g.wix_tok, [128, NT, 8], F32)
                kb.barrier()
                if "ssd" in phases:
                    phase_ssd(g, l)
            if "sb" in phases:
                phase_sb(g, l)
            if "dsa" in phases:
                phase_dsa(g, l)
            if "merge" in phases:
                phase_merge(g, l)
            dump_xT(g, f"xmix{l}")
        with ExitStack() as hsc:
            g.hT = kb.sb(hsc, "hT", [128, KC, L], BF16)
            if "norm2" in phases:
                phase_norm(g, l, which=2)
            if "mlp" in phases:
                phase_mlp(g, l)
        dump_xT(g, f"xout{l}")

    final_norm(kb, nc, xT, xTb, fnw, onesb, ident, out_d)
    kb.finish([out_d] + list(g.dbg_out.values()))
    return nc, sorted(g.dbg_out.keys())


def load_cols(g, es_, name, src_ap, n):
    kb = g.kb
    t = kb.sb(es_, name, [128, n], F32)
    with g.nc.allow_non_contiguous_dma(reason="tiny per-feature vector load"):
        for c0 in range(0, n, 8):
            c1 = min(n, c0 + 8)
            kb.dma("sp", t[:, c0:c1], src_ap[c0 * 128:c1 * 128].rearrange("(c p) -> p c", p=128), W=[t])
    return t


def dump_sb(g, name, t, shape, dtype=F32):
    if name not in g.dbg:
        return
    d = g.kb.dram("dbg_" + name, shape, dtype, kind="ExternalOutput")
    g.kb.barrier()
    g.kb.dma("sp", d.t, t[:], R=[t], W=[d])
    g.dbg_out["dbg_" + name] = d


def dump_xT(g, name):
    if name not in g.dbg:
        return
    d = g.kb.dram("dbg_" + name, [128, KC, L], F32, kind="ExternalOutput")
    g.kb.barrier()
    g.kb.dma("sp", d.t, g.xT[:], R=[b for row in g.xTb for b in row], W=[d])
    g.dbg_out["dbg_" + name] = d


def dump_hT(g, name):
    if name not in g.dbg:
        return
    d = g.kb.dram("dbg_" + name, [128, KC, L], BF16, kind="ExternalOutput")
    g.kb.barrier()
    g.kb.dma("sp", d.t, g.hT[:], R=[b for row in g.hTb for b in row], W=[d])
    g.dbg_out["dbg_" + name] = d


def load_w(g, t, src_rows_ap):
    g.kb.dma("pool", t[:], src_rows_ap.rearrange("(kc p) n -> p kc n", p=128), W=[t])


def mod_piece_load(g, l, j4, w):
    g.kb.dma("pool", w[:], g.inp["ada_w"].t[l][:, j4 * 512:(j4 + 1) * 512].rearrange("(kc p) n -> p kc n", p=128), W=[w])


def mod_piece_mm(g, l, j4, w):
    kb = g.kb
    p = kb.ps()
    for jl in range(4):
        for kc in range(KC):
            kb.op("pe", lambda e, jl=jl, kc=kc: e.matmul(p[:, jl:jl + 1], w[:, kc, jl * 128:(jl + 1) * 128], g.csilu[:, kc:kc + 1],
                                                         start=(kc == 0), stop=(kc == KC - 1)), R=[w, g.csilu], W=[p])
    raw = g.modraw[l]
    kb.op("act", lambda e: e.activation(out=raw[:, j4 * 4:(j4 + 1) * 4], in_=p[:, 0:4], func=AF.Copy), R=[p], Wd=[raw])


def mod_piece(g, l, j4, w):
    mod_piece_load(g, l, j4, w)
    mod_piece_mm(g, l, j4, w)


def mod_finish(g, l):
    kb = g.kb
    with ExitStack() as ph:
        adab = load_cols(g, ph, "adab", g.inp["ada_b"].t[l], 48)
        n1w = load_cols(g, ph, "n1w", g.inp["norm1_w"].t[l], KC)
        n2w = load_cols(g, ph, "n2w", g.inp["norm2_w"].t[l], KC)
        modT, A1, A2 = g.modT_l[l], g.A1_l[l], g.A2_l[l]
        kb.op("dve", lambda e: e.tensor_tensor(out=modT[:], in0=g.modraw[l][:], in1=adab[:], op=ALU.add),
              R=[g.modraw[l], adab], W=[modT])
        kb.op("dve", lambda e: e.scalar_tensor_tensor(out=A1[:], in0=modT[:, 8:16], scalar=1.0, in1=n1w[:],
                                                      op0=ALU.add, op1=ALU.mult), R=[modT, n1w], W=[A1])
        kb.op("dve", lambda e: e.scalar_tensor_tensor(out=A2[:], in0=modT[:, 32:40], scalar=1.0, in1=n2w[:],
                                                      op0=ALU.add, op1=ALU.mult), R=[modT, n2w], W=[A2])
        kb.barrier()


def phase_mod(g, l):
    kb = g.kb
    with ExitStack() as ph:
        wt = [kb.sb(ph, f"adaw{i}", [128, KC, 512], BF16) for i in range(3)]
        for j4 in range(12):
            mod_piece(g, l, j4, wt[j4 % 3])
        kb.barrier()
    mod_finish(g, l)


def phase_norm(g, l, which):
    kb = g.kb
    A = g.A1 if which == 1 else g.A2
    sh0 = 0 if which == 1 else 24
    with ExitStack() as ph:
        sq = [kb.sb(ph, f"nsq{i}", [128, 512], BF16) for i in range(2)]
        rs = kb.sb(ph, "nrs", [128, 512], F32)
        tmp = [kb.sb(ph, f"ntmp{i}", [128, 512], F32) for i in range(2)]
        for tb in range(4):
            rstd_bcast(kb, ph, g.xT, g.xTb, tb, g.onesb, sq, rs)
            for c in range(KC):
                t = tmp[c % 2]
                kb.op("dve", lambda e, t=t, c=c: e.tensor_tensor(out=t[:], in0=g.xT[:, c, tb * 512:(tb + 1) * 512], in1=rs[:],
                                                                 op=ALU.mult), R=[g.xTb[c][tb], rs], W=[t])
                kb.op("act", lambda e, t=t, c=c: e.activation(
                    out=g.hT[:, c, tb * 512:(tb + 1) * 512], in_=t[:], func=AF.Identity,
                    scale=A[:, c:c + 1], bias=g.modT[:, sh0 + c:sh0 + c + 1]), R=[t, A, g.modT], W=[g.hTb[c][tb]])
        kb.barrier()


OFF = dict(z=0, xbc=1024, dt=2560, sbq=2576, sbk=3088, sbv=3600, dsq=4112, dsk=4624, dsv=4752,
           ixq=4880, ixk=5392, ixw=5456, gate=5464)


def rope(g, f3, nh, tt, rt):
    kb = g.kb
    fb = f3._buf
    cos = g.ropecs[:, tt:tt + 1, 0:8].broadcast_to([128, nh, 8])
    sin = g.ropecs[:, tt:tt + 1, 8:16].broadcast_to([128, nh, 8])
    x1 = f3.ap[:, :, 0:8]
    x2 = f3.ap[:, :, 8:16]
    t = [rt[:, i, 0:nh, :] for i in range(4)]
    kb.op("dve", lambda e: e.tensor_tensor(out=t[0], in0=x1, in1=cos, op=ALU.mult), R=[fb, g.ropecs], W=[rt])
    kb.op("dve", lambda e: e.tensor_tensor(out=t[1], in0=x2, in1=sin, op=ALU.mult), R=[fb, g.ropecs], W=[rt])
    kb.op("dve", lambda e: e.tensor_tensor(out=t[2], in0=x2, in1=cos, op=ALU.mult), R=[fb, g.ropecs], W=[rt])
    kb.op("dve", lambda e: e.tensor_tensor(out=t[3], in0=x1, in1=sin, op=ALU.mult), R=[fb, g.ropecs], W=[rt])
    kb.op("dve", lambda e: e.tensor_tensor(out=x1, in0=t[0], in1=t[1], op=ALU.subtract), R=[rt], W=[fb])
    kb.op("dve", lambda e: e.tensor_tensor(out=x2, in0=t[2], in1=t[3], op=ALU.add), R=[rt], W=[fb])


class V:
    def __init__(self, ap, buf):
        self.ap = ap
        self._buf = buf


def phase_inproj(g, l):
    kb, nc = g.kb, g.nc
    win = g.inp["w_in"].t[l]
    S = g.scr
    with ExitStack() as ph:
        wt = [kb.sb(ph, "winw", [128, KC, 512], BF16) for _ in range(2)]
        nw = [0]

        def getw(n0, N):
            w = wt[nw[0] % 2]
            nw[0] += 1
            kb.dma("pool", w[:, :, 0:N], win[:, n0:n0 + N].rearrange("(kc p) n -> p kc n", p=128), W=[w])
            return w

        stgb = [kb.sb(ph, "stgb", [128, 512], BF16) for _ in range(3)]
        stgf = [kb.sb(ph, "stgf", [128, 512], F32) for _ in range(2)]
        rt = kb.sb(ph, "ropetmp", [128, 4, 16, 8], F32)
        tst = [kb.sb(ph, "tst", [64, 8, 512], BF16) for _ in range(2)]
        cnt = dict(b=0, f=0, t=0)

        def nxt(lst, k):
            r = lst[cnt[k] % len(lst)]
            cnt[k] += 1
            return r

        def tok_mm(w, N, tt):
            p = kb.ps()
            for kc in range(KC):
                kb.op("pe", lambda e, kc=kc: e.matmul(p[:, 0:N], g.hT[:, kc, tt * 128:(tt + 1) * 128], w[:, kc, 0:N],
                                                      start=(kc == 0), stop=(kc == KC - 1)),
                      R=[g.hTb[kc][tt // 4], w], W=[p])
            return p

        def feat_mm(w, ci, tb):
            p = kb.ps()
            for kc in range(KC):
                kb.op("pe", lambda e, kc=kc: e.matmul(p[:, :], w[:, kc, ci * 128:(ci + 1) * 128],
                                                      g.hT[:, kc, tb * 512:(tb + 1) * 512],
                                                      start=(kc == 0), stop=(kc == KC - 1)),
                      R=[g.hTb[kc][tb], w], W=[p])
            return p

        for zi in range(2):
            w = getw(OFF["z"] + zi * 512, 512)
            for tt in range(NT):
                p = tok_mm(w, 512, tt)
                sb_ = nxt(stgb, "b")
                kb.op("act", lambda e: e.activation(out=sb_[:], in_=p[:, :], func=AF.Silu), R=[p], W=[sb_])
                kb.dma("sp", S["zs"].t[tt * 128:(tt + 1) * 128, zi * 512:(zi + 1) * 512], sb_[:], R=[sb_])
        w = getw(OFF["dt"], 16)
        for tt in range(NT):
            p = tok_mm(w, 16, tt)
            f = nxt(stgf, "f")
            kb.op("dve", lambda e: e.tensor_tensor(out=f[:, 0:16], in0=p[:, 0:16], in1=g.dtb_bc[:], op=ALU.add),
                  R=[p, g.dtb_bc], W=[f])
            kb.op("act", lambda e: e.activation(out=f[:, 0:16], in_=f[:, 0:16], func=AF.Exp), R=[f], W=[f])
            kb.op("act", lambda e: e.activation(out=g.dt_tok[:, tt, :], in_=f[:, 0:16], func=AF.Ln, bias=1.0),
                  R=[f], W=[g.dt_tok])
        w = getw(OFF["sbv"], 512)
        for tt in range(NT):
            p = tok_mm(w, 512, tt)
            sb_ = nxt(stgb, "b")
            kb.op("act", lambda e: e.activation(out=sb_[:], in_=p[:, :], func=AF.Copy), R=[p], W=[sb_])
            kb.dma("sp", S["sbv"].t[tt * 128:(tt + 1) * 128, :], sb_[:], R=[sb_])

        def roped_group(n0, N, nh, dstT, extra=None):
            w = getw(n0, N)
            fs, bs, sts = {}, {}, {}

            def stA(tt):
                p = tok_mm(w, N, tt)
                f = nxt(stgf, "f")
                kb.op("act", lambda e: e.activation(out=f[:, 0:N], in_=p[:, 0:N], func=AF.Copy), R=[p], W=[f])
                fs[tt] = f

            def stB(tt):
                f = fs.pop(tt)
                rope(g, V(f[:, 0:nh * 64].rearrange("p (h d) -> p h d", d=64), f), nh, tt, rt)
                b = nxt(stgb, "b")
                kb.op("act", lambda e: e.activation(out=b[:, 0:N], in_=f[:, 0:N], func=AF.Copy), R=[f], W=[b])
                if extra is not None:
                    extra(tt, f, b)
                bs[tt] = b

            def stC(tt):
                b = bs.pop(tt)
                pt = kb.ps()
                ptb = pt[:, :].bitcast(BF16)
                for h in range(nh):
                    kb.op("pe", lambda e, h=h: e.transpose(ptb[0:64, h * 128:(h + 1) * 128], b[:, h * 64:(h + 1) * 64], g.identb[:]),
                          R=[b, g.identb], W=[pt])
                if tt % 4 == 0:
                    sts[tt // 4] = nxt(tst, "t")
                st = sts[tt // 4]
                kb.op("act", lambda e: e.activation(
                    out=st[0:64, 0:nh, (tt % 4) * 128:(tt % 4 + 1) * 128],
                    in_=ptb[0:64, 0:nh * 128].rearrange("p (h t) -> p h t", h=nh), func=AF.Copy), R=[pt], Wd=[st])
                if tt % 4 == 3:
                    tb = tt // 4
                    kb.dma("sp", dstT.t[:, :, tb * 512:(tb + 1) * 512].rearrange("h d t -> d h t"), st[0:64, 0:nh, :], R=[st])

            for step in range(NT + 2):
                if step < NT:
                    stA(step)
                if 0 <= step - 1 < NT:
                    stB(step - 1)
                if 0 <= step - 2 < NT:
                    stC(step - 2)

        roped_group(OFF["dsq"], 512, 8, S["dqT"])

        def dsv_extra(tt, f, b):
            kb.dma("sp", S["dv"].t[tt * 128:(tt + 1) * 128, :], b[:, 128:256], R=[b])
        roped_group(OFF["dsk"], 256, 2, S["dkT"], dsv_extra)
        roped_group(OFF["ixq"], 512, 8, S["iqT"])

        def ixw_extra(tt, f, b):
            kb.op("dve", lambda e: e.tensor_copy(out=g.wix_tok[:, tt, :], in_=f[:, 64:72]), R=[f], W=[g.wix_tok])
        roped_group(OFF["ixk"], 72, 1, S["ikT"], ixw_extra)

        def feat_group(n0, dstT, c0, func):
            w = getw(n0, 512)
            for ci in range(4):
                for tb in range(4):
                    p = feat_mm(w, ci, tb)
                    sb_ = nxt(stgb, "b")
                    kb.op("act", lambda e: e.activation(out=sb_[:], in_=p[:, :], func=func), R=[p], W=[sb_])
                    kb.dma("sp", dstT.t[c0 + ci, :, tb * 512:(tb + 1) * 512], sb_[:], R=[sb_])

        feat_group(OFF["sbq"], S["sbqT"], 0, AF.Copy)
        feat_group(OFF["sbk"], S["sbkT"], 0, AF.Copy)
        for gi in range(6):
            feat_group(OFF["gate"] + gi * 512, S["gT"], gi * 4, AF.Sigmoid)

        kb.barrier()

    with ExitStack() as ph:
        wt = [kb.sb(ph, "winw", [128, KC, 512], BF16) for _ in range(2)]
        nw = [0]
        xpad = [kb.sb(ph, "xpad", [128, 3 + L], F32) for _ in range(2)]
        cva = [kb.sb(ph, "cva", [128, L], F32) for _ in range(1)]
        cvo = [kb.sb(ph, "cvo", [128, L], BF16) for _ in range(1)]
        for xp in xpad:
            kb.op("dve", lambda e, xp=xp: e.memset(xp[:, 0:3], 0.0), W=[xp])
        ws = {}
        outs = {}

        def X1(cidx):
            gi, ci = cidx // 4, cidx % 4
            if ci == 0:
                ws[gi] = getw(OFF["xbc"] + gi * 512, 512)
            w = ws[gi]
            xp = xpad[cidx % 2]
            for tb in range(4):
                p = feat_mm(w, ci, tb)
                kb.op("act", lambda e, p=p, tb=tb: e.activation(out=xp[:, 3 + tb * 512:3 + (tb + 1) * 512], in_=p[:, :], func=AF.Copy),
                      R=[p], Wd=[xp])

        def X2(cidx):
            xp, acc = xpad[cidx % 2], cva[0]
            kb.op("act", lambda e: e.activation(out=acc[:], in_=xp[:, 3:3 + L], func=AF.Identity,
                                                scale=g.cw[3][:, cidx:cidx + 1], bias=g.cb[:, cidx:cidx + 1]),
                  R=[xp, g.cw[3], g.cb], W=[acc])
            for k in (2, 1, 0):
                kb.op("dve", lambda e, k=k: e.scalar_tensor_tensor(out=acc[:], in0=xp[:, k:k + L], scalar=g.cw[k][:, cidx:cidx + 1],
                                                                   in1=acc[:], op0=ALU.mult, op1=ALU.add),
                      R=[xp, g.cw[k], acc], W=[acc])
            if cidx < 8:
                o = cvo[0]
                ov = o[:, :]
            else:
                o = g.bcT
                ov = g.bcT[:, cidx - 8, :]
            kb.op("act", lambda e: e.activation(out=ov, in_=acc[:], func=AF.Silu), R=[acc], W=[o])
            outs[cidx] = (o, ov)

        def X3(cidx):
            o, ov = outs.pop(cidx)
            if cidx >= 10:
                return
            for half in range(2):
                pt = kb.ps()
                ptb = pt[:, :].bitcast(BF16)
                for j in range(8):
                    tt = half * 8 + j
                    kb.op("pe", lambda e, j=j, tt=tt: e.transpose(ptb[:, j * 128:(j + 1) * 128], ov[:, tt * 128:(tt + 1) * 128],
                                                                  g.identb[:]), R=[o, g.identb], W=[pt])
                if cidx < 8:
                    dst, dt_ = g.xs_tok[:, half * 8:(half + 1) * 8, cidx * 128:(cidx + 1) * 128], g.xs_tok
                else:
                    dst, dt_ = g.bt_tok[:, half * 8:(half + 1) * 8, (cidx - 8) * 128:(cidx - 7) * 128], g.bt_tok
                kb.op("dve", lambda e, dst=dst, ptb=ptb: e.tensor_copy(out=dst, in_=ptb.rearrange("p (j f) -> p j f", j=8)), R=[pt], Wd=[dt_])

        for step in range(12 + 2):
            if step < 12:
                X1(step)
            if 0 <= step - 2 < 12:
                X3(step - 2)
            if 0 <= step - 1 < 12:
                X2(step - 1)
        kb.barrier()


def phase_ssd(g, l):
    kb, nc = g.kb, g.nc
    S = g.scr
    with ExitStack() as ph:
        def sb(name, shape, dt_=F32):
            return kb.sb(ph, name, shape, dt_)
        a_bc = sb("a_bc", [128, 16])
        dsk_bc = sb("dsk_bc", [128, 16])
        nw_bc = sb("nw_bc", [128, 1024])
        Dmat = sb("Dmat", [128, 16, 128], BF16)
        with nc.allow_non_contiguous_dma(reason="tiny vectors"):
            kb.dma("sp", a_bc[:], g.inp["a_log"].t[l].partition_broadcast(128), W=[a_bc])
            kb.dma("sp", dsk_bc[:], g.inp["d_skip"].t[l].partition_broadcast(128), W=[dsk_bc])
            kb.dma("sp", nw_bc[:], g.inp["ssd_norm_w"].t[l].partition_broadcast(128), W=[nw_bc])
        kb.op("act", lambda e: e.activation(out=a_bc[:], in_=a_bc[:], func=AF.Exp), R=[a_bc], W=[a_bc])
        kb.op("dve", lambda e: e.tensor_scalar(out=a_bc[:], in0=a_bc[:], scalar1=-1.0, scalar2=None, op0=ALU.mult), R=[a_bc], W=[a_bc])
        for h in range(16):
            kb.op("dve", lambda e, h=h: e.tensor_scalar(out=Dmat[:, h, :], in0=g.ident[:], scalar1=dsk_bc[:, h:h + 1], scalar2=None,
                                                        op0=ALU.mult), R=[g.ident, dsk_bc], Wd=[Dmat])
        NB = 2
        X = [sb("X", [128, 2, 16, 128], BF16)] * 2
        dd = [sb("dd", [128, 16, 128])] * 2
        Mt = [sb("Mt", [128, 16, 128], BF16) for _ in range(NB)]
        xw = [sb("xw", [128, 1024], BF16) for _ in range(NB)]
        zst = [sb("zst", [128, 1024], BF16) for _ in range(NB)]
        yy = [sb("yy", [128, 1024])] * 2
        yz = yy
        ss = [sb("ss", [128, 1]) for _ in range(NB)]
        epsc = sb("epsc", [128, 1])
        kb.op("dve", lambda e: e.memset(epsc[:], EPS), W=[epsc])
        yn = [sb("yn", [128, 1024], BF16) for _ in range(NB)]
        yst = [sb("yst", [128, KC, 256], BF16)] * 2
        hst = [sb("hst", [128, 512]) for _ in range(2)]
        htmp = [sb("htmp", [128, 512]) for _ in range(2)]
        prevb = [[sb("prevb", [128, 512], BF16) for _ in range(2)] for _ in range(2)]
        ytmp = [sb("ytmp", [128, 512]) for _ in range(2)]

        dA_all = sb("dA_all", [128, 256])
        dAs_all = sb("dAs_all", [128, 2, 256], BF16)
        ac_all = sb("ac_all", [128, 256])
        eac_all = sb("eac_all", [128, 256])
        cd_all = sb("cd_all", [128, 256])
        wgt_all = sb("wgt_all", [128, 256])
        v_all = sb("v_all", [128, 256])
        vs_all = sb("vs_all", [128, 2, 256], BF16)
        acs_all = sb("acs_all", [128, 2, 256], BF16)
        dtf = g.dt_tok[:, :, :].rearrange("p c h -> p (c h)")
        kb.op("dve", lambda e: e.tensor_tensor(out=dA_all[:, :].rearrange("p (c h) -> p c h", h=16), in0=g.dt_tok[:, :, :],
                                               in1=a_bc[:, :].unsqueeze(1).broadcast_to([128, NT, 16]), op=ALU.mult),
              R=[g.dt_tok, a_bc], W=[dA_all])
        kb.op("dve", lambda e: e.tensor_copy(out=dAs_all[:, 0, :], in_=dA_all[:]), R=[dA_all], W=[dAs_all])
        kb.op("dve", lambda e: e.tensor_tensor(out=dAs_all[:, 1, :], in0=dA_all[:], in1=dAs_all[:, 0, :], op=ALU.subtract),
              R=[dA_all, dAs_all], W=[dAs_all])
        pac = kb.ps(pin=True)
        for c in range(NT):
            for half, lhs in ((0, g.triub), (1, g.onesb)):
                for hl_ in range(2):
                    kb.op("pe", lambda e, c=c, half=half, lhs=lhs, hl_=hl_: e.matmul(
                        pac[:, half * 256 + c * 16:half * 256 + (c + 1) * 16], lhs[:], dAs_all[:, hl_, c * 16:(c + 1) * 16],
                        start=(hl_ == 0), stop=(hl_ == 1)), R=[lhs, dAs_all], W=[pac])
        kb.unpin(pac)
        kb.op("dve", lambda e: e.tensor_copy(out=ac_all[:], in_=pac[:, 0:256]), R=[pac], W=[ac_all])
        kb.op("act", lambda e: e.activation(out=eac_all[:], in_=pac[:, 0:256], func=AF.Exp), R=[pac], W=[eac_all])
        kb.op("act", lambda e: e.activation(out=cd_all[:], in_=pac[:, 256:512], func=AF.Exp), R=[pac], W=[cd_all])
        kb.op("dve", lambda e: e.tensor_tensor(out=wgt_all[:], in0=pac[:, 256:512], in1=ac_all[:], op=ALU.subtract), R=[pac, ac_all], W=[wgt_all])
        kb.op("act", lambda e: e.activation(out=wgt_all[:], in_=wgt_all[:], func=AF.Exp), R=[wgt_all], W=[wgt_all])
        kb.op("dve", lambda e: e.tensor_tensor(out=wgt_all[:], in0=wgt_all[:], in1=dtf, op=ALU.mult), R=[wgt_all, g.dt_tok], W=[wgt_all])
        kb.op("act", lambda e: e.activation(out=v_all[:], in_=dtf, func=AF.Ln), R=[g.dt_tok], W=[v_all])
        kb.op("dve", lambda e: e.tensor_tensor(out=v_all[:], in0=v_all[:], in1=ac_all[:], op=ALU.subtract), R=[v_all, ac_all], W=[v_all])
        kb.op("dve", lambda e: e.tensor_copy(out=vs_all[:, 0, :], in_=v_all[:]), R=[v_all], W=[vs_all])
        kb.op("dve", lambda e: e.tensor_tensor(out=vs_all[:, 1, :], in0=v_all[:], in1=vs_all[:, 0, :], op=ALU.subtract), R=[v_all, vs_all], W=[vs_all])
        kb.op("dve", lambda e: e.tensor_copy(out=acs_all[:, 0, :], in_=ac_all[:]), R=[ac_all], W=[acs_all])
        kb.op("dve", lambda e: e.tensor_tensor(out=acs_all[:, 1, :], in0=ac_all[:], in1=acs_all[:, 0, :], op=ALU.subtract), R=[ac_all, acs_all], W=[acs_all])

        pcbs = {}

        def P1(c):
            k = c % NB
            tsl = slice(c * 128, (c + 1) * 128)
            c16 = slice(c * 16, (c + 1) * 16)
            kb.op("dve", lambda e: e.tensor_tensor(
                out=xw[k][:, :].rearrange("p (h d) -> p h d", d=64), in0=g.xs_tok[:, c, :].rearrange("p (h d) -> p h d", d=64),
                in1=wgt_all[:, c16].unsqueeze(2).broadcast_to([128, 16, 64]), op=ALU.mult), R=[g.xs_tok, wgt_all], W=[xw[k]])
            for hl_ in range(2):
                kb.op("dve", lambda e, hl_=hl_: e.tensor_tensor(
                    out=X[k][:, hl_, :, :], in0=g.identb[:, :].unsqueeze(1).broadcast_to([128, 16, 128]),
                    in1=acs_all[:, hl_, c16].unsqueeze(2).broadcast_to([128, 16, 128]), op=ALU.mult), R=[g.identb, acs_all], Wd=[X[k]])
            for q4 in range(4):
                pb = kb.ps()
                h4 = slice(c * 16 + q4 * 4, c * 16 + q4 * 4 + 4)
                for hl_ in range(2):
                    kb.op("pe", lambda e, hl_=hl_: e.matmul(pb[:, :], g.onesb[:], X[k][:, hl_, q4 * 4:(q4 + 1) * 4, :],
                                                            start=(hl_ == 0), stop=False), R=[g.onesb, X[k]], W=[pb])
                for hl_ in range(2):
                    kb.op("pe", lambda e, hl_=hl_: e.matmul(pb[:, :], g.identb[:], vs_all[:, hl_, h4].unsqueeze(2).broadcast_to([128, 4, 128]),
                                                            start=False, stop=False), R=[g.identb, vs_all], W=[pb])
                kb.op("pe", lambda e: e.matmul(pb[:, :], g.identb[:], g.negmb[:, :].unsqueeze(1).broadcast_to([128, 4, 128]),
                                               start=False, stop=True), R=[g.identb, g.negmb], W=[pb])
                kb.op("act", lambda e, q4=q4, pb=pb: e.activation(out=dd[k][:, q4 * 4:(q4 + 1) * 4, :],
                                                                  in_=pb[:, :].rearrange("p (h i) -> p h i", h=4), func=AF.Exp),
                      R=[pb], Wd=[dd[k]])
            pcb = kb.ps(pin=True)
            for gg in range(2):
                kb.op("pe", lambda e, gg=gg: e.matmul(pcb[:, gg * 128:(gg + 1) * 128], g.bcT[:, gg, tsl], g.bcT[:, 2 + gg, tsl],
                                                      start=True, stop=True), R=[g.bcT], W=[pcb])
            pcbs[c] = pcb

        def P1b(c):
            k = c % NB
            pcb = pcbs.pop(c)
            kb.unpin(pcb)
            for gg in range(2):
                kb.op("dve", lambda e, gg=gg: e.tensor_tensor(
                    out=Mt[k][:, gg * 8:(gg + 1) * 8, :], in0=dd[k][:, gg * 8:(gg + 1) * 8, :],
                    in1=pcb[:, gg * 128:(gg + 1) * 128].unsqueeze(1).broadcast_to([128, 8, 128]), op=ALU.mult),
                    R=[dd[k], pcb], Wd=[Mt[k]])

        pend2 = {}

        def P2a(c):
            k = c % NB
            tsl = slice(c * 128, (c + 1) * 128)
            kb.dma("sp", zst[k][:], S["zs"].t[tsl, :], W=[zst[k]])
            rec = []
            for gg in range(2):
                pA = kb.ps()
                for hl in range(8):
                    h = gg * 8 + hl
                    xsl = g.xs_tok[:, c, h * 64:(h + 1) * 64]
                    kb.op("pe", lambda e, h=h, hl=hl, xsl=xsl: e.matmul(pA[:, hl * 64:(hl + 1) * 64], Mt[k][:, h, :], xsl, start=True, stop=False),
                          R=[Mt[k], g.xs_tok], W=[pA])
                    kb.op("pe", lambda e, h=h, hl=hl, xsl=xsl: e.matmul(pA[:, hl * 64:(hl + 1) * 64], Dmat[:, h, :], xsl, start=False, stop=True),
                          R=[Dmat, g.xs_tok], W=[pA])
                pB = pS = None
                if c > 0:
                    pB = kb.ps()
                    pv = prevb[gg][(c - 1) % 2]
                    kb.op("pe", lambda e, pB=pB, pv=pv, gg=gg: e.matmul(pB[:, :], g.bcT[:, 2 + gg, tsl], pv[:], start=True, stop=True),
                          R=[g.bcT, pv], W=[pB])
                if c < NT - 1:
                    pS = kb.ps()
                    kb.op("pe", lambda e, pS=pS, gg=gg: e.matmul(pS[:, :], g.bt_tok[:, c, gg * 128:(gg + 1) * 128], xw[k][:, gg * 512:(gg + 1) * 512],
                                                                start=True, stop=True), R=[g.bt_tok, xw[k]], W=[pS])
                rec.append((pA, pB, pS))
            pend2[c] = rec

        def P2b(c):
            k = c % NB
            rec = pend2.pop(c)
            for gg in range(2):
                pA, pB, pS = rec[gg]
                ysl = yy[k][:, gg * 512:(gg + 1) * 512]
                if c > 0:
                    yt = ytmp[gg]
                    kb.op("dve", lambda e: e.tensor_tensor(
                        out=yt[:, :].rearrange("p (h d) -> p h d", d=64), in0=pB[:, :].rearrange("p (h d) -> p h d", d=64),
                        in1=eac_all[:, c * 16 + gg * 8:c * 16 + (gg + 1) * 8].unsqueeze(2).broadcast_to([128, 8, 64]), op=ALU.mult),
                        R=[pB, eac_all], W=[yt])
                    kb.op("dve", lambda e: e.tensor_tensor(out=ysl, in0=yt[:], in1=pA[:, :], op=ALU.add), R=[yt, pA], W=[yy[k]])
                else:
                    kb.op("act", lambda e: e.activation(out=ysl, in_=pA[:, :], func=AF.Copy), R=[pA], W=[yy[k]])
                if c < NT - 1:
                    if c == 0:
                        kb.op("dve", lambda e: e.tensor_copy(out=hst[gg][:], in_=pS[:, :]), R=[pS], W=[hst[gg]])
                    else:
                        kb.op("dve", lambda e: e.tensor_tensor(
                            out=htmp[gg][:, :].rearrange("p (h d) -> p h d", d=64), in0=hst[gg][:, :].rearrange("p (h d) -> p h d", d=64),
                            in1=cd_all[:, c * 16 + gg * 8:c * 16 + (gg + 1) * 8].unsqueeze(2).broadcast_to([128, 8, 64]), op=ALU.mult),
                            R=[hst[gg], cd_all], W=[htmp[gg]])
                        kb.op("dve", lambda e: e.tensor_tensor(out=hst[gg][:], in0=htmp[gg][:], in1=pS[:, :], op=ALU.add),
                              R=[htmp[gg], pS], W=[hst[gg]])
                    pn = prevb[gg][c % 2]
                    kb.op("act", lambda e: e.activation(out=pn[:], in_=hst[gg][:], func=AF.Copy), R=[hst[gg]], W=[pn])

        def P3(c):
            k = c % NB
            tsl = slice(c * 128, (c + 1) * 128)
            kb.op("dve", lambda e: e.tensor_tensor(out=yz[k][:], in0=yy[k][:], in1=zst[k][:], op=ALU.mult), R=[yy[k], zst[k]], W=[yz[k]])
            kb.op("act", lambda e: e.activation(out=yn[k][:], in_=yz[k][:], func=AF.Square, accum_out=ss[k][:]), R=[yz[k]], W=[yn[k], ss[k]])
            kb.op("act", lambda e: e.activation(out=ss[k][:], in_=ss[k][:], func=AF.Ln, scale=1.0 / 1024, bias=epsc[:, 0:1]), R=[ss[k], epsc], W=[ss[k]])
            kb.op("act", lambda e: e.activation(out=ss[k][:], in_=ss[k][:], func=AF.Exp, scale=-0.5), R=[ss[k]], W=[ss[k]])
            kb.op("dve", lambda e: e.scalar_tensor_tensor(out=yn[k][:], in0=yz[k][:], scalar=ss[k][:, 0:1], in1=nw_bc[:],
                                                          op0=ALU.mult, op1=ALU.mult), R=[yz[k], ss[k], nw_bc], W=[yn[k]])
            pt = kb.ps()
            ptb = pt[:, :].bitcast(BF16)
            for kc in range(KC):
                kb.op("pe", lambda e, kc=kc: e.transpose(ptb[:, kc * 128:(kc + 1) * 128], yn[k][:, kc * 128:(kc + 1) * 128], g.identb[:]),
                      R=[yn[k], g.identb], W=[pt])
            st = yst[(c // 2) % 2]
            kb.op("act", lambda e: e.activation(out=st[:, :, (c % 2) * 128:(c % 2 + 1) * 128],
                                                in_=ptb.rearrange("p (kc t) -> p kc t", kc=KC), func=AF.Copy), R=[pt], Wd=[st])
            if c % 2 == 1:
                t2 = c // 2
                kb.dma("sp", S["yT"].t[0:8, :, t2 * 256:(t2 + 1) * 256].rearrange("kc p t -> p kc t"), st[:], R=[st])

        for step in range(-1, NT + 1):
            if 0 <= step + 1 < NT:
                P1(step + 1)
            if 0 <= step < NT:
                P2a(step)
            if 0 <= step - 1 < NT:
                P3(step - 1)
            if 0 <= step + 1 < NT:
                P1b(step + 1)
            if 0 <= step < NT:
                P2b(step)
        kb.barrier()


def phase_sb(g, l):
    kb, nc = g.kb, g.nc
    S = g.scr
    with ExitStack() as ph:
        def sb(name, shape, dt_=F32):
            return kb.sb(ph, name, shape, dt_)
        qT = sb("sbq", [128, 4, L], BF16)
        kT = sb("sbk", [128, 4, L], BF16)
        v = sb("sbv", [128, NT, 512], BF16)
        onesw = sb("onesw", [128, L], BF16)
        kb.op("dve", lambda e: e.memset(onesw[:], 1.0), W=[onesw])
        kb.dma("sp", qT[:], S["sbqT"].t.rearrange("c p t -> p c t"), W=[qT])
        kb.dma("sp", kT[:], S["sbkT"].t.rearrange("c p t -> p c t"), W=[kT])
        kb.dma("sp", v[:], S["sbv"].t.rearrange("(t p) f -> p t f", p=128), W=[v])
        e1b = [sb("e1", [128, L]) for _ in range(3)]
        spb = [sb("sp", [128, L]) for _ in range(2)]
        Fb = [sb("F", [128, L + 1]) for _ in range(2)]
        attb = [sb("att", [128, L], BF16) for _ in range(2)]
        attT = [sb("attT", [128, NT, 128], BF16) for _ in range(2)]
        ftn = [sb("ftn", [128, 1]) for _ in range(2)]
        yst = [sb("yst", [128, 128], BF16) for _ in range(4)]
        for F in Fb:
            kb.op("dve", lambda e, F=F: e.memset(F[:, 0:1], 0.0), W=[F])
        iters = [(qt, h) for qt in range(NT) for h in range(8)]

        def s1(i):
            qt, h = iters[i]
            W = 128 * (qt + 1)
            nch = (W + 511) // 512
            dsl = slice(qt * 128, (qt + 1) * 128)
            hp, hc = (h % 2) * 64, h // 2
            e1, sp_ = e1b[i % 3], spb[i % 2]
            for ch in range(nch):
                n = min(512, W - ch * 512)
                p = kb.ps()
                kb.op("pe", lambda e, p=p, ch=ch, n=n: e.matmul(p[:, 0:n], qT[hp:hp + 64, hc, dsl], kT[hp:hp + 64, hc, ch * 512:ch * 512 + n],
                                                                start=True, stop=True), R=[qT, kT], W=[p])
                kb.op("act", lambda e, p=p, ch=ch, n=n: e.activation(out=e1[:, ch * 512:ch * 512 + n], in_=p[:, 0:n], func=AF.Exp, scale=0.125),
                      R=[p], Wd=[e1])
            kb.op("act", lambda e: e.activation(out=sp_[:, 0:W], in_=e1[:, 0:W], func=AF.Ln, bias=1.0), R=[e1], W=[sp_])

        def s2(i):
            qt, h = iters[i]
            k = i % 2
            W = 128 * (qt + 1)
            dsl = slice(qt * 128, (qt + 1) * 128)
            sp_, F = spb[k], Fb[k]
            kb.op("dve", lambda e: e.tensor_tensor(out=sp_[:, dsl], in0=sp_[:, dsl], in1=g.strl[:], op=ALU.mult), R=[sp_, g.strl], W=[sp_])
            kb.op("dve", lambda e: e.memset(F[:, 0:1], 0.0), W=[F])
            kb.op("dve", lambda e: e.tensor_tensor_scan(out=F[:, 1:W + 1], data0=onesw[:, 0:W], data1=sp_[:, 0:W], initial=0.0,
                                                        op0=ALU.mult, op1=ALU.add), R=[onesw, sp_], W=[F])
            kb.op("dve", lambda e: e.tensor_scalar(out=ftn[k][:], in0=F[:, W:W + 1], scalar1=-1.0, scalar2=None, op0=ALU.mult),
                  R=[F], W=[ftn[k]])

        def s3a(i):
            qt, h = iters[i]
            k = i % 2
            W = 128 * (qt + 1)
            F = Fb[k]
            kb.op("act", lambda e: e.activation(out=F[:, 0:W], in_=F[:, 0:W], func=AF.Exp, bias=ftn[k][:, 0:1]), R=[F, ftn[k]], W=[F])

        def s3b(i):
            qt, h = iters[i]
            k = i % 2
            W = 128 * (qt + 1)
            dsl = slice(qt * 128, (qt + 1) * 128)
            e1, F, att = e1b[i % 3], Fb[k], attb[k]
            kb.op("dve", lambda e: e.tensor_tensor(out=att[:, 0:W], in0=e1[:, 0:W], in1=F[:, 0:W], op=ALU.mult), R=[e1, F], W=[att])
            kb.op("dve", lambda e: e.tensor_tensor(out=att[:, dsl], in0=att[:, dsl], in1=g.strlb[:], op=ALU.mult), R=[att, g.strlb], W=[att])

        def s3c(i):
            qt, h = iters[i]
            k = i % 2
            att, aT = attb[k], attT[k]
            for b0 in range(0, qt + 1, 8):
                nb = min(8, qt + 1 - b0)
                pt = kb.ps()
                ptb = pt[:, :].bitcast(BF16)
                for j in range(nb):
                    kb.op("pe", lambda e, j=j, b0=b0: e.transpose(ptb[:, j * 128:(j + 1) * 128], att[:, (b0 + j) * 128:(b0 + j + 1) * 128], g.identb[:]),
                          R=[att, g.identb], W=[pt])
                kb.op("act", lambda e, b0=b0, nb=nb: e.activation(out=aT[:, b0:b0 + nb, :], in_=ptb[:, 0:nb * 128].rearrange("p (j t) -> p j t", j=nb),
                                                                 func=AF.Copy), R=[pt], Wd=[aT])

        def s3d(i):
            qt, h = iters[i]
            k = i % 2
            dsl = slice(qt * 128, (qt + 1) * 128)
            hp, hc = (h % 2) * 64, h // 2
            aT = attT[k]
            py = kb.ps()
            for sbk in range(qt + 1):
                kb.op("pe", lambda e, sbk=sbk: e.matmul(py[0:64, 0:128], v[:, sbk, h * 64:(h + 1) * 64], aT[:, sbk, :],
                                                        start=(sbk == 0), stop=(sbk == qt)), R=[v, aT], W=[py])
            pys[i] = py

        def s3e(i):
            qt, h = iters[i]
            dsl = slice(qt * 128, (qt + 1) * 128)
            hp, hc = (h % 2) * 64, h // 2
            py = pys.pop(i)
            ys = yst[i % 4]
            kb.op("dve", lambda e: e.tensor_copy(out=ys[hp:hp + 64, :], in_=py[0:64, 0:128]), R=[py], W=[ys])
            kb.dma("sp", S["yT"].t[8 + hc, hp:hp + 64, dsl], ys[hp:hp + 64, :], R=[ys])

        n_it = len(iters)

        def run(fn, i):
            if 0 <= i < n_it:
                fn(i)
        pys = {}
        for step in range(n_it + 4):
            run(s3d, step - 4)
            run(s3a, step - 2)
            run(s1, step)
            run(s3c, step - 3)
            run(s2, step - 1)
            run(s3e, step - 4)
            run(s3b, step - 2)
        kb.barrier()


NBIS = 16


def phase_dsa(g, l):
    kb, nc = g.kb, g.nc
    S = g.scr
    with ExitStack() as ph:
        def sb(name, shape, dt_=F32):
            return kb.sb(ph, name, shape, dt_)
        dk = sb("dk", [128, L], BF16)
        ik = sb("ik", [128, L], BF16)
        vaug = sb("vaug", [128, NT, 2, 128], BF16)
        kb.dma("sp", dk[:], S["dkT"].t.rearrange("h d t -> (h d) t"), W=[dk])
        kb.dma("sp", ik[0:64, :], S["ikT"].t[0], W=[ik])
        kb.dma("sp", ik[64:128, :], S["ikT"].t[0], W=[ik])
        kb.op("dve", lambda e: e.memset(vaug[:], 1.0), W=[vaug])
        for gg in range(2):
            kb.dma("sp", vaug[:, :, gg, 0:64], S["dv"].t[:, gg * 64:(gg + 1) * 64].rearrange("(t p) d -> p t d", p=128), W=[vaug])
        dqt = [sb("dqt", [128, 4, 128], BF16) for _ in range(2)]
        iqt = [sb("iqt", [128, 4, 128], BF16) for _ in range(2)]
        score = [sb("score", [128, L]) for _ in range(2)]
        rl = [sb("rl", [128, 512], BF16) for _ in range(4)]
        wabs = [sb("wabs", [128, 8]) for _ in range(2)]
        wsgn = [sb("wsgn", [128, 8]) for _ in range(2)]
        Dg = [sb("Dg", [128, 8, 128], BF16) for _ in range(2)]
        junk = sb("junk", [128, L], BF16)
        maskb = [sb("maskb", [128, L], BF16) for _ in range(2)]
        maskT = [sb("maskT", [128, NT, 128], BF16) for _ in range(2)]
        Eb = [sb("E", [128, 512], BF16) for _ in range(4)]
        M_ = [sb("M", [128, 1]) for _ in range(2)]
        A_ = [sb("A", [128, NBIS + 1]) for _ in range(2)]
        mid = [sb("mid", [128, 1]) for _ in range(2)]
        cnt = [sb("cnt", [128, 1]) for _ in range(2)]
        sela = [sb("sela", [128, 1]) for _ in range(2)]
        R0 = [sb("R0", [64, 512]) for _ in range(2)]
        ytmp = [sb("ytmp", [64, 512], BF16) for _ in range(2)]
        yraw = [sb("yraw", [64, 512]) for _ in range(2)]
        yst = [sb("dyst", [128, 4, 128], BF16) for _ in range(2)]
        cn = dict(rl=0, E=0)

        def stage_I(qt):
            k = qt % 2
            W = 128 * (qt + 1)
            dsl = slice(qt * 128, (qt + 1) * 128)
            if qt < 2:
                return
            iq_src = S["iqT"].t.rearrange("(hp e) d t -> e d hp t", e=2)
            for e_ in range(2):
                kb.dma("sp", iqt[k][e_ * 64:(e_ + 1) * 64, :, :], iq_src[e_][:, :, dsl], W=[iqt[k]])
            sc = score[k]
            nch = (W + 511) // 512
            wv = g.wix_tok[:, qt, :]
            kb.op("act", lambda e: e.activation(out=wabs[k][:], in_=wv, func=AF.Abs), R=[g.wix_tok], W=[wabs[k]])
            kb.op("dve", lambda e: e.tensor_scalar(out=wsgn[k][:], in0=wv, scalar1=0.0, scalar2=2.0, op0=ALU.is_ge, op1=ALU.mult),
                  R=[g.wix_tok], W=[wsgn[k]])
            kb.op("dve", lambda e: e.tensor_scalar(out=wsgn[k][:], in0=wsgn[k][:], scalar1=-1.0, scalar2=None, op0=ALU.add),
                  R=[wsgn[k]], W=[wsgn[k]])
            for h in range(8):
                kb.op("dve", lambda e, h=h: e.tensor_scalar(out=Dg[k][:, h, :], in0=g.identb[:], scalar1=wsgn[k][:, h:h + 1], scalar2=None,
                                                            op0=ALU.mult), R=[g.identb, wsgn[k]], Wd=[Dg[k]])
            items = [(ch, h) for ch in range(nch) for h in range(8)]
            pend = {}
            pscs = {}

            def mm(j):
                ch, h = items[j]
                n = min(512, W - ch * 512)
                p = kb.ps()
                hp_ = (h % 2) * 64
                kb.op("pe", lambda e: e.matmul(p[:, 0:n], iqt[k][hp_:hp_ + 64, h // 2, :], ik[hp_:hp_ + 64, ch * 512:ch * 512 + n],
                                               start=True, stop=True), R=[iqt[k], ik], W=[p])
                r = rl[cn["rl"] % 4]
                cn["rl"] += 1
                kb.op("act", lambda e: e.activation(out=r[:, 0:n], in_=p[:, 0:n], func=AF.Relu, scale=wabs[k][:, h:h + 1]),
                      R=[p, wabs[k]], W=[r])
                pend[j] = r

            def acc(j):
                ch, h = items[j]
                n = min(512, W - ch * 512)
                r = pend.pop(j)
                if h == 0:
                    pscs[ch] = kb.ps(pin=True)
                psc = pscs[ch]
                kb.op("pe", lambda e: e.matmul(psc[:, 0:n], Dg[k][:, h, :], r[:, 0:n], start=(h == 0), stop=(h == 7)),
                      R=[Dg[k], r], W=[psc])
                if h == 7:
                    kb.unpin(psc)
                    kb.op("act", lambda e: e.activation(out=sc[:, ch * 512:ch * 512 + n], in_=psc[:, 0:n], func=AF.Copy), R=[psc], Wd=[sc])

            for j in range(0, len(items) + 2, 2):
                for jj in (j, j + 1):
                    if jj < len(items):
                        mm(jj)
                for jj in (j - 2, j - 1):
                    if 0 <= jj < len(items):
                        acc(jj)

        def stage_B(qt):
            k = qt % 2
            W = 128 * (qt + 1)
            dsl = slice(qt * 128, (qt + 1) * 128)
            mT = maskT[k]
            if qt < 2:
                if qt == 1:
                    kb.op("dve", lambda e: e.memset(mT[:, 0, :], 0.0), W=[mT])
                kb.op("dve", lambda e: e.tensor_copy(out=mT[:, qt, :], in_=g.negmb[:]), R=[g.negmb], W=[mT])
                return
            sc = score[k]
            kb.op("dve", lambda e: e.tensor_reduce(out=M_[k][:], in_=sc[:, 0:W], axis=AX.X, op=ALU.max, apply_absolute_value=True),
                  R=[sc], W=[M_[k]])
            kb.op("dve", lambda e: e.tensor_scalar(out=A_[k][:, 0:NBIS], in0=g.pow2[:, 0:NBIS], scalar1=M_[k][:, 0:1], scalar2=None, op0=ALU.mult),
                  R=[g.pow2, M_[k]], W=[A_[k]])
            kb.op("dve", lambda e: e.tensor_copy(out=A_[k][:, NBIS:NBIS + 1], in_=A_[k][:, NBIS - 1:NBIS]), R=[A_[k]], W=[A_[k]])
            kb.op("dve", lambda e: e.tensor_tensor(out=sc[:, dsl], in0=sc[:, dsl], in1=g.negup[:], op=ALU.add), R=[sc, g.negup], W=[sc])
            kb.op("dve", lambda e: e.memset(mid[k][:], 0.0), W=[mid[k]])
            for it in range(NBIS):
                kb.op("dve", lambda e: e.tensor_scalar(out=junk[:, 0:W], in0=sc[:, 0:W], scalar1=mid[k][:, 0:1], scalar2=0.0,
                                                       op0=ALU.is_ge, op1=ALU.add, accum_out=cnt[k][:]),
                      R=[sc, mid[k]], W=[junk, cnt[k]])
                kb.op("dve", lambda e, it=it: e.tensor_scalar(out=sela[k][:], in0=cnt[k][:], scalar1=255.5, scalar2=A_[k][:, it:it + 1],
                                                              op0=ALU.is_ge, op1=ALU.mult), R=[cnt[k], A_[k]], W=[sela[k]])
                kb.op("dve", lambda e, it=it: e.scalar_tensor_tensor(out=mid[k][:], in0=sela[k][:], scalar=A_[k][:, it + 1:it + 2], in1=mid[k][:],
                                                                     op0=ALU.subtract, op1=ALU.add), R=[sela[k], A_[k], mid[k]], W=[mid[k]])
            mb = maskb[k]
            kb.op("dve", lambda e: e.tensor_scalar(out=mb[:, 0:W], in0=sc[:, 0:W], scalar1=mid[k][:, 0:1], scalar2=-30000.0,
                                                   op0=ALU.is_lt, op1=ALU.mult), R=[sc, mid[k]], W=[mb])

        def stage_T(qt):
            if qt < 2:
                return
            k = qt % 2
            mb, mT = maskb[k], maskT[k]
            for b0 in range(0, qt + 1, 8):
                nb = min(8, qt + 1 - b0)
                pt = kb.ps()
                ptb = pt[:, :].bitcast(BF16)
                for j in range(nb):
                    kb.op("pe", lambda e, j=j, b0=b0: e.transpose(ptb[:, j * 128:(j + 1) * 128], mb[:, (b0 + j) * 128:(b0 + j + 1) * 128], g.identb[:]),
                          R=[mb, g.identb], W=[pt])
                kb.op("act", lambda e, b0=b0, nb=nb: e.activation(out=mT[:, b0:b0 + nb, :], in_=ptb[:, 0:nb * 128].rearrange("p (j t) -> p j t", j=nb),
                                                                 func=AF.Copy), R=[pt], Wd=[mT])

        def load_dq(buf, qt_):
            src = S["dqT"].t.rearrange("(g hh) d t -> g d hh t", g=2)
            for g_ in range(2):
                kb.dma("sp", buf[g_ * 64:(g_ + 1) * 64, :, :], src[g_][:, :, qt_ * 128:(qt_ + 1) * 128], W=[buf])

        def stage_A(qt):
            k = qt % 2
            dsl = slice(qt * 128, (qt + 1) * 128)
            mT = maskT[k]
            ys = yst[k]
            if qt + 1 < NT:
                load_dq(dqt[1 - k], qt + 1)
            pOs = [kb.ps(pin=True) for _ in range(2)]
            Es = {}

            def qk(sbk):
                pSs = [kb.ps() for _ in range(2)]
                for gq in range(2):
                    kb.op("pe", lambda e, gq=gq: e.matmul(pSs[gq][:, :], dk[gq * 64:(gq + 1) * 64, sbk * 128:(sbk + 1) * 128],
                                                          dqt[k][gq * 64:(gq + 1) * 64, :, :], start=True, stop=False),
                          R=[dk, dqt[k]], W=[pSs[gq]])
                for gq in range(2):
                    kb.op("pe", lambda e, gq=gq: e.matmul(pSs[gq][:, :], g.identb[:], mT[:, sbk, :].unsqueeze(1).broadcast_to([128, 4, 128]),
                                                          start=False, stop=True), R=[g.identb, mT], W=[pSs[gq]])
                for gq in range(2):
                    E = Eb[cn["E"] % 4]
                    cn["E"] += 1
                    kb.op("act", lambda e, gq=gq, E=E: e.activation(out=E[:], in_=pSs[gq][:, :], func=AF.Exp, scale=0.125), R=[pSs[gq]], W=[E])
                    Es[(sbk, gq)] = E

            def av(sbk):
                for gq in range(2):
                    E = Es.pop((sbk, gq))
                    kb.op("pe", lambda e, gq=gq, E=E: e.matmul(pOs[gq][:, :], vaug[:, sbk, gq, :], E[:], start=(sbk == 0), stop=(sbk == qt)),
                          R=[vaug, E], W=[pOs[gq]])

            for sbk in range(qt + 2):
                if sbk <= qt:
                    qk(sbk)
                if sbk >= 1:
                    av(sbk - 1)
            for gq in range(2):
                def norm(pO=pOs[gq], gq=gq, ys=ys, dsl=dsl, last=(gq == 1)):
                    kb.unpin(pO)
                    r0, yt = R0[gq], ytmp[gq]
                    kb.op("act", lambda e: e.activation(out=r0[:], in_=pO[64:128, :], func=AF.Ln), R=[pO], W=[r0])
                    kb.op("act", lambda e: e.activation(out=r0[:], in_=r0[:], func=AF.Exp, scale=-1.0), R=[r0], W=[r0])
                    yr = yraw[gq]
                    kb.op("act", lambda e: e.activation(out=yr[:], in_=pO[0:64, :], func=AF.Copy), R=[pO], W=[yr])
                    kb.op("pool", lambda e: e.tensor_tensor(out=yt[:], in0=yr[:], in1=r0[:], op=ALU.mult), R=[yr, r0], W=[yt])
                    for hh in range(4):
                        h = 4 * gq + hh
                        hp, hc = (h % 2) * 64, h // 2
                        kb.op("act", lambda e, hh=hh, hp=hp, hc=hc: e.activation(out=ys[hp:hp + 64, hc, :], in_=yt[:, hh * 128:(hh + 1) * 128], func=AF.Copy),
                              R=[yt], Wd=[ys])
                    if last:
                        kb.dma("sp", S["yT"].t[12:16, :, dsl].rearrange("c p t -> p c t"), ys[:], R=[ys])
                pending.append(norm)

        pending = []
        load_dq(dqt[0], 0)
        stage_I(0)
        stage_B(0)
        stage_T(0)
        stage_I(1)
        modw = [sb("adaw", [128, KC, 512], BF16) for _ in range(2)] if g.dsa_hook is not None else None
        for qt in range(NT):
            if qt + 2 < NT:
                stage_I(qt + 2)
            prev_norms = pending[:]
            del pending[:]
            stage_A(qt)
            for nf in prev_norms:
                nf()
            if modw is not None and 1 <= qt < 13:
                mod_piece_load(g, g.dsa_hook, qt - 1, modw[(qt - 1) % 2])
            if modw is not None and 2 <= qt < 14:
                mod_piece_mm(g, g.dsa_hook, qt - 2, modw[(qt - 2) % 2])
            if qt + 1 < NT:
                stage_B(qt + 1)
                stage_T(qt + 1)
        while pending:
            pending.pop(0)()
        kb.barrier()


def phase_merge(g, l):
    kb, nc = g.kb, g.nc
    S = g.scr
    with ExitStack() as ph:
        def sb(name, shape, dt_=F32):
            return kb.sb(ph, name, shape, dt_)
        yT = sb("yTall", [128, 16, L], BF16)
        yTb = [kb.buf() for _ in range(4)]
        for q in range(4):
            kb.dma("sp", yT[:, q * 4:(q + 1) * 4, :], S["yT"].t[q * 4:(q + 1) * 4].rearrange("c p t -> p c t"), W=[yTb[q]])
        mT = sb("mergedT", [128, KC, L], BF16)
        mTb = [[kb.buf() for tb in range(4)] for c in range(KC)]
        wbr = [sb("wbr", [128, 16, 256], BF16) for _ in range(2)]
        gt = [sb("gt", [128, 3, 512], BF16) for _ in range(2)]
        gbr = [sb("gbr", [128, 3, 512], BF16) for _ in range(2)]
        tiles = [(c2, cl, tb) for c2 in range(4) for cl in range(2) for tb in range(4)]
        wcur = {}
        pbrs = {}

        def M1(i):
            c2, cl, tb = tiles[i]
            k = i % 2
            c = c2 * 2 + cl
            tsl = slice(tb * 512, (tb + 1) * 512)
            if cl == 0 and tb == 0:
                w = wbr[c2 % 2]
                csl = slice(c2 * 256, (c2 + 1) * 256)
                kb.dma("pool", w[:, 0:8, :], g.inp["w_br_ssd"].t[l][:, csl].rearrange("(kc p) n -> p kc n", p=128), W=[w])
                kb.dma("pool", w[:, 8:12, :], g.inp["w_br_sb"].t[l][:, csl].rearrange("(kc p) n -> p kc n", p=128), W=[w])
                kb.dma("pool", w[:, 12:16, :], g.inp["w_br_dsa"].t[l][:, csl].rearrange("(kc p) n -> p kc n", p=128), W=[w])
                wcur[c2] = w
            w = wcur[c2]
            for br in range(3):
                kb.dma("sp", gt[k][:, br, :], S["gT"].t[br * 8 + c, :, tsl], W=[gt[k]])
            pbr = []
            for br, (k0, k1) in enumerate(((0, 8), (8, 12), (12, 16))):
                p = kb.ps()
                for kc in range(k0, k1):
                    kb.op("pe", lambda e, p=p, kc=kc, k0=k0, k1=k1: e.matmul(
                        p[:, :], w[:, kc, cl * 128:(cl + 1) * 128], yT[:, kc, tsl], start=(kc == k0), stop=(kc == k1 - 1)),
                        R=[w, yTb[kc // 4]], W=[p])
                pbr.append(p)
            pbrs[i] = pbr

        def M2(i):
            k = i % 2
            pbr = pbrs.pop(i)
            gb = gbr[k]
            for br in range(3):
                kb.op("dve", lambda e, br=br: e.tensor_tensor(out=gb[:, br, :], in0=pbr[br][:, :], in1=gt[k][:, br, :], op=ALU.mult),
                      R=[pbr[br], gt[k]], Wd=[gb])

        def M3(i):
            c2, cl, tb = tiles[i]
            k = i % 2
            c = c2 * 2 + cl
            tsl = slice(tb * 512, (tb + 1) * 512)
            gb = gbr[k]
            pm = kb.ps()
            for br in range(3):
                kb.op("pe", lambda e, br=br: e.matmul(pm[:, :], g.identb[:], gb[:, br, :], start=(br == 0), stop=(br == 2)),
                      R=[g.identb, gb], W=[pm])
            kb.op("act", lambda e: e.activation(out=mT[:, c, tsl], in_=pm[:, :], func=AF.Copy), R=[pm], W=[mTb[c][tb]])

        for i in range(len(tiles) + 1):
            if i < len(tiles):
                M1(i)
            if i >= 1:
                M2(i - 1)
                M3(i - 1)
        wo = [sb("wo", [128, KC, 256], BF16) for _ in range(2)]
        for c2 in range(4):
            w = wo[c2 % 2]
            kb.dma("pool", w[:], g.inp["w_out"].t[l][:, c2 * 256:(c2 + 1) * 256].rearrange("(kc p) n -> p kc n", p=128), W=[w])
            for cl in range(2):
                c = c2 * 2 + cl
                for tb in range(4):
                    tsl = slice(tb * 512, (tb + 1) * 512)
                    p = kb.ps()
                    for kc in range(KC):
                        kb.op("pe", lambda e, p=p, kc=kc: e.matmul(p[:, :], w[:, kc, cl * 128:(cl + 1) * 128], mT[:, kc, tsl],
                                                                   start=(kc == 0), stop=(kc == KC - 1)), R=[w, mTb[kc][tb]], W=[p])
                    xs = g.xT[:, c, tsl]
                    kb.op("dve", lambda e, p=p, c=c, xs=xs: e.scalar_tensor_tensor(
                        out=xs, in0=p[:, :], scalar=g.modT[:, 16 + c:17 + c], in1=xs, op0=ALU.mult, op1=ALU.add),
                        R=[p, g.modT, g.xTb[c][tb]], W=[g.xTb[c][tb]])
        kb.barrier()


def phase_mlp(g, l):
    kb = g.kb
    with ExitStack() as ph:
        wup = [kb.sb(ph, f"wup{i}", [128, KC, 512], BF16) for i in range(2)]
        wdn = [kb.sb(ph, f"wdn{i}", [128, 4, D], BF16) for i in range(2)]
        act = [kb.sb(ph, f"mact{i}", [128, 4, L], BF16) for i in range(2)]
        actb = [[[kb.buf() for tb in range(4)] for hc in range(4)] for i in range(2)]
        rl = [kb.sb(ph, f"mrl{i}", [128, 512], F32) for i in range(2)]
        nrl = 0
        for gi in range(8):
            wu, wd, a, ab = wup[gi % 2], wdn[gi % 2], act[gi % 2], actb[gi % 2]
            load_w(g, wu, g.inp["w_up"].t[l][:, gi * 512:(gi + 1) * 512])
            load_w(g, wd, g.inp["w_down"].t[l][gi * 512:(gi + 1) * 512, :])
            for tb in range(4):
                for hc in range(4):
                    p = kb.ps()
                    for kc in range(KC):
                        kb.op("pe", lambda e, p=p, kc=kc, hc=hc: e.matmul(
                            p[:, :], wu[:, kc, hc * 128:(hc + 1) * 128], g.hT[:, kc, tb * 512:(tb + 1) * 512],
                            start=(kc == 0), stop=(kc == KC - 1)), R=[wu, g.hTb[kc][tb]], W=[p])
                    r = rl[nrl % 2]
                    nrl += 1
                    kb.op("act", lambda e, p=p, r=r: e.activation(out=r[:], in_=p[:, :], func=AF.Relu), R=[p], W=[r])
                    kb.op("act", lambda e, r=r, hc=hc: e.activation(out=a[:, hc, tb * 512:(tb + 1) * 512], in_=r[:], func=AF.Square),
                          R=[r], W=[ab[hc][tb]])
            import os
            if os.environ.get("MLP_UP_ONLY"):
                continue
            for tb in range(4):
                for c in range(KC):
                    p = kb.ps()
                    for hc in range(4):
                        kb.op("pe", lambda e, p=p, hc=hc, c=c: e.matmul(
                            p[:, :], wd[:, hc, c * 128:(c + 1) * 128], a[:, hc, tb * 512:(tb + 1) * 512],
                            start=(hc == 0), stop=(hc == 3)), R=[wd, ab[hc][tb]], W=[p])
                    xs = g.xT[:, c, tb * 512:(tb + 1) * 512]
                    kb.op("dve", lambda e, p=p, c=c, xs=xs: e.scalar_tensor_tensor(
                        out=xs, in0=p[:, :], scalar=g.modT[:, 40 + c:41 + c], in1=xs, op0=ALU.mult, op1=ALU.add),
                        R=[p, g.modT, g.xTb[c][tb]], W=[g.xTb[c][tb]])
        kb.barrier()


def rstd_bcast(kb, ph, xT, xTb, tb, onesb, sq, rs):
    p = kb.ps()
    for c in range(KC):
        s = sq[c % 2]
        kb.op("act", lambda e, s=s, c=c: e.activation(out=s[:], in_=xT[:, c, tb * 512:(tb + 1) * 512], func=AF.Square),
              R=[xTb[c][tb]], W=[s])
        kb.op("pe", lambda e, s=s, c=c, p=p: e.matmul(p[:, :], onesb[:], s[:], start=(c == 0), stop=(c == KC - 1)),
              R=[s, onesb], W=[p])
    kb.op("dve", lambda e: e.tensor_scalar(out=rs[:], in0=p[:, :], scalar1=1.0 / D, scalar2=EPS, op0=ALU.mult, op1=ALU.add),
          R=[p], W=[rs])
    kb.op("act", lambda e: e.activation(out=rs[:], in_=rs[:], func=AF.Ln), R=[rs], W=[rs])
    kb.op("act", lambda e: e.activation(out=rs[:], in_=rs[:], func=AF.Exp, scale=-0.5), R=[rs], W=[rs])


def final_norm(kb, nc, xT, xTb, fnw, onesb, ident, out_d):
    with ExitStack() as ph:
        sq = [kb.sb(ph, f"fsq{i}", [128, 512], BF16) for i in range(2)]
        rs = kb.sb(ph, "frs", [128, 512], F32)
        yT = [kb.sb(ph, f"fyT{i}", [128, 512], F32) for i in range(2)]
        ost = [kb.sb(ph, f"fost{i}", [128, 4, D], F32) for i in range(2)]
        for tb in range(4):
            rstd_bcast(kb, ph, xT, xTb, tb, onesb, sq, rs)
            o = ost[tb % 2]
            for c in range(KC):
                y = yT[c % 2]
                kb.op("dve", lambda e, y=y, c=c: e.scalar_tensor_tensor(
                    out=y[:], in0=xT[:, c, tb * 512:(tb + 1) * 512], scalar=fnw[:, c:c + 1], in1=rs[:],
                    op0=ALU.mult, op1=ALU.mult), R=[xTb[c][tb], fnw, rs], W=[y])
                p = kb.ps()
                for j in range(4):
                    kb.op("pe", lambda e, y=y, j=j, p=p: e.transpose(
                        p[:, j * 128:(j + 1) * 128], y[:, j * 128:(j + 1) * 128], ident[:]), R=[y, ident], W=[p])
                src = p[:, :].rearrange("p (j f) -> p j f", j=4)
                dst = o[:, :, c * 128:(c + 1) * 128]
                kb.op("act", lambda e, dst=dst, src=src: e.activation(out=dst, in_=src, func=AF.Copy), R=[p], Wd=[o])
            kb.dma("sp", out_d.t[tb * 512:(tb + 1) * 512, :].rearrange("(j p) d -> p j d", p=128), o[:], R=[o], W=[out_d])
        kb.barrier()


_NC_CACHE = {}


def make_in_maps(inputs):
    nb = inputs["x"].shape[0]
    inputs = dict(inputs)
    inputs.update(host_consts())
    shared = {n: np.ascontiguousarray(np.asarray(inputs[n], dtype=np.float32)) for n in IN_SHAPES if n not in ("x", "c")}
    in_maps = []
    for b in range(nb):
        m = dict(shared)
        m["x"] = np.ascontiguousarray(inputs["x"][b])
        m["c"] = np.ascontiguousarray(inputs["c"][b])
        in_maps.append(m)
    return in_maps


def kernel(**inputs):
    nb = inputs["x"].shape[0]
    if "nc" not in _NC_CACHE:
        _NC_CACHE["nc"] = build()[0]
    nc = _NC_CACHE["nc"]
    res = run_bass_kernel_spmd(nc, make_in_maps(inputs), core_ids=list(range(nb)))
    return np.stack([r["out"] for r in res.results], axis=0)
```
